# Optimizing a Trainium2 kernel written in Bass

```python
import math
import jax
import jax.numpy as jnp
from jax import lax
import numpy as np

D_MODEL = 1024
BATCH = 4
SEQ = 8192
DEPTH = 2

CTX_LEN = 256
GRID_W = 64
HEAD_DIM = 64
BRANCH_W = D_MODEL // 2

ATT_HEADS = BRANCH_W // HEAD_DIM
ATT_KV_HEADS = 2
ATT_GROUP = ATT_HEADS // ATT_KV_HEADS
WINDOW = 128
BLOCK = 128
ROPE_BASE = 10000.0
NEG_INF = -1e30

RWKV_HEADS = BRANCH_W // HEAD_DIM
RWKV_W = RWKV_HEADS * HEAD_DIM
W_LORA = 64
A_LORA = 64
G_LORA = 128
DECAY_SCALE = 0.606531
GN_EPS = 64e-5

S5_W = BRANCH_W
S5_GROUP_CH = 16
S5_GROUPS = S5_W // S5_GROUP_CH
S5_STATE = 64

D_FF = 2816
FFN_RES = 0.5
N_BRANCH = 3
N_MOD = 9
ALPHA = (2.0 * DEPTH) ** 0.25
BETA = (8.0 * DEPTH) ** -0.25
LN_EPS = 1e-6

ATT_Q = ATT_HEADS * HEAD_DIM
ATT_KV = ATT_KV_HEADS * HEAD_DIM
ATT_IN = ATT_Q + 2 * ATT_KV
RWKV_IN = 3 * RWKV_W + 2 * W_LORA + A_LORA + G_LORA
N_IN = ATT_IN + RWKV_IN + S5_W + N_BRANCH * D_MODEL

F32 = jnp.float32

kernel_name = 'hybrid_dit_gqa_rwkv7_s5_block'


def _normalise(x):
    xf = x.astype(F32)
    mean = jnp.mean(xf, -1, keepdims=True)
    var = jnp.mean(jnp.square(xf - mean), -1, keepdims=True)
    return (xf - mean) * lax.rsqrt(var + LN_EPS)


def ln_plain(x):
    return _normalise(x).astype(x.dtype)


def ln_affine(x, g, b):
    return (_normalise(x) * g.astype(F32) + b.astype(F32)).astype(x.dtype)


def modulate(x, shift, scale):
    return ln_plain(x) * (1.0 + scale) + shift


def swiglu(u, w1, w2):
    gate, up = jnp.split(u @ w1, 2, axis=-1)
    return (jax.nn.silu(gate) * up) @ w2


def ffn_sublayer(x, shift, scale, gate, w1, w2, g, b):
    h = swiglu(modulate(x, shift, scale), w1, w2)
    return ln_affine(ALPHA * x + FFN_RES * gate * h, g, b)


def centred_shift(z):
    zp = jnp.pad(z, ((0, 0), (1, 1), (0, 0)))
    return 0.5 * (zp[:, :-2] + zp[:, 2:])


def rope_1d(x, pos):
    n = x.shape[-1] // 2
    inv = ROPE_BASE ** (-jnp.arange(n, dtype=F32) / n)
    ang = pos.astype(F32)[:, None] * inv[None, :]
    shape = (pos.shape[0],) + (1,) * (x.ndim - 3) + (n,)
    cos = jnp.cos(ang).reshape(shape)
    sin = jnp.sin(ang).reshape(shape)
    xf = x.astype(F32)
    x1, x2 = xf[..., :n], xf[..., n:]
    return jnp.concatenate([x1 * cos - x2 * sin, x2 * cos + x1 * sin], axis=-1).astype(x.dtype)


def axial_rope(x, row, col):
    h = HEAD_DIM // 2
    return jnp.concatenate([rope_1d(x[..., :h], row), rope_1d(x[..., h:], col)], axis=-1)


def attention_branch(zl, zc, sink, row, col, ctx_out):
    bsz, t_len = zl.shape[:2]
    c_len = zc.shape[1]
    nb = t_len // BLOCK
    scale = HEAD_DIM ** -0.5

    def split_heads(z):
        q, k, v = jnp.split(z, [ATT_Q, ATT_Q + ATT_KV], axis=-1)
        lead = z.shape[:2]
        return (q.reshape(lead + (ATT_KV_HEADS, ATT_GROUP, HEAD_DIM)),
                k.reshape(lead + (ATT_KV_HEADS, HEAD_DIM)),
                v.reshape(lead + (ATT_KV_HEADS, HEAD_DIM)))

    ql, kl, vl = split_heads(zl)
    qc, kc, vc = split_heads(zc)
    ql = axial_rope(ql, row, col)
    kl = axial_rope(kl, row, col)

    def band(t):
        tp = jnp.pad(t, ((0, 0), (BLOCK, BLOCK), (0, 0), (0, 0)))
        tp = tp.reshape(bsz, nb + 2, BLOCK, ATT_KV_HEADS, HEAD_DIM)
        return jnp.concatenate([tp[:, :-2], tp[:, 1:-1], tp[:, 2:]], axis=2)

    qb = ql.reshape(bsz, nb, BLOCK, ATT_KV_HEADS, ATT_GROUP, HEAD_DIM)
    kb, vb = band(kl), band(vl)
    s_loc = jnp.einsum('bnqhgd,bnkhd->bhgnqk', qb, kb).astype(F32) * scale
    q_off = jnp.arange(BLOCK)[:, None]
    k_off = jnp.arange(3 * BLOCK)[None, :]
    k_pos = jnp.arange(nb)[:, None, None] * BLOCK + k_off[None] - BLOCK
    mask = (jnp.abs(k_off - BLOCK - q_off) <= WINDOW)[None] & (k_pos >= 0) & (k_pos < t_len)
    s_loc = jnp.where(mask, s_loc, NEG_INF)
    s_ctx = jnp.einsum('bnqhgd,bchd->bhgnqc', qb, kc).astype(F32) * scale
    sink_f = sink.astype(F32).reshape(1, ATT_KV_HEADS, ATT_GROUP, 1, 1, 1)
    s_sink = jnp.broadcast_to(sink_f, s_loc.shape[:-1] + (1,))
    p = jax.nn.softmax(jnp.concatenate([s_loc, s_ctx, s_sink], axis=-1), axis=-1)
    p_loc = p[..., :3 * BLOCK].astype(vb.dtype)
    p_ctx = p[..., 3 * BLOCK:3 * BLOCK + c_len].astype(vc.dtype)
    ol = (jnp.einsum('bhgnqk,bnkhd->bnqhgd', p_loc, vb)
          + jnp.einsum('bhgnqc,bchd->bnqhgd', p_ctx, vc))
    yl = ol.reshape(bsz, t_len, ATT_Q)
    yc = None
    if ctx_out:
        sc = jnp.einsum('bchgd,bkhd->bhgck', qc, kc).astype(F32) * scale
        sc_sink = jnp.broadcast_to(sink_f[..., 0], sc.shape[:-1] + (1,))
        pc = jax.nn.softmax(jnp.concatenate([sc, sc_sink], axis=-1), axis=-1)[..., :c_len]
        yc = jnp.einsum('bhgck,bkhd->bchgd', pc.astype(vc.dtype), vc).reshape(bsz, c_len, ATT_Q)
    return yl, yc


def rwkv7_scan(state0, r, decay, k, v, kk, kka, reverse):
    def step(state, inp):
        r_t, w_t, k_t, v_t, kk_t, b_t = inp
        state = (state * w_t[:, :, None, :]
                 - jnp.einsum('bhvk,bhk->bhv', state, kk_t)[..., None] * b_t[:, :, None, :]
                 + v_t[..., None] * k_t[:, :, None, :])
        return state, jnp.einsum('bhvk,bhk->bhv', state, r_t)

    xs = tuple(jnp.swapaxes(t, 0, 1) for t in (r, decay, k, v, kk, kka))
    state, out = lax.scan(step, state0, xs, reverse=reverse)
    return state, jnp.swapaxes(out, 0, 1)


def rwkv_branch(zl, zc, mu, w0, w2, a0, a2, g2, k_k, k_a, r_k, gn_g, gn_b, ctx_out):
    cuts = [RWKV_W, 2 * RWKV_W, 3 * RWKV_W, 3 * RWKV_W + 2 * W_LORA,
            3 * RWKV_W + 2 * W_LORA + A_LORA]
    mu_f, w0_f, w2_f = mu.astype(F32), w0.astype(F32), w2.astype(F32)
    a0_f, a2_f, g2_f = a0.astype(F32), a2.astype(F32), g2.astype(F32)
    kk_f, ka_f, rk_f = k_k.astype(F32), k_a.astype(F32), r_k.astype(F32)
    gng_f, gnb_f = gn_g.astype(F32), gn_b.astype(F32)

    def prepare(z):
        bsz, t_len = z.shape[:2]
        zf = z.astype(F32)
        zf = zf + mu_f * (centred_shift(zf) - zf)
        r, k, v, w_lo, a_lo, g_lo = jnp.split(zf, cuts, axis=-1)
        heads = lambda t: t.reshape(bsz, t_len, RWKV_HEADS, HEAD_DIM)
        w_lo = w_lo.reshape(bsz, t_len, 2, W_LORA)
        decay = jnp.exp(-DECAY_SCALE * jax.nn.sigmoid(
            w0_f + jnp.einsum('btdr,drc->btdc', jnp.tanh(w_lo), w2_f)))
        a = jax.nn.sigmoid(a0_f + a_lo @ a2_f)
        g = jax.nn.sigmoid(g_lo) @ g2_f
        kk = heads(k * kk_f)
        kk = kk * lax.rsqrt(jnp.sum(jnp.square(kk), -1, keepdims=True) + 1e-12)
        k = k * (1.0 + (a - 1.0) * ka_f)
        return (heads(r), heads(decay[:, :, 0]), heads(decay[:, :, 1]), heads(k), heads(v),
                kk, kk * heads(a), g)

    def finish(o, r, k, v, g, dtype):
        bsz, t_len = o.shape[:2]
        mean = jnp.mean(o, -1, keepdims=True)
        var = jnp.mean(jnp.square(o - mean), -1, keepdims=True)
        on = ((o - mean) * lax.rsqrt(var + GN_EPS)).reshape(bsz, t_len, RWKV_W) * gng_f + gnb_f
        bonus = (jnp.sum(r * k * rk_f, -1, keepdims=True) * v).reshape(bsz, t_len, RWKV_W)
        return ((on + bonus) * g).astype(dtype)

    rc, dfc, dbc, kc, vc, kkc, bc, gc = prepare(zc)
    rl, dfl, dbl, kl, vl, kkl, bl, gl = prepare(zl)
    zero = jnp.zeros((zc.shape[0], RWKV_HEADS, HEAD_DIM, HEAD_DIM), F32)
    s_f, ocf = rwkv7_scan(zero, rc, dfc, kc, vc, kkc, bc, reverse=False)
    s_b, ocb = rwkv7_scan(zero, rc, dbc, kc, vc, kkc, bc, reverse=True)
    _, olf = rwkv7_scan(s_f, rl, dfl, kl, vl, kkl, bl, reverse=False)
    _, olb = rwkv7_scan(s_b, rl, dbl, kl, vl, kkl, bl, reverse=True)
    yl = finish(olf + olb, rl, kl, vl, gl, zl.dtype)
    yc = finish(ocf + ocb, rc, kc, vc, gc, zc.dtype) if ctx_out else None
    return yl, yc


def s5_discretise(a_re, a_im, log_step, b_re, b_im):
    a_re, a_im = a_re.astype(F32), a_im.astype(F32)
    dt = jnp.exp(log_step.astype(F32))[:, None]
    mag = jnp.exp(a_re * dt)
    lr, li = mag * jnp.cos(a_im * dt), mag * jnp.sin(a_im * dt)
    den = jnp.square(a_re) + jnp.square(a_im)
    cr = ((lr - 1.0) * a_re + li * a_im) / den
    ci = (li * a_re - (lr - 1.0) * a_im) / den
    br, bi = b_re.astype(F32), b_im.astype(F32)
    bbr = cr[..., None] * br - ci[..., None] * bi
    bbi = cr[..., None] * bi + ci[..., None] * br
    return lr, li, bbr, bbi


def s5_combine(e1, e2):
    a1r, a1i, b1r, b1i = e1
    a2r, a2i, b2r, b2i = e2
    return (a2r * a1r - a2i * a1i, a2r * a1i + a2i * a1r,
            a2r * b1r - a2i * b1i + b2r, a2r * b1i + a2i * b1r + b2i)


def s5_scan_fwd(lr, li, br, bi, h0r, h0i):
    br = br.at[:, 0].add(lr * h0r - li * h0i)
    bi = bi.at[:, 0].add(lr * h0i + li * h0r)
    shape = (1, br.shape[1]) + lr.shape
    ar = jnp.broadcast_to(lr, shape)
    ai = jnp.broadcast_to(li, shape)
    _, _, xr, xi = lax.associative_scan(s5_combine, (ar, ai, br, bi), axis=1)
    return xr, xi


def s5_scan_bwd(lr, li, br, bi, h0r, h0i):
    xr, xi = s5_scan_fwd(lr, li, jnp.flip(br, 1), jnp.flip(bi, 1), h0r, h0i)
    return jnp.flip(xr, 1), jnp.flip(xi, 1)


def s5_branch(ul, uc, a_re, a_im, log_step, b_re, b_im, c_re, c_im, d, glu_w, glu_b, ctx_out):
    fwd = s5_discretise(a_re[0], a_im[0], log_step[0], b_re, b_im)
    bwd = s5_discretise(a_re[1], a_im[1], log_step[1], b_re, b_im)
    cr, ci = c_re.astype(F32), c_im.astype(F32)

    def drive(u, disc):
        ug = u.astype(F32).reshape(u.shape[:2] + (S5_GROUPS, S5_GROUP_CH))
        return (jnp.einsum('btgc,gpc->btgp', ug, disc[2]),
                jnp.einsum('btgc,gpc->btgp', ug, disc[3]))

    def read(xr, xi):
        return jnp.einsum('btgp,gcp->btgc', xr, cr) - jnp.einsum('btgp,gcp->btgc', xi, ci)

    def finish(u, y):
        y = y.reshape(u.shape[:2] + (S5_W,)) + d.astype(F32) * u.astype(F32)
        y = jax.nn.gelu(y)
        return (y * jax.nn.sigmoid(y @ glu_w.astype(F32) + glu_b.astype(F32))).astype(u.dtype)

    zero = jnp.zeros((uc.shape[0], S5_GROUPS, S5_STATE), F32)
    xcf = s5_scan_fwd(fwd[0], fwd[1], *drive(uc, fwd), zero, zero)
    xcb = s5_scan_bwd(bwd[0], bwd[1], *drive(uc, bwd), zero, zero)
    ylf = read(*s5_scan_fwd(fwd[0], fwd[1], *drive(ul, fwd), xcf[0][:, -1], xcf[1][:, -1]))
    ylb = read(*s5_scan_bwd(bwd[0], bwd[1], *drive(ul, bwd), xcb[0][:, 0], xcb[1][:, 0]))
    yl = finish(ul, ylf + ylb)
    yc = finish(uc, read(*xcf) + read(*xcb)) if ctx_out else None
    return yl, yc


def gated_merge(z_gate, ya, yr, ys, branch_proj, w_out):
    ga, gr, gs = jnp.split(jax.nn.sigmoid(z_gate), N_BRANCH, axis=-1)
    m = ga * (ya @ branch_proj[0]) + gr * (yr @ branch_proj[1]) + gs * (ys @ branch_proj[2])
    return m @ w_out


def mixer_sublayer(ul, uc, row, col, ctx_out, w_in, attn_sink, rwkv_mu, rwkv_w0, rwkv_w2,
                   rwkv_a0, rwkv_a2, rwkv_g2, rwkv_k_k, rwkv_k_a, rwkv_r_k, rwkv_gn_g, rwkv_gn_b,
                   s5_a_re, s5_a_im, s5_log_step, s5_b_re, s5_b_im, s5_c_re, s5_c_im, s5_d,
                   s5_glu_w, s5_glu_b, branch_proj, w_out):
    cuts = [ATT_IN, ATT_IN + RWKV_IN, ATT_IN + RWKV_IN + S5_W]
    za_l, zr_l, zs_l, zg_l = jnp.split(ul @ w_in, cuts, axis=-1)
    za_c, zr_c, zs_c, zg_c = jnp.split(uc @ w_in, cuts, axis=-1)
    ya_l, ya_c = attention_branch(za_l, za_c, attn_sink, row, col, ctx_out)
    yr_l, yr_c = rwkv_branch(zr_l, zr_c, rwkv_mu, rwkv_w0, rwkv_w2, rwkv_a0, rwkv_a2, rwkv_g2,
                             rwkv_k_k, rwkv_k_a, rwkv_r_k, rwkv_gn_g, rwkv_gn_b, ctx_out)
    ys_l, ys_c = s5_branch(zs_l, zs_c, s5_a_re, s5_a_im, s5_log_step, s5_b_re, s5_b_im,
                           s5_c_re, s5_c_im, s5_d, s5_glu_w, s5_glu_b, ctx_out)
    out_l = gated_merge(zg_l, ya_l, yr_l, ys_l, branch_proj, w_out)
    out_c = gated_merge(zg_c, ya_c, yr_c, ys_c, branch_proj, w_out) if ctx_out else None
    return out_l, out_c


def setup_inputs(seed: int = 0) -> dict:
    key = jax.random.key(seed)
    keys = jax.random.split(key, 40)

    def nrm(i, shape, scale):
        return scale * jax.random.normal(keys[i], shape, F32)

    def uni(i, shape, lo, hi):
        return jax.random.uniform(keys[i], shape, F32, lo, hi)

    D = D_MODEL
    L = DEPTH
    n_idx = jnp.arange(S5_STATE, dtype=F32)
    s5_shape = (L, 2, S5_GROUPS, S5_STATE)
    return {
        'x': nrm(0, (BATCH, SEQ, D), 1.0),
        'c': nrm(1, (BATCH, D), 1.0),
        'ctx': nrm(2, (BATCH, CTX_LEN, D), 1.0),
        'c_ctx': nrm(3, (D,), 1.0),
        'w_ada': nrm(4, (L, D, N_MOD * D), 0.5 * D ** -0.5),
        'b_ada': nrm(5, (L, N_MOD * D), 0.02),
        'ln_g': 1.0 + nrm(6, (L, 3, D), 0.05),
        'ln_b': nrm(7, (L, 3, D), 0.02),
        'ffn_w_in': nrm(8, (L, 2, D, 2 * D_FF), D ** -0.5),
        'ffn_w_out': nrm(9, (L, 2, D_FF, D), BETA * D_FF ** -0.5),
        'w_in': nrm(10, (L, D, N_IN), D ** -0.5),
        'attn_sink': nrm(11, (L, ATT_HEADS), 0.5),
        'rwkv_mu': uni(12, (L, RWKV_IN), 0.0, 1.0),
        'rwkv_w0': uni(13, (L, 2, RWKV_W), -2.0, 2.0),
        'rwkv_w2': nrm(14, (L, 2, W_LORA, RWKV_W), 0.5 * W_LORA ** -0.5),
        'rwkv_a0': nrm(15, (L, RWKV_W), 0.1),
        'rwkv_a2': nrm(16, (L, A_LORA, RWKV_W), 0.5 * A_LORA ** -0.5),
        'rwkv_g2': nrm(17, (L, G_LORA, RWKV_W), G_LORA ** -0.5),
        'rwkv_k_k': 0.85 + nrm(18, (L, RWKV_W), 0.05),
        'rwkv_k_a': 1.0 + nrm(19, (L, RWKV_W), 0.05),
        'rwkv_r_k': nrm(20, (L, RWKV_HEADS, HEAD_DIM), 0.1),
        'rwkv_gn_g': 1.0 + nrm(21, (L, RWKV_W), 0.05),
        'rwkv_gn_b': nrm(22, (L, RWKV_W), 0.02),
        's5_a_re': -0.5 * jnp.exp(nrm(23, s5_shape, 0.05)),
        's5_a_im': math.pi * n_idx + nrm(24, s5_shape, 0.01),
        's5_log_step': uni(25, (L, 2, S5_GROUPS), math.log(1e-3), math.log(1e-1)),
        's5_b_re': nrm(26, (L, S5_GROUPS, S5_STATE, S5_GROUP_CH), (2.0 * S5_GROUP_CH) ** -0.5),
        's5_b_im': nrm(27, (L, S5_GROUPS, S5_STATE, S5_GROUP_CH), (2.0 * S5_GROUP_CH) ** -0.5),
        's5_c_re': nrm(28, (L, S5_GROUPS, S5_GROUP_CH, S5_STATE), S5_STATE ** -0.5),
        's5_c_im': nrm(29, (L, S5_GROUPS, S5_GROUP_CH, S5_STATE), S5_STATE ** -0.5),
        's5_d': nrm(30, (L, S5_W), 1.0),
        's5_glu_w': nrm(31, (L, S5_W, S5_W), S5_W ** -0.5),
        's5_glu_b': nrm(32, (L, S5_W), 0.02),
        'branch_proj': nrm(33, (L, N_BRANCH, BRANCH_W, D), BRANCH_W ** -0.5),
        'w_out': nrm(34, (L, D, D), BETA * D ** -0.5),
    }


def reference(x, c, ctx, c_ctx, w_ada, b_ada, ln_g, ln_b, ffn_w_in, ffn_w_out, w_in, attn_sink,
              rwkv_mu, rwkv_w0, rwkv_w2, rwkv_a0, rwkv_a2, rwkv_g2, rwkv_k_k, rwkv_k_a, rwkv_r_k,
              rwkv_gn_g, rwkv_gn_b, s5_a_re, s5_a_im, s5_log_step, s5_b_re, s5_b_im, s5_c_re,
              s5_c_im, s5_d, s5_glu_w, s5_glu_b, branch_proj, w_out):
    t_len = x.shape[1]
    rows = t_len // GRID_W
    row = jnp.repeat(jnp.arange(rows, dtype=jnp.int32), GRID_W)
    col = jnp.tile(jnp.arange(GRID_W, dtype=jnp.int32), rows)
    cond_l = jax.nn.silu(c)
    cond_c = jax.nn.silu(c_ctx)
    xl, xc = x, ctx
    for l in range(DEPTH):
        ctx_out = l < DEPTH - 1
        ml = jnp.split((cond_l @ w_ada[l] + b_ada[l])[:, None, :], N_MOD, axis=-1)
        mc = jnp.split(cond_c @ w_ada[l] + b_ada[l], N_MOD, axis=-1)
        xl = ffn_sublayer(xl, ml[0], ml[1], ml[2], ffn_w_in[l, 0], ffn_w_out[l, 0], ln_g[l, 0], ln_b[l, 0])
        xc = ffn_sublayer(xc, mc[0], mc[1], mc[2], ffn_w_in[l, 0], ffn_w_out[l, 0], ln_g[l, 0], ln_b[l, 0])
        yl, yc = mixer_sublayer(
            modulate(xl, ml[3], ml[4]), modulate(xc, mc[3], mc[4]), row, col, ctx_out,
            w_in[l], attn_sink[l], rwkv_mu[l], rwkv_w0[l], rwkv_w2[l], rwkv_a0[l], rwkv_a2[l],
            rwkv_g2[l], rwkv_k_k[l], rwkv_k_a[l], rwkv_r_k[l], rwkv_gn_g[l], rwkv_gn_b[l],
            s5_a_re[l], s5_a_im[l], s5_log_step[l], s5_b_re[l], s5_b_im[l], s5_c_re[l],
            s5_c_im[l], s5_d[l], s5_glu_w[l], s5_glu_b[l], branch_proj[l], w_out[l])
        xl = ln_affine(ALPHA * xl + ml[5] * yl, ln_g[l, 1], ln_b[l, 1])
        xl = ffn_sublayer(xl, ml[6], ml[7], ml[8], ffn_w_in[l, 1], ffn_w_out[l, 1], ln_g[l, 2], ln_b[l, 2])
        if ctx_out:
            xc = ln_affine(ALPHA * xc + mc[5] * yc, ln_g[l, 1], ln_b[l, 1])
            xc = ffn_sublayer(xc, mc[6], mc[7], mc[8], ffn_w_in[l, 1], ffn_w_out[l, 1], ln_g[l, 2], ln_b[l, 2])
    return xl
```

```python
import contextlib
import numpy as np
import concourse.bass as bass
import concourse.mybir as mybir
from concourse.bass_utils import run_bass_kernel_spmd

F32 = mybir.dt.float32
BF16 = mybir.dt.bfloat16
AF = mybir.ActivationFunctionType
ALU = mybir.AluOpType
AX = mybir.AxisListType

D = 1024
NCH = 8
CTX = 256
DFF = 2816
NFF = 22
DEPTH = 2
ALPHA = (2.0 * DEPTH) ** 0.25
LN_EPS = 1e-6
DECAY_SCALE = 0.606531
GN_EPS = 64e-5
N_IN = 6208


class Sem:
    _n = 0

    def __init__(self, h):
        self.h = h
        Sem._n += 1
        self.uid = Sem._n


class Buf:
    def __init__(self, name, t):
        self.name = name
        self.t = t
        self.dsem = None
        self.dcnt = 0

    def __getitem__(self, k):
        return self.t[k]

    def __repr__(self):
        return "Buf(%s)" % self.name


class KB:
    EPOCH = 20000

    def __init__(self, nc):
        self.nc = nc
        self.es = contextlib.ExitStack()
        self.eng = {"pe": nc.tensor, "dve": nc.vector, "act": nc.scalar, "pool": nc.gpsimd, "sp": nc.sync}
        self.esem = {}
        self.ecnt = {}
        for e in self.eng:
            self.esem[e] = Sem(self.es.enter_context(nc.semaphore("c_%s_0" % e)))
            self.ecnt[e] = 0
        self.eepoch = {e: 0 for e in self.eng}
        self.seen = {e: {} for e in self.eng}
        self.lastw = {}
        self.reads = {}
        self.nbuf = 0
        self.ninstr = 0
        self.nwait = 0
        self.all_events = {}
        self.free_dsems = []
        self.phase_bufs = []
        self.ndsem = 0

    def sb(self, name, shape, dtype, stack=None):
        self.nbuf += 1
        t = (stack or self.es).enter_context(self.nc.sbuf_tensor("%s_%d" % (name, self.nbuf), list(shape), dtype))
        b = Buf(name, t)
        if stack is not None:
            self.phase_bufs.append(b)
        return b

    def ps(self, name, shape, dtype=F32, stack=None):
        self.nbuf += 1
        t = (stack or self.es).enter_context(self.nc.psum_tensor("%s_%d" % (name, self.nbuf), list(shape), dtype))
        return Buf(name, t)

    def _dsem(self, b):
        if b.dsem is None:
            if self.free_dsems:
                b.dsem, b.dcnt = self.free_dsems.pop()
            else:
                self.ndsem += 1
                b.dsem = Sem(self.es.enter_context(self.nc.semaphore("d_%d" % self.ndsem)))
                b.dcnt = 0
        return b.dsem

    @contextlib.contextmanager
    def phase(self):
        st = contextlib.ExitStack()
        self.phase_bufs = []
        try:
            yield st
        finally:
            self.barrier()
            for b in self.phase_bufs:
                if b.dsem is not None:
                    self.free_dsems.append((b.dsem, b.dcnt))
                    b.dsem = None
            self.phase_bufs = []
            st.close()

    def _need(self, e, r, w):
        need = {}

        def add(evs):
            for uid, (s, v) in evs.items():
                if uid not in need or need[uid][1] < v:
                    need[uid] = (s, v)
        for k in r:
            add(self.lastw.get(k, {}))
        for k in w:
            add(self.lastw.get(k, {}))
            add(self.reads.get(k, {}))
        return need

    def _wait(self, e, need, own_ok):
        eng = self.eng[e]
        for uid, (s, v) in need.items():
            if own_ok and uid == self.esem[e].uid:
                continue
            if self.seen[e].get(uid, 0) >= v:
                continue
            eng.wait_ge(s.h, v)
            self.nwait += 1
            self.seen[e][uid] = v

    def _record(self, ev, r, w):
        uid = ev[0].uid
        for k in r:
            self.reads.setdefault(k, {})[uid] = ev
        for k in w:
            self.lastw.setdefault(k, {})[uid] = ev
            self.reads[k] = {}
        self.all_events[uid] = ev

    def _bump(self, e):
        if self.ecnt[e] >= self.EPOCH:
            self.eepoch[e] += 1
            self.esem[e] = Sem(self.es.enter_context(self.nc.semaphore("c_%s_%d" % (e, self.eepoch[e]))))
            self.ecnt[e] = 0
        self.ecnt[e] += 1
        return (self.esem[e], self.ecnt[e])

    def op(self, e, fn, r=(), w=(), same_ok=False):
        need = self._need(e, r, w)
        self._wait(e, need, own_ok=(e == "pe" or same_ok))
        ins = fn(self.eng[e])
        ev = self._bump(e)
        ins.then_inc(ev[0].h, 1)
        self._record(ev, r, w)
        self.ninstr += 1
        return ins

    def dma(self, q, out, in_, sbuf, r=(), w=(), **kw):
        need = self._need(q, r, w)
        self._wait(q, need, own_ok=False)
        s = self._dsem(sbuf)
        ins = self.eng[q].dma_start(out=out, in_=in_, **kw)
        sbuf.dcnt += 16
        ev = (s, sbuf.dcnt)
        ins.then_inc(s.h, 16)
        self._record(ev, r, w)
        self.ninstr += 1
        return ins

    def barrier(self):
        for e in self.eng:
            self._wait(e, dict(self.all_events), own_ok=True)
        self.lastw = {}
        self.reads = {}
        self.all_events = {}

    def finish(self, e="sp"):
        self._wait(e, dict(self.all_events), own_ok=True)


class Pool:
    def __init__(self, kb, name, n, shape, dtype, psum=False, stack=None):
        self.bufs = [(kb.ps if psum else kb.sb)("%s%d" % (name, i), shape, dtype, stack=stack) for i in range(n)]
        self.i = 0

    def next(self):
        b = self.bufs[self.i % len(self.bufs)]
        self.i += 1
        return b


class Model:
    def __init__(self, T, dbg=(), nlayers=DEPTH, stop_after=None):
        self.T = T
        self.NT = CTX + T
        self.dbg = set(dbg)
        self.nlayers = nlayers
        self.stop_after = stop_after
        nc = bass.Bass("TRN2", target_bir_lowering=False)
        self.nc = nc
        self.kb = KB(nc)
        self.dram_in = {}
        self.dram_out = {}
        self.scr_n = 0

    def din(self, name, shape, dtype=F32):
        t = self.nc.dram_tensor(name, list(shape), dtype, kind="ExternalInput").ap()
        self.dram_in[name] = t
        return t

    def dout(self, name, shape, dtype=F32):
        t = self.nc.dram_tensor(name, list(shape), dtype, kind="ExternalOutput").ap()
        self.dram_out[name] = t
        return t

    def scratch(self, name, shape, dtype=F32):
        if name in self.dbg:
            return self.dout(name, shape, dtype)
        return self.nc.dram_tensor(name, list(shape), dtype, kind="Internal").ap()

    def tiles(self, W, ctx=True, lat=True):
        out = []
        if ctx:
            for t0 in range(0, CTX, W):
                out.append((1, t0, W))
        if lat:
            for t0 in range(0, self.T, W):
                out.append((0, CTX + t0, W))
        return out

    def declare_inputs(self):
        L = DEPTH
        self.xT = self.din("xT", [D, self.NT])
        self.condT = self.din("condT", [128, NCH, 2])
        self.w_ada = self.din("w_ada", [L, D, 9 * D])
        self.b_ada = self.din("b_ada", [L, 128, 72])
        self.ln_g = self.din("ln_g", [128, L * 3 * NCH])
        self.ln_b = self.din("ln_b", [128, L * 3 * NCH])
        self.ffn_w_in = self.din("ffn_w_in", [L, 2, D, 2 * DFF])
        self.ffn_w_out = self.din("ffn_w_out", [L, 2, DFF, D])

    def setup_consts(self):
        kb = self.kb
        self.onesb = kb.sb("onesb", [128, 128], BF16)
        kb.op("dve", lambda e: e.memset(self.onesb[:], 1.0 / D), w=[self.onesb])
        self.lng = kb.sb("lng", [128, DEPTH * 3 * NCH], F32)
        self.lnb = kb.sb("lnb", [128, DEPTH * 3 * NCH], F32)
        kb.dma("sp", self.lng[:], self.ln_g[:, :], self.lng, w=[self.lng])
        kb.dma("sp", self.lnb[:], self.ln_b[:, :], self.lnb, w=[self.lnb])
        self.scond = kb.sb("scond", [128, NCH, 2], F32)
        kb.dma("sp", self.scond[:], self.condT[:, :, :], self.scond, w=[self.scond])
        kb.op("act", lambda e: e.activation(out=self.scond[:], in_=self.scond[:], func=AF.Silu),
              r=[self.scond], w=[self.scond])
        self.mods = kb.sb("mods", [128, 72, 2], F32)
        self.modp1 = kb.sb("modp1", [128, 72, 2], F32)
        self.modh = kb.sb("modh", [128, 72, 2], F32)

    def adaln_phase(self, l):
        kb = self.kb
        with kb.phase() as st:
            self.psum = Pool(kb, "ps", 8, [128, 512], F32, psum=True, stack=st)
            wa = Pool(kb, "wa", 2, [128, NCH, D], F32, stack=st)
            bada = kb.sb("bada", [128, 72], F32, st)
            kb.dma("sp", bada[:], self.b_ada[l, :, :], bada, w=[bada])
            P = self.psum.next()
            for m in range(9):
                w = wa.next()
                for kc in range(NCH):
                    kb.dma("sp" if kc % 2 == 0 else "act", w[:, kc, :],
                           self.w_ada[l, kc * 128:(kc + 1) * 128, m * D:(m + 1) * D], w, w=[w])
                for oc in range(NCH):
                    j = m * NCH + oc
                    for kc in range(NCH):
                        kb.op("pe", lambda e: e.matmul(P[:, 2 * j:2 * j + 2], lhsT=w[:, kc, oc * 128:(oc + 1) * 128],
                                                       rhs=self.scond[:, kc, :], start=(kc == 0), stop=(kc == NCH - 1)),
                              r=[w, self.scond], w=[P])
            kb.op("dve", lambda e: e.tensor_tensor(out=self.mods[:], in0=P[:, 0:144].rearrange("p (j s) -> p j s", s=2),
                                                   in1=bada[:, :].unsqueeze(2).broadcast_to([128, 72, 2]), op=ALU.add),
                  r=[P, bada], w=[self.mods])
            kb.op("dve", lambda e: e.tensor_scalar(out=self.modp1[:], in0=self.mods[:], scalar1=1.0, scalar2=None,
                                                   op0=ALU.add), r=[self.mods], w=[self.modp1])
            kb.op("dve", lambda e: e.tensor_scalar(out=self.modh[:], in0=self.mods[:], scalar1=0.5, scalar2=None,
                                                   op0=ALU.mult), r=[self.mods], w=[self.modh])

    def ln_stats(self, x, W, P, pl):
        kb = self.kb
        xb, sq = pl["xb"].next(), pl["sq"].next()
        kb.op("act", lambda e: e.activation(out=xb[:, :, :W], in_=x[:, :, :W], func=AF.Copy), r=[x], w=[xb])
        kb.op("act", lambda e: e.activation(out=sq[:, :, :W], in_=x[:, :, :W], func=AF.Square), r=[x], w=[sq])
        for c in range(NCH):
            kb.op("pe", lambda e: e.matmul(P[:, 0:W], lhsT=self.onesb[:, :], rhs=xb[:, c, :W],
                                           start=(c == 0), stop=(c == NCH - 1)), r=[xb, self.onesb], w=[P])
        for c in range(NCH):
            kb.op("pe", lambda e: e.matmul(P[:, W:2 * W], lhsT=self.onesb[:, :], rhs=sq[:, c, :W],
                                           start=(c == 0), stop=(c == NCH - 1)), r=[sq, self.onesb], w=[P])
        m2, var, rstd, nmr = pl["m2"].next(), pl["var"].next(), pl["rstd"].next(), pl["nmr"].next()
        kb.op("act", lambda e: e.activation(out=m2[:, 0, :W], in_=P[:, 0:W], func=AF.Square), r=[P], w=[m2])
        kb.op("dve", lambda e: e.scalar_tensor_tensor(out=var[:, 0, :W], in0=P[:, W:2 * W], scalar=LN_EPS,
                                                      in1=m2[:, 0, :W], op0=ALU.add, op1=ALU.subtract),
              r=[P, m2], w=[var])
        kb.op("act", lambda e: e.activation(out=var[:, 0, :W], in_=var[:, 0, :W], func=AF.Sqrt), r=[var], w=[var])
        kb.op("dve", lambda e: e.reciprocal(out=rstd[:, 0, :W], in_=var[:, 0, :W]), r=[var], w=[rstd])
        kb.op("dve", lambda e: e.scalar_tensor_tensor(out=nmr[:, 0, :W], in0=P[:, 0:W], scalar=-1.0,
                                                      in1=rstd[:, 0, :W], op0=ALU.mult, op1=ALU.mult),
              r=[P, rstd], w=[nmr])
        return rstd, nmr

    def normalize(self, out, x, W, rstd, nmr):
        kb = self.kb
        kb.op("dve", lambda e: e.tensor_tensor(out=out[:, :, :W], in0=x[:, :, :W],
                                               in1=rstd[:, 0:1, :W].broadcast_to([128, NCH, W]), op=ALU.mult),
              r=[x, rstd], w=[out])
        kb.op("pool", lambda e: e.tensor_tensor(out=out[:, :, :W], in0=out[:, :, :W],
                                                in1=nmr[:, 0:1, :W].broadcast_to([128, NCH, W]), op=ALU.add),
              r=[out, nmr], w=[out])

    def stat_pools(self, W, st):
        kb = self.kb
        return {
            "xb": Pool(kb, "xb", 1, [128, NCH, W], BF16, stack=st),
            "sq": Pool(kb, "sq", 1, [128, NCH, W], BF16, stack=st),
            "m2": Pool(kb, "m2", 2, [128, 1, W], F32, stack=st),
            "var": Pool(kb, "var", 2, [128, 1, W], F32, stack=st),
            "rstd": Pool(kb, "rstd", 2, [128, 1, W], F32, stack=st),
            "nmr": Pool(kb, "nmr", 2, [128, 1, W], F32, stack=st),
        }

    def ffn_phase(self, l, s, src, dst, dst_off, ctx):
        kb = self.kb
        W = 256
        mb = 0 if s == 0 else 6
        lni = (l * 3 + (0 if s == 0 else 2)) * NCH
        with kb.phase() as st:
            self.psum = Pool(kb, "ps", 8, [128, 512], F32, psum=True, stack=st)
            w1 = kb.sb("w1", [128, NCH, 2 * DFF], BF16, st)
            w2 = kb.sb("w2", [128, NFF, D], BF16, st)
            for c in range(NCH):
                for hh in range(2):
                    kb.dma("pool", w1[:, c, hh * DFF:(hh + 1) * DFF],
                           self.ffn_w_in[l, s, c * 128:(c + 1) * 128, hh * DFF:(hh + 1) * DFF], w1, w=[w1])
            for c in range(NFF):
                kb.dma("pool", w2[:, c, :], self.ffn_w_out[l, s, c * 128:(c + 1) * 128, :], w2, w=[w2])
            pl = self.stat_pools(W, st)
            xp = Pool(kb, "x", 2, [128, NCH, W], F32, stack=st)
            xnp = Pool(kb, "xn", 2, [128, NCH, W], F32, stack=st)
            up = Pool(kb, "u", 1, [128, NCH, W], BF16, stack=st)
            hp = Pool(kb, "h", 1, [128, NFF, W], BF16, stack=st)
            gp = Pool(kb, "g", 2, [128, W], F32, stack=st)
            srcv = src.rearrange("(c p) t -> p c t", p=128)
            dstv = dst.rearrange("(c p) t -> p c t", p=128)
            for (seg, t0, _) in self.tiles(W, ctx=ctx):
                x = xp.next()
                kb.dma("sp", x[:, :, :], srcv[:, :, t0:t0 + W], x, r=[("dram", id(src))], w=[x])
                P = self.psum.next()
                rstd, nmr = self.ln_stats(x, W, P, pl)
                xn = xnp.next()
                self.normalize(xn, x, W, rstd, nmr)
                u = up.next()
                for c in range(NCH):
                    kb.op("act", lambda e: e.activation(out=u[:, c, :], in_=xn[:, c, :], func=AF.Identity,
                                                        scale=self.modp1[:, (mb + 1) * NCH + c, seg:seg + 1],
                                                        bias=self.mods[:, mb * NCH + c, seg:seg + 1]),
                          r=[xn, self.modp1, self.mods], w=[u])
                kb.op("pool", lambda e: e.tensor_scalar(out=x[:, :, :], in0=x[:, :, :], scalar1=ALPHA, scalar2=None,
                                                        op0=ALU.mult), r=[x], w=[x])
                h = hp.next()
                for j in range(NFF):
                    P = self.psum.next()
                    for c in range(NCH):
                        kb.op("pe", lambda e: e.matmul(P[:, 0:W], lhsT=w1[:, c, j * 128:(j + 1) * 128], rhs=u[:, c, :],
                                                       start=(c == 0), stop=(c == NCH - 1)), r=[w1, u], w=[P])
                    for c in range(NCH):
                        kb.op("pe", lambda e: e.matmul(P[:, W:2 * W], lhsT=w1[:, c, DFF + j * 128:DFF + (j + 1) * 128],
                                                       rhs=u[:, c, :], start=(c == 0), stop=(c == NCH - 1)),
                              r=[w1, u], w=[P])
                    g = gp.next()
                    kb.op("act", lambda e: e.activation(out=g[:, :], in_=P[:, 0:W], func=AF.Silu), r=[P], w=[g])
                    kb.op("dve", lambda e: e.tensor_tensor(out=h[:, j, :], in0=P[:, W:2 * W], in1=g[:, :], op=ALU.mult),
                          r=[P, g], w=[h])
                for oc in range(NCH):
                    if oc % 2 == 0:
                        P = self.psum.next()
                    o0 = (oc % 2) * W
                    for j in range(NFF):
                        kb.op("pe", lambda e: e.matmul(P[:, o0:o0 + W], lhsT=w2[:, j, oc * 128:(oc + 1) * 128],
                                                       rhs=h[:, j, :], start=(j == 0), stop=(j == NFF - 1)),
                              r=[w2, h], w=[P])
                    kb.op("dve", lambda e: e.scalar_tensor_tensor(
                        out=x[:, oc, :], in0=P[:, o0:o0 + W], scalar=self.modh[:, (mb + 2) * NCH + oc, seg:seg + 1],
                        in1=x[:, oc, :], op0=ALU.mult, op1=ALU.add), r=[P, x, self.modh], w=[x])
                P = self.psum.next()
                rstd, nmr = self.ln_stats(x, W, P, pl)
                self.normalize(xn, x, W, rstd, nmr)
                for c in range(NCH):
                    kb.op("act", lambda e: e.activation(out=xn[:, c, :], in_=xn[:, c, :], func=AF.Identity,
                                                        scale=self.lng[:, lni + c:lni + c + 1],
                                                        bias=self.lnb[:, lni + c:lni + c + 1]),
                          r=[xn, self.lng, self.lnb], w=[xn])
                kb.dma("sp", dstv[:, :, t0 - dst_off:t0 - dst_off + W], xn[:, :, :], xn,
                       r=[xn], w=[("dram", id(dst))])


def host_inputs(inp, b, T):
    L = DEPTH
    d = {}
    d["xT"] = np.ascontiguousarray(np.concatenate([inp["ctx"][b], inp["x"][b, :T]], 0).T)
    cond = np.stack([inp["c"][b], inp["c_ctx"]], -1)
    d["condT"] = np.ascontiguousarray(cond.reshape(NCH, 128, 2).transpose(1, 0, 2))
    d["w_ada"] = inp["w_ada"]
    d["b_ada"] = np.ascontiguousarray(inp["b_ada"].reshape(L, 72, 128).transpose(0, 2, 1))
    d["ln_g"] = np.ascontiguousarray(inp["ln_g"].reshape(L * 3 * NCH, 128).T)
    d["ln_b"] = np.ascontiguousarray(inp["ln_b"].reshape(L * 3 * NCH, 128).T)
    d["ffn_w_in"] = inp["ffn_w_in"]
    d["ffn_w_out"] = inp["ffn_w_out"]
    return d


NWA = 10 + 1 + 4 + 24
CH_Q, CH_QS, CH_K, CH_KS, CH_KB, CH_KBS, CH_V, CH_S5, CH_G = 0, 4, 8, 9, 10, 11, 12, 13, 17
NCH_A = 41


def _mixA_declare(self):
    L = DEPTH
    self.w_inA = self.din("w_inA", [L, D, NCH_A * 128])
    self.ropeC = self.din("ropeC", [128, self.NT])
    self.ropeS = self.din("ropeS", [128, self.NT])
    self.sinkT = self.din("sinkT", [L, 64, 8])
    self.maskP = self.din("maskP", [128, 512])
    self.maskN = self.din("maskN", [128, 512])


def _mixA_phase(self, l, src, qS, kS, vS, usS, gS):
    kb = self.kb
    W = 256
    with kb.phase() as st:
        self.psum = Pool(kb, "ps", 8, [128, 512], F32, psum=True, stack=st)
        w = kb.sb("wA", [128, NCH, NCH_A * 128], BF16, st)
        for c in range(NCH):
            for hh in range(2):
                n0, n1 = (0, 21 * 128) if hh == 0 else (21 * 128, NCH_A * 128)
                kb.dma("pool", w[:, c, n0:n1], self.w_inA[l, c * 128:(c + 1) * 128, n0:n1], w, w=[w])
        pl = self.stat_pools(W, st)
        xp = Pool(kb, "x", 2, [128, NCH, W], F32, stack=st)
        xnp = Pool(kb, "xn", 1, [128, NCH, W], F32, stack=st)
        up = Pool(kb, "u", 2, [128, NCH, W], BF16, stack=st)
        cp = Pool(kb, "rc", 2, [128, W], F32, stack=st)
        sp_ = Pool(kb, "rs", 2, [128, W], F32, stack=st)
        t1p = Pool(kb, "t1", 2, [128, W], F32, stack=st)
        t2p = Pool(kb, "t2", 2, [128, W], F32, stack=st)
        qp = Pool(kb, "qo", 2, [128, 4, W], BF16, stack=st)
        kp = Pool(kb, "ko", 2, [128, 2, W], BF16, stack=st)
        vp = Pool(kb, "vo", 2, [128, 2, 128], BF16, stack=st)
        usp = Pool(kb, "uso", 2, [128, 4, W], F32, stack=st)
        gp = Pool(kb, "go", 2, [128, 24, W], BF16, stack=st)
        srcv = src.rearrange("(c p) t -> p c t", p=128)

        def proj(P, o0, ch, u):
            for c in range(NCH):
                kb.op("pe", lambda e: e.matmul(P[:, o0:o0 + W], lhsT=w[:, c, ch * 128:(ch + 1) * 128], rhs=u[:, c, :],
                                               start=(c == 0), stop=(c == NCH - 1)), r=[w, u], w=[P])

        for (seg, t0, _) in self.tiles(W):
            x = xp.next()
            kb.dma("sp", x[:, :, :], srcv[:, :, t0:t0 + W], x, r=[("dram", id(src))], w=[x])
            cT, sT = cp.next(), sp_.next()
            kb.dma("sp", cT[:, :], self.ropeC[:, t0:t0 + W], cT, w=[cT])
            kb.dma("sp", sT[:, :], self.ropeS[:, t0:t0 + W], sT, w=[sT])
            P = self.psum.next()
            rstd, nmr = self.ln_stats(x, W, P, pl)
            xn = xnp.next()
            self.normalize(xn, x, W, rstd, nmr)
            u = up.next()
            for c in range(NCH):
                kb.op("act", lambda e: e.activation(out=u[:, c, :], in_=xn[:, c, :], func=AF.Identity,
                                                    scale=self.modp1[:, 4 * NCH + c, seg:seg + 1],
                                                    bias=self.mods[:, 3 * NCH + c, seg:seg + 1]),
                      r=[xn, self.modp1, self.mods], w=[u])
            qo, ko = qp.next(), kp.next()

            def rope(dst_ap, dst, ch, chs):
                P = self.psum.next()
                proj(P, 0, ch, u)
                proj(P, W, chs, u)
                t1, t2 = t1p.next(), t2p.next()
                kb.op("dve", lambda e: e.tensor_tensor(out=t1[:, :], in0=P[:, 0:W], in1=cT[:, :], op=ALU.mult),
                      r=[P, cT], w=[t1])
                kb.op("dve", lambda e: e.tensor_tensor(out=t2[:, :], in0=P[:, W:2 * W], in1=sT[:, :], op=ALU.mult),
                      r=[P, sT], w=[t2])
                kb.op("pool", lambda e: e.tensor_tensor(out=dst_ap, in0=t1[:, :], in1=t2[:, :], op=ALU.add),
                      r=[t1, t2], w=[dst])
            for c in range(4):
                rope(qo[:, c, :], qo, CH_Q + c, CH_QS + c)
            rope(ko[:, 0, :], ko, CH_K, CH_KS)
            rope(ko[:, 1, :], ko, CH_KB, CH_KBS)
            kb.dma("sp", qS.rearrange("(c p) t -> p c t", p=128)[:, :, t0:t0 + W], qo[:, :, :], qo, r=[qo],
                   w=[("dram", id(qS))])
            kb.dma("sp", kS.rearrange("(c p) t -> p c t", p=128)[:, :, t0:t0 + W], ko[:, :, :], ko, r=[ko],
                   w=[("dram", id(kS))])
            vo = vp.next()
            P = self.psum.next()
            for tb in range(W // 128):
                for c in range(NCH):
                    kb.op("pe", lambda e: e.matmul(P[:, tb * 128:(tb + 1) * 128], lhsT=u[:, c, tb * 128:(tb + 1) * 128],
                                                   rhs=w[:, c, CH_V * 128:(CH_V + 1) * 128],
                                                   start=(c == 0), stop=(c == NCH - 1)), r=[w, u], w=[P])
            kb.op("act", lambda e: e.activation(out=vo[:, :, :], in_=P[:, 0:W].rearrange("p (b n) -> p b n", b=2),
                                                func=AF.Copy), r=[P], w=[vo])
            kb.dma("sp", vS[t0:t0 + W, :].rearrange("(b p) n -> p b n", p=128), vo[:, :, :], vo, r=[vo],
                   w=[("dram", id(vS))])
            uso = usp.next()
            for c in range(4):
                if c % 2 == 0:
                    P = self.psum.next()
                o0 = (c % 2) * W
                proj(P, o0, CH_S5 + c, u)
                kb.op("act", lambda e: e.activation(out=uso[:, c, :], in_=P[:, o0:o0 + W], func=AF.Copy),
                      r=[P], w=[uso])
            kb.dma("sp", usS.rearrange("(c p) t -> p c t", p=128)[:, :, t0:t0 + W], uso[:, :, :], uso, r=[uso],
                   w=[("dram", id(usS))])
            go = gp.next()
            for c in range(24):
                if c % 2 == 0:
                    P = self.psum.next()
                o0 = (c % 2) * W
                proj(P, o0, CH_G + c, u)
                kb.op("act", lambda e: e.activation(out=go[:, c, :], in_=P[:, o0:o0 + W], func=AF.Sigmoid),
                      r=[P], w=[go])
            kb.dma("sp", gS.rearrange("(c p) t -> p c t", p=128)[:, :, t0:t0 + W], go[:, :, :], go, r=[go],
                   w=[("dram", id(gS))])


def _attn_phase(self, l, qS, kS, vS, yaS):
    kb = self.kb
    NB = self.T // 128
    with kb.phase() as st:
        self.psum = Pool(kb, "ps", 8, [128, 512], F32, psum=True, stack=st)
        mP = kb.sb("mP", [128, 512], BF16, st)
        mN = kb.sb("mN", [128, 512], BF16, st)
        kb.dma("pool", mP[:, :], self.maskP[:, :], mP, w=[mP])
        kb.dma("pool", mN[:, :], self.maskN[:, :], mN, w=[mN])
        ones = kb.sb("ones64", [128, 64], BF16, st)
        kb.op("dve", lambda e: e.memset(ones[:], 1.0), w=[ones])
        esk = kb.sb("esk", [64, 8], F32, st)
        kb.dma("sp", esk[:, :], self.sinkT[l, :, :], esk, w=[esk])
        kb.op("act", lambda e: e.activation(out=esk[:, :], in_=esk[:, :], func=AF.Exp), r=[esk], w=[esk])
        eskb = kb.sb("eskb", [64, 8, 128], F32, st)
        kb.op("dve", lambda e: e.tensor_copy(out=eskb[:, :, :], in_=esk[:, :].unsqueeze(2).broadcast_to([64, 8, 128])),
              r=[esk], w=[eskb])
        kc = kb.sb("kc", [128, 2, CTX], BF16, st)
        kb.dma("sp", kc[:, :, :], kS.rearrange("(c p) t -> p c t", p=128)[:, :, 0:CTX], kc, r=[("dram", id(kS))], w=[kc])
        vc = kb.sb("vc", [128, 2, 128], BF16, st)
        kb.dma("sp", vc[:, :, :], vS[0:CTX, :].rearrange("(b p) n -> p b n", p=128), vc, r=[("dram", id(vS))], w=[vc])
        qp = Pool(kb, "aq", 2, [128, 4, 128], BF16, stack=st)
        kwp = Pool(kb, "akw", 2, [128, 2, 384], BF16, stack=st)
        vwp = Pool(kb, "avw", 2, [128, 3, 128], BF16, stack=st)
        pp = Pool(kb, "ap", 4, [128, 512], BF16, stack=st)
        dp = Pool(kb, "ad", 2, [64, 512], F32, stack=st)
        op_ = Pool(kb, "ao", 2, [64, 8, 128], BF16, stack=st)
        qv = qS.rearrange("(c p) t -> p c t", p=128)
        kv = kS.rearrange("(c p) t -> p c t", p=128)
        blocks = [(1, b) for b in range(CTX // 128)] + [(0, b) for b in range(NB)]
        self._acc_i = 0
        self._sc_i = 0
        for (seg, b) in blocks:
            t0 = b * 128 if seg == 1 else CTX + b * 128
            q = qp.next()
            kb.dma("sp", q[:, :, :], qv[:, :, t0:t0 + 128], q, r=[("dram", id(qS))], w=[q])
            keyblocks = []
            if seg == 0:
                lo = max(b - 1, 0)
                hi = min(b + 1, NB - 1)
                nb_ = hi - lo + 1
                kw, vw = kwp.next(), vwp.next()
                kb.dma("sp", kw[:, :, 0:nb_ * 128], kv[:, :, CTX + lo * 128:CTX + (hi + 1) * 128], kw,
                       r=[("dram", id(kS))], w=[kw])
                kb.dma("sp", vw[:, 0:nb_, :],
                       vS[CTX + lo * 128:CTX + (hi + 1) * 128, :].rearrange("(b p) n -> p b n", p=128), vw,
                       r=[("dram", id(vS))], w=[vw])
                for bb in range(lo, hi + 1):
                    i = bb - lo
                    mask = mP if bb < b else (mN if bb > b else None)
                    keyblocks.append((kw, i * 128, vw, i, mask))
            for i in range(CTX // 128):
                keyblocks.append((kc, i * 128, vc, i, None))
            oo = op_.next()
            accb = self.psum.bufs[0:4]
            scb = self.psum.bufs[4:8]
            for kvh in range(2):
                Pn = accb[(self._acc_i) % 4]
                Pd = accb[(self._acc_i + 1) % 4]
                self._acc_i += 2
                for bi, (kbuf, koff, vbuf, vi, mask) in enumerate(keyblocks):
                    Ps = [scb[self._sc_i % 4], scb[(self._sc_i + 1) % 4]]
                    self._sc_i += 2
                    pt = pp.next()
                    for par in range(2):
                        base = par * 64
                        var = 0 if (kvh * 64 == base) else 1
                        for j in range(2):
                            h = kvh * 4 + 2 * j + par
                            kb.op("pe", lambda e: e.matmul(Ps[par][:, j * 128:(j + 1) * 128],
                                                           lhsT=kbuf[base:base + 64, var, koff:koff + 128],
                                                           rhs=q[base:base + 64, h // 2, :], start=True, stop=True),
                                  r=[kbuf, q], w=[Ps[par]])
                        kb.op("act", lambda e: e.activation(out=pt[:, par * 256:(par + 1) * 256], in_=Ps[par][:, 0:256],
                                                            func=AF.Exp, scale=0.125), r=[Ps[par]], w=[pt])
                    if mask is not None:
                        kb.op("pool", lambda e: e.tensor_tensor(out=pt[:, :], in0=pt[:, :], in1=mask[:, :], op=ALU.mult),
                              r=[pt, mask], w=[pt])
                    first, last = bi == 0, bi == len(keyblocks) - 1
                    kb.op("pe", lambda e: e.matmul(Pn[0:64, :], lhsT=vbuf[:, vi, kvh * 64:(kvh + 1) * 64], rhs=pt[:, :],
                                                   start=first, stop=last), r=[vbuf, pt], w=[Pn])
                    kb.op("pe", lambda e: e.matmul(Pd[0:64, :], lhsT=ones[:, :], rhs=pt[:, :],
                                                   start=first, stop=last), r=[ones, pt], w=[Pd])
                den = dp.next()
                kb.op("dve", lambda e: e.tensor_tensor(
                    out=den[:, :].rearrange("p (r j q) -> p r j q", r=2, j=2), in0=Pd[0:64, :].rearrange("p (r j q) -> p r j q", r=2, j=2),
                    in1=eskb[:, kvh * 4:(kvh + 1) * 4, :].rearrange("p (j r) q -> p r j q", r=2),
                    op=ALU.add), r=[Pd, eskb], w=[den])
                kb.op("dve", lambda e: e.reciprocal(out=den[:, :], in_=den[:, :]), r=[den], w=[den])
                kb.op("dve", lambda e: e.tensor_tensor(
                    out=oo[:, kvh * 4:(kvh + 1) * 4, :].rearrange("p (j r) q -> p r j q", r=2),
                    in0=Pn[0:64, :].rearrange("p (r j q) -> p r j q", r=2, j=2),
                    in1=den[:, :].rearrange("p (r j q) -> p r j q", r=2, j=2), op=ALU.mult),
                      r=[Pn, den], w=[oo])
            kb.dma("sp", yaS.rearrange("(h p) t -> p h t", p=64)[:, :, t0:t0 + 128], oo[:, :, :], oo, r=[oo],
                   w=[("dram", id(yaS))])


Model.mixA_declare = _mixA_declare
Model.mixA_phase = _mixA_phase
Model.attn_phase = _attn_phase


def _rope_tables(T):
    NT = CTX + T
    C = np.ones((128, NT), np.float32)
    S = np.zeros((128, NT), np.float32)
    t = np.arange(T)
    row = (t // 64).astype(np.float32)
    col = (t % 64).astype(np.float32)
    inv = (10000.0 ** (-np.arange(16, dtype=np.float32) / 16)).astype(np.float32)
    for d in range(64):
        i = d % 16
        pos = row if d < 32 else col
        ang = (pos * inv[i]).astype(np.float32)
        sign = -1.0 if (d % 32) < 16 else 1.0
        for hb in (0, 64):
            C[hb + d, CTX:] = np.cos(ang)
            S[hb + d, CTX:] = sign * np.sin(ang)
    return C, S


def _swap_perm(n_heads):
    idx = []
    for h in range(n_heads):
        for d in range(64):
            p = d + 16 if (d % 32) < 16 else d - 16
            idx.append(h * 64 + p)
    return np.array(idx)


def host_inputs_A(inp, T):
    d = {}
    w = inp["w_in"]
    q = w[:, :, 0:512]
    k = w[:, :, 512:640]
    v = w[:, :, 640:768]
    kB = np.concatenate([k[:, :, 64:128], k[:, :, 0:64]], -1)
    s5 = w[:, :, 2624:3136]
    g = w[:, :, 3136:6208]
    d["w_inA"] = np.ascontiguousarray(np.concatenate(
        [q, q[:, :, _swap_perm(8)], k, k[:, :, _swap_perm(2)], kB, kB[:, :, _swap_perm(2)], v, s5, g], -1))
    C, S = _rope_tables(T)
    d["ropeC"], d["ropeS"] = C, S
    d["sinkT"] = np.ascontiguousarray(np.broadcast_to(inp["attn_sink"][:, None, :], (DEPTH, 64, 8)))
    j = np.arange(128)[:, None]
    i = np.arange(128)[None, :]
    d["maskP"] = np.ascontiguousarray(np.tile((j >= i).astype(np.float32), (1, 4)))
    d["maskN"] = np.ascontiguousarray(np.tile((j <= i).astype(np.float32), (1, 4)))
    return d


I32 = mybir.dt.int32
TWO_PI = 2.0 * np.pi


def _s5_declare(self):
    L = DEPTH
    self.s5_are = self.din("s5_are", [L, 2, 128, 4, 64])
    self.s5_aim = self.din("s5_aim", [L, 2, 128, 4, 64])
    self.s5_ls = self.din("s5_ls", [L, 2, 128, 4])
    self.s5_brT = self.din("s5_brT", [L, 128, 4, 64])
    self.s5_biT = self.din("s5_biT", [L, 128, 4, 64])
    self.s5_are2 = self.din("s5_are2", [L, 2, 128, 16])
    self.s5_aim2 = self.din("s5_aim2", [L, 2, 128, 16])
    self.s5_ls2 = self.din("s5_ls2", [L, 2, 128, 16])
    self.s5_crT = self.din("s5_crT", [L, 128, 16, 16])
    self.s5_ciT = self.din("s5_ciT", [L, 128, 16, 16])
    self.s5_rowmask = self.din("s5_rowmask", [128, 16, 2])
    self.s5_dT = self.din("s5_dT", [128, L * 4])
    self.s5_glub = self.din("s5_glub", [128, L * 4])
    self.s5_gluw = self.din("s5_gluw", [L, 512, 512])
    self.tauT = self.din("tauT", [128, 128])


def _sincos(self, ang, angk, n, S, Sk, C, Ck, st):
    kb = self.kb
    t = kb.sb("sc_t", [128, n], F32, st)
    ti = kb.sb("sc_i", [128, n], I32, st)
    tf = kb.sb("sc_f", [128, n], F32, st)
    for (off, dst, dk) in ((0.0, S, Sk), (0.25, C, Ck)):
        kb.op("dve", lambda e: e.tensor_scalar(out=t[:, :], in0=ang, scalar1=1.0 / TWO_PI, scalar2=off,
                                               op0=ALU.mult, op1=ALU.add), r=[angk], w=[t])
        kb.op("dve", lambda e: e.tensor_copy(out=ti[:, :], in_=t[:, :]), r=[t], w=[ti])
        kb.op("dve", lambda e: e.tensor_copy(out=tf[:, :], in_=ti[:, :]), r=[ti], w=[tf])
        kb.op("dve", lambda e: e.tensor_tensor(out=tf[:, :], in0=t[:, :], in1=tf[:, :], op=ALU.subtract),
              r=[t, tf], w=[tf])
        kb.op("act", lambda e: e.activation(out=dst, in_=tf[:, :], func=AF.Sin, scale=TWO_PI), r=[tf], w=[dk])


def _s5_dir(self, l, d, usS, ysbS, ysS, st, PS2):
    kb = self.kb
    NT = self.NT
    nchunk = NT // 128
    usv = usS.rearrange("(c p) t -> p c t", p=128)
    ybv = ysbS.rearrange("(c p) t -> p c t", p=128)
    ysv = ysS.rearrange("(c p) t -> p c t", p=128)
    V = lambda e_, f, r, w: kb.op(e_, f, r=r, w=w)
    _pers = {}
    for (n_, shp_, dt_) in (("DR", [128, 16, 128], BF16), ("DI", [128, 16, 128], BF16), ("COS", [128, 16, 128], F32),
                            ("SIN", [128, 16, 128], F32), ("RHO0", [128, 16, 128], F32), ("rho", [128, 16], F32),
                            ("lr2", [128, 16], F32), ("li2", [128, 16], F32), ("CR", [128, 16, 128], BF16),
                            ("CIn", [128, 16, 128], BF16), ("s5d", [128, 4], F32), ("s5gb", [128, 4], F32),
                            ("gluw", [128, 4, 512], BF16)):
        _pers[n_] = kb.sb(n_, shp_, dt_, st)
    sbf = lambda n, shp, dt=F32: _pers[n] if n in _pers else kb.sb(n, shp, dt, st)
    st2 = contextlib.ExitStack()
    tmpf = lambda n, shp, dt=F32: kb.sb(n, shp, dt, st2)
    are, aim = tmpf("are", [128, 4, 64]), tmpf("aim", [128, 4, 64])
    ls = tmpf("ls", [128, 4])
    br, bi = tmpf("br", [128, 4, 64]), tmpf("bi", [128, 4, 64])
    kb.dma("sp", are[:, :, :], self.s5_are[l, d], are, w=[are])
    kb.dma("sp", aim[:, :, :], self.s5_aim[l, d], aim, w=[aim])
    kb.dma("sp", ls[:, :], self.s5_ls[l, d], ls, w=[ls])
    kb.dma("sp", br[:, :, :], self.s5_brT[l], br, w=[br])
    kb.dma("sp", bi[:, :, :], self.s5_biT[l], bi, w=[bi])
    rmask = tmpf("rmask", [128, 16, 2])
    kb.dma("sp", rmask[:, :, :], self.s5_rowmask[:, :, :], rmask, w=[rmask])
    V("act", lambda e: e.activation(out=ls[:, :], in_=ls[:, :], func=AF.Exp), [ls], [ls])
    dtb = ls[:, :].unsqueeze(2).broadcast_to([128, 4, 64])
    adt, th = tmpf("adt", [128, 4, 64]), tmpf("th", [128, 4, 64])
    V("dve", lambda e: e.tensor_tensor(out=adt[:, :, :], in0=are[:, :, :], in1=dtb, op=ALU.mult), [are, ls], [adt])
    V("act", lambda e: e.activation(out=adt[:, :, :], in_=adt[:, :, :], func=AF.Exp), [adt], [adt])
    V("dve", lambda e: e.tensor_tensor(out=th[:, :, :], in0=aim[:, :, :], in1=dtb, op=ALU.mult), [aim, ls], [th])
    Sd, Cd = tmpf("Sd", [128, 256]), tmpf("Cd", [128, 256])
    thf = th[:, :, :].rearrange("p a b -> p (a b)")
    self.sincos(thf, th, 256, Sd[:, :], Sd, Cd[:, :], Cd, st2)
    lr, li = tmpf("lr", [128, 256]), tmpf("li", [128, 256])
    magf = adt[:, :, :].rearrange("p a b -> p (a b)")
    V("dve", lambda e: e.tensor_tensor(out=lr[:, :], in0=magf, in1=Cd[:, :], op=ALU.mult), [adt, Cd], [lr])
    V("dve", lambda e: e.tensor_tensor(out=li[:, :], in0=magf, in1=Sd[:, :], op=ALU.mult), [adt, Sd], [li])
    aref = are[:, :, :].rearrange("p a b -> p (a b)")
    aimf = aim[:, :, :].rearrange("p a b -> p (a b)")
    t1, t2, den = tmpf("t1", [128, 256]), tmpf("t2", [128, 256]), tmpf("den", [128, 256])
    V("dve", lambda e: e.tensor_tensor(out=t1[:, :], in0=aref, in1=aref, op=ALU.mult), [are], [t1])
    V("dve", lambda e: e.tensor_tensor(out=t2[:, :], in0=aimf, in1=aimf, op=ALU.mult), [aim], [t2])
    V("dve", lambda e: e.tensor_tensor(out=den[:, :], in0=t1[:, :], in1=t2[:, :], op=ALU.add), [t1, t2], [den])
    V("dve", lambda e: e.reciprocal(out=den[:, :], in_=den[:, :]), [den], [den])
    V("dve", lambda e: e.tensor_scalar(out=lr[:, :], in0=lr[:, :], scalar1=-1.0, scalar2=None, op0=ALU.add),
      [lr], [lr])
    cr, ci = tmpf("cr", [128, 256]), tmpf("ci", [128, 256])
    V("dve", lambda e: e.tensor_tensor(out=t1[:, :], in0=lr[:, :], in1=aref, op=ALU.mult), [lr, are], [t1])
    V("dve", lambda e: e.tensor_tensor(out=t2[:, :], in0=li[:, :], in1=aimf, op=ALU.mult), [li, aim], [t2])
    V("dve", lambda e: e.tensor_tensor(out=cr[:, :], in0=t1[:, :], in1=t2[:, :], op=ALU.add), [t1, t2], [cr])
    V("dve", lambda e: e.tensor_tensor(out=cr[:, :], in0=cr[:, :], in1=den[:, :], op=ALU.mult), [cr, den], [cr])
    V("dve", lambda e: e.tensor_tensor(out=t1[:, :], in0=li[:, :], in1=aref, op=ALU.mult), [li, are], [t1])
    V("dve", lambda e: e.tensor_tensor(out=t2[:, :], in0=lr[:, :], in1=aimf, op=ALU.mult), [lr, aim], [t2])
    V("dve", lambda e: e.tensor_tensor(out=ci[:, :], in0=t1[:, :], in1=t2[:, :], op=ALU.subtract), [t1, t2], [ci])
    V("dve", lambda e: e.tensor_tensor(out=ci[:, :], in0=ci[:, :], in1=den[:, :], op=ALU.mult), [ci, den], [ci])
    brf = br[:, :, :].rearrange("p a b -> p (a b)")
    bif = bi[:, :, :].rearrange("p a b -> p (a b)")
    bbr, bbi = tmpf("bbr", [128, 4, 64]), tmpf("bbi", [128, 4, 64])
    bbrf = bbr[:, :, :].rearrange("p a b -> p (a b)")
    bbif = bbi[:, :, :].rearrange("p a b -> p (a b)")
    V("dve", lambda e: e.tensor_tensor(out=t1[:, :], in0=cr[:, :], in1=brf, op=ALU.mult), [cr, br], [t1])
    V("dve", lambda e: e.tensor_tensor(out=t2[:, :], in0=ci[:, :], in1=bif, op=ALU.mult), [ci, bi], [t2])
    V("dve", lambda e: e.tensor_tensor(out=bbrf, in0=t1[:, :], in1=t2[:, :], op=ALU.subtract), [t1, t2], [bbr])
    V("dve", lambda e: e.tensor_tensor(out=t1[:, :], in0=cr[:, :], in1=bif, op=ALU.mult), [cr, bi], [t1])
    V("dve", lambda e: e.tensor_tensor(out=t2[:, :], in0=ci[:, :], in1=brf, op=ALU.mult), [ci, br], [t2])
    V("dve", lambda e: e.tensor_tensor(out=bbif, in0=t1[:, :], in1=t2[:, :], op=ALU.add), [t1, t2], [bbi])
    DR, DI = sbf("DR", [128, 16, 128], BF16), sbf("DI", [128, 16, 128], BF16)
    for j in range(16):
        for gp in range(2):
            for (dst, srcb) in ((DR, bbr), (DI, bbi)):
                V("dve", lambda e: e.tensor_scalar(out=dst[:, j, gp * 64:(gp + 1) * 64], in0=srcb[:, j // 4, :],
                                                   scalar1=rmask[:, j, gp:gp + 1], scalar2=None, op0=ALU.mult),
                  [srcb, rmask], [dst])
    are2, aim2, ls2 = tmpf("are2", [128, 16]), tmpf("aim2", [128, 16]), tmpf("ls2", [128, 16])
    kb.dma("sp", are2[:, :], self.s5_are2[l, d], are2, w=[are2])
    kb.dma("sp", aim2[:, :], self.s5_aim2[l, d], aim2, w=[aim2])
    kb.dma("sp", ls2[:, :], self.s5_ls2[l, d], ls2, w=[ls2])
    tau = tmpf("tau", [128, 128])
    kb.dma("sp", tau[:, :], self.tauT[:, :], tau, w=[tau])
    V("act", lambda e: e.activation(out=ls2[:, :], in_=ls2[:, :], func=AF.Exp), [ls2], [ls2])
    rho, th2 = sbf("rho", [128, 16]), tmpf("th2", [128, 16])
    V("dve", lambda e: e.tensor_tensor(out=rho[:, :], in0=are2[:, :], in1=ls2[:, :], op=ALU.mult), [are2, ls2], [rho])
    V("act", lambda e: e.activation(out=rho[:, :], in_=rho[:, :], func=AF.Exp), [rho], [rho])
    V("dve", lambda e: e.tensor_tensor(out=th2[:, :], in0=aim2[:, :], in1=ls2[:, :], op=ALU.mult), [aim2, ls2], [th2])
    ang = tmpf("ang", [128, 16, 128])
    V("dve", lambda e: e.tensor_tensor(out=ang[:, :, :], in0=th2[:, :].unsqueeze(2).broadcast_to([128, 16, 128]),
                                       in1=tau[:, :].unsqueeze(1).broadcast_to([128, 16, 128]), op=ALU.mult),
      [th2, tau], [ang])
    COS, SIN = sbf("COS", [128, 16, 128]), sbf("SIN", [128, 16, 128])
    self.sincos(ang[:, :, :].rearrange("p a b -> p (a b)"), ang, 2048,
                SIN[:, :, :].rearrange("p a b -> p (a b)"), SIN, COS[:, :, :].rearrange("p a b -> p (a b)"), COS, st2)
    S1, C1 = tmpf("S1", [128, 16]), tmpf("C1", [128, 16])
    self.sincos(th2[:, :], th2, 16, S1[:, :], S1, C1[:, :], C1, st2)
    lr2, li2 = sbf("lr2", [128, 16]), sbf("li2", [128, 16])
    V("dve", lambda e: e.tensor_tensor(out=lr2[:, :], in0=rho[:, :], in1=C1[:, :], op=ALU.mult), [rho, C1], [lr2])
    V("dve", lambda e: e.tensor_tensor(out=li2[:, :], in0=rho[:, :], in1=S1[:, :], op=ALU.mult), [rho, S1], [li2])
    RHO0 = sbf("RHO0", [128, 16, 128])
    V("dve", lambda e: e.tensor_copy(out=RHO0[:, :, :], in_=rho[:, :].unsqueeze(2).broadcast_to([128, 16, 128])),
      [rho], [RHO0])
    f0 = 127 if d == 1 else 0
    V("dve", lambda e: e.memset(RHO0[:, :, f0:f0 + 1], 0.0), [], [RHO0])
    crT, ciT = tmpf("crT", [128, 16, 16]), tmpf("ciT", [128, 16, 16])
    kb.dma("sp", crT[:, :, :], self.s5_crT[l], crT, w=[crT])
    kb.dma("sp", ciT[:, :, :], self.s5_ciT[l], ciT, w=[ciT])
    CR, CIn = sbf("CR", [128, 16, 128], BF16), sbf("CIn", [128, 16, 128], BF16)
    V("dve", lambda e: e.memset(CR[:, :, :], 0.0), [], [CR])
    V("dve", lambda e: e.memset(CIn[:, :, :], 0.0), [], [CIn])
    for j in range(16):
        for gp in range(2):
            c0 = 32 * (j % 4) + 16 * gp
            V("dve", lambda e: e.tensor_copy(out=CR[gp * 64:(gp + 1) * 64, j, c0:c0 + 16],
                                             in_=crT[gp * 64:(gp + 1) * 64, j, :]), [crT], [CR])
            V("dve", lambda e: e.tensor_scalar(out=CIn[gp * 64:(gp + 1) * 64, j, c0:c0 + 16],
                                               in0=ciT[gp * 64:(gp + 1) * 64, j, :], scalar1=-1.0, scalar2=None,
                                               op0=ALU.mult), [ciT], [CIn])
    if d == 0:
        dv, gb = sbf("s5d", [128, 4]), sbf("s5gb", [128, 4])
        kb.dma("sp", dv[:, :], self.s5_dT[:, l * 4:(l + 1) * 4], dv, w=[dv])
        kb.dma("sp", gb[:, :], self.s5_glub[:, l * 4:(l + 1) * 4], gb, w=[gb])
        gw = sbf("gluw", [128, 4, 512], BF16)
        for c in range(4):
            kb.dma("pool", gw[:, c, :], self.s5_gluw[l, c * 128:(c + 1) * 128, :], gw, w=[gw])
    kb.barrier()
    st2.close()
    usp = Pool(kb, "s5u", 2, [128, 4, 128], BF16, stack=st)
    usfp = Pool(kb, "s5uf", 2, [128, 4, 128], F32, stack=st)
    ybp = Pool(kb, "s5yb", 2, [128, 4, 128], F32, stack=st)
    mp = [Pool(kb, "s5m%d" % i, 1, [128, 8, 128], F32, stack=st) for i in range(4)]
    ZR, ZI = sbf("ZR", [128, 16, 128]), sbf("ZI", [128, 16, 128])
    XZR, XZI = sbf("XZR", [128, 16, 128]), sbf("XZI", [128, 16, 128])
    up_ = [Pool(kb, "s5t%d" % i, 1, [128, 16, 128], F32, stack=st) for i in range(2)]
    XR, XI = sbf("XR", [128, 16, 128], BF16), sbf("XI", [128, 16, 128], BF16)
    xlr, xli = sbf("xlr", [128, 16]), sbf("xli", [128, 16])
    cjr, cji = sbf("cjr", [128, 16]), sbf("cji", [128, 16])
    tt = [sbf("s5tt%d" % i, [128, 16]) for i in range(4)]
    yo = Pool(kb, "s5yo", 2, [128, 4, 128], F32, stack=st)
    rev = (d == 1)
    R3 = (lambda ap: ap[:, :, ::-1]) if rev else (lambda ap: ap)
    first, last = (127, 0) if rev else (0, 127)
    order = [0, 1] + list(range(2, nchunk))
    if rev:
        order = [1, 0] + list(range(nchunk - 1, 1, -1))
    yield
    for ci_, ch in enumerate(order):
        if ci_ > 0:
            yield
        t0 = ch * 128
        us = usp.next()
        kb.dma("pool", us[:, :, :], usv[:, :, t0:t0 + 128], us, r=[("dram", id(usS))], w=[us])
        for hf in range(2):
            PR, PI = PS2.next(), PS2.next()
            for jj in range(8):
                j = hf * 8 + jj
                kb.op("pe", lambda e: e.matmul(PR[:, jj * 128:(jj + 1) * 128], lhsT=DR[:, j, :], rhs=us[:, j // 4, :],
                                               start=True, stop=True), r=[DR, us], w=[PR])
                kb.op("pe", lambda e: e.matmul(PI[:, jj * 128:(jj + 1) * 128], lhsT=DI[:, j, :], rhs=us[:, j // 4, :],
                                               start=True, stop=True), r=[DI, us], w=[PI])
            prv = PR[:, :].rearrange("p (a b) -> p a b", a=8)
            piv = PI[:, :].rearrange("p (a b) -> p a b", a=8)
            cs = R3(COS[:, hf * 8:(hf + 1) * 8, :])
            sn = R3(SIN[:, hf * 8:(hf + 1) * 8, :])
            m = [p.next() for p in mp]
            V("dve", lambda e: e.tensor_tensor(out=m[0][:, :, :], in0=prv, in1=cs, op=ALU.mult), [PR, COS], [m[0]])
            V("dve", lambda e: e.tensor_tensor(out=m[1][:, :, :], in0=piv, in1=sn, op=ALU.mult), [PI, SIN], [m[1]])
            V("dve", lambda e: e.tensor_tensor(out=m[2][:, :, :], in0=piv, in1=cs, op=ALU.mult), [PI, COS], [m[2]])
            V("dve", lambda e: e.tensor_tensor(out=m[3][:, :, :], in0=prv, in1=sn, op=ALU.mult), [PR, SIN], [m[3]])
            V("pool", lambda e: e.tensor_tensor(out=ZR[:, hf * 8:(hf + 1) * 8, :], in0=m[0][:, :, :], in1=m[1][:, :, :],
                                                op=ALU.add), [m[0], m[1]], [ZR])
            V("pool", lambda e: e.tensor_tensor(out=ZI[:, hf * 8:(hf + 1) * 8, :], in0=m[2][:, :, :], in1=m[3][:, :, :],
                                                op=ALU.subtract), [m[2], m[3]], [ZI])
        if ci_ > 0:
            V("pool", lambda e: e.tensor_tensor(out=tt[0][:, :], in0=lr2[:, :], in1=xlr[:, :], op=ALU.mult), [lr2, xlr], [tt[0]])
            V("pool", lambda e: e.tensor_tensor(out=tt[1][:, :], in0=li2[:, :], in1=xli[:, :], op=ALU.mult), [li2, xli], [tt[1]])
            V("pool", lambda e: e.tensor_tensor(out=cjr[:, :], in0=tt[0][:, :], in1=tt[1][:, :], op=ALU.subtract), [tt[0], tt[1]], [cjr])
            V("pool", lambda e: e.tensor_tensor(out=tt[2][:, :], in0=lr2[:, :], in1=xli[:, :], op=ALU.mult), [lr2, xli], [tt[2]])
            V("pool", lambda e: e.tensor_tensor(out=tt[3][:, :], in0=li2[:, :], in1=xlr[:, :], op=ALU.mult), [li2, xlr], [tt[3]])
            V("pool", lambda e: e.tensor_tensor(out=cji[:, :], in0=tt[2][:, :], in1=tt[3][:, :], op=ALU.add), [tt[2], tt[3]], [cji])
            V("pool", lambda e: e.tensor_tensor(out=ZR[:, :, first], in0=ZR[:, :, first], in1=cjr[:, :], op=ALU.add), [ZR, cjr], [ZR])
            V("pool", lambda e: e.tensor_tensor(out=ZI[:, :, first], in0=ZI[:, :, first], in1=cji[:, :], op=ALU.add), [ZI, cji], [ZI])
        fl = lambda b_: (b_[:, :, :].rearrange("p a b -> p (a b)")[:, ::-1] if rev
                         else b_[:, :, :].rearrange("p a b -> p (a b)"))
        V("dve", lambda e: e.tensor_tensor_scan(out=fl(XZR), data0=fl(RHO0), data1=fl(ZR), initial=0.0,
                                                op0=ALU.mult, op1=ALU.add), [RHO0, ZR], [XZR])
        V("dve", lambda e: e.tensor_tensor_scan(out=fl(XZI), data0=fl(RHO0), data1=fl(ZI), initial=0.0,
                                                op0=ALU.mult, op1=ALU.add), [RHO0, ZI], [XZI])
        cs, sn = R3(COS[:, :, :]), R3(SIN[:, :, :])
        ua, ub = up_[0].next(), up_[1].next()
        V("dve", lambda e: e.tensor_tensor(out=ua[:, :, :], in0=XZR[:, :, :], in1=cs, op=ALU.mult), [XZR, COS], [ua])
        V("pool", lambda e: e.tensor_tensor(out=ub[:, :, :], in0=XZI[:, :, :], in1=sn, op=ALU.mult), [XZI, SIN], [ub])
        V("dve", lambda e: e.tensor_tensor(out=XR[:, :, :], in0=ua[:, :, :], in1=ub[:, :, :], op=ALU.subtract), [ua, ub], [XR])
        V("pool", lambda e: e.tensor_tensor(out=xlr[:, :], in0=ua[:, :, last], in1=ub[:, :, last], op=ALU.subtract), [ua, ub], [xlr])
        V("pool", lambda e: e.tensor_tensor(out=ua[:, :, :], in0=XZR[:, :, :], in1=sn, op=ALU.mult), [XZR, SIN], [ua])
        V("dve", lambda e: e.tensor_tensor(out=ub[:, :, :], in0=XZI[:, :, :], in1=cs, op=ALU.mult), [XZI, COS], [ub])
        V("pool", lambda e: e.tensor_tensor(out=XI[:, :, :], in0=ua[:, :, :], in1=ub[:, :, :], op=ALU.add), [ua, ub], [XI])
        V("pool", lambda e: e.tensor_tensor(out=xli[:, :], in0=ua[:, :, last], in1=ub[:, :, last], op=ALU.add), [ua, ub], [xli])
        PY = PS2.next()
        for cc in range(4):
            for jj in range(4):
                j = cc * 4 + jj
                kb.op("pe", lambda e: e.matmul(PY[:, cc * 128:(cc + 1) * 128], lhsT=CR[:, j, :], rhs=XR[:, j, :],
                                               start=(jj == 0), stop=False), r=[CR, XR], w=[PY])
                kb.op("pe", lambda e: e.matmul(PY[:, cc * 128:(cc + 1) * 128], lhsT=CIn[:, j, :], rhs=XI[:, j, :],
                                               start=False, stop=(jj == 3)), r=[CIn, XI], w=[PY])
        pyv = PY[:, 0:512].rearrange("p (a b) -> p a b", a=4)
        if d == 1:
            y = yo.next()
            V("act", lambda e: e.activation(out=y[:, :, :], in_=pyv, func=AF.Copy), [PY], [y])
            kb.dma("act", ybv[:, :, t0:t0 + 128], y[:, :, :], y, r=[y], w=[("dram", id(ysbS))])
        else:
            yb, usf = ybp.next(), usfp.next()
            kb.dma("sp", yb[:, :, :], ybv[:, :, t0:t0 + 128], yb, r=[("dram", id(ysbS))], w=[yb])
            kb.dma("sp", usf[:, :, :], usv[:, :, t0:t0 + 128], usf, r=[("dram", id(usS))], w=[usf])
            y = yo.next()
            V("dve", lambda e: e.tensor_tensor(out=y[:, :, :], in0=pyv, in1=yb[:, :, :], op=ALU.add), [PY, yb], [y])
            V("pool", lambda e: e.tensor_tensor(out=usf[:, :, :], in0=usf[:, :, :],
                                                in1=dv[:, :].unsqueeze(2).broadcast_to([128, 4, 128]), op=ALU.mult),
              [usf, dv], [usf])
            V("pool", lambda e: e.tensor_tensor(out=y[:, :, :], in0=y[:, :, :], in1=usf[:, :, :], op=ALU.add), [y, usf], [y])
            g1 = yb
            V("pool", lambda e: e.tensor_tensor(out=g1[:, :, :], in0=y[:, :, :], in1=y[:, :, :], op=ALU.mult), [y], [g1])
            V("dve", lambda e: e.tensor_scalar(out=g1[:, :, :], in0=g1[:, :, :], scalar1=0.044715, scalar2=1.0,
                                               op0=ALU.mult, op1=ALU.add), [g1], [g1])
            V("dve", lambda e: e.tensor_tensor(out=g1[:, :, :], in0=g1[:, :, :], in1=y[:, :, :], op=ALU.mult), [g1, y], [g1])
            V("act", lambda e: e.activation(out=g1[:, :, :], in_=g1[:, :, :], func=AF.Sigmoid, scale=1.5957691216057308),
              [g1], [g1])
            V("dve", lambda e: e.tensor_tensor(out=y[:, :, :], in0=y[:, :, :], in1=g1[:, :, :], op=ALU.mult), [y, g1], [y])
            geb = us
            V("act", lambda e: e.activation(out=geb[:, :, :], in_=y[:, :, :], func=AF.Copy), [y], [geb])
            PG = PS2.next()
            for oc in range(4):
                for kc in range(4):
                    kb.op("pe", lambda e: e.matmul(PG[:, oc * 128:(oc + 1) * 128], lhsT=gw[:, kc, oc * 128:(oc + 1) * 128],
                                                   rhs=geb[:, kc, :], start=(kc == 0), stop=(kc == 3)), r=[gw, geb], w=[PG])
                V("act", lambda e: e.activation(out=usf[:, oc, :], in_=PG[:, oc * 128:(oc + 1) * 128], func=AF.Sigmoid,
                                                bias=gb[:, oc:oc + 1]), [PG, gb], [usf])
            yso = usp.next()
            V("dve", lambda e: e.tensor_tensor(out=yso[:, :, :], in0=y[:, :, :], in1=usf[:, :, :], op=ALU.mult), [y, usf], [yso])
            kb.dma("sp", ysv[:, :, t0:t0 + 128], yso[:, :, :], yso, r=[yso], w=[("dram", id(ysS))])


Model.s5_declare = _s5_declare
Model.sincos = _sincos
Model.s5_dir = _s5_dir


def host_inputs_s5(inp):
    L = DEPTH
    d = {}

    def drive(a):
        a = a.reshape(L, 2, 4, 8, 1, 64)
        a = np.broadcast_to(a, (L, 2, 4, 8, 16, 64))
        return np.ascontiguousarray(a.transpose(0, 1, 3, 4, 2, 5).reshape(L, 2, 128, 4, 64))
    d["s5_are"] = drive(inp["s5_a_re"])
    d["s5_aim"] = drive(inp["s5_a_im"])
    lsd = inp["s5_log_step"].reshape(L, 2, 4, 8, 1)
    d["s5_ls"] = np.ascontiguousarray(np.broadcast_to(lsd, (L, 2, 4, 8, 16)).transpose(0, 1, 3, 4, 2).reshape(L, 2, 128, 4))

    def bT(b):
        b = b.reshape(L, 4, 8, 64, 16)
        return np.ascontiguousarray(b.transpose(0, 2, 4, 1, 3).reshape(L, 128, 4, 64))
    d["s5_brT"] = bT(inp["s5_b_re"])
    d["s5_biT"] = bT(inp["s5_b_im"])

    def st2(a):
        a = a.reshape(L, 2, 16, 2, 64)
        return np.ascontiguousarray(a.transpose(0, 1, 3, 4, 2).reshape(L, 2, 128, 16))
    d["s5_are2"] = st2(inp["s5_a_re"])
    d["s5_aim2"] = st2(inp["s5_a_im"])
    ls2 = np.broadcast_to(inp["s5_log_step"].reshape(L, 2, 16, 2, 1), (L, 2, 16, 2, 64))
    d["s5_ls2"] = np.ascontiguousarray(ls2.transpose(0, 1, 3, 4, 2).reshape(L, 2, 128, 16))

    def cT(c):
        c = c.reshape(L, 16, 2, 16, 64)
        return np.ascontiguousarray(c.transpose(0, 2, 4, 1, 3).reshape(L, 128, 16, 16))
    d["s5_crT"] = cT(inp["s5_c_re"])
    d["s5_ciT"] = cT(inp["s5_c_im"])
    k = np.arange(128)[:, None, None] // 16
    j = np.arange(16)[None, :, None]
    gp = np.arange(2)[None, None, :]
    d["s5_rowmask"] = np.ascontiguousarray((k == 2 * (j % 4) + gp).astype(np.float32))
    d["s5_dT"] = np.ascontiguousarray(inp["s5_d"].reshape(L * 4, 128).T)
    d["s5_glub"] = np.ascontiguousarray(inp["s5_glu_b"].reshape(L * 4, 128).T)
    d["s5_gluw"] = inp["s5_glu_w"]
    d["tauT"] = np.ascontiguousarray(np.broadcast_to(np.arange(128, dtype=np.float32)[None, :], (128, 128)))
    return d


NCH_B = 15


def _rwkv_declare(self):
    L = DEPTH
    self.w_inB = self.din("w_inB", [L, D, NCH_B * 128])
    self.rk_mu = self.din("rk_mu", [128, L * NCH_B])
    self.rk_w0 = self.din("rk_w0", [128, L * 8])
    for n in ("a0", "kk", "ka", "rk", "gng", "gnb"):
        setattr(self, "rk_" + n, self.din("rk_" + n, [128, L * 4]))
    self.rk_w2 = self.din("rk_w2", [L, 128, 512])
    self.rk_a2 = self.din("rk_a2", [L, 64, 512])
    self.rk_g2 = self.din("rk_g2", [L, 128, 512])
    self.bd64 = self.din("bd64", [128, 128])
    self.identb = self.din("identb", [128, 128])
    self.ones0 = self.din("ones0", [2, 128, 128])
    self.rk_MT = self.din("rk_MT", [2, 128, 512])
    self.rk_MN = self.din("rk_MN", [2, 128, 512])


def _rwkv_prep_phase(self, l, src, S):
    kb = self.kb
    W = 256
    Wh = W + 2
    NB = W // 128
    with kb.phase() as st:
        self.psum = Pool(kb, "ps", 8, [128, 512], F32, psum=True, stack=st)
        V = lambda e_, f, r, w: kb.op(e_, f, r=r, w=w)
        sbf = lambda n, shp, dt=F32: kb.sb(n, shp, dt, st)
        w = sbf("wB", [128, NCH, NCH_B * 128], BF16)
        for c in range(NCH):
            kb.dma("pool", w[:, c, :], self.w_inB[l, c * 128:(c + 1) * 128, :], w, w=[w])
        w2b, a2b, g2b = sbf("w2b", [128, 512], BF16), sbf("a2b", [64, 512], BF16), sbf("g2b", [128, 512], BF16)
        kb.dma("pool", w2b[:, :], self.rk_w2[l], w2b, w=[w2b])
        kb.dma("pool", a2b[:, :], self.rk_a2[l], a2b, w=[a2b])
        kb.dma("pool", g2b[:, :], self.rk_g2[l], g2b, w=[g2b])
        bd, idb = sbf("bd", [128, 128], BF16), sbf("idb", [128, 128], BF16)
        kb.dma("pool", bd[:, :], self.bd64[:, :], bd, w=[bd])
        kb.dma("pool", idb[:, :], self.identb[:, :], idb, w=[idb])
        on0 = sbf("on0", [128, 2, 128])
        kb.dma("sp", on0[:, :, :], self.ones0.rearrange("d p t -> p d t"), on0, w=[on0])
        ON = [sbf("ON%d" % d_, [128, 4 * (W // 128), 128]) for d_ in range(2)]
        for d_ in range(2):
            V("dve", lambda e: e.tensor_copy(out=ON[d_][:, :, :], in_=on0[:, d_:d_ + 1, :].broadcast_to([128, 4 * (W // 128), 128])),
              [on0], [ON[d_]])
        mu, omu, hmu = sbf("mu", [128, NCH_B]), sbf("omu", [128, NCH_B]), sbf("hmu", [128, NCH_B])
        kb.dma("sp", mu[:, :], self.rk_mu[:, l * NCH_B:(l + 1) * NCH_B], mu, w=[mu])
        V("dve", lambda e: e.tensor_scalar(out=omu[:, :], in0=mu[:, :], scalar1=-1.0, scalar2=1.0, op0=ALU.mult, op1=ALU.add), [mu], [omu])
        V("dve", lambda e: e.tensor_scalar(out=hmu[:, :], in0=mu[:, :], scalar1=0.5, scalar2=None, op0=ALU.mult), [mu], [hmu])
        w0 = sbf("w0", [128, 8])
        kb.dma("sp", w0[:, :], self.rk_w0[:, l * 8:(l + 1) * 8], w0, w=[w0])
        pv = {}
        for n in ("a0", "kk", "ka", "rk"):
            pv[n] = sbf("p_" + n, [128, 4])
            kb.dma("sp", pv[n][:, :], getattr(self, "rk_" + n)[:, l * 4:(l + 1) * 4], pv[n], w=[pv[n]])
        omka = sbf("omka", [128, 4])
        V("dve", lambda e: e.tensor_scalar(out=omka[:, :], in0=pv["ka"][:, :], scalar1=-1.0, scalar2=1.0, op0=ALU.mult, op1=ALU.add),
          [pv["ka"]], [omka])
        pl = self.stat_pools(Wh, st)
        xp = Pool(kb, "x", 2, [128, NCH, Wh], F32, stack=st)
        xn = sbf("xn", [128, NCH, Wh])
        u = sbf("u", [128, NCH, Wh], BF16)
        zcp = Pool(kb, "zc", 2, [128, Wh], F32, stack=st)
        tmpp = Pool(kb, "ztmp", 2, [128, W], F32, stack=st)
        Z = sbf("Z", [128, NCH_B, W])
        tw, sg, alb = sbf("tw", [128, W], BF16), sbf("sg", [128, W], BF16), sbf("alb", [64, W], BF16)
        LW = [sbf("LW%d" % d_, [128, 4, W]) for d_ in range(2)]
        A, KK, KM, Bv = sbf("A", [128, 4, W]), sbf("KK", [128, 4, W]), sbf("KM", [128, 4, W]), sbf("Bv", [128, 4, W])
        SQ = sbf("SQ", [128, 4, W], BF16)
        T1, T2 = sbf("T1", [128, 4, W]), sbf("T2", [128, 4, W])
        Gp = Pool(kb, "Go", 2, [128, 4, W], BF16, stack=st)
        Bop = Pool(kb, "Bo", 2, [128, 4, W], BF16, stack=st)
        Vb = sbf("Vb", [128, 4, W], BF16)
        tokp = Pool(kb, "tok", 3, [128, NB, 512], BF16, stack=st)
        Lc, Lr, Lq = sbf("Lc", [128, 4, W]), sbf("Lr", [128, 4, W]), sbf("Lq", [128, 4, W])
        E = sbf("E", [128, 4, W])
        outp = {n: Pool(kb, n, 2, [128, 4, W], BF16, stack=st) for n in ("RHO", "KAP", "BET", "KTI")}
        scp = Pool(kb, "sco", 2, [128, NB, 4, 3], F32, stack=st)
        lmn = sbf("lmn", [128, 4, NB])
        srcv = src.rearrange("(c p) t -> p c t", p=128)
        fmv = lambda t_: t_.rearrange("(c p) t -> p c t", p=128)

        def transpose_store(srcb, dst, t0):
            tk = tokp.next()
            for tb in range(NB):
                P = self.psum.next()
                pb = P[:, 0:256].bitcast(BF16)
                for c in range(4):
                    kb.op("pe", lambda e: e.transpose(pb[:, c * 128:(c + 1) * 128], srcb[:, c, tb * 128:(tb + 1) * 128], idb[:, :]),
                          r=[srcb, idb], w=[P])
                V("act", lambda e: e.activation(out=tk[:, tb, :], in_=pb[:, 0:512], func=AF.Copy), [P], [tk])
            kb.dma("sp", dst[t0:t0 + W, :].rearrange("(b p) n -> p b n", p=128), tk[:, :, :], tk, r=[tk], w=[("dram", id(dst))])

        for (seg, t0, _) in self.tiles(W):
            seg_lo, seg_hi = (0, CTX) if seg == 1 else (CTX, self.NT)
            lo, hi = max(t0 - 1, seg_lo), min(t0 + W + 1, seg_hi)
            x = xp.next()
            c0 = lo - (t0 - 1)
            if c0 > 0:
                V("dve", lambda e: e.memset(x[:, :, 0:1], 0.0), [], [x])
            if hi < t0 + W + 1:
                V("dve", lambda e: e.memset(x[:, :, Wh - 1:Wh], 0.0), [], [x])
            kb.dma("sp", x[:, :, c0:c0 + (hi - lo)], srcv[:, :, lo:hi], x, r=[("dram", id(src))], w=[x])
            rstd, nmr = self.ln_stats_w(x, Wh, pl)
            self.normalize(xn, x, Wh, rstd, nmr)
            for c in range(NCH):
                V("act", lambda e: e.activation(out=u[:, c, :], in_=xn[:, c, :], func=AF.Identity,
                                                scale=self.modp1[:, 4 * NCH + c, seg:seg + 1],
                                                bias=self.mods[:, 3 * NCH + c, seg:seg + 1]), [xn, self.modp1, self.mods], [u])
            if c0 > 0:
                V("dve", lambda e: e.memset(u[:, :, 0:1], 0.0), [], [u])
            if hi < t0 + W + 1:
                V("dve", lambda e: e.memset(u[:, :, Wh - 1:Wh], 0.0), [], [u])
            for ch in range(NCH_B):
                P = self.psum.next()
                for c in range(NCH):
                    kb.op("pe", lambda e: e.matmul(P[:, 0:Wh], lhsT=w[:, c, ch * 128:(ch + 1) * 128], rhs=u[:, c, :],
                                                   start=(c == 0), stop=(c == NCH - 1)), r=[w, u], w=[P])
                zc, tm = zcp.next(), tmpp.next()
                V("act", lambda e: e.activation(out=zc[:, :], in_=P[:, 0:Wh], func=AF.Copy), [P], [zc])
                V("pool", lambda e: e.tensor_tensor(out=tm[:, :], in0=zc[:, 0:W], in1=zc[:, 2:W + 2], op=ALU.add), [zc], [tm])
                V("act", lambda e: e.activation(out=tm[:, :], in_=tm[:, :], func=AF.Identity, scale=hmu[:, ch:ch + 1]), [tm, hmu], [tm])
                V("dve", lambda e: e.scalar_tensor_tensor(out=Z[:, ch, :], in0=zc[:, 1:W + 1], scalar=omu[:, ch:ch + 1],
                                                          in1=tm[:, :], op0=ALU.mult, op1=ALU.add), [zc, omu, tm], [Z])
            R_, K_, V_ = Z[:, 0:4, :], Z[:, 4:8, :], Z[:, 8:12, :]
            V("act", lambda e: e.activation(out=tw[:, :], in_=Z[:, 12, :], func=AF.Tanh), [Z], [tw])
            V("act", lambda e: e.activation(out=sg[:, :], in_=Z[:, 13, :], func=AF.Sigmoid), [Z], [sg])
            V("act", lambda e: e.activation(out=alb[:, :], in_=Z[0:64, 14, :], func=AF.Copy), [Z], [alb])
            for d_ in range(2):
                for c in range(4):
                    P = self.psum.next()
                    kb.op("pe", lambda e: e.matmul(P[:, 0:W], lhsT=w2b[d_ * 64:(d_ + 1) * 64, c * 128:(c + 1) * 128],
                                                   rhs=tw[d_ * 64:(d_ + 1) * 64, :], start=True, stop=True), r=[w2b, tw], w=[P])
                    V("act", lambda e: e.activation(out=LW[d_][:, c, :], in_=P[:, 0:W], func=AF.Sigmoid,
                                                    bias=w0[:, d_ * 4 + c:d_ * 4 + c + 1]), [P, w0], [LW[d_]])
                V("pool", lambda e: e.tensor_scalar(out=LW[d_][:, :, :], in0=LW[d_][:, :, :], scalar1=-DECAY_SCALE, scalar2=None,
                                                    op0=ALU.mult), [LW[d_]], [LW[d_]])
            Go = Gp.next()
            for c in range(4):
                P = self.psum.next()
                kb.op("pe", lambda e: e.matmul(P[:, 0:W], lhsT=a2b[0:64, c * 128:(c + 1) * 128], rhs=alb[0:64, :],
                                               start=True, stop=True), r=[a2b, alb], w=[P])
                V("act", lambda e: e.activation(out=A[:, c, :], in_=P[:, 0:W], func=AF.Sigmoid, bias=pv["a0"][:, c:c + 1]),
                  [P, pv["a0"]], [A])
                P = self.psum.next()
                kb.op("pe", lambda e: e.matmul(P[:, 0:W], lhsT=g2b[:, c * 128:(c + 1) * 128], rhs=sg[:, :],
                                               start=True, stop=True), r=[g2b, sg], w=[P])
                V("act", lambda e: e.activation(out=Go[:, c, :], in_=P[:, 0:W], func=AF.Copy), [P], [Go])
            kb.dma("sp", fmv(S["gR"])[:, :, t0:t0 + W], Go[:, :, :], Go, r=[Go], w=[("dram", id(S["gR"]))])
            for c in range(4):
                V("pool", lambda e: e.tensor_scalar(out=KK[:, c, :], in0=Z[:, 4 + c, :], scalar1=pv["kk"][:, c:c + 1], scalar2=None,
                                                    op0=ALU.mult), [Z, pv["kk"]], [KK])
            V("act", lambda e: e.activation(out=SQ[:, :, :], in_=KK[:, :, :], func=AF.Square), [KK], [SQ])
            for c in range(4):
                P = self.psum.next()
                kb.op("pe", lambda e: e.matmul(P[:, 0:W], lhsT=bd[:, :], rhs=SQ[:, c, :], start=True, stop=True), r=[bd, SQ], w=[P])
                V("dve", lambda e: e.tensor_scalar(out=T1[:, c, :], in0=P[:, 0:W], scalar1=1e-12, scalar2=None, op0=ALU.add), [P], [T1])
            V("act", lambda e: e.activation(out=T1[:, :, :], in_=T1[:, :, :], func=AF.Sqrt), [T1], [T1])
            V("dve", lambda e: e.reciprocal(out=T1[:, :, :], in_=T1[:, :, :]), [T1], [T1])
            V("dve", lambda e: e.tensor_tensor(out=KK[:, :, :], in0=KK[:, :, :], in1=T1[:, :, :], op=ALU.mult), [KK, T1], [KK])
            for c in range(4):
                V("dve", lambda e: e.tensor_scalar(out=T2[:, c, :], in0=A[:, c, :], scalar1=pv["ka"][:, c:c + 1],
                                                   scalar2=omka[:, c:c + 1], op0=ALU.mult, op1=ALU.add), [A, pv["ka"], omka], [T2])
            V("pool", lambda e: e.tensor_tensor(out=KM[:, :, :], in0=K_, in1=T2[:, :, :], op=ALU.mult), [Z, T2], [KM])
            V("pool", lambda e: e.tensor_tensor(out=Bv[:, :, :], in0=KK[:, :, :], in1=A[:, :, :], op=ALU.mult), [KK, A], [Bv])
            V("dve", lambda e: e.tensor_tensor(out=T1[:, :, :], in0=R_, in1=KM[:, :, :], op=ALU.mult), [Z, KM], [T1])
            for c in range(4):
                V("act", lambda e: e.activation(out=SQ[:, c, :], in_=T1[:, c, :], func=AF.Identity, scale=pv["rk"][:, c:c + 1]),
                  [T1, pv["rk"]], [SQ])
            Bo = Bop.next()
            for c in range(4):
                P = self.psum.next()
                kb.op("pe", lambda e: e.matmul(P[:, 0:W], lhsT=bd[:, :], rhs=SQ[:, c, :], start=True, stop=True), r=[bd, SQ], w=[P])
                V("dve", lambda e: e.tensor_tensor(out=Bo[:, c, :], in0=P[:, 0:W], in1=Z[:, 8 + c, :], op=ALU.mult), [P, Z], [Bo])
            kb.dma("sp", fmv(S["bon"])[:, :, t0:t0 + W], Bo[:, :, :], Bo, r=[Bo], w=[("dram", id(S["bon"]))])
            V("act", lambda e: e.activation(out=Vb[:, :, :], in_=V_, func=AF.Copy), [Z], [Vb])
            transpose_store(Vb, S["vt"], t0)
            for d_ in range(2):
                rev = d_ == 1
                fl = (lambda ap: ap.rearrange("p a b -> p (a b)")[:, ::-1]) if rev else (lambda ap: ap.rearrange("p a b -> p (a b)"))
                V("dve", lambda e: e.tensor_tensor_scan(out=fl(Lc[:, :, :]), data0=fl(ON[d_][:, :, :]), data1=fl(LW[d_][:, :, :]),
                                                        initial=0.0, op0=ALU.mult, op1=ALU.add), [ON[d_], LW[d_]], [Lc])
                mid, last = (64, 0) if rev else (63, 127)
                L4 = Lc[:, :, :].rearrange("p c (b t) -> p c b t", b=NB)
                sc = scp.next()
                V("dve", lambda e: e.tensor_copy(out=lmn[:, :, :], in_=L4[:, :, :, mid]), [Lc], [lmn])
                scv = sc[:, :, :, :].rearrange("p b c s -> p c b s")
                V("act", lambda e: e.activation(out=scv[:, :, :, 0], in_=lmn[:, :, :], func=AF.Exp), [lmn], [sc])
                V("act", lambda e: e.activation(out=scv[:, :, :, 2], in_=L4[:, :, :, last], func=AF.Exp), [Lc], [sc])
                V("dve", lambda e: e.tensor_tensor(out=scv[:, :, :, 1], in0=L4[:, :, :, last], in1=lmn[:, :, :], op=ALU.subtract),
                  [Lc, lmn], [sc])
                V("act", lambda e: e.activation(out=scv[:, :, :, 1], in_=scv[:, :, :, 1], func=AF.Exp), [sc], [sc])
                kb.dma("sp", S["sc"][d_][:, t0 // 128:t0 // 128 + NB, :, :], sc[:, :, :, :], sc, r=[sc], w=[("dram", id(S["sc"][d_]))])
                Lr4 = Lr[:, :, :].rearrange("p c (b t) -> p c b t", b=NB)
                V("pool", lambda e: e.tensor_tensor(out=Lr4, in0=L4, in1=lmn[:, :, :].unsqueeze(3).broadcast_to([128, 4, NB, 128]),
                                                    op=ALU.subtract), [Lc, lmn], [Lr])
                V("pool", lambda e: e.tensor_tensor(out=Lq[:, :, :], in0=Lr[:, :, :], in1=LW[d_][:, :, :], op=ALU.subtract),
                  [Lr, LW[d_]], [Lq])
                o = {n: outp[n].next() for n in outp}
                V("act", lambda e: e.activation(out=E[:, :, :], in_=Lr[:, :, :], func=AF.Exp), [Lr], [E])
                V("dve", lambda e: e.tensor_tensor(out=o["RHO"][:, :, :], in0=R_, in1=E[:, :, :], op=ALU.mult), [Z, E], [o["RHO"]])
                V("act", lambda e: e.activation(out=E[:, :, :], in_=Lq[:, :, :], func=AF.Exp), [Lq], [E])
                V("dve", lambda e: e.tensor_tensor(out=o["KAP"][:, :, :], in0=KK[:, :, :], in1=E[:, :, :], op=ALU.mult), [KK, E], [o["KAP"]])
                V("act", lambda e: e.activation(out=E[:, :, :], in_=Lr[:, :, :], func=AF.Exp, scale=-1.0), [Lr], [E])
                V("dve", lambda e: e.tensor_tensor(out=o["BET"][:, :, :], in0=Bv[:, :, :], in1=E[:, :, :], op=ALU.mult), [Bv, E], [o["BET"]])
                V("pool", lambda e: e.tensor_tensor(out=o["KTI"][:, :, :], in0=KM[:, :, :], in1=E[:, :, :], op=ALU.mult), [KM, E], [o["KTI"]])
                for n in ("RHO", "KAP", "BET", "KTI"):
                    kb.dma("sp", fmv(S[n][d_])[:, :, t0:t0 + W], o[n][:, :, :], o[n], r=[o[n]], w=[("dram", id(S[n][d_]))])
                transpose_store(o["BET"], S["bt"][d_], t0)
                transpose_store(o["KTI"], S["kt"][d_], t0)


def _ln_stats_w(self, x, Wc, pl):
    kb = self.kb
    xb, sq = pl["xb"].next(), pl["sq"].next()
    kb.op("act", lambda e: e.activation(out=xb[:, :, :Wc], in_=x[:, :, :Wc], func=AF.Copy), r=[x], w=[xb])
    kb.op("act", lambda e: e.activation(out=sq[:, :, :Wc], in_=x[:, :, :Wc], func=AF.Square), r=[x], w=[sq])
    P1, P2 = self.psum.next(), self.psum.next()
    for c in range(NCH):
        kb.op("pe", lambda e: e.matmul(P1[:, 0:Wc], lhsT=self.onesb[:, :], rhs=xb[:, c, :Wc],
                                       start=(c == 0), stop=(c == NCH - 1)), r=[xb, self.onesb], w=[P1])
    for c in range(NCH):
        kb.op("pe", lambda e: e.matmul(P2[:, 0:Wc], lhsT=self.onesb[:, :], rhs=sq[:, c, :Wc],
                                       start=(c == 0), stop=(c == NCH - 1)), r=[sq, self.onesb], w=[P2])
    m2, var, rstd, nmr = pl["m2"].next(), pl["var"].next(), pl["rstd"].next(), pl["nmr"].next()
    kb.op("act", lambda e: e.activation(out=m2[:, 0, :Wc], in_=P1[:, 0:Wc], func=AF.Square), r=[P1], w=[m2])
    kb.op("dve", lambda e: e.scalar_tensor_tensor(out=var[:, 0, :Wc], in0=P2[:, 0:Wc], scalar=LN_EPS,
                                                  in1=m2[:, 0, :Wc], op0=ALU.add, op1=ALU.subtract), r=[P2, m2], w=[var])
    kb.op("act", lambda e: e.activation(out=var[:, 0, :Wc], in_=var[:, 0, :Wc], func=AF.Sqrt), r=[var], w=[var])
    kb.op("dve", lambda e: e.reciprocal(out=rstd[:, 0, :Wc], in_=var[:, 0, :Wc]), r=[var], w=[rstd])
    kb.op("dve", lambda e: e.scalar_tensor_tensor(out=nmr[:, 0, :Wc], in0=P1[:, 0:Wc], scalar=-1.0,
                                                  in1=rstd[:, 0, :Wc], op0=ALU.mult, op1=ALU.mult), r=[P1, rstd], w=[nmr])
    return rstd, nmr


Model.rwkv_declare = _rwkv_declare
Model.rwkv_prep_phase = _rwkv_prep_phase
Model.ln_stats_w = _ln_stats_w


def _rwkv_scan_dir(self, l, d, S, st, psr):
    kb = self.kb
    NT = self.NT
    nchunk = NT // 128
    fmv = lambda t_: t_.rearrange("(c p) t -> p c t", p=128)
    V = lambda e_, f, r, w: kb.op(e_, f, r=r, w=w)
    sbf = lambda n, shp, dt=F32: kb.sb(n, shp, dt, st)
    MT, MN = sbf("MT", [128, 512], BF16), sbf("MN", [128, 512], BF16)
    kb.dma("pool", MT[:, :], self.rk_MT[d], MT, w=[MT])
    kb.dma("pool", MN[:, :], self.rk_MN[d], MN, w=[MN])
    KRp = Pool(kb, "KR", 2, [128, 4, 2, 128], BF16, stack=st)
    BTZp = Pool(kb, "BTZ", 2, [128, 4, 2, 128], BF16, stack=st)
    KTZp = Pool(kb, "KTZ", 2, [128, 4, 2, 128], BF16, stack=st)
    KAZp = Pool(kb, "KAZ", 2, [128, 4, 2, 128], BF16, stack=st)
    for p_ in (BTZp, KTZp, KAZp):
        for b_ in p_.bufs:
            V("pool", lambda e: e.memset(b_[:, :, :, :], 0.0), [], [b_])
    S0Z = sbf("S0Z", [128, 4, 2, 64], BF16)
    V("pool", lambda e: e.memset(S0Z[:, :, :, :], 0.0), [], [S0Z])
    St = sbf("St", [128, 4, 64])
    V("pool", lambda e: e.memset(St[:, :, :], 0.0), [], [St])
    St1 = sbf("St1", [128, 4, 64])
    tokp = {n: Pool(kb, n, 2, [128, 512], BF16, stack=st) for n in ("Btok", "Ktok", "Vtok")}
    scp = Pool(kb, "sc", 2, [128, 4, 3], F32, stack=st)
    AMp = Pool(kb, "AM", 2, [128, 8, 512], BF16, stack=st)
    Pm = [Pool(kb, "Pm%d" % i, 2, [128, 8, 128], BF16, stack=st) for i in range(2)]
    PTm = Pool(kb, "PTm", 2, [128, 8, 128], BF16, stack=st)
    X32, X16p = sbf("X32", [128, 512]), Pool(kb, "X16", 2, [128, 512], BF16, stack=st)
    Oop = Pool(kb, "Oo", 2, [128, 512], F32, stack=st)
    order = [0, 1] + list(range(2, nchunk))
    if d == 1:
        order = [1, 0] + list(range(nchunk - 1, 1, -1))
    ev = 0
    yield
    for ci_, ch in enumerate(order):
        if ci_ > 0:
            yield
        t0 = ch * 128
        KR, BTZ, KTZ, KAZ = KRp.next(), BTZp.next(), KTZp.next(), KAZp.next()
        kb.dma("sp", KR[:, :, 0, :], fmv(S["KAP"][d])[:, :, t0:t0 + 128], KR, r=[("dram", id(S["KAP"][d]))], w=[KR])
        kb.dma("sp", KR[:, :, 1, :], fmv(S["RHO"][d])[:, :, t0:t0 + 128], KR, r=[("dram", id(S["RHO"][d]))], w=[KR])
        for par in range(2):
            ps_ = slice(par * 64, (par + 1) * 64)
            kb.dma("sp", BTZ[ps_, :, par, :], fmv(S["BET"][d])[ps_, :, t0:t0 + 128], BTZ, r=[("dram", id(S["BET"][d]))], w=[BTZ])
            kb.dma("sp", KTZ[ps_, :, par, :], fmv(S["KTI"][d])[ps_, :, t0:t0 + 128], KTZ, r=[("dram", id(S["KTI"][d]))], w=[KTZ])
            kb.dma("sp", KAZ[ps_, :, par, :], fmv(S["KAP"][d])[ps_, :, t0:t0 + 128], KAZ, r=[("dram", id(S["KAP"][d]))], w=[KAZ])
        tk = {}
        for n, key in (("Btok", "bt"), ("Ktok", "kt"), ("Vtok", "vt")):
            tk[n] = tokp[n].next()
            srcd = S[key][d] if key != "vt" else S[key]
            kb.dma("sp", tk[n][:, :], srcd[t0:t0 + 128, :], tk[n], r=[("dram", id(srcd))], w=[tk[n]])
        Btok, Ktok, Vtok = tk["Btok"], tk["Ktok"], tk["Vtok"]
        sc = scp.next()
        kb.dma("sp", sc[:, :, :], S["sc"][d][:, ch, :, :], sc, r=[("dram", id(S["sc"][d]))], w=[sc])
        for par in range(2):
            ps_ = slice(par * 64, (par + 1) * 64)
            V("pool", lambda e: e.tensor_tensor(out=S0Z[ps_, :, par, :], in0=St[ps_, :, :],
                                                in1=sc[ps_, :, 0:1].broadcast_to([64, 4, 64]), op=ALU.mult), [St, sc], [S0Z])
        AM = AMp.next()
        for h in range(8):
            c, par = h // 2, h % 2
            P = psr.next()
            rhs = KR[:, c, :, :].rearrange("p s t -> p (s t)")
            kb.op("pe", lambda e: e.matmul(P[:, 0:256], lhsT=BTZ[:, c, par, :], rhs=rhs, start=True, stop=True), r=[BTZ, KR], w=[P])
            kb.op("pe", lambda e: e.matmul(P[:, 256:512], lhsT=KTZ[:, c, par, :], rhs=rhs, start=True, stop=True), r=[KTZ, KR], w=[P])
            V("dve", lambda e: e.tensor_tensor(out=AM[:, h, :], in0=P[:, :], in1=MT[:, :], op=ALU.mult), [P, MT], [AM])
        Pj, PTj = Pm[0].next(), PTm.next()
        for g4 in range(2):
            P = psr.next()
            for hh in range(4):
                h = g4 * 4 + hh
                c, par = h // 2, h % 2
                kb.op("pe", lambda e: e.matmul(P[:, hh * 128:(hh + 1) * 128], lhsT=KAZ[:, c, par, :], rhs=BTZ[:, c, par, :],
                                               start=True, stop=True), r=[KAZ, BTZ], w=[P])
            V("dve", lambda e: e.tensor_tensor(out=Pj[:, g4 * 4:(g4 + 1) * 4, :].rearrange("p h t -> p (h t)"), in0=P[:, :],
                                               in1=MN[:, :], op=ALU.mult), [P, MN], [Pj])
        V("act", lambda e: e.activation(out=PTj[:, :, :], in_=AM[:, :, 0:128], func=AF.Copy), [AM], [PTj])
        P = psr.next()
        for h in range(8):
            c, par = h // 2, h % 2
            kb.op("pe", lambda e: e.matmul(P[:, h * 64:(h + 1) * 64], lhsT=KR[:, c, 0, :], rhs=S0Z[:, c, par, :],
                                           start=True, stop=False), r=[KR, S0Z], w=[P])
            kb.op("pe", lambda e: e.matmul(P[:, h * 64:(h + 1) * 64], lhsT=AM[:, h, 256:384], rhs=Vtok[:, h * 64:(h + 1) * 64],
                                           start=False, stop=True), r=[AM, Vtok], w=[P])
        V("act", lambda e: e.activation(out=X32[:, :], in_=P[:, :], func=AF.Identity, scale=-1.0), [P], [X32])
        X16 = X16p.next()
        V("dve", lambda e: e.tensor_copy(out=X16[:, :], in_=X32[:, :]), [X32], [X16])
        for j in range(7):
            P = psr.next()
            for h in range(8):
                kb.op("pe", lambda e: e.matmul(P[:, h * 64:(h + 1) * 64], lhsT=PTj[:, h, :], rhs=X16[:, h * 64:(h + 1) * 64],
                                               start=True, stop=True), r=[PTj, X16], w=[P])
            V("dve", lambda e: e.tensor_tensor(out=X32[:, :], in0=P[:, :], in1=X32[:, :], op=ALU.add), [P, X32], [X32])
            X16 = X16p.next()
            V("act", lambda e: e.activation(out=X16[:, :], in_=X32[:, :], func=AF.Copy), [X32], [X16])
            if j < 6:
                PTn = PTm.next()
                Pn = Pm[(j + 1) % 2].next() if j < 5 else None
                for g4 in range(2):
                    P = psr.next()
                    for hh in range(4):
                        h = g4 * 4 + hh
                        kb.op("pe", lambda e: e.matmul(P[:, hh * 128:(hh + 1) * 128], lhsT=Pj[:, h, :], rhs=PTj[:, h, :],
                                                       start=True, stop=True), r=[Pj, PTj], w=[P])
                    eng = "act" if (ev % 2 == 0) else "dve"
                    ev += 1
                    dst = PTn[:, g4 * 4:(g4 + 1) * 4, :].rearrange("p h t -> p (h t)")
                    if eng == "act":
                        V("act", lambda e: e.activation(out=dst, in_=P[:, :], func=AF.Copy), [P], [PTn])
                    else:
                        V("dve", lambda e: e.tensor_copy(out=dst, in_=P[:, :]), [P], [PTn])
                    if Pn is not None:
                        P = psr.next()
                        for hh in range(4):
                            h = g4 * 4 + hh
                            kb.op("pe", lambda e: e.matmul(P[:, hh * 128:(hh + 1) * 128], lhsT=PTj[:, h, :], rhs=Pj[:, h, :],
                                                           start=True, stop=True), r=[Pj, PTj], w=[P])
                        eng = "act" if (ev % 2 == 0) else "dve"
                        ev += 1
                        dst = Pn[:, g4 * 4:(g4 + 1) * 4, :].rearrange("p h t -> p (h t)")
                        if eng == "act":
                            V("act", lambda e: e.activation(out=dst, in_=P[:, :], func=AF.Copy), [P], [Pn])
                        else:
                            V("dve", lambda e: e.tensor_copy(out=dst, in_=P[:, :]), [P], [Pn])
                PTj = PTn
                if Pn is not None:
                    Pj = Pn
        U16 = X16
        P = psr.next()
        for h in range(8):
            c, par = h // 2, h % 2
            hs = slice(h * 64, (h + 1) * 64)
            kb.op("pe", lambda e: e.matmul(P[:, hs], lhsT=KR[:, c, 1, :], rhs=S0Z[:, c, par, :], start=True, stop=False),
                  r=[KR, S0Z], w=[P])
            kb.op("pe", lambda e: e.matmul(P[:, hs], lhsT=AM[:, h, 128:256], rhs=U16[:, hs], start=False, stop=False),
                  r=[AM, U16], w=[P])
            kb.op("pe", lambda e: e.matmul(P[:, hs], lhsT=AM[:, h, 384:512], rhs=Vtok[:, hs], start=False, stop=True),
                  r=[AM, Vtok], w=[P])
        Oo = Oop.next()
        V("act", lambda e: e.activation(out=Oo[:, :], in_=P[:, :], func=AF.Copy), [P], [Oo])
        kb.dma("act", S["O"][d][t0:t0 + 128, :], Oo[:, :], Oo, r=[Oo], w=[("dram", id(S["O"][d]))])
        P = psr.next()
        for c in range(4):
            cs_ = slice(c * 128, (c + 1) * 128)
            kb.op("pe", lambda e: e.matmul(P[:, cs_], lhsT=Btok[:, cs_], rhs=U16[:, cs_], start=True, stop=False),
                  r=[Btok, U16], w=[P])
            kb.op("pe", lambda e: e.matmul(P[:, cs_], lhsT=Ktok[:, cs_], rhs=Vtok[:, cs_], start=False, stop=True),
                  r=[Ktok, Vtok], w=[P])
        V("pool", lambda e: e.tensor_tensor(out=St1[:, :, :], in0=St[:, :, :], in1=sc[:, :, 2:3].broadcast_to([128, 4, 64]),
                                            op=ALU.mult), [St, sc], [St1])
        pv_ = P[:, :].rearrange("p (c x) -> p c x", c=4)
        for par in range(2):
            ps_ = slice(par * 64, (par + 1) * 64)
            V("dve", lambda e: e.tensor_tensor(out=St[ps_, :, :], in0=pv_[ps_, :, par * 64:(par + 1) * 64],
                                               in1=sc[ps_, :, 1:2].broadcast_to([64, 4, 64]), op=ALU.mult), [P, sc, St1], [St])
        V("pool", lambda e: e.tensor_tensor(out=St[:, :, :], in0=St[:, :, :], in1=St1[:, :, :], op=ALU.add), [St, St1], [St])


def _merge_declare(self):
    L = DEPTH
    self.branch_proj = self.din("branch_proj", [L, 3, 512, D])
    self.w_out = self.din("w_out", [L, D, D])


def _merge_phase(self, l, src, dst, S, ctx):
    kb = self.kb
    W = 256
    NB = W // 128
    lni = (l * 3 + 1) * NCH
    with kb.phase() as st:
        self.psum = Pool(kb, "ps", 8, [128, 512], F32, psum=True, stack=st)
        V = lambda e_, f, r, w: kb.op(e_, f, r=r, w=w)
        sbf = lambda n, shp, dt=F32: kb.sb(n, shp, dt, st)
        bp = sbf("bp", [128, 12, D], BF16)
        wo = sbf("wo", [128, NCH, D], BF16)
        for b_ in range(3):
            for c in range(4):
                kb.dma("pool", bp[:, b_ * 4 + c, :], self.branch_proj[l, b_, c * 128:(c + 1) * 128, :], bp, w=[bp])
        for c in range(NCH):
            kb.dma("pool", wo[:, c, :], self.w_out[l, c * 128:(c + 1) * 128, :], wo, w=[wo])
        idb = sbf("idb", [128, 128], BF16)
        kb.dma("pool", idb[:, :], self.identb[:, :], idb, w=[idb])
        gng, gnb = sbf("gng", [128, 4]), sbf("gnb", [128, 4])
        kb.dma("sp", gng[:, :], self.rk_gng[:, l * 4:(l + 1) * 4], gng, w=[gng])
        kb.dma("sp", gnb[:, :], self.rk_gnb[:, l * 4:(l + 1) * 4], gnb, w=[gnb])
        pl = self.stat_pools(W, st)
        xp = Pool(kb, "x", 2, [128, NCH, W], F32, stack=st)
        xn = sbf("xn", [128, NCH, W])
        Ofp = Pool(kb, "Of", 2, [128, NB, 512], F32, stack=st)
        Obp = Pool(kb, "Ob", 2, [128, NB, 512], F32, stack=st)
        onb = sbf("onb", [128, NB, 512], BF16)
        st8 = [sbf("st8_%d" % i, [128, NB, 8]) for i in range(3)]
        sqt = sbf("sqt", [128, NB, 512])
        Y = {n: Pool(kb, "y" + n, 2, [128, 4, W], BF16, stack=st) for n in ("a", "s", "bon", "g")}
        yr = sbf("yr", [128, 4, W], BF16)
        yt = sbf("yrt", [128, 4, W])
        gp = Pool(kb, "gates", 2, [128, 24, W], BF16, stack=st)
        m1, m2, m3 = sbf("m1", [128, W]), sbf("m2", [128, W]), sbf("m3", [128, W])
        mT = sbf("mT", [128, NCH, W], BF16)
        srcv = src.rearrange("(c p) t -> p c t", p=128)
        dstv = dst.rearrange("(c p) t -> p c t", p=128)
        fmv = lambda t_: t_.rearrange("(c p) t -> p c t", p=128)
        for (seg, t0, _) in self.tiles(W, ctx=ctx):
            x = xp.next()
            kb.dma("sp", x[:, :, :], srcv[:, :, t0:t0 + W], x, r=[("dram", id(src))], w=[x])
            Of, Ob = Ofp.next(), Obp.next()
            kb.dma("sp", Of[:, :, :], S["O"][0][t0:t0 + W, :].rearrange("(b p) n -> p b n", p=128), Of, r=[("dram", id(S["O"][0]))], w=[Of])
            kb.dma("sp", Ob[:, :, :], S["O"][1][t0:t0 + W, :].rearrange("(b p) n -> p b n", p=128), Ob, r=[("dram", id(S["O"][1]))], w=[Ob])
            ld = {}
            for n, key in (("a", "ya"), ("s", "ys"), ("bon", "bon"), ("g", "gR")):
                ld[n] = Y[n].next()
                kb.dma("sp", ld[n][:, :, :], fmv(S[key])[:, :, t0:t0 + W], ld[n], r=[("dram", id(S[key]))], w=[ld[n]])
            gt = gp.next()
            kb.dma("sp", gt[:, :, :], fmv(S["gS"])[:, :, t0:t0 + W], gt, r=[("dram", id(S["gS"]))], w=[gt])
            V("dve", lambda e: e.tensor_tensor(out=Of[:, :, :], in0=Of[:, :, :], in1=Ob[:, :, :], op=ALU.add), [Of, Ob], [Of])
            O4 = Of[:, :, :].rearrange("p b (h v) -> p b h v", h=8)
            sm, vr, rs = st8
            V("dve", lambda e: e.tensor_reduce(out=sm[:, :, :], in_=O4, axis=AX.X, op=ALU.add), [Of], [sm])
            V("dve", lambda e: e.tensor_scalar(out=sm[:, :, :], in0=sm[:, :, :], scalar1=1.0 / 64, scalar2=None, op0=ALU.mult), [sm], [sm])
            V("dve", lambda e: e.tensor_tensor(out=O4, in0=O4, in1=sm[:, :, :].unsqueeze(3).broadcast_to([128, NB, 8, 64]),
                                               op=ALU.subtract), [Of, sm], [Of])
            V("act", lambda e: e.activation(out=sqt[:, :, :], in_=Of[:, :, :], func=AF.Square), [Of], [sqt])
            V("dve", lambda e: e.tensor_reduce(out=vr[:, :, :], in_=sqt[:, :, :].rearrange("p b (h v) -> p b h v", h=8), axis=AX.X,
                                               op=ALU.add), [sqt], [vr])
            V("dve", lambda e: e.tensor_scalar(out=vr[:, :, :], in0=vr[:, :, :], scalar1=1.0 / 64, scalar2=GN_EPS, op0=ALU.mult,
                                               op1=ALU.add), [vr], [vr])
            V("act", lambda e: e.activation(out=vr[:, :, :], in_=vr[:, :, :], func=AF.Sqrt), [vr], [vr])
            V("dve", lambda e: e.reciprocal(out=rs[:, :, :], in_=vr[:, :, :]), [vr], [rs])
            V("dve", lambda e: e.tensor_tensor(out=onb[:, :, :].rearrange("p b (h v) -> p b h v", h=8), in0=O4,
                                               in1=rs[:, :, :].unsqueeze(3).broadcast_to([128, NB, 8, 64]), op=ALU.mult), [Of, rs], [onb])
            for tb in range(NB):
                P = self.psum.next()
                pb = P[:, 0:256].bitcast(BF16)
                for c in range(4):
                    kb.op("pe", lambda e: e.transpose(pb[:, c * 128:(c + 1) * 128], onb[:, tb, c * 128:(c + 1) * 128], idb[:, :]),
                          r=[onb, idb], w=[P])
                for c in range(4):
                    V("act", lambda e: e.activation(out=yt[:, c, tb * 128:(tb + 1) * 128], in_=pb[:, c * 128:(c + 1) * 128],
                                                    func=AF.Identity, scale=gng[:, c:c + 1], bias=gnb[:, c:c + 1]), [P, gng, gnb], [yt])
            V("pool", lambda e: e.tensor_tensor(out=yt[:, :, :], in0=yt[:, :, :], in1=ld["bon"][:, :, :], op=ALU.add), [yt, ld["bon"]], [yt])
            V("dve", lambda e: e.tensor_tensor(out=yr[:, :, :], in0=yt[:, :, :], in1=ld["g"][:, :, :], op=ALU.mult), [yt, ld["g"]], [yr])
            if "yrS" in S:
                kb.dma("sp", fmv(S["yrS"])[:, :, t0:t0 + W], yr[:, :, :], yr, r=[yr], w=[("dram", id(S["yrS"]))])
            ysrc = [ld["a"], yr, ld["s"]]
            for oc in range(NCH):
                Pa, Pb = self.psum.next(), self.psum.next()
                tgt = [(Pa, 0), (Pa, W), (Pb, 0)]
                for b_ in range(3):
                    Pt, o0 = tgt[b_]
                    for kc in range(4):
                        kb.op("pe", lambda e: e.matmul(Pt[:, o0:o0 + W], lhsT=bp[:, b_ * 4 + kc, oc * 128:(oc + 1) * 128],
                                                       rhs=ysrc[b_][:, kc, :], start=(kc == 0), stop=(kc == 3)), r=[bp, ysrc[b_]], w=[Pt])
                V("dve", lambda e: e.tensor_tensor(out=m1[:, :], in0=Pa[:, 0:W], in1=gt[:, oc, :], op=ALU.mult), [Pa, gt], [m1])
                V("dve", lambda e: e.tensor_tensor(out=m2[:, :], in0=Pa[:, W:2 * W], in1=gt[:, 8 + oc, :], op=ALU.mult), [Pa, gt], [m2])
                V("dve", lambda e: e.tensor_tensor(out=m3[:, :], in0=Pb[:, 0:W], in1=gt[:, 16 + oc, :], op=ALU.mult), [Pb, gt], [m3])
                V("pool", lambda e: e.tensor_tensor(out=m1[:, :], in0=m1[:, :], in1=m2[:, :], op=ALU.add), [m1, m2], [m1])
                V("pool", lambda e: e.tensor_tensor(out=mT[:, oc, :], in0=m1[:, :], in1=m3[:, :], op=ALU.add), [m1, m3], [mT])
            V("pool", lambda e: e.tensor_scalar(out=x[:, :, :], in0=x[:, :, :], scalar1=ALPHA, scalar2=None, op0=ALU.mult), [x], [x])
            for oc in range(NCH):
                if oc % 2 == 0:
                    P = self.psum.next()
                o0 = (oc % 2) * W
                for kc in range(NCH):
                    kb.op("pe", lambda e: e.matmul(P[:, o0:o0 + W], lhsT=wo[:, kc, oc * 128:(oc + 1) * 128], rhs=mT[:, kc, :],
                                                   start=(kc == 0), stop=(kc == NCH - 1)), r=[wo, mT], w=[P])
                V("dve", lambda e: e.scalar_tensor_tensor(out=x[:, oc, :], in0=P[:, o0:o0 + W],
                                                          scalar=self.mods[:, 5 * NCH + oc, seg:seg + 1], in1=x[:, oc, :],
                                                          op0=ALU.mult, op1=ALU.add), [P, x, self.mods], [x])
            P = self.psum.next()
            rstd, nmr = self.ln_stats(x, W, P, pl)
            self.normalize(xn, x, W, rstd, nmr)
            for c in range(NCH):
                V("act", lambda e: e.activation(out=xn[:, c, :], in_=xn[:, c, :], func=AF.Identity,
                                                scale=self.lng[:, lni + c:lni + c + 1], bias=self.lnb[:, lni + c:lni + c + 1]),
                  [xn, self.lng, self.lnb], [xn])
            kb.dma("sp", dstv[:, :, t0:t0 + W], xn[:, :, :], xn, r=[xn], w=[("dram", id(dst))])


Model.rwkv_scan_dir = _rwkv_scan_dir
Model.merge_declare = _merge_declare
Model.merge_phase = _merge_phase


def host_inputs_rwkv(inp):
    L = DEPTH
    d = {}
    w = inp["w_in"]
    rw = w[:, :, 768:2624]
    r, k, v = rw[:, :, 0:512], rw[:, :, 512:1024], rw[:, :, 1024:1536]
    wlo, alo, glo = rw[:, :, 1536:1664], rw[:, :, 1664:1728], rw[:, :, 1728:1856]
    pad = np.zeros_like(alo)
    d["w_inB"] = np.ascontiguousarray(np.concatenate([r, k, v, wlo, glo, alo, pad], -1))
    mu = inp["rwkv_mu"]
    mu_r = np.concatenate([mu[:, 0:1536], mu[:, 1536:1664], mu[:, 1728:1856], mu[:, 1664:1728], np.zeros((L, 64), np.float32)], -1)
    d["rk_mu"] = np.ascontiguousarray(mu_r.reshape(L * NCH_B, 128).T)
    d["rk_w0"] = np.ascontiguousarray(inp["rwkv_w0"].reshape(L * 8, 128).T)
    for n, src in (("a0", "rwkv_a0"), ("kk", "rwkv_k_k"), ("ka", "rwkv_k_a"), ("rk", "rwkv_r_k"), ("gng", "rwkv_gn_g"), ("gnb", "rwkv_gn_b")):
        d["rk_" + n] = np.ascontiguousarray(inp[src].reshape(L * 4, 128).T)
    d["rk_w2"] = np.ascontiguousarray(inp["rwkv_w2"].reshape(L, 128, 512))
    d["rk_a2"] = inp["rwkv_a2"]
    d["rk_g2"] = inp["rwkv_g2"]
    i = np.arange(128)
    d["bd64"] = np.ascontiguousarray(((i[:, None] // 64) == (i[None, :] // 64)).astype(np.float32))
    d["identb"] = np.eye(128, dtype=np.float32)
    on = np.ones((2, 128, 128), np.float32)
    on[0, :, 0] = 0.0
    on[1, :, 127] = 0.0
    d["ones0"] = on
    MT = np.zeros((2, 128, 512), np.float32)
    MN = np.zeros((2, 128, 512), np.float32)
    ii, tt = i[:, None], i[None, :]
    for dd in range(2):
        prev = (ii < tt) if dd == 0 else (ii > tt)
        incl = prev | (ii == tt)
        MT[dd, :, 0:128] = -(prev.astype(np.float32))
        MT[dd, :, 128:256] = incl
        MT[dd, :, 256:384] = prev
        MT[dd, :, 384:512] = incl
        MN[dd] = np.tile(-(prev.T.astype(np.float32)), (1, 4))
    d["rk_MT"], d["rk_MN"] = MT, MN
    d["branch_proj"] = inp["branch_proj"]
    d["w_out"] = inp["w_out"]
    return d


def _make_scratch(self):
    NT = self.NT
    S = {}
    for n, shp, dt in (("qS", [512, NT], BF16), ("kS", [256, NT], BF16), ("vS", [NT, 128], BF16), ("usS", [512, NT], F32),
                       ("gS", [3072, NT], BF16), ("ya", [512, NT], BF16), ("ysb", [512, NT], F32), ("ys", [512, NT], BF16),
                       ("gR", [512, NT], BF16), ("bon", [512, NT], BF16), ("vt", [NT, 512], BF16)):
        S[n] = self.scratch(n, shp, dt)
    for n in ("RHO", "KAP", "BET", "KTI"):
        S[n] = [self.scratch("%s%d" % (n, d), [512, NT], BF16) for d in range(2)]
    for n in ("bt", "kt"):
        S[n] = [self.scratch("%s%d" % (n, d), [NT, 512], BF16) for d in range(2)]
    S["sc"] = [self.scratch("sc%d" % d, [128, NT // 128, 4, 3], F32) for d in range(2)]
    S["O"] = [self.scratch("O%d" % d, [NT, 512], F32) for d in range(2)]
    if "yrS" in self.dbg:
        S["yrS"] = self.scratch("yrS", [512, NT], BF16)
    return S


def _mixer(self, l, src, dst, S, ctx_out):
    self.mixA_phase(l, src, S["qS"], S["kS"], S["vS"], S["usS"], S["gS"])
    self.attn_phase(l, S["qS"], S["kS"], S["vS"], S["ya"])
    self.rwkv_prep_phase(l, src, S)
    self.scan_phase(l, S)
    self.merge_phase(l, src, dst, S, ctx=ctx_out)


def _scan_phase(self, l, S):
    kb = self.kb
    for (ds5, drk) in ((1, 0), (0, 1)):
        with kb.phase() as st:
            PS2 = Pool(kb, "ps2", 2, [128, 1024], F32, psum=True, stack=st)
            psr = Pool(kb, "psr", 4, [128, 512], F32, psum=True, stack=st)
            g1 = self.s5_dir(l, ds5, S["usS"], S["ysb"], S["ys"], st, PS2)
            next(g1)
            g2 = self.rwkv_scan_dir(l, drk, S, st, psr)
            next(g2)
            alive = [g1, g2]
            while alive:
                for g in list(alive):
                    try:
                        next(g)
                    except StopIteration:
                        alive.remove(g)


Model.scan_phase = _scan_phase
Model.make_scratch = _make_scratch
Model.mixer = _mixer


def build_model(T, dbg=()):
    m = Model(T, dbg=dbg)
    m.declare_inputs()
    m.mixA_declare()
    m.s5_declare()
    m.rwkv_declare()
    m.merge_declare()
    m.setup_consts()
    NT = m.NT
    S = m.make_scratch()
    streams = [m.scratch("str%d" % i, [D, NT], F32) for i in range(3)]
    outT = m.dout("outT", [D, T])
    cur = m.xT
    for l in range(DEPTH):
        last = l == DEPTH - 1
        m.adaln_phase(l)
        m.ffn_phase(l, 0, cur, streams[0], 0, ctx=True)
        m.mixer(l, streams[0], streams[1], S, ctx_out=not last)
        if last:
            m.ffn_phase(l, 1, streams[1], outT, CTX, ctx=False)
        else:
            m.ffn_phase(l, 1, streams[1], streams[2], 0, ctx=True)
            cur = streams[2]
    m.kb.finish()
    return m


def all_host_inputs(inp, b, T):
    d = host_inputs(inp, b, T)
    d.update(host_inputs_A(inp, T))
    d.update(host_inputs_s5(inp))
    d.update(host_inputs_rwkv(inp))
    return d


T_FULL = 8192
N_CORES = 8


def kernel(**inputs):
    inp = {k: np.asarray(v) for k, v in inputs.items()}
    m = build_model(T_FULL)
    B = inp["x"].shape[0]
    shared = None
    in_maps = []
    for core in range(N_CORES):
        b = core % B
        d = all_host_inputs(inp, b, T_FULL) if shared is None else dict(shared)
        if shared is None:
            shared = d
        else:
            d["xT"] = np.ascontiguousarray(np.concatenate([inp["ctx"][b], inp["x"][b, :T_FULL]], 0).T)
            cond = np.stack([inp["c"][b], inp["c_ctx"]], -1)
            d["condT"] = np.ascontiguousarray(cond.reshape(NCH, 128, 2).transpose(1, 0, 2))
        in_maps.append({k: v for k, v in d.items() if k in m.dram_in})
    res = run_bass_kernel_spmd(m.nc, in_maps, core_ids=list(range(N_CORES)))
    out = np.stack([np.ascontiguousarray(res.results[b]["outT"].T) for b in range(B)], 0)
    return out.astype(np.float32)
```

```python
import contextlib
import numpy as np
import concourse.bass as bass
import concourse.mybir as mybir
from concourse.bass_utils import run_bass_kernel_spmd

F32 = mybir.dt.float32
BF16 = mybir.dt.bfloat16
AF = mybir.ActivationFunctionType
ALU = mybir.AluOpType
AX = mybir.AxisListType

D = 1024
NCH = 8
CTX = 256
DFF = 2816
NFF = 22
DEPTH = 2
ALPHA = (2.0 * DEPTH) ** 0.25
LN_EPS = 1e-6
DECAY_SCALE = 0.606531
GN_EPS = 64e-5
N_IN = 6208


class Sem:
    _n = 0

    def __init__(self, h):
        self.h = h
        Sem._n += 1
        self.uid = Sem._n


class Buf:
    def __init__(self, name, t):
        self.name = name
        self.t = t
        self.dsem = None
        self.dcnt = 0

    def __getitem__(self, k):
        return self.t[k]

    def __repr__(self):
        return "Buf(%s)" % self.name


class KB:
    EPOCH = 20000

    def __init__(self, nc):
        self.nc = nc
        self.es = contextlib.ExitStack()
        self.eng = {"pe": nc.tensor, "dve": nc.vector, "act": nc.scalar, "pool": nc.gpsimd, "sp": nc.sync}
        self.esem = {}
        self.ecnt = {}
        for e in self.eng:
            self.esem[e] = Sem(self.es.enter_context(nc.semaphore("c_%s_0" % e)))
            self.ecnt[e] = 0
        self.eepoch = {e: 0 for e in self.eng}
        self.seen = {e: {} for e in self.eng}
        self.lastw = {}
        self.reads = {}
        self.nbuf = 0
        self.ninstr = 0
        self.nwait = 0
        self.all_events = {}
        self.free_dsems = []
        self.phase_bufs = []
        self.ndsem = 0

    def sb(self, name, shape, dtype, stack=None):
        self.nbuf += 1
        t = (stack or self.es).enter_context(self.nc.sbuf_tensor("%s_%d" % (name, self.nbuf), list(shape), dtype))
        b = Buf(name, t)
        if stack is not None:
            self.phase_bufs.append(b)
        return b

    def ps(self, name, shape, dtype=F32, stack=None):
        self.nbuf += 1
        t = (stack or self.es).enter_context(self.nc.psum_tensor("%s_%d" % (name, self.nbuf), list(shape), dtype))
        return Buf(name, t)

    def _dsem(self, b):
        if b.dsem is None:
            if self.free_dsems:
                b.dsem, b.dcnt = self.free_dsems.pop()
            else:
                self.ndsem += 1
                b.dsem = Sem(self.es.enter_context(self.nc.semaphore("d_%d" % self.ndsem)))
                b.dcnt = 0
        return b.dsem

    @contextlib.contextmanager
    def phase(self):
        st = contextlib.ExitStack()
        self.phase_bufs = []
        try:
            yield st
        finally:
            self.barrier()
            for b in self.phase_bufs:
                if b.dsem is not None:
                    self.free_dsems.append((b.dsem, b.dcnt))
                    b.dsem = None
            self.phase_bufs = []
            st.close()

    def _need(self, e, r, w):
        need = {}

        def add(evs):
            for uid, (s, v) in evs.items():
                if uid not in need or need[uid][1] < v:
                    need[uid] = (s, v)
        for k in r:
            add(self.lastw.get(k, {}))
        for k in w:
            add(self.lastw.get(k, {}))
            add(self.reads.get(k, {}))
        return need

    def _wait(self, e, need, own_ok):
        eng = self.eng[e]
        for uid, (s, v) in need.items():
            if own_ok and uid == self.esem[e].uid:
                continue
            if self.seen[e].get(uid, 0) >= v:
                continue
            eng.wait_ge(s.h, v)
            self.nwait += 1
            self.seen[e][uid] = v

    def _record(self, ev, r, w):
        uid = ev[0].uid
        for k in r:
            self.reads.setdefault(k, {})[uid] = ev
        for k in w:
            self.lastw.setdefault(k, {})[uid] = ev
            self.reads[k] = {}
        self.all_events[uid] = ev

    def _bump(self, e):
        if self.ecnt[e] >= self.EPOCH:
            self.eepoch[e] += 1
            self.esem[e] = Sem(self.es.enter_context(self.nc.semaphore("c_%s_%d" % (e, self.eepoch[e]))))
            self.ecnt[e] = 0
        self.ecnt[e] += 1
        return (self.esem[e], self.ecnt[e])

    def op(self, e, fn, r=(), w=(), same_ok=False):
        need = self._need(e, r, w)
        self._wait(e, need, own_ok=(e == "pe" or same_ok))
        ins = fn(self.eng[e])
        ev = self._bump(e)
        ins.then_inc(ev[0].h, 1)
        self._record(ev, r, w)
        self.ninstr += 1
        return ins

    def dma(self, q, out, in_, sbuf, r=(), w=(), **kw):
        need = self._need(q, r, w)
        self._wait(q, need, own_ok=False)
        s = self._dsem(sbuf)
        ins = self.eng[q].dma_start(out=out, in_=in_, **kw)
        sbuf.dcnt += 16
        ev = (s, sbuf.dcnt)
        ins.then_inc(s.h, 16)
        self._record(ev, r, w)
        self.ninstr += 1
        return ins

    def barrier(self):
        for e in self.eng:
            self._wait(e, dict(self.all_events), own_ok=True)
        self.lastw = {}
        self.reads = {}
        self.all_events = {}

    def finish(self, e="sp"):
        self._wait(e, dict(self.all_events), own_ok=True)


class Pool:
    def __init__(self, kb, name, n, shape, dtype, psum=False, stack=None):
        self.bufs = [(kb.ps if psum else kb.sb)("%s%d" % (name, i), shape, dtype, stack=stack) for i in range(n)]
        self.i = 0

    def next(self):
        b = self.bufs[self.i % len(self.bufs)]
        self.i += 1
        return b


class Model:
    def __init__(self, T, dbg=(), nlayers=DEPTH, stop_after=None):
        self.T = T
        self.NT = CTX + T
        self.dbg = set(dbg)
        self.nlayers = nlayers
        self.stop_after = stop_after
        nc = bass.Bass("TRN2", target_bir_lowering=False)
        self.nc = nc
        self.kb = KB(nc)
        self.dram_in = {}
        self.dram_out = {}
        self.scr_n = 0

    def din(self, name, shape, dtype=F32):
        t = self.nc.dram_tensor(name, list(shape), dtype, kind="ExternalInput").ap()
        self.dram_in[name] = t
        return t

    def dout(self, name, shape, dtype=F32):
        t = self.nc.dram_tensor(name, list(shape), dtype, kind="ExternalOutput").ap()
        self.dram_out[name] = t
        return t

    def scratch(self, name, shape, dtype=F32):
        if name in self.dbg:
            return self.dout(name, shape, dtype)
        return self.nc.dram_tensor(name, list(shape), dtype, kind="Internal").ap()

    def tiles(self, W, ctx=True, lat=True):
        out = []
        if ctx:
            for t0 in range(0, CTX, W):
                out.append((1, t0, W))
        if lat:
            for t0 in range(0, self.T, W):
                out.append((0, CTX + t0, W))
        return out

    def declare_inputs(self):
        L = DEPTH
        self.xT = self.din("xT", [D, self.NT])
        self.condT = self.din("condT", [128, NCH, 2])
        self.w_ada = self.din("w_ada", [L, D, 9 * D])
        self.b_ada = self.din("b_ada", [L, 128, 72])
        self.ln_g = self.din("ln_g", [128, L * 3 * NCH])
        self.ln_b = self.din("ln_b", [128, L * 3 * NCH])
        self.ffn_w_in = self.din("ffn_w_in", [L, 2, D, 2 * DFF])
        self.ffn_w_out = self.din("ffn_w_out", [L, 2, DFF, D])

    def setup_consts(self):
        kb = self.kb
        self.onesb = kb.sb("onesb", [128, 128], BF16)
        kb.op("dve", lambda e: e.memset(self.onesb[:], 1.0 / D), w=[self.onesb])
        self.lng = kb.sb("lng", [128, DEPTH * 3 * NCH], F32)
        self.lnb = kb.sb("lnb", [128, DEPTH * 3 * NCH], F32)
        kb.dma("sp", self.lng[:], self.ln_g[:, :], self.lng, w=[self.lng])
        kb.dma("sp", self.lnb[:], self.ln_b[:, :], self.lnb, w=[self.lnb])
        self.scond = kb.sb("scond", [128, NCH, 2], F32)
        kb.dma("sp", self.scond[:], self.condT[:, :, :], self.scond, w=[self.scond])
        kb.op("act", lambda e: e.activation(out=self.scond[:], in_=self.scond[:], func=AF.Silu),
              r=[self.scond], w=[self.scond])
        self.mods = kb.sb("mods", [128, 72, 2], F32)
        self.modp1 = kb.sb("modp1", [128, 72, 2], F32)
        self.modh = kb.sb("modh", [128, 72, 2], F32)

    def adaln_phase(self, l):
        kb = self.kb
        with kb.phase() as st:
            self.psum = Pool(kb, "ps", 8, [128, 512], F32, psum=True, stack=st)
            wa = Pool(kb, "wa", 2, [128, NCH, D], F32, stack=st)
            bada = kb.sb("bada", [128, 72], F32, st)
            kb.dma("sp", bada[:], self.b_ada[l, :, :], bada, w=[bada])
            P = self.psum.next()
            for m in range(9):
                w = wa.next()
                for kc in range(NCH):
                    kb.dma("sp" if kc % 2 == 0 else "act", w[:, kc, :],
                           self.w_ada[l, kc * 128:(kc + 1) * 128, m * D:(m + 1) * D], w, w=[w])
                for oc in range(NCH):
                    j = m * NCH + oc
                    for kc in range(NCH):
                        kb.op("pe", lambda e: e.matmul(P[:, 2 * j:2 * j + 2], lhsT=w[:, kc, oc * 128:(oc + 1) * 128],
                                                       rhs=self.scond[:, kc, :], start=(kc == 0), stop=(kc == NCH - 1)),
                              r=[w, self.scond], w=[P])
            kb.op("dve", lambda e: e.tensor_tensor(out=self.mods[:], in0=P[:, 0:144].rearrange("p (j s) -> p j s", s=2),
                                                   in1=bada[:, :].unsqueeze(2).broadcast_to([128, 72, 2]), op=ALU.add),
                  r=[P, bada], w=[self.mods])
            kb.op("dve", lambda e: e.tensor_scalar(out=self.modp1[:], in0=self.mods[:], scalar1=1.0, scalar2=None,
                                                   op0=ALU.add), r=[self.mods], w=[self.modp1])
            kb.op("dve", lambda e: e.tensor_scalar(out=self.modh[:], in0=self.mods[:], scalar1=0.5, scalar2=None,
                                                   op0=ALU.mult), r=[self.mods], w=[self.modh])

    def ln_stats(self, x, W, P, pl):
        kb = self.kb
        xb, sq = pl["xb"].next(), pl["sq"].next()
        kb.op("act", lambda e: e.activation(out=xb[:, :, :W], in_=x[:, :, :W], func=AF.Copy), r=[x], w=[xb])
        kb.op("act", lambda e: e.activation(out=sq[:, :, :W], in_=x[:, :, :W], func=AF.Square), r=[x], w=[sq])
        for c in range(NCH):
            kb.op("pe", lambda e: e.matmul(P[:, 0:W], lhsT=self.onesb[:, :], rhs=xb[:, c, :W],
                                           start=(c == 0), stop=(c == NCH - 1)), r=[xb, self.onesb], w=[P])
        for c in range(NCH):
            kb.op("pe", lambda e: e.matmul(P[:, W:2 * W], lhsT=self.onesb[:, :], rhs=sq[:, c, :W],
                                           start=(c == 0), stop=(c == NCH - 1)), r=[sq, self.onesb], w=[P])
        m2, var, rstd, nmr = pl["m2"].next(), pl["var"].next(), pl["rstd"].next(), pl["nmr"].next()
        kb.op("act", lambda e: e.activation(out=m2[:, 0, :W], in_=P[:, 0:W], func=AF.Square), r=[P], w=[m2])
        kb.op("dve", lambda e: e.scalar_tensor_tensor(out=var[:, 0, :W], in0=P[:, W:2 * W], scalar=LN_EPS,
                                                      in1=m2[:, 0, :W], op0=ALU.add, op1=ALU.subtract),
              r=[P, m2], w=[var])
        kb.op("act", lambda e: e.activation(out=var[:, 0, :W], in_=var[:, 0, :W], func=AF.Sqrt), r=[var], w=[var])
        kb.op("dve", lambda e: e.reciprocal(out=rstd[:, 0, :W], in_=var[:, 0, :W]), r=[var], w=[rstd])
        kb.op("dve", lambda e: e.scalar_tensor_tensor(out=nmr[:, 0, :W], in0=P[:, 0:W], scalar=-1.0,
                                                      in1=rstd[:, 0, :W], op0=ALU.mult, op1=ALU.mult),
              r=[P, rstd], w=[nmr])
        return rstd, nmr

    def normalize(self, out, x, W, rstd, nmr):
        kb = self.kb
        kb.op("dve", lambda e: e.tensor_tensor(out=out[:, :, :W], in0=x[:, :, :W],
                                               in1=rstd[:, 0:1, :W].broadcast_to([128, NCH, W]), op=ALU.mult),
              r=[x, rstd], w=[out])
        kb.op("pool", lambda e: e.tensor_tensor(out=out[:, :, :W], in0=out[:, :, :W],
                                                in1=nmr[:, 0:1, :W].broadcast_to([128, NCH, W]), op=ALU.add),
              r=[out, nmr], w=[out])

    def stat_pools(self, W, st):
        kb = self.kb
        return {
            "xb": Pool(kb, "xb", 1, [128, NCH, W], BF16, stack=st),
            "sq": Pool(kb, "sq", 1, [128, NCH, W], BF16, stack=st),
            "m2": Pool(kb, "m2", 2, [128, 1, W], F32, stack=st),
            "var": Pool(kb, "var", 2, [128, 1, W], F32, stack=st),
            "rstd": Pool(kb, "rstd", 2, [128, 1, W], F32, stack=st),
            "nmr": Pool(kb, "nmr", 2, [128, 1, W], F32, stack=st),
        }

    def ffn_phase(self, l, s, src, dst, dst_off, ctx):
        kb = self.kb
        W = 256
        mb = 0 if s == 0 else 6
        lni = (l * 3 + (0 if s == 0 else 2)) * NCH
        with kb.phase() as st:
            self.psum = Pool(kb, "ps", 8, [128, 512], F32, psum=True, stack=st)
            w1 = kb.sb("w1", [128, NCH, 2 * DFF], BF16, st)
            w2 = kb.sb("w2", [128, NFF, D], BF16, st)
            for c in range(NCH):
                for hh in range(2):
                    kb.dma("pool", w1[:, c, hh * DFF:(hh + 1) * DFF],
                           self.ffn_w_in[l, s, c * 128:(c + 1) * 128, hh * DFF:(hh + 1) * DFF], w1, w=[w1])
            for c in range(NFF):
                kb.dma("pool", w2[:, c, :], self.ffn_w_out[l, s, c * 128:(c + 1) * 128, :], w2, w=[w2])
            pl = self.stat_pools(W, st)
            xp = Pool(kb, "x", 2, [128, NCH, W], F32, stack=st)
            xnp = Pool(kb, "xn", 2, [128, NCH, W], F32, stack=st)
            up = Pool(kb, "u", 2, [128, NCH, W], BF16, stack=st)
            hp = Pool(kb, "h", 1, [128, NFF, W], BF16, stack=st)
            gp = Pool(kb, "g", 2, [128, W], F32, stack=st)
            srcv = src.rearrange("(c p) t -> p c t", p=128)
            dstv = dst.rearrange("(c p) t -> p c t", p=128)
            tl = self.tiles(W, ctx=ctx)
            T_ = {}

            def stage_a(i):
                seg, t0, _ = tl[i]
                x = xp.next()
                kb.dma("sp", x[:, :, :], srcv[:, :, t0:t0 + W], x, r=[("dram", id(src))], w=[x])
                P = self.psum.next()
                rstd, nmr = self.ln_stats(x, W, P, pl)
                xn = xnp.next()
                self.normalize(xn, x, W, rstd, nmr)
                u = up.next()
                for c in range(NCH):
                    kb.op("act", lambda e: e.activation(out=u[:, c, :], in_=xn[:, c, :], func=AF.Identity,
                                                        scale=self.modp1[:, (mb + 1) * NCH + c, seg:seg + 1],
                                                        bias=self.mods[:, mb * NCH + c, seg:seg + 1]),
                          r=[xn, self.modp1, self.mods], w=[u])
                kb.op("pool", lambda e: e.tensor_scalar(out=x[:, :, :], in0=x[:, :, :], scalar1=ALPHA, scalar2=None,
                                                        op0=ALU.mult), r=[x], w=[x])
                T_[i] = (x, xn, u)

            def stage_b(i):
                x, xn, u = T_[i]
                h = hp.next()
                for j in range(NFF):
                    P = self.psum.next()
                    for c in range(NCH):
                        kb.op("pe", lambda e: e.matmul(P[:, 0:W], lhsT=w1[:, c, j * 128:(j + 1) * 128], rhs=u[:, c, :],
                                                       start=(c == 0), stop=(c == NCH - 1)), r=[w1, u], w=[P])
                    for c in range(NCH):
                        kb.op("pe", lambda e: e.matmul(P[:, W:2 * W], lhsT=w1[:, c, DFF + j * 128:DFF + (j + 1) * 128],
                                                       rhs=u[:, c, :], start=(c == 0), stop=(c == NCH - 1)),
                              r=[w1, u], w=[P])
                    g = gp.next()
                    kb.op("act", lambda e: e.activation(out=g[:, :], in_=P[:, 0:W], func=AF.Silu), r=[P], w=[g])
                    kb.op("dve", lambda e: e.tensor_tensor(out=h[:, j, :], in0=P[:, W:2 * W], in1=g[:, :], op=ALU.mult),
                          r=[P, g], w=[h])
                T_[i] = (x, xn, u, h)

            def stage_c(i):
                seg, t0, _ = tl[i]
                x, xn, u, h = T_.pop(i)
                for oc in range(NCH):
                    if oc % 2 == 0:
                        P = self.psum.next()
                    o0 = (oc % 2) * W
                    for j in range(NFF):
                        kb.op("pe", lambda e: e.matmul(P[:, o0:o0 + W], lhsT=w2[:, j, oc * 128:(oc + 1) * 128],
                                                       rhs=h[:, j, :], start=(j == 0), stop=(j == NFF - 1)),
                              r=[w2, h], w=[P])
                    kb.op("dve", lambda e: e.scalar_tensor_tensor(
                        out=x[:, oc, :], in0=P[:, o0:o0 + W], scalar=self.modh[:, (mb + 2) * NCH + oc, seg:seg + 1],
                        in1=x[:, oc, :], op0=ALU.mult, op1=ALU.add), r=[P, x, self.modh], w=[x])
                P = self.psum.next()
                rstd, nmr = self.ln_stats(x, W, P, pl)
                self.normalize(xn, x, W, rstd, nmr)
                for c in range(NCH):
                    kb.op("act", lambda e: e.activation(out=xn[:, c, :], in_=xn[:, c, :], func=AF.Identity,
                                                        scale=self.lng[:, lni + c:lni + c + 1],
                                                        bias=self.lnb[:, lni + c:lni + c + 1]),
                          r=[xn, self.lng, self.lnb], w=[xn])
                kb.dma("sp", dstv[:, :, t0 - dst_off:t0 - dst_off + W], xn[:, :, :], xn,
                       r=[xn], w=[("dram", id(dst))])

            stage_a(0)
            for i in range(len(tl)):
                stage_b(i)
                if i + 1 < len(tl):
                    stage_a(i + 1)
                stage_c(i)


def host_inputs(inp, b, T):
    L = DEPTH
    d = {}
    d["xT"] = np.ascontiguousarray(np.concatenate([inp["ctx"][b], inp["x"][b, :T]], 0).T)
    cond = np.stack([inp["c"][b], inp["c_ctx"]], -1)
    d["condT"] = np.ascontiguousarray(cond.reshape(NCH, 128, 2).transpose(1, 0, 2))
    d["w_ada"] = inp["w_ada"]
    d["b_ada"] = np.ascontiguousarray(inp["b_ada"].reshape(L, 72, 128).transpose(0, 2, 1))
    d["ln_g"] = np.ascontiguousarray(inp["ln_g"].reshape(L * 3 * NCH, 128).T)
    d["ln_b"] = np.ascontiguousarray(inp["ln_b"].reshape(L * 3 * NCH, 128).T)
    d["ffn_w_in"] = inp["ffn_w_in"]
    d["ffn_w_out"] = inp["ffn_w_out"]
    return d


NWA = 10 + 1 + 4 + 24
CH_Q, CH_QS, CH_K, CH_KS, CH_KB, CH_KBS, CH_V, CH_S5, CH_G = 0, 4, 8, 9, 10, 11, 12, 13, 17
NCH_A = 41


def _mixA_declare(self):
    L = DEPTH
    self.w_inA = self.din("w_inA", [L, D, NCH_A * 128])
    self.ropeC = self.din("ropeC", [128, self.NT])
    self.ropeS = self.din("ropeS", [128, self.NT])
    self.sinkT = self.din("sinkT", [L, 64, 8])
    self.maskP = self.din("maskP", [128, 512])
    self.maskN = self.din("maskN", [128, 512])


def _mixA_phase(self, l, src, qS, kS, vS, usS, gS):
    kb = self.kb
    W = 256
    with kb.phase() as st:
        self.psum = Pool(kb, "ps", 8, [128, 512], F32, psum=True, stack=st)
        w = kb.sb("wA", [128, NCH, NCH_A * 128], BF16, st)
        for c in range(NCH):
            for hh in range(2):
                n0, n1 = (0, 21 * 128) if hh == 0 else (21 * 128, NCH_A * 128)
                kb.dma("pool", w[:, c, n0:n1], self.w_inA[l, c * 128:(c + 1) * 128, n0:n1], w, w=[w])
        pl = self.stat_pools(W, st)
        xp = Pool(kb, "x", 2, [128, NCH, W], F32, stack=st)
        xnp = Pool(kb, "xn", 1, [128, NCH, W], F32, stack=st)
        up = Pool(kb, "u", 2, [128, NCH, W], BF16, stack=st)
        cp = Pool(kb, "rc", 2, [128, W], F32, stack=st)
        sp_ = Pool(kb, "rs", 2, [128, W], F32, stack=st)
        t1p = Pool(kb, "t1", 2, [128, W], F32, stack=st)
        t2p = Pool(kb, "t2", 2, [128, W], F32, stack=st)
        qp = Pool(kb, "qo", 2, [128, 4, W], BF16, stack=st)
        kp = Pool(kb, "ko", 2, [128, 2, W], BF16, stack=st)
        vp = Pool(kb, "vo", 2, [128, 2, 128], BF16, stack=st)
        usp = Pool(kb, "uso", 2, [128, 4, W], F32, stack=st)
        gp = Pool(kb, "go", 2, [128, 24, W], BF16, stack=st)
        srcv = src.rearrange("(c p) t -> p c t", p=128)

        def proj(P, o0, ch, u):
            for c in range(NCH):
                kb.op("pe", lambda e: e.matmul(P[:, o0:o0 + W], lhsT=w[:, c, ch * 128:(ch + 1) * 128], rhs=u[:, c, :],
                                               start=(c == 0), stop=(c == NCH - 1)), r=[w, u], w=[P])

        for (seg, t0, _) in self.tiles(W):
            x = xp.next()
            kb.dma("sp", x[:, :, :], srcv[:, :, t0:t0 + W], x, r=[("dram", id(src))], w=[x])
            cT, sT = cp.next(), sp_.next()
            kb.dma("sp", cT[:, :], self.ropeC[:, t0:t0 + W], cT, w=[cT])
            kb.dma("sp", sT[:, :], self.ropeS[:, t0:t0 + W], sT, w=[sT])
            P = self.psum.next()
            rstd, nmr = self.ln_stats(x, W, P, pl)
            xn = xnp.next()
            self.normalize(xn, x, W, rstd, nmr)
            u = up.next()
            for c in range(NCH):
                kb.op("act", lambda e: e.activation(out=u[:, c, :], in_=xn[:, c, :], func=AF.Identity,
                                                    scale=self.modp1[:, 4 * NCH + c, seg:seg + 1],
                                                    bias=self.mods[:, 3 * NCH + c, seg:seg + 1]),
                      r=[xn, self.modp1, self.mods], w=[u])
            qo, ko = qp.next(), kp.next()

            def rope(dst_ap, dst, ch, chs):
                P = self.psum.next()
                proj(P, 0, ch, u)
                proj(P, W, chs, u)
                t1, t2 = t1p.next(), t2p.next()
                kb.op("dve", lambda e: e.tensor_tensor(out=t1[:, :], in0=P[:, 0:W], in1=cT[:, :], op=ALU.mult),
                      r=[P, cT], w=[t1])
                kb.op("dve", lambda e: e.tensor_tensor(out=t2[:, :], in0=P[:, W:2 * W], in1=sT[:, :], op=ALU.mult),
                      r=[P, sT], w=[t2])
                kb.op("pool", lambda e: e.tensor_tensor(out=dst_ap, in0=t1[:, :], in1=t2[:, :], op=ALU.add),
                      r=[t1, t2], w=[dst])
            for c in range(4):
                rope(qo[:, c, :], qo, CH_Q + c, CH_QS + c)
            rope(ko[:, 0, :], ko, CH_K, CH_KS)
            rope(ko[:, 1, :], ko, CH_KB, CH_KBS)
            kb.dma("sp", qS.rearrange("(c p) t -> p c t", p=128)[:, :, t0:t0 + W], qo[:, :, :], qo, r=[qo],
                   w=[("dram", id(qS))])
            kb.dma("sp", kS.rearrange("(c p) t -> p c t", p=128)[:, :, t0:t0 + W], ko[:, :, :], ko, r=[ko],
                   w=[("dram", id(kS))])
            vo = vp.next()
            P = self.psum.next()
            for tb in range(W // 128):
                for c in range(NCH):
                    kb.op("pe", lambda e: e.matmul(P[:, tb * 128:(tb + 1) * 128], lhsT=u[:, c, tb * 128:(tb + 1) * 128],
                                                   rhs=w[:, c, CH_V * 128:(CH_V + 1) * 128],
                                                   start=(c == 0), stop=(c == NCH - 1)), r=[w, u], w=[P])
            kb.op("act", lambda e: e.activation(out=vo[:, :, :], in_=P[:, 0:W].rearrange("p (b n) -> p b n", b=2),
                                                func=AF.Copy), r=[P], w=[vo])
            kb.dma("sp", vS[t0:t0 + W, :].rearrange("(b p) n -> p b n", p=128), vo[:, :, :], vo, r=[vo],
                   w=[("dram", id(vS))])
            uso = usp.next()
            for c in range(4):
                if c % 2 == 0:
                    P = self.psum.next()
                o0 = (c % 2) * W
                proj(P, o0, CH_S5 + c, u)
                kb.op("act", lambda e: e.activation(out=uso[:, c, :], in_=P[:, o0:o0 + W], func=AF.Copy),
                      r=[P], w=[uso])
            kb.dma("sp", usS.rearrange("(c p) t -> p c t", p=128)[:, :, t0:t0 + W], uso[:, :, :], uso, r=[uso],
                   w=[("dram", id(usS))])
            go = gp.next()
            for c in range(24):
                if c % 2 == 0:
                    P = self.psum.next()
                o0 = (c % 2) * W
                proj(P, o0, CH_G + c, u)
                kb.op("act", lambda e: e.activation(out=go[:, c, :], in_=P[:, o0:o0 + W], func=AF.Sigmoid),
                      r=[P], w=[go])
            kb.dma("sp", gS.rearrange("(c p) t -> p c t", p=128)[:, :, t0:t0 + W], go[:, :, :], go, r=[go],
                   w=[("dram", id(gS))])


def _attn_phase(self, l, qS, kS, vS, yaS):
    kb = self.kb
    NB = self.T // 128
    with kb.phase() as st:
        self.psum = Pool(kb, "ps", 8, [128, 512], F32, psum=True, stack=st)
        mP = kb.sb("mP", [128, 512], BF16, st)
        mN = kb.sb("mN", [128, 512], BF16, st)
        kb.dma("pool", mP[:, :], self.maskP[:, :], mP, w=[mP])
        kb.dma("pool", mN[:, :], self.maskN[:, :], mN, w=[mN])
        ones = kb.sb("ones64", [128, 64], BF16, st)
        kb.op("dve", lambda e: e.memset(ones[:], 1.0), w=[ones])
        esk = kb.sb("esk", [64, 8], F32, st)
        kb.dma("sp", esk[:, :], self.sinkT[l, :, :], esk, w=[esk])
        kb.op("act", lambda e: e.activation(out=esk[:, :], in_=esk[:, :], func=AF.Exp), r=[esk], w=[esk])
        eskb = kb.sb("eskb", [64, 8, 128], F32, st)
        kb.op("dve", lambda e: e.tensor_copy(out=eskb[:, :, :], in_=esk[:, :].unsqueeze(2).broadcast_to([64, 8, 128])),
              r=[esk], w=[eskb])
        kc = kb.sb("kc", [128, 2, CTX], BF16, st)
        kb.dma("sp", kc[:, :, :], kS.rearrange("(c p) t -> p c t", p=128)[:, :, 0:CTX], kc, r=[("dram", id(kS))], w=[kc])
        vc = kb.sb("vc", [128, 2, 128], BF16, st)
        kb.dma("sp", vc[:, :, :], vS[0:CTX, :].rearrange("(b p) n -> p b n", p=128), vc, r=[("dram", id(vS))], w=[vc])
        qp = Pool(kb, "aq", 2, [128, 4, 128], BF16, stack=st)
        kwp = Pool(kb, "akw", 2, [128, 2, 384], BF16, stack=st)
        vwp = Pool(kb, "avw", 2, [128, 3, 128], BF16, stack=st)
        pp = Pool(kb, "ap", 4, [128, 512], BF16, stack=st)
        dp = Pool(kb, "ad", 2, [64, 512], F32, stack=st)
        op_ = Pool(kb, "ao", 2, [64, 8, 128], BF16, stack=st)
        qv = qS.rearrange("(c p) t -> p c t", p=128)
        kv = kS.rearrange("(c p) t -> p c t", p=128)
        blocks = [(1, b) for b in range(CTX // 128)] + [(0, b) for b in range(NB)]
        self._acc_i = 0
        self._sc_i = 0
        for (seg, b) in blocks:
            t0 = b * 128 if seg == 1 else CTX + b * 128
            q = qp.next()
            kb.dma("sp", q[:, :, :], qv[:, :, t0:t0 + 128], q, r=[("dram", id(qS))], w=[q])
            keyblocks = []
            if seg == 0:
                lo = max(b - 1, 0)
                hi = min(b + 1, NB - 1)
                nb_ = hi - lo + 1
                kw, vw = kwp.next(), vwp.next()
                kb.dma("sp", kw[:, :, 0:nb_ * 128], kv[:, :, CTX + lo * 128:CTX + (hi + 1) * 128], kw,
                       r=[("dram", id(kS))], w=[kw])
                kb.dma("sp", vw[:, 0:nb_, :],
                       vS[CTX + lo * 128:CTX + (hi + 1) * 128, :].rearrange("(b p) n -> p b n", p=128), vw,
                       r=[("dram", id(vS))], w=[vw])
                for bb in range(lo, hi + 1):
                    i = bb - lo
                    mask = mP if bb < b else (mN if bb > b else None)
                    keyblocks.append((kw, i * 128, vw, i, mask))
            for i in range(CTX // 128):
                keyblocks.append((kc, i * 128, vc, i, None))
            oo = op_.next()
            accb = self.psum.bufs[0:4]
            scb = self.psum.bufs[4:8]
            for kvh in range(2):
                Pn = accb[(self._acc_i) % 4]
                Pd = accb[(self._acc_i + 1) % 4]
                self._acc_i += 2
                for bi, (kbuf, koff, vbuf, vi, mask) in enumerate(keyblocks):
                    Ps = [scb[self._sc_i % 4], scb[(self._sc_i + 1) % 4]]
                    self._sc_i += 2
                    pt = pp.next()
                    for par in range(2):
                        base = par * 64
                        var = 0 if (kvh * 64 == base) else 1
                        for j in range(2):
                            h = kvh * 4 + 2 * j + par
                            kb.op("pe", lambda e: e.matmul(Ps[par][:, j * 128:(j + 1) * 128],
                                                           lhsT=kbuf[base:base + 64, var, koff:koff + 128],
                                                           rhs=q[base:base + 64, h // 2, :], start=True, stop=True),
                                  r=[kbuf, q], w=[Ps[par]])
                        kb.op("act", lambda e: e.activation(out=pt[:, par * 256:(par + 1) * 256], in_=Ps[par][:, 0:256],
                                                            func=AF.Exp, scale=0.125), r=[Ps[par]], w=[pt])
                    if mask is not None:
                        kb.op("pool", lambda e: e.tensor_tensor(out=pt[:, :], in0=pt[:, :], in1=mask[:, :], op=ALU.mult),
                              r=[pt, mask], w=[pt])
                    first, last = bi == 0, bi == len(keyblocks) - 1
                    kb.op("pe", lambda e: e.matmul(Pn[0:64, :], lhsT=vbuf[:, vi, kvh * 64:(kvh + 1) * 64], rhs=pt[:, :],
                                                   start=first, stop=last), r=[vbuf, pt], w=[Pn])
                    kb.op("pe", lambda e: e.matmul(Pd[0:64, :], lhsT=ones[:, :], rhs=pt[:, :],
                                                   start=first, stop=last), r=[ones, pt], w=[Pd])
                den = dp.next()
                kb.op("dve", lambda e: e.tensor_tensor(
                    out=den[:, :].rearrange("p (r j q) -> p r j q", r=2, j=2), in0=Pd[0:64, :].rearrange("p (r j q) -> p r j q", r=2, j=2),
                    in1=eskb[:, kvh * 4:(kvh + 1) * 4, :].rearrange("p (j r) q -> p r j q", r=2),
                    op=ALU.add), r=[Pd, eskb], w=[den])
                kb.op("dve", lambda e: e.reciprocal(out=den[:, :], in_=den[:, :]), r=[den], w=[den])
                kb.op("dve", lambda e: e.tensor_tensor(
                    out=oo[:, kvh * 4:(kvh + 1) * 4, :].rearrange("p (j r) q -> p r j q", r=2),
                    in0=Pn[0:64, :].rearrange("p (r j q) -> p r j q", r=2, j=2),
                    in1=den[:, :].rearrange("p (r j q) -> p r j q", r=2, j=2), op=ALU.mult),
                      r=[Pn, den], w=[oo])
            kb.dma("sp", yaS.rearrange("(h p) t -> p h t", p=64)[:, :, t0:t0 + 128], oo[:, :, :], oo, r=[oo],
                   w=[("dram", id(yaS))])


Model.mixA_declare = _mixA_declare
Model.mixA_phase = _mixA_phase
Model.attn_phase = _attn_phase


def _rope_tables(T):
    NT = CTX + T
    C = np.ones((128, NT), np.float32)
    S = np.zeros((128, NT), np.float32)
    t = np.arange(T)
    row = (t // 64).astype(np.float32)
    col = (t % 64).astype(np.float32)
    inv = (10000.0 ** (-np.arange(16, dtype=np.float32) / 16)).astype(np.float32)
    for d in range(64):
        i = d % 16
        pos = row if d < 32 else col
        ang = (pos * inv[i]).astype(np.float32)
        sign = -1.0 if (d % 32) < 16 else 1.0
        for hb in (0, 64):
            C[hb + d, CTX:] = np.cos(ang)
            S[hb + d, CTX:] = sign * np.sin(ang)
    return C, S


def _swap_perm(n_heads):
    idx = []
    for h in range(n_heads):
        for d in range(64):
            p = d + 16 if (d % 32) < 16 else d - 16
            idx.append(h * 64 + p)
    return np.array(idx)


def host_inputs_A(inp, T):
    d = {}
    w = inp["w_in"]
    q = w[:, :, 0:512]
    k = w[:, :, 512:640]
    v = w[:, :, 640:768]
    kB = np.concatenate([k[:, :, 64:128], k[:, :, 0:64]], -1)
    s5 = w[:, :, 2624:3136]
    g = w[:, :, 3136:6208]
    d["w_inA"] = np.ascontiguousarray(np.concatenate(
        [q, q[:, :, _swap_perm(8)], k, k[:, :, _swap_perm(2)], kB, kB[:, :, _swap_perm(2)], v, s5, g], -1))
    C, S = _rope_tables(T)
    d["ropeC"], d["ropeS"] = C, S
    d["sinkT"] = np.ascontiguousarray(np.broadcast_to(inp["attn_sink"][:, None, :], (DEPTH, 64, 8)))
    j = np.arange(128)[:, None]
    i = np.arange(128)[None, :]
    d["maskP"] = np.ascontiguousarray(np.tile((j >= i).astype(np.float32), (1, 4)))
    d["maskN"] = np.ascontiguousarray(np.tile((j <= i).astype(np.float32), (1, 4)))
    return d


I32 = mybir.dt.int32
TWO_PI = 2.0 * np.pi


def _s5_declare(self):
    L = DEPTH
    self.s5_are = self.din("s5_are", [L, 2, 128, 4, 64])
    self.s5_aim = self.din("s5_aim", [L, 2, 128, 4, 64])
    self.s5_ls = self.din("s5_ls", [L, 2, 128, 4])
    self.s5_brT = self.din("s5_brT", [L, 128, 4, 64])
    self.s5_biT = self.din("s5_biT", [L, 128, 4, 64])
    self.s5_are2 = self.din("s5_are2", [L, 2, 128, 16])
    self.s5_aim2 = self.din("s5_aim2", [L, 2, 128, 16])
    self.s5_ls2 = self.din("s5_ls2", [L, 2, 128, 16])
    self.s5_crT = self.din("s5_crT", [L, 128, 16, 16])
    self.s5_ciT = self.din("s5_ciT", [L, 128, 16, 16])
    self.s5_rowmask = self.din("s5_rowmask", [128, 16, 2])
    self.s5_dT = self.din("s5_dT", [128, L * 4])
    self.s5_glub = self.din("s5_glub", [128, L * 4])
    self.s5_gluw = self.din("s5_gluw", [L, 512, 512])
    self.tauT = self.din("tauT", [128, 128])


def _sincos(self, ang, angk, n, S, Sk, C, Ck, st):
    kb = self.kb
    t = kb.sb("sc_t", [128, n], F32, st)
    ti = kb.sb("sc_i", [128, n], I32, st)
    tf = kb.sb("sc_f", [128, n], F32, st)
    for (off, dst, dk) in ((0.0, S, Sk), (0.25, C, Ck)):
        kb.op("dve", lambda e: e.tensor_scalar(out=t[:, :], in0=ang, scalar1=1.0 / TWO_PI, scalar2=off,
                                               op0=ALU.mult, op1=ALU.add), r=[angk], w=[t])
        kb.op("dve", lambda e: e.tensor_copy(out=ti[:, :], in_=t[:, :]), r=[t], w=[ti])
        kb.op("dve", lambda e: e.tensor_copy(out=tf[:, :], in_=ti[:, :]), r=[ti], w=[tf])
        kb.op("dve", lambda e: e.tensor_tensor(out=tf[:, :], in0=t[:, :], in1=tf[:, :], op=ALU.subtract),
              r=[t, tf], w=[tf])
        kb.op("act", lambda e: e.activation(out=dst, in_=tf[:, :], func=AF.Sin, scale=TWO_PI), r=[tf], w=[dk])


def _s5_dir(self, l, d, usS, ysbS, ysS, st, PS2):
    kb = self.kb
    NT = self.NT
    nchunk = NT // 128
    usv = usS.rearrange("(c p) t -> p c t", p=128)
    ybv = ysbS.rearrange("(c p) t -> p c t", p=128)
    ysv = ysS.rearrange("(c p) t -> p c t", p=128)
    V = lambda e_, f, r, w: kb.op(e_, f, r=r, w=w)
    _pers = {}
    for (n_, shp_, dt_) in (("DR", [128, 16, 128], BF16), ("DI", [128, 16, 128], BF16), ("COS", [128, 16, 128], F32),
                            ("SIN", [128, 16, 128], F32), ("RHO0", [128, 16, 128], F32), ("rho", [128, 16], F32),
                            ("lr2", [128, 16], F32), ("li2", [128, 16], F32), ("CR", [128, 16, 128], BF16),
                            ("CIn", [128, 16, 128], BF16), ("s5d", [128, 4], F32), ("s5gb", [128, 4], F32),
                            ("gluw", [128, 4, 512], BF16)):
        _pers[n_] = kb.sb(n_, shp_, dt_, st)
    sbf = lambda n, shp, dt=F32: _pers[n] if n in _pers else kb.sb(n, shp, dt, st)
    st2 = contextlib.ExitStack()
    tmpf = lambda n, shp, dt=F32: kb.sb(n, shp, dt, st2)
    are, aim = tmpf("are", [128, 4, 64]), tmpf("aim", [128, 4, 64])
    ls = tmpf("ls", [128, 4])
    br, bi = tmpf("br", [128, 4, 64]), tmpf("bi", [128, 4, 64])
    kb.dma("sp", are[:, :, :], self.s5_are[l, d], are, w=[are])
    kb.dma("sp", aim[:, :, :], self.s5_aim[l, d], aim, w=[aim])
    kb.dma("sp", ls[:, :], self.s5_ls[l, d], ls, w=[ls])
    kb.dma("sp", br[:, :, :], self.s5_brT[l], br, w=[br])
    kb.dma("sp", bi[:, :, :], self.s5_biT[l], bi, w=[bi])
    rmask = tmpf("rmask", [128, 16, 2])
    kb.dma("sp", rmask[:, :, :], self.s5_rowmask[:, :, :], rmask, w=[rmask])
    V("act", lambda e: e.activation(out=ls[:, :], in_=ls[:, :], func=AF.Exp), [ls], [ls])
    dtb = ls[:, :].unsqueeze(2).broadcast_to([128, 4, 64])
    adt, th = tmpf("adt", [128, 4, 64]), tmpf("th", [128, 4, 64])
    V("dve", lambda e: e.tensor_tensor(out=adt[:, :, :], in0=are[:, :, :], in1=dtb, op=ALU.mult), [are, ls], [adt])
    V("act", lambda e: e.activation(out=adt[:, :, :], in_=adt[:, :, :], func=AF.Exp), [adt], [adt])
    V("dve", lambda e: e.tensor_tensor(out=th[:, :, :], in0=aim[:, :, :], in1=dtb, op=ALU.mult), [aim, ls], [th])
    Sd, Cd = tmpf("Sd", [128, 256]), tmpf("Cd", [128, 256])
    thf = th[:, :, :].rearrange("p a b -> p (a b)")
    self.sincos(thf, th, 256, Sd[:, :], Sd, Cd[:, :], Cd, st2)
    lr, li = tmpf("lr", [128, 256]), tmpf("li", [128, 256])
    magf = adt[:, :, :].rearrange("p a b -> p (a b)")
    V("dve", lambda e: e.tensor_tensor(out=lr[:, :], in0=magf, in1=Cd[:, :], op=ALU.mult), [adt, Cd], [lr])
    V("dve", lambda e: e.tensor_tensor(out=li[:, :], in0=magf, in1=Sd[:, :], op=ALU.mult), [adt, Sd], [li])
    aref = are[:, :, :].rearrange("p a b -> p (a b)")
    aimf = aim[:, :, :].rearrange("p a b -> p (a b)")
    t1, t2, den = tmpf("t1", [128, 256]), tmpf("t2", [128, 256]), tmpf("den", [128, 256])
    V("dve", lambda e: e.tensor_tensor(out=t1[:, :], in0=aref, in1=aref, op=ALU.mult), [are], [t1])
    V("dve", lambda e: e.tensor_tensor(out=t2[:, :], in0=aimf, in1=aimf, op=ALU.mult), [aim], [t2])
    V("dve", lambda e: e.tensor_tensor(out=den[:, :], in0=t1[:, :], in1=t2[:, :], op=ALU.add), [t1, t2], [den])
    V("dve", lambda e: e.reciprocal(out=den[:, :], in_=den[:, :]), [den], [den])
    V("dve", lambda e: e.tensor_scalar(out=lr[:, :], in0=lr[:, :], scalar1=-1.0, scalar2=None, op0=ALU.add),
      [lr], [lr])
    cr, ci = tmpf("cr", [128, 256]), tmpf("ci", [128, 256])
    V("dve", lambda e: e.tensor_tensor(out=t1[:, :], in0=lr[:, :], in1=aref, op=ALU.mult), [lr, are], [t1])
    V("dve", lambda e: e.tensor_tensor(out=t2[:, :], in0=li[:, :], in1=aimf, op=ALU.mult), [li, aim], [t2])
    V("dve", lambda e: e.tensor_tensor(out=cr[:, :], in0=t1[:, :], in1=t2[:, :], op=ALU.add), [t1, t2], [cr])
    V("dve", lambda e: e.tensor_tensor(out=cr[:, :], in0=cr[:, :], in1=den[:, :], op=ALU.mult), [cr, den], [cr])
    V("dve", lambda e: e.tensor_tensor(out=t1[:, :], in0=li[:, :], in1=aref, op=ALU.mult), [li, are], [t1])
    V("dve", lambda e: e.tensor_tensor(out=t2[:, :], in0=lr[:, :], in1=aimf, op=ALU.mult), [lr, aim], [t2])
    V("dve", lambda e: e.tensor_tensor(out=ci[:, :], in0=t1[:, :], in1=t2[:, :], op=ALU.subtract), [t1, t2], [ci])
    V("dve", lambda e: e.tensor_tensor(out=ci[:, :], in0=ci[:, :], in1=den[:, :], op=ALU.mult), [ci, den], [ci])
    brf = br[:, :, :].rearrange("p a b -> p (a b)")
    bif = bi[:, :, :].rearrange("p a b -> p (a b)")
    bbr, bbi = tmpf("bbr", [128, 4, 64]), tmpf("bbi", [128, 4, 64])
    bbrf = bbr[:, :, :].rearrange("p a b -> p (a b)")
    bbif = bbi[:, :, :].rearrange("p a b -> p (a b)")
    V("dve", lambda e: e.tensor_tensor(out=t1[:, :], in0=cr[:, :], in1=brf, op=ALU.mult), [cr, br], [t1])
    V("dve", lambda e: e.tensor_tensor(out=t2[:, :], in0=ci[:, :], in1=bif, op=ALU.mult), [ci, bi], [t2])
    V("dve", lambda e: e.tensor_tensor(out=bbrf, in0=t1[:, :], in1=t2[:, :], op=ALU.subtract), [t1, t2], [bbr])
    V("dve", lambda e: e.tensor_tensor(out=t1[:, :], in0=cr[:, :], in1=bif, op=ALU.mult), [cr, bi], [t1])
    V("dve", lambda e: e.tensor_tensor(out=t2[:, :], in0=ci[:, :], in1=brf, op=ALU.mult), [ci, br], [t2])
    V("dve", lambda e: e.tensor_tensor(out=bbif, in0=t1[:, :], in1=t2[:, :], op=ALU.add), [t1, t2], [bbi])
    DR, DI = sbf("DR", [128, 16, 128], BF16), sbf("DI", [128, 16, 128], BF16)
    for j in range(16):
        for gp in range(2):
            for (dst, srcb) in ((DR, bbr), (DI, bbi)):
                V("dve", lambda e: e.tensor_scalar(out=dst[:, j, gp * 64:(gp + 1) * 64], in0=srcb[:, j // 4, :],
                                                   scalar1=rmask[:, j, gp:gp + 1], scalar2=None, op0=ALU.mult),
                  [srcb, rmask], [dst])
    are2, aim2, ls2 = tmpf("are2", [128, 16]), tmpf("aim2", [128, 16]), tmpf("ls2", [128, 16])
    kb.dma("sp", are2[:, :], self.s5_are2[l, d], are2, w=[are2])
    kb.dma("sp", aim2[:, :], self.s5_aim2[l, d], aim2, w=[aim2])
    kb.dma("sp", ls2[:, :], self.s5_ls2[l, d], ls2, w=[ls2])
    tau = tmpf("tau", [128, 128])
    kb.dma("sp", tau[:, :], self.tauT[:, :], tau, w=[tau])
    V("act", lambda e: e.activation(out=ls2[:, :], in_=ls2[:, :], func=AF.Exp), [ls2], [ls2])
    rho, th2 = sbf("rho", [128, 16]), tmpf("th2", [128, 16])
    V("dve", lambda e: e.tensor_tensor(out=rho[:, :], in0=are2[:, :], in1=ls2[:, :], op=ALU.mult), [are2, ls2], [rho])
    V("act", lambda e: e.activation(out=rho[:, :], in_=rho[:, :], func=AF.Exp), [rho], [rho])
    V("dve", lambda e: e.tensor_tensor(out=th2[:, :], in0=aim2[:, :], in1=ls2[:, :], op=ALU.mult), [aim2, ls2], [th2])
    ang = tmpf("ang", [128, 16, 128])
    V("dve", lambda e: e.tensor_tensor(out=ang[:, :, :], in0=th2[:, :].unsqueeze(2).broadcast_to([128, 16, 128]),
                                       in1=tau[:, :].unsqueeze(1).broadcast_to([128, 16, 128]), op=ALU.mult),
      [th2, tau], [ang])
    COS, SIN = sbf("COS", [128, 16, 128]), sbf("SIN", [128, 16, 128])
    self.sincos(ang[:, :, :].rearrange("p a b -> p (a b)"), ang, 2048,
                SIN[:, :, :].rearrange("p a b -> p (a b)"), SIN, COS[:, :, :].rearrange("p a b -> p (a b)"), COS, st2)
    S1, C1 = tmpf("S1", [128, 16]), tmpf("C1", [128, 16])
    self.sincos(th2[:, :], th2, 16, S1[:, :], S1, C1[:, :], C1, st2)
    lr2, li2 = sbf("lr2", [128, 16]), sbf("li2", [128, 16])
    V("dve", lambda e: e.tensor_tensor(out=lr2[:, :], in0=rho[:, :], in1=C1[:, :], op=ALU.mult), [rho, C1], [lr2])
    V("dve", lambda e: e.tensor_tensor(out=li2[:, :], in0=rho[:, :], in1=S1[:, :], op=ALU.mult), [rho, S1], [li2])
    RHO0 = sbf("RHO0", [128, 16, 128])
    V("dve", lambda e: e.tensor_copy(out=RHO0[:, :, :], in_=rho[:, :].unsqueeze(2).broadcast_to([128, 16, 128])),
      [rho], [RHO0])
    f0 = 127 if d == 1 else 0
    V("dve", lambda e: e.memset(RHO0[:, :, f0:f0 + 1], 0.0), [], [RHO0])
    crT, ciT = tmpf("crT", [128, 16, 16]), tmpf("ciT", [128, 16, 16])
    kb.dma("sp", crT[:, :, :], self.s5_crT[l], crT, w=[crT])
    kb.dma("sp", ciT[:, :, :], self.s5_ciT[l], ciT, w=[ciT])
    CR, CIn = sbf("CR", [128, 16, 128], BF16), sbf("CIn", [128, 16, 128], BF16)
    V("dve", lambda e: e.memset(CR[:, :, :], 0.0), [], [CR])
    V("dve", lambda e: e.memset(CIn[:, :, :], 0.0), [], [CIn])
    for j in range(16):
        for gp in range(2):
            c0 = 32 * (j % 4) + 16 * gp
            V("dve", lambda e: e.tensor_copy(out=CR[gp * 64:(gp + 1) * 64, j, c0:c0 + 16],
                                             in_=crT[gp * 64:(gp + 1) * 64, j, :]), [crT], [CR])
            V("dve", lambda e: e.tensor_scalar(out=CIn[gp * 64:(gp + 1) * 64, j, c0:c0 + 16],
                                               in0=ciT[gp * 64:(gp + 1) * 64, j, :], scalar1=-1.0, scalar2=None,
                                               op0=ALU.mult), [ciT], [CIn])
    if d == 0:
        dv, gb = sbf("s5d", [128, 4]), sbf("s5gb", [128, 4])
        kb.dma("sp", dv[:, :], self.s5_dT[:, l * 4:(l + 1) * 4], dv, w=[dv])
        kb.dma("sp", gb[:, :], self.s5_glub[:, l * 4:(l + 1) * 4], gb, w=[gb])
        gw = sbf("gluw", [128, 4, 512], BF16)
        for c in range(4):
            kb.dma("pool", gw[:, c, :], self.s5_gluw[l, c * 128:(c + 1) * 128, :], gw, w=[gw])
    kb.barrier()
    st2.close()
    usp = Pool(kb, "s5u", 2, [128, 4, 128], BF16, stack=st)
    usfp = Pool(kb, "s5uf", 2, [128, 4, 128], F32, stack=st)
    ybp = Pool(kb, "s5yb", 2, [128, 4, 128], F32, stack=st)
    mp = [Pool(kb, "s5m%d" % i, 1, [128, 8, 128], F32, stack=st) for i in range(4)]
    ZR, ZI = sbf("ZR", [128, 16, 128]), sbf("ZI", [128, 16, 128])
    XZR, XZI = sbf("XZR", [128, 16, 128]), sbf("XZI", [128, 16, 128])
    up_ = [Pool(kb, "s5t%d" % i, 1, [128, 16, 128], F32, stack=st) for i in range(2)]
    XR, XI = sbf("XR", [128, 16, 128], BF16), sbf("XI", [128, 16, 128], BF16)
    xlr, xli = sbf("xlr", [128, 16]), sbf("xli", [128, 16])
    cjr, cji = sbf("cjr", [128, 16]), sbf("cji", [128, 16])
    tt = [sbf("s5tt%d" % i, [128, 16]) for i in range(4)]
    yo = Pool(kb, "s5yo", 2, [128, 4, 128], F32, stack=st)
    rev = (d == 1)
    R3 = (lambda ap: ap[:, :, ::-1]) if rev else (lambda ap: ap)
    first, last = (127, 0) if rev else (0, 127)
    order = [0, 1] + list(range(2, nchunk))
    if rev:
        order = [1, 0] + list(range(nchunk - 1, 1, -1))
    yield
    for ci_, ch in enumerate(order):
        if ci_ > 0:
            yield
        t0 = ch * 128
        us = usp.next()
        kb.dma("pool", us[:, :, :], usv[:, :, t0:t0 + 128], us, r=[("dram", id(usS))], w=[us])
        for hf in range(2):
            PR, PI = PS2.next(), PS2.next()
            for jj in range(8):
                j = hf * 8 + jj
                kb.op("pe", lambda e: e.matmul(PR[:, jj * 128:(jj + 1) * 128], lhsT=DR[:, j, :], rhs=us[:, j // 4, :],
                                               start=True, stop=True), r=[DR, us], w=[PR])
                kb.op("pe", lambda e: e.matmul(PI[:, jj * 128:(jj + 1) * 128], lhsT=DI[:, j, :], rhs=us[:, j // 4, :],
                                               start=True, stop=True), r=[DI, us], w=[PI])
            prv = PR[:, :].rearrange("p (a b) -> p a b", a=8)
            piv = PI[:, :].rearrange("p (a b) -> p a b", a=8)
            cs = R3(COS[:, hf * 8:(hf + 1) * 8, :])
            sn = R3(SIN[:, hf * 8:(hf + 1) * 8, :])
            m = [p.next() for p in mp]
            V("dve", lambda e: e.tensor_tensor(out=m[0][:, :, :], in0=prv, in1=cs, op=ALU.mult), [PR, COS], [m[0]])
            V("dve", lambda e: e.tensor_tensor(out=m[1][:, :, :], in0=piv, in1=sn, op=ALU.mult), [PI, SIN], [m[1]])
            V("dve", lambda e: e.tensor_tensor(out=m[2][:, :, :], in0=piv, in1=cs, op=ALU.mult), [PI, COS], [m[2]])
            V("dve", lambda e: e.tensor_tensor(out=m[3][:, :, :], in0=prv, in1=sn, op=ALU.mult), [PR, SIN], [m[3]])
            V("pool", lambda e: e.tensor_tensor(out=ZR[:, hf * 8:(hf + 1) * 8, :], in0=m[0][:, :, :], in1=m[1][:, :, :],
                                                op=ALU.add), [m[0], m[1]], [ZR])
            V("pool", lambda e: e.tensor_tensor(out=ZI[:, hf * 8:(hf + 1) * 8, :], in0=m[2][:, :, :], in1=m[3][:, :, :],
                                                op=ALU.subtract), [m[2], m[3]], [ZI])
        if ci_ > 0:
            V("pool", lambda e: e.tensor_tensor(out=tt[0][:, :], in0=lr2[:, :], in1=xlr[:, :], op=ALU.mult), [lr2, xlr], [tt[0]])
            V("pool", lambda e: e.tensor_tensor(out=tt[1][:, :], in0=li2[:, :], in1=xli[:, :], op=ALU.mult), [li2, xli], [tt[1]])
            V("pool", lambda e: e.tensor_tensor(out=cjr[:, :], in0=tt[0][:, :], in1=tt[1][:, :], op=ALU.subtract), [tt[0], tt[1]], [cjr])
            V("pool", lambda e: e.tensor_tensor(out=tt[2][:, :], in0=lr2[:, :], in1=xli[:, :], op=ALU.mult), [lr2, xli], [tt[2]])
            V("pool", lambda e: e.tensor_tensor(out=tt[3][:, :], in0=li2[:, :], in1=xlr[:, :], op=ALU.mult), [li2, xlr], [tt[3]])
            V("pool", lambda e: e.tensor_tensor(out=cji[:, :], in0=tt[2][:, :], in1=tt[3][:, :], op=ALU.add), [tt[2], tt[3]], [cji])
            V("pool", lambda e: e.tensor_tensor(out=ZR[:, :, first], in0=ZR[:, :, first], in1=cjr[:, :], op=ALU.add), [ZR, cjr], [ZR])
            V("pool", lambda e: e.tensor_tensor(out=ZI[:, :, first], in0=ZI[:, :, first], in1=cji[:, :], op=ALU.add), [ZI, cji], [ZI])
        fl = lambda b_: (b_[:, :, :].rearrange("p a b -> p (a b)")[:, ::-1] if rev
                         else b_[:, :, :].rearrange("p a b -> p (a b)"))
        V("dve", lambda e: e.tensor_tensor_scan(out=fl(XZR), data0=fl(RHO0), data1=fl(ZR), initial=0.0,
                                                op0=ALU.mult, op1=ALU.add), [RHO0, ZR], [XZR])
        V("dve", lambda e: e.tensor_tensor_scan(out=fl(XZI), data0=fl(RHO0), data1=fl(ZI), initial=0.0,
                                                op0=ALU.mult, op1=ALU.add), [RHO0, ZI], [XZI])
        cs, sn = R3(COS[:, :, :]), R3(SIN[:, :, :])
        ua, ub = up_[0].next(), up_[1].next()
        V("dve", lambda e: e.tensor_tensor(out=ua[:, :, :], in0=XZR[:, :, :], in1=cs, op=ALU.mult), [XZR, COS], [ua])
        V("pool", lambda e: e.tensor_tensor(out=ub[:, :, :], in0=XZI[:, :, :], in1=sn, op=ALU.mult), [XZI, SIN], [ub])
        V("dve", lambda e: e.tensor_tensor(out=XR[:, :, :], in0=ua[:, :, :], in1=ub[:, :, :], op=ALU.subtract), [ua, ub], [XR])
        V("pool", lambda e: e.tensor_tensor(out=xlr[:, :], in0=ua[:, :, last], in1=ub[:, :, last], op=ALU.subtract), [ua, ub], [xlr])
        V("pool", lambda e: e.tensor_tensor(out=ua[:, :, :], in0=XZR[:, :, :], in1=sn, op=ALU.mult), [XZR, SIN], [ua])
        V("dve", lambda e: e.tensor_tensor(out=ub[:, :, :], in0=XZI[:, :, :], in1=cs, op=ALU.mult), [XZI, COS], [ub])
        V("pool", lambda e: e.tensor_tensor(out=XI[:, :, :], in0=ua[:, :, :], in1=ub[:, :, :], op=ALU.add), [ua, ub], [XI])
        V("pool", lambda e: e.tensor_tensor(out=xli[:, :], in0=ua[:, :, last], in1=ub[:, :, last], op=ALU.add), [ua, ub], [xli])
        PY = PS2.next()
        for cc in range(4):
            for jj in range(4):
                j = cc * 4 + jj
                kb.op("pe", lambda e: e.matmul(PY[:, cc * 128:(cc + 1) * 128], lhsT=CR[:, j, :], rhs=XR[:, j, :],
                                               start=(jj == 0), stop=False), r=[CR, XR], w=[PY])
                kb.op("pe", lambda e: e.matmul(PY[:, cc * 128:(cc + 1) * 128], lhsT=CIn[:, j, :], rhs=XI[:, j, :],
                                               start=False, stop=(jj == 3)), r=[CIn, XI], w=[PY])
        pyv = PY[:, 0:512].rearrange("p (a b) -> p a b", a=4)
        if d == 1:
            y = yo.next()
            V("act", lambda e: e.activation(out=y[:, :, :], in_=pyv, func=AF.Copy), [PY], [y])
            kb.dma("act", ybv[:, :, t0:t0 + 128], y[:, :, :], y, r=[y], w=[("dram", id(ysbS))])
        else:
            yb, usf = ybp.next(), usfp.next()
            kb.dma("sp", yb[:, :, :], ybv[:, :, t0:t0 + 128], yb, r=[("dram", id(ysbS))], w=[yb])
            kb.dma("sp", usf[:, :, :], usv[:, :, t0:t0 + 128], usf, r=[("dram", id(usS))], w=[usf])
            y = yo.next()
            V("dve", lambda e: e.tensor_tensor(out=y[:, :, :], in0=pyv, in1=yb[:, :, :], op=ALU.add), [PY, yb], [y])
            V("pool", lambda e: e.tensor_tensor(out=usf[:, :, :], in0=usf[:, :, :],
                                                in1=dv[:, :].unsqueeze(2).broadcast_to([128, 4, 128]), op=ALU.mult),
              [usf, dv], [usf])
            V("pool", lambda e: e.tensor_tensor(out=y[:, :, :], in0=y[:, :, :], in1=usf[:, :, :], op=ALU.add), [y, usf], [y])
            g1 = yb
            V("pool", lambda e: e.tensor_tensor(out=g1[:, :, :], in0=y[:, :, :], in1=y[:, :, :], op=ALU.mult), [y], [g1])
            V("dve", lambda e: e.tensor_scalar(out=g1[:, :, :], in0=g1[:, :, :], scalar1=0.044715, scalar2=1.0,
                                               op0=ALU.mult, op1=ALU.add), [g1], [g1])
            V("dve", lambda e: e.tensor_tensor(out=g1[:, :, :], in0=g1[:, :, :], in1=y[:, :, :], op=ALU.mult), [g1, y], [g1])
            V("act", lambda e: e.activation(out=g1[:, :, :], in_=g1[:, :, :], func=AF.Sigmoid, scale=1.5957691216057308),
              [g1], [g1])
            V("dve", lambda e: e.tensor_tensor(out=y[:, :, :], in0=y[:, :, :], in1=g1[:, :, :], op=ALU.mult), [y, g1], [y])
            geb = us
            V("act", lambda e: e.activation(out=geb[:, :, :], in_=y[:, :, :], func=AF.Copy), [y], [geb])
            PG = PS2.next()
            for oc in range(4):
                for kc in range(4):
                    kb.op("pe", lambda e: e.matmul(PG[:, oc * 128:(oc + 1) * 128], lhsT=gw[:, kc, oc * 128:(oc + 1) * 128],
                                                   rhs=geb[:, kc, :], start=(kc == 0), stop=(kc == 3)), r=[gw, geb], w=[PG])
                V("act", lambda e: e.activation(out=usf[:, oc, :], in_=PG[:, oc * 128:(oc + 1) * 128], func=AF.Sigmoid,
                                                bias=gb[:, oc:oc + 1]), [PG, gb], [usf])
            yso = usp.next()
            V("dve", lambda e: e.tensor_tensor(out=yso[:, :, :], in0=y[:, :, :], in1=usf[:, :, :], op=ALU.mult), [y, usf], [yso])
            kb.dma("sp", ysv[:, :, t0:t0 + 128], yso[:, :, :], yso, r=[yso], w=[("dram", id(ysS))])


Model.s5_declare = _s5_declare
Model.sincos = _sincos
Model.s5_dir = _s5_dir


def host_inputs_s5(inp):
    L = DEPTH
    d = {}

    def drive(a):
        a = a.reshape(L, 2, 4, 8, 1, 64)
        a = np.broadcast_to(a, (L, 2, 4, 8, 16, 64))
        return np.ascontiguousarray(a.transpose(0, 1, 3, 4, 2, 5).reshape(L, 2, 128, 4, 64))
    d["s5_are"] = drive(inp["s5_a_re"])
    d["s5_aim"] = drive(inp["s5_a_im"])
    lsd = inp["s5_log_step"].reshape(L, 2, 4, 8, 1)
    d["s5_ls"] = np.ascontiguousarray(np.broadcast_to(lsd, (L, 2, 4, 8, 16)).transpose(0, 1, 3, 4, 2).reshape(L, 2, 128, 4))

    def bT(b):
        b = b.reshape(L, 4, 8, 64, 16)
        return np.ascontiguousarray(b.transpose(0, 2, 4, 1, 3).reshape(L, 128, 4, 64))
    d["s5_brT"] = bT(inp["s5_b_re"])
    d["s5_biT"] = bT(inp["s5_b_im"])

    def st2(a):
        a = a.reshape(L, 2, 16, 2, 64)
        return np.ascontiguousarray(a.transpose(0, 1, 3, 4, 2).reshape(L, 2, 128, 16))
    d["s5_are2"] = st2(inp["s5_a_re"])
    d["s5_aim2"] = st2(inp["s5_a_im"])
    ls2 = np.broadcast_to(inp["s5_log_step"].reshape(L, 2, 16, 2, 1), (L, 2, 16, 2, 64))
    d["s5_ls2"] = np.ascontiguousarray(ls2.transpose(0, 1, 3, 4, 2).reshape(L, 2, 128, 16))

    def cT(c):
        c = c.reshape(L, 16, 2, 16, 64)
        return np.ascontiguousarray(c.transpose(0, 2, 4, 1, 3).reshape(L, 128, 16, 16))
    d["s5_crT"] = cT(inp["s5_c_re"])
    d["s5_ciT"] = cT(inp["s5_c_im"])
    k = np.arange(128)[:, None, None] // 16
    j = np.arange(16)[None, :, None]
    gp = np.arange(2)[None, None, :]
    d["s5_rowmask"] = np.ascontiguousarray((k == 2 * (j % 4) + gp).astype(np.float32))
    d["s5_dT"] = np.ascontiguousarray(inp["s5_d"].reshape(L * 4, 128).T)
    d["s5_glub"] = np.ascontiguousarray(inp["s5_glu_b"].reshape(L * 4, 128).T)
    d["s5_gluw"] = inp["s5_glu_w"]
    d["tauT"] = np.ascontiguousarray(np.broadcast_to(np.arange(128, dtype=np.float32)[None, :], (128, 128)))
    return d


NCH_B = 15


def _rwkv_declare(self):
    L = DEPTH
    self.w_inB = self.din("w_inB", [L, D, NCH_B * 128])
    self.rk_mu = self.din("rk_mu", [128, L * NCH_B])
    self.rk_w0 = self.din("rk_w0", [128, L * 8])
    for n in ("a0", "kk", "ka", "rk", "gng", "gnb"):
        setattr(self, "rk_" + n, self.din("rk_" + n, [128, L * 4]))
    self.rk_w2 = self.din("rk_w2", [L, 128, 512])
    self.rk_a2 = self.din("rk_a2", [L, 64, 512])
    self.rk_g2 = self.din("rk_g2", [L, 128, 512])
    self.bd64 = self.din("bd64", [128, 128])
    self.identb = self.din("identb", [128, 128])
    self.ones0 = self.din("ones0", [2, 128, 128])
    self.rk_MT = self.din("rk_MT", [2, 128, 512])
    self.rk_MN = self.din("rk_MN", [2, 128, 512])


def _rwkv_prep_phase(self, l, src, S):
    kb = self.kb
    W = 256
    Wh = W + 2
    NB = W // 128
    with kb.phase() as st:
        self.psum = Pool(kb, "ps", 8, [128, 512], F32, psum=True, stack=st)
        V = lambda e_, f, r, w: kb.op(e_, f, r=r, w=w)
        sbf = lambda n, shp, dt=F32: kb.sb(n, shp, dt, st)
        w = sbf("wB", [128, NCH, NCH_B * 128], BF16)
        for c in range(NCH):
            kb.dma("pool", w[:, c, :], self.w_inB[l, c * 128:(c + 1) * 128, :], w, w=[w])
        w2b, a2b, g2b = sbf("w2b", [128, 512], BF16), sbf("a2b", [64, 512], BF16), sbf("g2b", [128, 512], BF16)
        kb.dma("pool", w2b[:, :], self.rk_w2[l], w2b, w=[w2b])
        kb.dma("pool", a2b[:, :], self.rk_a2[l], a2b, w=[a2b])
        kb.dma("pool", g2b[:, :], self.rk_g2[l], g2b, w=[g2b])
        bd, idb = sbf("bd", [128, 128], BF16), sbf("idb", [128, 128], BF16)
        kb.dma("pool", bd[:, :], self.bd64[:, :], bd, w=[bd])
        kb.dma("pool", idb[:, :], self.identb[:, :], idb, w=[idb])
        on0 = sbf("on0", [128, 2, 128])
        kb.dma("sp", on0[:, :, :], self.ones0.rearrange("d p t -> p d t"), on0, w=[on0])
        ON = [sbf("ON%d" % d_, [128, 4 * (W // 128), 128]) for d_ in range(2)]
        for d_ in range(2):
            V("dve", lambda e: e.tensor_copy(out=ON[d_][:, :, :], in_=on0[:, d_:d_ + 1, :].broadcast_to([128, 4 * (W // 128), 128])),
              [on0], [ON[d_]])
        mu, omu, hmu = sbf("mu", [128, NCH_B]), sbf("omu", [128, NCH_B]), sbf("hmu", [128, NCH_B])
        kb.dma("sp", mu[:, :], self.rk_mu[:, l * NCH_B:(l + 1) * NCH_B], mu, w=[mu])
        V("dve", lambda e: e.tensor_scalar(out=omu[:, :], in0=mu[:, :], scalar1=-1.0, scalar2=1.0, op0=ALU.mult, op1=ALU.add), [mu], [omu])
        V("dve", lambda e: e.tensor_scalar(out=hmu[:, :], in0=mu[:, :], scalar1=0.5, scalar2=None, op0=ALU.mult), [mu], [hmu])
        w0 = sbf("w0", [128, 8])
        kb.dma("sp", w0[:, :], self.rk_w0[:, l * 8:(l + 1) * 8], w0, w=[w0])
        pv = {}
        for n in ("a0", "kk", "ka", "rk"):
            pv[n] = sbf("p_" + n, [128, 4])
            kb.dma("sp", pv[n][:, :], getattr(self, "rk_" + n)[:, l * 4:(l + 1) * 4], pv[n], w=[pv[n]])
        omka = sbf("omka", [128, 4])
        V("dve", lambda e: e.tensor_scalar(out=omka[:, :], in0=pv["ka"][:, :], scalar1=-1.0, scalar2=1.0, op0=ALU.mult, op1=ALU.add),
          [pv["ka"]], [omka])
        pl = self.stat_pools(Wh, st)
        xp = Pool(kb, "x", 2, [128, NCH, Wh], F32, stack=st)
        xn = sbf("xn", [128, NCH, Wh])
        u = sbf("u", [128, NCH, Wh], BF16)
        zcp = Pool(kb, "zc", 2, [128, Wh], F32, stack=st)
        tmpp = Pool(kb, "ztmp", 2, [128, W], F32, stack=st)
        Z = sbf("Z", [128, NCH_B, W])
        tw, sg, alb = sbf("tw", [128, W], BF16), sbf("sg", [128, W], BF16), sbf("alb", [64, W], BF16)
        LW = [sbf("LW%d" % d_, [128, 4, W]) for d_ in range(2)]
        A, KK, KM, Bv = sbf("A", [128, 4, W]), sbf("KK", [128, 4, W]), sbf("KM", [128, 4, W]), sbf("Bv", [128, 4, W])
        SQ = sbf("SQ", [128, 4, W], BF16)
        T1, T2 = sbf("T1", [128, 4, W]), sbf("T2", [128, 4, W])
        Gp = Pool(kb, "Go", 2, [128, 4, W], BF16, stack=st)
        Bop = Pool(kb, "Bo", 2, [128, 4, W], BF16, stack=st)
        Vb = sbf("Vb", [128, 4, W], BF16)
        tokp = Pool(kb, "tok", 3, [128, NB, 512], BF16, stack=st)
        Lc, Lr, Lq = sbf("Lc", [128, 4, W]), sbf("Lr", [128, 4, W]), sbf("Lq", [128, 4, W])
        E = sbf("E", [128, 4, W])
        outp = {n: Pool(kb, n, 2, [128, 4, W], BF16, stack=st) for n in ("RHO", "KAP", "BET", "KTI")}
        scp = Pool(kb, "sco", 2, [128, NB, 4, 3], F32, stack=st)
        lmn = sbf("lmn", [128, 4, NB])
        srcv = src.rearrange("(c p) t -> p c t", p=128)
        fmv = lambda t_: t_.rearrange("(c p) t -> p c t", p=128)

        def transpose_store(srcb, dst, t0):
            tk = tokp.next()
            for tb in range(NB):
                P = self.psum.next()
                pb = P[:, 0:256].bitcast(BF16)
                for c in range(4):
                    kb.op("pe", lambda e: e.transpose(pb[:, c * 128:(c + 1) * 128], srcb[:, c, tb * 128:(tb + 1) * 128], idb[:, :]),
                          r=[srcb, idb], w=[P])
                V("act", lambda e: e.activation(out=tk[:, tb, :], in_=pb[:, 0:512], func=AF.Copy), [P], [tk])
            kb.dma("sp", dst[t0:t0 + W, :].rearrange("(b p) n -> p b n", p=128), tk[:, :, :], tk, r=[tk], w=[("dram", id(dst))])

        for (seg, t0, _) in self.tiles(W):
            seg_lo, seg_hi = (0, CTX) if seg == 1 else (CTX, self.NT)
            lo, hi = max(t0 - 1, seg_lo), min(t0 + W + 1, seg_hi)
            x = xp.next()
            c0 = lo - (t0 - 1)
            if c0 > 0:
                V("dve", lambda e: e.memset(x[:, :, 0:1], 0.0), [], [x])
            if hi < t0 + W + 1:
                V("dve", lambda e: e.memset(x[:, :, Wh - 1:Wh], 0.0), [], [x])
            kb.dma("sp", x[:, :, c0:c0 + (hi - lo)], srcv[:, :, lo:hi], x, r=[("dram", id(src))], w=[x])
            rstd, nmr = self.ln_stats_w(x, Wh, pl)
            self.normalize(xn, x, Wh, rstd, nmr)
            for c in range(NCH):
                V("act", lambda e: e.activation(out=u[:, c, :], in_=xn[:, c, :], func=AF.Identity,
                                                scale=self.modp1[:, 4 * NCH + c, seg:seg + 1],
                                                bias=self.mods[:, 3 * NCH + c, seg:seg + 1]), [xn, self.modp1, self.mods], [u])
            if c0 > 0:
                V("dve", lambda e: e.memset(u[:, :, 0:1], 0.0), [], [u])
            if hi < t0 + W + 1:
                V("dve", lambda e: e.memset(u[:, :, Wh - 1:Wh], 0.0), [], [u])
            for ch in range(NCH_B):
                P = self.psum.next()
                for c in range(NCH):
                    kb.op("pe", lambda e: e.matmul(P[:, 0:Wh], lhsT=w[:, c, ch * 128:(ch + 1) * 128], rhs=u[:, c, :],
                                                   start=(c == 0), stop=(c == NCH - 1)), r=[w, u], w=[P])
                zc, tm = zcp.next(), tmpp.next()
                V("act", lambda e: e.activation(out=zc[:, :], in_=P[:, 0:Wh], func=AF.Copy), [P], [zc])
                V("pool", lambda e: e.tensor_tensor(out=tm[:, :], in0=zc[:, 0:W], in1=zc[:, 2:W + 2], op=ALU.add), [zc], [tm])
                V("act", lambda e: e.activation(out=tm[:, :], in_=tm[:, :], func=AF.Identity, scale=hmu[:, ch:ch + 1]), [tm, hmu], [tm])
                V("dve", lambda e: e.scalar_tensor_tensor(out=Z[:, ch, :], in0=zc[:, 1:W + 1], scalar=omu[:, ch:ch + 1],
                                                          in1=tm[:, :], op0=ALU.mult, op1=ALU.add), [zc, omu, tm], [Z])
            R_, K_, V_ = Z[:, 0:4, :], Z[:, 4:8, :], Z[:, 8:12, :]
            V("act", lambda e: e.activation(out=tw[:, :], in_=Z[:, 12, :], func=AF.Tanh), [Z], [tw])
            V("act", lambda e: e.activation(out=sg[:, :], in_=Z[:, 13, :], func=AF.Sigmoid), [Z], [sg])
            V("act", lambda e: e.activation(out=alb[:, :], in_=Z[0:64, 14, :], func=AF.Copy), [Z], [alb])
            for d_ in range(2):
                for c in range(4):
                    P = self.psum.next()
                    kb.op("pe", lambda e: e.matmul(P[:, 0:W], lhsT=w2b[d_ * 64:(d_ + 1) * 64, c * 128:(c + 1) * 128],
                                                   rhs=tw[d_ * 64:(d_ + 1) * 64, :], start=True, stop=True), r=[w2b, tw], w=[P])
                    V("act", lambda e: e.activation(out=LW[d_][:, c, :], in_=P[:, 0:W], func=AF.Sigmoid,
                                                    bias=w0[:, d_ * 4 + c:d_ * 4 + c + 1]), [P, w0], [LW[d_]])
                V("pool", lambda e: e.tensor_scalar(out=LW[d_][:, :, :], in0=LW[d_][:, :, :], scalar1=-DECAY_SCALE, scalar2=None,
                                                    op0=ALU.mult), [LW[d_]], [LW[d_]])
            Go = Gp.next()
            for c in range(4):
                P = self.psum.next()
                kb.op("pe", lambda e: e.matmul(P[:, 0:W], lhsT=a2b[0:64, c * 128:(c + 1) * 128], rhs=alb[0:64, :],
                                               start=True, stop=True), r=[a2b, alb], w=[P])
                V("act", lambda e: e.activation(out=A[:, c, :], in_=P[:, 0:W], func=AF.Sigmoid, bias=pv["a0"][:, c:c + 1]),
                  [P, pv["a0"]], [A])
                P = self.psum.next()
                kb.op("pe", lambda e: e.matmul(P[:, 0:W], lhsT=g2b[:, c * 128:(c + 1) * 128], rhs=sg[:, :],
                                               start=True, stop=True), r=[g2b, sg], w=[P])
                V("act", lambda e: e.activation(out=Go[:, c, :], in_=P[:, 0:W], func=AF.Copy), [P], [Go])
            kb.dma("sp", fmv(S["gR"])[:, :, t0:t0 + W], Go[:, :, :], Go, r=[Go], w=[("dram", id(S["gR"]))])
            for c in range(4):
                V("pool", lambda e: e.tensor_scalar(out=KK[:, c, :], in0=Z[:, 4 + c, :], scalar1=pv["kk"][:, c:c + 1], scalar2=None,
                                                    op0=ALU.mult), [Z, pv["kk"]], [KK])
            V("act", lambda e: e.activation(out=SQ[:, :, :], in_=KK[:, :, :], func=AF.Square), [KK], [SQ])
            for c in range(4):
                P = self.psum.next()
                kb.op("pe", lambda e: e.matmul(P[:, 0:W], lhsT=bd[:, :], rhs=SQ[:, c, :], start=True, stop=True), r=[bd, SQ], w=[P])
                V("dve", lambda e: e.tensor_scalar(out=T1[:, c, :], in0=P[:, 0:W], scalar1=1e-12, scalar2=None, op0=ALU.add), [P], [T1])
            V("act", lambda e: e.activation(out=T1[:, :, :], in_=T1[:, :, :], func=AF.Sqrt), [T1], [T1])
            V("dve", lambda e: e.reciprocal(out=T1[:, :, :], in_=T1[:, :, :]), [T1], [T1])
            V("dve", lambda e: e.tensor_tensor(out=KK[:, :, :], in0=KK[:, :, :], in1=T1[:, :, :], op=ALU.mult), [KK, T1], [KK])
            for c in range(4):
                V("dve", lambda e: e.tensor_scalar(out=T2[:, c, :], in0=A[:, c, :], scalar1=pv["ka"][:, c:c + 1],
                                                   scalar2=omka[:, c:c + 1], op0=ALU.mult, op1=ALU.add), [A, pv["ka"], omka], [T2])
            V("pool", lambda e: e.tensor_tensor(out=KM[:, :, :], in0=K_, in1=T2[:, :, :], op=ALU.mult), [Z, T2], [KM])
            V("pool", lambda e: e.tensor_tensor(out=Bv[:, :, :], in0=KK[:, :, :], in1=A[:, :, :], op=ALU.mult), [KK, A], [Bv])
            V("dve", lambda e: e.tensor_tensor(out=T1[:, :, :], in0=R_, in1=KM[:, :, :], op=ALU.mult), [Z, KM], [T1])
            for c in range(4):
                V("act", lambda e: e.activation(out=SQ[:, c, :], in_=T1[:, c, :], func=AF.Identity, scale=pv["rk"][:, c:c + 1]),
                  [T1, pv["rk"]], [SQ])
            Bo = Bop.next()
            for c in range(4):
                P = self.psum.next()
                kb.op("pe", lambda e: e.matmul(P[:, 0:W], lhsT=bd[:, :], rhs=SQ[:, c, :], start=True, stop=True), r=[bd, SQ], w=[P])
                V("dve", lambda e: e.tensor_tensor(out=Bo[:, c, :], in0=P[:, 0:W], in1=Z[:, 8 + c, :], op=ALU.mult), [P, Z], [Bo])
            kb.dma("sp", fmv(S["bon"])[:, :, t0:t0 + W], Bo[:, :, :], Bo, r=[Bo], w=[("dram", id(S["bon"]))])
            V("act", lambda e: e.activation(out=Vb[:, :, :], in_=V_, func=AF.Copy), [Z], [Vb])
            transpose_store(Vb, S["vt"], t0)
            for d_ in range(2):
                rev = d_ == 1
                fl = (lambda ap: ap.rearrange("p a b -> p (a b)")[:, ::-1]) if rev else (lambda ap: ap.rearrange("p a b -> p (a b)"))
                V("dve", lambda e: e.tensor_tensor_scan(out=fl(Lc[:, :, :]), data0=fl(ON[d_][:, :, :]), data1=fl(LW[d_][:, :, :]),
                                                        initial=0.0, op0=ALU.mult, op1=ALU.add), [ON[d_], LW[d_]], [Lc])
                mid, last = (64, 0) if rev else (63, 127)
                L4 = Lc[:, :, :].rearrange("p c (b t) -> p c b t", b=NB)
                sc = scp.next()
                V("dve", lambda e: e.tensor_copy(out=lmn[:, :, :], in_=L4[:, :, :, mid]), [Lc], [lmn])
                scv = sc[:, :, :, :].rearrange("p b c s -> p c b s")
                V("act", lambda e: e.activation(out=scv[:, :, :, 0], in_=lmn[:, :, :], func=AF.Exp), [lmn], [sc])
                V("act", lambda e: e.activation(out=scv[:, :, :, 2], in_=L4[:, :, :, last], func=AF.Exp), [Lc], [sc])
                V("dve", lambda e: e.tensor_tensor(out=scv[:, :, :, 1], in0=L4[:, :, :, last], in1=lmn[:, :, :], op=ALU.subtract),
                  [Lc, lmn], [sc])
                V("act", lambda e: e.activation(out=scv[:, :, :, 1], in_=scv[:, :, :, 1], func=AF.Exp), [sc], [sc])
                kb.dma("sp", S["sc"][d_][:, t0 // 128:t0 // 128 + NB, :, :], sc[:, :, :, :], sc, r=[sc], w=[("dram", id(S["sc"][d_]))])
                Lr4 = Lr[:, :, :].rearrange("p c (b t) -> p c b t", b=NB)
                V("pool", lambda e: e.tensor_tensor(out=Lr4, in0=L4, in1=lmn[:, :, :].unsqueeze(3).broadcast_to([128, 4, NB, 128]),
                                                    op=ALU.subtract), [Lc, lmn], [Lr])
                V("pool", lambda e: e.tensor_tensor(out=Lq[:, :, :], in0=Lr[:, :, :], in1=LW[d_][:, :, :], op=ALU.subtract),
                  [Lr, LW[d_]], [Lq])
                o = {n: outp[n].next() for n in outp}
                V("act", lambda e: e.activation(out=E[:, :, :], in_=Lr[:, :, :], func=AF.Exp), [Lr], [E])
                V("dve", lambda e: e.tensor_tensor(out=o["RHO"][:, :, :], in0=R_, in1=E[:, :, :], op=ALU.mult), [Z, E], [o["RHO"]])
                V("act", lambda e: e.activation(out=E[:, :, :], in_=Lq[:, :, :], func=AF.Exp), [Lq], [E])
                V("dve", lambda e: e.tensor_tensor(out=o["KAP"][:, :, :], in0=KK[:, :, :], in1=E[:, :, :], op=ALU.mult), [KK, E], [o["KAP"]])
                V("act", lambda e: e.activation(out=E[:, :, :], in_=Lr[:, :, :], func=AF.Exp, scale=-1.0), [Lr], [E])
                V("dve", lambda e: e.tensor_tensor(out=o["BET"][:, :, :], in0=Bv[:, :, :], in1=E[:, :, :], op=ALU.mult), [Bv, E], [o["BET"]])
                V("pool", lambda e: e.tensor_tensor(out=o["KTI"][:, :, :], in0=KM[:, :, :], in1=E[:, :, :], op=ALU.mult), [KM, E], [o["KTI"]])
                for n in ("RHO", "KAP", "BET", "KTI"):
                    kb.dma("sp", fmv(S[n][d_])[:, :, t0:t0 + W], o[n][:, :, :], o[n], r=[o[n]], w=[("dram", id(S[n][d_]))])
                transpose_store(o["BET"], S["bt"][d_], t0)
                transpose_store(o["KTI"], S["kt"][d_], t0)


def _ln_stats_w(self, x, Wc, pl):
    kb = self.kb
    xb, sq = pl["xb"].next(), pl["sq"].next()
    kb.op("act", lambda e: e.activation(out=xb[:, :, :Wc], in_=x[:, :, :Wc], func=AF.Copy), r=[x], w=[xb])
    kb.op("act", lambda e: e.activation(out=sq[:, :, :Wc], in_=x[:, :, :Wc], func=AF.Square), r=[x], w=[sq])
    P1, P2 = self.psum.next(), self.psum.next()
    for c in range(NCH):
        kb.op("pe", lambda e: e.matmul(P1[:, 0:Wc], lhsT=self.onesb[:, :], rhs=xb[:, c, :Wc],
                                       start=(c == 0), stop=(c == NCH - 1)), r=[xb, self.onesb], w=[P1])
    for c in range(NCH):
        kb.op("pe", lambda e: e.matmul(P2[:, 0:Wc], lhsT=self.onesb[:, :], rhs=sq[:, c, :Wc],
                                       start=(c == 0), stop=(c == NCH - 1)), r=[sq, self.onesb], w=[P2])
    m2, var, rstd, nmr = pl["m2"].next(), pl["var"].next(), pl["rstd"].next(), pl["nmr"].next()
    kb.op("act", lambda e: e.activation(out=m2[:, 0, :Wc], in_=P1[:, 0:Wc], func=AF.Square), r=[P1], w=[m2])
    kb.op("dve", lambda e: e.scalar_tensor_tensor(out=var[:, 0, :Wc], in0=P2[:, 0:Wc], scalar=LN_EPS,
                                                  in1=m2[:, 0, :Wc], op0=ALU.add, op1=ALU.subtract), r=[P2, m2], w=[var])
    kb.op("act", lambda e: e.activation(out=var[:, 0, :Wc], in_=var[:, 0, :Wc], func=AF.Sqrt), r=[var], w=[var])
    kb.op("dve", lambda e: e.reciprocal(out=rstd[:, 0, :Wc], in_=var[:, 0, :Wc]), r=[var], w=[rstd])
    kb.op("dve", lambda e: e.scalar_tensor_tensor(out=nmr[:, 0, :Wc], in0=P1[:, 0:Wc], scalar=-1.0,
                                                  in1=rstd[:, 0, :Wc], op0=ALU.mult, op1=ALU.mult), r=[P1, rstd], w=[nmr])
    return rstd, nmr


Model.rwkv_declare = _rwkv_declare
Model.rwkv_prep_phase = _rwkv_prep_phase
Model.ln_stats_w = _ln_stats_w


def _rwkv_scan_dir(self, l, d, S, st, psr):
    kb = self.kb
    NT = self.NT
    nchunk = NT // 128
    fmv = lambda t_: t_.rearrange("(c p) t -> p c t", p=128)
    V = lambda e_, f, r, w: kb.op(e_, f, r=r, w=w)
    sbf = lambda n, shp, dt=F32: kb.sb(n, shp, dt, st)
    MT, MN = sbf("MT", [128, 512], BF16), sbf("MN", [128, 512], BF16)
    kb.dma("pool", MT[:, :], self.rk_MT[d], MT, w=[MT])
    kb.dma("pool", MN[:, :], self.rk_MN[d], MN, w=[MN])
    KRp = Pool(kb, "KR", 2, [128, 4, 2, 128], BF16, stack=st)
    BTZp = Pool(kb, "BTZ", 2, [128, 4, 2, 128], BF16, stack=st)
    KTZp = Pool(kb, "KTZ", 2, [128, 4, 2, 128], BF16, stack=st)
    KAZp = Pool(kb, "KAZ", 2, [128, 4, 2, 128], BF16, stack=st)
    for p_ in (BTZp, KTZp, KAZp):
        for b_ in p_.bufs:
            V("pool", lambda e: e.memset(b_[:, :, :, :], 0.0), [], [b_])
    S0Z = sbf("S0Z", [128, 4, 2, 64], BF16)
    V("pool", lambda e: e.memset(S0Z[:, :, :, :], 0.0), [], [S0Z])
    St = sbf("St", [128, 4, 64])
    V("pool", lambda e: e.memset(St[:, :, :], 0.0), [], [St])
    St1 = sbf("St1", [128, 4, 64])
    tokp = {n: Pool(kb, n, 2, [128, 512], BF16, stack=st) for n in ("Btok", "Ktok", "Vtok")}
    scp = Pool(kb, "sc", 2, [128, 4, 3], F32, stack=st)
    AMp = Pool(kb, "AM", 2, [128, 8, 512], BF16, stack=st)
    Pm = [Pool(kb, "Pm%d" % i, 2, [128, 8, 128], BF16, stack=st) for i in range(2)]
    PTm = Pool(kb, "PTm", 2, [128, 8, 128], BF16, stack=st)
    X32, X16p = sbf("X32", [128, 512]), Pool(kb, "X16", 2, [128, 512], BF16, stack=st)
    Oop = Pool(kb, "Oo", 2, [128, 512], F32, stack=st)
    order = [0, 1] + list(range(2, nchunk))
    if d == 1:
        order = [1, 0] + list(range(nchunk - 1, 1, -1))
    ev = 0
    yield
    for ci_, ch in enumerate(order):
        if ci_ > 0:
            yield
        t0 = ch * 128
        KR, BTZ, KTZ, KAZ = KRp.next(), BTZp.next(), KTZp.next(), KAZp.next()
        kb.dma("sp", KR[:, :, 0, :], fmv(S["KAP"][d])[:, :, t0:t0 + 128], KR, r=[("dram", id(S["KAP"][d]))], w=[KR])
        kb.dma("sp", KR[:, :, 1, :], fmv(S["RHO"][d])[:, :, t0:t0 + 128], KR, r=[("dram", id(S["RHO"][d]))], w=[KR])
        for par in range(2):
            ps_ = slice(par * 64, (par + 1) * 64)
            kb.dma("sp", BTZ[ps_, :, par, :], fmv(S["BET"][d])[ps_, :, t0:t0 + 128], BTZ, r=[("dram", id(S["BET"][d]))], w=[BTZ])
            kb.dma("sp", KTZ[ps_, :, par, :], fmv(S["KTI"][d])[ps_, :, t0:t0 + 128], KTZ, r=[("dram", id(S["KTI"][d]))], w=[KTZ])
            kb.dma("sp", KAZ[ps_, :, par, :], fmv(S["KAP"][d])[ps_, :, t0:t0 + 128], KAZ, r=[("dram", id(S["KAP"][d]))], w=[KAZ])
        tk = {}
        for n, key in (("Btok", "bt"), ("Ktok", "kt"), ("Vtok", "vt")):
            tk[n] = tokp[n].next()
            srcd = S[key][d] if key != "vt" else S[key]
            kb.dma("sp", tk[n][:, :], srcd[t0:t0 + 128, :], tk[n], r=[("dram", id(srcd))], w=[tk[n]])
        Btok, Ktok, Vtok = tk["Btok"], tk["Ktok"], tk["Vtok"]
        sc = scp.next()
        kb.dma("sp", sc[:, :, :], S["sc"][d][:, ch, :, :], sc, r=[("dram", id(S["sc"][d]))], w=[sc])
        for par in range(2):
            ps_ = slice(par * 64, (par + 1) * 64)
            V("pool", lambda e: e.tensor_tensor(out=S0Z[ps_, :, par, :], in0=St[ps_, :, :],
                                                in1=sc[ps_, :, 0:1].broadcast_to([64, 4, 64]), op=ALU.mult), [St, sc], [S0Z])
        AM = AMp.next()
        for h in range(8):
            c, par = h // 2, h % 2
            P = psr.next()
            rhs = KR[:, c, :, :].rearrange("p s t -> p (s t)")
            kb.op("pe", lambda e: e.matmul(P[:, 0:256], lhsT=BTZ[:, c, par, :], rhs=rhs, start=True, stop=True), r=[BTZ, KR], w=[P])
            kb.op("pe", lambda e: e.matmul(P[:, 256:512], lhsT=KTZ[:, c, par, :], rhs=rhs, start=True, stop=True), r=[KTZ, KR], w=[P])
            V("dve", lambda e: e.tensor_tensor(out=AM[:, h, :], in0=P[:, :], in1=MT[:, :], op=ALU.mult), [P, MT], [AM])
        Pj, PTj = Pm[0].next(), PTm.next()
        for g4 in range(2):
            P = psr.next()
            for hh in range(4):
                h = g4 * 4 + hh
                c, par = h // 2, h % 2
                kb.op("pe", lambda e: e.matmul(P[:, hh * 128:(hh + 1) * 128], lhsT=KAZ[:, c, par, :], rhs=BTZ[:, c, par, :],
                                               start=True, stop=True), r=[KAZ, BTZ], w=[P])
            V("dve", lambda e: e.tensor_tensor(out=Pj[:, g4 * 4:(g4 + 1) * 4, :].rearrange("p h t -> p (h t)"), in0=P[:, :],
                                               in1=MN[:, :], op=ALU.mult), [P, MN], [Pj])
        V("act", lambda e: e.activation(out=PTj[:, :, :], in_=AM[:, :, 0:128], func=AF.Copy), [AM], [PTj])
        P = psr.next()
        for h in range(8):
            c, par = h // 2, h % 2
            kb.op("pe", lambda e: e.matmul(P[:, h * 64:(h + 1) * 64], lhsT=KR[:, c, 0, :], rhs=S0Z[:, c, par, :],
                                           start=True, stop=False), r=[KR, S0Z], w=[P])
            kb.op("pe", lambda e: e.matmul(P[:, h * 64:(h + 1) * 64], lhsT=AM[:, h, 256:384], rhs=Vtok[:, h * 64:(h + 1) * 64],
                                           start=False, stop=True), r=[AM, Vtok], w=[P])
        V("act", lambda e: e.activation(out=X32[:, :], in_=P[:, :], func=AF.Identity, scale=-1.0), [P], [X32])
        X16 = X16p.next()
        V("dve", lambda e: e.tensor_copy(out=X16[:, :], in_=X32[:, :]), [X32], [X16])
        for j in range(7):
            P = psr.next()
            for h in range(8):
                kb.op("pe", lambda e: e.matmul(P[:, h * 64:(h + 1) * 64], lhsT=PTj[:, h, :], rhs=X16[:, h * 64:(h + 1) * 64],
                                               start=True, stop=True), r=[PTj, X16], w=[P])
            V("dve", lambda e: e.tensor_tensor(out=X32[:, :], in0=P[:, :], in1=X32[:, :], op=ALU.add), [P, X32], [X32])
            X16 = X16p.next()
            V("act", lambda e: e.activation(out=X16[:, :], in_=X32[:, :], func=AF.Copy), [X32], [X16])
            if j < 6:
                PTn = PTm.next()
                Pn = Pm[(j + 1) % 2].next() if j < 5 else None
                for g4 in range(2):
                    P = psr.next()
                    for hh in range(4):
                        h = g4 * 4 + hh
                        kb.op("pe", lambda e: e.matmul(P[:, hh * 128:(hh + 1) * 128], lhsT=Pj[:, h, :], rhs=PTj[:, h, :],
                                                       start=True, stop=True), r=[Pj, PTj], w=[P])
                    eng = "act" if (ev % 2 == 0) else "dve"
                    ev += 1
                    dst = PTn[:, g4 * 4:(g4 + 1) * 4, :].rearrange("p h t -> p (h t)")
                    if eng == "act":
                        V("act", lambda e: e.activation(out=dst, in_=P[:, :], func=AF.Copy), [P], [PTn])
                    else:
                        V("dve", lambda e: e.tensor_copy(out=dst, in_=P[:, :]), [P], [PTn])
                    if Pn is not None:
                        P = psr.next()
                        for hh in range(4):
                            h = g4 * 4 + hh
                            kb.op("pe", lambda e: e.matmul(P[:, hh * 128:(hh + 1) * 128], lhsT=PTj[:, h, :], rhs=Pj[:, h, :],
                                                           start=True, stop=True), r=[Pj, PTj], w=[P])
                        eng = "act" if (ev % 2 == 0) else "dve"
                        ev += 1
                        dst = Pn[:, g4 * 4:(g4 + 1) * 4, :].rearrange("p h t -> p (h t)")
                        if eng == "act":
                            V("act", lambda e: e.activation(out=dst, in_=P[:, :], func=AF.Copy), [P], [Pn])
                        else:
                            V("dve", lambda e: e.tensor_copy(out=dst, in_=P[:, :]), [P], [Pn])
                PTj = PTn
                if Pn is not None:
                    Pj = Pn
        U16 = X16
        P = psr.next()
        for h in range(8):
            c, par = h // 2, h % 2
            hs = slice(h * 64, (h + 1) * 64)
            kb.op("pe", lambda e: e.matmul(P[:, hs], lhsT=KR[:, c, 1, :], rhs=S0Z[:, c, par, :], start=True, stop=False),
                  r=[KR, S0Z], w=[P])
            kb.op("pe", lambda e: e.matmul(P[:, hs], lhsT=AM[:, h, 128:256], rhs=U16[:, hs], start=False, stop=False),
                  r=[AM, U16], w=[P])
            kb.op("pe", lambda e: e.matmul(P[:, hs], lhsT=AM[:, h, 384:512], rhs=Vtok[:, hs], start=False, stop=True),
                  r=[AM, Vtok], w=[P])
        Oo = Oop.next()
        V("act", lambda e: e.activation(out=Oo[:, :], in_=P[:, :], func=AF.Copy), [P], [Oo])
        kb.dma("act", S["O"][d][t0:t0 + 128, :], Oo[:, :], Oo, r=[Oo], w=[("dram", id(S["O"][d]))])
        P = psr.next()
        for c in range(4):
            cs_ = slice(c * 128, (c + 1) * 128)
            kb.op("pe", lambda e: e.matmul(P[:, cs_], lhsT=Btok[:, cs_], rhs=U16[:, cs_], start=True, stop=False),
                  r=[Btok, U16], w=[P])
            kb.op("pe", lambda e: e.matmul(P[:, cs_], lhsT=Ktok[:, cs_], rhs=Vtok[:, cs_], start=False, stop=True),
                  r=[Ktok, Vtok], w=[P])
        V("pool", lambda e: e.tensor_tensor(out=St1[:, :, :], in0=St[:, :, :], in1=sc[:, :, 2:3].broadcast_to([128, 4, 64]),
                                            op=ALU.mult), [St, sc], [St1])
        pv_ = P[:, :].rearrange("p (c x) -> p c x", c=4)
        for par in range(2):
            ps_ = slice(par * 64, (par + 1) * 64)
            V("dve", lambda e: e.tensor_tensor(out=St[ps_, :, :], in0=pv_[ps_, :, par * 64:(par + 1) * 64],
                                               in1=sc[ps_, :, 1:2].broadcast_to([64, 4, 64]), op=ALU.mult), [P, sc, St1], [St])
        V("pool", lambda e: e.tensor_tensor(out=St[:, :, :], in0=St[:, :, :], in1=St1[:, :, :], op=ALU.add), [St, St1], [St])


def _merge_declare(self):
    L = DEPTH
    self.branch_proj = self.din("branch_proj", [L, 3, 512, D])
    self.w_out = self.din("w_out", [L, D, D])


def _merge_phase(self, l, src, dst, S, ctx):
    kb = self.kb
    W = 256
    NB = W // 128
    lni = (l * 3 + 1) * NCH
    with kb.phase() as st:
        self.psum = Pool(kb, "ps", 8, [128, 512], F32, psum=True, stack=st)
        V = lambda e_, f, r, w: kb.op(e_, f, r=r, w=w)
        sbf = lambda n, shp, dt=F32: kb.sb(n, shp, dt, st)
        bp = sbf("bp", [128, 12, D], BF16)
        wo = sbf("wo", [128, NCH, D], BF16)
        for b_ in range(3):
            for c in range(4):
                kb.dma("pool", bp[:, b_ * 4 + c, :], self.branch_proj[l, b_, c * 128:(c + 1) * 128, :], bp, w=[bp])
        for c in range(NCH):
            kb.dma("pool", wo[:, c, :], self.w_out[l, c * 128:(c + 1) * 128, :], wo, w=[wo])
        idb = sbf("idb", [128, 128], BF16)
        kb.dma("pool", idb[:, :], self.identb[:, :], idb, w=[idb])
        gng, gnb = sbf("gng", [128, 4]), sbf("gnb", [128, 4])
        kb.dma("sp", gng[:, :], self.rk_gng[:, l * 4:(l + 1) * 4], gng, w=[gng])
        kb.dma("sp", gnb[:, :], self.rk_gnb[:, l * 4:(l + 1) * 4], gnb, w=[gnb])
        pl = self.stat_pools(W, st)
        xp = Pool(kb, "x", 2, [128, NCH, W], F32, stack=st)
        xn = sbf("xn", [128, NCH, W])
        Ofp = Pool(kb, "Of", 2, [128, NB, 512], F32, stack=st)
        Obp = Pool(kb, "Ob", 2, [128, NB, 512], F32, stack=st)
        onb = sbf("onb", [128, NB, 512], BF16)
        st8 = [sbf("st8_%d" % i, [128, NB, 8]) for i in range(3)]
        sqt = sbf("sqt", [128, NB, 512])
        Y = {n: Pool(kb, "y" + n, 2, [128, 4, W], BF16, stack=st) for n in ("a", "s", "bon", "g")}
        yr = sbf("yr", [128, 4, W], BF16)
        yt = sbf("yrt", [128, 4, W])
        gp = Pool(kb, "gates", 2, [128, 24, W], BF16, stack=st)
        m1, m2, m3 = sbf("m1", [128, W]), sbf("m2", [128, W]), sbf("m3", [128, W])
        mT = sbf("mT", [128, NCH, W], BF16)
        srcv = src.rearrange("(c p) t -> p c t", p=128)
        dstv = dst.rearrange("(c p) t -> p c t", p=128)
        fmv = lambda t_: t_.rearrange("(c p) t -> p c t", p=128)
        for (seg, t0, _) in self.tiles(W, ctx=ctx):
            x = xp.next()
            kb.dma("sp", x[:, :, :], srcv[:, :, t0:t0 + W], x, r=[("dram", id(src))], w=[x])
            Of, Ob = Ofp.next(), Obp.next()
            kb.dma("sp", Of[:, :, :], S["O"][0][t0:t0 + W, :].rearrange("(b p) n -> p b n", p=128), Of, r=[("dram", id(S["O"][0]))], w=[Of])
            kb.dma("sp", Ob[:, :, :], S["O"][1][t0:t0 + W, :].rearrange("(b p) n -> p b n", p=128), Ob, r=[("dram", id(S["O"][1]))], w=[Ob])
            ld = {}
            for n, key in (("a", "ya"), ("s", "ys"), ("bon", "bon"), ("g", "gR")):
                ld[n] = Y[n].next()
                kb.dma("sp", ld[n][:, :, :], fmv(S[key])[:, :, t0:t0 + W], ld[n], r=[("dram", id(S[key]))], w=[ld[n]])
            gt = gp.next()
            kb.dma("sp", gt[:, :, :], fmv(S["gS"])[:, :, t0:t0 + W], gt, r=[("dram", id(S["gS"]))], w=[gt])
            V("dve", lambda e: e.tensor_tensor(out=Of[:, :, :], in0=Of[:, :, :], in1=Ob[:, :, :], op=ALU.add), [Of, Ob], [Of])
            O4 = Of[:, :, :].rearrange("p b (h v) -> p b h v", h=8)
            sm, vr, rs = st8
            V("dve", lambda e: e.tensor_reduce(out=sm[:, :, :], in_=O4, axis=AX.X, op=ALU.add), [Of], [sm])
            V("dve", lambda e: e.tensor_scalar(out=sm[:, :, :], in0=sm[:, :, :], scalar1=1.0 / 64, scalar2=None, op0=ALU.mult), [sm], [sm])
            V("dve", lambda e: e.tensor_tensor(out=O4, in0=O4, in1=sm[:, :, :].unsqueeze(3).broadcast_to([128, NB, 8, 64]),
                                               op=ALU.subtract), [Of, sm], [Of])
            V("act", lambda e: e.activation(out=sqt[:, :, :], in_=Of[:, :, :], func=AF.Square), [Of], [sqt])
            V("dve", lambda e: e.tensor_reduce(out=vr[:, :, :], in_=sqt[:, :, :].rearrange("p b (h v) -> p b h v", h=8), axis=AX.X,
                                               op=ALU.add), [sqt], [vr])
            V("dve", lambda e: e.tensor_scalar(out=vr[:, :, :], in0=vr[:, :, :], scalar1=1.0 / 64, scalar2=GN_EPS, op0=ALU.mult,
                                               op1=ALU.add), [vr], [vr])
            V("act", lambda e: e.activation(out=vr[:, :, :], in_=vr[:, :, :], func=AF.Sqrt), [vr], [vr])
            V("dve", lambda e: e.reciprocal(out=rs[:, :, :], in_=vr[:, :, :]), [vr], [rs])
            V("dve", lambda e: e.tensor_tensor(out=onb[:, :, :].rearrange("p b (h v) -> p b h v", h=8), in0=O4,
                                               in1=rs[:, :, :].unsqueeze(3).broadcast_to([128, NB, 8, 64]), op=ALU.mult), [Of, rs], [onb])
            for tb in range(NB):
                P = self.psum.next()
                pb = P[:, 0:256].bitcast(BF16)
                for c in range(4):
                    kb.op("pe", lambda e: e.transpose(pb[:, c * 128:(c + 1) * 128], onb[:, tb, c * 128:(c + 1) * 128], idb[:, :]),
                          r=[onb, idb], w=[P])
                for c in range(4):
                    V("act", lambda e: e.activation(out=yt[:, c, tb * 128:(tb + 1) * 128], in_=pb[:, c * 128:(c + 1) * 128],
                                                    func=AF.Identity, scale=gng[:, c:c + 1], bias=gnb[:, c:c + 1]), [P, gng, gnb], [yt])
            V("pool", lambda e: e.tensor_tensor(out=yt[:, :, :], in0=yt[:, :, :], in1=ld["bon"][:, :, :], op=ALU.add), [yt, ld["bon"]], [yt])
            V("dve", lambda e: e.tensor_tensor(out=yr[:, :, :], in0=yt[:, :, :], in1=ld["g"][:, :, :], op=ALU.mult), [yt, ld["g"]], [yr])
            if "yrS" in S:
                kb.dma("sp", fmv(S["yrS"])[:, :, t0:t0 + W], yr[:, :, :], yr, r=[yr], w=[("dram", id(S["yrS"]))])
            ysrc = [ld["a"], yr, ld["s"]]
            for oc in range(NCH):
                Pa, Pb = self.psum.next(), self.psum.next()
                tgt = [(Pa, 0), (Pa, W), (Pb, 0)]
                for b_ in range(3):
                    Pt, o0 = tgt[b_]
                    for kc in range(4):
                        kb.op("pe", lambda e: e.matmul(Pt[:, o0:o0 + W], lhsT=bp[:, b_ * 4 + kc, oc * 128:(oc + 1) * 128],
                                                       rhs=ysrc[b_][:, kc, :], start=(kc == 0), stop=(kc == 3)), r=[bp, ysrc[b_]], w=[Pt])
                V("dve", lambda e: e.tensor_tensor(out=m1[:, :], in0=Pa[:, 0:W], in1=gt[:, oc, :], op=ALU.mult), [Pa, gt], [m1])
                V("dve", lambda e: e.tensor_tensor(out=m2[:, :], in0=Pa[:, W:2 * W], in1=gt[:, 8 + oc, :], op=ALU.mult), [Pa, gt], [m2])
                V("dve", lambda e: e.tensor_tensor(out=m3[:, :], in0=Pb[:, 0:W], in1=gt[:, 16 + oc, :], op=ALU.mult), [Pb, gt], [m3])
                V("pool", lambda e: e.tensor_tensor(out=m1[:, :], in0=m1[:, :], in1=m2[:, :], op=ALU.add), [m1, m2], [m1])
                V("pool", lambda e: e.tensor_tensor(out=mT[:, oc, :], in0=m1[:, :], in1=m3[:, :], op=ALU.add), [m1, m3], [mT])
            V("pool", lambda e: e.tensor_scalar(out=x[:, :, :], in0=x[:, :, :], scalar1=ALPHA, scalar2=None, op0=ALU.mult), [x], [x])
            for oc in range(NCH):
                if oc % 2 == 0:
                    P = self.psum.next()
                o0 = (oc % 2) * W
                for kc in range(NCH):
                    kb.op("pe", lambda e: e.matmul(P[:, o0:o0 + W], lhsT=wo[:, kc, oc * 128:(oc + 1) * 128], rhs=mT[:, kc, :],
                                                   start=(kc == 0), stop=(kc == NCH - 1)), r=[wo, mT], w=[P])
                V("dve", lambda e: e.scalar_tensor_tensor(out=x[:, oc, :], in0=P[:, o0:o0 + W],
                                                          scalar=self.mods[:, 5 * NCH + oc, seg:seg + 1], in1=x[:, oc, :],
                                                          op0=ALU.mult, op1=ALU.add), [P, x, self.mods], [x])
            P = self.psum.next()
            rstd, nmr = self.ln_stats(x, W, P, pl)
            self.normalize(xn, x, W, rstd, nmr)
            for c in range(NCH):
                V("act", lambda e: e.activation(out=xn[:, c, :], in_=xn[:, c, :], func=AF.Identity,
                                                scale=self.lng[:, lni + c:lni + c + 1], bias=self.lnb[:, lni + c:lni + c + 1]),
                  [xn, self.lng, self.lnb], [xn])
            kb.dma("sp", dstv[:, :, t0:t0 + W], xn[:, :, :], xn, r=[xn], w=[("dram", id(dst))])


Model.rwkv_scan_dir = _rwkv_scan_dir
Model.merge_declare = _merge_declare
Model.merge_phase = _merge_phase


def host_inputs_rwkv(inp):
    L = DEPTH
    d = {}
    w = inp["w_in"]
    rw = w[:, :, 768:2624]
    r, k, v = rw[:, :, 0:512], rw[:, :, 512:1024], rw[:, :, 1024:1536]
    wlo, alo, glo = rw[:, :, 1536:1664], rw[:, :, 1664:1728], rw[:, :, 1728:1856]
    pad = np.zeros_like(alo)
    d["w_inB"] = np.ascontiguousarray(np.concatenate([r, k, v, wlo, glo, alo, pad], -1))
    mu = inp["rwkv_mu"]
    mu_r = np.concatenate([mu[:, 0:1536], mu[:, 1536:1664], mu[:, 1728:1856], mu[:, 1664:1728], np.zeros((L, 64), np.float32)], -1)
    d["rk_mu"] = np.ascontiguousarray(mu_r.reshape(L * NCH_B, 128).T)
    d["rk_w0"] = np.ascontiguousarray(inp["rwkv_w0"].reshape(L * 8, 128).T)
    for n, src in (("a0", "rwkv_a0"), ("kk", "rwkv_k_k"), ("ka", "rwkv_k_a"), ("rk", "rwkv_r_k"), ("gng", "rwkv_gn_g"), ("gnb", "rwkv_gn_b")):
        d["rk_" + n] = np.ascontiguousarray(inp[src].reshape(L * 4, 128).T)
    d["rk_w2"] = np.ascontiguousarray(inp["rwkv_w2"].reshape(L, 128, 512))
    d["rk_a2"] = inp["rwkv_a2"]
    d["rk_g2"] = inp["rwkv_g2"]
    i = np.arange(128)
    d["bd64"] = np.ascontiguousarray(((i[:, None] // 64) == (i[None, :] // 64)).astype(np.float32))
    d["identb"] = np.eye(128, dtype=np.float32)
    on = np.ones((2, 128, 128), np.float32)
    on[0, :, 0] = 0.0
    on[1, :, 127] = 0.0
    d["ones0"] = on
    MT = np.zeros((2, 128, 512), np.float32)
    MN = np.zeros((2, 128, 512), np.float32)
    ii, tt = i[:, None], i[None, :]
    for dd in range(2):
        prev = (ii < tt) if dd == 0 else (ii > tt)
        incl = prev | (ii == tt)
        MT[dd, :, 0:128] = -(prev.astype(np.float32))
        MT[dd, :, 128:256] = incl
        MT[dd, :, 256:384] = prev
        MT[dd, :, 384:512] = incl
        MN[dd] = np.tile(-(prev.T.astype(np.float32)), (1, 4))
    d["rk_MT"], d["rk_MN"] = MT, MN
    d["branch_proj"] = inp["branch_proj"]
    d["w_out"] = inp["w_out"]
    return d


def _make_scratch(self):
    NT = self.NT
    S = {}
    for n, shp, dt in (("qS", [512, NT], BF16), ("kS", [256, NT], BF16), ("vS", [NT, 128], BF16), ("usS", [512, NT], F32),
                       ("gS", [3072, NT], BF16), ("ya", [512, NT], BF16), ("ysb", [512, NT], F32), ("ys", [512, NT], BF16),
                       ("gR", [512, NT], BF16), ("bon", [512, NT], BF16), ("vt", [NT, 512], BF16)):
        S[n] = self.scratch(n, shp, dt)
    for n in ("RHO", "KAP", "BET", "KTI"):
        S[n] = [self.scratch("%s%d" % (n, d), [512, NT], BF16) for d in range(2)]
    for n in ("bt", "kt"):
        S[n] = [self.scratch("%s%d" % (n, d), [NT, 512], BF16) for d in range(2)]
    S["sc"] = [self.scratch("sc%d" % d, [128, NT // 128, 4, 3], F32) for d in range(2)]
    S["O"] = [self.scratch("O%d" % d, [NT, 512], F32) for d in range(2)]
    if "yrS" in self.dbg:
        S["yrS"] = self.scratch("yrS", [512, NT], BF16)
    return S


def _mixer(self, l, src, dst, S, ctx_out):
    self.mixA_phase(l, src, S["qS"], S["kS"], S["vS"], S["usS"], S["gS"])
    self.attn_phase(l, S["qS"], S["kS"], S["vS"], S["ya"])
    self.rwkv_prep_phase(l, src, S)
    self.scan_phase(l, S)
    self.merge_phase(l, src, dst, S, ctx=ctx_out)


def _scan_phase(self, l, S):
    kb = self.kb
    for (ds5, drk) in ((1, 0), (0, 1)):
        with kb.phase() as st:
            PS2 = Pool(kb, "ps2", 2, [128, 1024], F32, psum=True, stack=st)
            psr = Pool(kb, "psr", 4, [128, 512], F32, psum=True, stack=st)
            g1 = self.s5_dir(l, ds5, S["usS"], S["ysb"], S["ys"], st, PS2)
            next(g1)
            g2 = self.rwkv_scan_dir(l, drk, S, st, psr)
            next(g2)
            alive = [g1, g2]
            while alive:
                for g in list(alive):
                    try:
                        next(g)
                    except StopIteration:
                        alive.remove(g)


Model.scan_phase = _scan_phase
Model.make_scratch = _make_scratch
Model.mixer = _mixer


def build_model(T, dbg=()):
    m = Model(T, dbg=dbg)
    m.declare_inputs()
    m.mixA_declare()
    m.s5_declare()
    m.rwkv_declare()
    m.merge_declare()
    m.setup_consts()
    NT = m.NT
    S = m.make_scratch()
    streams = [m.scratch("str%d" % i, [D, NT], F32) for i in range(3)]
    outT = m.dout("outT", [D, T])
    cur = m.xT
    for l in range(DEPTH):
        last = l == DEPTH - 1
        m.adaln_phase(l)
        m.ffn_phase(l, 0, cur, streams[0], 0, ctx=True)
        m.mixer(l, streams[0], streams[1], S, ctx_out=not last)
        if last:
            m.ffn_phase(l, 1, streams[1], outT, CTX, ctx=False)
        else:
            m.ffn_phase(l, 1, streams[1], streams[2], 0, ctx=True)
            cur = streams[2]
    m.kb.finish()
    return m


def all_host_inputs(inp, b, T):
    d = host_inputs(inp, b, T)
    d.update(host_inputs_A(inp, T))
    d.update(host_inputs_s5(inp))
    d.update(host_inputs_rwkv(inp))
    return d


T_FULL = 8192
N_CORES = 8


def kernel(**inputs):
    inp = {k: np.asarray(v) for k, v in inputs.items()}
    m = build_model(T_FULL)
    B = inp["x"].shape[0]
    shared = None
    in_maps = []
    for core in range(N_CORES):
        b = core % B
        d = all_host_inputs(inp, b, T_FULL) if shared is None else dict(shared)
        if shared is None:
            shared = d
        else:
            d["xT"] = np.ascontiguousarray(np.concatenate([inp["ctx"][b], inp["x"][b, :T_FULL]], 0).T)
            cond = np.stack([inp["c"][b], inp["c_ctx"]], -1)
            d["condT"] = np.ascontiguousarray(cond.reshape(NCH, 128, 2).transpose(1, 0, 2))
        in_maps.append({k: v for k, v in d.items() if k in m.dram_in})
    res = run_bass_kernel_spmd(m.nc, in_maps, core_ids=list(range(N_CORES)))
    out = np.stack([np.ascontiguousarray(res.results[b]["outT"].T) for b in range(B)], 0)
    return out.astype(np.float32)
```

```python
import contextlib
import numpy as np
import concourse.bass as bass
import concourse.mybir as mybir
from concourse.bass_utils import run_bass_kernel_spmd

F32 = mybir.dt.float32
BF16 = mybir.dt.bfloat16
AF = mybir.ActivationFunctionType
ALU = mybir.AluOpType
AX = mybir.AxisListType

D = 1024
NCH = 8
CTX = 256
DFF = 2816
NFF = 22
DEPTH = 2
ALPHA = (2.0 * DEPTH) ** 0.25
LN_EPS = 1e-6
DECAY_SCALE = 0.606531
GN_EPS = 64e-5
N_IN = 6208


class Sem:
    _n = 0

    def __init__(self, h):
        self.h = h
        Sem._n += 1
        self.uid = Sem._n


class Buf:
    def __init__(self, name, t):
        self.name = name
        self.t = t
        self.dsem = None
        self.dcnt = 0

    def __getitem__(self, k):
        return self.t[k]

    def __repr__(self):
        return "Buf(%s)" % self.name


class KB:
    EPOCH = 20000

    def __init__(self, nc):
        self.nc = nc
        self.es = contextlib.ExitStack()
        self.eng = {"pe": nc.tensor, "dve": nc.vector, "act": nc.scalar, "pool": nc.gpsimd, "sp": nc.sync}
        self.esem = {}
        self.ecnt = {}
        for e in self.eng:
            self.esem[e] = Sem(self.es.enter_context(nc.semaphore("c_%s_0" % e)))
            self.ecnt[e] = 0
        self.eepoch = {e: 0 for e in self.eng}
        self.seen = {e: {} for e in self.eng}
        self.lastw = {}
        self.reads = {}
        self.nbuf = 0
        self.ninstr = 0
        self.nwait = 0
        self.all_events = {}
        self.free_dsems = []
        self.phase_bufs = []
        self.ndsem = 0

    def sb(self, name, shape, dtype, stack=None):
        self.nbuf += 1
        t = (stack or self.es).enter_context(self.nc.sbuf_tensor("%s_%d" % (name, self.nbuf), list(shape), dtype))
        b = Buf(name, t)
        if stack is not None:
            self.phase_bufs.append(b)
        return b

    def ps(self, name, shape, dtype=F32, stack=None):
        self.nbuf += 1
        t = (stack or self.es).enter_context(self.nc.psum_tensor("%s_%d" % (name, self.nbuf), list(shape), dtype))
        return Buf(name, t)

    def _dsem(self, b):
        if b.dsem is None:
            if self.free_dsems:
                b.dsem, b.dcnt = self.free_dsems.pop()
            else:
                self.ndsem += 1
                b.dsem = Sem(self.es.enter_context(self.nc.semaphore("d_%d" % self.ndsem)))
                b.dcnt = 0
        return b.dsem

    @contextlib.contextmanager
    def phase(self):
        st = contextlib.ExitStack()
        self.phase_bufs = []
        try:
            yield st
        finally:
            self.barrier()
            for b in self.phase_bufs:
                if b.dsem is not None:
                    self.free_dsems.append((b.dsem, b.dcnt))
                    b.dsem = None
            self.phase_bufs = []
            st.close()

    def _need(self, e, r, w):
        need = {}

        def add(evs):
            for uid, (s, v) in evs.items():
                if uid not in need or need[uid][1] < v:
                    need[uid] = (s, v)
        for k in r:
            add(self.lastw.get(k, {}))
        for k in w:
            add(self.lastw.get(k, {}))
            add(self.reads.get(k, {}))
        return need

    def _wait(self, e, need, own_ok):
        eng = self.eng[e]
        for uid, (s, v) in need.items():
            if own_ok and uid == self.esem[e].uid:
                continue
            if self.seen[e].get(uid, 0) >= v:
                continue
            eng.wait_ge(s.h, v)
            self.nwait += 1
            self.seen[e][uid] = v

    def _record(self, ev, r, w):
        uid = ev[0].uid
        for k in r:
            self.reads.setdefault(k, {})[uid] = ev
        for k in w:
            self.lastw.setdefault(k, {})[uid] = ev
            self.reads[k] = {}
        self.all_events[uid] = ev

    def _bump(self, e):
        if self.ecnt[e] >= self.EPOCH:
            self.eepoch[e] += 1
            self.esem[e] = Sem(self.es.enter_context(self.nc.semaphore("c_%s_%d" % (e, self.eepoch[e]))))
            self.ecnt[e] = 0
        self.ecnt[e] += 1
        return (self.esem[e], self.ecnt[e])

    def op(self, e, fn, r=(), w=(), same_ok=False):
        need = self._need(e, r, w)
        self._wait(e, need, own_ok=(e == "pe" or same_ok))
        ins = fn(self.eng[e])
        ev = self._bump(e)
        ins.then_inc(ev[0].h, 1)
        self._record(ev, r, w)
        self.ninstr += 1
        return ins

    def dma(self, q, out, in_, sbuf, r=(), w=(), **kw):
        need = self._need(q, r, w)
        self._wait(q, need, own_ok=False)
        s = self._dsem(sbuf)
        ins = self.eng[q].dma_start(out=out, in_=in_, **kw)
        sbuf.dcnt += 16
        ev = (s, sbuf.dcnt)
        ins.then_inc(s.h, 16)
        self._record(ev, r, w)
        self.ninstr += 1
        return ins

    def barrier(self):
        for e in self.eng:
            self._wait(e, dict(self.all_events), own_ok=True)
        self.lastw = {}
        self.reads = {}
        self.all_events = {}

    def finish(self, e="sp"):
        self._wait(e, dict(self.all_events), own_ok=True)


class Pool:
    def __init__(self, kb, name, n, shape, dtype, psum=False, stack=None):
        self.bufs = [(kb.ps if psum else kb.sb)("%s%d" % (name, i), shape, dtype, stack=stack) for i in range(n)]
        self.i = 0

    def next(self):
        b = self.bufs[self.i % len(self.bufs)]
        self.i += 1
        return b


class Model:
    def __init__(self, T, dbg=(), nlayers=DEPTH, stop_after=None):
        self.T = T
        self.NT = CTX + T
        self.dbg = set(dbg)
        self.nlayers = nlayers
        self.stop_after = stop_after
        nc = bass.Bass("TRN2", target_bir_lowering=False)
        self.nc = nc
        self.kb = KB(nc)
        self.dram_in = {}
        self.dram_out = {}
        self.scr_n = 0

    def din(self, name, shape, dtype=F32):
        t = self.nc.dram_tensor(name, list(shape), dtype, kind="ExternalInput").ap()
        self.dram_in[name] = t
        return t

    def dout(self, name, shape, dtype=F32):
        t = self.nc.dram_tensor(name, list(shape), dtype, kind="ExternalOutput").ap()
        self.dram_out[name] = t
        return t

    def scratch(self, name, shape, dtype=F32):
        if name in self.dbg:
            return self.dout(name, shape, dtype)
        return self.nc.dram_tensor(name, list(shape), dtype, kind="Internal").ap()

    def tiles(self, W, ctx=True, lat=True):
        out = []
        if ctx:
            for t0 in range(0, CTX, W):
                out.append((1, t0, W))
        if lat:
            for t0 in range(0, self.T, W):
                out.append((0, CTX + t0, W))
        return out

    def declare_inputs(self):
        L = DEPTH
        self.xT = self.din("xT", [D, self.NT])
        self.condT = self.din("condT", [128, NCH, 2])
        self.w_ada = self.din("w_ada", [L, D, 9 * D])
        self.b_ada = self.din("b_ada", [L, 128, 72])
        self.ln_g = self.din("ln_g", [128, L * 3 * NCH])
        self.ln_b = self.din("ln_b", [128, L * 3 * NCH])
        self.ffn_w_in = self.din("ffn_w_in", [L, 2, D, 2 * DFF])
        self.ffn_w_out = self.din("ffn_w_out", [L, 2, DFF, D])

    def setup_consts(self):
        kb = self.kb
        self.onesb = kb.sb("onesb", [128, 128], BF16)
        kb.op("dve", lambda e: e.memset(self.onesb[:], 1.0 / D), w=[self.onesb])
        self.lng = kb.sb("lng", [128, DEPTH * 3 * NCH], F32)
        self.lnb = kb.sb("lnb", [128, DEPTH * 3 * NCH], F32)
        kb.dma("sp", self.lng[:], self.ln_g[:, :], self.lng, w=[self.lng])
        kb.dma("sp", self.lnb[:], self.ln_b[:, :], self.lnb, w=[self.lnb])
        self.scond = kb.sb("scond", [128, NCH, 2], F32)
        kb.dma("sp", self.scond[:], self.condT[:, :, :], self.scond, w=[self.scond])
        kb.op("act", lambda e: e.activation(out=self.scond[:], in_=self.scond[:], func=AF.Silu),
              r=[self.scond], w=[self.scond])
        self.mods = kb.sb("mods", [128, 72, 2], F32)
        self.modp1 = kb.sb("modp1", [128, 72, 2], F32)
        self.modh = kb.sb("modh", [128, 72, 2], F32)

    def adaln_phase(self, l):
        kb = self.kb
        with kb.phase() as st:
            self.psum = Pool(kb, "ps", 8, [128, 512], F32, psum=True, stack=st)
            wa = Pool(kb, "wa", 2, [128, NCH, D], F32, stack=st)
            bada = kb.sb("bada", [128, 72], F32, st)
            kb.dma("sp", bada[:], self.b_ada[l, :, :], bada, w=[bada])
            P = self.psum.next()
            for m in range(9):
                w = wa.next()
                for kc in range(NCH):
                    kb.dma("sp" if kc % 2 == 0 else "act", w[:, kc, :],
                           self.w_ada[l, kc * 128:(kc + 1) * 128, m * D:(m + 1) * D], w, w=[w])
                for oc in range(NCH):
                    j = m * NCH + oc
                    for kc in range(NCH):
                        kb.op("pe", lambda e: e.matmul(P[:, 2 * j:2 * j + 2], lhsT=w[:, kc, oc * 128:(oc + 1) * 128],
                                                       rhs=self.scond[:, kc, :], start=(kc == 0), stop=(kc == NCH - 1)),
                              r=[w, self.scond], w=[P])
            kb.op("dve", lambda e: e.tensor_tensor(out=self.mods[:], in0=P[:, 0:144].rearrange("p (j s) -> p j s", s=2),
                                                   in1=bada[:, :].unsqueeze(2).broadcast_to([128, 72, 2]), op=ALU.add),
                  r=[P, bada], w=[self.mods])
            kb.op("dve", lambda e: e.tensor_scalar(out=self.modp1[:], in0=self.mods[:], scalar1=1.0, scalar2=None,
                                                   op0=ALU.add), r=[self.mods], w=[self.modp1])
            kb.op("dve", lambda e: e.tensor_scalar(out=self.modh[:], in0=self.mods[:], scalar1=0.5, scalar2=None,
                                                   op0=ALU.mult), r=[self.mods], w=[self.modh])

    def ln_stats(self, x, W, P, pl):
        kb = self.kb
        xb, sq = pl["xb"].next(), pl["sq"].next()
        kb.op("act", lambda e: e.activation(out=xb[:, :, :W], in_=x[:, :, :W], func=AF.Copy), r=[x], w=[xb])
        kb.op("act", lambda e: e.activation(out=sq[:, :, :W], in_=x[:, :, :W], func=AF.Square), r=[x], w=[sq])
        for c in range(NCH):
            kb.op("pe", lambda e: e.matmul(P[:, 0:W], lhsT=self.onesb[:, :], rhs=xb[:, c, :W],
                                           start=(c == 0), stop=(c == NCH - 1)), r=[xb, self.onesb], w=[P])
        for c in range(NCH):
            kb.op("pe", lambda e: e.matmul(P[:, W:2 * W], lhsT=self.onesb[:, :], rhs=sq[:, c, :W],
                                           start=(c == 0), stop=(c == NCH - 1)), r=[sq, self.onesb], w=[P])
        m2, var, rstd, nmr = pl["m2"].next(), pl["var"].next(), pl["rstd"].next(), pl["nmr"].next()
        kb.op("act", lambda e: e.activation(out=m2[:, 0, :W], in_=P[:, 0:W], func=AF.Square), r=[P], w=[m2])
        kb.op("dve", lambda e: e.scalar_tensor_tensor(out=var[:, 0, :W], in0=P[:, W:2 * W], scalar=LN_EPS,
                                                      in1=m2[:, 0, :W], op0=ALU.add, op1=ALU.subtract),
              r=[P, m2], w=[var])
        kb.op("act", lambda e: e.activation(out=var[:, 0, :W], in_=var[:, 0, :W], func=AF.Sqrt), r=[var], w=[var])
        kb.op("dve", lambda e: e.reciprocal(out=rstd[:, 0, :W], in_=var[:, 0, :W]), r=[var], w=[rstd])
        kb.op("dve", lambda e: e.scalar_tensor_tensor(out=nmr[:, 0, :W], in0=P[:, 0:W], scalar=-1.0,
                                                      in1=rstd[:, 0, :W], op0=ALU.mult, op1=ALU.mult),
              r=[P, rstd], w=[nmr])
        return rstd, nmr

    def normalize(self, out, x, W, rstd, nmr):
        kb = self.kb
        kb.op("dve", lambda e: e.tensor_tensor(out=out[:, :, :W], in0=x[:, :, :W],
                                               in1=rstd[:, 0:1, :W].broadcast_to([128, NCH, W]), op=ALU.mult),
              r=[x, rstd], w=[out])
        kb.op("pool", lambda e: e.tensor_tensor(out=out[:, :, :W], in0=out[:, :, :W],
                                                in1=nmr[:, 0:1, :W].broadcast_to([128, NCH, W]), op=ALU.add),
              r=[out, nmr], w=[out])

    def stat_pools(self, W, st):
        kb = self.kb
        return {
            "xb": Pool(kb, "xb", 1, [128, NCH, W], BF16, stack=st),
            "sq": Pool(kb, "sq", 1, [128, NCH, W], BF16, stack=st),
            "m2": Pool(kb, "m2", 2, [128, 1, W], F32, stack=st),
            "var": Pool(kb, "var", 2, [128, 1, W], F32, stack=st),
            "rstd": Pool(kb, "rstd", 2, [128, 1, W], F32, stack=st),
            "nmr": Pool(kb, "nmr", 2, [128, 1, W], F32, stack=st),
        }

    def ffn_phase(self, l, s, src, dst, dst_off, ctx):
        kb = self.kb
        W = 256
        mb = 0 if s == 0 else 6
        lni = (l * 3 + (0 if s == 0 else 2)) * NCH
        with kb.phase() as st:
            self.psum = Pool(kb, "ps", 8, [128, 512], F32, psum=True, stack=st)
            w1 = kb.sb("w1", [128, NCH, 2 * DFF], BF16, st)
            w2 = kb.sb("w2", [128, NFF, D], BF16, st)
            for c in range(NCH):
                for hh in range(2):
                    kb.dma("pool", w1[:, c, hh * DFF:(hh + 1) * DFF],
                           self.ffn_w_in[l, s, c * 128:(c + 1) * 128, hh * DFF:(hh + 1) * DFF], w1, w=[w1])
            for c in range(NFF):
                kb.dma("pool", w2[:, c, :], self.ffn_w_out[l, s, c * 128:(c + 1) * 128, :], w2, w=[w2])
            pl = self.stat_pools(W, st)
            xp = Pool(kb, "x", 2, [128, NCH, W], F32, stack=st)
            xnp = Pool(kb, "xn", 2, [128, NCH, W], F32, stack=st)
            up = Pool(kb, "u", 2, [128, NCH, W], BF16, stack=st)
            hp = Pool(kb, "h", 1, [128, NFF, W], BF16, stack=st)
            gp = Pool(kb, "g", 2, [128, W], F32, stack=st)
            srcv = src.rearrange("(c p) t -> p c t", p=128)
            dstv = dst.rearrange("(c p) t -> p c t", p=128)
            tl = self.tiles(W, ctx=ctx)
            T_ = {}

            def stage_a(i):
                seg, t0, _ = tl[i]
                x = xp.next()
                kb.dma("sp", x[:, :, :], srcv[:, :, t0:t0 + W], x, r=[("dram", id(src))], w=[x])
                P = self.psum.next()
                rstd, nmr = self.ln_stats(x, W, P, pl)
                xn = xnp.next()
                self.normalize(xn, x, W, rstd, nmr)
                u = up.next()
                for c in range(NCH):
                    kb.op("act", lambda e: e.activation(out=u[:, c, :], in_=xn[:, c, :], func=AF.Identity,
                                                        scale=self.modp1[:, (mb + 1) * NCH + c, seg:seg + 1],
                                                        bias=self.mods[:, mb * NCH + c, seg:seg + 1]),
                          r=[xn, self.modp1, self.mods], w=[u])
                kb.op("pool", lambda e: e.tensor_scalar(out=x[:, :, :], in0=x[:, :, :], scalar1=ALPHA, scalar2=None,
                                                        op0=ALU.mult), r=[x], w=[x])
                T_[i] = (x, xn, u)

            def stage_b(i):
                x, xn, u = T_[i]
                h = hp.next()
                for j in range(NFF):
                    P = self.psum.next()
                    for c in range(NCH):
                        kb.op("pe", lambda e: e.matmul(P[:, 0:W], lhsT=w1[:, c, j * 128:(j + 1) * 128], rhs=u[:, c, :],
                                                       start=(c == 0), stop=(c == NCH - 1)), r=[w1, u], w=[P])
                    for c in range(NCH):
                        kb.op("pe", lambda e: e.matmul(P[:, W:2 * W], lhsT=w1[:, c, DFF + j * 128:DFF + (j + 1) * 128],
                                                       rhs=u[:, c, :], start=(c == 0), stop=(c == NCH - 1)),
                              r=[w1, u], w=[P])
                    g = gp.next()
                    kb.op("act", lambda e: e.activation(out=g[:, :], in_=P[:, 0:W], func=AF.Silu), r=[P], w=[g])
                    kb.op("dve", lambda e: e.tensor_tensor(out=h[:, j, :], in0=P[:, W:2 * W], in1=g[:, :], op=ALU.mult),
                          r=[P, g], w=[h])
                T_[i] = (x, xn, u, h)

            def stage_c(i):
                seg, t0, _ = tl[i]
                x, xn, u, h = T_.pop(i)
                for oc in range(NCH):
                    if oc % 2 == 0:
                        P = self.psum.next()
                    o0 = (oc % 2) * W
                    for j in range(NFF):
                        kb.op("pe", lambda e: e.matmul(P[:, o0:o0 + W], lhsT=w2[:, j, oc * 128:(oc + 1) * 128],
                                                       rhs=h[:, j, :], start=(j == 0), stop=(j == NFF - 1)),
                              r=[w2, h], w=[P])
                    kb.op("dve", lambda e: e.scalar_tensor_tensor(
                        out=x[:, oc, :], in0=P[:, o0:o0 + W], scalar=self.modh[:, (mb + 2) * NCH + oc, seg:seg + 1],
                        in1=x[:, oc, :], op0=ALU.mult, op1=ALU.add), r=[P, x, self.modh], w=[x])
                P = self.psum.next()
                rstd, nmr = self.ln_stats(x, W, P, pl)
                self.normalize(xn, x, W, rstd, nmr)
                for c in range(NCH):
                    kb.op("act", lambda e: e.activation(out=xn[:, c, :], in_=xn[:, c, :], func=AF.Identity,
                                                        scale=self.lng[:, lni + c:lni + c + 1],
                                                        bias=self.lnb[:, lni + c:lni + c + 1]),
                          r=[xn, self.lng, self.lnb], w=[xn])
                kb.dma("sp", dstv[:, :, t0 - dst_off:t0 - dst_off + W], xn[:, :, :], xn,
                       r=[xn], w=[("dram", id(dst))])

            stage_a(0)
            for i in range(len(tl)):
                stage_b(i)
                if i + 1 < len(tl):
                    stage_a(i + 1)
                stage_c(i)


def host_inputs(inp, b, T):
    L = DEPTH
    d = {}
    d["xT"] = np.ascontiguousarray(np.concatenate([inp["ctx"][b], inp["x"][b, :T]], 0).T)
    cond = np.stack([inp["c"][b], inp["c_ctx"]], -1)
    d["condT"] = np.ascontiguousarray(cond.reshape(NCH, 128, 2).transpose(1, 0, 2))
    d["w_ada"] = inp["w_ada"]
    d["b_ada"] = np.ascontiguousarray(inp["b_ada"].reshape(L, 72, 128).transpose(0, 2, 1))
    d["ln_g"] = np.ascontiguousarray(inp["ln_g"].reshape(L * 3 * NCH, 128).T)
    d["ln_b"] = np.ascontiguousarray(inp["ln_b"].reshape(L * 3 * NCH, 128).T)
    d["ffn_w_in"] = inp["ffn_w_in"]
    d["ffn_w_out"] = inp["ffn_w_out"]
    return d


NWA = 10 + 1 + 4 + 24
CH_Q, CH_QS, CH_K, CH_KS, CH_KB, CH_KBS, CH_V, CH_S5, CH_G = 0, 4, 8, 9, 10, 11, 12, 13, 17
NCH_A = 41


def _mixA_declare(self):
    L = DEPTH
    self.w_inA = self.din("w_inA", [L, D, NCH_A * 128])
    self.ropeC = self.din("ropeC", [128, self.NT])
    self.ropeS = self.din("ropeS", [128, self.NT])
    self.sinkT = self.din("sinkT", [L, 64, 8])
    self.maskP = self.din("maskP", [128, 512])
    self.maskN = self.din("maskN", [128, 512])


def _mixA_phase(self, l, src, qS, kS, vS, usS, gS):
    kb = self.kb
    W = 256
    with kb.phase() as st:
        self.psum = Pool(kb, "ps", 8, [128, 512], F32, psum=True, stack=st)
        w = kb.sb("wA", [128, NCH, NCH_A * 128], BF16, st)
        for c in range(NCH):
            for hh in range(2):
                n0, n1 = (0, 21 * 128) if hh == 0 else (21 * 128, NCH_A * 128)
                kb.dma("pool", w[:, c, n0:n1], self.w_inA[l, c * 128:(c + 1) * 128, n0:n1], w, w=[w])
        pl = self.stat_pools(W, st)
        xp = Pool(kb, "x", 2, [128, NCH, W], F32, stack=st)
        xnp = Pool(kb, "xn", 1, [128, NCH, W], F32, stack=st)
        up = Pool(kb, "u", 2, [128, NCH, W], BF16, stack=st)
        cp = Pool(kb, "rc", 2, [128, W], F32, stack=st)
        sp_ = Pool(kb, "rs", 2, [128, W], F32, stack=st)
        t1p = Pool(kb, "t1", 2, [128, W], F32, stack=st)
        t2p = Pool(kb, "t2", 2, [128, W], F32, stack=st)
        qp = Pool(kb, "qo", 2, [128, 4, W], BF16, stack=st)
        kp = Pool(kb, "ko", 2, [128, 2, W], BF16, stack=st)
        vp = Pool(kb, "vo", 2, [128, 2, 128], BF16, stack=st)
        usp = Pool(kb, "uso", 2, [128, 4, W], F32, stack=st)
        gp = Pool(kb, "go", 2, [128, 24, W], BF16, stack=st)
        srcv = src.rearrange("(c p) t -> p c t", p=128)

        def proj(P, o0, ch, u):
            for c in range(NCH):
                kb.op("pe", lambda e: e.matmul(P[:, o0:o0 + W], lhsT=w[:, c, ch * 128:(ch + 1) * 128], rhs=u[:, c, :],
                                               start=(c == 0), stop=(c == NCH - 1)), r=[w, u], w=[P])

        for (seg, t0, _) in self.tiles(W):
            x = xp.next()
            kb.dma("sp", x[:, :, :], srcv[:, :, t0:t0 + W], x, r=[("dram", id(src))], w=[x])
            cT, sT = cp.next(), sp_.next()
            kb.dma("sp", cT[:, :], self.ropeC[:, t0:t0 + W], cT, w=[cT])
            kb.dma("sp", sT[:, :], self.ropeS[:, t0:t0 + W], sT, w=[sT])
            P = self.psum.next()
            rstd, nmr = self.ln_stats(x, W, P, pl)
            xn = xnp.next()
            self.normalize(xn, x, W, rstd, nmr)
            u = up.next()
            for c in range(NCH):
                kb.op("act", lambda e: e.activation(out=u[:, c, :], in_=xn[:, c, :], func=AF.Identity,
                                                    scale=self.modp1[:, 4 * NCH + c, seg:seg + 1],
                                                    bias=self.mods[:, 3 * NCH + c, seg:seg + 1]),
                      r=[xn, self.modp1, self.mods], w=[u])
            qo, ko = qp.next(), kp.next()

            def rope(dst_ap, dst, ch, chs):
                P = self.psum.next()
                proj(P, 0, ch, u)
                proj(P, W, chs, u)
                t1, t2 = t1p.next(), t2p.next()
                kb.op("dve", lambda e: e.tensor_tensor(out=t1[:, :], in0=P[:, 0:W], in1=cT[:, :], op=ALU.mult),
                      r=[P, cT], w=[t1])
                kb.op("dve", lambda e: e.tensor_tensor(out=t2[:, :], in0=P[:, W:2 * W], in1=sT[:, :], op=ALU.mult),
                      r=[P, sT], w=[t2])
                kb.op("pool", lambda e: e.tensor_tensor(out=dst_ap, in0=t1[:, :], in1=t2[:, :], op=ALU.add),
                      r=[t1, t2], w=[dst])
            for c in range(4):
                rope(qo[:, c, :], qo, CH_Q + c, CH_QS + c)
            rope(ko[:, 0, :], ko, CH_K, CH_KS)
            rope(ko[:, 1, :], ko, CH_KB, CH_KBS)
            kb.dma("sp", qS.rearrange("(c p) t -> p c t", p=128)[:, :, t0:t0 + W], qo[:, :, :], qo, r=[qo],
                   w=[("dram", id(qS))])
            kb.dma("sp", kS.rearrange("(c p) t -> p c t", p=128)[:, :, t0:t0 + W], ko[:, :, :], ko, r=[ko],
                   w=[("dram", id(kS))])
            vo = vp.next()
            P = self.psum.next()
            for tb in range(W // 128):
                for c in range(NCH):
                    kb.op("pe", lambda e: e.matmul(P[:, tb * 128:(tb + 1) * 128], lhsT=u[:, c, tb * 128:(tb + 1) * 128],
                                                   rhs=w[:, c, CH_V * 128:(CH_V + 1) * 128],
                                                   start=(c == 0), stop=(c == NCH - 1)), r=[w, u], w=[P])
            kb.op("act", lambda e: e.activation(out=vo[:, :, :], in_=P[:, 0:W].rearrange("p (b n) -> p b n", b=2),
                                                func=AF.Copy), r=[P], w=[vo])
            kb.dma("sp", vS[t0:t0 + W, :].rearrange("(b p) n -> p b n", p=128), vo[:, :, :], vo, r=[vo],
                   w=[("dram", id(vS))])
            uso = usp.next()
            for c in range(4):
                if c % 2 == 0:
                    P = self.psum.next()
                o0 = (c % 2) * W
                proj(P, o0, CH_S5 + c, u)
                kb.op("act", lambda e: e.activation(out=uso[:, c, :], in_=P[:, o0:o0 + W], func=AF.Copy),
                      r=[P], w=[uso])
            kb.dma("sp", usS.rearrange("(c p) t -> p c t", p=128)[:, :, t0:t0 + W], uso[:, :, :], uso, r=[uso],
                   w=[("dram", id(usS))])
            go = gp.next()
            for c in range(24):
                if c % 2 == 0:
                    P = self.psum.next()
                o0 = (c % 2) * W
                proj(P, o0, CH_G + c, u)
                kb.op("act", lambda e: e.activation(out=go[:, c, :], in_=P[:, o0:o0 + W], func=AF.Sigmoid),
                      r=[P], w=[go])
            kb.dma("sp", gS.rearrange("(c p) t -> p c t", p=128)[:, :, t0:t0 + W], go[:, :, :], go, r=[go],
                   w=[("dram", id(gS))])


def _attn_phase(self, l, qS, kS, vS, yaS):
    kb = self.kb
    NB = self.T // 128
    with kb.phase() as st:
        self.psum = Pool(kb, "ps", 8, [128, 512], F32, psum=True, stack=st)
        mP = kb.sb("mP", [128, 512], BF16, st)
        mN = kb.sb("mN", [128, 512], BF16, st)
        kb.dma("pool", mP[:, :], self.maskP[:, :], mP, w=[mP])
        kb.dma("pool", mN[:, :], self.maskN[:, :], mN, w=[mN])
        ones = kb.sb("ones64", [128, 64], BF16, st)
        kb.op("dve", lambda e: e.memset(ones[:], 1.0), w=[ones])
        esk = kb.sb("esk", [64, 8], F32, st)
        kb.dma("sp", esk[:, :], self.sinkT[l, :, :], esk, w=[esk])
        kb.op("act", lambda e: e.activation(out=esk[:, :], in_=esk[:, :], func=AF.Exp), r=[esk], w=[esk])
        eskb = kb.sb("eskb", [64, 8, 128], F32, st)
        kb.op("dve", lambda e: e.tensor_copy(out=eskb[:, :, :], in_=esk[:, :].unsqueeze(2).broadcast_to([64, 8, 128])),
              r=[esk], w=[eskb])
        kc = kb.sb("kc", [128, 2, CTX], BF16, st)
        kb.dma("sp", kc[:, :, :], kS.rearrange("(c p) t -> p c t", p=128)[:, :, 0:CTX], kc, r=[("dram", id(kS))], w=[kc])
        vc = kb.sb("vc", [128, 2, 128], BF16, st)
        kb.dma("sp", vc[:, :, :], vS[0:CTX, :].rearrange("(b p) n -> p b n", p=128), vc, r=[("dram", id(vS))], w=[vc])
        qp = Pool(kb, "aq", 2, [128, 4, 128], BF16, stack=st)
        kwp = Pool(kb, "akw", 2, [128, 2, 384], BF16, stack=st)
        vwp = Pool(kb, "avw", 2, [128, 3, 128], BF16, stack=st)
        pp = Pool(kb, "ap", 4, [128, 512], BF16, stack=st)
        dp = Pool(kb, "ad", 2, [64, 512], F32, stack=st)
        op_ = Pool(kb, "ao", 2, [64, 8, 128], BF16, stack=st)
        qv = qS.rearrange("(c p) t -> p c t", p=128)
        kv = kS.rearrange("(c p) t -> p c t", p=128)
        blocks = [(1, b) for b in range(CTX // 128)] + [(0, b) for b in range(NB)]
        self._acc_i = 0
        self._sc_i = 0
        for (seg, b) in blocks:
            t0 = b * 128 if seg == 1 else CTX + b * 128
            q = qp.next()
            kb.dma("sp", q[:, :, :], qv[:, :, t0:t0 + 128], q, r=[("dram", id(qS))], w=[q])
            keyblocks = []
            if seg == 0:
                lo = max(b - 1, 0)
                hi = min(b + 1, NB - 1)
                nb_ = hi - lo + 1
                kw, vw = kwp.next(), vwp.next()
                kb.dma("sp", kw[:, :, 0:nb_ * 128], kv[:, :, CTX + lo * 128:CTX + (hi + 1) * 128], kw,
                       r=[("dram", id(kS))], w=[kw])
                kb.dma("sp", vw[:, 0:nb_, :],
                       vS[CTX + lo * 128:CTX + (hi + 1) * 128, :].rearrange("(b p) n -> p b n", p=128), vw,
                       r=[("dram", id(vS))], w=[vw])
                for bb in range(lo, hi + 1):
                    i = bb - lo
                    mask = mP if bb < b else (mN if bb > b else None)
                    keyblocks.append((kw, i * 128, vw, i, mask))
            for i in range(CTX // 128):
                keyblocks.append((kc, i * 128, vc, i, None))
            oo = op_.next()
            accb = self.psum.bufs[0:4]
            scb = self.psum.bufs[4:8]
            for kvh in range(2):
                Pn = accb[(self._acc_i) % 4]
                Pd = accb[(self._acc_i + 1) % 4]
                self._acc_i += 2
                for bi, (kbuf, koff, vbuf, vi, mask) in enumerate(keyblocks):
                    Ps = [scb[self._sc_i % 4], scb[(self._sc_i + 1) % 4]]
                    self._sc_i += 2
                    pt = pp.next()
                    for par in range(2):
                        base = par * 64
                        var = 0 if (kvh * 64 == base) else 1
                        for j in range(2):
                            h = kvh * 4 + 2 * j + par
                            kb.op("pe", lambda e: e.matmul(Ps[par][:, j * 128:(j + 1) * 128],
                                                           lhsT=kbuf[base:base + 64, var, koff:koff + 128],
                                                           rhs=q[base:base + 64, h // 2, :], start=True, stop=True),
                                  r=[kbuf, q], w=[Ps[par]])
                        kb.op("act", lambda e: e.activation(out=pt[:, par * 256:(par + 1) * 256], in_=Ps[par][:, 0:256],
                                                            func=AF.Exp, scale=0.125), r=[Ps[par]], w=[pt])
                    if mask is not None:
                        kb.op("pool", lambda e: e.tensor_tensor(out=pt[:, :], in0=pt[:, :], in1=mask[:, :], op=ALU.mult),
                              r=[pt, mask], w=[pt])
                    first, last = bi == 0, bi == len(keyblocks) - 1
                    kb.op("pe", lambda e: e.matmul(Pn[0:64, :], lhsT=vbuf[:, vi, kvh * 64:(kvh + 1) * 64], rhs=pt[:, :],
                                                   start=first, stop=last), r=[vbuf, pt], w=[Pn])
                    kb.op("pe", lambda e: e.matmul(Pd[0:64, :], lhsT=ones[:, :], rhs=pt[:, :],
                                                   start=first, stop=last), r=[ones, pt], w=[Pd])
                den = dp.next()
                kb.op("dve", lambda e: e.tensor_tensor(
                    out=den[:, :].rearrange("p (r j q) -> p r j q", r=2, j=2), in0=Pd[0:64, :].rearrange("p (r j q) -> p r j q", r=2, j=2),
                    in1=eskb[:, kvh * 4:(kvh + 1) * 4, :].rearrange("p (j r) q -> p r j q", r=2),
                    op=ALU.add), r=[Pd, eskb], w=[den])
                kb.op("dve", lambda e: e.reciprocal(out=den[:, :], in_=den[:, :]), r=[den], w=[den])
                kb.op("dve", lambda e: e.tensor_tensor(
                    out=oo[:, kvh * 4:(kvh + 1) * 4, :].rearrange("p (j r) q -> p r j q", r=2),
                    in0=Pn[0:64, :].rearrange("p (r j q) -> p r j q", r=2, j=2),
                    in1=den[:, :].rearrange("p (r j q) -> p r j q", r=2, j=2), op=ALU.mult),
                      r=[Pn, den], w=[oo])
            kb.dma("sp", yaS.rearrange("(h p) t -> p h t", p=64)[:, :, t0:t0 + 128], oo[:, :, :], oo, r=[oo],
                   w=[("dram", id(yaS))])


Model.mixA_declare = _mixA_declare
Model.mixA_phase = _mixA_phase
Model.attn_phase = _attn_phase


def _rope_tables(T):
    NT = CTX + T
    C = np.ones((128, NT), np.float32)
    S = np.zeros((128, NT), np.float32)
    t = np.arange(T)
    row = (t // 64).astype(np.float32)
    col = (t % 64).astype(np.float32)
    inv = (10000.0 ** (-np.arange(16, dtype=np.float32) / 16)).astype(np.float32)
    for d in range(64):
        i = d % 16
        pos = row if d < 32 else col
        ang = (pos * inv[i]).astype(np.float32)
        sign = -1.0 if (d % 32) < 16 else 1.0
        for hb in (0, 64):
            C[hb + d, CTX:] = np.cos(ang)
            S[hb + d, CTX:] = sign * np.sin(ang)
    return C, S


def _swap_perm(n_heads):
    idx = []
    for h in range(n_heads):
        for d in range(64):
            p = d + 16 if (d % 32) < 16 else d - 16
            idx.append(h * 64 + p)
    return np.array(idx)


def host_inputs_A(inp, T):
    d = {}
    w = inp["w_in"]
    q = w[:, :, 0:512]
    k = w[:, :, 512:640]
    v = w[:, :, 640:768]
    kB = np.concatenate([k[:, :, 64:128], k[:, :, 0:64]], -1)
    s5 = w[:, :, 2624:3136]
    g = w[:, :, 3136:6208]
    d["w_inA"] = np.ascontiguousarray(np.concatenate(
        [q, q[:, :, _swap_perm(8)], k, k[:, :, _swap_perm(2)], kB, kB[:, :, _swap_perm(2)], v, s5, g], -1))
    C, S = _rope_tables(T)
    d["ropeC"], d["ropeS"] = C, S
    d["sinkT"] = np.ascontiguousarray(np.broadcast_to(inp["attn_sink"][:, None, :], (DEPTH, 64, 8)))
    j = np.arange(128)[:, None]
    i = np.arange(128)[None, :]
    d["maskP"] = np.ascontiguousarray(np.tile((j >= i).astype(np.float32), (1, 4)))
    d["maskN"] = np.ascontiguousarray(np.tile((j <= i).astype(np.float32), (1, 4)))
    return d


I32 = mybir.dt.int32
TWO_PI = 2.0 * np.pi


def _s5_declare(self):
    L = DEPTH
    self.s5_are = self.din("s5_are", [L, 2, 128, 4, 64])
    self.s5_aim = self.din("s5_aim", [L, 2, 128, 4, 64])
    self.s5_ls = self.din("s5_ls", [L, 2, 128, 4])
    self.s5_brT = self.din("s5_brT", [L, 128, 4, 64])
    self.s5_biT = self.din("s5_biT", [L, 128, 4, 64])
    self.s5_are2 = self.din("s5_are2", [L, 2, 128, 16])
    self.s5_aim2 = self.din("s5_aim2", [L, 2, 128, 16])
    self.s5_ls2 = self.din("s5_ls2", [L, 2, 128, 16])
    self.s5_crT = self.din("s5_crT", [L, 128, 16, 16])
    self.s5_ciT = self.din("s5_ciT", [L, 128, 16, 16])
    self.s5_rowmask = self.din("s5_rowmask", [128, 16, 2])
    self.s5_dT = self.din("s5_dT", [128, L * 4])
    self.s5_glub = self.din("s5_glub", [128, L * 4])
    self.s5_gluw = self.din("s5_gluw", [L, 512, 512])
    self.tauT = self.din("tauT", [128, 128])


def _sincos(self, ang, angk, n, S, Sk, C, Ck, st):
    kb = self.kb
    t = kb.sb("sc_t", [128, n], F32, st)
    ti = kb.sb("sc_i", [128, n], I32, st)
    tf = kb.sb("sc_f", [128, n], F32, st)
    for (off, dst, dk) in ((0.0, S, Sk), (0.25, C, Ck)):
        kb.op("dve", lambda e: e.tensor_scalar(out=t[:, :], in0=ang, scalar1=1.0 / TWO_PI, scalar2=off,
                                               op0=ALU.mult, op1=ALU.add), r=[angk], w=[t])
        kb.op("dve", lambda e: e.tensor_copy(out=ti[:, :], in_=t[:, :]), r=[t], w=[ti])
        kb.op("dve", lambda e: e.tensor_copy(out=tf[:, :], in_=ti[:, :]), r=[ti], w=[tf])
        kb.op("dve", lambda e: e.tensor_tensor(out=tf[:, :], in0=t[:, :], in1=tf[:, :], op=ALU.subtract),
              r=[t, tf], w=[tf])
        kb.op("act", lambda e: e.activation(out=dst, in_=tf[:, :], func=AF.Sin, scale=TWO_PI), r=[tf], w=[dk])


def _s5_dir(self, l, d, usS, ysbS, ysS, st, PS2):
    kb = self.kb
    NT = self.NT
    nchunk = NT // 128
    usv = usS.rearrange("(c p) t -> p c t", p=128)
    ybv = ysbS.rearrange("(c p) t -> p c t", p=128)
    ysv = ysS.rearrange("(c p) t -> p c t", p=128)
    V = lambda e_, f, r, w: kb.op(e_, f, r=r, w=w)
    _pers = {}
    for (n_, shp_, dt_) in (("DR", [128, 16, 128], BF16), ("DI", [128, 16, 128], BF16), ("COS", [128, 16, 128], F32),
                            ("SIN", [128, 16, 128], F32), ("RHO0", [128, 16, 128], F32), ("rho", [128, 16], F32),
                            ("lr2", [128, 16], F32), ("li2", [128, 16], F32), ("CR", [128, 16, 128], BF16),
                            ("CIn", [128, 16, 128], BF16), ("s5d", [128, 4], F32), ("s5gb", [128, 4], F32),
                            ("gluw", [128, 4, 512], BF16)):
        _pers[n_] = kb.sb(n_, shp_, dt_, st)
    sbf = lambda n, shp, dt=F32: _pers[n] if n in _pers else kb.sb(n, shp, dt, st)
    st2 = contextlib.ExitStack()
    tmpf = lambda n, shp, dt=F32: kb.sb(n, shp, dt, st2)
    are, aim = tmpf("are", [128, 4, 64]), tmpf("aim", [128, 4, 64])
    ls = tmpf("ls", [128, 4])
    br, bi = tmpf("br", [128, 4, 64]), tmpf("bi", [128, 4, 64])
    kb.dma("sp", are[:, :, :], self.s5_are[l, d], are, w=[are])
    kb.dma("sp", aim[:, :, :], self.s5_aim[l, d], aim, w=[aim])
    kb.dma("sp", ls[:, :], self.s5_ls[l, d], ls, w=[ls])
    kb.dma("sp", br[:, :, :], self.s5_brT[l], br, w=[br])
    kb.dma("sp", bi[:, :, :], self.s5_biT[l], bi, w=[bi])
    rmask = tmpf("rmask", [128, 16, 2])
    kb.dma("sp", rmask[:, :, :], self.s5_rowmask[:, :, :], rmask, w=[rmask])
    V("act", lambda e: e.activation(out=ls[:, :], in_=ls[:, :], func=AF.Exp), [ls], [ls])
    dtb = ls[:, :].unsqueeze(2).broadcast_to([128, 4, 64])
    adt, th = tmpf("adt", [128, 4, 64]), tmpf("th", [128, 4, 64])
    V("dve", lambda e: e.tensor_tensor(out=adt[:, :, :], in0=are[:, :, :], in1=dtb, op=ALU.mult), [are, ls], [adt])
    V("act", lambda e: e.activation(out=adt[:, :, :], in_=adt[:, :, :], func=AF.Exp), [adt], [adt])
    V("dve", lambda e: e.tensor_tensor(out=th[:, :, :], in0=aim[:, :, :], in1=dtb, op=ALU.mult), [aim, ls], [th])
    Sd, Cd = tmpf("Sd", [128, 256]), tmpf("Cd", [128, 256])
    thf = th[:, :, :].rearrange("p a b -> p (a b)")
    self.sincos(thf, th, 256, Sd[:, :], Sd, Cd[:, :], Cd, st2)
    lr, li = tmpf("lr", [128, 256]), tmpf("li", [128, 256])
    magf = adt[:, :, :].rearrange("p a b -> p (a b)")
    V("dve", lambda e: e.tensor_tensor(out=lr[:, :], in0=magf, in1=Cd[:, :], op=ALU.mult), [adt, Cd], [lr])
    V("dve", lambda e: e.tensor_tensor(out=li[:, :], in0=magf, in1=Sd[:, :], op=ALU.mult), [adt, Sd], [li])
    aref = are[:, :, :].rearrange("p a b -> p (a b)")
    aimf = aim[:, :, :].rearrange("p a b -> p (a b)")
    t1, t2, den = tmpf("t1", [128, 256]), tmpf("t2", [128, 256]), tmpf("den", [128, 256])
    V("dve", lambda e: e.tensor_tensor(out=t1[:, :], in0=aref, in1=aref, op=ALU.mult), [are], [t1])
    V("dve", lambda e: e.tensor_tensor(out=t2[:, :], in0=aimf, in1=aimf, op=ALU.mult), [aim], [t2])
    V("dve", lambda e: e.tensor_tensor(out=den[:, :], in0=t1[:, :], in1=t2[:, :], op=ALU.add), [t1, t2], [den])
    V("dve", lambda e: e.reciprocal(out=den[:, :], in_=den[:, :]), [den], [den])
    V("dve", lambda e: e.tensor_scalar(out=lr[:, :], in0=lr[:, :], scalar1=-1.0, scalar2=None, op0=ALU.add),
      [lr], [lr])
    cr, ci = tmpf("cr", [128, 256]), tmpf("ci", [128, 256])
    V("dve", lambda e: e.tensor_tensor(out=t1[:, :], in0=lr[:, :], in1=aref, op=ALU.mult), [lr, are], [t1])
    V("dve", lambda e: e.tensor_tensor(out=t2[:, :], in0=li[:, :], in1=aimf, op=ALU.mult), [li, aim], [t2])
    V("dve", lambda e: e.tensor_tensor(out=cr[:, :], in0=t1[:, :], in1=t2[:, :], op=ALU.add), [t1, t2], [cr])
    V("dve", lambda e: e.tensor_tensor(out=cr[:, :], in0=cr[:, :], in1=den[:, :], op=ALU.mult), [cr, den], [cr])
    V("dve", lambda e: e.tensor_tensor(out=t1[:, :], in0=li[:, :], in1=aref, op=ALU.mult), [li, are], [t1])
    V("dve", lambda e: e.tensor_tensor(out=t2[:, :], in0=lr[:, :], in1=aimf, op=ALU.mult), [lr, aim], [t2])
    V("dve", lambda e: e.tensor_tensor(out=ci[:, :], in0=t1[:, :], in1=t2[:, :], op=ALU.subtract), [t1, t2], [ci])
    V("dve", lambda e: e.tensor_tensor(out=ci[:, :], in0=ci[:, :], in1=den[:, :], op=ALU.mult), [ci, den], [ci])
    brf = br[:, :, :].rearrange("p a b -> p (a b)")
    bif = bi[:, :, :].rearrange("p a b -> p (a b)")
    bbr, bbi = tmpf("bbr", [128, 4, 64]), tmpf("bbi", [128, 4, 64])
    bbrf = bbr[:, :, :].rearrange("p a b -> p (a b)")
    bbif = bbi[:, :, :].rearrange("p a b -> p (a b)")
    V("dve", lambda e: e.tensor_tensor(out=t1[:, :], in0=cr[:, :], in1=brf, op=ALU.mult), [cr, br], [t1])
    V("dve", lambda e: e.tensor_tensor(out=t2[:, :], in0=ci[:, :], in1=bif, op=ALU.mult), [ci, bi], [t2])
    V("dve", lambda e: e.tensor_tensor(out=bbrf, in0=t1[:, :], in1=t2[:, :], op=ALU.subtract), [t1, t2], [bbr])
    V("dve", lambda e: e.tensor_tensor(out=t1[:, :], in0=cr[:, :], in1=bif, op=ALU.mult), [cr, bi], [t1])
    V("dve", lambda e: e.tensor_tensor(out=t2[:, :], in0=ci[:, :], in1=brf, op=ALU.mult), [ci, br], [t2])
    V("dve", lambda e: e.tensor_tensor(out=bbif, in0=t1[:, :], in1=t2[:, :], op=ALU.add), [t1, t2], [bbi])
    DR, DI = sbf("DR", [128, 16, 128], BF16), sbf("DI", [128, 16, 128], BF16)
    for j in range(16):
        for gp in range(2):
            for (dst, srcb) in ((DR, bbr), (DI, bbi)):
                V("dve", lambda e: e.tensor_scalar(out=dst[:, j, gp * 64:(gp + 1) * 64], in0=srcb[:, j // 4, :],
                                                   scalar1=rmask[:, j, gp:gp + 1], scalar2=None, op0=ALU.mult),
                  [srcb, rmask], [dst])
    are2, aim2, ls2 = tmpf("are2", [128, 16]), tmpf("aim2", [128, 16]), tmpf("ls2", [128, 16])
    kb.dma("sp", are2[:, :], self.s5_are2[l, d], are2, w=[are2])
    kb.dma("sp", aim2[:, :], self.s5_aim2[l, d], aim2, w=[aim2])
    kb.dma("sp", ls2[:, :], self.s5_ls2[l, d], ls2, w=[ls2])
    tau = tmpf("tau", [128, 128])
    kb.dma("sp", tau[:, :], self.tauT[:, :], tau, w=[tau])
    V("act", lambda e: e.activation(out=ls2[:, :], in_=ls2[:, :], func=AF.Exp), [ls2], [ls2])
    rho, th2 = sbf("rho", [128, 16]), tmpf("th2", [128, 16])
    V("dve", lambda e: e.tensor_tensor(out=rho[:, :], in0=are2[:, :], in1=ls2[:, :], op=ALU.mult), [are2, ls2], [rho])
    V("act", lambda e: e.activation(out=rho[:, :], in_=rho[:, :], func=AF.Exp), [rho], [rho])
    V("dve", lambda e: e.tensor_tensor(out=th2[:, :], in0=aim2[:, :], in1=ls2[:, :], op=ALU.mult), [aim2, ls2], [th2])
    ang = tmpf("ang", [128, 16, 128])
    V("dve", lambda e: e.tensor_tensor(out=ang[:, :, :], in0=th2[:, :].unsqueeze(2).broadcast_to([128, 16, 128]),
                                       in1=tau[:, :].unsqueeze(1).broadcast_to([128, 16, 128]), op=ALU.mult),
      [th2, tau], [ang])
    COS, SIN = sbf("COS", [128, 16, 128]), sbf("SIN", [128, 16, 128])
    self.sincos(ang[:, :, :].rearrange("p a b -> p (a b)"), ang, 2048,
                SIN[:, :, :].rearrange("p a b -> p (a b)"), SIN, COS[:, :, :].rearrange("p a b -> p (a b)"), COS, st2)
    S1, C1 = tmpf("S1", [128, 16]), tmpf("C1", [128, 16])
    self.sincos(th2[:, :], th2, 16, S1[:, :], S1, C1[:, :], C1, st2)
    lr2, li2 = sbf("lr2", [128, 16]), sbf("li2", [128, 16])
    V("dve", lambda e: e.tensor_tensor(out=lr2[:, :], in0=rho[:, :], in1=C1[:, :], op=ALU.mult), [rho, C1], [lr2])
    V("dve", lambda e: e.tensor_tensor(out=li2[:, :], in0=rho[:, :], in1=S1[:, :], op=ALU.mult), [rho, S1], [li2])
    RHO0 = sbf("RHO0", [128, 16, 128])
    V("dve", lambda e: e.tensor_copy(out=RHO0[:, :, :], in_=rho[:, :].unsqueeze(2).broadcast_to([128, 16, 128])),
      [rho], [RHO0])
    f0 = 127 if d == 1 else 0
    V("dve", lambda e: e.memset(RHO0[:, :, f0:f0 + 1], 0.0), [], [RHO0])
    crT, ciT = tmpf("crT", [128, 16, 16]), tmpf("ciT", [128, 16, 16])
    kb.dma("sp", crT[:, :, :], self.s5_crT[l], crT, w=[crT])
    kb.dma("sp", ciT[:, :, :], self.s5_ciT[l], ciT, w=[ciT])
    CR, CIn = sbf("CR", [128, 16, 128], BF16), sbf("CIn", [128, 16, 128], BF16)
    V("dve", lambda e: e.memset(CR[:, :, :], 0.0), [], [CR])
    V("dve", lambda e: e.memset(CIn[:, :, :], 0.0), [], [CIn])
    for j in range(16):
        for gp in range(2):
            c0 = 32 * (j % 4) + 16 * gp
            V("dve", lambda e: e.tensor_copy(out=CR[gp * 64:(gp + 1) * 64, j, c0:c0 + 16],
                                             in_=crT[gp * 64:(gp + 1) * 64, j, :]), [crT], [CR])
            V("dve", lambda e: e.tensor_scalar(out=CIn[gp * 64:(gp + 1) * 64, j, c0:c0 + 16],
                                               in0=ciT[gp * 64:(gp + 1) * 64, j, :], scalar1=-1.0, scalar2=None,
                                               op0=ALU.mult), [ciT], [CIn])
    if d == 0:
        dv, gb = sbf("s5d", [128, 4]), sbf("s5gb", [128, 4])
        kb.dma("sp", dv[:, :], self.s5_dT[:, l * 4:(l + 1) * 4], dv, w=[dv])
        kb.dma("sp", gb[:, :], self.s5_glub[:, l * 4:(l + 1) * 4], gb, w=[gb])
        gw = sbf("gluw", [128, 4, 512], BF16)
        for c in range(4):
            kb.dma("pool", gw[:, c, :], self.s5_gluw[l, c * 128:(c + 1) * 128, :], gw, w=[gw])
    kb.barrier()
    st2.close()
    usp = Pool(kb, "s5u", 2, [128, 4, 128], BF16, stack=st)
    usfp = Pool(kb, "s5uf", 2, [128, 4, 128], F32, stack=st)
    ybp = Pool(kb, "s5yb", 2, [128, 4, 128], F32, stack=st)
    mp = [Pool(kb, "s5m%d" % i, 1, [128, 8, 128], F32, stack=st) for i in range(4)]
    ZR, ZI = sbf("ZR", [128, 16, 128]), sbf("ZI", [128, 16, 128])
    XZR, XZI = sbf("XZR", [128, 16, 128]), sbf("XZI", [128, 16, 128])
    up_ = [Pool(kb, "s5t%d" % i, 1, [128, 16, 128], F32, stack=st) for i in range(2)]
    XR, XI = sbf("XR", [128, 16, 128], BF16), sbf("XI", [128, 16, 128], BF16)
    xlr, xli = sbf("xlr", [128, 16]), sbf("xli", [128, 16])
    cjr, cji = sbf("cjr", [128, 16]), sbf("cji", [128, 16])
    tt = [sbf("s5tt%d" % i, [128, 16]) for i in range(4)]
    yo = Pool(kb, "s5yo", 2, [128, 4, 128], F32, stack=st)
    rev = (d == 1)
    R3 = (lambda ap: ap[:, :, ::-1]) if rev else (lambda ap: ap)
    first, last = (127, 0) if rev else (0, 127)
    order = [0, 1] + list(range(2, nchunk))
    if rev:
        order = [1, 0] + list(range(nchunk - 1, 1, -1))
    yield
    for ci_, ch in enumerate(order):
        if ci_ > 0:
            yield
        t0 = ch * 128
        us = usp.next()
        kb.dma("pool", us[:, :, :], usv[:, :, t0:t0 + 128], us, r=[("dram", id(usS))], w=[us])
        for hf in range(2):
            PR, PI = PS2.next(), PS2.next()
            for jj in range(8):
                j = hf * 8 + jj
                kb.op("pe", lambda e: e.matmul(PR[:, jj * 128:(jj + 1) * 128], lhsT=DR[:, j, :], rhs=us[:, j // 4, :],
                                               start=True, stop=True), r=[DR, us], w=[PR])
                kb.op("pe", lambda e: e.matmul(PI[:, jj * 128:(jj + 1) * 128], lhsT=DI[:, j, :], rhs=us[:, j // 4, :],
                                               start=True, stop=True), r=[DI, us], w=[PI])
            prv = PR[:, :].rearrange("p (a b) -> p a b", a=8)
            piv = PI[:, :].rearrange("p (a b) -> p a b", a=8)
            cs = R3(COS[:, hf * 8:(hf + 1) * 8, :])
            sn = R3(SIN[:, hf * 8:(hf + 1) * 8, :])
            m = [p.next() for p in mp]
            V("dve", lambda e: e.tensor_tensor(out=m[0][:, :, :], in0=prv, in1=cs, op=ALU.mult), [PR, COS], [m[0]])
            V("dve", lambda e: e.tensor_tensor(out=m[1][:, :, :], in0=piv, in1=sn, op=ALU.mult), [PI, SIN], [m[1]])
            V("dve", lambda e: e.tensor_tensor(out=m[2][:, :, :], in0=piv, in1=cs, op=ALU.mult), [PI, COS], [m[2]])
            V("dve", lambda e: e.tensor_tensor(out=m[3][:, :, :], in0=prv, in1=sn, op=ALU.mult), [PR, SIN], [m[3]])
            V("pool", lambda e: e.tensor_tensor(out=ZR[:, hf * 8:(hf + 1) * 8, :], in0=m[0][:, :, :], in1=m[1][:, :, :],
                                                op=ALU.add), [m[0], m[1]], [ZR])
            V("pool", lambda e: e.tensor_tensor(out=ZI[:, hf * 8:(hf + 1) * 8, :], in0=m[2][:, :, :], in1=m[3][:, :, :],
                                                op=ALU.subtract), [m[2], m[3]], [ZI])
            yield
        if ci_ > 0:
            V("pool", lambda e: e.tensor_tensor(out=tt[0][:, :], in0=lr2[:, :], in1=xlr[:, :], op=ALU.mult), [lr2, xlr], [tt[0]])
            V("pool", lambda e: e.tensor_tensor(out=tt[1][:, :], in0=li2[:, :], in1=xli[:, :], op=ALU.mult), [li2, xli], [tt[1]])
            V("pool", lambda e: e.tensor_tensor(out=cjr[:, :], in0=tt[0][:, :], in1=tt[1][:, :], op=ALU.subtract), [tt[0], tt[1]], [cjr])
            V("pool", lambda e: e.tensor_tensor(out=tt[2][:, :], in0=lr2[:, :], in1=xli[:, :], op=ALU.mult), [lr2, xli], [tt[2]])
            V("pool", lambda e: e.tensor_tensor(out=tt[3][:, :], in0=li2[:, :], in1=xlr[:, :], op=ALU.mult), [li2, xlr], [tt[3]])
            V("pool", lambda e: e.tensor_tensor(out=cji[:, :], in0=tt[2][:, :], in1=tt[3][:, :], op=ALU.add), [tt[2], tt[3]], [cji])
            V("pool", lambda e: e.tensor_tensor(out=ZR[:, :, first], in0=ZR[:, :, first], in1=cjr[:, :], op=ALU.add), [ZR, cjr], [ZR])
            V("pool", lambda e: e.tensor_tensor(out=ZI[:, :, first], in0=ZI[:, :, first], in1=cji[:, :], op=ALU.add), [ZI, cji], [ZI])
        fl = lambda b_: (b_[:, :, :].rearrange("p a b -> p (a b)")[:, ::-1] if rev
                         else b_[:, :, :].rearrange("p a b -> p (a b)"))
        V("dve", lambda e: e.tensor_tensor_scan(out=fl(XZR), data0=fl(RHO0), data1=fl(ZR), initial=0.0,
                                                op0=ALU.mult, op1=ALU.add), [RHO0, ZR], [XZR])
        V("dve", lambda e: e.tensor_tensor_scan(out=fl(XZI), data0=fl(RHO0), data1=fl(ZI), initial=0.0,
                                                op0=ALU.mult, op1=ALU.add), [RHO0, ZI], [XZI])
        yield
        cs, sn = R3(COS[:, :, :]), R3(SIN[:, :, :])
        ua, ub = up_[0].next(), up_[1].next()
        V("dve", lambda e: e.tensor_tensor(out=ua[:, :, :], in0=XZR[:, :, :], in1=cs, op=ALU.mult), [XZR, COS], [ua])
        V("pool", lambda e: e.tensor_tensor(out=ub[:, :, :], in0=XZI[:, :, :], in1=sn, op=ALU.mult), [XZI, SIN], [ub])
        V("dve", lambda e: e.tensor_tensor(out=XR[:, :, :], in0=ua[:, :, :], in1=ub[:, :, :], op=ALU.subtract), [ua, ub], [XR])
        V("pool", lambda e: e.tensor_tensor(out=xlr[:, :], in0=ua[:, :, last], in1=ub[:, :, last], op=ALU.subtract), [ua, ub], [xlr])
        V("pool", lambda e: e.tensor_tensor(out=ua[:, :, :], in0=XZR[:, :, :], in1=sn, op=ALU.mult), [XZR, SIN], [ua])
        V("dve", lambda e: e.tensor_tensor(out=ub[:, :, :], in0=XZI[:, :, :], in1=cs, op=ALU.mult), [XZI, COS], [ub])
        V("pool", lambda e: e.tensor_tensor(out=XI[:, :, :], in0=ua[:, :, :], in1=ub[:, :, :], op=ALU.add), [ua, ub], [XI])
        V("pool", lambda e: e.tensor_tensor(out=xli[:, :], in0=ua[:, :, last], in1=ub[:, :, last], op=ALU.add), [ua, ub], [xli])
        yield
        PY = PS2.next()
        for cc in range(4):
            for jj in range(4):
                j = cc * 4 + jj
                kb.op("pe", lambda e: e.matmul(PY[:, cc * 128:(cc + 1) * 128], lhsT=CR[:, j, :], rhs=XR[:, j, :],
                                               start=(jj == 0), stop=False), r=[CR, XR], w=[PY])
                kb.op("pe", lambda e: e.matmul(PY[:, cc * 128:(cc + 1) * 128], lhsT=CIn[:, j, :], rhs=XI[:, j, :],
                                               start=False, stop=(jj == 3)), r=[CIn, XI], w=[PY])
        pyv = PY[:, 0:512].rearrange("p (a b) -> p a b", a=4)
        if d == 1:
            y = yo.next()
            V("act", lambda e: e.activation(out=y[:, :, :], in_=pyv, func=AF.Copy), [PY], [y])
            kb.dma("act", ybv[:, :, t0:t0 + 128], y[:, :, :], y, r=[y], w=[("dram", id(ysbS))])
        else:
            yb, usf = ybp.next(), usfp.next()
            kb.dma("sp", yb[:, :, :], ybv[:, :, t0:t0 + 128], yb, r=[("dram", id(ysbS))], w=[yb])
            kb.dma("sp", usf[:, :, :], usv[:, :, t0:t0 + 128], usf, r=[("dram", id(usS))], w=[usf])
            y = yo.next()
            V("dve", lambda e: e.tensor_tensor(out=y[:, :, :], in0=pyv, in1=yb[:, :, :], op=ALU.add), [PY, yb], [y])
            V("pool", lambda e: e.tensor_tensor(out=usf[:, :, :], in0=usf[:, :, :],
                                                in1=dv[:, :].unsqueeze(2).broadcast_to([128, 4, 128]), op=ALU.mult),
              [usf, dv], [usf])
            V("pool", lambda e: e.tensor_tensor(out=y[:, :, :], in0=y[:, :, :], in1=usf[:, :, :], op=ALU.add), [y, usf], [y])
            g1 = yb
            V("pool", lambda e: e.tensor_tensor(out=g1[:, :, :], in0=y[:, :, :], in1=y[:, :, :], op=ALU.mult), [y], [g1])
            V("dve", lambda e: e.tensor_scalar(out=g1[:, :, :], in0=g1[:, :, :], scalar1=0.044715, scalar2=1.0,
                                               op0=ALU.mult, op1=ALU.add), [g1], [g1])
            V("dve", lambda e: e.tensor_tensor(out=g1[:, :, :], in0=g1[:, :, :], in1=y[:, :, :], op=ALU.mult), [g1, y], [g1])
            V("act", lambda e: e.activation(out=g1[:, :, :], in_=g1[:, :, :], func=AF.Sigmoid, scale=1.5957691216057308),
              [g1], [g1])
            V("dve", lambda e: e.tensor_tensor(out=y[:, :, :], in0=y[:, :, :], in1=g1[:, :, :], op=ALU.mult), [y, g1], [y])
            geb = us
            V("act", lambda e: e.activation(out=geb[:, :, :], in_=y[:, :, :], func=AF.Copy), [y], [geb])
            PG = PS2.next()
            for oc in range(4):
                for kc in range(4):
                    kb.op("pe", lambda e: e.matmul(PG[:, oc * 128:(oc + 1) * 128], lhsT=gw[:, kc, oc * 128:(oc + 1) * 128],
                                                   rhs=geb[:, kc, :], start=(kc == 0), stop=(kc == 3)), r=[gw, geb], w=[PG])
                V("act", lambda e: e.activation(out=usf[:, oc, :], in_=PG[:, oc * 128:(oc + 1) * 128], func=AF.Sigmoid,
                                                bias=gb[:, oc:oc + 1]), [PG, gb], [usf])
            yso = usp.next()
            V("dve", lambda e: e.tensor_tensor(out=yso[:, :, :], in0=y[:, :, :], in1=usf[:, :, :], op=ALU.mult), [y, usf], [yso])
            kb.dma("sp", ysv[:, :, t0:t0 + 128], yso[:, :, :], yso, r=[yso], w=[("dram", id(ysS))])


Model.s5_declare = _s5_declare
Model.sincos = _sincos
Model.s5_dir = _s5_dir


def host_inputs_s5(inp):
    L = DEPTH
    d = {}

    def drive(a):
        a = a.reshape(L, 2, 4, 8, 1, 64)
        a = np.broadcast_to(a, (L, 2, 4, 8, 16, 64))
        return np.ascontiguousarray(a.transpose(0, 1, 3, 4, 2, 5).reshape(L, 2, 128, 4, 64))
    d["s5_are"] = drive(inp["s5_a_re"])
    d["s5_aim"] = drive(inp["s5_a_im"])
    lsd = inp["s5_log_step"].reshape(L, 2, 4, 8, 1)
    d["s5_ls"] = np.ascontiguousarray(np.broadcast_to(lsd, (L, 2, 4, 8, 16)).transpose(0, 1, 3, 4, 2).reshape(L, 2, 128, 4))

    def bT(b):
        b = b.reshape(L, 4, 8, 64, 16)
        return np.ascontiguousarray(b.transpose(0, 2, 4, 1, 3).reshape(L, 128, 4, 64))
    d["s5_brT"] = bT(inp["s5_b_re"])
    d["s5_biT"] = bT(inp["s5_b_im"])

    def st2(a):
        a = a.reshape(L, 2, 16, 2, 64)
        return np.ascontiguousarray(a.transpose(0, 1, 3, 4, 2).reshape(L, 2, 128, 16))
    d["s5_are2"] = st2(inp["s5_a_re"])
    d["s5_aim2"] = st2(inp["s5_a_im"])
    ls2 = np.broadcast_to(inp["s5_log_step"].reshape(L, 2, 16, 2, 1), (L, 2, 16, 2, 64))
    d["s5_ls2"] = np.ascontiguousarray(ls2.transpose(0, 1, 3, 4, 2).reshape(L, 2, 128, 16))

    def cT(c):
        c = c.reshape(L, 16, 2, 16, 64)
        return np.ascontiguousarray(c.transpose(0, 2, 4, 1, 3).reshape(L, 128, 16, 16))
    d["s5_crT"] = cT(inp["s5_c_re"])
    d["s5_ciT"] = cT(inp["s5_c_im"])
    k = np.arange(128)[:, None, None] // 16
    j = np.arange(16)[None, :, None]
    gp = np.arange(2)[None, None, :]
    d["s5_rowmask"] = np.ascontiguousarray((k == 2 * (j % 4) + gp).astype(np.float32))
    d["s5_dT"] = np.ascontiguousarray(inp["s5_d"].reshape(L * 4, 128).T)
    d["s5_glub"] = np.ascontiguousarray(inp["s5_glu_b"].reshape(L * 4, 128).T)
    d["s5_gluw"] = inp["s5_glu_w"]
    d["tauT"] = np.ascontiguousarray(np.broadcast_to(np.arange(128, dtype=np.float32)[None, :], (128, 128)))
    return d


NCH_B = 15


def _rwkv_declare(self):
    L = DEPTH
    self.w_inB = self.din("w_inB", [L, D, NCH_B * 128])
    self.rk_mu = self.din("rk_mu", [128, L * NCH_B])
    self.rk_w0 = self.din("rk_w0", [128, L * 8])
    for n in ("a0", "kk", "ka", "rk", "gng", "gnb"):
        setattr(self, "rk_" + n, self.din("rk_" + n, [128, L * 4]))
    self.rk_w2 = self.din("rk_w2", [L, 128, 512])
    self.rk_a2 = self.din("rk_a2", [L, 64, 512])
    self.rk_g2 = self.din("rk_g2", [L, 128, 512])
    self.bd64 = self.din("bd64", [128, 128])
    self.identb = self.din("identb", [128, 128])
    self.ones0 = self.din("ones0", [2, 128, 128])
    self.rk_MT = self.din("rk_MT", [2, 128, 512])
    self.rk_MN = self.din("rk_MN", [2, 128, 512])


def _rwkv_prep_phase(self, l, src, S):
    kb = self.kb
    W = 256
    Wh = W + 2
    NB = W // 128
    with kb.phase() as st:
        self.psum = Pool(kb, "ps", 8, [128, 512], F32, psum=True, stack=st)
        V = lambda e_, f, r, w: kb.op(e_, f, r=r, w=w)
        sbf = lambda n, shp, dt=F32: kb.sb(n, shp, dt, st)
        w = sbf("wB", [128, NCH, NCH_B * 128], BF16)
        for c in range(NCH):
            kb.dma("pool", w[:, c, :], self.w_inB[l, c * 128:(c + 1) * 128, :], w, w=[w])
        w2b, a2b, g2b = sbf("w2b", [128, 512], BF16), sbf("a2b", [64, 512], BF16), sbf("g2b", [128, 512], BF16)
        kb.dma("pool", w2b[:, :], self.rk_w2[l], w2b, w=[w2b])
        kb.dma("pool", a2b[:, :], self.rk_a2[l], a2b, w=[a2b])
        kb.dma("pool", g2b[:, :], self.rk_g2[l], g2b, w=[g2b])
        bd, idb = sbf("bd", [128, 128], BF16), sbf("idb", [128, 128], BF16)
        kb.dma("pool", bd[:, :], self.bd64[:, :], bd, w=[bd])
        kb.dma("pool", idb[:, :], self.identb[:, :], idb, w=[idb])
        on0 = sbf("on0", [128, 2, 128])
        kb.dma("sp", on0[:, :, :], self.ones0.rearrange("d p t -> p d t"), on0, w=[on0])
        ON = [sbf("ON%d" % d_, [128, 4 * (W // 128), 128]) for d_ in range(2)]
        for d_ in range(2):
            V("dve", lambda e: e.tensor_copy(out=ON[d_][:, :, :], in_=on0[:, d_:d_ + 1, :].broadcast_to([128, 4 * (W // 128), 128])),
              [on0], [ON[d_]])
        mu, omu, hmu = sbf("mu", [128, NCH_B]), sbf("omu", [128, NCH_B]), sbf("hmu", [128, NCH_B])
        kb.dma("sp", mu[:, :], self.rk_mu[:, l * NCH_B:(l + 1) * NCH_B], mu, w=[mu])
        V("dve", lambda e: e.tensor_scalar(out=omu[:, :], in0=mu[:, :], scalar1=-1.0, scalar2=1.0, op0=ALU.mult, op1=ALU.add), [mu], [omu])
        V("dve", lambda e: e.tensor_scalar(out=hmu[:, :], in0=mu[:, :], scalar1=0.5, scalar2=None, op0=ALU.mult), [mu], [hmu])
        w0 = sbf("w0", [128, 8])
        kb.dma("sp", w0[:, :], self.rk_w0[:, l * 8:(l + 1) * 8], w0, w=[w0])
        pv = {}
        for n in ("a0", "kk", "ka", "rk"):
            pv[n] = sbf("p_" + n, [128, 4])
            kb.dma("sp", pv[n][:, :], getattr(self, "rk_" + n)[:, l * 4:(l + 1) * 4], pv[n], w=[pv[n]])
        omka = sbf("omka", [128, 4])
        V("dve", lambda e: e.tensor_scalar(out=omka[:, :], in0=pv["ka"][:, :], scalar1=-1.0, scalar2=1.0, op0=ALU.mult, op1=ALU.add),
          [pv["ka"]], [omka])
        pl = self.stat_pools(Wh, st)
        xp = Pool(kb, "x", 2, [128, NCH, Wh], F32, stack=st)
        xn = sbf("xn", [128, NCH, Wh])
        u = sbf("u", [128, NCH, Wh], BF16)
        zcp = Pool(kb, "zc", 2, [128, Wh], F32, stack=st)
        tmpp = Pool(kb, "ztmp", 2, [128, W], F32, stack=st)
        Z = sbf("Z", [128, NCH_B, W])
        tw, sg, alb = sbf("tw", [128, W], BF16), sbf("sg", [128, W], BF16), sbf("alb", [64, W], BF16)
        LW = [sbf("LW%d" % d_, [128, 4, W]) for d_ in range(2)]
        A, KK, KM, Bv = sbf("A", [128, 4, W]), sbf("KK", [128, 4, W]), sbf("KM", [128, 4, W]), sbf("Bv", [128, 4, W])
        SQ = sbf("SQ", [128, 4, W], BF16)
        T1, T2 = sbf("T1", [128, 4, W]), sbf("T2", [128, 4, W])
        Gp = Pool(kb, "Go", 2, [128, 4, W], BF16, stack=st)
        Bop = Pool(kb, "Bo", 2, [128, 4, W], BF16, stack=st)
        Vb = sbf("Vb", [128, 4, W], BF16)
        tokp = Pool(kb, "tok", 3, [128, NB, 512], BF16, stack=st)
        Lc, Lr, Lq = sbf("Lc", [128, 4, W]), sbf("Lr", [128, 4, W]), sbf("Lq", [128, 4, W])
        E = sbf("E", [128, 4, W])
        outp = {n: Pool(kb, n, 2, [128, 4, W], BF16, stack=st) for n in ("RHO", "KAP", "BET", "KTI")}
        scp = Pool(kb, "sco", 2, [128, NB, 4, 3], F32, stack=st)
        lmn = sbf("lmn", [128, 4, NB])
        srcv = src.rearrange("(c p) t -> p c t", p=128)
        fmv = lambda t_: t_.rearrange("(c p) t -> p c t", p=128)

        def transpose_store(srcb, dst, t0):
            tk = tokp.next()
            for tb in range(NB):
                P = self.psum.next()
                pb = P[:, 0:256].bitcast(BF16)
                for c in range(4):
                    kb.op("pe", lambda e: e.transpose(pb[:, c * 128:(c + 1) * 128], srcb[:, c, tb * 128:(tb + 1) * 128], idb[:, :]),
                          r=[srcb, idb], w=[P])
                V("act", lambda e: e.activation(out=tk[:, tb, :], in_=pb[:, 0:512], func=AF.Copy), [P], [tk])
            kb.dma("sp", dst[t0:t0 + W, :].rearrange("(b p) n -> p b n", p=128), tk[:, :, :], tk, r=[tk], w=[("dram", id(dst))])

        for (seg, t0, _) in self.tiles(W):
            seg_lo, seg_hi = (0, CTX) if seg == 1 else (CTX, self.NT)
            lo, hi = max(t0 - 1, seg_lo), min(t0 + W + 1, seg_hi)
            x = xp.next()
            c0 = lo - (t0 - 1)
            if c0 > 0:
                V("dve", lambda e: e.memset(x[:, :, 0:1], 0.0), [], [x])
            if hi < t0 + W + 1:
                V("dve", lambda e: e.memset(x[:, :, Wh - 1:Wh], 0.0), [], [x])
            kb.dma("sp", x[:, :, c0:c0 + (hi - lo)], srcv[:, :, lo:hi], x, r=[("dram", id(src))], w=[x])
            rstd, nmr = self.ln_stats_w(x, Wh, pl)
            self.normalize(xn, x, Wh, rstd, nmr)
            for c in range(NCH):
                V("act", lambda e: e.activation(out=u[:, c, :], in_=xn[:, c, :], func=AF.Identity,
                                                scale=self.modp1[:, 4 * NCH + c, seg:seg + 1],
                                                bias=self.mods[:, 3 * NCH + c, seg:seg + 1]), [xn, self.modp1, self.mods], [u])
            if c0 > 0:
                V("dve", lambda e: e.memset(u[:, :, 0:1], 0.0), [], [u])
            if hi < t0 + W + 1:
                V("dve", lambda e: e.memset(u[:, :, Wh - 1:Wh], 0.0), [], [u])
            for ch in range(NCH_B):
                P = self.psum.next()
                for c in range(NCH):
                    kb.op("pe", lambda e: e.matmul(P[:, 0:Wh], lhsT=w[:, c, ch * 128:(ch + 1) * 128], rhs=u[:, c, :],
                                                   start=(c == 0), stop=(c == NCH - 1)), r=[w, u], w=[P])
                zc, tm = zcp.next(), tmpp.next()
                V("act", lambda e: e.activation(out=zc[:, :], in_=P[:, 0:Wh], func=AF.Copy), [P], [zc])
                V("pool", lambda e: e.tensor_tensor(out=tm[:, :], in0=zc[:, 0:W], in1=zc[:, 2:W + 2], op=ALU.add), [zc], [tm])
                V("act", lambda e: e.activation(out=tm[:, :], in_=tm[:, :], func=AF.Identity, scale=hmu[:, ch:ch + 1]), [tm, hmu], [tm])
                V("dve", lambda e: e.scalar_tensor_tensor(out=Z[:, ch, :], in0=zc[:, 1:W + 1], scalar=omu[:, ch:ch + 1],
                                                          in1=tm[:, :], op0=ALU.mult, op1=ALU.add), [zc, omu, tm], [Z])
            R_, K_, V_ = Z[:, 0:4, :], Z[:, 4:8, :], Z[:, 8:12, :]
            V("act", lambda e: e.activation(out=tw[:, :], in_=Z[:, 12, :], func=AF.Tanh), [Z], [tw])
            V("act", lambda e: e.activation(out=sg[:, :], in_=Z[:, 13, :], func=AF.Sigmoid), [Z], [sg])
            V("act", lambda e: e.activation(out=alb[:, :], in_=Z[0:64, 14, :], func=AF.Copy), [Z], [alb])
            for d_ in range(2):
                for c in range(4):
                    P = self.psum.next()
                    kb.op("pe", lambda e: e.matmul(P[:, 0:W], lhsT=w2b[d_ * 64:(d_ + 1) * 64, c * 128:(c + 1) * 128],
                                                   rhs=tw[d_ * 64:(d_ + 1) * 64, :], start=True, stop=True), r=[w2b, tw], w=[P])
                    V("act", lambda e: e.activation(out=LW[d_][:, c, :], in_=P[:, 0:W], func=AF.Sigmoid,
                                                    bias=w0[:, d_ * 4 + c:d_ * 4 + c + 1]), [P, w0], [LW[d_]])
                V("pool", lambda e: e.tensor_scalar(out=LW[d_][:, :, :], in0=LW[d_][:, :, :], scalar1=-DECAY_SCALE, scalar2=None,
                                                    op0=ALU.mult), [LW[d_]], [LW[d_]])
            Go = Gp.next()
            for c in range(4):
                P = self.psum.next()
                kb.op("pe", lambda e: e.matmul(P[:, 0:W], lhsT=a2b[0:64, c * 128:(c + 1) * 128], rhs=alb[0:64, :],
                                               start=True, stop=True), r=[a2b, alb], w=[P])
                V("act", lambda e: e.activation(out=A[:, c, :], in_=P[:, 0:W], func=AF.Sigmoid, bias=pv["a0"][:, c:c + 1]),
                  [P, pv["a0"]], [A])
                P = self.psum.next()
                kb.op("pe", lambda e: e.matmul(P[:, 0:W], lhsT=g2b[:, c * 128:(c + 1) * 128], rhs=sg[:, :],
                                               start=True, stop=True), r=[g2b, sg], w=[P])
                V("act", lambda e: e.activation(out=Go[:, c, :], in_=P[:, 0:W], func=AF.Copy), [P], [Go])
            kb.dma("sp", fmv(S["gR"])[:, :, t0:t0 + W], Go[:, :, :], Go, r=[Go], w=[("dram", id(S["gR"]))])
            for c in range(4):
                V("pool", lambda e: e.tensor_scalar(out=KK[:, c, :], in0=Z[:, 4 + c, :], scalar1=pv["kk"][:, c:c + 1], scalar2=None,
                                                    op0=ALU.mult), [Z, pv["kk"]], [KK])
            V("act", lambda e: e.activation(out=SQ[:, :, :], in_=KK[:, :, :], func=AF.Square), [KK], [SQ])
            for c in range(4):
                P = self.psum.next()
                kb.op("pe", lambda e: e.matmul(P[:, 0:W], lhsT=bd[:, :], rhs=SQ[:, c, :], start=True, stop=True), r=[bd, SQ], w=[P])
                V("dve", lambda e: e.tensor_scalar(out=T1[:, c, :], in0=P[:, 0:W], scalar1=1e-12, scalar2=None, op0=ALU.add), [P], [T1])
            V("act", lambda e: e.activation(out=T1[:, :, :], in_=T1[:, :, :], func=AF.Sqrt), [T1], [T1])
            V("dve", lambda e: e.reciprocal(out=T1[:, :, :], in_=T1[:, :, :]), [T1], [T1])
            V("dve", lambda e: e.tensor_tensor(out=KK[:, :, :], in0=KK[:, :, :], in1=T1[:, :, :], op=ALU.mult), [KK, T1], [KK])
            for c in range(4):
                V("dve", lambda e: e.tensor_scalar(out=T2[:, c, :], in0=A[:, c, :], scalar1=pv["ka"][:, c:c + 1],
                                                   scalar2=omka[:, c:c + 1], op0=ALU.mult, op1=ALU.add), [A, pv["ka"], omka], [T2])
            V("pool", lambda e: e.tensor_tensor(out=KM[:, :, :], in0=K_, in1=T2[:, :, :], op=ALU.mult), [Z, T2], [KM])
            V("pool", lambda e: e.tensor_tensor(out=Bv[:, :, :], in0=KK[:, :, :], in1=A[:, :, :], op=ALU.mult), [KK, A], [Bv])
            V("dve", lambda e: e.tensor_tensor(out=T1[:, :, :], in0=R_, in1=KM[:, :, :], op=ALU.mult), [Z, KM], [T1])
            for c in range(4):
                V("act", lambda e: e.activation(out=SQ[:, c, :], in_=T1[:, c, :], func=AF.Identity, scale=pv["rk"][:, c:c + 1]),
                  [T1, pv["rk"]], [SQ])
            Bo = Bop.next()
            for c in range(4):
                P = self.psum.next()
                kb.op("pe", lambda e: e.matmul(P[:, 0:W], lhsT=bd[:, :], rhs=SQ[:, c, :], start=True, stop=True), r=[bd, SQ], w=[P])
                V("dve", lambda e: e.tensor_tensor(out=Bo[:, c, :], in0=P[:, 0:W], in1=Z[:, 8 + c, :], op=ALU.mult), [P, Z], [Bo])
            kb.dma("sp", fmv(S["bon"])[:, :, t0:t0 + W], Bo[:, :, :], Bo, r=[Bo], w=[("dram", id(S["bon"]))])
            V("act", lambda e: e.activation(out=Vb[:, :, :], in_=V_, func=AF.Copy), [Z], [Vb])
            transpose_store(Vb, S["vt"], t0)
            for d_ in range(2):
                rev = d_ == 1
                fl = (lambda ap: ap.rearrange("p a b -> p (a b)")[:, ::-1]) if rev else (lambda ap: ap.rearrange("p a b -> p (a b)"))
                V("dve", lambda e: e.tensor_tensor_scan(out=fl(Lc[:, :, :]), data0=fl(ON[d_][:, :, :]), data1=fl(LW[d_][:, :, :]),
                                                        initial=0.0, op0=ALU.mult, op1=ALU.add), [ON[d_], LW[d_]], [Lc])
                mid, last = (64, 0) if rev else (63, 127)
                L4 = Lc[:, :, :].rearrange("p c (b t) -> p c b t", b=NB)
                sc = scp.next()
                V("dve", lambda e: e.tensor_copy(out=lmn[:, :, :], in_=L4[:, :, :, mid]), [Lc], [lmn])
                scv = sc[:, :, :, :].rearrange("p b c s -> p c b s")
                V("act", lambda e: e.activation(out=scv[:, :, :, 0], in_=lmn[:, :, :], func=AF.Exp), [lmn], [sc])
                V("act", lambda e: e.activation(out=scv[:, :, :, 2], in_=L4[:, :, :, last], func=AF.Exp), [Lc], [sc])
                V("dve", lambda e: e.tensor_tensor(out=scv[:, :, :, 1], in0=L4[:, :, :, last], in1=lmn[:, :, :], op=ALU.subtract),
                  [Lc, lmn], [sc])
                V("act", lambda e: e.activation(out=scv[:, :, :, 1], in_=scv[:, :, :, 1], func=AF.Exp), [sc], [sc])
                kb.dma("sp", S["sc"][d_][:, t0 // 128:t0 // 128 + NB, :, :], sc[:, :, :, :], sc, r=[sc], w=[("dram", id(S["sc"][d_]))])
                Lr4 = Lr[:, :, :].rearrange("p c (b t) -> p c b t", b=NB)
                V("pool", lambda e: e.tensor_tensor(out=Lr4, in0=L4, in1=lmn[:, :, :].unsqueeze(3).broadcast_to([128, 4, NB, 128]),
                                                    op=ALU.subtract), [Lc, lmn], [Lr])
                V("pool", lambda e: e.tensor_tensor(out=Lq[:, :, :], in0=Lr[:, :, :], in1=LW[d_][:, :, :], op=ALU.subtract),
                  [Lr, LW[d_]], [Lq])
                o = {n: outp[n].next() for n in outp}
                V("act", lambda e: e.activation(out=E[:, :, :], in_=Lr[:, :, :], func=AF.Exp), [Lr], [E])
                V("dve", lambda e: e.tensor_tensor(out=o["RHO"][:, :, :], in0=R_, in1=E[:, :, :], op=ALU.mult), [Z, E], [o["RHO"]])
                V("act", lambda e: e.activation(out=E[:, :, :], in_=Lq[:, :, :], func=AF.Exp), [Lq], [E])
                V("dve", lambda e: e.tensor_tensor(out=o["KAP"][:, :, :], in0=KK[:, :, :], in1=E[:, :, :], op=ALU.mult), [KK, E], [o["KAP"]])
                V("act", lambda e: e.activation(out=E[:, :, :], in_=Lr[:, :, :], func=AF.Exp, scale=-1.0), [Lr], [E])
                V("dve", lambda e: e.tensor_tensor(out=o["BET"][:, :, :], in0=Bv[:, :, :], in1=E[:, :, :], op=ALU.mult), [Bv, E], [o["BET"]])
                V("pool", lambda e: e.tensor_tensor(out=o["KTI"][:, :, :], in0=KM[:, :, :], in1=E[:, :, :], op=ALU.mult), [KM, E], [o["KTI"]])
                for n in ("RHO", "KAP", "BET", "KTI"):
                    kb.dma("sp", fmv(S[n][d_])[:, :, t0:t0 + W], o[n][:, :, :], o[n], r=[o[n]], w=[("dram", id(S[n][d_]))])
                transpose_store(o["BET"], S["bt"][d_], t0)
                transpose_store(o["KTI"], S["kt"][d_], t0)


def _ln_stats_w(self, x, Wc, pl):
    kb = self.kb
    xb, sq = pl["xb"].next(), pl["sq"].next()
    kb.op("act", lambda e: e.activation(out=xb[:, :, :Wc], in_=x[:, :, :Wc], func=AF.Copy), r=[x], w=[xb])
    kb.op("act", lambda e: e.activation(out=sq[:, :, :Wc], in_=x[:, :, :Wc], func=AF.Square), r=[x], w=[sq])
    P1, P2 = self.psum.next(), self.psum.next()
    for c in range(NCH):
        kb.op("pe", lambda e: e.matmul(P1[:, 0:Wc], lhsT=self.onesb[:, :], rhs=xb[:, c, :Wc],
                                       start=(c == 0), stop=(c == NCH - 1)), r=[xb, self.onesb], w=[P1])
    for c in range(NCH):
        kb.op("pe", lambda e: e.matmul(P2[:, 0:Wc], lhsT=self.onesb[:, :], rhs=sq[:, c, :Wc],
                                       start=(c == 0), stop=(c == NCH - 1)), r=[sq, self.onesb], w=[P2])
    m2, var, rstd, nmr = pl["m2"].next(), pl["var"].next(), pl["rstd"].next(), pl["nmr"].next()
    kb.op("act", lambda e: e.activation(out=m2[:, 0, :Wc], in_=P1[:, 0:Wc], func=AF.Square), r=[P1], w=[m2])
    kb.op("dve", lambda e: e.scalar_tensor_tensor(out=var[:, 0, :Wc], in0=P2[:, 0:Wc], scalar=LN_EPS,
                                                  in1=m2[:, 0, :Wc], op0=ALU.add, op1=ALU.subtract), r=[P2, m2], w=[var])
    kb.op("act", lambda e: e.activation(out=var[:, 0, :Wc], in_=var[:, 0, :Wc], func=AF.Sqrt), r=[var], w=[var])
    kb.op("dve", lambda e: e.reciprocal(out=rstd[:, 0, :Wc], in_=var[:, 0, :Wc]), r=[var], w=[rstd])
    kb.op("dve", lambda e: e.scalar_tensor_tensor(out=nmr[:, 0, :Wc], in0=P1[:, 0:Wc], scalar=-1.0,
                                                  in1=rstd[:, 0, :Wc], op0=ALU.mult, op1=ALU.mult), r=[P1, rstd], w=[nmr])
    return rstd, nmr


Model.rwkv_declare = _rwkv_declare
Model.rwkv_prep_phase = _rwkv_prep_phase
Model.ln_stats_w = _ln_stats_w


def _rwkv_scan_dir(self, l, d, S, st, psr):
    kb = self.kb
    NT = self.NT
    nchunk = NT // 128
    fmv = lambda t_: t_.rearrange("(c p) t -> p c t", p=128)
    V = lambda e_, f, r, w: kb.op(e_, f, r=r, w=w)
    sbf = lambda n, shp, dt=F32: kb.sb(n, shp, dt, st)
    MT, MN = sbf("MT", [128, 512], BF16), sbf("MN", [128, 512], BF16)
    kb.dma("pool", MT[:, :], self.rk_MT[d], MT, w=[MT])
    kb.dma("pool", MN[:, :], self.rk_MN[d], MN, w=[MN])
    KRp = Pool(kb, "KR", 2, [128, 4, 2, 128], BF16, stack=st)
    BTZp = Pool(kb, "BTZ", 2, [128, 4, 2, 128], BF16, stack=st)
    KTZp = Pool(kb, "KTZ", 2, [128, 4, 2, 128], BF16, stack=st)
    KAZp = Pool(kb, "KAZ", 2, [128, 4, 2, 128], BF16, stack=st)
    for p_ in (BTZp, KTZp, KAZp):
        for b_ in p_.bufs:
            V("pool", lambda e: e.memset(b_[:, :, :, :], 0.0), [], [b_])
    S0Z = sbf("S0Z", [128, 4, 2, 64], BF16)
    V("pool", lambda e: e.memset(S0Z[:, :, :, :], 0.0), [], [S0Z])
    St = sbf("St", [128, 4, 64])
    V("pool", lambda e: e.memset(St[:, :, :], 0.0), [], [St])
    St1 = sbf("St1", [128, 4, 64])
    tokp = {n: Pool(kb, n, 2, [128, 512], BF16, stack=st) for n in ("Btok", "Ktok", "Vtok")}
    scp = Pool(kb, "sc", 2, [128, 4, 3], F32, stack=st)
    AMp = Pool(kb, "AM", 2, [128, 8, 512], BF16, stack=st)
    Pm = [Pool(kb, "Pm%d" % i, 2, [128, 8, 128], BF16, stack=st) for i in range(2)]
    PTm = Pool(kb, "PTm", 2, [128, 8, 128], BF16, stack=st)
    X32, X16p = sbf("X32", [128, 512]), Pool(kb, "X16", 2, [128, 512], BF16, stack=st)
    Oop = Pool(kb, "Oo", 2, [128, 512], F32, stack=st)
    order = [0, 1] + list(range(2, nchunk))
    if d == 1:
        order = [1, 0] + list(range(nchunk - 1, 1, -1))
    ev = 0
    yield
    for ci_, ch in enumerate(order):
        if ci_ > 0:
            yield
        t0 = ch * 128
        KR, BTZ, KTZ, KAZ = KRp.next(), BTZp.next(), KTZp.next(), KAZp.next()
        kb.dma("sp", KR[:, :, 0, :], fmv(S["KAP"][d])[:, :, t0:t0 + 128], KR, r=[("dram", id(S["KAP"][d]))], w=[KR])
        kb.dma("sp", KR[:, :, 1, :], fmv(S["RHO"][d])[:, :, t0:t0 + 128], KR, r=[("dram", id(S["RHO"][d]))], w=[KR])
        for par in range(2):
            ps_ = slice(par * 64, (par + 1) * 64)
            kb.dma("sp", BTZ[ps_, :, par, :], fmv(S["BET"][d])[ps_, :, t0:t0 + 128], BTZ, r=[("dram", id(S["BET"][d]))], w=[BTZ])
            kb.dma("sp", KTZ[ps_, :, par, :], fmv(S["KTI"][d])[ps_, :, t0:t0 + 128], KTZ, r=[("dram", id(S["KTI"][d]))], w=[KTZ])
            kb.dma("sp", KAZ[ps_, :, par, :], fmv(S["KAP"][d])[ps_, :, t0:t0 + 128], KAZ, r=[("dram", id(S["KAP"][d]))], w=[KAZ])
        tk = {}
        for n, key in (("Btok", "bt"), ("Ktok", "kt"), ("Vtok", "vt")):
            tk[n] = tokp[n].next()
            srcd = S[key][d] if key != "vt" else S[key]
            kb.dma("sp", tk[n][:, :], srcd[t0:t0 + 128, :], tk[n], r=[("dram", id(srcd))], w=[tk[n]])
        Btok, Ktok, Vtok = tk["Btok"], tk["Ktok"], tk["Vtok"]
        sc = scp.next()
        kb.dma("sp", sc[:, :, :], S["sc"][d][:, ch, :, :], sc, r=[("dram", id(S["sc"][d]))], w=[sc])
        for par in range(2):
            ps_ = slice(par * 64, (par + 1) * 64)
            V("pool", lambda e: e.tensor_tensor(out=S0Z[ps_, :, par, :], in0=St[ps_, :, :],
                                                in1=sc[ps_, :, 0:1].broadcast_to([64, 4, 64]), op=ALU.mult), [St, sc], [S0Z])
        AM = AMp.next()
        for h in range(8):
            c, par = h // 2, h % 2
            P = psr.next()
            rhs = KR[:, c, :, :].rearrange("p s t -> p (s t)")
            kb.op("pe", lambda e: e.matmul(P[:, 0:256], lhsT=BTZ[:, c, par, :], rhs=rhs, start=True, stop=True), r=[BTZ, KR], w=[P])
            kb.op("pe", lambda e: e.matmul(P[:, 256:512], lhsT=KTZ[:, c, par, :], rhs=rhs, start=True, stop=True), r=[KTZ, KR], w=[P])
            V("dve", lambda e: e.tensor_tensor(out=AM[:, h, :], in0=P[:, :], in1=MT[:, :], op=ALU.mult), [P, MT], [AM])
        yield
        Pj, PTj = Pm[0].next(), PTm.next()
        for g4 in range(2):
            P = psr.next()
            for hh in range(4):
                h = g4 * 4 + hh
                c, par = h // 2, h % 2
                kb.op("pe", lambda e: e.matmul(P[:, hh * 128:(hh + 1) * 128], lhsT=KAZ[:, c, par, :], rhs=BTZ[:, c, par, :],
                                               start=True, stop=True), r=[KAZ, BTZ], w=[P])
            V("dve", lambda e: e.tensor_tensor(out=Pj[:, g4 * 4:(g4 + 1) * 4, :].rearrange("p h t -> p (h t)"), in0=P[:, :],
                                               in1=MN[:, :], op=ALU.mult), [P, MN], [Pj])
        V("act", lambda e: e.activation(out=PTj[:, :, :], in_=AM[:, :, 0:128], func=AF.Copy), [AM], [PTj])
        P = psr.next()
        for h in range(8):
            c, par = h // 2, h % 2
            kb.op("pe", lambda e: e.matmul(P[:, h * 64:(h + 1) * 64], lhsT=KR[:, c, 0, :], rhs=S0Z[:, c, par, :],
                                           start=True, stop=False), r=[KR, S0Z], w=[P])
            kb.op("pe", lambda e: e.matmul(P[:, h * 64:(h + 1) * 64], lhsT=AM[:, h, 256:384], rhs=Vtok[:, h * 64:(h + 1) * 64],
                                           start=False, stop=True), r=[AM, Vtok], w=[P])
        V("act", lambda e: e.activation(out=X32[:, :], in_=P[:, :], func=AF.Identity, scale=-1.0), [P], [X32])
        X16 = X16p.next()
        V("dve", lambda e: e.tensor_copy(out=X16[:, :], in_=X32[:, :]), [X32], [X16])
        yield
        for j in range(7):
            P = psr.next()
            for h in range(8):
                kb.op("pe", lambda e: e.matmul(P[:, h * 64:(h + 1) * 64], lhsT=PTj[:, h, :], rhs=X16[:, h * 64:(h + 1) * 64],
                                               start=True, stop=True), r=[PTj, X16], w=[P])
            V("dve", lambda e: e.tensor_tensor(out=X32[:, :], in0=P[:, :], in1=X32[:, :], op=ALU.add), [P, X32], [X32])
            X16 = X16p.next()
            V("act", lambda e: e.activation(out=X16[:, :], in_=X32[:, :], func=AF.Copy), [X32], [X16])
            if j < 6:
                PTn = PTm.next()
                Pn = Pm[(j + 1) % 2].next() if j < 5 else None
                for g4 in range(2):
                    P = psr.next()
                    for hh in range(4):
                        h = g4 * 4 + hh
                        kb.op("pe", lambda e: e.matmul(P[:, hh * 128:(hh + 1) * 128], lhsT=Pj[:, h, :], rhs=PTj[:, h, :],
                                                       start=True, stop=True), r=[Pj, PTj], w=[P])
                    eng = "act" if (ev % 2 == 0) else "dve"
                    ev += 1
                    dst = PTn[:, g4 * 4:(g4 + 1) * 4, :].rearrange("p h t -> p (h t)")
                    if eng == "act":
                        V("act", lambda e: e.activation(out=dst, in_=P[:, :], func=AF.Copy), [P], [PTn])
                    else:
                        V("dve", lambda e: e.tensor_copy(out=dst, in_=P[:, :]), [P], [PTn])
                    if Pn is not None:
                        P = psr.next()
                        for hh in range(4):
                            h = g4 * 4 + hh
                            kb.op("pe", lambda e: e.matmul(P[:, hh * 128:(hh + 1) * 128], lhsT=PTj[:, h, :], rhs=Pj[:, h, :],
                                                           start=True, stop=True), r=[Pj, PTj], w=[P])
                        eng = "act" if (ev % 2 == 0) else "dve"
                        ev += 1
                        dst = Pn[:, g4 * 4:(g4 + 1) * 4, :].rearrange("p h t -> p (h t)")
                        if eng == "act":
                            V("act", lambda e: e.activation(out=dst, in_=P[:, :], func=AF.Copy), [P], [Pn])
                        else:
                            V("dve", lambda e: e.tensor_copy(out=dst, in_=P[:, :]), [P], [Pn])
                PTj = PTn
                if Pn is not None:
                    Pj = Pn
            if j in (1, 3, 5):
                yield
        U16 = X16
        P = psr.next()
        for h in range(8):
            c, par = h // 2, h % 2
            hs = slice(h * 64, (h + 1) * 64)
            kb.op("pe", lambda e: e.matmul(P[:, hs], lhsT=KR[:, c, 1, :], rhs=S0Z[:, c, par, :], start=True, stop=False),
                  r=[KR, S0Z], w=[P])
            kb.op("pe", lambda e: e.matmul(P[:, hs], lhsT=AM[:, h, 128:256], rhs=U16[:, hs], start=False, stop=False),
                  r=[AM, U16], w=[P])
            kb.op("pe", lambda e: e.matmul(P[:, hs], lhsT=AM[:, h, 384:512], rhs=Vtok[:, hs], start=False, stop=True),
                  r=[AM, Vtok], w=[P])
        Oo = Oop.next()
        V("act", lambda e: e.activation(out=Oo[:, :], in_=P[:, :], func=AF.Copy), [P], [Oo])
        kb.dma("act", S["O"][d][t0:t0 + 128, :], Oo[:, :], Oo, r=[Oo], w=[("dram", id(S["O"][d]))])
        yield
        P = psr.next()
        for c in range(4):
            cs_ = slice(c * 128, (c + 1) * 128)
            kb.op("pe", lambda e: e.matmul(P[:, cs_], lhsT=Btok[:, cs_], rhs=U16[:, cs_], start=True, stop=False),
                  r=[Btok, U16], w=[P])
            kb.op("pe", lambda e: e.matmul(P[:, cs_], lhsT=Ktok[:, cs_], rhs=Vtok[:, cs_], start=False, stop=True),
                  r=[Ktok, Vtok], w=[P])
        V("pool", lambda e: e.tensor_tensor(out=St1[:, :, :], in0=St[:, :, :], in1=sc[:, :, 2:3].broadcast_to([128, 4, 64]),
                                            op=ALU.mult), [St, sc], [St1])
        pv_ = P[:, :].rearrange("p (c x) -> p c x", c=4)
        for par in range(2):
            ps_ = slice(par * 64, (par + 1) * 64)
            V("dve", lambda e: e.tensor_tensor(out=St[ps_, :, :], in0=pv_[ps_, :, par * 64:(par + 1) * 64],
                                               in1=sc[ps_, :, 1:2].broadcast_to([64, 4, 64]), op=ALU.mult), [P, sc, St1], [St])
        V("pool", lambda e: e.tensor_tensor(out=St[:, :, :], in0=St[:, :, :], in1=St1[:, :, :], op=ALU.add), [St, St1], [St])


def _merge_declare(self):
    L = DEPTH
    self.branch_proj = self.din("branch_proj", [L, 3, 512, D])
    self.w_out = self.din("w_out", [L, D, D])


def _merge_phase(self, l, src, dst, S, ctx):
    kb = self.kb
    W = 256
    NB = W // 128
    lni = (l * 3 + 1) * NCH
    with kb.phase() as st:
        self.psum = Pool(kb, "ps", 8, [128, 512], F32, psum=True, stack=st)
        V = lambda e_, f, r, w: kb.op(e_, f, r=r, w=w)
        sbf = lambda n, shp, dt=F32: kb.sb(n, shp, dt, st)
        bp = sbf("bp", [128, 12, D], BF16)
        wo = sbf("wo", [128, NCH, D], BF16)
        for b_ in range(3):
            for c in range(4):
                kb.dma("pool", bp[:, b_ * 4 + c, :], self.branch_proj[l, b_, c * 128:(c + 1) * 128, :], bp, w=[bp])
        for c in range(NCH):
            kb.dma("pool", wo[:, c, :], self.w_out[l, c * 128:(c + 1) * 128, :], wo, w=[wo])
        idb = sbf("idb", [128, 128], BF16)
        kb.dma("pool", idb[:, :], self.identb[:, :], idb, w=[idb])
        gng, gnb = sbf("gng", [128, 4]), sbf("gnb", [128, 4])
        kb.dma("sp", gng[:, :], self.rk_gng[:, l * 4:(l + 1) * 4], gng, w=[gng])
        kb.dma("sp", gnb[:, :], self.rk_gnb[:, l * 4:(l + 1) * 4], gnb, w=[gnb])
        pl = self.stat_pools(W, st)
        xp = Pool(kb, "x", 2, [128, NCH, W], F32, stack=st)
        xn = sbf("xn", [128, NCH, W])
        Ofp = Pool(kb, "Of", 2, [128, NB, 512], F32, stack=st)
        Obp = Pool(kb, "Ob", 2, [128, NB, 512], F32, stack=st)
        onb = sbf("onb", [128, NB, 512], BF16)
        st8 = [sbf("st8_%d" % i, [128, NB, 8]) for i in range(3)]
        sqt = sbf("sqt", [128, NB, 512])
        Y = {n: Pool(kb, "y" + n, 2, [128, 4, W], BF16, stack=st) for n in ("a", "s", "bon", "g")}
        yr = sbf("yr", [128, 4, W], BF16)
        yt = sbf("yrt", [128, 4, W])
        gp = Pool(kb, "gates", 2, [128, 24, W], BF16, stack=st)
        m1, m2, m3 = sbf("m1", [128, W]), sbf("m2", [128, W]), sbf("m3", [128, W])
        mT = sbf("mT", [128, NCH, W], BF16)
        srcv = src.rearrange("(c p) t -> p c t", p=128)
        dstv = dst.rearrange("(c p) t -> p c t", p=128)
        fmv = lambda t_: t_.rearrange("(c p) t -> p c t", p=128)
        for (seg, t0, _) in self.tiles(W, ctx=ctx):
            x = xp.next()
            kb.dma("sp", x[:, :, :], srcv[:, :, t0:t0 + W], x, r=[("dram", id(src))], w=[x])
            Of, Ob = Ofp.next(), Obp.next()
            kb.dma("sp", Of[:, :, :], S["O"][0][t0:t0 + W, :].rearrange("(b p) n -> p b n", p=128), Of, r=[("dram", id(S["O"][0]))], w=[Of])
            kb.dma("sp", Ob[:, :, :], S["O"][1][t0:t0 + W, :].rearrange("(b p) n -> p b n", p=128), Ob, r=[("dram", id(S["O"][1]))], w=[Ob])
            ld = {}
            for n, key in (("a", "ya"), ("s", "ys"), ("bon", "bon"), ("g", "gR")):
                ld[n] = Y[n].next()
                kb.dma("sp", ld[n][:, :, :], fmv(S[key])[:, :, t0:t0 + W], ld[n], r=[("dram", id(S[key]))], w=[ld[n]])
            gt = gp.next()
            kb.dma("sp", gt[:, :, :], fmv(S["gS"])[:, :, t0:t0 + W], gt, r=[("dram", id(S["gS"]))], w=[gt])
            V("dve", lambda e: e.tensor_tensor(out=Of[:, :, :], in0=Of[:, :, :], in1=Ob[:, :, :], op=ALU.add), [Of, Ob], [Of])
            O4 = Of[:, :, :].rearrange("p b (h v) -> p b h v", h=8)
            sm, vr, rs = st8
            V("dve", lambda e: e.tensor_reduce(out=sm[:, :, :], in_=O4, axis=AX.X, op=ALU.add), [Of], [sm])
            V("dve", lambda e: e.tensor_scalar(out=sm[:, :, :], in0=sm[:, :, :], scalar1=1.0 / 64, scalar2=None, op0=ALU.mult), [sm], [sm])
            V("dve", lambda e: e.tensor_tensor(out=O4, in0=O4, in1=sm[:, :, :].unsqueeze(3).broadcast_to([128, NB, 8, 64]),
                                               op=ALU.subtract), [Of, sm], [Of])
            V("act", lambda e: e.activation(out=sqt[:, :, :], in_=Of[:, :, :], func=AF.Square), [Of], [sqt])
            V("dve", lambda e: e.tensor_reduce(out=vr[:, :, :], in_=sqt[:, :, :].rearrange("p b (h v) -> p b h v", h=8), axis=AX.X,
                                               op=ALU.add), [sqt], [vr])
            V("dve", lambda e: e.tensor_scalar(out=vr[:, :, :], in0=vr[:, :, :], scalar1=1.0 / 64, scalar2=GN_EPS, op0=ALU.mult,
                                               op1=ALU.add), [vr], [vr])
            V("act", lambda e: e.activation(out=vr[:, :, :], in_=vr[:, :, :], func=AF.Sqrt), [vr], [vr])
            V("dve", lambda e: e.reciprocal(out=rs[:, :, :], in_=vr[:, :, :]), [vr], [rs])
            V("dve", lambda e: e.tensor_tensor(out=onb[:, :, :].rearrange("p b (h v) -> p b h v", h=8), in0=O4,
                                               in1=rs[:, :, :].unsqueeze(3).broadcast_to([128, NB, 8, 64]), op=ALU.mult), [Of, rs], [onb])
            for tb in range(NB):
                P = self.psum.next()
                pb = P[:, 0:256].bitcast(BF16)
                for c in range(4):
                    kb.op("pe", lambda e: e.transpose(pb[:, c * 128:(c + 1) * 128], onb[:, tb, c * 128:(c + 1) * 128], idb[:, :]),
                          r=[onb, idb], w=[P])
                for c in range(4):
                    V("act", lambda e: e.activation(out=yt[:, c, tb * 128:(tb + 1) * 128], in_=pb[:, c * 128:(c + 1) * 128],
                                                    func=AF.Identity, scale=gng[:, c:c + 1], bias=gnb[:, c:c + 1]), [P, gng, gnb], [yt])
            V("pool", lambda e: e.tensor_tensor(out=yt[:, :, :], in0=yt[:, :, :], in1=ld["bon"][:, :, :], op=ALU.add), [yt, ld["bon"]], [yt])
            V("dve", lambda e: e.tensor_tensor(out=yr[:, :, :], in0=yt[:, :, :], in1=ld["g"][:, :, :], op=ALU.mult), [yt, ld["g"]], [yr])
            if "yrS" in S:
                kb.dma("sp", fmv(S["yrS"])[:, :, t0:t0 + W], yr[:, :, :], yr, r=[yr], w=[("dram", id(S["yrS"]))])
            ysrc = [ld["a"], yr, ld["s"]]
            for oc in range(NCH):
                Pa, Pb = self.psum.next(), self.psum.next()
                tgt = [(Pa, 0), (Pa, W), (Pb, 0)]
                for b_ in range(3):
                    Pt, o0 = tgt[b_]
                    for kc in range(4):
                        kb.op("pe", lambda e: e.matmul(Pt[:, o0:o0 + W], lhsT=bp[:, b_ * 4 + kc, oc * 128:(oc + 1) * 128],
                                                       rhs=ysrc[b_][:, kc, :], start=(kc == 0), stop=(kc == 3)), r=[bp, ysrc[b_]], w=[Pt])
                V("dve", lambda e: e.tensor_tensor(out=m1[:, :], in0=Pa[:, 0:W], in1=gt[:, oc, :], op=ALU.mult), [Pa, gt], [m1])
                V("dve", lambda e: e.tensor_tensor(out=m2[:, :], in0=Pa[:, W:2 * W], in1=gt[:, 8 + oc, :], op=ALU.mult), [Pa, gt], [m2])
                V("dve", lambda e: e.tensor_tensor(out=m3[:, :], in0=Pb[:, 0:W], in1=gt[:, 16 + oc, :], op=ALU.mult), [Pb, gt], [m3])
                V("pool", lambda e: e.tensor_tensor(out=m1[:, :], in0=m1[:, :], in1=m2[:, :], op=ALU.add), [m1, m2], [m1])
                V("pool", lambda e: e.tensor_tensor(out=mT[:, oc, :], in0=m1[:, :], in1=m3[:, :], op=ALU.add), [m1, m3], [mT])
            V("pool", lambda e: e.tensor_scalar(out=x[:, :, :], in0=x[:, :, :], scalar1=ALPHA, scalar2=None, op0=ALU.mult), [x], [x])
            for oc in range(NCH):
                if oc % 2 == 0:
                    P = self.psum.next()
                o0 = (oc % 2) * W
                for kc in range(NCH):
                    kb.op("pe", lambda e: e.matmul(P[:, o0:o0 + W], lhsT=wo[:, kc, oc * 128:(oc + 1) * 128], rhs=mT[:, kc, :],
                                                   start=(kc == 0), stop=(kc == NCH - 1)), r=[wo, mT], w=[P])
                V("dve", lambda e: e.scalar_tensor_tensor(out=x[:, oc, :], in0=P[:, o0:o0 + W],
                                                          scalar=self.mods[:, 5 * NCH + oc, seg:seg + 1], in1=x[:, oc, :],
                                                          op0=ALU.mult, op1=ALU.add), [P, x, self.mods], [x])
            P = self.psum.next()
            rstd, nmr = self.ln_stats(x, W, P, pl)
            self.normalize(xn, x, W, rstd, nmr)
            for c in range(NCH):
                V("act", lambda e: e.activation(out=xn[:, c, :], in_=xn[:, c, :], func=AF.Identity,
                                                scale=self.lng[:, lni + c:lni + c + 1], bias=self.lnb[:, lni + c:lni + c + 1]),
                  [xn, self.lng, self.lnb], [xn])
            kb.dma("sp", dstv[:, :, t0:t0 + W], xn[:, :, :], xn, r=[xn], w=[("dram", id(dst))])


Model.rwkv_scan_dir = _rwkv_scan_dir
Model.merge_declare = _merge_declare
Model.merge_phase = _merge_phase


def host_inputs_rwkv(inp):
    L = DEPTH
    d = {}
    w = inp["w_in"]
    rw = w[:, :, 768:2624]
    r, k, v = rw[:, :, 0:512], rw[:, :, 512:1024], rw[:, :, 1024:1536]
    wlo, alo, glo = rw[:, :, 1536:1664], rw[:, :, 1664:1728], rw[:, :, 1728:1856]
    pad = np.zeros_like(alo)
    d["w_inB"] = np.ascontiguousarray(np.concatenate([r, k, v, wlo, glo, alo, pad], -1))
    mu = inp["rwkv_mu"]
    mu_r = np.concatenate([mu[:, 0:1536], mu[:, 1536:1664], mu[:, 1728:1856], mu[:, 1664:1728], np.zeros((L, 64), np.float32)], -1)
    d["rk_mu"] = np.ascontiguousarray(mu_r.reshape(L * NCH_B, 128).T)
    d["rk_w0"] = np.ascontiguousarray(inp["rwkv_w0"].reshape(L * 8, 128).T)
    for n, src in (("a0", "rwkv_a0"), ("kk", "rwkv_k_k"), ("ka", "rwkv_k_a"), ("rk", "rwkv_r_k"), ("gng", "rwkv_gn_g"), ("gnb", "rwkv_gn_b")):
        d["rk_" + n] = np.ascontiguousarray(inp[src].reshape(L * 4, 128).T)
    d["rk_w2"] = np.ascontiguousarray(inp["rwkv_w2"].reshape(L, 128, 512))
    d["rk_a2"] = inp["rwkv_a2"]
    d["rk_g2"] = inp["rwkv_g2"]
    i = np.arange(128)
    d["bd64"] = np.ascontiguousarray(((i[:, None] // 64) == (i[None, :] // 64)).astype(np.float32))
    d["identb"] = np.eye(128, dtype=np.float32)
    on = np.ones((2, 128, 128), np.float32)
    on[0, :, 0] = 0.0
    on[1, :, 127] = 0.0
    d["ones0"] = on
    MT = np.zeros((2, 128, 512), np.float32)
    MN = np.zeros((2, 128, 512), np.float32)
    ii, tt = i[:, None], i[None, :]
    for dd in range(2):
        prev = (ii < tt) if dd == 0 else (ii > tt)
        incl = prev | (ii == tt)
        MT[dd, :, 0:128] = -(prev.astype(np.float32))
        MT[dd, :, 128:256] = incl
        MT[dd, :, 256:384] = prev
        MT[dd, :, 384:512] = incl
        MN[dd] = np.tile(-(prev.T.astype(np.float32)), (1, 4))
    d["rk_MT"], d["rk_MN"] = MT, MN
    d["branch_proj"] = inp["branch_proj"]
    d["w_out"] = inp["w_out"]
    return d


def _make_scratch(self):
    NT = self.NT
    S = {}
    for n, shp, dt in (("qS", [512, NT], BF16), ("kS", [256, NT], BF16), ("vS", [NT, 128], BF16), ("usS", [512, NT], F32),
                       ("gS", [3072, NT], BF16), ("ya", [512, NT], BF16), ("ysb", [512, NT], F32), ("ys", [512, NT], BF16),
                       ("gR", [512, NT], BF16), ("bon", [512, NT], BF16), ("vt", [NT, 512], BF16)):
        S[n] = self.scratch(n, shp, dt)
    for n in ("RHO", "KAP", "BET", "KTI"):
        S[n] = [self.scratch("%s%d" % (n, d), [512, NT], BF16) for d in range(2)]
    for n in ("bt", "kt"):
        S[n] = [self.scratch("%s%d" % (n, d), [NT, 512], BF16) for d in range(2)]
    S["sc"] = [self.scratch("sc%d" % d, [128, NT // 128, 4, 3], F32) for d in range(2)]
    S["O"] = [self.scratch("O%d" % d, [NT, 512], F32) for d in range(2)]
    if "yrS" in self.dbg:
        S["yrS"] = self.scratch("yrS", [512, NT], BF16)
    return S


def _mixer(self, l, src, dst, S, ctx_out):
    self.mixA_phase(l, src, S["qS"], S["kS"], S["vS"], S["usS"], S["gS"])
    self.attn_phase(l, S["qS"], S["kS"], S["vS"], S["ya"])
    self.rwkv_prep_phase(l, src, S)
    self.scan_phase(l, S)
    self.merge_phase(l, src, dst, S, ctx=ctx_out)


def _scan_phase(self, l, S):
    kb = self.kb
    for (ds5, drk) in ((1, 0), (0, 1)):
        with kb.phase() as st:
            PS2 = Pool(kb, "ps2", 2, [128, 1024], F32, psum=True, stack=st)
            psr = Pool(kb, "psr", 4, [128, 512], F32, psum=True, stack=st)
            g1 = self.s5_dir(l, ds5, S["usS"], S["ysb"], S["ys"], st, PS2)
            next(g1)
            g2 = self.rwkv_scan_dir(l, drk, S, st, psr)
            next(g2)
            alive = [g1, g2]
            while alive:
                for g in list(alive):
                    try:
                        next(g)
                    except StopIteration:
                        alive.remove(g)


Model.scan_phase = _scan_phase
Model.make_scratch = _make_scratch
Model.mixer = _mixer


def build_model(T, dbg=()):
    m = Model(T, dbg=dbg)
    m.declare_inputs()
    m.mixA_declare()
    m.s5_declare()
    m.rwkv_declare()
    m.merge_declare()
    m.setup_consts()
    NT = m.NT
    S = m.make_scratch()
    streams = [m.scratch("str%d" % i, [D, NT], F32) for i in range(3)]
    outT = m.dout("outT", [D, T])
    cur = m.xT
    for l in range(DEPTH):
        last = l == DEPTH - 1
        m.adaln_phase(l)
        m.ffn_phase(l, 0, cur, streams[0], 0, ctx=True)
        m.mixer(l, streams[0], streams[1], S, ctx_out=not last)
        if last:
            m.ffn_phase(l, 1, streams[1], outT, CTX, ctx=False)
        else:
            m.ffn_phase(l, 1, streams[1], streams[2], 0, ctx=True)
            cur = streams[2]
    m.kb.finish()
    return m


def all_host_inputs(inp, b, T):
    d = host_inputs(inp, b, T)
    d.update(host_inputs_A(inp, T))
    d.update(host_inputs_s5(inp))
    d.update(host_inputs_rwkv(inp))
    return d


T_FULL = 8192
N_CORES = 8


def kernel(**inputs):
    inp = {k: np.asarray(v) for k, v in inputs.items()}
    m = build_model(T_FULL)
    B = inp["x"].shape[0]
    shared = None
    in_maps = []
    for core in range(N_CORES):
        b = core % B
        d = all_host_inputs(inp, b, T_FULL) if shared is None else dict(shared)
        if shared is None:
            shared = d
        else:
            d["xT"] = np.ascontiguousarray(np.concatenate([inp["ctx"][b], inp["x"][b, :T_FULL]], 0).T)
            cond = np.stack([inp["c"][b], inp["c_ctx"]], -1)
            d["condT"] = np.ascontiguousarray(cond.reshape(NCH, 128, 2).transpose(1, 0, 2))
        in_maps.append({k: v for k, v in d.items() if k in m.dram_in})
    res = run_bass_kernel_spmd(m.nc, in_maps, core_ids=list(range(N_CORES)))
    out = np.stack([np.ascontiguousarray(res.results[b]["outT"].T) for b in range(B)], 0)
    return out.astype(np.float32)
```

```python
import contextlib
import numpy as np
import concourse.bass as bass
import concourse.mybir as mybir
from concourse.bass_utils import run_bass_kernel_spmd

F32 = mybir.dt.float32
BF16 = mybir.dt.bfloat16
AF = mybir.ActivationFunctionType
ALU = mybir.AluOpType
AX = mybir.AxisListType

D = 1024
NCH = 8
CTX = 256
DFF = 2816
NFF = 22
DEPTH = 2
ALPHA = (2.0 * DEPTH) ** 0.25
LN_EPS = 1e-6
DECAY_SCALE = 0.606531
GN_EPS = 64e-5
N_IN = 6208


class Sem:
    _n = 0

    def __init__(self, h):
        self.h = h
        Sem._n += 1
        self.uid = Sem._n


class Buf:
    def __init__(self, name, t):
        self.name = name
        self.t = t
        self.dsem = None
        self.dcnt = 0

    def __getitem__(self, k):
        return self.t[k]

    def __repr__(self):
        return "Buf(%s)" % self.name


class KB:
    EPOCH = 20000

    def __init__(self, nc):
        self.nc = nc
        self.es = contextlib.ExitStack()
        self.eng = {"pe": nc.tensor, "dve": nc.vector, "act": nc.scalar, "pool": nc.gpsimd, "sp": nc.sync}
        self.esem = {}
        self.ecnt = {}
        for e in self.eng:
            self.esem[e] = Sem(self.es.enter_context(nc.semaphore("c_%s_0" % e)))
            self.ecnt[e] = 0
        self.eepoch = {e: 0 for e in self.eng}
        self.seen = {e: {} for e in self.eng}
        self.lastw = {}
        self.reads = {}
        self.nbuf = 0
        self.ninstr = 0
        self.nwait = 0
        self.all_events = {}
        self.free_dsems = []
        self.phase_bufs = []
        self.ndsem = 0

    def sb(self, name, shape, dtype, stack=None):
        self.nbuf += 1
        t = (stack or self.es).enter_context(self.nc.sbuf_tensor("%s_%d" % (name, self.nbuf), list(shape), dtype))
        b = Buf(name, t)
        if stack is not None:
            self.phase_bufs.append(b)
        return b

    def ps(self, name, shape, dtype=F32, stack=None):
        self.nbuf += 1
        t = (stack or self.es).enter_context(self.nc.psum_tensor("%s_%d" % (name, self.nbuf), list(shape), dtype))
        return Buf(name, t)

    def _dsem(self, b):
        if b.dsem is None:
            if self.free_dsems:
                b.dsem, b.dcnt = self.free_dsems.pop()
            else:
                self.ndsem += 1
                b.dsem = Sem(self.es.enter_context(self.nc.semaphore("d_%d" % self.ndsem)))
                b.dcnt = 0
        return b.dsem

    @contextlib.contextmanager
    def phase(self):
        st = contextlib.ExitStack()
        self.phase_bufs = []
        try:
            yield st
        finally:
            self.barrier()
            for b in self.phase_bufs:
                if b.dsem is not None:
                    self.free_dsems.append((b.dsem, b.dcnt))
                    b.dsem = None
            self.phase_bufs = []
            st.close()

    def _need(self, e, r, w):
        need = {}

        def add(evs):
            for uid, (s, v) in evs.items():
                if uid not in need or need[uid][1] < v:
                    need[uid] = (s, v)
        for k in r:
            add(self.lastw.get(k, {}))
        for k in w:
            add(self.lastw.get(k, {}))
            add(self.reads.get(k, {}))
        return need

    def _wait(self, e, need, own_ok):
        eng = self.eng[e]
        for uid, (s, v) in need.items():
            if own_ok and uid == self.esem[e].uid:
                continue
            if self.seen[e].get(uid, 0) >= v:
                continue
            eng.wait_ge(s.h, v)
            self.nwait += 1
            self.seen[e][uid] = v

    def _record(self, ev, r, w):
        uid = ev[0].uid
        for k in r:
            self.reads.setdefault(k, {})[uid] = ev
        for k in w:
            self.lastw.setdefault(k, {})[uid] = ev
            self.reads[k] = {}
        self.all_events[uid] = ev

    def _bump(self, e):
        if self.ecnt[e] >= self.EPOCH:
            self.eepoch[e] += 1
            self.esem[e] = Sem(self.es.enter_context(self.nc.semaphore("c_%s_%d" % (e, self.eepoch[e]))))
            self.ecnt[e] = 0
        self.ecnt[e] += 1
        return (self.esem[e], self.ecnt[e])

    def op(self, e, fn, r=(), w=(), same_ok=False):
        need = self._need(e, r, w)
        self._wait(e, need, own_ok=(e == "pe" or same_ok))
        ins = fn(self.eng[e])
        ev = self._bump(e)
        ins.then_inc(ev[0].h, 1)
        self._record(ev, r, w)
        self.ninstr += 1
        return ins

    def dma(self, q, out, in_, sbuf, r=(), w=(), **kw):
        need = self._need(q, r, w)
        self._wait(q, need, own_ok=False)
        s = self._dsem(sbuf)
        ins = self.eng[q].dma_start(out=out, in_=in_, **kw)
        sbuf.dcnt += 16
        ev = (s, sbuf.dcnt)
        ins.then_inc(s.h, 16)
        self._record(ev, r, w)
        self.ninstr += 1
        return ins

    def barrier(self):
        for e in self.eng:
            self._wait(e, dict(self.all_events), own_ok=True)
        self.lastw = {}
        self.reads = {}
        self.all_events = {}

    def finish(self, e="sp"):
        self._wait(e, dict(self.all_events), own_ok=True)


class Pool:
    def __init__(self, kb, name, n, shape, dtype, psum=False, stack=None):
        self.bufs = [(kb.ps if psum else kb.sb)("%s%d" % (name, i), shape, dtype, stack=stack) for i in range(n)]
        self.i = 0

    def next(self):
        b = self.bufs[self.i % len(self.bufs)]
        self.i += 1
        return b


class Model:
    def __init__(self, T, dbg=(), nlayers=DEPTH, stop_after=None):
        self.T = T
        self.NT = CTX + T
        self.dbg = set(dbg)
        self.nlayers = nlayers
        self.stop_after = stop_after
        nc = bass.Bass("TRN2", target_bir_lowering=False)
        self.nc = nc
        self.kb = KB(nc)
        self.dram_in = {}
        self.dram_out = {}
        self.scr_n = 0

    def din(self, name, shape, dtype=F32):
        t = self.nc.dram_tensor(name, list(shape), dtype, kind="ExternalInput").ap()
        self.dram_in[name] = t
        return t

    def dout(self, name, shape, dtype=F32):
        t = self.nc.dram_tensor(name, list(shape), dtype, kind="ExternalOutput").ap()
        self.dram_out[name] = t
        return t

    def scratch(self, name, shape, dtype=F32):
        if name in self.dbg:
            return self.dout(name, shape, dtype)
        return self.nc.dram_tensor(name, list(shape), dtype, kind="Internal").ap()

    def tiles(self, W, ctx=True, lat=True):
        out = []
        if ctx:
            for t0 in range(0, CTX, W):
                out.append((1, t0, W))
        if lat:
            for t0 in range(0, self.T, W):
                out.append((0, CTX + t0, W))
        return out

    def declare_inputs(self):
        L = DEPTH
        self.xT = self.din("xT", [D, self.NT])
        self.condT = self.din("condT", [128, NCH, 2])
        self.w_ada = self.din("w_ada", [L, D, 9 * D])
        self.b_ada = self.din("b_ada", [L, 128, 72])
        self.ln_g = self.din("ln_g", [128, L * 3 * NCH])
        self.ln_b = self.din("ln_b", [128, L * 3 * NCH])
        self.ffn_w_in = self.din("ffn_w_in", [L, 2, D, 2 * DFF])
        self.ffn_w_out = self.din("ffn_w_out", [L, 2, DFF, D])

    def setup_consts(self):
        kb = self.kb
        self.onesb = kb.sb("onesb", [128, 128], BF16)
        kb.op("dve", lambda e: e.memset(self.onesb[:], 1.0 / D), w=[self.onesb])
        self.lng = kb.sb("lng", [128, DEPTH * 3 * NCH], F32)
        self.lnb = kb.sb("lnb", [128, DEPTH * 3 * NCH], F32)
        kb.dma("sp", self.lng[:], self.ln_g[:, :], self.lng, w=[self.lng])
        kb.dma("sp", self.lnb[:], self.ln_b[:, :], self.lnb, w=[self.lnb])
        self.scond = kb.sb("scond", [128, NCH, 2], F32)
        kb.dma("sp", self.scond[:], self.condT[:, :, :], self.scond, w=[self.scond])
        kb.op("act", lambda e: e.activation(out=self.scond[:], in_=self.scond[:], func=AF.Silu),
              r=[self.scond], w=[self.scond])
        self.mods = kb.sb("mods", [128, 72, 2], F32)
        self.modp1 = kb.sb("modp1", [128, 72, 2], F32)
        self.modh = kb.sb("modh", [128, 72, 2], F32)

    def adaln_phase(self, l):
        kb = self.kb
        with kb.phase() as st:
            self.psum = Pool(kb, "ps", 8, [128, 512], F32, psum=True, stack=st)
            wa = Pool(kb, "wa", 2, [128, NCH, D], F32, stack=st)
            bada = kb.sb("bada", [128, 72], F32, st)
            kb.dma("sp", bada[:], self.b_ada[l, :, :], bada, w=[bada])
            P = self.psum.next()
            for m in range(9):
                w = wa.next()
                for kc in range(NCH):
                    kb.dma("sp" if kc % 2 == 0 else "act", w[:, kc, :],
                           self.w_ada[l, kc * 128:(kc + 1) * 128, m * D:(m + 1) * D], w, w=[w])
                for oc in range(NCH):
                    j = m * NCH + oc
                    for kc in range(NCH):
                        kb.op("pe", lambda e: e.matmul(P[:, 2 * j:2 * j + 2], lhsT=w[:, kc, oc * 128:(oc + 1) * 128],
                                                       rhs=self.scond[:, kc, :], start=(kc == 0), stop=(kc == NCH - 1)),
                              r=[w, self.scond], w=[P])
            kb.op("dve", lambda e: e.tensor_tensor(out=self.mods[:], in0=P[:, 0:144].rearrange("p (j s) -> p j s", s=2),
                                                   in1=bada[:, :].unsqueeze(2).broadcast_to([128, 72, 2]), op=ALU.add),
                  r=[P, bada], w=[self.mods])
            kb.op("dve", lambda e: e.tensor_scalar(out=self.modp1[:], in0=self.mods[:], scalar1=1.0, scalar2=None,
                                                   op0=ALU.add), r=[self.mods], w=[self.modp1])
            kb.op("dve", lambda e: e.tensor_scalar(out=self.modh[:], in0=self.mods[:], scalar1=0.5, scalar2=None,
                                                   op0=ALU.mult), r=[self.mods], w=[self.modh])

    def ln_stats(self, x, W, P, pl):
        kb = self.kb
        xb, sq = pl["xb"].next(), pl["sq"].next()
        kb.op("act", lambda e: e.activation(out=xb[:, :, :W], in_=x[:, :, :W], func=AF.Copy), r=[x], w=[xb])
        kb.op("act", lambda e: e.activation(out=sq[:, :, :W], in_=x[:, :, :W], func=AF.Square), r=[x], w=[sq])
        for c in range(NCH):
            kb.op("pe", lambda e: e.matmul(P[:, 0:W], lhsT=self.onesb[:, :], rhs=xb[:, c, :W],
                                           start=(c == 0), stop=(c == NCH - 1)), r=[xb, self.onesb], w=[P])
        for c in range(NCH):
            kb.op("pe", lambda e: e.matmul(P[:, W:2 * W], lhsT=self.onesb[:, :], rhs=sq[:, c, :W],
                                           start=(c == 0), stop=(c == NCH - 1)), r=[sq, self.onesb], w=[P])
        m2, var, rstd, nmr = pl["m2"].next(), pl["var"].next(), pl["rstd"].next(), pl["nmr"].next()
        kb.op("act", lambda e: e.activation(out=m2[:, 0, :W], in_=P[:, 0:W], func=AF.Square), r=[P], w=[m2])
        kb.op("dve", lambda e: e.scalar_tensor_tensor(out=var[:, 0, :W], in0=P[:, W:2 * W], scalar=LN_EPS,
                                                      in1=m2[:, 0, :W], op0=ALU.add, op1=ALU.subtract),
              r=[P, m2], w=[var])
        kb.op("act", lambda e: e.activation(out=var[:, 0, :W], in_=var[:, 0, :W], func=AF.Sqrt), r=[var], w=[var])
        kb.op("dve", lambda e: e.reciprocal(out=rstd[:, 0, :W], in_=var[:, 0, :W]), r=[var], w=[rstd])
        kb.op("dve", lambda e: e.scalar_tensor_tensor(out=nmr[:, 0, :W], in0=P[:, 0:W], scalar=-1.0,
                                                      in1=rstd[:, 0, :W], op0=ALU.mult, op1=ALU.mult),
              r=[P, rstd], w=[nmr])
        return rstd, nmr

    def normalize(self, out, x, W, rstd, nmr):
        kb = self.kb
        kb.op("dve", lambda e: e.tensor_tensor(out=out[:, :, :W], in0=x[:, :, :W],
                                               in1=rstd[:, 0:1, :W].broadcast_to([128, NCH, W]), op=ALU.mult),
              r=[x, rstd], w=[out])
        kb.op("pool", lambda e: e.tensor_tensor(out=out[:, :, :W], in0=out[:, :, :W],
                                                in1=nmr[:, 0:1, :W].broadcast_to([128, NCH, W]), op=ALU.add),
              r=[out, nmr], w=[out])

    def stat_pools(self, W, st):
        kb = self.kb
        return {
            "xb": Pool(kb, "xb", 1, [128, NCH, W], BF16, stack=st),
            "sq": Pool(kb, "sq", 1, [128, NCH, W], BF16, stack=st),
            "m2": Pool(kb, "m2", 2, [128, 1, W], F32, stack=st),
            "var": Pool(kb, "var", 2, [128, 1, W], F32, stack=st),
            "rstd": Pool(kb, "rstd", 2, [128, 1, W], F32, stack=st),
            "nmr": Pool(kb, "nmr", 2, [128, 1, W], F32, stack=st),
        }

    def ffn_phase(self, l, s, src, dst, dst_off, ctx):
        kb = self.kb
        W = 256
        mb = 0 if s == 0 else 6
        lni = (l * 3 + (0 if s == 0 else 2)) * NCH
        with kb.phase() as st:
            self.psum = Pool(kb, "ps", 8, [128, 512], F32, psum=True, stack=st)
            w1 = kb.sb("w1", [128, NCH, 2 * DFF], BF16, st)
            w2 = kb.sb("w2", [128, NFF, D], BF16, st)
            for c in range(NCH):
                for hh in range(2):
                    kb.dma("pool", w1[:, c, hh * DFF:(hh + 1) * DFF],
                           self.ffn_w_in[l, s, c * 128:(c + 1) * 128, hh * DFF:(hh + 1) * DFF], w1, w=[w1])
            for c in range(NFF):
                kb.dma("pool", w2[:, c, :], self.ffn_w_out[l, s, c * 128:(c + 1) * 128, :], w2, w=[w2])
            pl = self.stat_pools(W, st)
            xp = Pool(kb, "x", 2, [128, NCH, W], F32, stack=st)
            xnp = Pool(kb, "xn", 2, [128, NCH, W], F32, stack=st)
            up = Pool(kb, "u", 2, [128, NCH, W], BF16, stack=st)
            hp = Pool(kb, "h", 1, [128, NFF, W], BF16, stack=st)
            gp = Pool(kb, "g", 2, [128, W], F32, stack=st)
            srcv = src.rearrange("(c p) t -> p c t", p=128)
            dstv = dst.rearrange("(c p) t -> p c t", p=128)
            tl = self.tiles(W, ctx=ctx)
            T_ = {}

            def stage_a(i):
                seg, t0, _ = tl[i]
                x = xp.next()
                kb.dma("sp", x[:, :, :], srcv[:, :, t0:t0 + W], x, r=[("dram", id(src))], w=[x])
                P = self.psum.next()
                rstd, nmr = self.ln_stats(x, W, P, pl)
                xn = xnp.next()
                self.normalize(xn, x, W, rstd, nmr)
                u = up.next()
                for c in range(NCH):
                    kb.op("act", lambda e: e.activation(out=u[:, c, :], in_=xn[:, c, :], func=AF.Identity,
                                                        scale=self.modp1[:, (mb + 1) * NCH + c, seg:seg + 1],
                                                        bias=self.mods[:, mb * NCH + c, seg:seg + 1]),
                          r=[xn, self.modp1, self.mods], w=[u])
                kb.op("pool", lambda e: e.tensor_scalar(out=x[:, :, :], in0=x[:, :, :], scalar1=ALPHA, scalar2=None,
                                                        op0=ALU.mult), r=[x], w=[x])
                T_[i] = (x, xn, u)

            def stage_b(i):
                x, xn, u = T_[i]
                h = hp.next()
                for j in range(NFF):
                    P = self.psum.next()
                    for c in range(NCH):
                        kb.op("pe", lambda e: e.matmul(P[:, 0:W], lhsT=w1[:, c, j * 128:(j + 1) * 128], rhs=u[:, c, :],
                                                       start=(c == 0), stop=(c == NCH - 1)), r=[w1, u], w=[P])
                    for c in range(NCH):
                        kb.op("pe", lambda e: e.matmul(P[:, W:2 * W], lhsT=w1[:, c, DFF + j * 128:DFF + (j + 1) * 128],
                                                       rhs=u[:, c, :], start=(c == 0), stop=(c == NCH - 1)),
                              r=[w1, u], w=[P])
                    g = gp.next()
                    kb.op("act", lambda e: e.activation(out=g[:, :], in_=P[:, 0:W], func=AF.Silu), r=[P], w=[g])
                    kb.op("dve", lambda e: e.tensor_tensor(out=h[:, j, :], in0=P[:, W:2 * W], in1=g[:, :], op=ALU.mult),
                          r=[P, g], w=[h])
                T_[i] = (x, xn, u, h)

            def stage_c(i):
                seg, t0, _ = tl[i]
                x, xn, u, h = T_.pop(i)
                for oc in range(NCH):
                    if oc % 2 == 0:
                        P = self.psum.next()
                    o0 = (oc % 2) * W
                    for j in range(NFF):
                        kb.op("pe", lambda e: e.matmul(P[:, o0:o0 + W], lhsT=w2[:, j, oc * 128:(oc + 1) * 128],
                                                       rhs=h[:, j, :], start=(j == 0), stop=(j == NFF - 1)),
                              r=[w2, h], w=[P])
                    kb.op("dve", lambda e: e.scalar_tensor_tensor(
                        out=x[:, oc, :], in0=P[:, o0:o0 + W], scalar=self.modh[:, (mb + 2) * NCH + oc, seg:seg + 1],
                        in1=x[:, oc, :], op0=ALU.mult, op1=ALU.add), r=[P, x, self.modh], w=[x])
                P = self.psum.next()
                rstd, nmr = self.ln_stats(x, W, P, pl)
                self.normalize(xn, x, W, rstd, nmr)
                for c in range(NCH):
                    kb.op("act", lambda e: e.activation(out=xn[:, c, :], in_=xn[:, c, :], func=AF.Identity,
                                                        scale=self.lng[:, lni + c:lni + c + 1],
                                                        bias=self.lnb[:, lni + c:lni + c + 1]),
                          r=[xn, self.lng, self.lnb], w=[xn])
                kb.dma("sp", dstv[:, :, t0 - dst_off:t0 - dst_off + W], xn[:, :, :], xn,
                       r=[xn], w=[("dram", id(dst))])

            stage_a(0)
            for i in range(len(tl)):
                stage_b(i)
                if i + 1 < len(tl):
                    stage_a(i + 1)
                stage_c(i)


def host_inputs(inp, b, T):
    L = DEPTH
    d = {}
    d["xT"] = np.ascontiguousarray(np.concatenate([inp["ctx"][b], inp["x"][b, :T]], 0).T)
    cond = np.stack([inp["c"][b], inp["c_ctx"]], -1)
    d["condT"] = np.ascontiguousarray(cond.reshape(NCH, 128, 2).transpose(1, 0, 2))
    d["w_ada"] = inp["w_ada"]
    d["b_ada"] = np.ascontiguousarray(inp["b_ada"].reshape(L, 72, 128).transpose(0, 2, 1))
    d["ln_g"] = np.ascontiguousarray(inp["ln_g"].reshape(L * 3 * NCH, 128).T)
    d["ln_b"] = np.ascontiguousarray(inp["ln_b"].reshape(L * 3 * NCH, 128).T)
    d["ffn_w_in"] = inp["ffn_w_in"]
    d["ffn_w_out"] = inp["ffn_w_out"]
    return d


NWA = 10 + 1 + 4 + 24
CH_Q, CH_QS, CH_K, CH_KS, CH_KB, CH_KBS, CH_V, CH_S5, CH_G = 0, 4, 8, 9, 10, 11, 12, 13, 17
NCH_A = 41


def _mixA_declare(self):
    L = DEPTH
    self.w_inA = self.din("w_inA", [L, D, NCH_A * 128])
    self.ropeC = self.din("ropeC", [128, self.NT])
    self.ropeS = self.din("ropeS", [128, self.NT])
    self.sinkT = self.din("sinkT", [L, 64, 8])
    self.maskP = self.din("maskP", [128, 512])
    self.maskN = self.din("maskN", [128, 512])


def _mixA_phase(self, l, src, qS, kS, vS, usS, gS):
    kb = self.kb
    W = 256
    with kb.phase() as st:
        self.psum = Pool(kb, "ps", 8, [128, 512], F32, psum=True, stack=st)
        w = kb.sb("wA", [128, NCH, NCH_A * 128], BF16, st)
        for c in range(NCH):
            for hh in range(2):
                n0, n1 = (0, 21 * 128) if hh == 0 else (21 * 128, NCH_A * 128)
                kb.dma("pool", w[:, c, n0:n1], self.w_inA[l, c * 128:(c + 1) * 128, n0:n1], w, w=[w])
        pl = self.stat_pools(W, st)
        xp = Pool(kb, "x", 2, [128, NCH, W], F32, stack=st)
        xnp = Pool(kb, "xn", 2, [128, NCH, W], F32, stack=st)
        up = Pool(kb, "u", 2, [128, NCH, W], BF16, stack=st)
        cp = Pool(kb, "rc", 2, [128, W], F32, stack=st)
        sp_ = Pool(kb, "rs", 2, [128, W], F32, stack=st)
        t1p = Pool(kb, "t1", 2, [128, W], F32, stack=st)
        t2p = Pool(kb, "t2", 2, [128, W], F32, stack=st)
        qp = Pool(kb, "qo", 2, [128, 4, W], BF16, stack=st)
        kp = Pool(kb, "ko", 2, [128, 2, W], BF16, stack=st)
        vp = Pool(kb, "vo", 2, [128, 2, 128], BF16, stack=st)
        usp = Pool(kb, "uso", 2, [128, 4, W], F32, stack=st)
        gp = Pool(kb, "go", 2, [128, 24, W], BF16, stack=st)
        srcv = src.rearrange("(c p) t -> p c t", p=128)

        def proj(P, o0, ch, u):
            for c in range(NCH):
                kb.op("pe", lambda e: e.matmul(P[:, o0:o0 + W], lhsT=w[:, c, ch * 128:(ch + 1) * 128], rhs=u[:, c, :],
                                               start=(c == 0), stop=(c == NCH - 1)), r=[w, u], w=[P])

        tl = self.tiles(W)
        T_ = {}

        def stage_a(i):
            seg, t0, _ = tl[i]
            x = xp.next()
            kb.dma("sp", x[:, :, :], srcv[:, :, t0:t0 + W], x, r=[("dram", id(src))], w=[x])
            cT, sT = cp.next(), sp_.next()
            kb.dma("sp", cT[:, :], self.ropeC[:, t0:t0 + W], cT, w=[cT])
            kb.dma("sp", sT[:, :], self.ropeS[:, t0:t0 + W], sT, w=[sT])
            P = self.psum.next()
            rstd, nmr = self.ln_stats(x, W, P, pl)
            xn = xnp.next()
            self.normalize(xn, x, W, rstd, nmr)
            u = up.next()
            for c in range(NCH):
                kb.op("act", lambda e: e.activation(out=u[:, c, :], in_=xn[:, c, :], func=AF.Identity,
                                                    scale=self.modp1[:, 4 * NCH + c, seg:seg + 1],
                                                    bias=self.mods[:, 3 * NCH + c, seg:seg + 1]),
                      r=[xn, self.modp1, self.mods], w=[u])
            T_[i] = (u, cT, sT)

        def stage_b1(i):
            seg, t0, _ = tl[i]
            u, cT, sT = T_[i]
            qo, ko = qp.next(), kp.next()

            def rope(dst_ap, dst, ch, chs):
                P = self.psum.next()
                proj(P, 0, ch, u)
                proj(P, W, chs, u)
                t1, t2 = t1p.next(), t2p.next()
                kb.op("dve", lambda e: e.tensor_tensor(out=t1[:, :], in0=P[:, 0:W], in1=cT[:, :], op=ALU.mult),
                      r=[P, cT], w=[t1])
                kb.op("dve", lambda e: e.tensor_tensor(out=t2[:, :], in0=P[:, W:2 * W], in1=sT[:, :], op=ALU.mult),
                      r=[P, sT], w=[t2])
                kb.op("pool", lambda e: e.tensor_tensor(out=dst_ap, in0=t1[:, :], in1=t2[:, :], op=ALU.add),
                      r=[t1, t2], w=[dst])
            for c in range(4):
                rope(qo[:, c, :], qo, CH_Q + c, CH_QS + c)
            rope(ko[:, 0, :], ko, CH_K, CH_KS)
            rope(ko[:, 1, :], ko, CH_KB, CH_KBS)
            kb.dma("sp", qS.rearrange("(c p) t -> p c t", p=128)[:, :, t0:t0 + W], qo[:, :, :], qo, r=[qo],
                   w=[("dram", id(qS))])
            kb.dma("sp", kS.rearrange("(c p) t -> p c t", p=128)[:, :, t0:t0 + W], ko[:, :, :], ko, r=[ko],
                   w=[("dram", id(kS))])

        def stage_b2(i):
            seg, t0, _ = tl[i]
            u, cT, sT = T_.pop(i)
            vo = vp.next()
            P = self.psum.next()
            for tb in range(W // 128):
                for c in range(NCH):
                    kb.op("pe", lambda e: e.matmul(P[:, tb * 128:(tb + 1) * 128], lhsT=u[:, c, tb * 128:(tb + 1) * 128],
                                                   rhs=w[:, c, CH_V * 128:(CH_V + 1) * 128],
                                                   start=(c == 0), stop=(c == NCH - 1)), r=[w, u], w=[P])
            kb.op("act", lambda e: e.activation(out=vo[:, :, :], in_=P[:, 0:W].rearrange("p (b n) -> p b n", b=2),
                                                func=AF.Copy), r=[P], w=[vo])
            kb.dma("sp", vS[t0:t0 + W, :].rearrange("(b p) n -> p b n", p=128), vo[:, :, :], vo, r=[vo],
                   w=[("dram", id(vS))])
            uso = usp.next()
            for c in range(4):
                if c % 2 == 0:
                    P = self.psum.next()
                o0 = (c % 2) * W
                proj(P, o0, CH_S5 + c, u)
                kb.op("act", lambda e: e.activation(out=uso[:, c, :], in_=P[:, o0:o0 + W], func=AF.Copy),
                      r=[P], w=[uso])
            kb.dma("sp", usS.rearrange("(c p) t -> p c t", p=128)[:, :, t0:t0 + W], uso[:, :, :], uso, r=[uso],
                   w=[("dram", id(usS))])
            go = gp.next()
            for c in range(24):
                if c % 2 == 0:
                    P = self.psum.next()
                o0 = (c % 2) * W
                proj(P, o0, CH_G + c, u)
                kb.op("act", lambda e: e.activation(out=go[:, c, :], in_=P[:, o0:o0 + W], func=AF.Sigmoid),
                      r=[P], w=[go])
            kb.dma("sp", gS.rearrange("(c p) t -> p c t", p=128)[:, :, t0:t0 + W], go[:, :, :], go, r=[go],
                   w=[("dram", id(gS))])

        stage_a(0)
        for i in range(len(tl)):
            stage_b1(i)
            if i + 1 < len(tl):
                stage_a(i + 1)
            stage_b2(i)


def _attn_phase(self, l, qS, kS, vS, yaS):
    kb = self.kb
    NB = self.T // 128
    with kb.phase() as st:
        self.psum = Pool(kb, "ps", 8, [128, 512], F32, psum=True, stack=st)
        mP = kb.sb("mP", [128, 512], BF16, st)
        mN = kb.sb("mN", [128, 512], BF16, st)
        kb.dma("pool", mP[:, :], self.maskP[:, :], mP, w=[mP])
        kb.dma("pool", mN[:, :], self.maskN[:, :], mN, w=[mN])
        ones = kb.sb("ones64", [128, 64], BF16, st)
        kb.op("dve", lambda e: e.memset(ones[:], 1.0), w=[ones])
        esk = kb.sb("esk", [64, 8], F32, st)
        kb.dma("sp", esk[:, :], self.sinkT[l, :, :], esk, w=[esk])
        kb.op("act", lambda e: e.activation(out=esk[:, :], in_=esk[:, :], func=AF.Exp), r=[esk], w=[esk])
        eskb = kb.sb("eskb", [64, 8, 128], F32, st)
        kb.op("dve", lambda e: e.tensor_copy(out=eskb[:, :, :], in_=esk[:, :].unsqueeze(2).broadcast_to([64, 8, 128])),
              r=[esk], w=[eskb])
        kc = kb.sb("kc", [128, 2, CTX], BF16, st)
        kb.dma("sp", kc[:, :, :], kS.rearrange("(c p) t -> p c t", p=128)[:, :, 0:CTX], kc, r=[("dram", id(kS))], w=[kc])
        vc = kb.sb("vc", [128, 2, 128], BF16, st)
        kb.dma("sp", vc[:, :, :], vS[0:CTX, :].rearrange("(b p) n -> p b n", p=128), vc, r=[("dram", id(vS))], w=[vc])
        qp = Pool(kb, "aq", 2, [128, 4, 128], BF16, stack=st)
        kwp = Pool(kb, "akw", 2, [128, 2, 384], BF16, stack=st)
        vwp = Pool(kb, "avw", 2, [128, 3, 128], BF16, stack=st)
        pp = Pool(kb, "ap", 4, [128, 512], BF16, stack=st)
        dp = Pool(kb, "ad", 2, [64, 512], F32, stack=st)
        op_ = Pool(kb, "ao", 2, [64, 8, 128], BF16, stack=st)
        qv = qS.rearrange("(c p) t -> p c t", p=128)
        kv = kS.rearrange("(c p) t -> p c t", p=128)
        blocks = [(1, b) for b in range(CTX // 128)] + [(0, b) for b in range(NB)]
        self._acc_i = 0
        self._sc_i = 0
        for (seg, b) in blocks:
            t0 = b * 128 if seg == 1 else CTX + b * 128
            q = qp.next()
            kb.dma("sp", q[:, :, :], qv[:, :, t0:t0 + 128], q, r=[("dram", id(qS))], w=[q])
            keyblocks = []
            if seg == 0:
                lo = max(b - 1, 0)
                hi = min(b + 1, NB - 1)
                nb_ = hi - lo + 1
                kw, vw = kwp.next(), vwp.next()
                kb.dma("sp", kw[:, :, 0:nb_ * 128], kv[:, :, CTX + lo * 128:CTX + (hi + 1) * 128], kw,
                       r=[("dram", id(kS))], w=[kw])
                kb.dma("sp", vw[:, 0:nb_, :],
                       vS[CTX + lo * 128:CTX + (hi + 1) * 128, :].rearrange("(b p) n -> p b n", p=128), vw,
                       r=[("dram", id(vS))], w=[vw])
                for bb in range(lo, hi + 1):
                    i = bb - lo
                    mask = mP if bb < b else (mN if bb > b else None)
                    keyblocks.append((kw, i * 128, vw, i, mask))
            for i in range(CTX // 128):
                keyblocks.append((kc, i * 128, vc, i, None))
            oo = op_.next()
            accb = self.psum.bufs[0:4]
            scb = self.psum.bufs[4:8]
            for kvh in range(2):
                Pn = accb[(self._acc_i) % 4]
                Pd = accb[(self._acc_i + 1) % 4]
                self._acc_i += 2
                for bi, (kbuf, koff, vbuf, vi, mask) in enumerate(keyblocks):
                    Ps = [scb[self._sc_i % 4], scb[(self._sc_i + 1) % 4]]
                    self._sc_i += 2
                    pt = pp.next()
                    for par in range(2):
                        base = par * 64
                        var = 0 if (kvh * 64 == base) else 1
                        for j in range(2):
                            h = kvh * 4 + 2 * j + par
                            kb.op("pe", lambda e: e.matmul(Ps[par][:, j * 128:(j + 1) * 128],
                                                           lhsT=kbuf[base:base + 64, var, koff:koff + 128],
                                                           rhs=q[base:base + 64, h // 2, :], start=True, stop=True),
                                  r=[kbuf, q], w=[Ps[par]])
                        kb.op("act", lambda e: e.activation(out=pt[:, par * 256:(par + 1) * 256], in_=Ps[par][:, 0:256],
                                                            func=AF.Exp, scale=0.125), r=[Ps[par]], w=[pt])
                    if mask is not None:
                        kb.op("pool", lambda e: e.tensor_tensor(out=pt[:, :], in0=pt[:, :], in1=mask[:, :], op=ALU.mult),
                              r=[pt, mask], w=[pt])
                    first, last = bi == 0, bi == len(keyblocks) - 1
                    kb.op("pe", lambda e: e.matmul(Pn[0:64, :], lhsT=vbuf[:, vi, kvh * 64:(kvh + 1) * 64], rhs=pt[:, :],
                                                   start=first, stop=last), r=[vbuf, pt], w=[Pn])
                    kb.op("pe", lambda e: e.matmul(Pd[0:64, :], lhsT=ones[:, :], rhs=pt[:, :],
                                                   start=first, stop=last), r=[ones, pt], w=[Pd])
                den = dp.next()
                kb.op("dve", lambda e: e.tensor_tensor(
                    out=den[:, :].rearrange("p (r j q) -> p r j q", r=2, j=2), in0=Pd[0:64, :].rearrange("p (r j q) -> p r j q", r=2, j=2),
                    in1=eskb[:, kvh * 4:(kvh + 1) * 4, :].rearrange("p (j r) q -> p r j q", r=2),
                    op=ALU.add), r=[Pd, eskb], w=[den])
                kb.op("dve", lambda e: e.reciprocal(out=den[:, :], in_=den[:, :]), r=[den], w=[den])
                kb.op("dve", lambda e: e.tensor_tensor(
                    out=oo[:, kvh * 4:(kvh + 1) * 4, :].rearrange("p (j r) q -> p r j q", r=2),
                    in0=Pn[0:64, :].rearrange("p (r j q) -> p r j q", r=2, j=2),
                    in1=den[:, :].rearrange("p (r j q) -> p r j q", r=2, j=2), op=ALU.mult),
                      r=[Pn, den], w=[oo])
            kb.dma("sp", yaS.rearrange("(h p) t -> p h t", p=64)[:, :, t0:t0 + 128], oo[:, :, :], oo, r=[oo],
                   w=[("dram", id(yaS))])


Model.mixA_declare = _mixA_declare
Model.mixA_phase = _mixA_phase
Model.attn_phase = _attn_phase


def _rope_tables(T):
    NT = CTX + T
    C = np.ones((128, NT), np.float32)
    S = np.zeros((128, NT), np.float32)
    t = np.arange(T)
    row = (t // 64).astype(np.float32)
    col = (t % 64).astype(np.float32)
    inv = (10000.0 ** (-np.arange(16, dtype=np.float32) / 16)).astype(np.float32)
    for d in range(64):
        i = d % 16
        pos = row if d < 32 else col
        ang = (pos * inv[i]).astype(np.float32)
        sign = -1.0 if (d % 32) < 16 else 1.0
        for hb in (0, 64):
            C[hb + d, CTX:] = np.cos(ang)
            S[hb + d, CTX:] = sign * np.sin(ang)
    return C, S


def _swap_perm(n_heads):
    idx = []
    for h in range(n_heads):
        for d in range(64):
            p = d + 16 if (d % 32) < 16 else d - 16
            idx.append(h * 64 + p)
    return np.array(idx)


def host_inputs_A(inp, T):
    d = {}
    w = inp["w_in"]
    q = w[:, :, 0:512]
    k = w[:, :, 512:640]
    v = w[:, :, 640:768]
    kB = np.concatenate([k[:, :, 64:128], k[:, :, 0:64]], -1)
    s5 = w[:, :, 2624:3136]
    g = w[:, :, 3136:6208]
    d["w_inA"] = np.ascontiguousarray(np.concatenate(
        [q, q[:, :, _swap_perm(8)], k, k[:, :, _swap_perm(2)], kB, kB[:, :, _swap_perm(2)], v, s5, g], -1))
    C, S = _rope_tables(T)
    d["ropeC"], d["ropeS"] = C, S
    d["sinkT"] = np.ascontiguousarray(np.broadcast_to(inp["attn_sink"][:, None, :], (DEPTH, 64, 8)))
    j = np.arange(128)[:, None]
    i = np.arange(128)[None, :]
    d["maskP"] = np.ascontiguousarray(np.tile((j >= i).astype(np.float32), (1, 4)))
    d["maskN"] = np.ascontiguousarray(np.tile((j <= i).astype(np.float32), (1, 4)))
    return d


I32 = mybir.dt.int32
TWO_PI = 2.0 * np.pi


def _s5_declare(self):
    L = DEPTH
    self.s5_are = self.din("s5_are", [L, 2, 128, 4, 64])
    self.s5_aim = self.din("s5_aim", [L, 2, 128, 4, 64])
    self.s5_ls = self.din("s5_ls", [L, 2, 128, 4])
    self.s5_brT = self.din("s5_brT", [L, 128, 4, 64])
    self.s5_biT = self.din("s5_biT", [L, 128, 4, 64])
    self.s5_are2 = self.din("s5_are2", [L, 2, 128, 16])
    self.s5_aim2 = self.din("s5_aim2", [L, 2, 128, 16])
    self.s5_ls2 = self.din("s5_ls2", [L, 2, 128, 16])
    self.s5_crT = self.din("s5_crT", [L, 128, 16, 16])
    self.s5_ciT = self.din("s5_ciT", [L, 128, 16, 16])
    self.s5_rowmask = self.din("s5_rowmask", [128, 16, 2])
    self.s5_dT = self.din("s5_dT", [128, L * 4])
    self.s5_glub = self.din("s5_glub", [128, L * 4])
    self.s5_gluw = self.din("s5_gluw", [L, 512, 512])
    self.tauT = self.din("tauT", [128, 128])


def _sincos(self, ang, angk, n, S, Sk, C, Ck, st):
    kb = self.kb
    t = kb.sb("sc_t", [128, n], F32, st)
    ti = kb.sb("sc_i", [128, n], I32, st)
    tf = kb.sb("sc_f", [128, n], F32, st)
    for (off, dst, dk) in ((0.0, S, Sk), (0.25, C, Ck)):
        kb.op("dve", lambda e: e.tensor_scalar(out=t[:, :], in0=ang, scalar1=1.0 / TWO_PI, scalar2=off,
                                               op0=ALU.mult, op1=ALU.add), r=[angk], w=[t])
        kb.op("dve", lambda e: e.tensor_copy(out=ti[:, :], in_=t[:, :]), r=[t], w=[ti])
        kb.op("dve", lambda e: e.tensor_copy(out=tf[:, :], in_=ti[:, :]), r=[ti], w=[tf])
        kb.op("dve", lambda e: e.tensor_tensor(out=tf[:, :], in0=t[:, :], in1=tf[:, :], op=ALU.subtract),
              r=[t, tf], w=[tf])
        kb.op("act", lambda e: e.activation(out=dst, in_=tf[:, :], func=AF.Sin, scale=TWO_PI), r=[tf], w=[dk])


def _s5_dir(self, l, d, usS, ysbS, ysS, st, PS2):
    kb = self.kb
    NT = self.NT
    nchunk = NT // 128
    usv = usS.rearrange("(c p) t -> p c t", p=128)
    ybv = ysbS.rearrange("(c p) t -> p c t", p=128)
    ysv = ysS.rearrange("(c p) t -> p c t", p=128)
    V = lambda e_, f, r, w: kb.op(e_, f, r=r, w=w)
    _pers = {}
    for (n_, shp_, dt_) in (("DR", [128, 16, 128], BF16), ("DI", [128, 16, 128], BF16), ("COS", [128, 16, 128], F32),
                            ("SIN", [128, 16, 128], F32), ("RHO0", [128, 16, 128], F32), ("rho", [128, 16], F32),
                            ("lr2", [128, 16], F32), ("li2", [128, 16], F32), ("CR", [128, 16, 128], BF16),
                            ("CIn", [128, 16, 128], BF16), ("s5d", [128, 4], F32), ("s5gb", [128, 4], F32),
                            ("gluw", [128, 4, 512], BF16)):
        _pers[n_] = kb.sb(n_, shp_, dt_, st)
    sbf = lambda n, shp, dt=F32: _pers[n] if n in _pers else kb.sb(n, shp, dt, st)
    st2 = contextlib.ExitStack()
    tmpf = lambda n, shp, dt=F32: kb.sb(n, shp, dt, st2)
    are, aim = tmpf("are", [128, 4, 64]), tmpf("aim", [128, 4, 64])
    ls = tmpf("ls", [128, 4])
    br, bi = tmpf("br", [128, 4, 64]), tmpf("bi", [128, 4, 64])
    kb.dma("sp", are[:, :, :], self.s5_are[l, d], are, w=[are])
    kb.dma("sp", aim[:, :, :], self.s5_aim[l, d], aim, w=[aim])
    kb.dma("sp", ls[:, :], self.s5_ls[l, d], ls, w=[ls])
    kb.dma("sp", br[:, :, :], self.s5_brT[l], br, w=[br])
    kb.dma("sp", bi[:, :, :], self.s5_biT[l], bi, w=[bi])
    rmask = tmpf("rmask", [128, 16, 2])
    kb.dma("sp", rmask[:, :, :], self.s5_rowmask[:, :, :], rmask, w=[rmask])
    V("act", lambda e: e.activation(out=ls[:, :], in_=ls[:, :], func=AF.Exp), [ls], [ls])
    dtb = ls[:, :].unsqueeze(2).broadcast_to([128, 4, 64])
    adt, th = tmpf("adt", [128, 4, 64]), tmpf("th", [128, 4, 64])
    V("dve", lambda e: e.tensor_tensor(out=adt[:, :, :], in0=are[:, :, :], in1=dtb, op=ALU.mult), [are, ls], [adt])
    V("act", lambda e: e.activation(out=adt[:, :, :], in_=adt[:, :, :], func=AF.Exp), [adt], [adt])
    V("dve", lambda e: e.tensor_tensor(out=th[:, :, :], in0=aim[:, :, :], in1=dtb, op=ALU.mult), [aim, ls], [th])
    Sd, Cd = tmpf("Sd", [128, 256]), tmpf("Cd", [128, 256])
    thf = th[:, :, :].rearrange("p a b -> p (a b)")
    self.sincos(thf, th, 256, Sd[:, :], Sd, Cd[:, :], Cd, st2)
    lr, li = tmpf("lr", [128, 256]), tmpf("li", [128, 256])
    magf = adt[:, :, :].rearrange("p a b -> p (a b)")
    V("dve", lambda e: e.tensor_tensor(out=lr[:, :], in0=magf, in1=Cd[:, :], op=ALU.mult), [adt, Cd], [lr])
    V("dve", lambda e: e.tensor_tensor(out=li[:, :], in0=magf, in1=Sd[:, :], op=ALU.mult), [adt, Sd], [li])
    aref = are[:, :, :].rearrange("p a b -> p (a b)")
    aimf = aim[:, :, :].rearrange("p a b -> p (a b)")
    t1, t2, den = tmpf("t1", [128, 256]), tmpf("t2", [128, 256]), tmpf("den", [128, 256])
    V("dve", lambda e: e.tensor_tensor(out=t1[:, :], in0=aref, in1=aref, op=ALU.mult), [are], [t1])
    V("dve", lambda e: e.tensor_tensor(out=t2[:, :], in0=aimf, in1=aimf, op=ALU.mult), [aim], [t2])
    V("dve", lambda e: e.tensor_tensor(out=den[:, :], in0=t1[:, :], in1=t2[:, :], op=ALU.add), [t1, t2], [den])
    V("dve", lambda e: e.reciprocal(out=den[:, :], in_=den[:, :]), [den], [den])
    V("dve", lambda e: e.tensor_scalar(out=lr[:, :], in0=lr[:, :], scalar1=-1.0, scalar2=None, op0=ALU.add),
      [lr], [lr])
    cr, ci = tmpf("cr", [128, 256]), tmpf("ci", [128, 256])
    V("dve", lambda e: e.tensor_tensor(out=t1[:, :], in0=lr[:, :], in1=aref, op=ALU.mult), [lr, are], [t1])
    V("dve", lambda e: e.tensor_tensor(out=t2[:, :], in0=li[:, :], in1=aimf, op=ALU.mult), [li, aim], [t2])
    V("dve", lambda e: e.tensor_tensor(out=cr[:, :], in0=t1[:, :], in1=t2[:, :], op=ALU.add), [t1, t2], [cr])
    V("dve", lambda e: e.tensor_tensor(out=cr[:, :], in0=cr[:, :], in1=den[:, :], op=ALU.mult), [cr, den], [cr])
    V("dve", lambda e: e.tensor_tensor(out=t1[:, :], in0=li[:, :], in1=aref, op=ALU.mult), [li, are], [t1])
    V("dve", lambda e: e.tensor_tensor(out=t2[:, :], in0=lr[:, :], in1=aimf, op=ALU.mult), [lr, aim], [t2])
    V("dve", lambda e: e.tensor_tensor(out=ci[:, :], in0=t1[:, :], in1=t2[:, :], op=ALU.subtract), [t1, t2], [ci])
    V("dve", lambda e: e.tensor_tensor(out=ci[:, :], in0=ci[:, :], in1=den[:, :], op=ALU.mult), [ci, den], [ci])
    brf = br[:, :, :].rearrange("p a b -> p (a b)")
    bif = bi[:, :, :].rearrange("p a b -> p (a b)")
    bbr, bbi = tmpf("bbr", [128, 4, 64]), tmpf("bbi", [128, 4, 64])
    bbrf = bbr[:, :, :].rearrange("p a b -> p (a b)")
    bbif = bbi[:, :, :].rearrange("p a b -> p (a b)")
    V("dve", lambda e: e.tensor_tensor(out=t1[:, :], in0=cr[:, :], in1=brf, op=ALU.mult), [cr, br], [t1])
    V("dve", lambda e: e.tensor_tensor(out=t2[:, :], in0=ci[:, :], in1=bif, op=ALU.mult), [ci, bi], [t2])
    V("dve", lambda e: e.tensor_tensor(out=bbrf, in0=t1[:, :], in1=t2[:, :], op=ALU.subtract), [t1, t2], [bbr])
    V("dve", lambda e: e.tensor_tensor(out=t1[:, :], in0=cr[:, :], in1=bif, op=ALU.mult), [cr, bi], [t1])
    V("dve", lambda e: e.tensor_tensor(out=t2[:, :], in0=ci[:, :], in1=brf, op=ALU.mult), [ci, br], [t2])
    V("dve", lambda e: e.tensor_tensor(out=bbif, in0=t1[:, :], in1=t2[:, :], op=ALU.add), [t1, t2], [bbi])
    DR, DI = sbf("DR", [128, 16, 128], BF16), sbf("DI", [128, 16, 128], BF16)
    for j in range(16):
        for gp in range(2):
            for (dst, srcb) in ((DR, bbr), (DI, bbi)):
                V("dve", lambda e: e.tensor_scalar(out=dst[:, j, gp * 64:(gp + 1) * 64], in0=srcb[:, j // 4, :],
                                                   scalar1=rmask[:, j, gp:gp + 1], scalar2=None, op0=ALU.mult),
                  [srcb, rmask], [dst])
    are2, aim2, ls2 = tmpf("are2", [128, 16]), tmpf("aim2", [128, 16]), tmpf("ls2", [128, 16])
    kb.dma("sp", are2[:, :], self.s5_are2[l, d], are2, w=[are2])
    kb.dma("sp", aim2[:, :], self.s5_aim2[l, d], aim2, w=[aim2])
    kb.dma("sp", ls2[:, :], self.s5_ls2[l, d], ls2, w=[ls2])
    tau = tmpf("tau", [128, 128])
    kb.dma("sp", tau[:, :], self.tauT[:, :], tau, w=[tau])
    V("act", lambda e: e.activation(out=ls2[:, :], in_=ls2[:, :], func=AF.Exp), [ls2], [ls2])
    rho, th2 = sbf("rho", [128, 16]), tmpf("th2", [128, 16])
    V("dve", lambda e: e.tensor_tensor(out=rho[:, :], in0=are2[:, :], in1=ls2[:, :], op=ALU.mult), [are2, ls2], [rho])
    V("act", lambda e: e.activation(out=rho[:, :], in_=rho[:, :], func=AF.Exp), [rho], [rho])
    V("dve", lambda e: e.tensor_tensor(out=th2[:, :], in0=aim2[:, :], in1=ls2[:, :], op=ALU.mult), [aim2, ls2], [th2])
    ang = tmpf("ang", [128, 16, 128])
    V("dve", lambda e: e.tensor_tensor(out=ang[:, :, :], in0=th2[:, :].unsqueeze(2).broadcast_to([128, 16, 128]),
                                       in1=tau[:, :].unsqueeze(1).broadcast_to([128, 16, 128]), op=ALU.mult),
      [th2, tau], [ang])
    COS, SIN = sbf("COS", [128, 16, 128]), sbf("SIN", [128, 16, 128])
    self.sincos(ang[:, :, :].rearrange("p a b -> p (a b)"), ang, 2048,
                SIN[:, :, :].rearrange("p a b -> p (a b)"), SIN, COS[:, :, :].rearrange("p a b -> p (a b)"), COS, st2)
    S1, C1 = tmpf("S1", [128, 16]), tmpf("C1", [128, 16])
    self.sincos(th2[:, :], th2, 16, S1[:, :], S1, C1[:, :], C1, st2)
    lr2, li2 = sbf("lr2", [128, 16]), sbf("li2", [128, 16])
    V("dve", lambda e: e.tensor_tensor(out=lr2[:, :], in0=rho[:, :], in1=C1[:, :], op=ALU.mult), [rho, C1], [lr2])
    V("dve", lambda e: e.tensor_tensor(out=li2[:, :], in0=rho[:, :], in1=S1[:, :], op=ALU.mult), [rho, S1], [li2])
    RHO0 = sbf("RHO0", [128, 16, 128])
    V("dve", lambda e: e.tensor_copy(out=RHO0[:, :, :], in_=rho[:, :].unsqueeze(2).broadcast_to([128, 16, 128])),
      [rho], [RHO0])
    f0 = 127 if d == 1 else 0
    V("dve", lambda e: e.memset(RHO0[:, :, f0:f0 + 1], 0.0), [], [RHO0])
    crT, ciT = tmpf("crT", [128, 16, 16]), tmpf("ciT", [128, 16, 16])
    kb.dma("sp", crT[:, :, :], self.s5_crT[l], crT, w=[crT])
    kb.dma("sp", ciT[:, :, :], self.s5_ciT[l], ciT, w=[ciT])
    CR, CIn = sbf("CR", [128, 16, 128], BF16), sbf("CIn", [128, 16, 128], BF16)
    V("dve", lambda e: e.memset(CR[:, :, :], 0.0), [], [CR])
    V("dve", lambda e: e.memset(CIn[:, :, :], 0.0), [], [CIn])
    for j in range(16):
        for gp in range(2):
            c0 = 32 * (j % 4) + 16 * gp
            V("dve", lambda e: e.tensor_copy(out=CR[gp * 64:(gp + 1) * 64, j, c0:c0 + 16],
                                             in_=crT[gp * 64:(gp + 1) * 64, j, :]), [crT], [CR])
            V("dve", lambda e: e.tensor_scalar(out=CIn[gp * 64:(gp + 1) * 64, j, c0:c0 + 16],
                                               in0=ciT[gp * 64:(gp + 1) * 64, j, :], scalar1=-1.0, scalar2=None,
                                               op0=ALU.mult), [ciT], [CIn])
    if d == 0:
        dv, gb = sbf("s5d", [128, 4]), sbf("s5gb", [128, 4])
        kb.dma("sp", dv[:, :], self.s5_dT[:, l * 4:(l + 1) * 4], dv, w=[dv])
        kb.dma("sp", gb[:, :], self.s5_glub[:, l * 4:(l + 1) * 4], gb, w=[gb])
        gw = sbf("gluw", [128, 4, 512], BF16)
        for c in range(4):
            kb.dma("pool", gw[:, c, :], self.s5_gluw[l, c * 128:(c + 1) * 128, :], gw, w=[gw])
    kb.barrier()
    st2.close()
    usp = Pool(kb, "s5u", 2, [128, 4, 128], BF16, stack=st)
    usfp = Pool(kb, "s5uf", 2, [128, 4, 128], F32, stack=st)
    ybp = Pool(kb, "s5yb", 2, [128, 4, 128], F32, stack=st)
    mp = [Pool(kb, "s5m%d" % i, 1, [128, 8, 128], F32, stack=st) for i in range(4)]
    ZR, ZI = sbf("ZR", [128, 16, 128]), sbf("ZI", [128, 16, 128])
    XZR, XZI = sbf("XZR", [128, 16, 128]), sbf("XZI", [128, 16, 128])
    up_ = [Pool(kb, "s5t%d" % i, 1, [128, 16, 128], F32, stack=st) for i in range(2)]
    XR, XI = sbf("XR", [128, 16, 128], BF16), sbf("XI", [128, 16, 128], BF16)
    xlr, xli = sbf("xlr", [128, 16]), sbf("xli", [128, 16])
    cjr, cji = sbf("cjr", [128, 16]), sbf("cji", [128, 16])
    tt = [sbf("s5tt%d" % i, [128, 16]) for i in range(4)]
    yo = Pool(kb, "s5yo", 2, [128, 4, 128], F32, stack=st)
    rev = (d == 1)
    R3 = (lambda ap: ap[:, :, ::-1]) if rev else (lambda ap: ap)
    first, last = (127, 0) if rev else (0, 127)
    order = [0, 1] + list(range(2, nchunk))
    if rev:
        order = [1, 0] + list(range(nchunk - 1, 1, -1))
    yield
    for ci_, ch in enumerate(order):
        if ci_ > 0:
            yield
        t0 = ch * 128
        us = usp.next()
        kb.dma("pool", us[:, :, :], usv[:, :, t0:t0 + 128], us, r=[("dram", id(usS))], w=[us])
        for hf in range(2):
            PR, PI = PS2.next(), PS2.next()
            for jj in range(8):
                j = hf * 8 + jj
                kb.op("pe", lambda e: e.matmul(PR[:, jj * 128:(jj + 1) * 128], lhsT=DR[:, j, :], rhs=us[:, j // 4, :],
                                               start=True, stop=True), r=[DR, us], w=[PR])
                kb.op("pe", lambda e: e.matmul(PI[:, jj * 128:(jj + 1) * 128], lhsT=DI[:, j, :], rhs=us[:, j // 4, :],
                                               start=True, stop=True), r=[DI, us], w=[PI])
            prv = PR[:, :].rearrange("p (a b) -> p a b", a=8)
            piv = PI[:, :].rearrange("p (a b) -> p a b", a=8)
            cs = R3(COS[:, hf * 8:(hf + 1) * 8, :])
            sn = R3(SIN[:, hf * 8:(hf + 1) * 8, :])
            m = [p.next() for p in mp]
            V("dve", lambda e: e.tensor_tensor(out=m[0][:, :, :], in0=prv, in1=cs, op=ALU.mult), [PR, COS], [m[0]])
            V("dve", lambda e: e.tensor_tensor(out=m[1][:, :, :], in0=piv, in1=sn, op=ALU.mult), [PI, SIN], [m[1]])
            V("dve", lambda e: e.tensor_tensor(out=m[2][:, :, :], in0=piv, in1=cs, op=ALU.mult), [PI, COS], [m[2]])
            V("dve", lambda e: e.tensor_tensor(out=m[3][:, :, :], in0=prv, in1=sn, op=ALU.mult), [PR, SIN], [m[3]])
            V("pool", lambda e: e.tensor_tensor(out=ZR[:, hf * 8:(hf + 1) * 8, :], in0=m[0][:, :, :], in1=m[1][:, :, :],
                                                op=ALU.add), [m[0], m[1]], [ZR])
            V("pool", lambda e: e.tensor_tensor(out=ZI[:, hf * 8:(hf + 1) * 8, :], in0=m[2][:, :, :], in1=m[3][:, :, :],
                                                op=ALU.subtract), [m[2], m[3]], [ZI])
            yield
        if ci_ > 0:
            V("pool", lambda e: e.tensor_tensor(out=tt[0][:, :], in0=lr2[:, :], in1=xlr[:, :], op=ALU.mult), [lr2, xlr], [tt[0]])
            V("pool", lambda e: e.tensor_tensor(out=tt[1][:, :], in0=li2[:, :], in1=xli[:, :], op=ALU.mult), [li2, xli], [tt[1]])
            V("pool", lambda e: e.tensor_tensor(out=cjr[:, :], in0=tt[0][:, :], in1=tt[1][:, :], op=ALU.subtract), [tt[0], tt[1]], [cjr])
            V("pool", lambda e: e.tensor_tensor(out=tt[2][:, :], in0=lr2[:, :], in1=xli[:, :], op=ALU.mult), [lr2, xli], [tt[2]])
            V("pool", lambda e: e.tensor_tensor(out=tt[3][:, :], in0=li2[:, :], in1=xlr[:, :], op=ALU.mult), [li2, xlr], [tt[3]])
            V("pool", lambda e: e.tensor_tensor(out=cji[:, :], in0=tt[2][:, :], in1=tt[3][:, :], op=ALU.add), [tt[2], tt[3]], [cji])
            V("pool", lambda e: e.tensor_tensor(out=ZR[:, :, first], in0=ZR[:, :, first], in1=cjr[:, :], op=ALU.add), [ZR, cjr], [ZR])
            V("pool", lambda e: e.tensor_tensor(out=ZI[:, :, first], in0=ZI[:, :, first], in1=cji[:, :], op=ALU.add), [ZI, cji], [ZI])
        fl = lambda b_: (b_[:, :, :].rearrange("p a b -> p (a b)")[:, ::-1] if rev
                         else b_[:, :, :].rearrange("p a b -> p (a b)"))
        V("dve", lambda e: e.tensor_tensor_scan(out=fl(XZR), data0=fl(RHO0), data1=fl(ZR), initial=0.0,
                                                op0=ALU.mult, op1=ALU.add), [RHO0, ZR], [XZR])
        yield
        V("dve", lambda e: e.tensor_tensor_scan(out=fl(XZI), data0=fl(RHO0), data1=fl(ZI), initial=0.0,
                                                op0=ALU.mult, op1=ALU.add), [RHO0, ZI], [XZI])
        yield
        cs, sn = R3(COS[:, :, :]), R3(SIN[:, :, :])
        ua, ub = up_[0].next(), up_[1].next()
        V("dve", lambda e: e.tensor_tensor(out=ua[:, :, :], in0=XZR[:, :, :], in1=cs, op=ALU.mult), [XZR, COS], [ua])
        V("pool", lambda e: e.tensor_tensor(out=ub[:, :, :], in0=XZI[:, :, :], in1=sn, op=ALU.mult), [XZI, SIN], [ub])
        V("dve", lambda e: e.tensor_tensor(out=XR[:, :, :], in0=ua[:, :, :], in1=ub[:, :, :], op=ALU.subtract), [ua, ub], [XR])
        V("pool", lambda e: e.tensor_tensor(out=xlr[:, :], in0=ua[:, :, last], in1=ub[:, :, last], op=ALU.subtract), [ua, ub], [xlr])
        yield
        V("pool", lambda e: e.tensor_tensor(out=ua[:, :, :], in0=XZR[:, :, :], in1=sn, op=ALU.mult), [XZR, SIN], [ua])
        V("dve", lambda e: e.tensor_tensor(out=ub[:, :, :], in0=XZI[:, :, :], in1=cs, op=ALU.mult), [XZI, COS], [ub])
        V("pool", lambda e: e.tensor_tensor(out=XI[:, :, :], in0=ua[:, :, :], in1=ub[:, :, :], op=ALU.add), [ua, ub], [XI])
        V("pool", lambda e: e.tensor_tensor(out=xli[:, :], in0=ua[:, :, last], in1=ub[:, :, last], op=ALU.add), [ua, ub], [xli])
        yield
        PY = PS2.next()
        for cc in range(4):
            for jj in range(4):
                j = cc * 4 + jj
                kb.op("pe", lambda e: e.matmul(PY[:, cc * 128:(cc + 1) * 128], lhsT=CR[:, j, :], rhs=XR[:, j, :],
                                               start=(jj == 0), stop=False), r=[CR, XR], w=[PY])
                kb.op("pe", lambda e: e.matmul(PY[:, cc * 128:(cc + 1) * 128], lhsT=CIn[:, j, :], rhs=XI[:, j, :],
                                               start=False, stop=(jj == 3)), r=[CIn, XI], w=[PY])
        pyv = PY[:, 0:512].rearrange("p (a b) -> p a b", a=4)
        if d == 1:
            y = yo.next()
            V("act", lambda e: e.activation(out=y[:, :, :], in_=pyv, func=AF.Copy), [PY], [y])
            kb.dma("act", ybv[:, :, t0:t0 + 128], y[:, :, :], y, r=[y], w=[("dram", id(ysbS))])
        else:
            yb, usf = ybp.next(), usfp.next()
            kb.dma("sp", yb[:, :, :], ybv[:, :, t0:t0 + 128], yb, r=[("dram", id(ysbS))], w=[yb])
            kb.dma("sp", usf[:, :, :], usv[:, :, t0:t0 + 128], usf, r=[("dram", id(usS))], w=[usf])
            y = yo.next()
            V("dve", lambda e: e.tensor_tensor(out=y[:, :, :], in0=pyv, in1=yb[:, :, :], op=ALU.add), [PY, yb], [y])
            V("pool", lambda e: e.tensor_tensor(out=usf[:, :, :], in0=usf[:, :, :],
                                                in1=dv[:, :].unsqueeze(2).broadcast_to([128, 4, 128]), op=ALU.mult),
              [usf, dv], [usf])
            V("pool", lambda e: e.tensor_tensor(out=y[:, :, :], in0=y[:, :, :], in1=usf[:, :, :], op=ALU.add), [y, usf], [y])
            g1 = yb
            V("pool", lambda e: e.tensor_tensor(out=g1[:, :, :], in0=y[:, :, :], in1=y[:, :, :], op=ALU.mult), [y], [g1])
            V("dve", lambda e: e.tensor_scalar(out=g1[:, :, :], in0=g1[:, :, :], scalar1=0.044715, scalar2=1.0,
                                               op0=ALU.mult, op1=ALU.add), [g1], [g1])
            V("dve", lambda e: e.tensor_tensor(out=g1[:, :, :], in0=g1[:, :, :], in1=y[:, :, :], op=ALU.mult), [g1, y], [g1])
            V("act", lambda e: e.activation(out=g1[:, :, :], in_=g1[:, :, :], func=AF.Sigmoid, scale=1.5957691216057308),
              [g1], [g1])
            V("dve", lambda e: e.tensor_tensor(out=y[:, :, :], in0=y[:, :, :], in1=g1[:, :, :], op=ALU.mult), [y, g1], [y])
            geb = us
            V("act", lambda e: e.activation(out=geb[:, :, :], in_=y[:, :, :], func=AF.Copy), [y], [geb])
            PG = PS2.next()
            for oc in range(4):
                for kc in range(4):
                    kb.op("pe", lambda e: e.matmul(PG[:, oc * 128:(oc + 1) * 128], lhsT=gw[:, kc, oc * 128:(oc + 1) * 128],
                                                   rhs=geb[:, kc, :], start=(kc == 0), stop=(kc == 3)), r=[gw, geb], w=[PG])
                V("act", lambda e: e.activation(out=usf[:, oc, :], in_=PG[:, oc * 128:(oc + 1) * 128], func=AF.Sigmoid,
                                                bias=gb[:, oc:oc + 1]), [PG, gb], [usf])
            yso = usp.next()
            V("dve", lambda e: e.tensor_tensor(out=yso[:, :, :], in0=y[:, :, :], in1=usf[:, :, :], op=ALU.mult), [y, usf], [yso])
            kb.dma("sp", ysv[:, :, t0:t0 + 128], yso[:, :, :], yso, r=[yso], w=[("dram", id(ysS))])


Model.s5_declare = _s5_declare
Model.sincos = _sincos
Model.s5_dir = _s5_dir


def host_inputs_s5(inp):
    L = DEPTH
    d = {}

    def drive(a):
        a = a.reshape(L, 2, 4, 8, 1, 64)
        a = np.broadcast_to(a, (L, 2, 4, 8, 16, 64))
        return np.ascontiguousarray(a.transpose(0, 1, 3, 4, 2, 5).reshape(L, 2, 128, 4, 64))
    d["s5_are"] = drive(inp["s5_a_re"])
    d["s5_aim"] = drive(inp["s5_a_im"])
    lsd = inp["s5_log_step"].reshape(L, 2, 4, 8, 1)
    d["s5_ls"] = np.ascontiguousarray(np.broadcast_to(lsd, (L, 2, 4, 8, 16)).transpose(0, 1, 3, 4, 2).reshape(L, 2, 128, 4))

    def bT(b):
        b = b.reshape(L, 4, 8, 64, 16)
        return np.ascontiguousarray(b.transpose(0, 2, 4, 1, 3).reshape(L, 128, 4, 64))
    d["s5_brT"] = bT(inp["s5_b_re"])
    d["s5_biT"] = bT(inp["s5_b_im"])

    def st2(a):
        a = a.reshape(L, 2, 16, 2, 64)
        return np.ascontiguousarray(a.transpose(0, 1, 3, 4, 2).reshape(L, 2, 128, 16))
    d["s5_are2"] = st2(inp["s5_a_re"])
    d["s5_aim2"] = st2(inp["s5_a_im"])
    ls2 = np.broadcast_to(inp["s5_log_step"].reshape(L, 2, 16, 2, 1), (L, 2, 16, 2, 64))
    d["s5_ls2"] = np.ascontiguousarray(ls2.transpose(0, 1, 3, 4, 2).reshape(L, 2, 128, 16))

    def cT(c):
        c = c.reshape(L, 16, 2, 16, 64)
        return np.ascontiguousarray(c.transpose(0, 2, 4, 1, 3).reshape(L, 128, 16, 16))
    d["s5_crT"] = cT(inp["s5_c_re"])
    d["s5_ciT"] = cT(inp["s5_c_im"])
    k = np.arange(128)[:, None, None] // 16
    j = np.arange(16)[None, :, None]
    gp = np.arange(2)[None, None, :]
    d["s5_rowmask"] = np.ascontiguousarray((k == 2 * (j % 4) + gp).astype(np.float32))
    d["s5_dT"] = np.ascontiguousarray(inp["s5_d"].reshape(L * 4, 128).T)
    d["s5_glub"] = np.ascontiguousarray(inp["s5_glu_b"].reshape(L * 4, 128).T)
    d["s5_gluw"] = inp["s5_glu_w"]
    d["tauT"] = np.ascontiguousarray(np.broadcast_to(np.arange(128, dtype=np.float32)[None, :], (128, 128)))
    return d


NCH_B = 15


def _rwkv_declare(self):
    L = DEPTH
    self.w_inB = self.din("w_inB", [L, D, NCH_B * 128])
    self.rk_mu = self.din("rk_mu", [128, L * NCH_B])
    self.rk_w0 = self.din("rk_w0", [128, L * 8])
    for n in ("a0", "kk", "ka", "rk", "gng", "gnb"):
        setattr(self, "rk_" + n, self.din("rk_" + n, [128, L * 4]))
    self.rk_w2 = self.din("rk_w2", [L, 128, 512])
    self.rk_a2 = self.din("rk_a2", [L, 64, 512])
    self.rk_g2 = self.din("rk_g2", [L, 128, 512])
    self.bd64 = self.din("bd64", [128, 128])
    self.identb = self.din("identb", [128, 128])
    self.ones0 = self.din("ones0", [2, 128, 128])
    self.rk_MT = self.din("rk_MT", [2, 128, 512])
    self.rk_MN = self.din("rk_MN", [2, 128, 512])


def _rwkv_prep_phase(self, l, src, S):
    kb = self.kb
    W = 256
    Wh = W + 2
    NB = W // 128
    with kb.phase() as st:
        self.psum = Pool(kb, "ps", 8, [128, 512], F32, psum=True, stack=st)
        V = lambda e_, f, r, w: kb.op(e_, f, r=r, w=w)
        sbf = lambda n, shp, dt=F32: kb.sb(n, shp, dt, st)
        w = sbf("wB", [128, NCH, NCH_B * 128], BF16)
        for c in range(NCH):
            kb.dma("pool", w[:, c, :], self.w_inB[l, c * 128:(c + 1) * 128, :], w, w=[w])
        w2b, a2b, g2b = sbf("w2b", [128, 512], BF16), sbf("a2b", [64, 512], BF16), sbf("g2b", [128, 512], BF16)
        kb.dma("pool", w2b[:, :], self.rk_w2[l], w2b, w=[w2b])
        kb.dma("pool", a2b[:, :], self.rk_a2[l], a2b, w=[a2b])
        kb.dma("pool", g2b[:, :], self.rk_g2[l], g2b, w=[g2b])
        bd, idb = sbf("bd", [128, 128], BF16), sbf("idb", [128, 128], BF16)
        kb.dma("pool", bd[:, :], self.bd64[:, :], bd, w=[bd])
        kb.dma("pool", idb[:, :], self.identb[:, :], idb, w=[idb])
        on0 = sbf("on0", [128, 2, 128])
        kb.dma("sp", on0[:, :, :], self.ones0.rearrange("d p t -> p d t"), on0, w=[on0])
        ON = [sbf("ON%d" % d_, [128, 4 * (W // 128), 128]) for d_ in range(2)]
        for d_ in range(2):
            V("dve", lambda e: e.tensor_copy(out=ON[d_][:, :, :], in_=on0[:, d_:d_ + 1, :].broadcast_to([128, 4 * (W // 128), 128])),
              [on0], [ON[d_]])
        mu, omu, hmu = sbf("mu", [128, NCH_B]), sbf("omu", [128, NCH_B]), sbf("hmu", [128, NCH_B])
        kb.dma("sp", mu[:, :], self.rk_mu[:, l * NCH_B:(l + 1) * NCH_B], mu, w=[mu])
        V("dve", lambda e: e.tensor_scalar(out=omu[:, :], in0=mu[:, :], scalar1=-1.0, scalar2=1.0, op0=ALU.mult, op1=ALU.add), [mu], [omu])
        V("dve", lambda e: e.tensor_scalar(out=hmu[:, :], in0=mu[:, :], scalar1=0.5, scalar2=None, op0=ALU.mult), [mu], [hmu])
        w0 = sbf("w0", [128, 8])
        kb.dma("sp", w0[:, :], self.rk_w0[:, l * 8:(l + 1) * 8], w0, w=[w0])
        pv = {}
        for n in ("a0", "kk", "ka", "rk"):
            pv[n] = sbf("p_" + n, [128, 4])
            kb.dma("sp", pv[n][:, :], getattr(self, "rk_" + n)[:, l * 4:(l + 1) * 4], pv[n], w=[pv[n]])
        omka = sbf("omka", [128, 4])
        V("dve", lambda e: e.tensor_scalar(out=omka[:, :], in0=pv["ka"][:, :], scalar1=-1.0, scalar2=1.0, op0=ALU.mult, op1=ALU.add),
          [pv["ka"]], [omka])
        pl = self.stat_pools(Wh, st)
        xp = Pool(kb, "x", 2, [128, NCH, Wh], F32, stack=st)
        xn = sbf("xn", [128, NCH, Wh])
        u = sbf("u", [128, NCH, Wh], BF16)
        zcp = Pool(kb, "zc", 2, [128, Wh], F32, stack=st)
        tmpp = Pool(kb, "ztmp", 2, [128, W], F32, stack=st)
        Z = sbf("Z", [128, NCH_B, W])
        tw, sg, alb = sbf("tw", [128, W], BF16), sbf("sg", [128, W], BF16), sbf("alb", [64, W], BF16)
        LW = [sbf("LW%d" % d_, [128, 4, W]) for d_ in range(2)]
        A, KK, KM, Bv = sbf("A", [128, 4, W]), sbf("KK", [128, 4, W]), sbf("KM", [128, 4, W]), sbf("Bv", [128, 4, W])
        SQ = sbf("SQ", [128, 4, W], BF16)
        T1, T2 = sbf("T1", [128, 4, W]), sbf("T2", [128, 4, W])
        Gp = Pool(kb, "Go", 2, [128, 4, W], BF16, stack=st)
        Bop = Pool(kb, "Bo", 2, [128, 4, W], BF16, stack=st)
        Vb = sbf("Vb", [128, 4, W], BF16)
        tokp = Pool(kb, "tok", 3, [128, NB, 512], BF16, stack=st)
        Lc, Lr, Lq = sbf("Lc", [128, 4, W]), sbf("Lr", [128, 4, W]), sbf("Lq", [128, 4, W])
        E = sbf("E", [128, 4, W])
        outp = {n: Pool(kb, n, 2, [128, 4, W], BF16, stack=st) for n in ("RHO", "KAP", "BET", "KTI")}
        scp = Pool(kb, "sco", 2, [128, NB, 4, 3], F32, stack=st)
        lmn = sbf("lmn", [128, 4, NB])
        srcv = src.rearrange("(c p) t -> p c t", p=128)
        fmv = lambda t_: t_.rearrange("(c p) t -> p c t", p=128)

        def transpose_store(srcb, dst, t0):
            tk = tokp.next()
            for tb in range(NB):
                P = self.psum.next()
                pb = P[:, 0:256].bitcast(BF16)
                for c in range(4):
                    kb.op("pe", lambda e: e.transpose(pb[:, c * 128:(c + 1) * 128], srcb[:, c, tb * 128:(tb + 1) * 128], idb[:, :]),
                          r=[srcb, idb], w=[P])
                V("act", lambda e: e.activation(out=tk[:, tb, :], in_=pb[:, 0:512], func=AF.Copy), [P], [tk])
            kb.dma("sp", dst[t0:t0 + W, :].rearrange("(b p) n -> p b n", p=128), tk[:, :, :], tk, r=[tk], w=[("dram", id(dst))])

        for (seg, t0, _) in self.tiles(W):
            seg_lo, seg_hi = (0, CTX) if seg == 1 else (CTX, self.NT)
            lo, hi = max(t0 - 1, seg_lo), min(t0 + W + 1, seg_hi)
            x = xp.next()
            c0 = lo - (t0 - 1)
            if c0 > 0:
                V("dve", lambda e: e.memset(x[:, :, 0:1], 0.0), [], [x])
            if hi < t0 + W + 1:
                V("dve", lambda e: e.memset(x[:, :, Wh - 1:Wh], 0.0), [], [x])
            kb.dma("sp", x[:, :, c0:c0 + (hi - lo)], srcv[:, :, lo:hi], x, r=[("dram", id(src))], w=[x])
            rstd, nmr = self.ln_stats_w(x, Wh, pl)
            self.normalize(xn, x, Wh, rstd, nmr)
            for c in range(NCH):
                V("act", lambda e: e.activation(out=u[:, c, :], in_=xn[:, c, :], func=AF.Identity,
                                                scale=self.modp1[:, 4 * NCH + c, seg:seg + 1],
                                                bias=self.mods[:, 3 * NCH + c, seg:seg + 1]), [xn, self.modp1, self.mods], [u])
            if c0 > 0:
                V("dve", lambda e: e.memset(u[:, :, 0:1], 0.0), [], [u])
            if hi < t0 + W + 1:
                V("dve", lambda e: e.memset(u[:, :, Wh - 1:Wh], 0.0), [], [u])
            for ch in range(NCH_B):
                P = self.psum.next()
                for c in range(NCH):
                    kb.op("pe", lambda e: e.matmul(P[:, 0:Wh], lhsT=w[:, c, ch * 128:(ch + 1) * 128], rhs=u[:, c, :],
                                                   start=(c == 0), stop=(c == NCH - 1)), r=[w, u], w=[P])
                zc, tm = zcp.next(), tmpp.next()
                V("act", lambda e: e.activation(out=zc[:, :], in_=P[:, 0:Wh], func=AF.Copy), [P], [zc])
                V("pool", lambda e: e.tensor_tensor(out=tm[:, :], in0=zc[:, 0:W], in1=zc[:, 2:W + 2], op=ALU.add), [zc], [tm])
                V("act", lambda e: e.activation(out=tm[:, :], in_=tm[:, :], func=AF.Identity, scale=hmu[:, ch:ch + 1]), [tm, hmu], [tm])
                V("dve", lambda e: e.scalar_tensor_tensor(out=Z[:, ch, :], in0=zc[:, 1:W + 1], scalar=omu[:, ch:ch + 1],
                                                          in1=tm[:, :], op0=ALU.mult, op1=ALU.add), [zc, omu, tm], [Z])
            R_, K_, V_ = Z[:, 0:4, :], Z[:, 4:8, :], Z[:, 8:12, :]
            V("act", lambda e: e.activation(out=tw[:, :], in_=Z[:, 12, :], func=AF.Tanh), [Z], [tw])
            V("act", lambda e: e.activation(out=sg[:, :], in_=Z[:, 13, :], func=AF.Sigmoid), [Z], [sg])
            V("act", lambda e: e.activation(out=alb[:, :], in_=Z[0:64, 14, :], func=AF.Copy), [Z], [alb])
            for d_ in range(2):
                for c in range(4):
                    P = self.psum.next()
                    kb.op("pe", lambda e: e.matmul(P[:, 0:W], lhsT=w2b[d_ * 64:(d_ + 1) * 64, c * 128:(c + 1) * 128],
                                                   rhs=tw[d_ * 64:(d_ + 1) * 64, :], start=True, stop=True), r=[w2b, tw], w=[P])
                    V("act", lambda e: e.activation(out=LW[d_][:, c, :], in_=P[:, 0:W], func=AF.Sigmoid,
                                                    bias=w0[:, d_ * 4 + c:d_ * 4 + c + 1]), [P, w0], [LW[d_]])
                V("pool", lambda e: e.tensor_scalar(out=LW[d_][:, :, :], in0=LW[d_][:, :, :], scalar1=-DECAY_SCALE, scalar2=None,
                                                    op0=ALU.mult), [LW[d_]], [LW[d_]])
            Go = Gp.next()
            for c in range(4):
                P = self.psum.next()
                kb.op("pe", lambda e: e.matmul(P[:, 0:W], lhsT=a2b[0:64, c * 128:(c + 1) * 128], rhs=alb[0:64, :],
                                               start=True, stop=True), r=[a2b, alb], w=[P])
                V("act", lambda e: e.activation(out=A[:, c, :], in_=P[:, 0:W], func=AF.Sigmoid, bias=pv["a0"][:, c:c + 1]),
                  [P, pv["a0"]], [A])
                P = self.psum.next()
                kb.op("pe", lambda e: e.matmul(P[:, 0:W], lhsT=g2b[:, c * 128:(c + 1) * 128], rhs=sg[:, :],
                                               start=True, stop=True), r=[g2b, sg], w=[P])
                V("act", lambda e: e.activation(out=Go[:, c, :], in_=P[:, 0:W], func=AF.Copy), [P], [Go])
            kb.dma("sp", fmv(S["gR"])[:, :, t0:t0 + W], Go[:, :, :], Go, r=[Go], w=[("dram", id(S["gR"]))])
            for c in range(4):
                V("pool", lambda e: e.tensor_scalar(out=KK[:, c, :], in0=Z[:, 4 + c, :], scalar1=pv["kk"][:, c:c + 1], scalar2=None,
                                                    op0=ALU.mult), [Z, pv["kk"]], [KK])
            V("act", lambda e: e.activation(out=SQ[:, :, :], in_=KK[:, :, :], func=AF.Square), [KK], [SQ])
            for c in range(4):
                P = self.psum.next()
                kb.op("pe", lambda e: e.matmul(P[:, 0:W], lhsT=bd[:, :], rhs=SQ[:, c, :], start=True, stop=True), r=[bd, SQ], w=[P])
                V("dve", lambda e: e.tensor_scalar(out=T1[:, c, :], in0=P[:, 0:W], scalar1=1e-12, scalar2=None, op0=ALU.add), [P], [T1])
            V("act", lambda e: e.activation(out=T1[:, :, :], in_=T1[:, :, :], func=AF.Sqrt), [T1], [T1])
            V("dve", lambda e: e.reciprocal(out=T1[:, :, :], in_=T1[:, :, :]), [T1], [T1])
            V("dve", lambda e: e.tensor_tensor(out=KK[:, :, :], in0=KK[:, :, :], in1=T1[:, :, :], op=ALU.mult), [KK, T1], [KK])
            for c in range(4):
                V("dve", lambda e: e.tensor_scalar(out=T2[:, c, :], in0=A[:, c, :], scalar1=pv["ka"][:, c:c + 1],
                                                   scalar2=omka[:, c:c + 1], op0=ALU.mult, op1=ALU.add), [A, pv["ka"], omka], [T2])
            V("pool", lambda e: e.tensor_tensor(out=KM[:, :, :], in0=K_, in1=T2[:, :, :], op=ALU.mult), [Z, T2], [KM])
            V("pool", lambda e: e.tensor_tensor(out=Bv[:, :, :], in0=KK[:, :, :], in1=A[:, :, :], op=ALU.mult), [KK, A], [Bv])
            V("dve", lambda e: e.tensor_tensor(out=T1[:, :, :], in0=R_, in1=KM[:, :, :], op=ALU.mult), [Z, KM], [T1])
            for c in range(4):
                V("act", lambda e: e.activation(out=SQ[:, c, :], in_=T1[:, c, :], func=AF.Identity, scale=pv["rk"][:, c:c + 1]),
                  [T1, pv["rk"]], [SQ])
            Bo = Bop.next()
            for c in range(4):
                P = self.psum.next()
                kb.op("pe", lambda e: e.matmul(P[:, 0:W], lhsT=bd[:, :], rhs=SQ[:, c, :], start=True, stop=True), r=[bd, SQ], w=[P])
                V("dve", lambda e: e.tensor_tensor(out=Bo[:, c, :], in0=P[:, 0:W], in1=Z[:, 8 + c, :], op=ALU.mult), [P, Z], [Bo])
            kb.dma("sp", fmv(S["bon"])[:, :, t0:t0 + W], Bo[:, :, :], Bo, r=[Bo], w=[("dram", id(S["bon"]))])
            V("act", lambda e: e.activation(out=Vb[:, :, :], in_=V_, func=AF.Copy), [Z], [Vb])
            transpose_store(Vb, S["vt"], t0)
            for d_ in range(2):
                rev = d_ == 1
                fl = (lambda ap: ap.rearrange("p a b -> p (a b)")[:, ::-1]) if rev else (lambda ap: ap.rearrange("p a b -> p (a b)"))
                V("dve", lambda e: e.tensor_tensor_scan(out=fl(Lc[:, :, :]), data0=fl(ON[d_][:, :, :]), data1=fl(LW[d_][:, :, :]),
                                                        initial=0.0, op0=ALU.mult, op1=ALU.add), [ON[d_], LW[d_]], [Lc])
                mid, last = (64, 0) if rev else (63, 127)
                L4 = Lc[:, :, :].rearrange("p c (b t) -> p c b t", b=NB)
                sc = scp.next()
                V("dve", lambda e: e.tensor_copy(out=lmn[:, :, :], in_=L4[:, :, :, mid]), [Lc], [lmn])
                scv = sc[:, :, :, :].rearrange("p b c s -> p c b s")
                V("act", lambda e: e.activation(out=scv[:, :, :, 0], in_=lmn[:, :, :], func=AF.Exp), [lmn], [sc])
                V("act", lambda e: e.activation(out=scv[:, :, :, 2], in_=L4[:, :, :, last], func=AF.Exp), [Lc], [sc])
                V("dve", lambda e: e.tensor_tensor(out=scv[:, :, :, 1], in0=L4[:, :, :, last], in1=lmn[:, :, :], op=ALU.subtract),
                  [Lc, lmn], [sc])
                V("act", lambda e: e.activation(out=scv[:, :, :, 1], in_=scv[:, :, :, 1], func=AF.Exp), [sc], [sc])
                kb.dma("sp", S["sc"][d_][:, t0 // 128:t0 // 128 + NB, :, :], sc[:, :, :, :], sc, r=[sc], w=[("dram", id(S["sc"][d_]))])
                Lr4 = Lr[:, :, :].rearrange("p c (b t) -> p c b t", b=NB)
                V("pool", lambda e: e.tensor_tensor(out=Lr4, in0=L4, in1=lmn[:, :, :].unsqueeze(3).broadcast_to([128, 4, NB, 128]),
                                                    op=ALU.subtract), [Lc, lmn], [Lr])
                V("pool", lambda e: e.tensor_tensor(out=Lq[:, :, :], in0=Lr[:, :, :], in1=LW[d_][:, :, :], op=ALU.subtract),
                  [Lr, LW[d_]], [Lq])
                o = {n: outp[n].next() for n in outp}
                V("act", lambda e: e.activation(out=E[:, :, :], in_=Lr[:, :, :], func=AF.Exp), [Lr], [E])
                V("dve", lambda e: e.tensor_tensor(out=o["RHO"][:, :, :], in0=R_, in1=E[:, :, :], op=ALU.mult), [Z, E], [o["RHO"]])
                V("act", lambda e: e.activation(out=E[:, :, :], in_=Lq[:, :, :], func=AF.Exp), [Lq], [E])
                V("dve", lambda e: e.tensor_tensor(out=o["KAP"][:, :, :], in0=KK[:, :, :], in1=E[:, :, :], op=ALU.mult), [KK, E], [o["KAP"]])
                V("act", lambda e: e.activation(out=E[:, :, :], in_=Lr[:, :, :], func=AF.Exp, scale=-1.0), [Lr], [E])
                V("dve", lambda e: e.tensor_tensor(out=o["BET"][:, :, :], in0=Bv[:, :, :], in1=E[:, :, :], op=ALU.mult), [Bv, E], [o["BET"]])
                V("pool", lambda e: e.tensor_tensor(out=o["KTI"][:, :, :], in0=KM[:, :, :], in1=E[:, :, :], op=ALU.mult), [KM, E], [o["KTI"]])
                for n in ("RHO", "KAP", "BET", "KTI"):
                    kb.dma("sp", fmv(S[n][d_])[:, :, t0:t0 + W], o[n][:, :, :], o[n], r=[o[n]], w=[("dram", id(S[n][d_]))])
                transpose_store(o["BET"], S["bt"][d_], t0)
                transpose_store(o["KTI"], S["kt"][d_], t0)


def _ln_stats_w(self, x, Wc, pl):
    kb = self.kb
    xb, sq = pl["xb"].next(), pl["sq"].next()
    kb.op("act", lambda e: e.activation(out=xb[:, :, :Wc], in_=x[:, :, :Wc], func=AF.Copy), r=[x], w=[xb])
    kb.op("act", lambda e: e.activation(out=sq[:, :, :Wc], in_=x[:, :, :Wc], func=AF.Square), r=[x], w=[sq])
    P1, P2 = self.psum.next(), self.psum.next()
    for c in range(NCH):
        kb.op("pe", lambda e: e.matmul(P1[:, 0:Wc], lhsT=self.onesb[:, :], rhs=xb[:, c, :Wc],
                                       start=(c == 0), stop=(c == NCH - 1)), r=[xb, self.onesb], w=[P1])
    for c in range(NCH):
        kb.op("pe", lambda e: e.matmul(P2[:, 0:Wc], lhsT=self.onesb[:, :], rhs=sq[:, c, :Wc],
                                       start=(c == 0), stop=(c == NCH - 1)), r=[sq, self.onesb], w=[P2])
    m2, var, rstd, nmr = pl["m2"].next(), pl["var"].next(), pl["rstd"].next(), pl["nmr"].next()
    kb.op("act", lambda e: e.activation(out=m2[:, 0, :Wc], in_=P1[:, 0:Wc], func=AF.Square), r=[P1], w=[m2])
    kb.op("dve", lambda e: e.scalar_tensor_tensor(out=var[:, 0, :Wc], in0=P2[:, 0:Wc], scalar=LN_EPS,
                                                  in1=m2[:, 0, :Wc], op0=ALU.add, op1=ALU.subtract), r=[P2, m2], w=[var])
    kb.op("act", lambda e: e.activation(out=var[:, 0, :Wc], in_=var[:, 0, :Wc], func=AF.Sqrt), r=[var], w=[var])
    kb.op("dve", lambda e: e.reciprocal(out=rstd[:, 0, :Wc], in_=var[:, 0, :Wc]), r=[var], w=[rstd])
    kb.op("dve", lambda e: e.scalar_tensor_tensor(out=nmr[:, 0, :Wc], in0=P1[:, 0:Wc], scalar=-1.0,
                                                  in1=rstd[:, 0, :Wc], op0=ALU.mult, op1=ALU.mult), r=[P1, rstd], w=[nmr])
    return rstd, nmr


Model.rwkv_declare = _rwkv_declare
Model.rwkv_prep_phase = _rwkv_prep_phase
Model.ln_stats_w = _ln_stats_w


def _rwkv_scan_dir(self, l, d, S, st, psr):
    kb = self.kb
    NT = self.NT
    nchunk = NT // 128
    fmv = lambda t_: t_.rearrange("(c p) t -> p c t", p=128)
    V = lambda e_, f, r, w: kb.op(e_, f, r=r, w=w)
    sbf = lambda n, shp, dt=F32: kb.sb(n, shp, dt, st)
    MT, MN = sbf("MT", [128, 512], BF16), sbf("MN", [128, 512], BF16)
    kb.dma("pool", MT[:, :], self.rk_MT[d], MT, w=[MT])
    kb.dma("pool", MN[:, :], self.rk_MN[d], MN, w=[MN])
    KRp = Pool(kb, "KR", 2, [128, 4, 2, 128], BF16, stack=st)
    BTZp = Pool(kb, "BTZ", 2, [128, 4, 2, 128], BF16, stack=st)
    KTZp = Pool(kb, "KTZ", 2, [128, 4, 2, 128], BF16, stack=st)
    KAZp = Pool(kb, "KAZ", 2, [128, 4, 2, 128], BF16, stack=st)
    for p_ in (BTZp, KTZp, KAZp):
        for b_ in p_.bufs:
            V("pool", lambda e: e.memset(b_[:, :, :, :], 0.0), [], [b_])
    S0Z = sbf("S0Z", [128, 4, 2, 64], BF16)
    V("pool", lambda e: e.memset(S0Z[:, :, :, :], 0.0), [], [S0Z])
    St = sbf("St", [128, 4, 64])
    V("pool", lambda e: e.memset(St[:, :, :], 0.0), [], [St])
    St1 = sbf("St1", [128, 4, 64])
    tokp = {n: Pool(kb, n, 2, [128, 512], BF16, stack=st) for n in ("Btok", "Ktok", "Vtok")}
    scp = Pool(kb, "sc", 2, [128, 4, 3], F32, stack=st)
    AMp = Pool(kb, "AM", 2, [128, 8, 512], BF16, stack=st)
    Pm = [Pool(kb, "Pm%d" % i, 2, [128, 8, 128], BF16, stack=st) for i in range(2)]
    PTm = Pool(kb, "PTm", 2, [128, 8, 128], BF16, stack=st)
    X32, X16p = sbf("X32", [128, 512]), Pool(kb, "X16", 2, [128, 512], BF16, stack=st)
    Oop = Pool(kb, "Oo", 2, [128, 512], F32, stack=st)
    order = [0, 1] + list(range(2, nchunk))
    if d == 1:
        order = [1, 0] + list(range(nchunk - 1, 1, -1))
    ev = 0
    yield
    for ci_, ch in enumerate(order):
        if ci_ > 0:
            yield
        t0 = ch * 128
        KR, BTZ, KTZ, KAZ = KRp.next(), BTZp.next(), KTZp.next(), KAZp.next()
        kb.dma("sp", KR[:, :, 0, :], fmv(S["KAP"][d])[:, :, t0:t0 + 128], KR, r=[("dram", id(S["KAP"][d]))], w=[KR])
        kb.dma("sp", KR[:, :, 1, :], fmv(S["RHO"][d])[:, :, t0:t0 + 128], KR, r=[("dram", id(S["RHO"][d]))], w=[KR])
        for par in range(2):
            ps_ = slice(par * 64, (par + 1) * 64)
            kb.dma("sp", BTZ[ps_, :, par, :], fmv(S["BET"][d])[ps_, :, t0:t0 + 128], BTZ, r=[("dram", id(S["BET"][d]))], w=[BTZ])
            kb.dma("sp", KTZ[ps_, :, par, :], fmv(S["KTI"][d])[ps_, :, t0:t0 + 128], KTZ, r=[("dram", id(S["KTI"][d]))], w=[KTZ])
            kb.dma("sp", KAZ[ps_, :, par, :], fmv(S["KAP"][d])[ps_, :, t0:t0 + 128], KAZ, r=[("dram", id(S["KAP"][d]))], w=[KAZ])
        tk = {}
        for n, key in (("Btok", "bt"), ("Ktok", "kt"), ("Vtok", "vt")):
            tk[n] = tokp[n].next()
            srcd = S[key][d] if key != "vt" else S[key]
            kb.dma("sp", tk[n][:, :], srcd[t0:t0 + 128, :], tk[n], r=[("dram", id(srcd))], w=[tk[n]])
        Btok, Ktok, Vtok = tk["Btok"], tk["Ktok"], tk["Vtok"]
        sc = scp.next()
        kb.dma("sp", sc[:, :, :], S["sc"][d][:, ch, :, :], sc, r=[("dram", id(S["sc"][d]))], w=[sc])
        for par in range(2):
            ps_ = slice(par * 64, (par + 1) * 64)
            V("pool", lambda e: e.tensor_tensor(out=S0Z[ps_, :, par, :], in0=St[ps_, :, :],
                                                in1=sc[ps_, :, 0:1].broadcast_to([64, 4, 64]), op=ALU.mult), [St, sc], [S0Z])
        AM = AMp.next()
        for h in range(8):
            c, par = h // 2, h % 2
            P = psr.next()
            rhs = KR[:, c, :, :].rearrange("p s t -> p (s t)")
            kb.op("pe", lambda e: e.matmul(P[:, 0:256], lhsT=BTZ[:, c, par, :], rhs=rhs, start=True, stop=True), r=[BTZ, KR], w=[P])
            kb.op("pe", lambda e: e.matmul(P[:, 256:512], lhsT=KTZ[:, c, par, :], rhs=rhs, start=True, stop=True), r=[KTZ, KR], w=[P])
            V("dve", lambda e: e.tensor_tensor(out=AM[:, h, :], in0=P[:, :], in1=MT[:, :], op=ALU.mult), [P, MT], [AM])
            if h == 3:
                yield
        yield
        Pj, PTj = Pm[0].next(), PTm.next()
        for g4 in range(2):
            P = psr.next()
            for hh in range(4):
                h = g4 * 4 + hh
                c, par = h // 2, h % 2
                kb.op("pe", lambda e: e.matmul(P[:, hh * 128:(hh + 1) * 128], lhsT=KAZ[:, c, par, :], rhs=BTZ[:, c, par, :],
                                               start=True, stop=True), r=[KAZ, BTZ], w=[P])
            V("dve", lambda e: e.tensor_tensor(out=Pj[:, g4 * 4:(g4 + 1) * 4, :].rearrange("p h t -> p (h t)"), in0=P[:, :],
                                               in1=MN[:, :], op=ALU.mult), [P, MN], [Pj])
        V("act", lambda e: e.activation(out=PTj[:, :, :], in_=AM[:, :, 0:128], func=AF.Copy), [AM], [PTj])
        P = psr.next()
        for h in range(8):
            c, par = h // 2, h % 2
            kb.op("pe", lambda e: e.matmul(P[:, h * 64:(h + 1) * 64], lhsT=KR[:, c, 0, :], rhs=S0Z[:, c, par, :],
                                           start=True, stop=False), r=[KR, S0Z], w=[P])
            kb.op("pe", lambda e: e.matmul(P[:, h * 64:(h + 1) * 64], lhsT=AM[:, h, 256:384], rhs=Vtok[:, h * 64:(h + 1) * 64],
                                           start=False, stop=True), r=[AM, Vtok], w=[P])
        V("act", lambda e: e.activation(out=X32[:, :], in_=P[:, :], func=AF.Identity, scale=-1.0), [P], [X32])
        X16 = X16p.next()
        V("dve", lambda e: e.tensor_copy(out=X16[:, :], in_=X32[:, :]), [X32], [X16])
        yield
        for j in range(7):
            P = psr.next()
            for h in range(8):
                kb.op("pe", lambda e: e.matmul(P[:, h * 64:(h + 1) * 64], lhsT=PTj[:, h, :], rhs=X16[:, h * 64:(h + 1) * 64],
                                               start=True, stop=True), r=[PTj, X16], w=[P])
            V("dve", lambda e: e.tensor_tensor(out=X32[:, :], in0=P[:, :], in1=X32[:, :], op=ALU.add), [P, X32], [X32])
            X16 = X16p.next()
            V("act", lambda e: e.activation(out=X16[:, :], in_=X32[:, :], func=AF.Copy), [X32], [X16])
            if j < 6:
                PTn = PTm.next()
                Pn = Pm[(j + 1) % 2].next() if j < 5 else None
                for g4 in range(2):
                    P = psr.next()
                    for hh in range(4):
                        h = g4 * 4 + hh
                        kb.op("pe", lambda e: e.matmul(P[:, hh * 128:(hh + 1) * 128], lhsT=Pj[:, h, :], rhs=PTj[:, h, :],
                                                       start=True, stop=True), r=[Pj, PTj], w=[P])
                    eng = "act" if (ev % 2 == 0) else "dve"
                    ev += 1
                    dst = PTn[:, g4 * 4:(g4 + 1) * 4, :].rearrange("p h t -> p (h t)")
                    if eng == "act":
                        V("act", lambda e: e.activation(out=dst, in_=P[:, :], func=AF.Copy), [P], [PTn])
                    else:
                        V("dve", lambda e: e.tensor_copy(out=dst, in_=P[:, :]), [P], [PTn])
                    if Pn is not None:
                        P = psr.next()
                        for hh in range(4):
                            h = g4 * 4 + hh
                            kb.op("pe", lambda e: e.matmul(P[:, hh * 128:(hh + 1) * 128], lhsT=PTj[:, h, :], rhs=Pj[:, h, :],
                                                           start=True, stop=True), r=[Pj, PTj], w=[P])
                        eng = "act" if (ev % 2 == 0) else "dve"
                        ev += 1
                        dst = Pn[:, g4 * 4:(g4 + 1) * 4, :].rearrange("p h t -> p (h t)")
                        if eng == "act":
                            V("act", lambda e: e.activation(out=dst, in_=P[:, :], func=AF.Copy), [P], [Pn])
                        else:
                            V("dve", lambda e: e.tensor_copy(out=dst, in_=P[:, :]), [P], [Pn])
                PTj = PTn
                if Pn is not None:
                    Pj = Pn
            if j < 6:
                yield
        U16 = X16
        P = psr.next()
        for h in range(8):
            c, par = h // 2, h % 2
            hs = slice(h * 64, (h + 1) * 64)
            kb.op("pe", lambda e: e.matmul(P[:, hs], lhsT=KR[:, c, 1, :], rhs=S0Z[:, c, par, :], start=True, stop=False),
                  r=[KR, S0Z], w=[P])
            kb.op("pe", lambda e: e.matmul(P[:, hs], lhsT=AM[:, h, 128:256], rhs=U16[:, hs], start=False, stop=False),
                  r=[AM, U16], w=[P])
            kb.op("pe", lambda e: e.matmul(P[:, hs], lhsT=AM[:, h, 384:512], rhs=Vtok[:, hs], start=False, stop=True),
                  r=[AM, Vtok], w=[P])
        Oo = Oop.next()
        V("act", lambda e: e.activation(out=Oo[:, :], in_=P[:, :], func=AF.Copy), [P], [Oo])
        kb.dma("act", S["O"][d][t0:t0 + 128, :], Oo[:, :], Oo, r=[Oo], w=[("dram", id(S["O"][d]))])
        yield
        P = psr.next()
        for c in range(4):
            cs_ = slice(c * 128, (c + 1) * 128)
            kb.op("pe", lambda e: e.matmul(P[:, cs_], lhsT=Btok[:, cs_], rhs=U16[:, cs_], start=True, stop=False),
                  r=[Btok, U16], w=[P])
            kb.op("pe", lambda e: e.matmul(P[:, cs_], lhsT=Ktok[:, cs_], rhs=Vtok[:, cs_], start=False, stop=True),
                  r=[Ktok, Vtok], w=[P])
        V("pool", lambda e: e.tensor_tensor(out=St1[:, :, :], in0=St[:, :, :], in1=sc[:, :, 2:3].broadcast_to([128, 4, 64]),
                                            op=ALU.mult), [St, sc], [St1])
        pv_ = P[:, :].rearrange("p (c x) -> p c x", c=4)
        for par in range(2):
            ps_ = slice(par * 64, (par + 1) * 64)
            V("dve", lambda e: e.tensor_tensor(out=St[ps_, :, :], in0=pv_[ps_, :, par * 64:(par + 1) * 64],
                                               in1=sc[ps_, :, 1:2].broadcast_to([64, 4, 64]), op=ALU.mult), [P, sc, St1], [St])
        V("pool", lambda e: e.tensor_tensor(out=St[:, :, :], in0=St[:, :, :], in1=St1[:, :, :], op=ALU.add), [St, St1], [St])


def _merge_declare(self):
    L = DEPTH
    self.branch_proj = self.din("branch_proj", [L, 3, 512, D])
    self.w_out = self.din("w_out", [L, D, D])


def _merge_phase(self, l, src, dst, S, ctx):
    kb = self.kb
    W = 256
    NB = W // 128
    lni = (l * 3 + 1) * NCH
    with kb.phase() as st:
        self.psum = Pool(kb, "ps", 8, [128, 512], F32, psum=True, stack=st)
        V = lambda e_, f, r, w: kb.op(e_, f, r=r, w=w)
        sbf = lambda n, shp, dt=F32: kb.sb(n, shp, dt, st)
        bp = sbf("bp", [128, 12, D], BF16)
        wo = sbf("wo", [128, NCH, D], BF16)
        for b_ in range(3):
            for c in range(4):
                kb.dma("pool", bp[:, b_ * 4 + c, :], self.branch_proj[l, b_, c * 128:(c + 1) * 128, :], bp, w=[bp])
        for c in range(NCH):
            kb.dma("pool", wo[:, c, :], self.w_out[l, c * 128:(c + 1) * 128, :], wo, w=[wo])
        idb = sbf("idb", [128, 128], BF16)
        kb.dma("pool", idb[:, :], self.identb[:, :], idb, w=[idb])
        gng, gnb = sbf("gng", [128, 4]), sbf("gnb", [128, 4])
        kb.dma("sp", gng[:, :], self.rk_gng[:, l * 4:(l + 1) * 4], gng, w=[gng])
        kb.dma("sp", gnb[:, :], self.rk_gnb[:, l * 4:(l + 1) * 4], gnb, w=[gnb])
        pl = self.stat_pools(W, st)
        xp = Pool(kb, "x", 2, [128, NCH, W], F32, stack=st)
        xn = sbf("xn", [128, NCH, W])
        Ofp = Pool(kb, "Of", 2, [128, NB, 512], F32, stack=st)
        Obp = Pool(kb, "Ob", 2, [128, NB, 512], F32, stack=st)
        onb = sbf("onb", [128, NB, 512], BF16)
        st8 = [sbf("st8_%d" % i, [128, NB, 8]) for i in range(3)]
        sqt = sbf("sqt", [128, NB, 512])
        Y = {n: Pool(kb, "y" + n, 2, [128, 4, W], BF16, stack=st) for n in ("a", "s", "bon", "g")}
        yr = sbf("yr", [128, 4, W], BF16)
        yt = sbf("yrt", [128, 4, W])
        gp = Pool(kb, "gates", 2, [128, 24, W], BF16, stack=st)
        m1, m2, m3 = sbf("m1", [128, W]), sbf("m2", [128, W]), sbf("m3", [128, W])
        mT = sbf("mT", [128, NCH, W], BF16)
        srcv = src.rearrange("(c p) t -> p c t", p=128)
        dstv = dst.rearrange("(c p) t -> p c t", p=128)
        fmv = lambda t_: t_.rearrange("(c p) t -> p c t", p=128)
        for (seg, t0, _) in self.tiles(W, ctx=ctx):
            x = xp.next()
            kb.dma("sp", x[:, :, :], srcv[:, :, t0:t0 + W], x, r=[("dram", id(src))], w=[x])
            Of, Ob = Ofp.next(), Obp.next()
            kb.dma("sp", Of[:, :, :], S["O"][0][t0:t0 + W, :].rearrange("(b p) n -> p b n", p=128), Of, r=[("dram", id(S["O"][0]))], w=[Of])
            kb.dma("sp", Ob[:, :, :], S["O"][1][t0:t0 + W, :].rearrange("(b p) n -> p b n", p=128), Ob, r=[("dram", id(S["O"][1]))], w=[Ob])
            ld = {}
            for n, key in (("a", "ya"), ("s", "ys"), ("bon", "bon"), ("g", "gR")):
                ld[n] = Y[n].next()
                kb.dma("sp", ld[n][:, :, :], fmv(S[key])[:, :, t0:t0 + W], ld[n], r=[("dram", id(S[key]))], w=[ld[n]])
            gt = gp.next()
            kb.dma("sp", gt[:, :, :], fmv(S["gS"])[:, :, t0:t0 + W], gt, r=[("dram", id(S["gS"]))], w=[gt])
            V("dve", lambda e: e.tensor_tensor(out=Of[:, :, :], in0=Of[:, :, :], in1=Ob[:, :, :], op=ALU.add), [Of, Ob], [Of])
            O4 = Of[:, :, :].rearrange("p b (h v) -> p b h v", h=8)
            sm, vr, rs = st8
            V("dve", lambda e: e.tensor_reduce(out=sm[:, :, :], in_=O4, axis=AX.X, op=ALU.add), [Of], [sm])
            V("dve", lambda e: e.tensor_scalar(out=sm[:, :, :], in0=sm[:, :, :], scalar1=1.0 / 64, scalar2=None, op0=ALU.mult), [sm], [sm])
            V("dve", lambda e: e.tensor_tensor(out=O4, in0=O4, in1=sm[:, :, :].unsqueeze(3).broadcast_to([128, NB, 8, 64]),
                                               op=ALU.subtract), [Of, sm], [Of])
            V("act", lambda e: e.activation(out=sqt[:, :, :], in_=Of[:, :, :], func=AF.Square), [Of], [sqt])
            V("dve", lambda e: e.tensor_reduce(out=vr[:, :, :], in_=sqt[:, :, :].rearrange("p b (h v) -> p b h v", h=8), axis=AX.X,
                                               op=ALU.add), [sqt], [vr])
            V("dve", lambda e: e.tensor_scalar(out=vr[:, :, :], in0=vr[:, :, :], scalar1=1.0 / 64, scalar2=GN_EPS, op0=ALU.mult,
                                               op1=ALU.add), [vr], [vr])
            V("act", lambda e: e.activation(out=vr[:, :, :], in_=vr[:, :, :], func=AF.Sqrt), [vr], [vr])
            V("dve", lambda e: e.reciprocal(out=rs[:, :, :], in_=vr[:, :, :]), [vr], [rs])
            V("dve", lambda e: e.tensor_tensor(out=onb[:, :, :].rearrange("p b (h v) -> p b h v", h=8), in0=O4,
                                               in1=rs[:, :, :].unsqueeze(3).broadcast_to([128, NB, 8, 64]), op=ALU.mult), [Of, rs], [onb])
            for tb in range(NB):
                P = self.psum.next()
                pb = P[:, 0:256].bitcast(BF16)
                for c in range(4):
                    kb.op("pe", lambda e: e.transpose(pb[:, c * 128:(c + 1) * 128], onb[:, tb, c * 128:(c + 1) * 128], idb[:, :]),
                          r=[onb, idb], w=[P])
                for c in range(4):
                    V("act", lambda e: e.activation(out=yt[:, c, tb * 128:(tb + 1) * 128], in_=pb[:, c * 128:(c + 1) * 128],
                                                    func=AF.Identity, scale=gng[:, c:c + 1], bias=gnb[:, c:c + 1]), [P, gng, gnb], [yt])
            V("pool", lambda e: e.tensor_tensor(out=yt[:, :, :], in0=yt[:, :, :], in1=ld["bon"][:, :, :], op=ALU.add), [yt, ld["bon"]], [yt])
            V("dve", lambda e: e.tensor_tensor(out=yr[:, :, :], in0=yt[:, :, :], in1=ld["g"][:, :, :], op=ALU.mult), [yt, ld["g"]], [yr])
            if "yrS" in S:
                kb.dma("sp", fmv(S["yrS"])[:, :, t0:t0 + W], yr[:, :, :], yr, r=[yr], w=[("dram", id(S["yrS"]))])
            ysrc = [ld["a"], yr, ld["s"]]
            for oc in range(NCH):
                Pa, Pb = self.psum.next(), self.psum.next()
                tgt = [(Pa, 0), (Pa, W), (Pb, 0)]
                for b_ in range(3):
                    Pt, o0 = tgt[b_]
                    for kc in range(4):
                        kb.op("pe", lambda e: e.matmul(Pt[:, o0:o0 + W], lhsT=bp[:, b_ * 4 + kc, oc * 128:(oc + 1) * 128],
                                                       rhs=ysrc[b_][:, kc, :], start=(kc == 0), stop=(kc == 3)), r=[bp, ysrc[b_]], w=[Pt])
                V("dve", lambda e: e.tensor_tensor(out=m1[:, :], in0=Pa[:, 0:W], in1=gt[:, oc, :], op=ALU.mult), [Pa, gt], [m1])
                V("dve", lambda e: e.tensor_tensor(out=m2[:, :], in0=Pa[:, W:2 * W], in1=gt[:, 8 + oc, :], op=ALU.mult), [Pa, gt], [m2])
                V("dve", lambda e: e.tensor_tensor(out=m3[:, :], in0=Pb[:, 0:W], in1=gt[:, 16 + oc, :], op=ALU.mult), [Pb, gt], [m3])
                V("pool", lambda e: e.tensor_tensor(out=m1[:, :], in0=m1[:, :], in1=m2[:, :], op=ALU.add), [m1, m2], [m1])
                V("pool", lambda e: e.tensor_tensor(out=mT[:, oc, :], in0=m1[:, :], in1=m3[:, :], op=ALU.add), [m1, m3], [mT])
            V("pool", lambda e: e.tensor_scalar(out=x[:, :, :], in0=x[:, :, :], scalar1=ALPHA, scalar2=None, op0=ALU.mult), [x], [x])
            for oc in range(NCH):
                if oc % 2 == 0:
                    P = self.psum.next()
                o0 = (oc % 2) * W
                for kc in range(NCH):
                    kb.op("pe", lambda e: e.matmul(P[:, o0:o0 + W], lhsT=wo[:, kc, oc * 128:(oc + 1) * 128], rhs=mT[:, kc, :],
                                                   start=(kc == 0), stop=(kc == NCH - 1)), r=[wo, mT], w=[P])
                V("dve", lambda e: e.scalar_tensor_tensor(out=x[:, oc, :], in0=P[:, o0:o0 + W],
                                                          scalar=self.mods[:, 5 * NCH + oc, seg:seg + 1], in1=x[:, oc, :],
                                                          op0=ALU.mult, op1=ALU.add), [P, x, self.mods], [x])
            P = self.psum.next()
            rstd, nmr = self.ln_stats(x, W, P, pl)
            self.normalize(xn, x, W, rstd, nmr)
            for c in range(NCH):
                V("act", lambda e: e.activation(out=xn[:, c, :], in_=xn[:, c, :], func=AF.Identity,
                                                scale=self.lng[:, lni + c:lni + c + 1], bias=self.lnb[:, lni + c:lni + c + 1]),
                  [xn, self.lng, self.lnb], [xn])
            kb.dma("sp", dstv[:, :, t0:t0 + W], xn[:, :, :], xn, r=[xn], w=[("dram", id(dst))])


Model.rwkv_scan_dir = _rwkv_scan_dir
Model.merge_declare = _merge_declare
Model.merge_phase = _merge_phase


def host_inputs_rwkv(inp):
    L = DEPTH
    d = {}
    w = inp["w_in"]
    rw = w[:, :, 768:2624]
    r, k, v = rw[:, :, 0:512], rw[:, :, 512:1024], rw[:, :, 1024:1536]
    wlo, alo, glo = rw[:, :, 1536:1664], rw[:, :, 1664:1728], rw[:, :, 1728:1856]
    pad = np.zeros_like(alo)
    d["w_inB"] = np.ascontiguousarray(np.concatenate([r, k, v, wlo, glo, alo, pad], -1))
    mu = inp["rwkv_mu"]
    mu_r = np.concatenate([mu[:, 0:1536], mu[:, 1536:1664], mu[:, 1728:1856], mu[:, 1664:1728], np.zeros((L, 64), np.float32)], -1)
    d["rk_mu"] = np.ascontiguousarray(mu_r.reshape(L * NCH_B, 128).T)
    d["rk_w0"] = np.ascontiguousarray(inp["rwkv_w0"].reshape(L * 8, 128).T)
    for n, src in (("a0", "rwkv_a0"), ("kk", "rwkv_k_k"), ("ka", "rwkv_k_a"), ("rk", "rwkv_r_k"), ("gng", "rwkv_gn_g"), ("gnb", "rwkv_gn_b")):
        d["rk_" + n] = np.ascontiguousarray(inp[src].reshape(L * 4, 128).T)
    d["rk_w2"] = np.ascontiguousarray(inp["rwkv_w2"].reshape(L, 128, 512))
    d["rk_a2"] = inp["rwkv_a2"]
    d["rk_g2"] = inp["rwkv_g2"]
    i = np.arange(128)
    d["bd64"] = np.ascontiguousarray(((i[:, None] // 64) == (i[None, :] // 64)).astype(np.float32))
    d["identb"] = np.eye(128, dtype=np.float32)
    on = np.ones((2, 128, 128), np.float32)
    on[0, :, 0] = 0.0
    on[1, :, 127] = 0.0
    d["ones0"] = on
    MT = np.zeros((2, 128, 512), np.float32)
    MN = np.zeros((2, 128, 512), np.float32)
    ii, tt = i[:, None], i[None, :]
    for dd in range(2):
        prev = (ii < tt) if dd == 0 else (ii > tt)
        incl = prev | (ii == tt)
        MT[dd, :, 0:128] = -(prev.astype(np.float32))
        MT[dd, :, 128:256] = incl
        MT[dd, :, 256:384] = prev
        MT[dd, :, 384:512] = incl
        MN[dd] = np.tile(-(prev.T.astype(np.float32)), (1, 4))
    d["rk_MT"], d["rk_MN"] = MT, MN
    d["branch_proj"] = inp["branch_proj"]
    d["w_out"] = inp["w_out"]
    return d


def _make_scratch(self):
    NT = self.NT
    S = {}
    for n, shp, dt in (("qS", [512, NT], BF16), ("kS", [256, NT], BF16), ("vS", [NT, 128], BF16), ("usS", [512, NT], F32),
                       ("gS", [3072, NT], BF16), ("ya", [512, NT], BF16), ("ysb", [512, NT], F32), ("ys", [512, NT], BF16),
                       ("gR", [512, NT], BF16), ("bon", [512, NT], BF16), ("vt", [NT, 512], BF16)):
        S[n] = self.scratch(n, shp, dt)
    for n in ("RHO", "KAP", "BET", "KTI"):
        S[n] = [self.scratch("%s%d" % (n, d), [512, NT], BF16) for d in range(2)]
    for n in ("bt", "kt"):
        S[n] = [self.scratch("%s%d" % (n, d), [NT, 512], BF16) for d in range(2)]
    S["sc"] = [self.scratch("sc%d" % d, [128, NT // 128, 4, 3], F32) for d in range(2)]
    S["O"] = [self.scratch("O%d" % d, [NT, 512], F32) for d in range(2)]
    if "yrS" in self.dbg:
        S["yrS"] = self.scratch("yrS", [512, NT], BF16)
    return S


def _mixer(self, l, src, dst, S, ctx_out):
    self.mixA_phase(l, src, S["qS"], S["kS"], S["vS"], S["usS"], S["gS"])
    self.attn_phase(l, S["qS"], S["kS"], S["vS"], S["ya"])
    self.rwkv_prep_phase(l, src, S)
    self.scan_phase(l, S)
    self.merge_phase(l, src, dst, S, ctx=ctx_out)


def _scan_phase(self, l, S):
    kb = self.kb
    for (ds5, drk) in ((1, 0), (0, 1)):
        with kb.phase() as st:
            PS2 = Pool(kb, "ps2", 2, [128, 1024], F32, psum=True, stack=st)
            psr = Pool(kb, "psr", 4, [128, 512], F32, psum=True, stack=st)
            g1 = self.s5_dir(l, ds5, S["usS"], S["ysb"], S["ys"], st, PS2)
            next(g1)
            g2 = self.rwkv_scan_dir(l, drk, S, st, psr)
            next(g2)
            alive = [g1, g2]
            while alive:
                for g in list(alive):
                    try:
                        next(g)
                    except StopIteration:
                        alive.remove(g)


Model.scan_phase = _scan_phase
Model.make_scratch = _make_scratch
Model.mixer = _mixer


def build_model(T, dbg=()):
    m = Model(T, dbg=dbg)
    m.declare_inputs()
    m.mixA_declare()
    m.s5_declare()
    m.rwkv_declare()
    m.merge_declare()
    m.setup_consts()
    NT = m.NT
    S = m.make_scratch()
    streams = [m.scratch("str%d" % i, [D, NT], F32) for i in range(3)]
    outT = m.dout("outT", [D, T])
    cur = m.xT
    for l in range(DEPTH):
        last = l == DEPTH - 1
        m.adaln_phase(l)
        m.ffn_phase(l, 0, cur, streams[0], 0, ctx=True)
        m.mixer(l, streams[0], streams[1], S, ctx_out=not last)
        if last:
            m.ffn_phase(l, 1, streams[1], outT, CTX, ctx=False)
        else:
            m.ffn_phase(l, 1, streams[1], streams[2], 0, ctx=True)
            cur = streams[2]
    m.kb.finish()
    return m


def all_host_inputs(inp, b, T):
    d = host_inputs(inp, b, T)
    d.update(host_inputs_A(inp, T))
    d.update(host_inputs_s5(inp))
    d.update(host_inputs_rwkv(inp))
    return d


T_FULL = 8192
N_CORES = 8


def kernel(**inputs):
    inp = {k: np.asarray(v) for k, v in inputs.items()}
    m = build_model(T_FULL)
    B = inp["x"].shape[0]
    shared = None
    in_maps = []
    for core in range(N_CORES):
        b = core % B
        d = all_host_inputs(inp, b, T_FULL) if shared is None else dict(shared)
        if shared is None:
            shared = d
        else:
            d["xT"] = np.ascontiguousarray(np.concatenate([inp["ctx"][b], inp["x"][b, :T_FULL]], 0).T)
            cond = np.stack([inp["c"][b], inp["c_ctx"]], -1)
            d["condT"] = np.ascontiguousarray(cond.reshape(NCH, 128, 2).transpose(1, 0, 2))
        in_maps.append({k: v for k, v in d.items() if k in m.dram_in})
    res = run_bass_kernel_spmd(m.nc, in_maps, core_ids=list(range(N_CORES)))
    out = np.stack([np.ascontiguousarray(res.results[b]["outT"].T) for b in range(B)], 0)
    return out.astype(np.float32)
```

```python
import contextlib
import numpy as np
import concourse.bass as bass
import concourse.mybir as mybir
from concourse.bass_utils import run_bass_kernel_spmd

F32 = mybir.dt.float32
BF16 = mybir.dt.bfloat16
AF = mybir.ActivationFunctionType
ALU = mybir.AluOpType
AX = mybir.AxisListType

D = 1024
NCH = 8
CTX = 256
DFF = 2816
NFF = 22
DEPTH = 2
ALPHA = (2.0 * DEPTH) ** 0.25
LN_EPS = 1e-6
DECAY_SCALE = 0.606531
GN_EPS = 64e-5
N_IN = 6208


class Sem:
    _n = 0

    def __init__(self, h):
        self.h = h
        Sem._n += 1
        self.uid = Sem._n


class Buf:
    def __init__(self, name, t):
        self.name = name
        self.t = t
        self.dsem = None
        self.dcnt = 0

    def __getitem__(self, k):
        return self.t[k]

    def __repr__(self):
        return "Buf(%s)" % self.name


class KB:
    EPOCH = 20000

    def __init__(self, nc):
        self.nc = nc
        self.es = contextlib.ExitStack()
        self.eng = {"pe": nc.tensor, "dve": nc.vector, "act": nc.scalar, "pool": nc.gpsimd, "sp": nc.sync}
        self.esem = {}
        self.ecnt = {}
        for e in self.eng:
            self.esem[e] = Sem(self.es.enter_context(nc.semaphore("c_%s_0" % e)))
            self.ecnt[e] = 0
        self.eepoch = {e: 0 for e in self.eng}
        self.seen = {e: {} for e in self.eng}
        self.lastw = {}
        self.reads = {}
        self.nbuf = 0
        self.ninstr = 0
        self.nwait = 0
        self.all_events = {}
        self.free_dsems = []
        self.phase_bufs = []
        self.ndsem = 0

    def sb(self, name, shape, dtype, stack=None):
        self.nbuf += 1
        t = (stack or self.es).enter_context(self.nc.sbuf_tensor("%s_%d" % (name, self.nbuf), list(shape), dtype))
        b = Buf(name, t)
        if stack is not None:
            self.phase_bufs.append(b)
        return b

    def ps(self, name, shape, dtype=F32, stack=None):
        self.nbuf += 1
        t = (stack or self.es).enter_context(self.nc.psum_tensor("%s_%d" % (name, self.nbuf), list(shape), dtype))
        return Buf(name, t)

    def _dsem(self, b):
        if b.dsem is None:
            if self.free_dsems:
                b.dsem, b.dcnt = self.free_dsems.pop()
            else:
                self.ndsem += 1
                b.dsem = Sem(self.es.enter_context(self.nc.semaphore("d_%d" % self.ndsem)))
                b.dcnt = 0
        return b.dsem

    @contextlib.contextmanager
    def phase(self):
        st = contextlib.ExitStack()
        self.phase_bufs = []
        try:
            yield st
        finally:
            self.barrier()
            for b in self.phase_bufs:
                if b.dsem is not None:
                    self.free_dsems.append((b.dsem, b.dcnt))
                    b.dsem = None
            self.phase_bufs = []
            st.close()

    def _need(self, e, r, w):
        need = {}

        def add(evs):
            for uid, (s, v) in evs.items():
                if uid not in need or need[uid][1] < v:
                    need[uid] = (s, v)
        for k in r:
            add(self.lastw.get(k, {}))
        for k in w:
            add(self.lastw.get(k, {}))
            add(self.reads.get(k, {}))
        return need

    def _wait(self, e, need, own_ok):
        eng = self.eng[e]
        for uid, (s, v) in need.items():
            if own_ok and uid == self.esem[e].uid:
                continue
            if self.seen[e].get(uid, 0) >= v:
                continue
            eng.wait_ge(s.h, v)
            self.nwait += 1
            self.seen[e][uid] = v

    def _record(self, ev, r, w):
        uid = ev[0].uid
        for k in r:
            self.reads.setdefault(k, {})[uid] = ev
        for k in w:
            self.lastw.setdefault(k, {})[uid] = ev
            self.reads[k] = {}
        self.all_events[uid] = ev

    def _bump(self, e):
        if self.ecnt[e] >= self.EPOCH:
            self.eepoch[e] += 1
            self.esem[e] = Sem(self.es.enter_context(self.nc.semaphore("c_%s_%d" % (e, self.eepoch[e]))))
            self.ecnt[e] = 0
        self.ecnt[e] += 1
        return (self.esem[e], self.ecnt[e])

    def op(self, e, fn, r=(), w=(), same_ok=False):
        need = self._need(e, r, w)
        self._wait(e, need, own_ok=(e == "pe" or same_ok))
        ins = fn(self.eng[e])
        ev = self._bump(e)
        ins.then_inc(ev[0].h, 1)
        self._record(ev, r, w)
        self.ninstr += 1
        return ins

    def dma(self, q, out, in_, sbuf, r=(), w=(), **kw):
        need = self._need(q, r, w)
        self._wait(q, need, own_ok=False)
        s = self._dsem(sbuf)
        ins = self.eng[q].dma_start(out=out, in_=in_, **kw)
        sbuf.dcnt += 16
        ev = (s, sbuf.dcnt)
        ins.then_inc(s.h, 16)
        self._record(ev, r, w)
        self.ninstr += 1
        return ins

    def barrier(self):
        for e in self.eng:
            self._wait(e, dict(self.all_events), own_ok=True)
        self.lastw = {}
        self.reads = {}
        self.all_events = {}

    def finish(self, e="sp"):
        self._wait(e, dict(self.all_events), own_ok=True)


class Pool:
    def __init__(self, kb, name, n, shape, dtype, psum=False, stack=None):
        self.bufs = [(kb.ps if psum else kb.sb)("%s%d" % (name, i), shape, dtype, stack=stack) for i in range(n)]
        self.i = 0

    def next(self):
        b = self.bufs[self.i % len(self.bufs)]
        self.i += 1
        return b


class Model:
    def __init__(self, T, dbg=(), nlayers=DEPTH, stop_after=None):
        self.T = T
        self.NT = CTX + T
        self.dbg = set(dbg)
        self.nlayers = nlayers
        self.stop_after = stop_after
        nc = bass.Bass("TRN2", target_bir_lowering=False)
        self.nc = nc
        self.kb = KB(nc)
        self.dram_in = {}
        self.dram_out = {}
        self.scr_n = 0

    def din(self, name, shape, dtype=F32):
        t = self.nc.dram_tensor(name, list(shape), dtype, kind="ExternalInput").ap()
        self.dram_in[name] = t
        return t

    def dout(self, name, shape, dtype=F32):
        t = self.nc.dram_tensor(name, list(shape), dtype, kind="ExternalOutput").ap()
        self.dram_out[name] = t
        return t

    def scratch(self, name, shape, dtype=F32):
        if name in self.dbg:
            return self.dout(name, shape, dtype)
        return self.nc.dram_tensor(name, list(shape), dtype, kind="Internal").ap()

    def tiles(self, W, ctx=True, lat=True):
        out = []
        if ctx:
            for t0 in range(0, CTX, W):
                out.append((1, t0, W))
        if lat:
            for t0 in range(0, self.T, W):
                out.append((0, CTX + t0, W))
        return out

    def declare_inputs(self):
        L = DEPTH
        self.xT = self.din("xT", [D, self.NT])
        self.condT = self.din("condT", [128, NCH, 2])
        self.w_ada = self.din("w_ada", [L, D, 9 * D])
        self.b_ada = self.din("b_ada", [L, 128, 72])
        self.ln_g = self.din("ln_g", [128, L * 3 * NCH])
        self.ln_b = self.din("ln_b", [128, L * 3 * NCH])
        self.ffn_w_in = self.din("ffn_w_in", [L, 2, D, 2 * DFF])
        self.ffn_w_out = self.din("ffn_w_out", [L, 2, DFF, D])

    def setup_consts(self):
        kb = self.kb
        self.onesb = kb.sb("onesb", [128, 128], BF16)
        kb.op("dve", lambda e: e.memset(self.onesb[:], 1.0 / D), w=[self.onesb])
        self.lng = kb.sb("lng", [128, DEPTH * 3 * NCH], F32)
        self.lnb = kb.sb("lnb", [128, DEPTH * 3 * NCH], F32)
        kb.dma("sp", self.lng[:], self.ln_g[:, :], self.lng, w=[self.lng])
        kb.dma("sp", self.lnb[:], self.ln_b[:, :], self.lnb, w=[self.lnb])
        self.scond = kb.sb("scond", [128, NCH, 2], F32)
        kb.dma("sp", self.scond[:], self.condT[:, :, :], self.scond, w=[self.scond])
        kb.op("act", lambda e: e.activation(out=self.scond[:], in_=self.scond[:], func=AF.Silu),
              r=[self.scond], w=[self.scond])
        self.mods = kb.sb("mods", [128, 72, 2], F32)
        self.modp1 = kb.sb("modp1", [128, 72, 2], F32)
        self.modh = kb.sb("modh", [128, 72, 2], F32)

    def adaln_phase(self, l):
        kb = self.kb
        with kb.phase() as st:
            self.psum = Pool(kb, "ps", 8, [128, 512], F32, psum=True, stack=st)
            wa = Pool(kb, "wa", 2, [128, NCH, D], F32, stack=st)
            bada = kb.sb("bada", [128, 72], F32, st)
            kb.dma("sp", bada[:], self.b_ada[l, :, :], bada, w=[bada])
            P = self.psum.next()
            for m in range(9):
                w = wa.next()
                for kc in range(NCH):
                    kb.dma("sp" if kc % 2 == 0 else "act", w[:, kc, :],
                           self.w_ada[l, kc * 128:(kc + 1) * 128, m * D:(m + 1) * D], w, w=[w])
                for oc in range(NCH):
                    j = m * NCH + oc
                    for kc in range(NCH):
                        kb.op("pe", lambda e: e.matmul(P[:, 2 * j:2 * j + 2], lhsT=w[:, kc, oc * 128:(oc + 1) * 128],
                                                       rhs=self.scond[:, kc, :], start=(kc == 0), stop=(kc == NCH - 1)),
                              r=[w, self.scond], w=[P])
            kb.op("dve", lambda e: e.tensor_tensor(out=self.mods[:], in0=P[:, 0:144].rearrange("p (j s) -> p j s", s=2),
                                                   in1=bada[:, :].unsqueeze(2).broadcast_to([128, 72, 2]), op=ALU.add),
                  r=[P, bada], w=[self.mods])
            kb.op("dve", lambda e: e.tensor_scalar(out=self.modp1[:], in0=self.mods[:], scalar1=1.0, scalar2=None,
                                                   op0=ALU.add), r=[self.mods], w=[self.modp1])
            kb.op("dve", lambda e: e.tensor_scalar(out=self.modh[:], in0=self.mods[:], scalar1=0.5, scalar2=None,
                                                   op0=ALU.mult), r=[self.mods], w=[self.modh])

    def ln_stats(self, x, W, P, pl):
        kb = self.kb
        xb, sq = pl["xb"].next(), pl["sq"].next()
        kb.op("act", lambda e: e.activation(out=xb[:, :, :W], in_=x[:, :, :W], func=AF.Copy), r=[x], w=[xb])
        kb.op("act", lambda e: e.activation(out=sq[:, :, :W], in_=x[:, :, :W], func=AF.Square), r=[x], w=[sq])
        for c in range(NCH):
            kb.op("pe", lambda e: e.matmul(P[:, 0:W], lhsT=self.onesb[:, :], rhs=xb[:, c, :W],
                                           start=(c == 0), stop=(c == NCH - 1)), r=[xb, self.onesb], w=[P])
        for c in range(NCH):
            kb.op("pe", lambda e: e.matmul(P[:, W:2 * W], lhsT=self.onesb[:, :], rhs=sq[:, c, :W],
                                           start=(c == 0), stop=(c == NCH - 1)), r=[sq, self.onesb], w=[P])
        m2, var, rstd, nmr = pl["m2"].next(), pl["var"].next(), pl["rstd"].next(), pl["nmr"].next()
        kb.op("act", lambda e: e.activation(out=m2[:, 0, :W], in_=P[:, 0:W], func=AF.Square), r=[P], w=[m2])
        kb.op("dve", lambda e: e.scalar_tensor_tensor(out=var[:, 0, :W], in0=P[:, W:2 * W], scalar=LN_EPS,
                                                      in1=m2[:, 0, :W], op0=ALU.add, op1=ALU.subtract),
              r=[P, m2], w=[var])
        kb.op("act", lambda e: e.activation(out=var[:, 0, :W], in_=var[:, 0, :W], func=AF.Sqrt), r=[var], w=[var])
        kb.op("dve", lambda e: e.reciprocal(out=rstd[:, 0, :W], in_=var[:, 0, :W]), r=[var], w=[rstd])
        kb.op("dve", lambda e: e.scalar_tensor_tensor(out=nmr[:, 0, :W], in0=P[:, 0:W], scalar=-1.0,
                                                      in1=rstd[:, 0, :W], op0=ALU.mult, op1=ALU.mult),
              r=[P, rstd], w=[nmr])
        return rstd, nmr

    def normalize(self, out, x, W, rstd, nmr):
        kb = self.kb
        kb.op("dve", lambda e: e.tensor_tensor(out=out[:, :, :W], in0=x[:, :, :W],
                                               in1=rstd[:, 0:1, :W].broadcast_to([128, NCH, W]), op=ALU.mult),
              r=[x, rstd], w=[out])
        kb.op("dve", lambda e: e.tensor_tensor(out=out[:, :, :W], in0=out[:, :, :W],
                                               in1=nmr[:, 0:1, :W].broadcast_to([128, NCH, W]), op=ALU.add),
              r=[out, nmr], w=[out])

    def stat_pools(self, W, st):
        kb = self.kb
        return {
            "xb": Pool(kb, "xb", 1, [128, NCH, W], BF16, stack=st),
            "sq": Pool(kb, "sq", 1, [128, NCH, W], BF16, stack=st),
            "m2": Pool(kb, "m2", 2, [128, 1, W], F32, stack=st),
            "var": Pool(kb, "var", 2, [128, 1, W], F32, stack=st),
            "rstd": Pool(kb, "rstd", 2, [128, 1, W], F32, stack=st),
            "nmr": Pool(kb, "nmr", 2, [128, 1, W], F32, stack=st),
        }

    def ffn_phase(self, l, s, src, dst, dst_off, ctx):
        kb = self.kb
        W = 256
        mb = 0 if s == 0 else 6
        lni = (l * 3 + (0 if s == 0 else 2)) * NCH
        with kb.phase() as st:
            self.psum = Pool(kb, "ps", 8, [128, 512], F32, psum=True, stack=st)
            w1 = kb.sb("w1", [128, NCH, 2 * DFF], BF16, st)
            w2 = kb.sb("w2", [128, NFF, D], BF16, st)
            for c in range(NCH):
                for hh in range(2):
                    kb.dma("pool", w1[:, c, hh * DFF:(hh + 1) * DFF],
                           self.ffn_w_in[l, s, c * 128:(c + 1) * 128, hh * DFF:(hh + 1) * DFF], w1, w=[w1])
            for c in range(NFF):
                kb.dma("pool", w2[:, c, :], self.ffn_w_out[l, s, c * 128:(c + 1) * 128, :], w2, w=[w2])
            pl = self.stat_pools(W, st)
            xp = Pool(kb, "x", 2, [128, NCH, W], F32, stack=st)
            xnp = Pool(kb, "xn", 2, [128, NCH, W], F32, stack=st)
            up = Pool(kb, "u", 2, [128, NCH, W], BF16, stack=st)
            hp = Pool(kb, "h", 1, [128, NFF, W], BF16, stack=st)
            gp = Pool(kb, "g", 2, [128, W], F32, stack=st)
            srcv = src.rearrange("(c p) t -> p c t", p=128)
            dstv = dst.rearrange("(c p) t -> p c t", p=128)
            tl = self.tiles(W, ctx=ctx)
            T_ = {}

            def stage_a(i):
                seg, t0, _ = tl[i]
                x = xp.next()
                kb.dma("sp", x[:, :, :], srcv[:, :, t0:t0 + W], x, r=[("dram", id(src))], w=[x])
                P = self.psum.next()
                rstd, nmr = self.ln_stats(x, W, P, pl)
                xn = xnp.next()
                self.normalize(xn, x, W, rstd, nmr)
                u = up.next()
                for c in range(NCH):
                    kb.op("act", lambda e: e.activation(out=u[:, c, :], in_=xn[:, c, :], func=AF.Identity,
                                                        scale=self.modp1[:, (mb + 1) * NCH + c, seg:seg + 1],
                                                        bias=self.mods[:, mb * NCH + c, seg:seg + 1]),
                          r=[xn, self.modp1, self.mods], w=[u])
                kb.op("pool", lambda e: e.tensor_scalar(out=x[:, :, :], in0=x[:, :, :], scalar1=ALPHA, scalar2=None,
                                                        op0=ALU.mult), r=[x], w=[x])
                T_[i] = (x, xn, u)

            def stage_b(i):
                x, xn, u = T_[i]
                h = hp.next()
                for j in range(NFF):
                    P = self.psum.next()
                    for c in range(NCH):
                        kb.op("pe", lambda e: e.matmul(P[:, 0:W], lhsT=w1[:, c, j * 128:(j + 1) * 128], rhs=u[:, c, :],
                                                       start=(c == 0), stop=(c == NCH - 1)), r=[w1, u], w=[P])
                    for c in range(NCH):
                        kb.op("pe", lambda e: e.matmul(P[:, W:2 * W], lhsT=w1[:, c, DFF + j * 128:DFF + (j + 1) * 128],
                                                       rhs=u[:, c, :], start=(c == 0), stop=(c == NCH - 1)),
                              r=[w1, u], w=[P])
                    g = gp.next()
                    kb.op("act", lambda e: e.activation(out=g[:, :], in_=P[:, 0:W], func=AF.Silu), r=[P], w=[g])
                    kb.op("dve", lambda e: e.tensor_tensor(out=h[:, j, :], in0=P[:, W:2 * W], in1=g[:, :], op=ALU.mult),
                          r=[P, g], w=[h])
                T_[i] = (x, xn, u, h)

            def stage_c(i):
                seg, t0, _ = tl[i]
                x, xn, u, h = T_.pop(i)
                for oc in range(NCH):
                    if oc % 2 == 0:
                        P = self.psum.next()
                    o0 = (oc % 2) * W
                    for j in range(NFF):
                        kb.op("pe", lambda e: e.matmul(P[:, o0:o0 + W], lhsT=w2[:, j, oc * 128:(oc + 1) * 128],
                                                       rhs=h[:, j, :], start=(j == 0), stop=(j == NFF - 1)),
                              r=[w2, h], w=[P])
                    kb.op("dve", lambda e: e.scalar_tensor_tensor(
                        out=x[:, oc, :], in0=P[:, o0:o0 + W], scalar=self.modh[:, (mb + 2) * NCH + oc, seg:seg + 1],
                        in1=x[:, oc, :], op0=ALU.mult, op1=ALU.add), r=[P, x, self.modh], w=[x])
                P = self.psum.next()
                rstd, nmr = self.ln_stats(x, W, P, pl)
                self.normalize(xn, x, W, rstd, nmr)
                for c in range(NCH):
                    kb.op("act", lambda e: e.activation(out=xn[:, c, :], in_=xn[:, c, :], func=AF.Identity,
                                                        scale=self.lng[:, lni + c:lni + c + 1],
                                                        bias=self.lnb[:, lni + c:lni + c + 1]),
                          r=[xn, self.lng, self.lnb], w=[xn])
                kb.dma("sp", dstv[:, :, t0 - dst_off:t0 - dst_off + W], xn[:, :, :], xn,
                       r=[xn], w=[("dram", id(dst))])

            stage_a(0)
            for i in range(len(tl)):
                stage_b(i)
                if i + 1 < len(tl):
                    stage_a(i + 1)
                stage_c(i)


def host_inputs(inp, b, T):
    L = DEPTH
    d = {}
    d["xT"] = np.ascontiguousarray(np.concatenate([inp["ctx"][b], inp["x"][b, :T]], 0).T)
    cond = np.stack([inp["c"][b], inp["c_ctx"]], -1)
    d["condT"] = np.ascontiguousarray(cond.reshape(NCH, 128, 2).transpose(1, 0, 2))
    d["w_ada"] = inp["w_ada"]
    d["b_ada"] = np.ascontiguousarray(inp["b_ada"].reshape(L, 72, 128).transpose(0, 2, 1))
    d["ln_g"] = np.ascontiguousarray(inp["ln_g"].reshape(L * 3 * NCH, 128).T)
    d["ln_b"] = np.ascontiguousarray(inp["ln_b"].reshape(L * 3 * NCH, 128).T)
    d["ffn_w_in"] = inp["ffn_w_in"]
    d["ffn_w_out"] = inp["ffn_w_out"]
    return d


NWA = 10 + 1 + 4 + 24
CH_Q, CH_QS, CH_K, CH_KS, CH_KB, CH_KBS, CH_V, CH_S5, CH_G = 0, 4, 8, 9, 10, 11, 12, 13, 17
NCH_A = 41


def _mixA_declare(self):
    L = DEPTH
    self.w_inA = self.din("w_inA", [L, D, NCH_A * 128])
    self.ropeC = self.din("ropeC", [128, self.NT])
    self.ropeS = self.din("ropeS", [128, self.NT])
    self.sinkT = self.din("sinkT", [L, 64, 8])
    self.maskP = self.din("maskP", [128, 512])
    self.maskN = self.din("maskN", [128, 512])


def _mixA_phase(self, l, src, qS, kS, vS, usS, gS):
    kb = self.kb
    W = 256
    with kb.phase() as st:
        self.psum = Pool(kb, "ps", 8, [128, 512], F32, psum=True, stack=st)
        w = kb.sb("wA", [128, NCH, NCH_A * 128], BF16, st)
        for c in range(NCH):
            for hh in range(2):
                n0, n1 = (0, 21 * 128) if hh == 0 else (21 * 128, NCH_A * 128)
                kb.dma("pool", w[:, c, n0:n1], self.w_inA[l, c * 128:(c + 1) * 128, n0:n1], w, w=[w])
        pl = self.stat_pools(W, st)
        xp = Pool(kb, "x", 2, [128, NCH, W], F32, stack=st)
        xnp = Pool(kb, "xn", 2, [128, NCH, W], F32, stack=st)
        up = Pool(kb, "u", 2, [128, NCH, W], BF16, stack=st)
        cp = Pool(kb, "rc", 2, [128, W], F32, stack=st)
        sp_ = Pool(kb, "rs", 2, [128, W], F32, stack=st)
        t1p = Pool(kb, "t1", 2, [128, W], F32, stack=st)
        t2p = Pool(kb, "t2", 2, [128, W], F32, stack=st)
        qp = Pool(kb, "qo", 2, [128, 4, W], BF16, stack=st)
        kp = Pool(kb, "ko", 2, [128, 2, W], BF16, stack=st)
        vp = Pool(kb, "vo", 2, [128, 2, 128], BF16, stack=st)
        usp = Pool(kb, "uso", 2, [128, 4, W], F32, stack=st)
        gp = Pool(kb, "go", 2, [128, 24, W], BF16, stack=st)
        srcv = src.rearrange("(c p) t -> p c t", p=128)

        def proj(P, o0, ch, u):
            for c in range(NCH):
                kb.op("pe", lambda e: e.matmul(P[:, o0:o0 + W], lhsT=w[:, c, ch * 128:(ch + 1) * 128], rhs=u[:, c, :],
                                               start=(c == 0), stop=(c == NCH - 1)), r=[w, u], w=[P])

        tl = self.tiles(W)
        T_ = {}

        def stage_a(i):
            seg, t0, _ = tl[i]
            x = xp.next()
            kb.dma("sp", x[:, :, :], srcv[:, :, t0:t0 + W], x, r=[("dram", id(src))], w=[x])
            cT, sT = cp.next(), sp_.next()
            kb.dma("sp", cT[:, :], self.ropeC[:, t0:t0 + W], cT, w=[cT])
            kb.dma("sp", sT[:, :], self.ropeS[:, t0:t0 + W], sT, w=[sT])
            P = self.psum.next()
            rstd, nmr = self.ln_stats(x, W, P, pl)
            xn = xnp.next()
            self.normalize(xn, x, W, rstd, nmr)
            u = up.next()
            for c in range(NCH):
                kb.op("act", lambda e: e.activation(out=u[:, c, :], in_=xn[:, c, :], func=AF.Identity,
                                                    scale=self.modp1[:, 4 * NCH + c, seg:seg + 1],
                                                    bias=self.mods[:, 3 * NCH + c, seg:seg + 1]),
                      r=[xn, self.modp1, self.mods], w=[u])
            T_[i] = (u, cT, sT)

        def stage_b1(i):
            seg, t0, _ = tl[i]
            u, cT, sT = T_[i]
            qo, ko = qp.next(), kp.next()

            def rope(dst_ap, dst, ch, chs):
                P = self.psum.next()
                proj(P, 0, ch, u)
                proj(P, W, chs, u)
                t1, t2 = t1p.next(), t2p.next()
                kb.op("dve", lambda e: e.tensor_tensor(out=t1[:, :], in0=P[:, 0:W], in1=cT[:, :], op=ALU.mult),
                      r=[P, cT], w=[t1])
                kb.op("dve", lambda e: e.tensor_tensor(out=t2[:, :], in0=P[:, W:2 * W], in1=sT[:, :], op=ALU.mult),
                      r=[P, sT], w=[t2])
                kb.op("pool", lambda e: e.tensor_tensor(out=dst_ap, in0=t1[:, :], in1=t2[:, :], op=ALU.add),
                      r=[t1, t2], w=[dst])
            for c in range(4):
                rope(qo[:, c, :], qo, CH_Q + c, CH_QS + c)
            rope(ko[:, 0, :], ko, CH_K, CH_KS)
            rope(ko[:, 1, :], ko, CH_KB, CH_KBS)
            kb.dma("sp", qS.rearrange("(c p) t -> p c t", p=128)[:, :, t0:t0 + W], qo[:, :, :], qo, r=[qo],
                   w=[("dram", id(qS))])
            kb.dma("sp", kS.rearrange("(c p) t -> p c t", p=128)[:, :, t0:t0 + W], ko[:, :, :], ko, r=[ko],
                   w=[("dram", id(kS))])

        def stage_b2(i):
            seg, t0, _ = tl[i]
            u, cT, sT = T_.pop(i)
            vo = vp.next()
            P = self.psum.next()
            for tb in range(W // 128):
                for c in range(NCH):
                    kb.op("pe", lambda e: e.matmul(P[:, tb * 128:(tb + 1) * 128], lhsT=u[:, c, tb * 128:(tb + 1) * 128],
                                                   rhs=w[:, c, CH_V * 128:(CH_V + 1) * 128],
                                                   start=(c == 0), stop=(c == NCH - 1)), r=[w, u], w=[P])
            kb.op("act", lambda e: e.activation(out=vo[:, :, :], in_=P[:, 0:W].rearrange("p (b n) -> p b n", b=2),
                                                func=AF.Copy), r=[P], w=[vo])
            kb.dma("sp", vS[t0:t0 + W, :].rearrange("(b p) n -> p b n", p=128), vo[:, :, :], vo, r=[vo],
                   w=[("dram", id(vS))])
            uso = usp.next()
            for c in range(4):
                if c % 2 == 0:
                    P = self.psum.next()
                o0 = (c % 2) * W
                proj(P, o0, CH_S5 + c, u)
                kb.op("act", lambda e: e.activation(out=uso[:, c, :], in_=P[:, o0:o0 + W], func=AF.Copy),
                      r=[P], w=[uso])
            kb.dma("sp", usS.rearrange("(c p) t -> p c t", p=128)[:, :, t0:t0 + W], uso[:, :, :], uso, r=[uso],
                   w=[("dram", id(usS))])
            go = gp.next()
            for c in range(24):
                if c % 2 == 0:
                    P = self.psum.next()
                o0 = (c % 2) * W
                proj(P, o0, CH_G + c, u)
                kb.op("act", lambda e: e.activation(out=go[:, c, :], in_=P[:, o0:o0 + W], func=AF.Sigmoid),
                      r=[P], w=[go])
            kb.dma("sp", gS.rearrange("(c p) t -> p c t", p=128)[:, :, t0:t0 + W], go[:, :, :], go, r=[go],
                   w=[("dram", id(gS))])

        stage_a(0)
        for i in range(len(tl)):
            stage_b1(i)
            if i + 1 < len(tl):
                stage_a(i + 1)
            stage_b2(i)


def _attn_phase(self, l, qS, kS, vS, yaS):
    kb = self.kb
    NB = self.T // 128
    with kb.phase() as st:
        self.psum = Pool(kb, "ps", 8, [128, 512], F32, psum=True, stack=st)
        mP = kb.sb("mP", [128, 512], BF16, st)
        mN = kb.sb("mN", [128, 512], BF16, st)
        kb.dma("pool", mP[:, :], self.maskP[:, :], mP, w=[mP])
        kb.dma("pool", mN[:, :], self.maskN[:, :], mN, w=[mN])
        ones = kb.sb("ones64", [128, 64], BF16, st)
        kb.op("dve", lambda e: e.memset(ones[:], 1.0), w=[ones])
        esk = kb.sb("esk", [64, 8], F32, st)
        kb.dma("sp", esk[:, :], self.sinkT[l, :, :], esk, w=[esk])
        kb.op("act", lambda e: e.activation(out=esk[:, :], in_=esk[:, :], func=AF.Exp), r=[esk], w=[esk])
        eskb = kb.sb("eskb", [64, 8, 128], F32, st)
        kb.op("dve", lambda e: e.tensor_copy(out=eskb[:, :, :], in_=esk[:, :].unsqueeze(2).broadcast_to([64, 8, 128])),
              r=[esk], w=[eskb])
        kc = kb.sb("kc", [128, 2, CTX], BF16, st)
        kb.dma("sp", kc[:, :, :], kS.rearrange("(c p) t -> p c t", p=128)[:, :, 0:CTX], kc, r=[("dram", id(kS))], w=[kc])
        vc = kb.sb("vc", [128, 2, 128], BF16, st)
        kb.dma("sp", vc[:, :, :], vS[0:CTX, :].rearrange("(b p) n -> p b n", p=128), vc, r=[("dram", id(vS))], w=[vc])
        qp = Pool(kb, "aq", 2, [128, 4, 128], BF16, stack=st)
        kwp = Pool(kb, "akw", 2, [128, 2, 384], BF16, stack=st)
        vwp = Pool(kb, "avw", 2, [128, 3, 128], BF16, stack=st)
        pp = Pool(kb, "ap", 4, [128, 512], BF16, stack=st)
        dp = Pool(kb, "ad", 2, [64, 512], F32, stack=st)
        op_ = Pool(kb, "ao", 2, [64, 8, 128], BF16, stack=st)
        qv = qS.rearrange("(c p) t -> p c t", p=128)
        kv = kS.rearrange("(c p) t -> p c t", p=128)
        blocks = [(1, b) for b in range(CTX // 128)] + [(0, b) for b in range(NB)]
        self._acc_i = 0
        self._sc_i = 0
        for (seg, b) in blocks:
            t0 = b * 128 if seg == 1 else CTX + b * 128
            q = qp.next()
            kb.dma("sp", q[:, :, :], qv[:, :, t0:t0 + 128], q, r=[("dram", id(qS))], w=[q])
            keyblocks = []
            if seg == 0:
                lo = max(b - 1, 0)
                hi = min(b + 1, NB - 1)
                nb_ = hi - lo + 1
                kw, vw = kwp.next(), vwp.next()
                kb.dma("sp", kw[:, :, 0:nb_ * 128], kv[:, :, CTX + lo * 128:CTX + (hi + 1) * 128], kw,
                       r=[("dram", id(kS))], w=[kw])
                kb.dma("sp", vw[:, 0:nb_, :],
                       vS[CTX + lo * 128:CTX + (hi + 1) * 128, :].rearrange("(b p) n -> p b n", p=128), vw,
                       r=[("dram", id(vS))], w=[vw])
                for bb in range(lo, hi + 1):
                    i = bb - lo
                    mask = mP if bb < b else (mN if bb > b else None)
                    keyblocks.append((kw, i * 128, vw, i, mask))
            for i in range(CTX // 128):
                keyblocks.append((kc, i * 128, vc, i, None))
            oo = op_.next()
            accb = self.psum.bufs[0:4]
            scb = self.psum.bufs[4:8]
            for kvh in range(2):
                Pn = accb[(self._acc_i) % 4]
                Pd = accb[(self._acc_i + 1) % 4]
                self._acc_i += 2
                for bi, (kbuf, koff, vbuf, vi, mask) in enumerate(keyblocks):
                    Ps = [scb[self._sc_i % 4], scb[(self._sc_i + 1) % 4]]
                    self._sc_i += 2
                    pt = pp.next()
                    for par in range(2):
                        base = par * 64
                        var = 0 if (kvh * 64 == base) else 1
                        for j in range(2):
                            h = kvh * 4 + 2 * j + par
                            kb.op("pe", lambda e: e.matmul(Ps[par][:, j * 128:(j + 1) * 128],
                                                           lhsT=kbuf[base:base + 64, var, koff:koff + 128],
                                                           rhs=q[base:base + 64, h // 2, :], start=True, stop=True),
                                  r=[kbuf, q], w=[Ps[par]])
                        kb.op("act", lambda e: e.activation(out=pt[:, par * 256:(par + 1) * 256], in_=Ps[par][:, 0:256],
                                                            func=AF.Exp, scale=0.125), r=[Ps[par]], w=[pt])
                    if mask is not None:
                        kb.op("pool", lambda e: e.tensor_tensor(out=pt[:, :], in0=pt[:, :], in1=mask[:, :], op=ALU.mult),
                              r=[pt, mask], w=[pt])
                    first, last = bi == 0, bi == len(keyblocks) - 1
                    kb.op("pe", lambda e: e.matmul(Pn[0:64, :], lhsT=vbuf[:, vi, kvh * 64:(kvh + 1) * 64], rhs=pt[:, :],
                                                   start=first, stop=last), r=[vbuf, pt], w=[Pn])
                    kb.op("pe", lambda e: e.matmul(Pd[0:64, :], lhsT=ones[:, :], rhs=pt[:, :],
                                                   start=first, stop=last), r=[ones, pt], w=[Pd])
                den = dp.next()
                kb.op("dve", lambda e: e.tensor_tensor(
                    out=den[:, :].rearrange("p (r j q) -> p r j q", r=2, j=2), in0=Pd[0:64, :].rearrange("p (r j q) -> p r j q", r=2, j=2),
                    in1=eskb[:, kvh * 4:(kvh + 1) * 4, :].rearrange("p (j r) q -> p r j q", r=2),
                    op=ALU.add), r=[Pd, eskb], w=[den])
                kb.op("dve", lambda e: e.reciprocal(out=den[:, :], in_=den[:, :]), r=[den], w=[den])
                kb.op("dve", lambda e: e.tensor_tensor(
                    out=oo[:, kvh * 4:(kvh + 1) * 4, :].rearrange("p (j r) q -> p r j q", r=2),
                    in0=Pn[0:64, :].rearrange("p (r j q) -> p r j q", r=2, j=2),
                    in1=den[:, :].rearrange("p (r j q) -> p r j q", r=2, j=2), op=ALU.mult),
                      r=[Pn, den], w=[oo])
            kb.dma("sp", yaS.rearrange("(h p) t -> p h t", p=64)[:, :, t0:t0 + 128], oo[:, :, :], oo, r=[oo],
                   w=[("dram", id(yaS))])


Model.mixA_declare = _mixA_declare
Model.mixA_phase = _mixA_phase
Model.attn_phase = _attn_phase


def _rope_tables(T):
    NT = CTX + T
    C = np.ones((128, NT), np.float32)
    S = np.zeros((128, NT), np.float32)
    t = np.arange(T)
    row = (t // 64).astype(np.float32)
    col = (t % 64).astype(np.float32)
    inv = (10000.0 ** (-np.arange(16, dtype=np.float32) / 16)).astype(np.float32)
    for d in range(64):
        i = d % 16
        pos = row if d < 32 else col
        ang = (pos * inv[i]).astype(np.float32)
        sign = -1.0 if (d % 32) < 16 else 1.0
        for hb in (0, 64):
            C[hb + d, CTX:] = np.cos(ang)
            S[hb + d, CTX:] = sign * np.sin(ang)
    return C, S


def _swap_perm(n_heads):
    idx = []
    for h in range(n_heads):
        for d in range(64):
            p = d + 16 if (d % 32) < 16 else d - 16
            idx.append(h * 64 + p)
    return np.array(idx)


def host_inputs_A(inp, T):
    d = {}
    w = inp["w_in"]
    q = w[:, :, 0:512]
    k = w[:, :, 512:640]
    v = w[:, :, 640:768]
    kB = np.concatenate([k[:, :, 64:128], k[:, :, 0:64]], -1)
    s5 = w[:, :, 2624:3136]
    g = w[:, :, 3136:6208]
    d["w_inA"] = np.ascontiguousarray(np.concatenate(
        [q, q[:, :, _swap_perm(8)], k, k[:, :, _swap_perm(2)], kB, kB[:, :, _swap_perm(2)], v, s5, g], -1))
    C, S = _rope_tables(T)
    d["ropeC"], d["ropeS"] = C, S
    d["sinkT"] = np.ascontiguousarray(np.broadcast_to(inp["attn_sink"][:, None, :], (DEPTH, 64, 8)))
    j = np.arange(128)[:, None]
    i = np.arange(128)[None, :]
    d["maskP"] = np.ascontiguousarray(np.tile((j >= i).astype(np.float32), (1, 4)))
    d["maskN"] = np.ascontiguousarray(np.tile((j <= i).astype(np.float32), (1, 4)))
    return d


I32 = mybir.dt.int32
TWO_PI = 2.0 * np.pi


def _s5_declare(self):
    L = DEPTH
    self.s5_are = self.din("s5_are", [L, 2, 128, 4, 64])
    self.s5_aim = self.din("s5_aim", [L, 2, 128, 4, 64])
    self.s5_ls = self.din("s5_ls", [L, 2, 128, 4])
    self.s5_brT = self.din("s5_brT", [L, 128, 4, 64])
    self.s5_biT = self.din("s5_biT", [L, 128, 4, 64])
    self.s5_are2 = self.din("s5_are2", [L, 2, 128, 16])
    self.s5_aim2 = self.din("s5_aim2", [L, 2, 128, 16])
    self.s5_ls2 = self.din("s5_ls2", [L, 2, 128, 16])
    self.s5_crT = self.din("s5_crT", [L, 128, 16, 16])
    self.s5_ciT = self.din("s5_ciT", [L, 128, 16, 16])
    self.s5_rowmask = self.din("s5_rowmask", [128, 16, 2])
    self.s5_dT = self.din("s5_dT", [128, L * 4])
    self.s5_glub = self.din("s5_glub", [128, L * 4])
    self.s5_gluw = self.din("s5_gluw", [L, 512, 512])
    self.tauT = self.din("tauT", [128, 128])


def _sincos(self, ang, angk, n, S, Sk, C, Ck, st):
    kb = self.kb
    t = kb.sb("sc_t", [128, n], F32, st)
    ti = kb.sb("sc_i", [128, n], I32, st)
    tf = kb.sb("sc_f", [128, n], F32, st)
    for (off, dst, dk) in ((0.0, S, Sk), (0.25, C, Ck)):
        kb.op("dve", lambda e: e.tensor_scalar(out=t[:, :], in0=ang, scalar1=1.0 / TWO_PI, scalar2=off,
                                               op0=ALU.mult, op1=ALU.add), r=[angk], w=[t])
        kb.op("dve", lambda e: e.tensor_copy(out=ti[:, :], in_=t[:, :]), r=[t], w=[ti])
        kb.op("dve", lambda e: e.tensor_copy(out=tf[:, :], in_=ti[:, :]), r=[ti], w=[tf])
        kb.op("dve", lambda e: e.tensor_tensor(out=tf[:, :], in0=t[:, :], in1=tf[:, :], op=ALU.subtract),
              r=[t, tf], w=[tf])
        kb.op("act", lambda e: e.activation(out=dst, in_=tf[:, :], func=AF.Sin, scale=TWO_PI), r=[tf], w=[dk])


def _s5_dir(self, l, d, usS, ysbS, ysS, st, PS2):
    kb = self.kb
    NT = self.NT
    nchunk = NT // 128
    usv = usS.rearrange("(c p) t -> p c t", p=128)
    ybv = ysbS.rearrange("(c p) t -> p c t", p=128)
    ysv = ysS.rearrange("(c p) t -> p c t", p=128)
    V = lambda e_, f, r, w: kb.op(e_, f, r=r, w=w)
    _pers = {}
    for (n_, shp_, dt_) in (("DR", [128, 16, 128], BF16), ("DI", [128, 16, 128], BF16), ("COS", [128, 16, 128], F32),
                            ("SIN", [128, 16, 128], F32), ("RHO0", [128, 16, 128], F32), ("rho", [128, 16], F32),
                            ("lr2", [128, 16], F32), ("li2", [128, 16], F32), ("CR", [128, 16, 128], BF16),
                            ("CIn", [128, 16, 128], BF16), ("s5d", [128, 4], F32), ("s5gb", [128, 4], F32),
                            ("gluw", [128, 4, 512], BF16)):
        _pers[n_] = kb.sb(n_, shp_, dt_, st)
    sbf = lambda n, shp, dt=F32: _pers[n] if n in _pers else kb.sb(n, shp, dt, st)
    st2 = contextlib.ExitStack()
    tmpf = lambda n, shp, dt=F32: kb.sb(n, shp, dt, st2)
    are, aim = tmpf("are", [128, 4, 64]), tmpf("aim", [128, 4, 64])
    ls = tmpf("ls", [128, 4])
    br, bi = tmpf("br", [128, 4, 64]), tmpf("bi", [128, 4, 64])
    kb.dma("sp", are[:, :, :], self.s5_are[l, d], are, w=[are])
    kb.dma("sp", aim[:, :, :], self.s5_aim[l, d], aim, w=[aim])
    kb.dma("sp", ls[:, :], self.s5_ls[l, d], ls, w=[ls])
    kb.dma("sp", br[:, :, :], self.s5_brT[l], br, w=[br])
    kb.dma("sp", bi[:, :, :], self.s5_biT[l], bi, w=[bi])
    rmask = tmpf("rmask", [128, 16, 2])
    kb.dma("sp", rmask[:, :, :], self.s5_rowmask[:, :, :], rmask, w=[rmask])
    V("act", lambda e: e.activation(out=ls[:, :], in_=ls[:, :], func=AF.Exp), [ls], [ls])
    dtb = ls[:, :].unsqueeze(2).broadcast_to([128, 4, 64])
    adt, th = tmpf("adt", [128, 4, 64]), tmpf("th", [128, 4, 64])
    V("dve", lambda e: e.tensor_tensor(out=adt[:, :, :], in0=are[:, :, :], in1=dtb, op=ALU.mult), [are, ls], [adt])
    V("act", lambda e: e.activation(out=adt[:, :, :], in_=adt[:, :, :], func=AF.Exp), [adt], [adt])
    V("dve", lambda e: e.tensor_tensor(out=th[:, :, :], in0=aim[:, :, :], in1=dtb, op=ALU.mult), [aim, ls], [th])
    Sd, Cd = tmpf("Sd", [128, 256]), tmpf("Cd", [128, 256])
    thf = th[:, :, :].rearrange("p a b -> p (a b)")
    self.sincos(thf, th, 256, Sd[:, :], Sd, Cd[:, :], Cd, st2)
    lr, li = tmpf("lr", [128, 256]), tmpf("li", [128, 256])
    magf = adt[:, :, :].rearrange("p a b -> p (a b)")
    V("dve", lambda e: e.tensor_tensor(out=lr[:, :], in0=magf, in1=Cd[:, :], op=ALU.mult), [adt, Cd], [lr])
    V("dve", lambda e: e.tensor_tensor(out=li[:, :], in0=magf, in1=Sd[:, :], op=ALU.mult), [adt, Sd], [li])
    aref = are[:, :, :].rearrange("p a b -> p (a b)")
    aimf = aim[:, :, :].rearrange("p a b -> p (a b)")
    t1, t2, den = tmpf("t1", [128, 256]), tmpf("t2", [128, 256]), tmpf("den", [128, 256])
    V("dve", lambda e: e.tensor_tensor(out=t1[:, :], in0=aref, in1=aref, op=ALU.mult), [are], [t1])
    V("dve", lambda e: e.tensor_tensor(out=t2[:, :], in0=aimf, in1=aimf, op=ALU.mult), [aim], [t2])
    V("dve", lambda e: e.tensor_tensor(out=den[:, :], in0=t1[:, :], in1=t2[:, :], op=ALU.add), [t1, t2], [den])
    V("dve", lambda e: e.reciprocal(out=den[:, :], in_=den[:, :]), [den], [den])
    V("dve", lambda e: e.tensor_scalar(out=lr[:, :], in0=lr[:, :], scalar1=-1.0, scalar2=None, op0=ALU.add),
      [lr], [lr])
    cr, ci = tmpf("cr", [128, 256]), tmpf("ci", [128, 256])
    V("dve", lambda e: e.tensor_tensor(out=t1[:, :], in0=lr[:, :], in1=aref, op=ALU.mult), [lr, are], [t1])
    V("dve", lambda e: e.tensor_tensor(out=t2[:, :], in0=li[:, :], in1=aimf, op=ALU.mult), [li, aim], [t2])
    V("dve", lambda e: e.tensor_tensor(out=cr[:, :], in0=t1[:, :], in1=t2[:, :], op=ALU.add), [t1, t2], [cr])
    V("dve", lambda e: e.tensor_tensor(out=cr[:, :], in0=cr[:, :], in1=den[:, :], op=ALU.mult), [cr, den], [cr])
    V("dve", lambda e: e.tensor_tensor(out=t1[:, :], in0=li[:, :], in1=aref, op=ALU.mult), [li, are], [t1])
    V("dve", lambda e: e.tensor_tensor(out=t2[:, :], in0=lr[:, :], in1=aimf, op=ALU.mult), [lr, aim], [t2])
    V("dve", lambda e: e.tensor_tensor(out=ci[:, :], in0=t1[:, :], in1=t2[:, :], op=ALU.subtract), [t1, t2], [ci])
    V("dve", lambda e: e.tensor_tensor(out=ci[:, :], in0=ci[:, :], in1=den[:, :], op=ALU.mult), [ci, den], [ci])
    brf = br[:, :, :].rearrange("p a b -> p (a b)")
    bif = bi[:, :, :].rearrange("p a b -> p (a b)")
    bbr, bbi = tmpf("bbr", [128, 4, 64]), tmpf("bbi", [128, 4, 64])
    bbrf = bbr[:, :, :].rearrange("p a b -> p (a b)")
    bbif = bbi[:, :, :].rearrange("p a b -> p (a b)")
    V("dve", lambda e: e.tensor_tensor(out=t1[:, :], in0=cr[:, :], in1=brf, op=ALU.mult), [cr, br], [t1])
    V("dve", lambda e: e.tensor_tensor(out=t2[:, :], in0=ci[:, :], in1=bif, op=ALU.mult), [ci, bi], [t2])
    V("dve", lambda e: e.tensor_tensor(out=bbrf, in0=t1[:, :], in1=t2[:, :], op=ALU.subtract), [t1, t2], [bbr])
    V("dve", lambda e: e.tensor_tensor(out=t1[:, :], in0=cr[:, :], in1=bif, op=ALU.mult), [cr, bi], [t1])
    V("dve", lambda e: e.tensor_tensor(out=t2[:, :], in0=ci[:, :], in1=brf, op=ALU.mult), [ci, br], [t2])
    V("dve", lambda e: e.tensor_tensor(out=bbif, in0=t1[:, :], in1=t2[:, :], op=ALU.add), [t1, t2], [bbi])
    DR, DI = sbf("DR", [128, 16, 128], BF16), sbf("DI", [128, 16, 128], BF16)
    for j in range(16):
        for gp in range(2):
            for (dst, srcb) in ((DR, bbr), (DI, bbi)):
                V("dve", lambda e: e.tensor_scalar(out=dst[:, j, gp * 64:(gp + 1) * 64], in0=srcb[:, j // 4, :],
                                                   scalar1=rmask[:, j, gp:gp + 1], scalar2=None, op0=ALU.mult),
                  [srcb, rmask], [dst])
    are2, aim2, ls2 = tmpf("are2", [128, 16]), tmpf("aim2", [128, 16]), tmpf("ls2", [128, 16])
    kb.dma("sp", are2[:, :], self.s5_are2[l, d], are2, w=[are2])
    kb.dma("sp", aim2[:, :], self.s5_aim2[l, d], aim2, w=[aim2])
    kb.dma("sp", ls2[:, :], self.s5_ls2[l, d], ls2, w=[ls2])
    tau = tmpf("tau", [128, 128])
    kb.dma("sp", tau[:, :], self.tauT[:, :], tau, w=[tau])
    V("act", lambda e: e.activation(out=ls2[:, :], in_=ls2[:, :], func=AF.Exp), [ls2], [ls2])
    rho, th2 = sbf("rho", [128, 16]), tmpf("th2", [128, 16])
    V("dve", lambda e: e.tensor_tensor(out=rho[:, :], in0=are2[:, :], in1=ls2[:, :], op=ALU.mult), [are2, ls2], [rho])
    V("act", lambda e: e.activation(out=rho[:, :], in_=rho[:, :], func=AF.Exp), [rho], [rho])
    V("dve", lambda e: e.tensor_tensor(out=th2[:, :], in0=aim2[:, :], in1=ls2[:, :], op=ALU.mult), [aim2, ls2], [th2])
    ang = tmpf("ang", [128, 16, 128])
    V("dve", lambda e: e.tensor_tensor(out=ang[:, :, :], in0=th2[:, :].unsqueeze(2).broadcast_to([128, 16, 128]),
                                       in1=tau[:, :].unsqueeze(1).broadcast_to([128, 16, 128]), op=ALU.mult),
      [th2, tau], [ang])
    COS, SIN = sbf("COS", [128, 16, 128]), sbf("SIN", [128, 16, 128])
    self.sincos(ang[:, :, :].rearrange("p a b -> p (a b)"), ang, 2048,
                SIN[:, :, :].rearrange("p a b -> p (a b)"), SIN, COS[:, :, :].rearrange("p a b -> p (a b)"), COS, st2)
    S1, C1 = tmpf("S1", [128, 16]), tmpf("C1", [128, 16])
    self.sincos(th2[:, :], th2, 16, S1[:, :], S1, C1[:, :], C1, st2)
    lr2, li2 = sbf("lr2", [128, 16]), sbf("li2", [128, 16])
    V("dve", lambda e: e.tensor_tensor(out=lr2[:, :], in0=rho[:, :], in1=C1[:, :], op=ALU.mult), [rho, C1], [lr2])
    V("dve", lambda e: e.tensor_tensor(out=li2[:, :], in0=rho[:, :], in1=S1[:, :], op=ALU.mult), [rho, S1], [li2])
    RHO0 = sbf("RHO0", [128, 16, 128])
    V("dve", lambda e: e.tensor_copy(out=RHO0[:, :, :], in_=rho[:, :].unsqueeze(2).broadcast_to([128, 16, 128])),
      [rho], [RHO0])
    f0 = 127 if d == 1 else 0
    V("dve", lambda e: e.memset(RHO0[:, :, f0:f0 + 1], 0.0), [], [RHO0])
    crT, ciT = tmpf("crT", [128, 16, 16]), tmpf("ciT", [128, 16, 16])
    kb.dma("sp", crT[:, :, :], self.s5_crT[l], crT, w=[crT])
    kb.dma("sp", ciT[:, :, :], self.s5_ciT[l], ciT, w=[ciT])
    CR, CIn = sbf("CR", [128, 16, 128], BF16), sbf("CIn", [128, 16, 128], BF16)
    V("dve", lambda e: e.memset(CR[:, :, :], 0.0), [], [CR])
    V("dve", lambda e: e.memset(CIn[:, :, :], 0.0), [], [CIn])
    for j in range(16):
        for gp in range(2):
            c0 = 32 * (j % 4) + 16 * gp
            V("dve", lambda e: e.tensor_copy(out=CR[gp * 64:(gp + 1) * 64, j, c0:c0 + 16],
                                             in_=crT[gp * 64:(gp + 1) * 64, j, :]), [crT], [CR])
            V("dve", lambda e: e.tensor_scalar(out=CIn[gp * 64:(gp + 1) * 64, j, c0:c0 + 16],
                                               in0=ciT[gp * 64:(gp + 1) * 64, j, :], scalar1=-1.0, scalar2=None,
                                               op0=ALU.mult), [ciT], [CIn])
    if d == 0:
        dv, gb = sbf("s5d", [128, 4]), sbf("s5gb", [128, 4])
        kb.dma("sp", dv[:, :], self.s5_dT[:, l * 4:(l + 1) * 4], dv, w=[dv])
        kb.dma("sp", gb[:, :], self.s5_glub[:, l * 4:(l + 1) * 4], gb, w=[gb])
        gw = sbf("gluw", [128, 4, 512], BF16)
        for c in range(4):
            kb.dma("pool", gw[:, c, :], self.s5_gluw[l, c * 128:(c + 1) * 128, :], gw, w=[gw])
    kb.barrier()
    st2.close()
    usp = Pool(kb, "s5u", 2, [128, 4, 128], BF16, stack=st)
    usfp = Pool(kb, "s5uf", 2, [128, 4, 128], F32, stack=st)
    ybp = Pool(kb, "s5yb", 2, [128, 4, 128], F32, stack=st)
    mp = [Pool(kb, "s5m%d" % i, 1, [128, 8, 128], F32, stack=st) for i in range(4)]
    ZR, ZI = sbf("ZR", [128, 16, 128]), sbf("ZI", [128, 16, 128])
    XZR, XZI = sbf("XZR", [128, 16, 128]), sbf("XZI", [128, 16, 128])
    up_ = [Pool(kb, "s5t%d" % i, 1, [128, 16, 128], F32, stack=st) for i in range(2)]
    XR, XI = sbf("XR", [128, 16, 128], BF16), sbf("XI", [128, 16, 128], BF16)
    xlr, xli = sbf("xlr", [128, 16]), sbf("xli", [128, 16])
    cjr, cji = sbf("cjr", [128, 16]), sbf("cji", [128, 16])
    tt = [sbf("s5tt%d" % i, [128, 16]) for i in range(4)]
    yo = Pool(kb, "s5yo", 2, [128, 4, 128], F32, stack=st)
    rev = (d == 1)
    R3 = (lambda ap: ap[:, :, ::-1]) if rev else (lambda ap: ap)
    first, last = (127, 0) if rev else (0, 127)
    order = [0, 1] + list(range(2, nchunk))
    if rev:
        order = [1, 0] + list(range(nchunk - 1, 1, -1))
    yield
    for ci_, ch in enumerate(order):
        if ci_ > 0:
            yield
        t0 = ch * 128
        us = usp.next()
        kb.dma("pool", us[:, :, :], usv[:, :, t0:t0 + 128], us, r=[("dram", id(usS))], w=[us])
        for hf in range(2):
            PR, PI = PS2.next(), PS2.next()
            for jj in range(8):
                j = hf * 8 + jj
                kb.op("pe", lambda e: e.matmul(PR[:, jj * 128:(jj + 1) * 128], lhsT=DR[:, j, :], rhs=us[:, j // 4, :],
                                               start=True, stop=True), r=[DR, us], w=[PR])
                kb.op("pe", lambda e: e.matmul(PI[:, jj * 128:(jj + 1) * 128], lhsT=DI[:, j, :], rhs=us[:, j // 4, :],
                                               start=True, stop=True), r=[DI, us], w=[PI])
            prv = PR[:, :].rearrange("p (a b) -> p a b", a=8)
            piv = PI[:, :].rearrange("p (a b) -> p a b", a=8)
            cs = R3(COS[:, hf * 8:(hf + 1) * 8, :])
            sn = R3(SIN[:, hf * 8:(hf + 1) * 8, :])
            m = [p.next() for p in mp]
            V("dve", lambda e: e.tensor_tensor(out=m[0][:, :, :], in0=prv, in1=cs, op=ALU.mult), [PR, COS], [m[0]])
            V("dve", lambda e: e.tensor_tensor(out=m[1][:, :, :], in0=piv, in1=sn, op=ALU.mult), [PI, SIN], [m[1]])
            V("dve", lambda e: e.tensor_tensor(out=m[2][:, :, :], in0=piv, in1=cs, op=ALU.mult), [PI, COS], [m[2]])
            V("dve", lambda e: e.tensor_tensor(out=m[3][:, :, :], in0=prv, in1=sn, op=ALU.mult), [PR, SIN], [m[3]])
            V("pool", lambda e: e.tensor_tensor(out=ZR[:, hf * 8:(hf + 1) * 8, :], in0=m[0][:, :, :], in1=m[1][:, :, :],
                                                op=ALU.add), [m[0], m[1]], [ZR])
            V("pool", lambda e: e.tensor_tensor(out=ZI[:, hf * 8:(hf + 1) * 8, :], in0=m[2][:, :, :], in1=m[3][:, :, :],
                                                op=ALU.subtract), [m[2], m[3]], [ZI])
            yield
        if ci_ > 0:
            V("pool", lambda e: e.tensor_tensor(out=tt[0][:, :], in0=lr2[:, :], in1=xlr[:, :], op=ALU.mult), [lr2, xlr], [tt[0]])
            V("pool", lambda e: e.tensor_tensor(out=tt[1][:, :], in0=li2[:, :], in1=xli[:, :], op=ALU.mult), [li2, xli], [tt[1]])
            V("pool", lambda e: e.tensor_tensor(out=cjr[:, :], in0=tt[0][:, :], in1=tt[1][:, :], op=ALU.subtract), [tt[0], tt[1]], [cjr])
            V("pool", lambda e: e.tensor_tensor(out=tt[2][:, :], in0=lr2[:, :], in1=xli[:, :], op=ALU.mult), [lr2, xli], [tt[2]])
            V("pool", lambda e: e.tensor_tensor(out=tt[3][:, :], in0=li2[:, :], in1=xlr[:, :], op=ALU.mult), [li2, xlr], [tt[3]])
            V("pool", lambda e: e.tensor_tensor(out=cji[:, :], in0=tt[2][:, :], in1=tt[3][:, :], op=ALU.add), [tt[2], tt[3]], [cji])
            V("pool", lambda e: e.tensor_tensor(out=ZR[:, :, first], in0=ZR[:, :, first], in1=cjr[:, :], op=ALU.add), [ZR, cjr], [ZR])
            V("pool", lambda e: e.tensor_tensor(out=ZI[:, :, first], in0=ZI[:, :, first], in1=cji[:, :], op=ALU.add), [ZI, cji], [ZI])
        fl = lambda b_: (b_[:, :, :].rearrange("p a b -> p (a b)")[:, ::-1] if rev
                         else b_[:, :, :].rearrange("p a b -> p (a b)"))
        V("dve", lambda e: e.tensor_tensor_scan(out=fl(XZR), data0=fl(RHO0), data1=fl(ZR), initial=0.0,
                                                op0=ALU.mult, op1=ALU.add), [RHO0, ZR], [XZR])
        yield
        V("dve", lambda e: e.tensor_tensor_scan(out=fl(XZI), data0=fl(RHO0), data1=fl(ZI), initial=0.0,
                                                op0=ALU.mult, op1=ALU.add), [RHO0, ZI], [XZI])
        yield
        cs, sn = R3(COS[:, :, :]), R3(SIN[:, :, :])
        ua, ub = up_[0].next(), up_[1].next()
        V("dve", lambda e: e.tensor_tensor(out=ua[:, :, :], in0=XZR[:, :, :], in1=cs, op=ALU.mult), [XZR, COS], [ua])
        V("pool", lambda e: e.tensor_tensor(out=ub[:, :, :], in0=XZI[:, :, :], in1=sn, op=ALU.mult), [XZI, SIN], [ub])
        V("dve", lambda e: e.tensor_tensor(out=XR[:, :, :], in0=ua[:, :, :], in1=ub[:, :, :], op=ALU.subtract), [ua, ub], [XR])
        V("pool", lambda e: e.tensor_tensor(out=xlr[:, :], in0=ua[:, :, last], in1=ub[:, :, last], op=ALU.subtract), [ua, ub], [xlr])
        yield
        V("pool", lambda e: e.tensor_tensor(out=ua[:, :, :], in0=XZR[:, :, :], in1=sn, op=ALU.mult), [XZR, SIN], [ua])
        V("dve", lambda e: e.tensor_tensor(out=ub[:, :, :], in0=XZI[:, :, :], in1=cs, op=ALU.mult), [XZI, COS], [ub])
        V("pool", lambda e: e.tensor_tensor(out=XI[:, :, :], in0=ua[:, :, :], in1=ub[:, :, :], op=ALU.add), [ua, ub], [XI])
        V("pool", lambda e: e.tensor_tensor(out=xli[:, :], in0=ua[:, :, last], in1=ub[:, :, last], op=ALU.add), [ua, ub], [xli])
        yield
        PY = PS2.next()
        for cc in range(4):
            for jj in range(4):
                j = cc * 4 + jj
                kb.op("pe", lambda e: e.matmul(PY[:, cc * 128:(cc + 1) * 128], lhsT=CR[:, j, :], rhs=XR[:, j, :],
                                               start=(jj == 0), stop=False), r=[CR, XR], w=[PY])
                kb.op("pe", lambda e: e.matmul(PY[:, cc * 128:(cc + 1) * 128], lhsT=CIn[:, j, :], rhs=XI[:, j, :],
                                               start=False, stop=(jj == 3)), r=[CIn, XI], w=[PY])
        pyv = PY[:, 0:512].rearrange("p (a b) -> p a b", a=4)
        if d == 1:
            y = yo.next()
            V("act", lambda e: e.activation(out=y[:, :, :], in_=pyv, func=AF.Copy), [PY], [y])
            kb.dma("act", ybv[:, :, t0:t0 + 128], y[:, :, :], y, r=[y], w=[("dram", id(ysbS))])
        else:
            yb, usf = ybp.next(), usfp.next()
            kb.dma("sp", yb[:, :, :], ybv[:, :, t0:t0 + 128], yb, r=[("dram", id(ysbS))], w=[yb])
            kb.dma("sp", usf[:, :, :], usv[:, :, t0:t0 + 128], usf, r=[("dram", id(usS))], w=[usf])
            y = yo.next()
            V("dve", lambda e: e.tensor_tensor(out=y[:, :, :], in0=pyv, in1=yb[:, :, :], op=ALU.add), [PY, yb], [y])
            V("pool", lambda e: e.tensor_tensor(out=usf[:, :, :], in0=usf[:, :, :],
                                                in1=dv[:, :].unsqueeze(2).broadcast_to([128, 4, 128]), op=ALU.mult),
              [usf, dv], [usf])
            V("pool", lambda e: e.tensor_tensor(out=y[:, :, :], in0=y[:, :, :], in1=usf[:, :, :], op=ALU.add), [y, usf], [y])
            g1 = yb
            V("pool", lambda e: e.tensor_tensor(out=g1[:, :, :], in0=y[:, :, :], in1=y[:, :, :], op=ALU.mult), [y], [g1])
            V("dve", lambda e: e.tensor_scalar(out=g1[:, :, :], in0=g1[:, :, :], scalar1=0.044715, scalar2=1.0,
                                               op0=ALU.mult, op1=ALU.add), [g1], [g1])
            V("dve", lambda e: e.tensor_tensor(out=g1[:, :, :], in0=g1[:, :, :], in1=y[:, :, :], op=ALU.mult), [g1, y], [g1])
            V("act", lambda e: e.activation(out=g1[:, :, :], in_=g1[:, :, :], func=AF.Sigmoid, scale=1.5957691216057308),
              [g1], [g1])
            V("dve", lambda e: e.tensor_tensor(out=y[:, :, :], in0=y[:, :, :], in1=g1[:, :, :], op=ALU.mult), [y, g1], [y])
            geb = us
            V("act", lambda e: e.activation(out=geb[:, :, :], in_=y[:, :, :], func=AF.Copy), [y], [geb])
            PG = PS2.next()
            for oc in range(4):
                for kc in range(4):
                    kb.op("pe", lambda e: e.matmul(PG[:, oc * 128:(oc + 1) * 128], lhsT=gw[:, kc, oc * 128:(oc + 1) * 128],
                                                   rhs=geb[:, kc, :], start=(kc == 0), stop=(kc == 3)), r=[gw, geb], w=[PG])
                V("act", lambda e: e.activation(out=usf[:, oc, :], in_=PG[:, oc * 128:(oc + 1) * 128], func=AF.Sigmoid,
                                                bias=gb[:, oc:oc + 1]), [PG, gb], [usf])
            yso = usp.next()
            V("dve", lambda e: e.tensor_tensor(out=yso[:, :, :], in0=y[:, :, :], in1=usf[:, :, :], op=ALU.mult), [y, usf], [yso])
            kb.dma("sp", ysv[:, :, t0:t0 + 128], yso[:, :, :], yso, r=[yso], w=[("dram", id(ysS))])


Model.s5_declare = _s5_declare
Model.sincos = _sincos
Model.s5_dir = _s5_dir


def host_inputs_s5(inp):
    L = DEPTH
    d = {}

    def drive(a):
        a = a.reshape(L, 2, 4, 8, 1, 64)
        a = np.broadcast_to(a, (L, 2, 4, 8, 16, 64))
        return np.ascontiguousarray(a.transpose(0, 1, 3, 4, 2, 5).reshape(L, 2, 128, 4, 64))
    d["s5_are"] = drive(inp["s5_a_re"])
    d["s5_aim"] = drive(inp["s5_a_im"])
    lsd = inp["s5_log_step"].reshape(L, 2, 4, 8, 1)
    d["s5_ls"] = np.ascontiguousarray(np.broadcast_to(lsd, (L, 2, 4, 8, 16)).transpose(0, 1, 3, 4, 2).reshape(L, 2, 128, 4))

    def bT(b):
        b = b.reshape(L, 4, 8, 64, 16)
        return np.ascontiguousarray(b.transpose(0, 2, 4, 1, 3).reshape(L, 128, 4, 64))
    d["s5_brT"] = bT(inp["s5_b_re"])
    d["s5_biT"] = bT(inp["s5_b_im"])

    def st2(a):
        a = a.reshape(L, 2, 16, 2, 64)
        return np.ascontiguousarray(a.transpose(0, 1, 3, 4, 2).reshape(L, 2, 128, 16))
    d["s5_are2"] = st2(inp["s5_a_re"])
    d["s5_aim2"] = st2(inp["s5_a_im"])
    ls2 = np.broadcast_to(inp["s5_log_step"].reshape(L, 2, 16, 2, 1), (L, 2, 16, 2, 64))
    d["s5_ls2"] = np.ascontiguousarray(ls2.transpose(0, 1, 3, 4, 2).reshape(L, 2, 128, 16))

    def cT(c):
        c = c.reshape(L, 16, 2, 16, 64)
        return np.ascontiguousarray(c.transpose(0, 2, 4, 1, 3).reshape(L, 128, 16, 16))
    d["s5_crT"] = cT(inp["s5_c_re"])
    d["s5_ciT"] = cT(inp["s5_c_im"])
    k = np.arange(128)[:, None, None] // 16
    j = np.arange(16)[None, :, None]
    gp = np.arange(2)[None, None, :]
    d["s5_rowmask"] = np.ascontiguousarray((k == 2 * (j % 4) + gp).astype(np.float32))
    d["s5_dT"] = np.ascontiguousarray(inp["s5_d"].reshape(L * 4, 128).T)
    d["s5_glub"] = np.ascontiguousarray(inp["s5_glu_b"].reshape(L * 4, 128).T)
    d["s5_gluw"] = inp["s5_glu_w"]
    d["tauT"] = np.ascontiguousarray(np.broadcast_to(np.arange(128, dtype=np.float32)[None, :], (128, 128)))
    return d


NCH_B = 15


def _rwkv_declare(self):
    L = DEPTH
    self.w_inB = self.din("w_inB", [L, D, NCH_B * 128])
    self.rk_mu = self.din("rk_mu", [128, L * NCH_B])
    self.rk_w0 = self.din("rk_w0", [128, L * 8])
    for n in ("a0", "kk", "ka", "rk", "gng", "gnb"):
        setattr(self, "rk_" + n, self.din("rk_" + n, [128, L * 4]))
    self.rk_w2 = self.din("rk_w2", [L, 128, 512])
    self.rk_a2 = self.din("rk_a2", [L, 64, 512])
    self.rk_g2 = self.din("rk_g2", [L, 128, 512])
    self.bd64 = self.din("bd64", [128, 128])
    self.identb = self.din("identb", [128, 128])
    self.ones0 = self.din("ones0", [2, 128, 128])
    self.rk_MT = self.din("rk_MT", [2, 128, 512])
    self.rk_MN = self.din("rk_MN", [2, 128, 512])


def _rwkv_prep_phase(self, l, src, S):
    kb = self.kb
    W = 256
    Wh = W + 2
    NB = W // 128
    with kb.phase() as st:
        self.psum = Pool(kb, "ps", 8, [128, 512], F32, psum=True, stack=st)
        V = lambda e_, f, r, w: kb.op(e_, f, r=r, w=w)
        sbf = lambda n, shp, dt=F32: kb.sb(n, shp, dt, st)
        w = sbf("wB", [128, NCH, NCH_B * 128], BF16)
        for c in range(NCH):
            kb.dma("pool", w[:, c, :], self.w_inB[l, c * 128:(c + 1) * 128, :], w, w=[w])
        w2b, a2b, g2b = sbf("w2b", [128, 512], BF16), sbf("a2b", [64, 512], BF16), sbf("g2b", [128, 512], BF16)
        kb.dma("pool", w2b[:, :], self.rk_w2[l], w2b, w=[w2b])
        kb.dma("pool", a2b[:, :], self.rk_a2[l], a2b, w=[a2b])
        kb.dma("pool", g2b[:, :], self.rk_g2[l], g2b, w=[g2b])
        bd, idb = sbf("bd", [128, 128], BF16), sbf("idb", [128, 128], BF16)
        kb.dma("pool", bd[:, :], self.bd64[:, :], bd, w=[bd])
        kb.dma("pool", idb[:, :], self.identb[:, :], idb, w=[idb])
        on0 = sbf("on0", [128, 2, 128])
        kb.dma("sp", on0[:, :, :], self.ones0.rearrange("d p t -> p d t"), on0, w=[on0])
        ON = [sbf("ON%d" % d_, [128, 4 * (W // 128), 128]) for d_ in range(2)]
        for d_ in range(2):
            V("dve", lambda e: e.tensor_copy(out=ON[d_][:, :, :], in_=on0[:, d_:d_ + 1, :].broadcast_to([128, 4 * (W // 128), 128])),
              [on0], [ON[d_]])
        mu, omu, hmu = sbf("mu", [128, NCH_B]), sbf("omu", [128, NCH_B]), sbf("hmu", [128, NCH_B])
        kb.dma("sp", mu[:, :], self.rk_mu[:, l * NCH_B:(l + 1) * NCH_B], mu, w=[mu])
        V("dve", lambda e: e.tensor_scalar(out=omu[:, :], in0=mu[:, :], scalar1=-1.0, scalar2=1.0, op0=ALU.mult, op1=ALU.add), [mu], [omu])
        V("dve", lambda e: e.tensor_scalar(out=hmu[:, :], in0=mu[:, :], scalar1=0.5, scalar2=None, op0=ALU.mult), [mu], [hmu])
        w0 = sbf("w0", [128, 8])
        kb.dma("sp", w0[:, :], self.rk_w0[:, l * 8:(l + 1) * 8], w0, w=[w0])
        pv = {}
        for n in ("a0", "kk", "ka", "rk"):
            pv[n] = sbf("p_" + n, [128, 4])
            kb.dma("sp", pv[n][:, :], getattr(self, "rk_" + n)[:, l * 4:(l + 1) * 4], pv[n], w=[pv[n]])
        omka = sbf("omka", [128, 4])
        V("dve", lambda e: e.tensor_scalar(out=omka[:, :], in0=pv["ka"][:, :], scalar1=-1.0, scalar2=1.0, op0=ALU.mult, op1=ALU.add),
          [pv["ka"]], [omka])
        pl = self.stat_pools(Wh, st)
        xp = Pool(kb, "x", 2, [128, NCH, Wh], F32, stack=st)
        xn = sbf("xn", [128, NCH, Wh])
        u = sbf("u", [128, NCH, Wh], BF16)
        zcp = Pool(kb, "zc", 2, [128, Wh], F32, stack=st)
        tmpp = Pool(kb, "ztmp", 2, [128, W], F32, stack=st)
        Z = sbf("Z", [128, NCH_B, W])
        tw, sg, alb = sbf("tw", [128, W], BF16), sbf("sg", [128, W], BF16), sbf("alb", [64, W], BF16)
        LW = [sbf("LW%d" % d_, [128, 4, W]) for d_ in range(2)]
        A, KK, KM, Bv = sbf("A", [128, 4, W]), sbf("KK", [128, 4, W]), sbf("KM", [128, 4, W]), sbf("Bv", [128, 4, W])
        SQ = sbf("SQ", [128, 4, W], BF16)
        T1, T2 = sbf("T1", [128, 4, W]), sbf("T2", [128, 4, W])
        Gp = Pool(kb, "Go", 2, [128, 4, W], BF16, stack=st)
        Bop = Pool(kb, "Bo", 2, [128, 4, W], BF16, stack=st)
        Vb = sbf("Vb", [128, 4, W], BF16)
        tokp = Pool(kb, "tok", 3, [128, NB, 512], BF16, stack=st)
        Lc, Lr, Lq = sbf("Lc", [128, 4, W]), sbf("Lr", [128, 4, W]), sbf("Lq", [128, 4, W])
        E = sbf("E", [128, 4, W])
        outp = {n: Pool(kb, n, 2, [128, 4, W], BF16, stack=st) for n in ("RHO", "KAP", "BET", "KTI")}
        scp = Pool(kb, "sco", 2, [128, NB, 4, 3], F32, stack=st)
        lmn = sbf("lmn", [128, 4, NB])
        srcv = src.rearrange("(c p) t -> p c t", p=128)
        fmv = lambda t_: t_.rearrange("(c p) t -> p c t", p=128)

        def transpose_store(srcb, dst, t0):
            tk = tokp.next()
            for tb in range(NB):
                P = self.psum.next()
                pb = P[:, 0:256].bitcast(BF16)
                for c in range(4):
                    kb.op("pe", lambda e: e.transpose(pb[:, c * 128:(c + 1) * 128], srcb[:, c, tb * 128:(tb + 1) * 128], idb[:, :]),
                          r=[srcb, idb], w=[P])
                V("act", lambda e: e.activation(out=tk[:, tb, :], in_=pb[:, 0:512], func=AF.Copy), [P], [tk])
            kb.dma("sp", dst[t0:t0 + W, :].rearrange("(b p) n -> p b n", p=128), tk[:, :, :], tk, r=[tk], w=[("dram", id(dst))])

        for (seg, t0, _) in self.tiles(W):
            seg_lo, seg_hi = (0, CTX) if seg == 1 else (CTX, self.NT)
            lo, hi = max(t0 - 1, seg_lo), min(t0 + W + 1, seg_hi)
            x = xp.next()
            c0 = lo - (t0 - 1)
            if c0 > 0:
                V("dve", lambda e: e.memset(x[:, :, 0:1], 0.0), [], [x])
            if hi < t0 + W + 1:
                V("dve", lambda e: e.memset(x[:, :, Wh - 1:Wh], 0.0), [], [x])
            kb.dma("sp", x[:, :, c0:c0 + (hi - lo)], srcv[:, :, lo:hi], x, r=[("dram", id(src))], w=[x])
            rstd, nmr = self.ln_stats_w(x, Wh, pl)
            self.normalize(xn, x, Wh, rstd, nmr)
            for c in range(NCH):
                V("act", lambda e: e.activation(out=u[:, c, :], in_=xn[:, c, :], func=AF.Identity,
                                                scale=self.modp1[:, 4 * NCH + c, seg:seg + 1],
                                                bias=self.mods[:, 3 * NCH + c, seg:seg + 1]), [xn, self.modp1, self.mods], [u])
            if c0 > 0:
                V("dve", lambda e: e.memset(u[:, :, 0:1], 0.0), [], [u])
            if hi < t0 + W + 1:
                V("dve", lambda e: e.memset(u[:, :, Wh - 1:Wh], 0.0), [], [u])
            for ch in range(NCH_B):
                P = self.psum.next()
                for c in range(NCH):
                    kb.op("pe", lambda e: e.matmul(P[:, 0:Wh], lhsT=w[:, c, ch * 128:(ch + 1) * 128], rhs=u[:, c, :],
                                                   start=(c == 0), stop=(c == NCH - 1)), r=[w, u], w=[P])
                zc, tm = zcp.next(), tmpp.next()
                V("act", lambda e: e.activation(out=zc[:, :], in_=P[:, 0:Wh], func=AF.Copy), [P], [zc])
                V("dve", lambda e: e.tensor_tensor(out=tm[:, :], in0=zc[:, 0:W], in1=zc[:, 2:W + 2], op=ALU.add), [zc], [tm])
                V("act", lambda e: e.activation(out=tm[:, :], in_=tm[:, :], func=AF.Identity, scale=hmu[:, ch:ch + 1]), [tm, hmu], [tm])
                V("dve", lambda e: e.scalar_tensor_tensor(out=Z[:, ch, :], in0=zc[:, 1:W + 1], scalar=omu[:, ch:ch + 1],
                                                          in1=tm[:, :], op0=ALU.mult, op1=ALU.add), [zc, omu, tm], [Z])
            R_, K_, V_ = Z[:, 0:4, :], Z[:, 4:8, :], Z[:, 8:12, :]
            V("act", lambda e: e.activation(out=tw[:, :], in_=Z[:, 12, :], func=AF.Tanh), [Z], [tw])
            V("act", lambda e: e.activation(out=sg[:, :], in_=Z[:, 13, :], func=AF.Sigmoid), [Z], [sg])
            V("act", lambda e: e.activation(out=alb[:, :], in_=Z[0:64, 14, :], func=AF.Copy), [Z], [alb])
            for d_ in range(2):
                for c in range(4):
                    P = self.psum.next()
                    kb.op("pe", lambda e: e.matmul(P[:, 0:W], lhsT=w2b[d_ * 64:(d_ + 1) * 64, c * 128:(c + 1) * 128],
                                                   rhs=tw[d_ * 64:(d_ + 1) * 64, :], start=True, stop=True), r=[w2b, tw], w=[P])
                    V("act", lambda e: e.activation(out=LW[d_][:, c, :], in_=P[:, 0:W], func=AF.Sigmoid,
                                                    bias=w0[:, d_ * 4 + c:d_ * 4 + c + 1]), [P, w0], [LW[d_]])
                V("pool", lambda e: e.tensor_scalar(out=LW[d_][:, :, :], in0=LW[d_][:, :, :], scalar1=-DECAY_SCALE, scalar2=None,
                                                    op0=ALU.mult), [LW[d_]], [LW[d_]])
            Go = Gp.next()
            for c in range(4):
                P = self.psum.next()
                kb.op("pe", lambda e: e.matmul(P[:, 0:W], lhsT=a2b[0:64, c * 128:(c + 1) * 128], rhs=alb[0:64, :],
                                               start=True, stop=True), r=[a2b, alb], w=[P])
                V("act", lambda e: e.activation(out=A[:, c, :], in_=P[:, 0:W], func=AF.Sigmoid, bias=pv["a0"][:, c:c + 1]),
                  [P, pv["a0"]], [A])
                P = self.psum.next()
                kb.op("pe", lambda e: e.matmul(P[:, 0:W], lhsT=g2b[:, c * 128:(c + 1) * 128], rhs=sg[:, :],
                                               start=True, stop=True), r=[g2b, sg], w=[P])
                V("act", lambda e: e.activation(out=Go[:, c, :], in_=P[:, 0:W], func=AF.Copy), [P], [Go])
            kb.dma("sp", fmv(S["gR"])[:, :, t0:t0 + W], Go[:, :, :], Go, r=[Go], w=[("dram", id(S["gR"]))])
            for c in range(4):
                V("dve", lambda e: e.tensor_scalar(out=KK[:, c, :], in0=Z[:, 4 + c, :], scalar1=pv["kk"][:, c:c + 1], scalar2=None,
                                                    op0=ALU.mult), [Z, pv["kk"]], [KK])
            V("act", lambda e: e.activation(out=SQ[:, :, :], in_=KK[:, :, :], func=AF.Square), [KK], [SQ])
            for c in range(4):
                P = self.psum.next()
                kb.op("pe", lambda e: e.matmul(P[:, 0:W], lhsT=bd[:, :], rhs=SQ[:, c, :], start=True, stop=True), r=[bd, SQ], w=[P])
                V("dve", lambda e: e.tensor_scalar(out=T1[:, c, :], in0=P[:, 0:W], scalar1=1e-12, scalar2=None, op0=ALU.add), [P], [T1])
            V("act", lambda e: e.activation(out=T1[:, :, :], in_=T1[:, :, :], func=AF.Sqrt), [T1], [T1])
            V("dve", lambda e: e.reciprocal(out=T1[:, :, :], in_=T1[:, :, :]), [T1], [T1])
            V("dve", lambda e: e.tensor_tensor(out=KK[:, :, :], in0=KK[:, :, :], in1=T1[:, :, :], op=ALU.mult), [KK, T1], [KK])
            for c in range(4):
                V("dve", lambda e: e.tensor_scalar(out=T2[:, c, :], in0=A[:, c, :], scalar1=pv["ka"][:, c:c + 1],
                                                   scalar2=omka[:, c:c + 1], op0=ALU.mult, op1=ALU.add), [A, pv["ka"], omka], [T2])
            V("pool", lambda e: e.tensor_tensor(out=KM[:, :, :], in0=K_, in1=T2[:, :, :], op=ALU.mult), [Z, T2], [KM])
            V("pool", lambda e: e.tensor_tensor(out=Bv[:, :, :], in0=KK[:, :, :], in1=A[:, :, :], op=ALU.mult), [KK, A], [Bv])
            V("dve", lambda e: e.tensor_tensor(out=T1[:, :, :], in0=R_, in1=KM[:, :, :], op=ALU.mult), [Z, KM], [T1])
            for c in range(4):
                V("act", lambda e: e.activation(out=SQ[:, c, :], in_=T1[:, c, :], func=AF.Identity, scale=pv["rk"][:, c:c + 1]),
                  [T1, pv["rk"]], [SQ])
            Bo = Bop.next()
            for c in range(4):
                P = self.psum.next()
                kb.op("pe", lambda e: e.matmul(P[:, 0:W], lhsT=bd[:, :], rhs=SQ[:, c, :], start=True, stop=True), r=[bd, SQ], w=[P])
                V("dve", lambda e: e.tensor_tensor(out=Bo[:, c, :], in0=P[:, 0:W], in1=Z[:, 8 + c, :], op=ALU.mult), [P, Z], [Bo])
            kb.dma("sp", fmv(S["bon"])[:, :, t0:t0 + W], Bo[:, :, :], Bo, r=[Bo], w=[("dram", id(S["bon"]))])
            V("act", lambda e: e.activation(out=Vb[:, :, :], in_=V_, func=AF.Copy), [Z], [Vb])
            transpose_store(Vb, S["vt"], t0)
            for d_ in range(2):
                rev = d_ == 1
                fl = (lambda ap: ap.rearrange("p a b -> p (a b)")[:, ::-1]) if rev else (lambda ap: ap.rearrange("p a b -> p (a b)"))
                V("dve", lambda e: e.tensor_tensor_scan(out=fl(Lc[:, :, :]), data0=fl(ON[d_][:, :, :]), data1=fl(LW[d_][:, :, :]),
                                                        initial=0.0, op0=ALU.mult, op1=ALU.add), [ON[d_], LW[d_]], [Lc])
                mid, last = (64, 0) if rev else (63, 127)
                L4 = Lc[:, :, :].rearrange("p c (b t) -> p c b t", b=NB)
                sc = scp.next()
                V("dve", lambda e: e.tensor_copy(out=lmn[:, :, :], in_=L4[:, :, :, mid]), [Lc], [lmn])
                scv = sc[:, :, :, :].rearrange("p b c s -> p c b s")
                V("act", lambda e: e.activation(out=scv[:, :, :, 0], in_=lmn[:, :, :], func=AF.Exp), [lmn], [sc])
                V("act", lambda e: e.activation(out=scv[:, :, :, 2], in_=L4[:, :, :, last], func=AF.Exp), [Lc], [sc])
                V("dve", lambda e: e.tensor_tensor(out=scv[:, :, :, 1], in0=L4[:, :, :, last], in1=lmn[:, :, :], op=ALU.subtract),
                  [Lc, lmn], [sc])
                V("act", lambda e: e.activation(out=scv[:, :, :, 1], in_=scv[:, :, :, 1], func=AF.Exp), [sc], [sc])
                kb.dma("sp", S["sc"][d_][:, t0 // 128:t0 // 128 + NB, :, :], sc[:, :, :, :], sc, r=[sc], w=[("dram", id(S["sc"][d_]))])
                Lr4 = Lr[:, :, :].rearrange("p c (b t) -> p c b t", b=NB)
                V("pool", lambda e: e.tensor_tensor(out=Lr4, in0=L4, in1=lmn[:, :, :].unsqueeze(3).broadcast_to([128, 4, NB, 128]),
                                                    op=ALU.subtract), [Lc, lmn], [Lr])
                V("pool", lambda e: e.tensor_tensor(out=Lq[:, :, :], in0=Lr[:, :, :], in1=LW[d_][:, :, :], op=ALU.subtract),
                  [Lr, LW[d_]], [Lq])
                o = {n: outp[n].next() for n in outp}
                V("act", lambda e: e.activation(out=E[:, :, :], in_=Lr[:, :, :], func=AF.Exp), [Lr], [E])
                V("dve", lambda e: e.tensor_tensor(out=o["RHO"][:, :, :], in0=R_, in1=E[:, :, :], op=ALU.mult), [Z, E], [o["RHO"]])
                V("act", lambda e: e.activation(out=E[:, :, :], in_=Lq[:, :, :], func=AF.Exp), [Lq], [E])
                V("dve", lambda e: e.tensor_tensor(out=o["KAP"][:, :, :], in0=KK[:, :, :], in1=E[:, :, :], op=ALU.mult), [KK, E], [o["KAP"]])
                V("act", lambda e: e.activation(out=E[:, :, :], in_=Lr[:, :, :], func=AF.Exp, scale=-1.0), [Lr], [E])
                V("dve", lambda e: e.tensor_tensor(out=o["BET"][:, :, :], in0=Bv[:, :, :], in1=E[:, :, :], op=ALU.mult), [Bv, E], [o["BET"]])
                V("pool", lambda e: e.tensor_tensor(out=o["KTI"][:, :, :], in0=KM[:, :, :], in1=E[:, :, :], op=ALU.mult), [KM, E], [o["KTI"]])
                for n in ("RHO", "KAP", "BET", "KTI"):
                    kb.dma("sp", fmv(S[n][d_])[:, :, t0:t0 + W], o[n][:, :, :], o[n], r=[o[n]], w=[("dram", id(S[n][d_]))])
                transpose_store(o["BET"], S["bt"][d_], t0)
                transpose_store(o["KTI"], S["kt"][d_], t0)


def _ln_stats_w(self, x, Wc, pl):
    kb = self.kb
    xb, sq = pl["xb"].next(), pl["sq"].next()
    kb.op("act", lambda e: e.activation(out=xb[:, :, :Wc], in_=x[:, :, :Wc], func=AF.Copy), r=[x], w=[xb])
    kb.op("act", lambda e: e.activation(out=sq[:, :, :Wc], in_=x[:, :, :Wc], func=AF.Square), r=[x], w=[sq])
    P1, P2 = self.psum.next(), self.psum.next()
    for c in range(NCH):
        kb.op("pe", lambda e: e.matmul(P1[:, 0:Wc], lhsT=self.onesb[:, :], rhs=xb[:, c, :Wc],
                                       start=(c == 0), stop=(c == NCH - 1)), r=[xb, self.onesb], w=[P1])
    for c in range(NCH):
        kb.op("pe", lambda e: e.matmul(P2[:, 0:Wc], lhsT=self.onesb[:, :], rhs=sq[:, c, :Wc],
                                       start=(c == 0), stop=(c == NCH - 1)), r=[sq, self.onesb], w=[P2])
    m2, var, rstd, nmr = pl["m2"].next(), pl["var"].next(), pl["rstd"].next(), pl["nmr"].next()
    kb.op("act", lambda e: e.activation(out=m2[:, 0, :Wc], in_=P1[:, 0:Wc], func=AF.Square), r=[P1], w=[m2])
    kb.op("dve", lambda e: e.scalar_tensor_tensor(out=var[:, 0, :Wc], in0=P2[:, 0:Wc], scalar=LN_EPS,
                                                  in1=m2[:, 0, :Wc], op0=ALU.add, op1=ALU.subtract), r=[P2, m2], w=[var])
    kb.op("act", lambda e: e.activation(out=var[:, 0, :Wc], in_=var[:, 0, :Wc], func=AF.Sqrt), r=[var], w=[var])
    kb.op("dve", lambda e: e.reciprocal(out=rstd[:, 0, :Wc], in_=var[:, 0, :Wc]), r=[var], w=[rstd])
    kb.op("dve", lambda e: e.scalar_tensor_tensor(out=nmr[:, 0, :Wc], in0=P1[:, 0:Wc], scalar=-1.0,
                                                  in1=rstd[:, 0, :Wc], op0=ALU.mult, op1=ALU.mult), r=[P1, rstd], w=[nmr])
    return rstd, nmr


Model.rwkv_declare = _rwkv_declare
Model.rwkv_prep_phase = _rwkv_prep_phase
Model.ln_stats_w = _ln_stats_w


def _rwkv_scan_dir(self, l, d, S, st, psr):
    kb = self.kb
    NT = self.NT
    nchunk = NT // 128
    fmv = lambda t_: t_.rearrange("(c p) t -> p c t", p=128)
    V = lambda e_, f, r, w: kb.op(e_, f, r=r, w=w)
    sbf = lambda n, shp, dt=F32: kb.sb(n, shp, dt, st)
    MT, MN = sbf("MT", [128, 512], BF16), sbf("MN", [128, 512], BF16)
    kb.dma("pool", MT[:, :], self.rk_MT[d], MT, w=[MT])
    kb.dma("pool", MN[:, :], self.rk_MN[d], MN, w=[MN])
    KRp = Pool(kb, "KR", 2, [128, 4, 2, 128], BF16, stack=st)
    BTZp = Pool(kb, "BTZ", 2, [128, 4, 2, 128], BF16, stack=st)
    KTZp = Pool(kb, "KTZ", 2, [128, 4, 2, 128], BF16, stack=st)
    KAZp = Pool(kb, "KAZ", 2, [128, 4, 2, 128], BF16, stack=st)
    for p_ in (BTZp, KTZp, KAZp):
        for b_ in p_.bufs:
            V("pool", lambda e: e.memset(b_[:, :, :, :], 0.0), [], [b_])
    S0Z = sbf("S0Z", [128, 4, 2, 64], BF16)
    V("pool", lambda e: e.memset(S0Z[:, :, :, :], 0.0), [], [S0Z])
    St = sbf("St", [128, 4, 64])
    V("pool", lambda e: e.memset(St[:, :, :], 0.0), [], [St])
    St1 = sbf("St1", [128, 4, 64])
    tokp = {n: Pool(kb, n, 2, [128, 512], BF16, stack=st) for n in ("Btok", "Ktok", "Vtok")}
    scp = Pool(kb, "sc", 2, [128, 4, 3], F32, stack=st)
    AMp = Pool(kb, "AM", 2, [128, 8, 512], BF16, stack=st)
    Pm = [Pool(kb, "Pm%d" % i, 2, [128, 8, 128], BF16, stack=st) for i in range(2)]
    PTm = Pool(kb, "PTm", 2, [128, 8, 128], BF16, stack=st)
    X32, X16p = sbf("X32", [128, 512]), Pool(kb, "X16", 2, [128, 512], BF16, stack=st)
    Oop = Pool(kb, "Oo", 2, [128, 512], F32, stack=st)
    order = [0, 1] + list(range(2, nchunk))
    if d == 1:
        order = [1, 0] + list(range(nchunk - 1, 1, -1))
    ev = 0
    yield
    for ci_, ch in enumerate(order):
        if ci_ > 0:
            yield
        t0 = ch * 128
        KR, BTZ, KTZ, KAZ = KRp.next(), BTZp.next(), KTZp.next(), KAZp.next()
        kb.dma("sp", KR[:, :, 0, :], fmv(S["KAP"][d])[:, :, t0:t0 + 128], KR, r=[("dram", id(S["KAP"][d]))], w=[KR])
        kb.dma("sp", KR[:, :, 1, :], fmv(S["RHO"][d])[:, :, t0:t0 + 128], KR, r=[("dram", id(S["RHO"][d]))], w=[KR])
        for par in range(2):
            ps_ = slice(par * 64, (par + 1) * 64)
            kb.dma("sp", BTZ[ps_, :, par, :], fmv(S["BET"][d])[ps_, :, t0:t0 + 128], BTZ, r=[("dram", id(S["BET"][d]))], w=[BTZ])
            kb.dma("sp", KTZ[ps_, :, par, :], fmv(S["KTI"][d])[ps_, :, t0:t0 + 128], KTZ, r=[("dram", id(S["KTI"][d]))], w=[KTZ])
            kb.dma("sp", KAZ[ps_, :, par, :], fmv(S["KAP"][d])[ps_, :, t0:t0 + 128], KAZ, r=[("dram", id(S["KAP"][d]))], w=[KAZ])
        tk = {}
        for n, key in (("Btok", "bt"), ("Ktok", "kt"), ("Vtok", "vt")):
            tk[n] = tokp[n].next()
            srcd = S[key][d] if key != "vt" else S[key]
            kb.dma("sp", tk[n][:, :], srcd[t0:t0 + 128, :], tk[n], r=[("dram", id(srcd))], w=[tk[n]])
        Btok, Ktok, Vtok = tk["Btok"], tk["Ktok"], tk["Vtok"]
        sc = scp.next()
        kb.dma("sp", sc[:, :, :], S["sc"][d][:, ch, :, :], sc, r=[("dram", id(S["sc"][d]))], w=[sc])
        for par in range(2):
            ps_ = slice(par * 64, (par + 1) * 64)
            V("pool", lambda e: e.tensor_tensor(out=S0Z[ps_, :, par, :], in0=St[ps_, :, :],
                                                in1=sc[ps_, :, 0:1].broadcast_to([64, 4, 64]), op=ALU.mult), [St, sc], [S0Z])
        AM = AMp.next()
        for h in range(8):
            c, par = h // 2, h % 2
            P = psr.next()
            rhs = KR[:, c, :, :].rearrange("p s t -> p (s t)")
            kb.op("pe", lambda e: e.matmul(P[:, 0:256], lhsT=BTZ[:, c, par, :], rhs=rhs, start=True, stop=True), r=[BTZ, KR], w=[P])
            kb.op("pe", lambda e: e.matmul(P[:, 256:512], lhsT=KTZ[:, c, par, :], rhs=rhs, start=True, stop=True), r=[KTZ, KR], w=[P])
            V("dve", lambda e: e.tensor_tensor(out=AM[:, h, :], in0=P[:, :], in1=MT[:, :], op=ALU.mult), [P, MT], [AM])
            if h == 3:
                yield
        yield
        Pj, PTj = Pm[0].next(), PTm.next()
        for g4 in range(2):
            P = psr.next()
            for hh in range(4):
                h = g4 * 4 + hh
                c, par = h // 2, h % 2
                kb.op("pe", lambda e: e.matmul(P[:, hh * 128:(hh + 1) * 128], lhsT=KAZ[:, c, par, :], rhs=BTZ[:, c, par, :],
                                               start=True, stop=True), r=[KAZ, BTZ], w=[P])
            V("dve", lambda e: e.tensor_tensor(out=Pj[:, g4 * 4:(g4 + 1) * 4, :].rearrange("p h t -> p (h t)"), in0=P[:, :],
                                               in1=MN[:, :], op=ALU.mult), [P, MN], [Pj])
        V("act", lambda e: e.activation(out=PTj[:, :, :], in_=AM[:, :, 0:128], func=AF.Copy), [AM], [PTj])
        P = psr.next()
        for h in range(8):
            c, par = h // 2, h % 2
            kb.op("pe", lambda e: e.matmul(P[:, h * 64:(h + 1) * 64], lhsT=KR[:, c, 0, :], rhs=S0Z[:, c, par, :],
                                           start=True, stop=False), r=[KR, S0Z], w=[P])
            kb.op("pe", lambda e: e.matmul(P[:, h * 64:(h + 1) * 64], lhsT=AM[:, h, 256:384], rhs=Vtok[:, h * 64:(h + 1) * 64],
                                           start=False, stop=True), r=[AM, Vtok], w=[P])
        V("act", lambda e: e.activation(out=X32[:, :], in_=P[:, :], func=AF.Identity, scale=-1.0), [P], [X32])
        X16 = X16p.next()
        V("act", lambda e: e.activation(out=X16[:, :], in_=X32[:, :], func=AF.Copy), [X32], [X16])
        yield
        for j in range(7):
            P = psr.next()
            for h in range(8):
                kb.op("pe", lambda e: e.matmul(P[:, h * 64:(h + 1) * 64], lhsT=PTj[:, h, :], rhs=X16[:, h * 64:(h + 1) * 64],
                                               start=True, stop=True), r=[PTj, X16], w=[P])
            V("dve", lambda e: e.tensor_tensor(out=X32[:, :], in0=P[:, :], in1=X32[:, :], op=ALU.add), [P, X32], [X32])
            X16 = X16p.next()
            V("act", lambda e: e.activation(out=X16[:, :], in_=X32[:, :], func=AF.Copy), [X32], [X16])
            if j < 6:
                PTn = PTm.next()
                Pn = Pm[(j + 1) % 2].next() if j < 5 else None
                for g4 in range(2):
                    P = psr.next()
                    for hh in range(4):
                        h = g4 * 4 + hh
                        kb.op("pe", lambda e: e.matmul(P[:, hh * 128:(hh + 1) * 128], lhsT=Pj[:, h, :], rhs=PTj[:, h, :],
                                                       start=True, stop=True), r=[Pj, PTj], w=[P])
                    eng = "act"
                    ev += 1
                    dst = PTn[:, g4 * 4:(g4 + 1) * 4, :].rearrange("p h t -> p (h t)")
                    if eng == "act":
                        V("act", lambda e: e.activation(out=dst, in_=P[:, :], func=AF.Copy), [P], [PTn])
                    else:
                        V("dve", lambda e: e.tensor_copy(out=dst, in_=P[:, :]), [P], [PTn])
                    if Pn is not None:
                        P = psr.next()
                        for hh in range(4):
                            h = g4 * 4 + hh
                            kb.op("pe", lambda e: e.matmul(P[:, hh * 128:(hh + 1) * 128], lhsT=PTj[:, h, :], rhs=Pj[:, h, :],
                                                           start=True, stop=True), r=[Pj, PTj], w=[P])
                        eng = "act"
                        ev += 1
                        dst = Pn[:, g4 * 4:(g4 + 1) * 4, :].rearrange("p h t -> p (h t)")
                        if eng == "act":
                            V("act", lambda e: e.activation(out=dst, in_=P[:, :], func=AF.Copy), [P], [Pn])
                        else:
                            V("dve", lambda e: e.tensor_copy(out=dst, in_=P[:, :]), [P], [Pn])
                PTj = PTn
                if Pn is not None:
                    Pj = Pn
            if j < 6:
                yield
        U16 = X16
        P = psr.next()
        for h in range(8):
            c, par = h // 2, h % 2
            hs = slice(h * 64, (h + 1) * 64)
            kb.op("pe", lambda e: e.matmul(P[:, hs], lhsT=KR[:, c, 1, :], rhs=S0Z[:, c, par, :], start=True, stop=False),
                  r=[KR, S0Z], w=[P])
            kb.op("pe", lambda e: e.matmul(P[:, hs], lhsT=AM[:, h, 128:256], rhs=U16[:, hs], start=False, stop=False),
                  r=[AM, U16], w=[P])
            kb.op("pe", lambda e: e.matmul(P[:, hs], lhsT=AM[:, h, 384:512], rhs=Vtok[:, hs], start=False, stop=True),
                  r=[AM, Vtok], w=[P])
        Oo = Oop.next()
        V("act", lambda e: e.activation(out=Oo[:, :], in_=P[:, :], func=AF.Copy), [P], [Oo])
        kb.dma("act", S["O"][d][t0:t0 + 128, :], Oo[:, :], Oo, r=[Oo], w=[("dram", id(S["O"][d]))])
        yield
        P = psr.next()
        for c in range(4):
            cs_ = slice(c * 128, (c + 1) * 128)
            kb.op("pe", lambda e: e.matmul(P[:, cs_], lhsT=Btok[:, cs_], rhs=U16[:, cs_], start=True, stop=False),
                  r=[Btok, U16], w=[P])
            kb.op("pe", lambda e: e.matmul(P[:, cs_], lhsT=Ktok[:, cs_], rhs=Vtok[:, cs_], start=False, stop=True),
                  r=[Ktok, Vtok], w=[P])
        V("pool", lambda e: e.tensor_tensor(out=St1[:, :, :], in0=St[:, :, :], in1=sc[:, :, 2:3].broadcast_to([128, 4, 64]),
                                            op=ALU.mult), [St, sc], [St1])
        pv_ = P[:, :].rearrange("p (c x) -> p c x", c=4)
        for par in range(2):
            ps_ = slice(par * 64, (par + 1) * 64)
            V("dve", lambda e: e.tensor_tensor(out=St[ps_, :, :], in0=pv_[ps_, :, par * 64:(par + 1) * 64],
                                               in1=sc[ps_, :, 1:2].broadcast_to([64, 4, 64]), op=ALU.mult), [P, sc, St1], [St])
        V("pool", lambda e: e.tensor_tensor(out=St[:, :, :], in0=St[:, :, :], in1=St1[:, :, :], op=ALU.add), [St, St1], [St])


def _merge_declare(self):
    L = DEPTH
    self.branch_proj = self.din("branch_proj", [L, 3, 512, D])
    self.w_out = self.din("w_out", [L, D, D])


def _merge_phase(self, l, src, dst, S, ctx):
    kb = self.kb
    W = 256
    NB = W // 128
    lni = (l * 3 + 1) * NCH
    with kb.phase() as st:
        self.psum = Pool(kb, "ps", 8, [128, 512], F32, psum=True, stack=st)
        V = lambda e_, f, r, w: kb.op(e_, f, r=r, w=w)
        sbf = lambda n, shp, dt=F32: kb.sb(n, shp, dt, st)
        bp = sbf("bp", [128, 12, D], BF16)
        wo = sbf("wo", [128, NCH, D], BF16)
        for b_ in range(3):
            for c in range(4):
                kb.dma("pool", bp[:, b_ * 4 + c, :], self.branch_proj[l, b_, c * 128:(c + 1) * 128, :], bp, w=[bp])
        for c in range(NCH):
            kb.dma("pool", wo[:, c, :], self.w_out[l, c * 128:(c + 1) * 128, :], wo, w=[wo])
        idb = sbf("idb", [128, 128], BF16)
        kb.dma("pool", idb[:, :], self.identb[:, :], idb, w=[idb])
        gng, gnb = sbf("gng", [128, 4]), sbf("gnb", [128, 4])
        kb.dma("sp", gng[:, :], self.rk_gng[:, l * 4:(l + 1) * 4], gng, w=[gng])
        kb.dma("sp", gnb[:, :], self.rk_gnb[:, l * 4:(l + 1) * 4], gnb, w=[gnb])
        pl = self.stat_pools(W, st)
        xp = Pool(kb, "x", 2, [128, NCH, W], F32, stack=st)
        xn = sbf("xn", [128, NCH, W])
        Ofp = Pool(kb, "Of", 2, [128, NB, 512], F32, stack=st)
        Obp = Pool(kb, "Ob", 2, [128, NB, 512], F32, stack=st)
        onb = sbf("onb", [128, NB, 512], BF16)
        st8 = [sbf("st8_%d" % i, [128, NB, 8]) for i in range(3)]
        sqt = sbf("sqt", [128, NB, 512])
        Y = {n: Pool(kb, "y" + n, 2, [128, 4, W], BF16, stack=st) for n in ("a", "s", "bon", "g")}
        yr = sbf("yr", [128, 4, W], BF16)
        yt = sbf("yrt", [128, 4, W])
        gp = Pool(kb, "gates", 2, [128, 24, W], BF16, stack=st)
        m1, m2, m3 = sbf("m1", [128, W]), sbf("m2", [128, W]), sbf("m3", [128, W])
        mT = sbf("mT", [128, NCH, W], BF16)
        srcv = src.rearrange("(c p) t -> p c t", p=128)
        dstv = dst.rearrange("(c p) t -> p c t", p=128)
        fmv = lambda t_: t_.rearrange("(c p) t -> p c t", p=128)
        for (seg, t0, _) in self.tiles(W, ctx=ctx):
            x = xp.next()
            kb.dma("sp", x[:, :, :], srcv[:, :, t0:t0 + W], x, r=[("dram", id(src))], w=[x])
            Of, Ob = Ofp.next(), Obp.next()
            kb.dma("sp", Of[:, :, :], S["O"][0][t0:t0 + W, :].rearrange("(b p) n -> p b n", p=128), Of, r=[("dram", id(S["O"][0]))], w=[Of])
            kb.dma("sp", Ob[:, :, :], S["O"][1][t0:t0 + W, :].rearrange("(b p) n -> p b n", p=128), Ob, r=[("dram", id(S["O"][1]))], w=[Ob])
            ld = {}
            for n, key in (("a", "ya"), ("s", "ys"), ("bon", "bon"), ("g", "gR")):
                ld[n] = Y[n].next()
                kb.dma("sp", ld[n][:, :, :], fmv(S[key])[:, :, t0:t0 + W], ld[n], r=[("dram", id(S[key]))], w=[ld[n]])
            gt = gp.next()
            kb.dma("sp", gt[:, :, :], fmv(S["gS"])[:, :, t0:t0 + W], gt, r=[("dram", id(S["gS"]))], w=[gt])
            V("dve", lambda e: e.tensor_tensor(out=Of[:, :, :], in0=Of[:, :, :], in1=Ob[:, :, :], op=ALU.add), [Of, Ob], [Of])
            O4 = Of[:, :, :].rearrange("p b (h v) -> p b h v", h=8)
            sm, vr, rs = st8
            V("dve", lambda e: e.tensor_reduce(out=sm[:, :, :], in_=O4, axis=AX.X, op=ALU.add), [Of], [sm])
            V("dve", lambda e: e.tensor_scalar(out=sm[:, :, :], in0=sm[:, :, :], scalar1=1.0 / 64, scalar2=None, op0=ALU.mult), [sm], [sm])
            V("dve", lambda e: e.tensor_tensor(out=O4, in0=O4, in1=sm[:, :, :].unsqueeze(3).broadcast_to([128, NB, 8, 64]),
                                               op=ALU.subtract), [Of, sm], [Of])
            V("act", lambda e: e.activation(out=sqt[:, :, :], in_=Of[:, :, :], func=AF.Square), [Of], [sqt])
            V("dve", lambda e: e.tensor_reduce(out=vr[:, :, :], in_=sqt[:, :, :].rearrange("p b (h v) -> p b h v", h=8), axis=AX.X,
                                               op=ALU.add), [sqt], [vr])
            V("dve", lambda e: e.tensor_scalar(out=vr[:, :, :], in0=vr[:, :, :], scalar1=1.0 / 64, scalar2=GN_EPS, op0=ALU.mult,
                                               op1=ALU.add), [vr], [vr])
            V("act", lambda e: e.activation(out=vr[:, :, :], in_=vr[:, :, :], func=AF.Sqrt), [vr], [vr])
            V("dve", lambda e: e.reciprocal(out=rs[:, :, :], in_=vr[:, :, :]), [vr], [rs])
            V("dve", lambda e: e.tensor_tensor(out=onb[:, :, :].rearrange("p b (h v) -> p b h v", h=8), in0=O4,
                                               in1=rs[:, :, :].unsqueeze(3).broadcast_to([128, NB, 8, 64]), op=ALU.mult), [Of, rs], [onb])
            for tb in range(NB):
                P = self.psum.next()
                pb = P[:, 0:256].bitcast(BF16)
                for c in range(4):
                    kb.op("pe", lambda e: e.transpose(pb[:, c * 128:(c + 1) * 128], onb[:, tb, c * 128:(c + 1) * 128], idb[:, :]),
                          r=[onb, idb], w=[P])
                for c in range(4):
                    V("act", lambda e: e.activation(out=yt[:, c, tb * 128:(tb + 1) * 128], in_=pb[:, c * 128:(c + 1) * 128],
                                                    func=AF.Identity, scale=gng[:, c:c + 1], bias=gnb[:, c:c + 1]), [P, gng, gnb], [yt])
            V("pool", lambda e: e.tensor_tensor(out=yt[:, :, :], in0=yt[:, :, :], in1=ld["bon"][:, :, :], op=ALU.add), [yt, ld["bon"]], [yt])
            V("dve", lambda e: e.tensor_tensor(out=yr[:, :, :], in0=yt[:, :, :], in1=ld["g"][:, :, :], op=ALU.mult), [yt, ld["g"]], [yr])
            if "yrS" in S:
                kb.dma("sp", fmv(S["yrS"])[:, :, t0:t0 + W], yr[:, :, :], yr, r=[yr], w=[("dram", id(S["yrS"]))])
            ysrc = [ld["a"], yr, ld["s"]]
            for oc in range(NCH):
                Pa, Pb = self.psum.next(), self.psum.next()
                tgt = [(Pa, 0), (Pa, W), (Pb, 0)]
                for b_ in range(3):
                    Pt, o0 = tgt[b_]
                    for kc in range(4):
                        kb.op("pe", lambda e: e.matmul(Pt[:, o0:o0 + W], lhsT=bp[:, b_ * 4 + kc, oc * 128:(oc + 1) * 128],
                                                       rhs=ysrc[b_][:, kc, :], start=(kc == 0), stop=(kc == 3)), r=[bp, ysrc[b_]], w=[Pt])
                V("dve", lambda e: e.tensor_tensor(out=m1[:, :], in0=Pa[:, 0:W], in1=gt[:, oc, :], op=ALU.mult), [Pa, gt], [m1])
                V("dve", lambda e: e.tensor_tensor(out=m2[:, :], in0=Pa[:, W:2 * W], in1=gt[:, 8 + oc, :], op=ALU.mult), [Pa, gt], [m2])
                V("dve", lambda e: e.tensor_tensor(out=m3[:, :], in0=Pb[:, 0:W], in1=gt[:, 16 + oc, :], op=ALU.mult), [Pb, gt], [m3])
                V("dve", lambda e: e.tensor_tensor(out=m1[:, :], in0=m1[:, :], in1=m2[:, :], op=ALU.add), [m1, m2], [m1])
                V("pool", lambda e: e.tensor_tensor(out=mT[:, oc, :], in0=m1[:, :], in1=m3[:, :], op=ALU.add), [m1, m3], [mT])
            V("pool", lambda e: e.tensor_scalar(out=x[:, :, :], in0=x[:, :, :], scalar1=ALPHA, scalar2=None, op0=ALU.mult), [x], [x])
            for oc in range(NCH):
                if oc % 2 == 0:
                    P = self.psum.next()
                o0 = (oc % 2) * W
                for kc in range(NCH):
                    kb.op("pe", lambda e: e.matmul(P[:, o0:o0 + W], lhsT=wo[:, kc, oc * 128:(oc + 1) * 128], rhs=mT[:, kc, :],
                                                   start=(kc == 0), stop=(kc == NCH - 1)), r=[wo, mT], w=[P])
                V("dve", lambda e: e.scalar_tensor_tensor(out=x[:, oc, :], in0=P[:, o0:o0 + W],
                                                          scalar=self.mods[:, 5 * NCH + oc, seg:seg + 1], in1=x[:, oc, :],
                                                          op0=ALU.mult, op1=ALU.add), [P, x, self.mods], [x])
            P = self.psum.next()
            rstd, nmr = self.ln_stats(x, W, P, pl)
            self.normalize(xn, x, W, rstd, nmr)
            for c in range(NCH):
                V("act", lambda e: e.activation(out=xn[:, c, :], in_=xn[:, c, :], func=AF.Identity,
                                                scale=self.lng[:, lni + c:lni + c + 1], bias=self.lnb[:, lni + c:lni + c + 1]),
                  [xn, self.lng, self.lnb], [xn])
            kb.dma("sp", dstv[:, :, t0:t0 + W], xn[:, :, :], xn, r=[xn], w=[("dram", id(dst))])


Model.rwkv_scan_dir = _rwkv_scan_dir
Model.merge_declare = _merge_declare
Model.merge_phase = _merge_phase


def host_inputs_rwkv(inp):
    L = DEPTH
    d = {}
    w = inp["w_in"]
    rw = w[:, :, 768:2624]
    r, k, v = rw[:, :, 0:512], rw[:, :, 512:1024], rw[:, :, 1024:1536]
    wlo, alo, glo = rw[:, :, 1536:1664], rw[:, :, 1664:1728], rw[:, :, 1728:1856]
    pad = np.zeros_like(alo)
    d["w_inB"] = np.ascontiguousarray(np.concatenate([r, k, v, wlo, glo, alo, pad], -1))
    mu = inp["rwkv_mu"]
    mu_r = np.concatenate([mu[:, 0:1536], mu[:, 1536:1664], mu[:, 1728:1856], mu[:, 1664:1728], np.zeros((L, 64), np.float32)], -1)
    d["rk_mu"] = np.ascontiguousarray(mu_r.reshape(L * NCH_B, 128).T)
    d["rk_w0"] = np.ascontiguousarray(inp["rwkv_w0"].reshape(L * 8, 128).T)
    for n, src in (("a0", "rwkv_a0"), ("kk", "rwkv_k_k"), ("ka", "rwkv_k_a"), ("rk", "rwkv_r_k"), ("gng", "rwkv_gn_g"), ("gnb", "rwkv_gn_b")):
        d["rk_" + n] = np.ascontiguousarray(inp[src].reshape(L * 4, 128).T)
    d["rk_w2"] = np.ascontiguousarray(inp["rwkv_w2"].reshape(L, 128, 512))
    d["rk_a2"] = inp["rwkv_a2"]
    d["rk_g2"] = inp["rwkv_g2"]
    i = np.arange(128)
    d["bd64"] = np.ascontiguousarray(((i[:, None] // 64) == (i[None, :] // 64)).astype(np.float32))
    d["identb"] = np.eye(128, dtype=np.float32)
    on = np.ones((2, 128, 128), np.float32)
    on[0, :, 0] = 0.0
    on[1, :, 127] = 0.0
    d["ones0"] = on
    MT = np.zeros((2, 128, 512), np.float32)
    MN = np.zeros((2, 128, 512), np.float32)
    ii, tt = i[:, None], i[None, :]
    for dd in range(2):
        prev = (ii < tt) if dd == 0 else (ii > tt)
        incl = prev | (ii == tt)
        MT[dd, :, 0:128] = -(prev.astype(np.float32))
        MT[dd, :, 128:256] = incl
        MT[dd, :, 256:384] = prev
        MT[dd, :, 384:512] = incl
        MN[dd] = np.tile(-(prev.T.astype(np.float32)), (1, 4))
    d["rk_MT"], d["rk_MN"] = MT, MN
    d["branch_proj"] = inp["branch_proj"]
    d["w_out"] = inp["w_out"]
    return d


def _make_scratch(self):
    NT = self.NT
    S = {}
    for n, shp, dt in (("qS", [512, NT], BF16), ("kS", [256, NT], BF16), ("vS", [NT, 128], BF16), ("usS", [512, NT], F32),
                       ("gS", [3072, NT], BF16), ("ya", [512, NT], BF16), ("ysb", [512, NT], F32), ("ys", [512, NT], BF16),
                       ("gR", [512, NT], BF16), ("bon", [512, NT], BF16), ("vt", [NT, 512], BF16)):
        S[n] = self.scratch(n, shp, dt)
    for n in ("RHO", "KAP", "BET", "KTI"):
        S[n] = [self.scratch("%s%d" % (n, d), [512, NT], BF16) for d in range(2)]
    for n in ("bt", "kt"):
        S[n] = [self.scratch("%s%d" % (n, d), [NT, 512], BF16) for d in range(2)]
    S["sc"] = [self.scratch("sc%d" % d, [128, NT // 128, 4, 3], F32) for d in range(2)]
    S["O"] = [self.scratch("O%d" % d, [NT, 512], F32) for d in range(2)]
    if "yrS" in self.dbg:
        S["yrS"] = self.scratch("yrS", [512, NT], BF16)
    return S


def _mixer(self, l, src, dst, S, ctx_out):
    self.mixA_phase(l, src, S["qS"], S["kS"], S["vS"], S["usS"], S["gS"])
    self.attn_phase(l, S["qS"], S["kS"], S["vS"], S["ya"])
    self.rwkv_prep_phase(l, src, S)
    self.scan_phase(l, S)
    self.merge_phase(l, src, dst, S, ctx=ctx_out)


def _scan_phase(self, l, S):
    kb = self.kb
    for (ds5, drk) in ((1, 0), (0, 1)):
        with kb.phase() as st:
            PS2 = Pool(kb, "ps2", 2, [128, 1024], F32, psum=True, stack=st)
            psr = Pool(kb, "psr", 4, [128, 512], F32, psum=True, stack=st)
            g1 = self.s5_dir(l, ds5, S["usS"], S["ysb"], S["ys"], st, PS2)
            next(g1)
            g2 = self.rwkv_scan_dir(l, drk, S, st, psr)
            next(g2)
            alive = [g1, g2]
            while alive:
                for g in list(alive):
                    try:
                        next(g)
                    except StopIteration:
                        alive.remove(g)


Model.scan_phase = _scan_phase
Model.make_scratch = _make_scratch
Model.mixer = _mixer


def build_model(T, dbg=()):
    m = Model(T, dbg=dbg)
    m.declare_inputs()
    m.mixA_declare()
    m.s5_declare()
    m.rwkv_declare()
    m.merge_declare()
    m.setup_consts()
    NT = m.NT
    S = m.make_scratch()
    streams = [m.scratch("str%d" % i, [D, NT], F32) for i in range(3)]
    outT = m.dout("outT", [D, T])
    cur = m.xT
    for l in range(DEPTH):
        last = l == DEPTH - 1
        m.adaln_phase(l)
        m.ffn_phase(l, 0, cur, streams[0], 0, ctx=True)
        m.mixer(l, streams[0], streams[1], S, ctx_out=not last)
        if last:
            m.ffn_phase(l, 1, streams[1], outT, CTX, ctx=False)
        else:
            m.ffn_phase(l, 1, streams[1], streams[2], 0, ctx=True)
            cur = streams[2]
    m.kb.finish()
    return m


def all_host_inputs(inp, b, T):
    d = host_inputs(inp, b, T)
    d.update(host_inputs_A(inp, T))
    d.update(host_inputs_s5(inp))
    d.update(host_inputs_rwkv(inp))
    return d


T_FULL = 8192
N_CORES = 8


def kernel(**inputs):
    inp = {k: np.asarray(v) for k, v in inputs.items()}
    m = build_model(T_FULL)
    B = inp["x"].shape[0]
    shared = None
    in_maps = []
    for core in range(N_CORES):
        b = core % B
        d = all_host_inputs(inp, b, T_FULL) if shared is None else dict(shared)
        if shared is None:
            shared = d
        else:
            d["xT"] = np.ascontiguousarray(np.concatenate([inp["ctx"][b], inp["x"][b, :T_FULL]], 0).T)
            cond = np.stack([inp["c"][b], inp["c_ctx"]], -1)
            d["condT"] = np.ascontiguousarray(cond.reshape(NCH, 128, 2).transpose(1, 0, 2))
        in_maps.append({k: v for k, v in d.items() if k in m.dram_in})
    res = run_bass_kernel_spmd(m.nc, in_maps, core_ids=list(range(N_CORES)))
    out = np.stack([np.ascontiguousarray(res.results[b]["outT"].T) for b in range(B)], 0)
    return out.astype(np.float32)
```

```python
import contextlib
import numpy as np
import concourse.bass as bass
import concourse.mybir as mybir
from concourse.bass_utils import run_bass_kernel_spmd

F32 = mybir.dt.float32
BF16 = mybir.dt.bfloat16
AF = mybir.ActivationFunctionType
ALU = mybir.AluOpType
AX = mybir.AxisListType

D = 1024
NCH = 8
CTX = 256
DFF = 2816
NFF = 22
DEPTH = 2
ALPHA = (2.0 * DEPTH) ** 0.25
LN_EPS = 1e-6
DECAY_SCALE = 0.606531
GN_EPS = 64e-5
N_IN = 6208


class Sem:
    _n = 0

    def __init__(self, h):
        self.h = h
        Sem._n += 1
        self.uid = Sem._n


class Buf:
    def __init__(self, name, t):
        self.name = name
        self.t = t
        self.dsem = None
        self.dcnt = 0

    def __getitem__(self, k):
        return self.t[k]

    def __repr__(self):
        return "Buf(%s)" % self.name


class KB:
    EPOCH = 20000

    def __init__(self, nc):
        self.nc = nc
        self.es = contextlib.ExitStack()
        self.eng = {"pe": nc.tensor, "dve": nc.vector, "act": nc.scalar, "pool": nc.gpsimd, "sp": nc.sync}
        self.esem = {}
        self.ecnt = {}
        for e in self.eng:
            self.esem[e] = Sem(self.es.enter_context(nc.semaphore("c_%s_0" % e)))
            self.ecnt[e] = 0
        self.eepoch = {e: 0 for e in self.eng}
        self.seen = {e: {} for e in self.eng}
        self.lastw = {}
        self.reads = {}
        self.nbuf = 0
        self.ninstr = 0
        self.nwait = 0
        self.all_events = {}
        self.free_dsems = []
        self.phase_bufs = []
        self.ndsem = 0

    def sb(self, name, shape, dtype, stack=None):
        self.nbuf += 1
        t = (stack or self.es).enter_context(self.nc.sbuf_tensor("%s_%d" % (name, self.nbuf), list(shape), dtype))
        b = Buf(name, t)
        if stack is not None:
            self.phase_bufs.append(b)
        return b

    def ps(self, name, shape, dtype=F32, stack=None):
        self.nbuf += 1
        t = (stack or self.es).enter_context(self.nc.psum_tensor("%s_%d" % (name, self.nbuf), list(shape), dtype))
        return Buf(name, t)

    def _dsem(self, b):
        if b.dsem is None:
            if self.free_dsems:
                b.dsem, b.dcnt = self.free_dsems.pop()
            else:
                self.ndsem += 1
                b.dsem = Sem(self.es.enter_context(self.nc.semaphore("d_%d" % self.ndsem)))
                b.dcnt = 0
        return b.dsem

    @contextlib.contextmanager
    def phase(self):
        st = contextlib.ExitStack()
        self.phase_bufs = []
        try:
            yield st
        finally:
            self.barrier()
            for b in self.phase_bufs:
                if b.dsem is not None:
                    self.free_dsems.append((b.dsem, b.dcnt))
                    b.dsem = None
            self.phase_bufs = []
            st.close()

    def _need(self, e, r, w):
        need = {}

        def add(evs):
            for uid, (s, v) in evs.items():
                if uid not in need or need[uid][1] < v:
                    need[uid] = (s, v)
        for k in r:
            add(self.lastw.get(k, {}))
        for k in w:
            add(self.lastw.get(k, {}))
            add(self.reads.get(k, {}))
        return need

    def _wait(self, e, need, own_ok):
        eng = self.eng[e]
        for uid, (s, v) in need.items():
            if own_ok and uid == self.esem[e].uid:
                continue
            if self.seen[e].get(uid, 0) >= v:
                continue
            eng.wait_ge(s.h, v)
            self.nwait += 1
            self.seen[e][uid] = v

    def _record(self, ev, r, w):
        uid = ev[0].uid
        for k in r:
            self.reads.setdefault(k, {})[uid] = ev
        for k in w:
            self.lastw.setdefault(k, {})[uid] = ev
            self.reads[k] = {}
        self.all_events[uid] = ev

    def _bump(self, e):
        if self.ecnt[e] >= self.EPOCH:
            self.eepoch[e] += 1
            self.esem[e] = Sem(self.es.enter_context(self.nc.semaphore("c_%s_%d" % (e, self.eepoch[e]))))
            self.ecnt[e] = 0
        self.ecnt[e] += 1
        return (self.esem[e], self.ecnt[e])

    def op(self, e, fn, r=(), w=(), same_ok=False):
        need = self._need(e, r, w)
        self._wait(e, need, own_ok=(e == "pe" or same_ok))
        ins = fn(self.eng[e])
        ev = self._bump(e)
        ins.then_inc(ev[0].h, 1)
        self._record(ev, r, w)
        self.ninstr += 1
        return ins

    def dma(self, q, out, in_, sbuf, r=(), w=(), **kw):
        need = self._need(q, r, w)
        self._wait(q, need, own_ok=False)
        s = self._dsem(sbuf)
        ins = self.eng[q].dma_start(out=out, in_=in_, **kw)
        sbuf.dcnt += 16
        ev = (s, sbuf.dcnt)
        ins.then_inc(s.h, 16)
        self._record(ev, r, w)
        self.ninstr += 1
        return ins

    def barrier(self):
        for e in self.eng:
            self._wait(e, dict(self.all_events), own_ok=True)
        self.lastw = {}
        self.reads = {}
        self.all_events = {}

    def finish(self, e="sp"):
        self._wait(e, dict(self.all_events), own_ok=True)


class Pool:
    def __init__(self, kb, name, n, shape, dtype, psum=False, stack=None):
        self.bufs = [(kb.ps if psum else kb.sb)("%s%d" % (name, i), shape, dtype, stack=stack) for i in range(n)]
        self.i = 0

    def next(self):
        b = self.bufs[self.i % len(self.bufs)]
        self.i += 1
        return b


class Model:
    def __init__(self, T, dbg=(), nlayers=DEPTH, stop_after=None):
        self.T = T
        self.NT = CTX + T
        self.dbg = set(dbg)
        self.nlayers = nlayers
        self.stop_after = stop_after
        nc = bass.Bass("TRN2", target_bir_lowering=False)
        self.nc = nc
        self.kb = KB(nc)
        self.dram_in = {}
        self.dram_out = {}
        self.scr_n = 0

    def din(self, name, shape, dtype=F32):
        t = self.nc.dram_tensor(name, list(shape), dtype, kind="ExternalInput").ap()
        self.dram_in[name] = t
        return t

    def dout(self, name, shape, dtype=F32):
        t = self.nc.dram_tensor(name, list(shape), dtype, kind="ExternalOutput").ap()
        self.dram_out[name] = t
        return t

    def scratch(self, name, shape, dtype=F32):
        if name in self.dbg:
            return self.dout(name, shape, dtype)
        return self.nc.dram_tensor(name, list(shape), dtype, kind="Internal").ap()

    def tiles(self, W, ctx=True, lat=True):
        out = []
        if ctx:
            for t0 in range(0, CTX, W):
                out.append((1, t0, W))
        if lat:
            for t0 in range(0, self.T, W):
                out.append((0, CTX + t0, W))
        return out

    def declare_inputs(self):
        L = DEPTH
        self.xT = self.din("xT", [D, self.NT])
        self.condT = self.din("condT", [128, NCH, 2])
        self.w_ada = self.din("w_ada", [L, D, 9 * D])
        self.b_ada = self.din("b_ada", [L, 128, 72])
        self.ln_g = self.din("ln_g", [128, L * 3 * NCH])
        self.ln_b = self.din("ln_b", [128, L * 3 * NCH])
        self.ffn_w_in = self.din("ffn_w_in", [L, 2, D, 2 * DFF])
        self.ffn_w_out = self.din("ffn_w_out", [L, 2, DFF, D])

    def setup_consts(self):
        kb = self.kb
        self.onesb = kb.sb("onesb", [128, 128], BF16)
        kb.op("dve", lambda e: e.memset(self.onesb[:], 1.0 / D), w=[self.onesb])
        self.lng = kb.sb("lng", [128, DEPTH * 3 * NCH], F32)
        self.lnb = kb.sb("lnb", [128, DEPTH * 3 * NCH], F32)
        kb.dma("sp", self.lng[:], self.ln_g[:, :], self.lng, w=[self.lng])
        kb.dma("sp", self.lnb[:], self.ln_b[:, :], self.lnb, w=[self.lnb])
        self.scond = kb.sb("scond", [128, NCH, 2], F32)
        kb.dma("sp", self.scond[:], self.condT[:, :, :], self.scond, w=[self.scond])
        kb.op("act", lambda e: e.activation(out=self.scond[:], in_=self.scond[:], func=AF.Silu),
              r=[self.scond], w=[self.scond])
        self.mods = kb.sb("mods", [128, 72, 2], F32)
        self.modp1 = kb.sb("modp1", [128, 72, 2], F32)
        self.modh = kb.sb("modh", [128, 72, 2], F32)

    def adaln_phase(self, l):
        kb = self.kb
        with kb.phase() as st:
            self.psum = Pool(kb, "ps", 8, [128, 512], F32, psum=True, stack=st)
            wa = Pool(kb, "wa", 2, [128, NCH, D], F32, stack=st)
            bada = kb.sb("bada", [128, 72], F32, st)
            kb.dma("sp", bada[:], self.b_ada[l, :, :], bada, w=[bada])
            P = self.psum.next()
            for m in range(9):
                w = wa.next()
                for kc in range(NCH):
                    kb.dma("sp" if kc % 2 == 0 else "act", w[:, kc, :],
                           self.w_ada[l, kc * 128:(kc + 1) * 128, m * D:(m + 1) * D], w, w=[w])
                for oc in range(NCH):
                    j = m * NCH + oc
                    for kc in range(NCH):
                        kb.op("pe", lambda e: e.matmul(P[:, 2 * j:2 * j + 2], lhsT=w[:, kc, oc * 128:(oc + 1) * 128],
                                                       rhs=self.scond[:, kc, :], start=(kc == 0), stop=(kc == NCH - 1)),
                              r=[w, self.scond], w=[P])
            kb.op("dve", lambda e: e.tensor_tensor(out=self.mods[:], in0=P[:, 0:144].rearrange("p (j s) -> p j s", s=2),
                                                   in1=bada[:, :].unsqueeze(2).broadcast_to([128, 72, 2]), op=ALU.add),
                  r=[P, bada], w=[self.mods])
            kb.op("dve", lambda e: e.tensor_scalar(out=self.modp1[:], in0=self.mods[:], scalar1=1.0, scalar2=None,
                                                   op0=ALU.add), r=[self.mods], w=[self.modp1])
            kb.op("dve", lambda e: e.tensor_scalar(out=self.modh[:], in0=self.mods[:], scalar1=0.5, scalar2=None,
                                                   op0=ALU.mult), r=[self.mods], w=[self.modh])

    def ln_stats(self, x, W, P, pl):
        kb = self.kb
        xb, sq = pl["xb"].next(), pl["sq"].next()
        kb.op("act", lambda e: e.activation(out=xb[:, :, :W], in_=x[:, :, :W], func=AF.Copy), r=[x], w=[xb])
        kb.op("act", lambda e: e.activation(out=sq[:, :, :W], in_=x[:, :, :W], func=AF.Square), r=[x], w=[sq])
        for c in range(NCH):
            kb.op("pe", lambda e: e.matmul(P[:, 0:W], lhsT=self.onesb[:, :], rhs=xb[:, c, :W],
                                           start=(c == 0), stop=(c == NCH - 1)), r=[xb, self.onesb], w=[P])
        for c in range(NCH):
            kb.op("pe", lambda e: e.matmul(P[:, W:2 * W], lhsT=self.onesb[:, :], rhs=sq[:, c, :W],
                                           start=(c == 0), stop=(c == NCH - 1)), r=[sq, self.onesb], w=[P])
        m2, var, rstd, nmr = pl["m2"].next(), pl["var"].next(), pl["rstd"].next(), pl["nmr"].next()
        kb.op("act", lambda e: e.activation(out=m2[:, 0, :W], in_=P[:, 0:W], func=AF.Square), r=[P], w=[m2])
        kb.op("dve", lambda e: e.scalar_tensor_tensor(out=var[:, 0, :W], in0=P[:, W:2 * W], scalar=LN_EPS,
                                                      in1=m2[:, 0, :W], op0=ALU.add, op1=ALU.subtract),
              r=[P, m2], w=[var])
        kb.op("act", lambda e: e.activation(out=var[:, 0, :W], in_=var[:, 0, :W], func=AF.Sqrt), r=[var], w=[var])
        kb.op("dve", lambda e: e.reciprocal(out=rstd[:, 0, :W], in_=var[:, 0, :W]), r=[var], w=[rstd])
        kb.op("dve", lambda e: e.scalar_tensor_tensor(out=nmr[:, 0, :W], in0=P[:, 0:W], scalar=-1.0,
                                                      in1=rstd[:, 0, :W], op0=ALU.mult, op1=ALU.mult),
              r=[P, rstd], w=[nmr])
        return rstd, nmr

    def normalize(self, out, x, W, rstd, nmr):
        kb = self.kb
        kb.op("dve", lambda e: e.tensor_tensor(out=out[:, :, :W], in0=x[:, :, :W],
                                               in1=rstd[:, 0:1, :W].broadcast_to([128, NCH, W]), op=ALU.mult),
              r=[x, rstd], w=[out])
        kb.op("dve", lambda e: e.tensor_tensor(out=out[:, :, :W], in0=out[:, :, :W],
                                               in1=nmr[:, 0:1, :W].broadcast_to([128, NCH, W]), op=ALU.add),
              r=[out, nmr], w=[out])

    def stat_pools(self, W, st):
        kb = self.kb
        return {
            "xb": Pool(kb, "xb", 1, [128, NCH, W], BF16, stack=st),
            "sq": Pool(kb, "sq", 1, [128, NCH, W], BF16, stack=st),
            "m2": Pool(kb, "m2", 2, [128, 1, W], F32, stack=st),
            "var": Pool(kb, "var", 2, [128, 1, W], F32, stack=st),
            "rstd": Pool(kb, "rstd", 2, [128, 1, W], F32, stack=st),
            "nmr": Pool(kb, "nmr", 2, [128, 1, W], F32, stack=st),
        }

    def ffn_phase(self, l, s, src, dst, dst_off, ctx):
        kb = self.kb
        W = 256
        mb = 0 if s == 0 else 6
        lni = (l * 3 + (0 if s == 0 else 2)) * NCH
        with kb.phase() as st:
            self.psum = Pool(kb, "ps", 8, [128, 512], F32, psum=True, stack=st)
            w1 = kb.sb("w1", [128, NCH, 2 * DFF], BF16, st)
            w2 = kb.sb("w2", [128, NFF, D], BF16, st)
            for c in range(NCH):
                for hh in range(2):
                    kb.dma("pool", w1[:, c, hh * DFF:(hh + 1) * DFF],
                           self.ffn_w_in[l, s, c * 128:(c + 1) * 128, hh * DFF:(hh + 1) * DFF], w1, w=[w1])
            for c in range(NFF):
                kb.dma("pool", w2[:, c, :], self.ffn_w_out[l, s, c * 128:(c + 1) * 128, :], w2, w=[w2])
            pl = self.stat_pools(W, st)
            xp = Pool(kb, "x", 2, [128, NCH, W], F32, stack=st)
            xnp = Pool(kb, "xn", 2, [128, NCH, W], F32, stack=st)
            up = Pool(kb, "u", 2, [128, NCH, W], BF16, stack=st)
            hp = Pool(kb, "h", 1, [128, NFF, W], BF16, stack=st)
            gp = Pool(kb, "g", 2, [128, W], F32, stack=st)
            srcv = src.rearrange("(c p) t -> p c t", p=128)
            dstv = dst.rearrange("(c p) t -> p c t", p=128)
            tl = self.tiles(W, ctx=ctx)
            T_ = {}

            def stage_a(i):
                seg, t0, _ = tl[i]
                x = xp.next()
                kb.dma("sp", x[:, :, :], srcv[:, :, t0:t0 + W], x, r=[("dram", id(src))], w=[x])
                P = self.psum.next()
                rstd, nmr = self.ln_stats(x, W, P, pl)
                xn = xnp.next()
                self.normalize(xn, x, W, rstd, nmr)
                u = up.next()
                for c in range(NCH):
                    kb.op("act", lambda e: e.activation(out=u[:, c, :], in_=xn[:, c, :], func=AF.Identity,
                                                        scale=self.modp1[:, (mb + 1) * NCH + c, seg:seg + 1],
                                                        bias=self.mods[:, mb * NCH + c, seg:seg + 1]),
                          r=[xn, self.modp1, self.mods], w=[u])
                kb.op("pool", lambda e: e.tensor_scalar(out=x[:, :, :], in0=x[:, :, :], scalar1=ALPHA, scalar2=None,
                                                        op0=ALU.mult), r=[x], w=[x])
                T_[i] = (x, xn, u)

            def stage_b(i):
                x, xn, u = T_[i]
                h = hp.next()
                for j in range(NFF):
                    P = self.psum.next()
                    for c in range(NCH):
                        kb.op("pe", lambda e: e.matmul(P[:, 0:W], lhsT=w1[:, c, j * 128:(j + 1) * 128], rhs=u[:, c, :],
                                                       start=(c == 0), stop=(c == NCH - 1)), r=[w1, u], w=[P])
                    for c in range(NCH):
                        kb.op("pe", lambda e: e.matmul(P[:, W:2 * W], lhsT=w1[:, c, DFF + j * 128:DFF + (j + 1) * 128],
                                                       rhs=u[:, c, :], start=(c == 0), stop=(c == NCH - 1)),
                              r=[w1, u], w=[P])
                    g = gp.next()
                    kb.op("act", lambda e: e.activation(out=g[:, :], in_=P[:, 0:W], func=AF.Silu), r=[P], w=[g])
                    kb.op("dve", lambda e: e.tensor_tensor(out=h[:, j, :], in0=P[:, W:2 * W], in1=g[:, :], op=ALU.mult),
                          r=[P, g], w=[h])
                T_[i] = (x, xn, u, h)

            def stage_c(i):
                seg, t0, _ = tl[i]
                x, xn, u, h = T_.pop(i)
                for oc in range(NCH):
                    if oc % 2 == 0:
                        P = self.psum.next()
                    o0 = (oc % 2) * W
                    for j in range(NFF):
                        kb.op("pe", lambda e: e.matmul(P[:, o0:o0 + W], lhsT=w2[:, j, oc * 128:(oc + 1) * 128],
                                                       rhs=h[:, j, :], start=(j == 0), stop=(j == NFF - 1)),
                              r=[w2, h], w=[P])
                    kb.op("dve", lambda e: e.scalar_tensor_tensor(
                        out=x[:, oc, :], in0=P[:, o0:o0 + W], scalar=self.modh[:, (mb + 2) * NCH + oc, seg:seg + 1],
                        in1=x[:, oc, :], op0=ALU.mult, op1=ALU.add), r=[P, x, self.modh], w=[x])
                P = self.psum.next()
                rstd, nmr = self.ln_stats(x, W, P, pl)
                self.normalize(xn, x, W, rstd, nmr)
                for c in range(NCH):
                    kb.op("act", lambda e: e.activation(out=xn[:, c, :], in_=xn[:, c, :], func=AF.Identity,
                                                        scale=self.lng[:, lni + c:lni + c + 1],
                                                        bias=self.lnb[:, lni + c:lni + c + 1]),
                          r=[xn, self.lng, self.lnb], w=[xn])
                kb.dma("sp", dstv[:, :, t0 - dst_off:t0 - dst_off + W], xn[:, :, :], xn,
                       r=[xn], w=[("dram", id(dst))])

            stage_a(0)
            for i in range(len(tl)):
                stage_b(i)
                if i + 1 < len(tl):
                    stage_a(i + 1)
                stage_c(i)


def host_inputs(inp, b, T):
    L = DEPTH
    d = {}
    d["xT"] = np.ascontiguousarray(np.concatenate([inp["ctx"][b], inp["x"][b, :T]], 0).T)
    cond = np.stack([inp["c"][b], inp["c_ctx"]], -1)
    d["condT"] = np.ascontiguousarray(cond.reshape(NCH, 128, 2).transpose(1, 0, 2))
    d["w_ada"] = inp["w_ada"]
    d["b_ada"] = np.ascontiguousarray(inp["b_ada"].reshape(L, 72, 128).transpose(0, 2, 1))
    d["ln_g"] = np.ascontiguousarray(inp["ln_g"].reshape(L * 3 * NCH, 128).T)
    d["ln_b"] = np.ascontiguousarray(inp["ln_b"].reshape(L * 3 * NCH, 128).T)
    d["ffn_w_in"] = inp["ffn_w_in"]
    d["ffn_w_out"] = inp["ffn_w_out"]
    return d


NWA = 10 + 1 + 4 + 24
CH_Q, CH_QS, CH_K, CH_KS, CH_KB, CH_KBS, CH_V, CH_S5, CH_G = 0, 4, 8, 9, 10, 11, 12, 13, 17
NCH_A = 41


def _mixA_declare(self):
    L = DEPTH
    self.w_inA = self.din("w_inA", [L, D, NCH_A * 128])
    self.ropeC = self.din("ropeC", [128, self.NT])
    self.ropeS = self.din("ropeS", [128, self.NT])
    self.sinkT = self.din("sinkT", [L, 64, 8])
    self.maskP = self.din("maskP", [128, 512])
    self.maskN = self.din("maskN", [128, 512])


def _mixA_phase(self, l, src, qS, kS, vS, usS, gS):
    kb = self.kb
    W = 256
    with kb.phase() as st:
        self.psum = Pool(kb, "ps", 8, [128, 512], F32, psum=True, stack=st)
        w = kb.sb("wA", [128, NCH, NCH_A * 128], BF16, st)
        for c in range(NCH):
            for hh in range(2):
                n0, n1 = (0, 21 * 128) if hh == 0 else (21 * 128, NCH_A * 128)
                kb.dma("pool", w[:, c, n0:n1], self.w_inA[l, c * 128:(c + 1) * 128, n0:n1], w, w=[w])
        pl = self.stat_pools(W, st)
        xp = Pool(kb, "x", 2, [128, NCH, W], F32, stack=st)
        xnp = Pool(kb, "xn", 2, [128, NCH, W], F32, stack=st)
        up = Pool(kb, "u", 2, [128, NCH, W], BF16, stack=st)
        cp = Pool(kb, "rc", 2, [128, W], F32, stack=st)
        sp_ = Pool(kb, "rs", 2, [128, W], F32, stack=st)
        t1p = Pool(kb, "t1", 2, [128, W], F32, stack=st)
        t2p = Pool(kb, "t2", 2, [128, W], F32, stack=st)
        qp = Pool(kb, "qo", 2, [128, 4, W], BF16, stack=st)
        kp = Pool(kb, "ko", 2, [128, 2, W], BF16, stack=st)
        vp = Pool(kb, "vo", 2, [128, 2, 128], BF16, stack=st)
        usp = Pool(kb, "uso", 2, [128, 4, W], F32, stack=st)
        gp = Pool(kb, "go", 2, [128, 24, W], BF16, stack=st)
        srcv = src.rearrange("(c p) t -> p c t", p=128)

        def proj(P, o0, ch, u):
            for c in range(NCH):
                kb.op("pe", lambda e: e.matmul(P[:, o0:o0 + W], lhsT=w[:, c, ch * 128:(ch + 1) * 128], rhs=u[:, c, :],
                                               start=(c == 0), stop=(c == NCH - 1)), r=[w, u], w=[P])

        tl = self.tiles(W)
        T_ = {}

        def stage_a(i):
            seg, t0, _ = tl[i]
            x = xp.next()
            kb.dma("sp", x[:, :, :], srcv[:, :, t0:t0 + W], x, r=[("dram", id(src))], w=[x])
            cT, sT = cp.next(), sp_.next()
            kb.dma("sp", cT[:, :], self.ropeC[:, t0:t0 + W], cT, w=[cT])
            kb.dma("sp", sT[:, :], self.ropeS[:, t0:t0 + W], sT, w=[sT])
            P = self.psum.next()
            rstd, nmr = self.ln_stats(x, W, P, pl)
            xn = xnp.next()
            self.normalize(xn, x, W, rstd, nmr)
            u = up.next()
            for c in range(NCH):
                kb.op("act", lambda e: e.activation(out=u[:, c, :], in_=xn[:, c, :], func=AF.Identity,
                                                    scale=self.modp1[:, 4 * NCH + c, seg:seg + 1],
                                                    bias=self.mods[:, 3 * NCH + c, seg:seg + 1]),
                      r=[xn, self.modp1, self.mods], w=[u])
            T_[i] = (u, cT, sT)

        def stage_b1(i):
            seg, t0, _ = tl[i]
            u, cT, sT = T_[i]
            qo, ko = qp.next(), kp.next()

            def rope(dst_ap, dst, ch, chs):
                P = self.psum.next()
                proj(P, 0, ch, u)
                proj(P, W, chs, u)
                t1, t2 = t1p.next(), t2p.next()
                kb.op("dve", lambda e: e.tensor_tensor(out=t1[:, :], in0=P[:, 0:W], in1=cT[:, :], op=ALU.mult),
                      r=[P, cT], w=[t1])
                kb.op("dve", lambda e: e.tensor_tensor(out=t2[:, :], in0=P[:, W:2 * W], in1=sT[:, :], op=ALU.mult),
                      r=[P, sT], w=[t2])
                kb.op("dve", lambda e: e.tensor_tensor(out=dst_ap, in0=t1[:, :], in1=t2[:, :], op=ALU.add),
                      r=[t1, t2], w=[dst])
            for c in range(4):
                rope(qo[:, c, :], qo, CH_Q + c, CH_QS + c)
            rope(ko[:, 0, :], ko, CH_K, CH_KS)
            rope(ko[:, 1, :], ko, CH_KB, CH_KBS)
            kb.dma("sp", qS.rearrange("(c p) t -> p c t", p=128)[:, :, t0:t0 + W], qo[:, :, :], qo, r=[qo],
                   w=[("dram", id(qS))])
            kb.dma("sp", kS.rearrange("(c p) t -> p c t", p=128)[:, :, t0:t0 + W], ko[:, :, :], ko, r=[ko],
                   w=[("dram", id(kS))])

        def stage_b2(i):
            seg, t0, _ = tl[i]
            u, cT, sT = T_.pop(i)
            vo = vp.next()
            P = self.psum.next()
            for tb in range(W // 128):
                for c in range(NCH):
                    kb.op("pe", lambda e: e.matmul(P[:, tb * 128:(tb + 1) * 128], lhsT=u[:, c, tb * 128:(tb + 1) * 128],
                                                   rhs=w[:, c, CH_V * 128:(CH_V + 1) * 128],
                                                   start=(c == 0), stop=(c == NCH - 1)), r=[w, u], w=[P])
            kb.op("act", lambda e: e.activation(out=vo[:, :, :], in_=P[:, 0:W].rearrange("p (b n) -> p b n", b=2),
                                                func=AF.Copy), r=[P], w=[vo])
            kb.dma("sp", vS[t0:t0 + W, :].rearrange("(b p) n -> p b n", p=128), vo[:, :, :], vo, r=[vo],
                   w=[("dram", id(vS))])
            uso = usp.next()
            for c in range(4):
                if c % 2 == 0:
                    P = self.psum.next()
                o0 = (c % 2) * W
                proj(P, o0, CH_S5 + c, u)
                kb.op("act", lambda e: e.activation(out=uso[:, c, :], in_=P[:, o0:o0 + W], func=AF.Copy),
                      r=[P], w=[uso])
            kb.dma("sp", usS.rearrange("(c p) t -> p c t", p=128)[:, :, t0:t0 + W], uso[:, :, :], uso, r=[uso],
                   w=[("dram", id(usS))])
            go = gp.next()
            for c in range(24):
                if c % 2 == 0:
                    P = self.psum.next()
                o0 = (c % 2) * W
                proj(P, o0, CH_G + c, u)
                kb.op("act", lambda e: e.activation(out=go[:, c, :], in_=P[:, o0:o0 + W], func=AF.Sigmoid),
                      r=[P], w=[go])
            kb.dma("sp", gS.rearrange("(c p) t -> p c t", p=128)[:, :, t0:t0 + W], go[:, :, :], go, r=[go],
                   w=[("dram", id(gS))])

        stage_a(0)
        for i in range(len(tl)):
            stage_b1(i)
            if i + 1 < len(tl):
                stage_a(i + 1)
            stage_b2(i)


def _attn_phase(self, l, qS, kS, vS, yaS):
    kb = self.kb
    NB = self.T // 128
    with kb.phase() as st:
        self.psum = Pool(kb, "ps", 8, [128, 512], F32, psum=True, stack=st)
        mP = kb.sb("mP", [128, 512], BF16, st)
        mN = kb.sb("mN", [128, 512], BF16, st)
        kb.dma("pool", mP[:, :], self.maskP[:, :], mP, w=[mP])
        kb.dma("pool", mN[:, :], self.maskN[:, :], mN, w=[mN])
        ones = kb.sb("ones64", [128, 64], BF16, st)
        kb.op("dve", lambda e: e.memset(ones[:], 1.0), w=[ones])
        esk = kb.sb("esk", [64, 8], F32, st)
        kb.dma("sp", esk[:, :], self.sinkT[l, :, :], esk, w=[esk])
        kb.op("act", lambda e: e.activation(out=esk[:, :], in_=esk[:, :], func=AF.Exp), r=[esk], w=[esk])
        eskb = kb.sb("eskb", [64, 8, 128], F32, st)
        kb.op("dve", lambda e: e.tensor_copy(out=eskb[:, :, :], in_=esk[:, :].unsqueeze(2).broadcast_to([64, 8, 128])),
              r=[esk], w=[eskb])
        kc = kb.sb("kc", [128, 2, CTX], BF16, st)
        kb.dma("sp", kc[:, :, :], kS.rearrange("(c p) t -> p c t", p=128)[:, :, 0:CTX], kc, r=[("dram", id(kS))], w=[kc])
        vc = kb.sb("vc", [128, 2, 128], BF16, st)
        kb.dma("sp", vc[:, :, :], vS[0:CTX, :].rearrange("(b p) n -> p b n", p=128), vc, r=[("dram", id(vS))], w=[vc])
        qp = Pool(kb, "aq", 2, [128, 4, 128], BF16, stack=st)
        kwp = Pool(kb, "akw", 2, [128, 2, 384], BF16, stack=st)
        vwp = Pool(kb, "avw", 2, [128, 3, 128], BF16, stack=st)
        pp = Pool(kb, "ap", 4, [128, 512], BF16, stack=st)
        dp = Pool(kb, "ad", 2, [64, 512], F32, stack=st)
        op_ = Pool(kb, "ao", 2, [64, 8, 128], BF16, stack=st)
        qv = qS.rearrange("(c p) t -> p c t", p=128)
        kv = kS.rearrange("(c p) t -> p c t", p=128)
        blocks = [(1, b) for b in range(CTX // 128)] + [(0, b) for b in range(NB)]
        self._acc_i = 0
        self._sc_i = 0
        for (seg, b) in blocks:
            t0 = b * 128 if seg == 1 else CTX + b * 128
            q = qp.next()
            kb.dma("sp", q[:, :, :], qv[:, :, t0:t0 + 128], q, r=[("dram", id(qS))], w=[q])
            keyblocks = []
            if seg == 0:
                lo = max(b - 1, 0)
                hi = min(b + 1, NB - 1)
                nb_ = hi - lo + 1
                kw, vw = kwp.next(), vwp.next()
                kb.dma("sp", kw[:, :, 0:nb_ * 128], kv[:, :, CTX + lo * 128:CTX + (hi + 1) * 128], kw,
                       r=[("dram", id(kS))], w=[kw])
                kb.dma("sp", vw[:, 0:nb_, :],
                       vS[CTX + lo * 128:CTX + (hi + 1) * 128, :].rearrange("(b p) n -> p b n", p=128), vw,
                       r=[("dram", id(vS))], w=[vw])
                for bb in range(lo, hi + 1):
                    i = bb - lo
                    mask = mP if bb < b else (mN if bb > b else None)
                    keyblocks.append((kw, i * 128, vw, i, mask))
            for i in range(CTX // 128):
                keyblocks.append((kc, i * 128, vc, i, None))
            oo = op_.next()
            accb = self.psum.bufs[0:4]
            scb = self.psum.bufs[4:8]
            for kvh in range(2):
                Pn = accb[(self._acc_i) % 4]
                Pd = accb[(self._acc_i + 1) % 4]
                self._acc_i += 2
                for bi, (kbuf, koff, vbuf, vi, mask) in enumerate(keyblocks):
                    Ps = [scb[self._sc_i % 4], scb[(self._sc_i + 1) % 4]]
                    self._sc_i += 2
                    pt = pp.next()
                    for par in range(2):
                        base = par * 64
                        var = 0 if (kvh * 64 == base) else 1
                        for j in range(2):
                            h = kvh * 4 + 2 * j + par
                            kb.op("pe", lambda e: e.matmul(Ps[par][:, j * 128:(j + 1) * 128],
                                                           lhsT=kbuf[base:base + 64, var, koff:koff + 128],
                                                           rhs=q[base:base + 64, h // 2, :], start=True, stop=True),
                                  r=[kbuf, q], w=[Ps[par]])
                        kb.op("act", lambda e: e.activation(out=pt[:, par * 256:(par + 1) * 256], in_=Ps[par][:, 0:256],
                                                            func=AF.Exp, scale=0.125), r=[Ps[par]], w=[pt])
                    if mask is not None:
                        kb.op("pool", lambda e: e.tensor_tensor(out=pt[:, :], in0=pt[:, :], in1=mask[:, :], op=ALU.mult),
                              r=[pt, mask], w=[pt])
                    first, last = bi == 0, bi == len(keyblocks) - 1
                    kb.op("pe", lambda e: e.matmul(Pn[0:64, :], lhsT=vbuf[:, vi, kvh * 64:(kvh + 1) * 64], rhs=pt[:, :],
                                                   start=first, stop=last), r=[vbuf, pt], w=[Pn])
                    kb.op("pe", lambda e: e.matmul(Pd[0:64, :], lhsT=ones[:, :], rhs=pt[:, :],
                                                   start=first, stop=last), r=[ones, pt], w=[Pd])
                den = dp.next()
                kb.op("dve", lambda e: e.tensor_tensor(
                    out=den[:, :].rearrange("p (r j q) -> p r j q", r=2, j=2), in0=Pd[0:64, :].rearrange("p (r j q) -> p r j q", r=2, j=2),
                    in1=eskb[:, kvh * 4:(kvh + 1) * 4, :].rearrange("p (j r) q -> p r j q", r=2),
                    op=ALU.add), r=[Pd, eskb], w=[den])
                kb.op("dve", lambda e: e.reciprocal(out=den[:, :], in_=den[:, :]), r=[den], w=[den])
                kb.op("dve", lambda e: e.tensor_tensor(
                    out=oo[:, kvh * 4:(kvh + 1) * 4, :].rearrange("p (j r) q -> p r j q", r=2),
                    in0=Pn[0:64, :].rearrange("p (r j q) -> p r j q", r=2, j=2),
                    in1=den[:, :].rearrange("p (r j q) -> p r j q", r=2, j=2), op=ALU.mult),
                      r=[Pn, den], w=[oo])
            kb.dma("sp", yaS.rearrange("(h p) t -> p h t", p=64)[:, :, t0:t0 + 128], oo[:, :, :], oo, r=[oo],
                   w=[("dram", id(yaS))])


Model.mixA_declare = _mixA_declare
Model.mixA_phase = _mixA_phase
Model.attn_phase = _attn_phase


def _rope_tables(T):
    NT = CTX + T
    C = np.ones((128, NT), np.float32)
    S = np.zeros((128, NT), np.float32)
    t = np.arange(T)
    row = (t // 64).astype(np.float32)
    col = (t % 64).astype(np.float32)
    inv = (10000.0 ** (-np.arange(16, dtype=np.float32) / 16)).astype(np.float32)
    for d in range(64):
        i = d % 16
        pos = row if d < 32 else col
        ang = (pos * inv[i]).astype(np.float32)
        sign = -1.0 if (d % 32) < 16 else 1.0
        for hb in (0, 64):
            C[hb + d, CTX:] = np.cos(ang)
            S[hb + d, CTX:] = sign * np.sin(ang)
    return C, S


def _swap_perm(n_heads):
    idx = []
    for h in range(n_heads):
        for d in range(64):
            p = d + 16 if (d % 32) < 16 else d - 16
            idx.append(h * 64 + p)
    return np.array(idx)


def host_inputs_A(inp, T):
    d = {}
    w = inp["w_in"]
    q = w[:, :, 0:512]
    k = w[:, :, 512:640]
    v = w[:, :, 640:768]
    kB = np.concatenate([k[:, :, 64:128], k[:, :, 0:64]], -1)
    s5 = w[:, :, 2624:3136]
    g = w[:, :, 3136:6208]
    d["w_inA"] = np.ascontiguousarray(np.concatenate(
        [q, q[:, :, _swap_perm(8)], k, k[:, :, _swap_perm(2)], kB, kB[:, :, _swap_perm(2)], v, s5, g], -1))
    C, S = _rope_tables(T)
    d["ropeC"], d["ropeS"] = C, S
    d["sinkT"] = np.ascontiguousarray(np.broadcast_to(inp["attn_sink"][:, None, :], (DEPTH, 64, 8)))
    j = np.arange(128)[:, None]
    i = np.arange(128)[None, :]
    d["maskP"] = np.ascontiguousarray(np.tile((j >= i).astype(np.float32), (1, 4)))
    d["maskN"] = np.ascontiguousarray(np.tile((j <= i).astype(np.float32), (1, 4)))
    return d


I32 = mybir.dt.int32
TWO_PI = 2.0 * np.pi


def _s5_declare(self):
    L = DEPTH
    self.s5_are = self.din("s5_are", [L, 2, 128, 4, 64])
    self.s5_aim = self.din("s5_aim", [L, 2, 128, 4, 64])
    self.s5_ls = self.din("s5_ls", [L, 2, 128, 4])
    self.s5_brT = self.din("s5_brT", [L, 128, 4, 64])
    self.s5_biT = self.din("s5_biT", [L, 128, 4, 64])
    self.s5_are2 = self.din("s5_are2", [L, 2, 128, 16])
    self.s5_aim2 = self.din("s5_aim2", [L, 2, 128, 16])
    self.s5_ls2 = self.din("s5_ls2", [L, 2, 128, 16])
    self.s5_crT = self.din("s5_crT", [L, 128, 16, 16])
    self.s5_ciT = self.din("s5_ciT", [L, 128, 16, 16])
    self.s5_rowmask = self.din("s5_rowmask", [128, 16, 2])
    self.s5_dT = self.din("s5_dT", [128, L * 4])
    self.s5_glub = self.din("s5_glub", [128, L * 4])
    self.s5_gluw = self.din("s5_gluw", [L, 512, 512])
    self.tauT = self.din("tauT", [128, 128])


def _sincos(self, ang, angk, n, S, Sk, C, Ck, st):
    kb = self.kb
    t = kb.sb("sc_t", [128, n], F32, st)
    ti = kb.sb("sc_i", [128, n], I32, st)
    tf = kb.sb("sc_f", [128, n], F32, st)
    for (off, dst, dk) in ((0.0, S, Sk), (0.25, C, Ck)):
        kb.op("dve", lambda e: e.tensor_scalar(out=t[:, :], in0=ang, scalar1=1.0 / TWO_PI, scalar2=off,
                                               op0=ALU.mult, op1=ALU.add), r=[angk], w=[t])
        kb.op("dve", lambda e: e.tensor_copy(out=ti[:, :], in_=t[:, :]), r=[t], w=[ti])
        kb.op("dve", lambda e: e.tensor_copy(out=tf[:, :], in_=ti[:, :]), r=[ti], w=[tf])
        kb.op("dve", lambda e: e.tensor_tensor(out=tf[:, :], in0=t[:, :], in1=tf[:, :], op=ALU.subtract),
              r=[t, tf], w=[tf])
        kb.op("act", lambda e: e.activation(out=dst, in_=tf[:, :], func=AF.Sin, scale=TWO_PI), r=[tf], w=[dk])


def _s5_dir(self, l, d, usS, ysbS, ysS, st, PS2):
    kb = self.kb
    NT = self.NT
    nchunk = NT // 128
    usv = usS.rearrange("(c p) t -> p c t", p=128)
    ybv = ysbS.rearrange("(c p) t -> p c t", p=128)
    ysv = ysS.rearrange("(c p) t -> p c t", p=128)
    V = lambda e_, f, r, w: kb.op(e_, f, r=r, w=w)
    _pers = {}
    for (n_, shp_, dt_) in (("DR", [128, 16, 128], BF16), ("DI", [128, 16, 128], BF16), ("COS", [128, 16, 128], F32),
                            ("SIN", [128, 16, 128], F32), ("RHO0", [128, 16, 128], F32), ("rho", [128, 16], F32),
                            ("lr2", [128, 16], F32), ("li2", [128, 16], F32), ("CR", [128, 16, 128], BF16),
                            ("CIn", [128, 16, 128], BF16), ("s5d", [128, 4], F32), ("s5gb", [128, 4], F32),
                            ("gluw", [128, 4, 512], BF16)):
        _pers[n_] = kb.sb(n_, shp_, dt_, st)
    sbf = lambda n, shp, dt=F32: _pers[n] if n in _pers else kb.sb(n, shp, dt, st)
    st2 = contextlib.ExitStack()
    tmpf = lambda n, shp, dt=F32: kb.sb(n, shp, dt, st2)
    are, aim = tmpf("are", [128, 4, 64]), tmpf("aim", [128, 4, 64])
    ls = tmpf("ls", [128, 4])
    br, bi = tmpf("br", [128, 4, 64]), tmpf("bi", [128, 4, 64])
    kb.dma("sp", are[:, :, :], self.s5_are[l, d], are, w=[are])
    kb.dma("sp", aim[:, :, :], self.s5_aim[l, d], aim, w=[aim])
    kb.dma("sp", ls[:, :], self.s5_ls[l, d], ls, w=[ls])
    kb.dma("sp", br[:, :, :], self.s5_brT[l], br, w=[br])
    kb.dma("sp", bi[:, :, :], self.s5_biT[l], bi, w=[bi])
    rmask = tmpf("rmask", [128, 16, 2])
    kb.dma("sp", rmask[:, :, :], self.s5_rowmask[:, :, :], rmask, w=[rmask])
    V("act", lambda e: e.activation(out=ls[:, :], in_=ls[:, :], func=AF.Exp), [ls], [ls])
    dtb = ls[:, :].unsqueeze(2).broadcast_to([128, 4, 64])
    adt, th = tmpf("adt", [128, 4, 64]), tmpf("th", [128, 4, 64])
    V("dve", lambda e: e.tensor_tensor(out=adt[:, :, :], in0=are[:, :, :], in1=dtb, op=ALU.mult), [are, ls], [adt])
    V("act", lambda e: e.activation(out=adt[:, :, :], in_=adt[:, :, :], func=AF.Exp), [adt], [adt])
    V("dve", lambda e: e.tensor_tensor(out=th[:, :, :], in0=aim[:, :, :], in1=dtb, op=ALU.mult), [aim, ls], [th])
    Sd, Cd = tmpf("Sd", [128, 256]), tmpf("Cd", [128, 256])
    thf = th[:, :, :].rearrange("p a b -> p (a b)")
    self.sincos(thf, th, 256, Sd[:, :], Sd, Cd[:, :], Cd, st2)
    lr, li = tmpf("lr", [128, 256]), tmpf("li", [128, 256])
    magf = adt[:, :, :].rearrange("p a b -> p (a b)")
    V("dve", lambda e: e.tensor_tensor(out=lr[:, :], in0=magf, in1=Cd[:, :], op=ALU.mult), [adt, Cd], [lr])
    V("dve", lambda e: e.tensor_tensor(out=li[:, :], in0=magf, in1=Sd[:, :], op=ALU.mult), [adt, Sd], [li])
    aref = are[:, :, :].rearrange("p a b -> p (a b)")
    aimf = aim[:, :, :].rearrange("p a b -> p (a b)")
    t1, t2, den = tmpf("t1", [128, 256]), tmpf("t2", [128, 256]), tmpf("den", [128, 256])
    V("dve", lambda e: e.tensor_tensor(out=t1[:, :], in0=aref, in1=aref, op=ALU.mult), [are], [t1])
    V("dve", lambda e: e.tensor_tensor(out=t2[:, :], in0=aimf, in1=aimf, op=ALU.mult), [aim], [t2])
    V("dve", lambda e: e.tensor_tensor(out=den[:, :], in0=t1[:, :], in1=t2[:, :], op=ALU.add), [t1, t2], [den])
    V("dve", lambda e: e.reciprocal(out=den[:, :], in_=den[:, :]), [den], [den])
    V("dve", lambda e: e.tensor_scalar(out=lr[:, :], in0=lr[:, :], scalar1=-1.0, scalar2=None, op0=ALU.add),
      [lr], [lr])
    cr, ci = tmpf("cr", [128, 256]), tmpf("ci", [128, 256])
    V("dve", lambda e: e.tensor_tensor(out=t1[:, :], in0=lr[:, :], in1=aref, op=ALU.mult), [lr, are], [t1])
    V("dve", lambda e: e.tensor_tensor(out=t2[:, :], in0=li[:, :], in1=aimf, op=ALU.mult), [li, aim], [t2])
    V("dve", lambda e: e.tensor_tensor(out=cr[:, :], in0=t1[:, :], in1=t2[:, :], op=ALU.add), [t1, t2], [cr])
    V("dve", lambda e: e.tensor_tensor(out=cr[:, :], in0=cr[:, :], in1=den[:, :], op=ALU.mult), [cr, den], [cr])
    V("dve", lambda e: e.tensor_tensor(out=t1[:, :], in0=li[:, :], in1=aref, op=ALU.mult), [li, are], [t1])
    V("dve", lambda e: e.tensor_tensor(out=t2[:, :], in0=lr[:, :], in1=aimf, op=ALU.mult), [lr, aim], [t2])
    V("dve", lambda e: e.tensor_tensor(out=ci[:, :], in0=t1[:, :], in1=t2[:, :], op=ALU.subtract), [t1, t2], [ci])
    V("dve", lambda e: e.tensor_tensor(out=ci[:, :], in0=ci[:, :], in1=den[:, :], op=ALU.mult), [ci, den], [ci])
    brf = br[:, :, :].rearrange("p a b -> p (a b)")
    bif = bi[:, :, :].rearrange("p a b -> p (a b)")
    bbr, bbi = tmpf("bbr", [128, 4, 64]), tmpf("bbi", [128, 4, 64])
    bbrf = bbr[:, :, :].rearrange("p a b -> p (a b)")
    bbif = bbi[:, :, :].rearrange("p a b -> p (a b)")
    V("dve", lambda e: e.tensor_tensor(out=t1[:, :], in0=cr[:, :], in1=brf, op=ALU.mult), [cr, br], [t1])
    V("dve", lambda e: e.tensor_tensor(out=t2[:, :], in0=ci[:, :], in1=bif, op=ALU.mult), [ci, bi], [t2])
    V("dve", lambda e: e.tensor_tensor(out=bbrf, in0=t1[:, :], in1=t2[:, :], op=ALU.subtract), [t1, t2], [bbr])
    V("dve", lambda e: e.tensor_tensor(out=t1[:, :], in0=cr[:, :], in1=bif, op=ALU.mult), [cr, bi], [t1])
    V("dve", lambda e: e.tensor_tensor(out=t2[:, :], in0=ci[:, :], in1=brf, op=ALU.mult), [ci, br], [t2])
    V("dve", lambda e: e.tensor_tensor(out=bbif, in0=t1[:, :], in1=t2[:, :], op=ALU.add), [t1, t2], [bbi])
    DR, DI = sbf("DR", [128, 16, 128], BF16), sbf("DI", [128, 16, 128], BF16)
    for j in range(16):
        for gp in range(2):
            for (dst, srcb) in ((DR, bbr), (DI, bbi)):
                V("dve", lambda e: e.tensor_scalar(out=dst[:, j, gp * 64:(gp + 1) * 64], in0=srcb[:, j // 4, :],
                                                   scalar1=rmask[:, j, gp:gp + 1], scalar2=None, op0=ALU.mult),
                  [srcb, rmask], [dst])
    are2, aim2, ls2 = tmpf("are2", [128, 16]), tmpf("aim2", [128, 16]), tmpf("ls2", [128, 16])
    kb.dma("sp", are2[:, :], self.s5_are2[l, d], are2, w=[are2])
    kb.dma("sp", aim2[:, :], self.s5_aim2[l, d], aim2, w=[aim2])
    kb.dma("sp", ls2[:, :], self.s5_ls2[l, d], ls2, w=[ls2])
    tau = tmpf("tau", [128, 128])
    kb.dma("sp", tau[:, :], self.tauT[:, :], tau, w=[tau])
    V("act", lambda e: e.activation(out=ls2[:, :], in_=ls2[:, :], func=AF.Exp), [ls2], [ls2])
    rho, th2 = sbf("rho", [128, 16]), tmpf("th2", [128, 16])
    V("dve", lambda e: e.tensor_tensor(out=rho[:, :], in0=are2[:, :], in1=ls2[:, :], op=ALU.mult), [are2, ls2], [rho])
    V("act", lambda e: e.activation(out=rho[:, :], in_=rho[:, :], func=AF.Exp), [rho], [rho])
    V("dve", lambda e: e.tensor_tensor(out=th2[:, :], in0=aim2[:, :], in1=ls2[:, :], op=ALU.mult), [aim2, ls2], [th2])
    ang = tmpf("ang", [128, 16, 128])
    V("dve", lambda e: e.tensor_tensor(out=ang[:, :, :], in0=th2[:, :].unsqueeze(2).broadcast_to([128, 16, 128]),
                                       in1=tau[:, :].unsqueeze(1).broadcast_to([128, 16, 128]), op=ALU.mult),
      [th2, tau], [ang])
    COS, SIN = sbf("COS", [128, 16, 128]), sbf("SIN", [128, 16, 128])
    self.sincos(ang[:, :, :].rearrange("p a b -> p (a b)"), ang, 2048,
                SIN[:, :, :].rearrange("p a b -> p (a b)"), SIN, COS[:, :, :].rearrange("p a b -> p (a b)"), COS, st2)
    S1, C1 = tmpf("S1", [128, 16]), tmpf("C1", [128, 16])
    self.sincos(th2[:, :], th2, 16, S1[:, :], S1, C1[:, :], C1, st2)
    lr2, li2 = sbf("lr2", [128, 16]), sbf("li2", [128, 16])
    V("dve", lambda e: e.tensor_tensor(out=lr2[:, :], in0=rho[:, :], in1=C1[:, :], op=ALU.mult), [rho, C1], [lr2])
    V("dve", lambda e: e.tensor_tensor(out=li2[:, :], in0=rho[:, :], in1=S1[:, :], op=ALU.mult), [rho, S1], [li2])
    RHO0 = sbf("RHO0", [128, 16, 128])
    V("dve", lambda e: e.tensor_copy(out=RHO0[:, :, :], in_=rho[:, :].unsqueeze(2).broadcast_to([128, 16, 128])),
      [rho], [RHO0])
    f0 = 127 if d == 1 else 0
    V("dve", lambda e: e.memset(RHO0[:, :, f0:f0 + 1], 0.0), [], [RHO0])
    crT, ciT = tmpf("crT", [128, 16, 16]), tmpf("ciT", [128, 16, 16])
    kb.dma("sp", crT[:, :, :], self.s5_crT[l], crT, w=[crT])
    kb.dma("sp", ciT[:, :, :], self.s5_ciT[l], ciT, w=[ciT])
    CR, CIn = sbf("CR", [128, 16, 128], BF16), sbf("CIn", [128, 16, 128], BF16)
    V("dve", lambda e: e.memset(CR[:, :, :], 0.0), [], [CR])
    V("dve", lambda e: e.memset(CIn[:, :, :], 0.0), [], [CIn])
    for j in range(16):
        for gp in range(2):
            c0 = 32 * (j % 4) + 16 * gp
            V("dve", lambda e: e.tensor_copy(out=CR[gp * 64:(gp + 1) * 64, j, c0:c0 + 16],
                                             in_=crT[gp * 64:(gp + 1) * 64, j, :]), [crT], [CR])
            V("dve", lambda e: e.tensor_scalar(out=CIn[gp * 64:(gp + 1) * 64, j, c0:c0 + 16],
                                               in0=ciT[gp * 64:(gp + 1) * 64, j, :], scalar1=-1.0, scalar2=None,
                                               op0=ALU.mult), [ciT], [CIn])
    if d == 0:
        dv, gb = sbf("s5d", [128, 4]), sbf("s5gb", [128, 4])
        kb.dma("sp", dv[:, :], self.s5_dT[:, l * 4:(l + 1) * 4], dv, w=[dv])
        kb.dma("sp", gb[:, :], self.s5_glub[:, l * 4:(l + 1) * 4], gb, w=[gb])
        gw = sbf("gluw", [128, 4, 512], BF16)
        for c in range(4):
            kb.dma("pool", gw[:, c, :], self.s5_gluw[l, c * 128:(c + 1) * 128, :], gw, w=[gw])
    kb.barrier()
    st2.close()
    usp = Pool(kb, "s5u", 2, [128, 4, 128], BF16, stack=st)
    usfp = Pool(kb, "s5uf", 2, [128, 4, 128], F32, stack=st)
    ybp = Pool(kb, "s5yb", 2, [128, 4, 128], F32, stack=st)
    mp = [Pool(kb, "s5m%d" % i, 1, [128, 8, 128], F32, stack=st) for i in range(4)]
    ZR, ZI = sbf("ZR", [128, 16, 128]), sbf("ZI", [128, 16, 128])
    XZR, XZI = sbf("XZR", [128, 16, 128]), sbf("XZI", [128, 16, 128])
    up_ = [Pool(kb, "s5t%d" % i, 1, [128, 16, 128], F32, stack=st) for i in range(2)]
    XR, XI = sbf("XR", [128, 16, 128], BF16), sbf("XI", [128, 16, 128], BF16)
    xlr, xli = sbf("xlr", [128, 16]), sbf("xli", [128, 16])
    cjr, cji = sbf("cjr", [128, 16]), sbf("cji", [128, 16])
    tt = [sbf("s5tt%d" % i, [128, 16]) for i in range(4)]
    yo = Pool(kb, "s5yo", 2, [128, 4, 128], F32, stack=st)
    rev = (d == 1)
    R3 = (lambda ap: ap[:, :, ::-1]) if rev else (lambda ap: ap)
    first, last = (127, 0) if rev else (0, 127)
    order = [0, 1] + list(range(2, nchunk))
    if rev:
        order = [1, 0] + list(range(nchunk - 1, 1, -1))
    yield
    for ci_, ch in enumerate(order):
        if ci_ > 0:
            yield
        t0 = ch * 128
        us = usp.next()
        kb.dma("pool", us[:, :, :], usv[:, :, t0:t0 + 128], us, r=[("dram", id(usS))], w=[us])
        for hf in range(2):
            PR, PI = PS2.next(), PS2.next()
            for jj in range(8):
                j = hf * 8 + jj
                kb.op("pe", lambda e: e.matmul(PR[:, jj * 128:(jj + 1) * 128], lhsT=DR[:, j, :], rhs=us[:, j // 4, :],
                                               start=True, stop=True), r=[DR, us], w=[PR])
                kb.op("pe", lambda e: e.matmul(PI[:, jj * 128:(jj + 1) * 128], lhsT=DI[:, j, :], rhs=us[:, j // 4, :],
                                               start=True, stop=True), r=[DI, us], w=[PI])
            prv = PR[:, :].rearrange("p (a b) -> p a b", a=8)
            piv = PI[:, :].rearrange("p (a b) -> p a b", a=8)
            cs = R3(COS[:, hf * 8:(hf + 1) * 8, :])
            sn = R3(SIN[:, hf * 8:(hf + 1) * 8, :])
            m = [p.next() for p in mp]
            V("dve", lambda e: e.tensor_tensor(out=m[0][:, :, :], in0=prv, in1=cs, op=ALU.mult), [PR, COS], [m[0]])
            V("dve", lambda e: e.tensor_tensor(out=m[1][:, :, :], in0=piv, in1=sn, op=ALU.mult), [PI, SIN], [m[1]])
            V("dve", lambda e: e.tensor_tensor(out=m[2][:, :, :], in0=piv, in1=cs, op=ALU.mult), [PI, COS], [m[2]])
            V("dve", lambda e: e.tensor_tensor(out=m[3][:, :, :], in0=prv, in1=sn, op=ALU.mult), [PR, SIN], [m[3]])
            V("pool", lambda e: e.tensor_tensor(out=ZR[:, hf * 8:(hf + 1) * 8, :], in0=m[0][:, :, :], in1=m[1][:, :, :],
                                                op=ALU.add), [m[0], m[1]], [ZR])
            V("pool", lambda e: e.tensor_tensor(out=ZI[:, hf * 8:(hf + 1) * 8, :], in0=m[2][:, :, :], in1=m[3][:, :, :],
                                                op=ALU.subtract), [m[2], m[3]], [ZI])
            yield
        if ci_ > 0:
            V("pool", lambda e: e.tensor_tensor(out=tt[0][:, :], in0=lr2[:, :], in1=xlr[:, :], op=ALU.mult), [lr2, xlr], [tt[0]])
            V("pool", lambda e: e.tensor_tensor(out=tt[1][:, :], in0=li2[:, :], in1=xli[:, :], op=ALU.mult), [li2, xli], [tt[1]])
            V("pool", lambda e: e.tensor_tensor(out=cjr[:, :], in0=tt[0][:, :], in1=tt[1][:, :], op=ALU.subtract), [tt[0], tt[1]], [cjr])
            V("pool", lambda e: e.tensor_tensor(out=tt[2][:, :], in0=lr2[:, :], in1=xli[:, :], op=ALU.mult), [lr2, xli], [tt[2]])
            V("pool", lambda e: e.tensor_tensor(out=tt[3][:, :], in0=li2[:, :], in1=xlr[:, :], op=ALU.mult), [li2, xlr], [tt[3]])
            V("pool", lambda e: e.tensor_tensor(out=cji[:, :], in0=tt[2][:, :], in1=tt[3][:, :], op=ALU.add), [tt[2], tt[3]], [cji])
            V("pool", lambda e: e.tensor_tensor(out=ZR[:, :, first], in0=ZR[:, :, first], in1=cjr[:, :], op=ALU.add), [ZR, cjr], [ZR])
            V("pool", lambda e: e.tensor_tensor(out=ZI[:, :, first], in0=ZI[:, :, first], in1=cji[:, :], op=ALU.add), [ZI, cji], [ZI])
        fl = lambda b_: (b_[:, :, :].rearrange("p a b -> p (a b)")[:, ::-1] if rev
                         else b_[:, :, :].rearrange("p a b -> p (a b)"))
        V("dve", lambda e: e.tensor_tensor_scan(out=fl(XZR), data0=fl(RHO0), data1=fl(ZR), initial=0.0,
                                                op0=ALU.mult, op1=ALU.add), [RHO0, ZR], [XZR])
        yield
        V("dve", lambda e: e.tensor_tensor_scan(out=fl(XZI), data0=fl(RHO0), data1=fl(ZI), initial=0.0,
                                                op0=ALU.mult, op1=ALU.add), [RHO0, ZI], [XZI])
        yield
        cs, sn = R3(COS[:, :, :]), R3(SIN[:, :, :])
        ua, ub = up_[0].next(), up_[1].next()
        V("dve", lambda e: e.tensor_tensor(out=ua[:, :, :], in0=XZR[:, :, :], in1=cs, op=ALU.mult), [XZR, COS], [ua])
        V("pool", lambda e: e.tensor_tensor(out=ub[:, :, :], in0=XZI[:, :, :], in1=sn, op=ALU.mult), [XZI, SIN], [ub])
        V("dve", lambda e: e.tensor_tensor(out=XR[:, :, :], in0=ua[:, :, :], in1=ub[:, :, :], op=ALU.subtract), [ua, ub], [XR])
        V("pool", lambda e: e.tensor_tensor(out=xlr[:, :], in0=ua[:, :, last], in1=ub[:, :, last], op=ALU.subtract), [ua, ub], [xlr])
        yield
        V("pool", lambda e: e.tensor_tensor(out=ua[:, :, :], in0=XZR[:, :, :], in1=sn, op=ALU.mult), [XZR, SIN], [ua])
        V("dve", lambda e: e.tensor_tensor(out=ub[:, :, :], in0=XZI[:, :, :], in1=cs, op=ALU.mult), [XZI, COS], [ub])
        V("pool", lambda e: e.tensor_tensor(out=XI[:, :, :], in0=ua[:, :, :], in1=ub[:, :, :], op=ALU.add), [ua, ub], [XI])
        V("pool", lambda e: e.tensor_tensor(out=xli[:, :], in0=ua[:, :, last], in1=ub[:, :, last], op=ALU.add), [ua, ub], [xli])
        yield
        PY = PS2.next()
        for cc in range(4):
            for jj in range(4):
                j = cc * 4 + jj
                kb.op("pe", lambda e: e.matmul(PY[:, cc * 128:(cc + 1) * 128], lhsT=CR[:, j, :], rhs=XR[:, j, :],
                                               start=(jj == 0), stop=False), r=[CR, XR], w=[PY])
                kb.op("pe", lambda e: e.matmul(PY[:, cc * 128:(cc + 1) * 128], lhsT=CIn[:, j, :], rhs=XI[:, j, :],
                                               start=False, stop=(jj == 3)), r=[CIn, XI], w=[PY])
        pyv = PY[:, 0:512].rearrange("p (a b) -> p a b", a=4)
        if d == 1:
            y = yo.next()
            V("act", lambda e: e.activation(out=y[:, :, :], in_=pyv, func=AF.Copy), [PY], [y])
            kb.dma("act", ybv[:, :, t0:t0 + 128], y[:, :, :], y, r=[y], w=[("dram", id(ysbS))])
        else:
            yb, usf = ybp.next(), usfp.next()
            kb.dma("sp", yb[:, :, :], ybv[:, :, t0:t0 + 128], yb, r=[("dram", id(ysbS))], w=[yb])
            kb.dma("sp", usf[:, :, :], usv[:, :, t0:t0 + 128], usf, r=[("dram", id(usS))], w=[usf])
            y = yo.next()
            V("dve", lambda e: e.tensor_tensor(out=y[:, :, :], in0=pyv, in1=yb[:, :, :], op=ALU.add), [PY, yb], [y])
            V("pool", lambda e: e.tensor_tensor(out=usf[:, :, :], in0=usf[:, :, :],
                                                in1=dv[:, :].unsqueeze(2).broadcast_to([128, 4, 128]), op=ALU.mult),
              [usf, dv], [usf])
            V("pool", lambda e: e.tensor_tensor(out=y[:, :, :], in0=y[:, :, :], in1=usf[:, :, :], op=ALU.add), [y, usf], [y])
            g1 = yb
            V("pool", lambda e: e.tensor_tensor(out=g1[:, :, :], in0=y[:, :, :], in1=y[:, :, :], op=ALU.mult), [y], [g1])
            V("dve", lambda e: e.tensor_scalar(out=g1[:, :, :], in0=g1[:, :, :], scalar1=0.044715, scalar2=1.0,
                                               op0=ALU.mult, op1=ALU.add), [g1], [g1])
            V("dve", lambda e: e.tensor_tensor(out=g1[:, :, :], in0=g1[:, :, :], in1=y[:, :, :], op=ALU.mult), [g1, y], [g1])
            V("act", lambda e: e.activation(out=g1[:, :, :], in_=g1[:, :, :], func=AF.Sigmoid, scale=1.5957691216057308),
              [g1], [g1])
            V("dve", lambda e: e.tensor_tensor(out=y[:, :, :], in0=y[:, :, :], in1=g1[:, :, :], op=ALU.mult), [y, g1], [y])
            geb = us
            V("act", lambda e: e.activation(out=geb[:, :, :], in_=y[:, :, :], func=AF.Copy), [y], [geb])
            PG = PS2.next()
            for oc in range(4):
                for kc in range(4):
                    kb.op("pe", lambda e: e.matmul(PG[:, oc * 128:(oc + 1) * 128], lhsT=gw[:, kc, oc * 128:(oc + 1) * 128],
                                                   rhs=geb[:, kc, :], start=(kc == 0), stop=(kc == 3)), r=[gw, geb], w=[PG])
                V("act", lambda e: e.activation(out=usf[:, oc, :], in_=PG[:, oc * 128:(oc + 1) * 128], func=AF.Sigmoid,
                                                bias=gb[:, oc:oc + 1]), [PG, gb], [usf])
            yso = usp.next()
            V("dve", lambda e: e.tensor_tensor(out=yso[:, :, :], in0=y[:, :, :], in1=usf[:, :, :], op=ALU.mult), [y, usf], [yso])
            kb.dma("sp", ysv[:, :, t0:t0 + 128], yso[:, :, :], yso, r=[yso], w=[("dram", id(ysS))])


Model.s5_declare = _s5_declare
Model.sincos = _sincos
Model.s5_dir = _s5_dir


def host_inputs_s5(inp):
    L = DEPTH
    d = {}

    def drive(a):
        a = a.reshape(L, 2, 4, 8, 1, 64)
        a = np.broadcast_to(a, (L, 2, 4, 8, 16, 64))
        return np.ascontiguousarray(a.transpose(0, 1, 3, 4, 2, 5).reshape(L, 2, 128, 4, 64))
    d["s5_are"] = drive(inp["s5_a_re"])
    d["s5_aim"] = drive(inp["s5_a_im"])
    lsd = inp["s5_log_step"].reshape(L, 2, 4, 8, 1)
    d["s5_ls"] = np.ascontiguousarray(np.broadcast_to(lsd, (L, 2, 4, 8, 16)).transpose(0, 1, 3, 4, 2).reshape(L, 2, 128, 4))

    def bT(b):
        b = b.reshape(L, 4, 8, 64, 16)
        return np.ascontiguousarray(b.transpose(0, 2, 4, 1, 3).reshape(L, 128, 4, 64))
    d["s5_brT"] = bT(inp["s5_b_re"])
    d["s5_biT"] = bT(inp["s5_b_im"])

    def st2(a):
        a = a.reshape(L, 2, 16, 2, 64)
        return np.ascontiguousarray(a.transpose(0, 1, 3, 4, 2).reshape(L, 2, 128, 16))
    d["s5_are2"] = st2(inp["s5_a_re"])
    d["s5_aim2"] = st2(inp["s5_a_im"])
    ls2 = np.broadcast_to(inp["s5_log_step"].reshape(L, 2, 16, 2, 1), (L, 2, 16, 2, 64))
    d["s5_ls2"] = np.ascontiguousarray(ls2.transpose(0, 1, 3, 4, 2).reshape(L, 2, 128, 16))

    def cT(c):
        c = c.reshape(L, 16, 2, 16, 64)
        return np.ascontiguousarray(c.transpose(0, 2, 4, 1, 3).reshape(L, 128, 16, 16))
    d["s5_crT"] = cT(inp["s5_c_re"])
    d["s5_ciT"] = cT(inp["s5_c_im"])
    k = np.arange(128)[:, None, None] // 16
    j = np.arange(16)[None, :, None]
    gp = np.arange(2)[None, None, :]
    d["s5_rowmask"] = np.ascontiguousarray((k == 2 * (j % 4) + gp).astype(np.float32))
    d["s5_dT"] = np.ascontiguousarray(inp["s5_d"].reshape(L * 4, 128).T)
    d["s5_glub"] = np.ascontiguousarray(inp["s5_glu_b"].reshape(L * 4, 128).T)
    d["s5_gluw"] = inp["s5_glu_w"]
    d["tauT"] = np.ascontiguousarray(np.broadcast_to(np.arange(128, dtype=np.float32)[None, :], (128, 128)))
    return d


NCH_B = 15


def _rwkv_declare(self):
    L = DEPTH
    self.w_inB = self.din("w_inB", [L, D, NCH_B * 128])
    self.rk_mu = self.din("rk_mu", [128, L * NCH_B])
    self.rk_w0 = self.din("rk_w0", [128, L * 8])
    for n in ("a0", "kk", "ka", "rk", "gng", "gnb"):
        setattr(self, "rk_" + n, self.din("rk_" + n, [128, L * 4]))
    self.rk_w2 = self.din("rk_w2", [L, 128, 512])
    self.rk_a2 = self.din("rk_a2", [L, 64, 512])
    self.rk_g2 = self.din("rk_g2", [L, 128, 512])
    self.bd64 = self.din("bd64", [128, 128])
    self.identb = self.din("identb", [128, 128])
    self.ones0 = self.din("ones0", [2, 128, 128])
    self.rk_MT = self.din("rk_MT", [2, 128, 512])
    self.rk_MN = self.din("rk_MN", [2, 128, 512])


def _rwkv_prep_phase(self, l, src, S):
    kb = self.kb
    W = 256
    Wh = W + 2
    NB = W // 128
    with kb.phase() as st:
        self.psum = Pool(kb, "ps", 8, [128, 512], F32, psum=True, stack=st)
        V = lambda e_, f, r, w: kb.op(e_, f, r=r, w=w)
        sbf = lambda n, shp, dt=F32: kb.sb(n, shp, dt, st)
        w = sbf("wB", [128, NCH, NCH_B * 128], BF16)
        for c in range(NCH):
            kb.dma("pool", w[:, c, :], self.w_inB[l, c * 128:(c + 1) * 128, :], w, w=[w])
        w2b, a2b, g2b = sbf("w2b", [128, 512], BF16), sbf("a2b", [64, 512], BF16), sbf("g2b", [128, 512], BF16)
        kb.dma("pool", w2b[:, :], self.rk_w2[l], w2b, w=[w2b])
        kb.dma("pool", a2b[:, :], self.rk_a2[l], a2b, w=[a2b])
        kb.dma("pool", g2b[:, :], self.rk_g2[l], g2b, w=[g2b])
        bd, idb = sbf("bd", [128, 128], BF16), sbf("idb", [128, 128], BF16)
        kb.dma("pool", bd[:, :], self.bd64[:, :], bd, w=[bd])
        kb.dma("pool", idb[:, :], self.identb[:, :], idb, w=[idb])
        on0 = sbf("on0", [128, 2, 128])
        kb.dma("sp", on0[:, :, :], self.ones0.rearrange("d p t -> p d t"), on0, w=[on0])
        ON = [sbf("ON%d" % d_, [128, 4 * (W // 128), 128]) for d_ in range(2)]
        for d_ in range(2):
            V("dve", lambda e: e.tensor_copy(out=ON[d_][:, :, :], in_=on0[:, d_:d_ + 1, :].broadcast_to([128, 4 * (W // 128), 128])),
              [on0], [ON[d_]])
        mu, omu, hmu = sbf("mu", [128, NCH_B]), sbf("omu", [128, NCH_B]), sbf("hmu", [128, NCH_B])
        kb.dma("sp", mu[:, :], self.rk_mu[:, l * NCH_B:(l + 1) * NCH_B], mu, w=[mu])
        V("dve", lambda e: e.tensor_scalar(out=omu[:, :], in0=mu[:, :], scalar1=-1.0, scalar2=1.0, op0=ALU.mult, op1=ALU.add), [mu], [omu])
        V("dve", lambda e: e.tensor_scalar(out=hmu[:, :], in0=mu[:, :], scalar1=0.5, scalar2=None, op0=ALU.mult), [mu], [hmu])
        w0 = sbf("w0", [128, 8])
        kb.dma("sp", w0[:, :], self.rk_w0[:, l * 8:(l + 1) * 8], w0, w=[w0])
        pv = {}
        for n in ("a0", "kk", "ka", "rk"):
            pv[n] = sbf("p_" + n, [128, 4])
            kb.dma("sp", pv[n][:, :], getattr(self, "rk_" + n)[:, l * 4:(l + 1) * 4], pv[n], w=[pv[n]])
        omka = sbf("omka", [128, 4])
        V("dve", lambda e: e.tensor_scalar(out=omka[:, :], in0=pv["ka"][:, :], scalar1=-1.0, scalar2=1.0, op0=ALU.mult, op1=ALU.add),
          [pv["ka"]], [omka])
        pl = self.stat_pools(Wh, st)
        xp = Pool(kb, "x", 2, [128, NCH, Wh], F32, stack=st)
        xn = sbf("xn", [128, NCH, Wh])
        u = sbf("u", [128, NCH, Wh], BF16)
        zcp = Pool(kb, "zc", 2, [128, Wh], F32, stack=st)
        tmpp = Pool(kb, "ztmp", 2, [128, W], F32, stack=st)
        Z = sbf("Z", [128, NCH_B, W])
        tw, sg, alb = sbf("tw", [128, W], BF16), sbf("sg", [128, W], BF16), sbf("alb", [64, W], BF16)
        LW = [sbf("LW%d" % d_, [128, 4, W]) for d_ in range(2)]
        A, KK, KM, Bv = sbf("A", [128, 4, W]), sbf("KK", [128, 4, W]), sbf("KM", [128, 4, W]), sbf("Bv", [128, 4, W])
        SQ = sbf("SQ", [128, 4, W], BF16)
        T1, T2 = sbf("T1", [128, 4, W]), sbf("T2", [128, 4, W])
        Gp = Pool(kb, "Go", 2, [128, 4, W], BF16, stack=st)
        Bop = Pool(kb, "Bo", 2, [128, 4, W], BF16, stack=st)
        Vb = sbf("Vb", [128, 4, W], BF16)
        tokp = Pool(kb, "tok", 3, [128, NB, 512], BF16, stack=st)
        Lc, Lr, Lq = sbf("Lc", [128, 4, W]), sbf("Lr", [128, 4, W]), sbf("Lq", [128, 4, W])
        E = sbf("E", [128, 4, W])
        outp = {n: Pool(kb, n, 2, [128, 4, W], BF16, stack=st) for n in ("RHO", "KAP", "BET", "KTI")}
        scp = Pool(kb, "sco", 2, [128, NB, 4, 3], F32, stack=st)
        lmn = sbf("lmn", [128, 4, NB])
        srcv = src.rearrange("(c p) t -> p c t", p=128)
        fmv = lambda t_: t_.rearrange("(c p) t -> p c t", p=128)

        def transpose_store(srcb, dst, t0):
            tk = tokp.next()
            for tb in range(NB):
                P = self.psum.next()
                pb = P[:, 0:256].bitcast(BF16)
                for c in range(4):
                    kb.op("pe", lambda e: e.transpose(pb[:, c * 128:(c + 1) * 128], srcb[:, c, tb * 128:(tb + 1) * 128], idb[:, :]),
                          r=[srcb, idb], w=[P])
                V("act", lambda e: e.activation(out=tk[:, tb, :], in_=pb[:, 0:512], func=AF.Copy), [P], [tk])
            kb.dma("sp", dst[t0:t0 + W, :].rearrange("(b p) n -> p b n", p=128), tk[:, :, :], tk, r=[tk], w=[("dram", id(dst))])

        for (seg, t0, _) in self.tiles(W):
            seg_lo, seg_hi = (0, CTX) if seg == 1 else (CTX, self.NT)
            lo, hi = max(t0 - 1, seg_lo), min(t0 + W + 1, seg_hi)
            x = xp.next()
            c0 = lo - (t0 - 1)
            if c0 > 0:
                V("dve", lambda e: e.memset(x[:, :, 0:1], 0.0), [], [x])
            if hi < t0 + W + 1:
                V("dve", lambda e: e.memset(x[:, :, Wh - 1:Wh], 0.0), [], [x])
            kb.dma("sp", x[:, :, c0:c0 + (hi - lo)], srcv[:, :, lo:hi], x, r=[("dram", id(src))], w=[x])
            rstd, nmr = self.ln_stats_w(x, Wh, pl)
            self.normalize(xn, x, Wh, rstd, nmr)
            for c in range(NCH):
                V("act", lambda e: e.activation(out=u[:, c, :], in_=xn[:, c, :], func=AF.Identity,
                                                scale=self.modp1[:, 4 * NCH + c, seg:seg + 1],
                                                bias=self.mods[:, 3 * NCH + c, seg:seg + 1]), [xn, self.modp1, self.mods], [u])
            if c0 > 0:
                V("dve", lambda e: e.memset(u[:, :, 0:1], 0.0), [], [u])
            if hi < t0 + W + 1:
                V("dve", lambda e: e.memset(u[:, :, Wh - 1:Wh], 0.0), [], [u])
            for ch in range(NCH_B):
                P = self.psum.next()
                for c in range(NCH):
                    kb.op("pe", lambda e: e.matmul(P[:, 0:Wh], lhsT=w[:, c, ch * 128:(ch + 1) * 128], rhs=u[:, c, :],
                                                   start=(c == 0), stop=(c == NCH - 1)), r=[w, u], w=[P])
                zc, tm = zcp.next(), tmpp.next()
                V("act", lambda e: e.activation(out=zc[:, :], in_=P[:, 0:Wh], func=AF.Copy), [P], [zc])
                V("dve", lambda e: e.tensor_tensor(out=tm[:, :], in0=zc[:, 0:W], in1=zc[:, 2:W + 2], op=ALU.add), [zc], [tm])
                V("act", lambda e: e.activation(out=tm[:, :], in_=tm[:, :], func=AF.Identity, scale=hmu[:, ch:ch + 1]), [tm, hmu], [tm])
                V("dve", lambda e: e.scalar_tensor_tensor(out=Z[:, ch, :], in0=zc[:, 1:W + 1], scalar=omu[:, ch:ch + 1],
                                                          in1=tm[:, :], op0=ALU.mult, op1=ALU.add), [zc, omu, tm], [Z])
            R_, K_, V_ = Z[:, 0:4, :], Z[:, 4:8, :], Z[:, 8:12, :]
            V("act", lambda e: e.activation(out=tw[:, :], in_=Z[:, 12, :], func=AF.Tanh), [Z], [tw])
            V("act", lambda e: e.activation(out=sg[:, :], in_=Z[:, 13, :], func=AF.Sigmoid), [Z], [sg])
            V("act", lambda e: e.activation(out=alb[:, :], in_=Z[0:64, 14, :], func=AF.Copy), [Z], [alb])
            for d_ in range(2):
                for c in range(4):
                    P = self.psum.next()
                    kb.op("pe", lambda e: e.matmul(P[:, 0:W], lhsT=w2b[d_ * 64:(d_ + 1) * 64, c * 128:(c + 1) * 128],
                                                   rhs=tw[d_ * 64:(d_ + 1) * 64, :], start=True, stop=True), r=[w2b, tw], w=[P])
                    V("act", lambda e: e.activation(out=LW[d_][:, c, :], in_=P[:, 0:W], func=AF.Sigmoid,
                                                    bias=w0[:, d_ * 4 + c:d_ * 4 + c + 1]), [P, w0], [LW[d_]])
                V("pool", lambda e: e.tensor_scalar(out=LW[d_][:, :, :], in0=LW[d_][:, :, :], scalar1=-DECAY_SCALE, scalar2=None,
                                                    op0=ALU.mult), [LW[d_]], [LW[d_]])
            Go = Gp.next()
            for c in range(4):
                P = self.psum.next()
                kb.op("pe", lambda e: e.matmul(P[:, 0:W], lhsT=a2b[0:64, c * 128:(c + 1) * 128], rhs=alb[0:64, :],
                                               start=True, stop=True), r=[a2b, alb], w=[P])
                V("act", lambda e: e.activation(out=A[:, c, :], in_=P[:, 0:W], func=AF.Sigmoid, bias=pv["a0"][:, c:c + 1]),
                  [P, pv["a0"]], [A])
                P = self.psum.next()
                kb.op("pe", lambda e: e.matmul(P[:, 0:W], lhsT=g2b[:, c * 128:(c + 1) * 128], rhs=sg[:, :],
                                               start=True, stop=True), r=[g2b, sg], w=[P])
                V("act", lambda e: e.activation(out=Go[:, c, :], in_=P[:, 0:W], func=AF.Copy), [P], [Go])
            kb.dma("sp", fmv(S["gR"])[:, :, t0:t0 + W], Go[:, :, :], Go, r=[Go], w=[("dram", id(S["gR"]))])
            for c in range(4):
                V("dve", lambda e: e.tensor_scalar(out=KK[:, c, :], in0=Z[:, 4 + c, :], scalar1=pv["kk"][:, c:c + 1], scalar2=None,
                                                    op0=ALU.mult), [Z, pv["kk"]], [KK])
            V("act", lambda e: e.activation(out=SQ[:, :, :], in_=KK[:, :, :], func=AF.Square), [KK], [SQ])
            for c in range(4):
                P = self.psum.next()
                kb.op("pe", lambda e: e.matmul(P[:, 0:W], lhsT=bd[:, :], rhs=SQ[:, c, :], start=True, stop=True), r=[bd, SQ], w=[P])
                V("dve", lambda e: e.tensor_scalar(out=T1[:, c, :], in0=P[:, 0:W], scalar1=1e-12, scalar2=None, op0=ALU.add), [P], [T1])
            V("act", lambda e: e.activation(out=T1[:, :, :], in_=T1[:, :, :], func=AF.Sqrt), [T1], [T1])
            V("dve", lambda e: e.reciprocal(out=T1[:, :, :], in_=T1[:, :, :]), [T1], [T1])
            V("dve", lambda e: e.tensor_tensor(out=KK[:, :, :], in0=KK[:, :, :], in1=T1[:, :, :], op=ALU.mult), [KK, T1], [KK])
            for c in range(4):
                V("dve", lambda e: e.tensor_scalar(out=T2[:, c, :], in0=A[:, c, :], scalar1=pv["ka"][:, c:c + 1],
                                                   scalar2=omka[:, c:c + 1], op0=ALU.mult, op1=ALU.add), [A, pv["ka"], omka], [T2])
            V("dve", lambda e: e.tensor_tensor(out=KM[:, :, :], in0=K_, in1=T2[:, :, :], op=ALU.mult), [Z, T2], [KM])
            V("pool", lambda e: e.tensor_tensor(out=Bv[:, :, :], in0=KK[:, :, :], in1=A[:, :, :], op=ALU.mult), [KK, A], [Bv])
            V("dve", lambda e: e.tensor_tensor(out=T1[:, :, :], in0=R_, in1=KM[:, :, :], op=ALU.mult), [Z, KM], [T1])
            for c in range(4):
                V("act", lambda e: e.activation(out=SQ[:, c, :], in_=T1[:, c, :], func=AF.Identity, scale=pv["rk"][:, c:c + 1]),
                  [T1, pv["rk"]], [SQ])
            Bo = Bop.next()
            for c in range(4):
                P = self.psum.next()
                kb.op("pe", lambda e: e.matmul(P[:, 0:W], lhsT=bd[:, :], rhs=SQ[:, c, :], start=True, stop=True), r=[bd, SQ], w=[P])
                V("dve", lambda e: e.tensor_tensor(out=Bo[:, c, :], in0=P[:, 0:W], in1=Z[:, 8 + c, :], op=ALU.mult), [P, Z], [Bo])
            kb.dma("sp", fmv(S["bon"])[:, :, t0:t0 + W], Bo[:, :, :], Bo, r=[Bo], w=[("dram", id(S["bon"]))])
            V("act", lambda e: e.activation(out=Vb[:, :, :], in_=V_, func=AF.Copy), [Z], [Vb])
            transpose_store(Vb, S["vt"], t0)
            for d_ in range(2):
                rev = d_ == 1
                fl = (lambda ap: ap.rearrange("p a b -> p (a b)")[:, ::-1]) if rev else (lambda ap: ap.rearrange("p a b -> p (a b)"))
                V("dve", lambda e: e.tensor_tensor_scan(out=fl(Lc[:, :, :]), data0=fl(ON[d_][:, :, :]), data1=fl(LW[d_][:, :, :]),
                                                        initial=0.0, op0=ALU.mult, op1=ALU.add), [ON[d_], LW[d_]], [Lc])
                mid, last = (64, 0) if rev else (63, 127)
                L4 = Lc[:, :, :].rearrange("p c (b t) -> p c b t", b=NB)
                sc = scp.next()
                V("dve", lambda e: e.tensor_copy(out=lmn[:, :, :], in_=L4[:, :, :, mid]), [Lc], [lmn])
                scv = sc[:, :, :, :].rearrange("p b c s -> p c b s")
                V("act", lambda e: e.activation(out=scv[:, :, :, 0], in_=lmn[:, :, :], func=AF.Exp), [lmn], [sc])
                V("act", lambda e: e.activation(out=scv[:, :, :, 2], in_=L4[:, :, :, last], func=AF.Exp), [Lc], [sc])
                V("dve", lambda e: e.tensor_tensor(out=scv[:, :, :, 1], in0=L4[:, :, :, last], in1=lmn[:, :, :], op=ALU.subtract),
                  [Lc, lmn], [sc])
                V("act", lambda e: e.activation(out=scv[:, :, :, 1], in_=scv[:, :, :, 1], func=AF.Exp), [sc], [sc])
                kb.dma("sp", S["sc"][d_][:, t0 // 128:t0 // 128 + NB, :, :], sc[:, :, :, :], sc, r=[sc], w=[("dram", id(S["sc"][d_]))])
                Lr4 = Lr[:, :, :].rearrange("p c (b t) -> p c b t", b=NB)
                V("pool", lambda e: e.tensor_tensor(out=Lr4, in0=L4, in1=lmn[:, :, :].unsqueeze(3).broadcast_to([128, 4, NB, 128]),
                                                    op=ALU.subtract), [Lc, lmn], [Lr])
                V("dve", lambda e: e.tensor_tensor(out=Lq[:, :, :], in0=Lr[:, :, :], in1=LW[d_][:, :, :], op=ALU.subtract),
                  [Lr, LW[d_]], [Lq])
                o = {n: outp[n].next() for n in outp}
                V("act", lambda e: e.activation(out=E[:, :, :], in_=Lr[:, :, :], func=AF.Exp), [Lr], [E])
                V("dve", lambda e: e.tensor_tensor(out=o["RHO"][:, :, :], in0=R_, in1=E[:, :, :], op=ALU.mult), [Z, E], [o["RHO"]])
                V("act", lambda e: e.activation(out=E[:, :, :], in_=Lq[:, :, :], func=AF.Exp), [Lq], [E])
                V("dve", lambda e: e.tensor_tensor(out=o["KAP"][:, :, :], in0=KK[:, :, :], in1=E[:, :, :], op=ALU.mult), [KK, E], [o["KAP"]])
                V("act", lambda e: e.activation(out=E[:, :, :], in_=Lr[:, :, :], func=AF.Exp, scale=-1.0), [Lr], [E])
                V("dve", lambda e: e.tensor_tensor(out=o["BET"][:, :, :], in0=Bv[:, :, :], in1=E[:, :, :], op=ALU.mult), [Bv, E], [o["BET"]])
                V("pool", lambda e: e.tensor_tensor(out=o["KTI"][:, :, :], in0=KM[:, :, :], in1=E[:, :, :], op=ALU.mult), [KM, E], [o["KTI"]])
                for n in ("RHO", "KAP", "BET", "KTI"):
                    kb.dma("sp", fmv(S[n][d_])[:, :, t0:t0 + W], o[n][:, :, :], o[n], r=[o[n]], w=[("dram", id(S[n][d_]))])
                transpose_store(o["BET"], S["bt"][d_], t0)
                transpose_store(o["KTI"], S["kt"][d_], t0)


def _ln_stats_w(self, x, Wc, pl):
    kb = self.kb
    xb, sq = pl["xb"].next(), pl["sq"].next()
    kb.op("act", lambda e: e.activation(out=xb[:, :, :Wc], in_=x[:, :, :Wc], func=AF.Copy), r=[x], w=[xb])
    kb.op("act", lambda e: e.activation(out=sq[:, :, :Wc], in_=x[:, :, :Wc], func=AF.Square), r=[x], w=[sq])
    P1, P2 = self.psum.next(), self.psum.next()
    for c in range(NCH):
        kb.op("pe", lambda e: e.matmul(P1[:, 0:Wc], lhsT=self.onesb[:, :], rhs=xb[:, c, :Wc],
                                       start=(c == 0), stop=(c == NCH - 1)), r=[xb, self.onesb], w=[P1])
    for c in range(NCH):
        kb.op("pe", lambda e: e.matmul(P2[:, 0:Wc], lhsT=self.onesb[:, :], rhs=sq[:, c, :Wc],
                                       start=(c == 0), stop=(c == NCH - 1)), r=[sq, self.onesb], w=[P2])
    m2, var, rstd, nmr = pl["m2"].next(), pl["var"].next(), pl["rstd"].next(), pl["nmr"].next()
    kb.op("act", lambda e: e.activation(out=m2[:, 0, :Wc], in_=P1[:, 0:Wc], func=AF.Square), r=[P1], w=[m2])
    kb.op("dve", lambda e: e.scalar_tensor_tensor(out=var[:, 0, :Wc], in0=P2[:, 0:Wc], scalar=LN_EPS,
                                                  in1=m2[:, 0, :Wc], op0=ALU.add, op1=ALU.subtract), r=[P2, m2], w=[var])
    kb.op("act", lambda e: e.activation(out=var[:, 0, :Wc], in_=var[:, 0, :Wc], func=AF.Sqrt), r=[var], w=[var])
    kb.op("dve", lambda e: e.reciprocal(out=rstd[:, 0, :Wc], in_=var[:, 0, :Wc]), r=[var], w=[rstd])
    kb.op("dve", lambda e: e.scalar_tensor_tensor(out=nmr[:, 0, :Wc], in0=P1[:, 0:Wc], scalar=-1.0,
                                                  in1=rstd[:, 0, :Wc], op0=ALU.mult, op1=ALU.mult), r=[P1, rstd], w=[nmr])
    return rstd, nmr


Model.rwkv_declare = _rwkv_declare
Model.rwkv_prep_phase = _rwkv_prep_phase
Model.ln_stats_w = _ln_stats_w


def _rwkv_scan_dir(self, l, d, S, st, psr):
    kb = self.kb
    NT = self.NT
    nchunk = NT // 128
    fmv = lambda t_: t_.rearrange("(c p) t -> p c t", p=128)
    V = lambda e_, f, r, w: kb.op(e_, f, r=r, w=w)
    sbf = lambda n, shp, dt=F32: kb.sb(n, shp, dt, st)
    MT, MN = sbf("MT", [128, 512], BF16), sbf("MN", [128, 512], BF16)
    kb.dma("pool", MT[:, :], self.rk_MT[d], MT, w=[MT])
    kb.dma("pool", MN[:, :], self.rk_MN[d], MN, w=[MN])
    KRp = Pool(kb, "KR", 2, [128, 4, 2, 128], BF16, stack=st)
    BTZp = Pool(kb, "BTZ", 2, [128, 4, 2, 128], BF16, stack=st)
    KTZp = Pool(kb, "KTZ", 2, [128, 4, 2, 128], BF16, stack=st)
    KAZp = Pool(kb, "KAZ", 2, [128, 4, 2, 128], BF16, stack=st)
    for p_ in (BTZp, KTZp, KAZp):
        for b_ in p_.bufs:
            V("pool", lambda e: e.memset(b_[:, :, :, :], 0.0), [], [b_])
    S0Z = sbf("S0Z", [128, 4, 2, 64], BF16)
    V("pool", lambda e: e.memset(S0Z[:, :, :, :], 0.0), [], [S0Z])
    St = sbf("St", [128, 4, 64])
    V("pool", lambda e: e.memset(St[:, :, :], 0.0), [], [St])
    St1 = sbf("St1", [128, 4, 64])
    tokp = {n: Pool(kb, n, 2, [128, 512], BF16, stack=st) for n in ("Btok", "Ktok", "Vtok")}
    scp = Pool(kb, "sc", 2, [128, 4, 3], F32, stack=st)
    AMp = Pool(kb, "AM", 2, [128, 8, 512], BF16, stack=st)
    Pm = [Pool(kb, "Pm%d" % i, 2, [128, 8, 128], BF16, stack=st) for i in range(2)]
    PTm = Pool(kb, "PTm", 2, [128, 8, 128], BF16, stack=st)
    X32, X16p = sbf("X32", [128, 512]), Pool(kb, "X16", 2, [128, 512], BF16, stack=st)
    Oop = Pool(kb, "Oo", 2, [128, 512], F32, stack=st)
    order = [0, 1] + list(range(2, nchunk))
    if d == 1:
        order = [1, 0] + list(range(nchunk - 1, 1, -1))
    ev = 0
    yield
    for ci_, ch in enumerate(order):
        if ci_ > 0:
            yield
        t0 = ch * 128
        KR, BTZ, KTZ, KAZ = KRp.next(), BTZp.next(), KTZp.next(), KAZp.next()
        kb.dma("sp", KR[:, :, 0, :], fmv(S["KAP"][d])[:, :, t0:t0 + 128], KR, r=[("dram", id(S["KAP"][d]))], w=[KR])
        kb.dma("sp", KR[:, :, 1, :], fmv(S["RHO"][d])[:, :, t0:t0 + 128], KR, r=[("dram", id(S["RHO"][d]))], w=[KR])
        for par in range(2):
            ps_ = slice(par * 64, (par + 1) * 64)
            kb.dma("sp", BTZ[ps_, :, par, :], fmv(S["BET"][d])[ps_, :, t0:t0 + 128], BTZ, r=[("dram", id(S["BET"][d]))], w=[BTZ])
            kb.dma("sp", KTZ[ps_, :, par, :], fmv(S["KTI"][d])[ps_, :, t0:t0 + 128], KTZ, r=[("dram", id(S["KTI"][d]))], w=[KTZ])
            kb.dma("sp", KAZ[ps_, :, par, :], fmv(S["KAP"][d])[ps_, :, t0:t0 + 128], KAZ, r=[("dram", id(S["KAP"][d]))], w=[KAZ])
        tk = {}
        for n, key in (("Btok", "bt"), ("Ktok", "kt"), ("Vtok", "vt")):
            tk[n] = tokp[n].next()
            srcd = S[key][d] if key != "vt" else S[key]
            kb.dma("sp", tk[n][:, :], srcd[t0:t0 + 128, :], tk[n], r=[("dram", id(srcd))], w=[tk[n]])
        Btok, Ktok, Vtok = tk["Btok"], tk["Ktok"], tk["Vtok"]
        sc = scp.next()
        kb.dma("sp", sc[:, :, :], S["sc"][d][:, ch, :, :], sc, r=[("dram", id(S["sc"][d]))], w=[sc])
        for par in range(2):
            ps_ = slice(par * 64, (par + 1) * 64)
            V("pool", lambda e: e.tensor_tensor(out=S0Z[ps_, :, par, :], in0=St[ps_, :, :],
                                                in1=sc[ps_, :, 0:1].broadcast_to([64, 4, 64]), op=ALU.mult), [St, sc], [S0Z])
        AM = AMp.next()
        for h in range(8):
            c, par = h // 2, h % 2
            P = psr.next()
            rhs = KR[:, c, :, :].rearrange("p s t -> p (s t)")
            kb.op("pe", lambda e: e.matmul(P[:, 0:256], lhsT=BTZ[:, c, par, :], rhs=rhs, start=True, stop=True), r=[BTZ, KR], w=[P])
            kb.op("pe", lambda e: e.matmul(P[:, 256:512], lhsT=KTZ[:, c, par, :], rhs=rhs, start=True, stop=True), r=[KTZ, KR], w=[P])
            V("dve", lambda e: e.tensor_tensor(out=AM[:, h, :], in0=P[:, :], in1=MT[:, :], op=ALU.mult), [P, MT], [AM])
            if h == 3:
                yield
        yield
        Pj, PTj = Pm[0].next(), PTm.next()
        for g4 in range(2):
            P = psr.next()
            for hh in range(4):
                h = g4 * 4 + hh
                c, par = h // 2, h % 2
                kb.op("pe", lambda e: e.matmul(P[:, hh * 128:(hh + 1) * 128], lhsT=KAZ[:, c, par, :], rhs=BTZ[:, c, par, :],
                                               start=True, stop=True), r=[KAZ, BTZ], w=[P])
            V("dve", lambda e: e.tensor_tensor(out=Pj[:, g4 * 4:(g4 + 1) * 4, :].rearrange("p h t -> p (h t)"), in0=P[:, :],
                                               in1=MN[:, :], op=ALU.mult), [P, MN], [Pj])
        V("act", lambda e: e.activation(out=PTj[:, :, :], in_=AM[:, :, 0:128], func=AF.Copy), [AM], [PTj])
        P = psr.next()
        for h in range(8):
            c, par = h // 2, h % 2
            kb.op("pe", lambda e: e.matmul(P[:, h * 64:(h + 1) * 64], lhsT=KR[:, c, 0, :], rhs=S0Z[:, c, par, :],
                                           start=True, stop=False), r=[KR, S0Z], w=[P])
            kb.op("pe", lambda e: e.matmul(P[:, h * 64:(h + 1) * 64], lhsT=AM[:, h, 256:384], rhs=Vtok[:, h * 64:(h + 1) * 64],
                                           start=False, stop=True), r=[AM, Vtok], w=[P])
        V("act", lambda e: e.activation(out=X32[:, :], in_=P[:, :], func=AF.Identity, scale=-1.0), [P], [X32])
        X16 = X16p.next()
        V("act", lambda e: e.activation(out=X16[:, :], in_=X32[:, :], func=AF.Copy), [X32], [X16])
        yield
        for j in range(7):
            P = psr.next()
            for h in range(8):
                kb.op("pe", lambda e: e.matmul(P[:, h * 64:(h + 1) * 64], lhsT=PTj[:, h, :], rhs=X16[:, h * 64:(h + 1) * 64],
                                               start=True, stop=True), r=[PTj, X16], w=[P])
            V("dve", lambda e: e.tensor_tensor(out=X32[:, :], in0=P[:, :], in1=X32[:, :], op=ALU.add), [P, X32], [X32])
            X16 = X16p.next()
            V("act", lambda e: e.activation(out=X16[:, :], in_=X32[:, :], func=AF.Copy), [X32], [X16])
            if j < 6:
                PTn = PTm.next()
                Pn = Pm[(j + 1) % 2].next() if j < 5 else None
                for g4 in range(2):
                    P = psr.next()
                    for hh in range(4):
                        h = g4 * 4 + hh
                        kb.op("pe", lambda e: e.matmul(P[:, hh * 128:(hh + 1) * 128], lhsT=Pj[:, h, :], rhs=PTj[:, h, :],
                                                       start=True, stop=True), r=[Pj, PTj], w=[P])
                    eng = "act"
                    ev += 1
                    dst = PTn[:, g4 * 4:(g4 + 1) * 4, :].rearrange("p h t -> p (h t)")
                    if eng == "act":
                        V("act", lambda e: e.activation(out=dst, in_=P[:, :], func=AF.Copy), [P], [PTn])
                    else:
                        V("dve", lambda e: e.tensor_copy(out=dst, in_=P[:, :]), [P], [PTn])
                    if Pn is not None:
                        P = psr.next()
                        for hh in range(4):
                            h = g4 * 4 + hh
                            kb.op("pe", lambda e: e.matmul(P[:, hh * 128:(hh + 1) * 128], lhsT=PTj[:, h, :], rhs=Pj[:, h, :],
                                                           start=True, stop=True), r=[Pj, PTj], w=[P])
                        eng = "act"
                        ev += 1
                        dst = Pn[:, g4 * 4:(g4 + 1) * 4, :].rearrange("p h t -> p (h t)")
                        if eng == "act":
                            V("act", lambda e: e.activation(out=dst, in_=P[:, :], func=AF.Copy), [P], [Pn])
                        else:
                            V("dve", lambda e: e.tensor_copy(out=dst, in_=P[:, :]), [P], [Pn])
                PTj = PTn
                if Pn is not None:
                    Pj = Pn
            if j < 6:
                yield
        U16 = X16
        P = psr.next()
        for h in range(8):
            c, par = h // 2, h % 2
            hs = slice(h * 64, (h + 1) * 64)
            kb.op("pe", lambda e: e.matmul(P[:, hs], lhsT=KR[:, c, 1, :], rhs=S0Z[:, c, par, :], start=True, stop=False),
                  r=[KR, S0Z], w=[P])
            kb.op("pe", lambda e: e.matmul(P[:, hs], lhsT=AM[:, h, 128:256], rhs=U16[:, hs], start=False, stop=False),
                  r=[AM, U16], w=[P])
            kb.op("pe", lambda e: e.matmul(P[:, hs], lhsT=AM[:, h, 384:512], rhs=Vtok[:, hs], start=False, stop=True),
                  r=[AM, Vtok], w=[P])
        Oo = Oop.next()
        V("act", lambda e: e.activation(out=Oo[:, :], in_=P[:, :], func=AF.Copy), [P], [Oo])
        kb.dma("act", S["O"][d][t0:t0 + 128, :], Oo[:, :], Oo, r=[Oo], w=[("dram", id(S["O"][d]))])
        yield
        P = psr.next()
        for c in range(4):
            cs_ = slice(c * 128, (c + 1) * 128)
            kb.op("pe", lambda e: e.matmul(P[:, cs_], lhsT=Btok[:, cs_], rhs=U16[:, cs_], start=True, stop=False),
                  r=[Btok, U16], w=[P])
            kb.op("pe", lambda e: e.matmul(P[:, cs_], lhsT=Ktok[:, cs_], rhs=Vtok[:, cs_], start=False, stop=True),
                  r=[Ktok, Vtok], w=[P])
        V("pool", lambda e: e.tensor_tensor(out=St1[:, :, :], in0=St[:, :, :], in1=sc[:, :, 2:3].broadcast_to([128, 4, 64]),
                                            op=ALU.mult), [St, sc], [St1])
        pv_ = P[:, :].rearrange("p (c x) -> p c x", c=4)
        for par in range(2):
            ps_ = slice(par * 64, (par + 1) * 64)
            V("dve", lambda e: e.tensor_tensor(out=St[ps_, :, :], in0=pv_[ps_, :, par * 64:(par + 1) * 64],
                                               in1=sc[ps_, :, 1:2].broadcast_to([64, 4, 64]), op=ALU.mult), [P, sc, St1], [St])
        V("pool", lambda e: e.tensor_tensor(out=St[:, :, :], in0=St[:, :, :], in1=St1[:, :, :], op=ALU.add), [St, St1], [St])


def _merge_declare(self):
    L = DEPTH
    self.branch_proj = self.din("branch_proj", [L, 3, 512, D])
    self.w_out = self.din("w_out", [L, D, D])


def _merge_phase(self, l, src, dst, S, ctx):
    kb = self.kb
    W = 256
    NB = W // 128
    lni = (l * 3 + 1) * NCH
    with kb.phase() as st:
        self.psum = Pool(kb, "ps", 8, [128, 512], F32, psum=True, stack=st)
        V = lambda e_, f, r, w: kb.op(e_, f, r=r, w=w)
        sbf = lambda n, shp, dt=F32: kb.sb(n, shp, dt, st)
        bp = sbf("bp", [128, 12, D], BF16)
        wo = sbf("wo", [128, NCH, D], BF16)
        for b_ in range(3):
            for c in range(4):
                kb.dma("pool", bp[:, b_ * 4 + c, :], self.branch_proj[l, b_, c * 128:(c + 1) * 128, :], bp, w=[bp])
        for c in range(NCH):
            kb.dma("pool", wo[:, c, :], self.w_out[l, c * 128:(c + 1) * 128, :], wo, w=[wo])
        idb = sbf("idb", [128, 128], BF16)
        kb.dma("pool", idb[:, :], self.identb[:, :], idb, w=[idb])
        gng, gnb = sbf("gng", [128, 4]), sbf("gnb", [128, 4])
        kb.dma("sp", gng[:, :], self.rk_gng[:, l * 4:(l + 1) * 4], gng, w=[gng])
        kb.dma("sp", gnb[:, :], self.rk_gnb[:, l * 4:(l + 1) * 4], gnb, w=[gnb])
        pl = self.stat_pools(W, st)
        xp = Pool(kb, "x", 2, [128, NCH, W], F32, stack=st)
        xn = sbf("xn", [128, NCH, W])
        Ofp = Pool(kb, "Of", 2, [128, NB, 512], F32, stack=st)
        Obp = Pool(kb, "Ob", 2, [128, NB, 512], F32, stack=st)
        onb = sbf("onb", [128, NB, 512], BF16)
        st8 = [sbf("st8_%d" % i, [128, NB, 8]) for i in range(3)]
        sqt = sbf("sqt", [128, NB, 512])
        Y = {n: Pool(kb, "y" + n, 2, [128, 4, W], BF16, stack=st) for n in ("a", "s", "bon", "g")}
        yr = sbf("yr", [128, 4, W], BF16)
        yt = sbf("yrt", [128, 4, W])
        gp = Pool(kb, "gates", 2, [128, 24, W], BF16, stack=st)
        m1, m2, m3 = sbf("m1", [128, W]), sbf("m2", [128, W]), sbf("m3", [128, W])
        mT = sbf("mT", [128, NCH, W], BF16)
        srcv = src.rearrange("(c p) t -> p c t", p=128)
        dstv = dst.rearrange("(c p) t -> p c t", p=128)
        fmv = lambda t_: t_.rearrange("(c p) t -> p c t", p=128)
        for (seg, t0, _) in self.tiles(W, ctx=ctx):
            x = xp.next()
            kb.dma("sp", x[:, :, :], srcv[:, :, t0:t0 + W], x, r=[("dram", id(src))], w=[x])
            Of, Ob = Ofp.next(), Obp.next()
            kb.dma("sp", Of[:, :, :], S["O"][0][t0:t0 + W, :].rearrange("(b p) n -> p b n", p=128), Of, r=[("dram", id(S["O"][0]))], w=[Of])
            kb.dma("sp", Ob[:, :, :], S["O"][1][t0:t0 + W, :].rearrange("(b p) n -> p b n", p=128), Ob, r=[("dram", id(S["O"][1]))], w=[Ob])
            ld = {}
            for n, key in (("a", "ya"), ("s", "ys"), ("bon", "bon"), ("g", "gR")):
                ld[n] = Y[n].next()
                kb.dma("sp", ld[n][:, :, :], fmv(S[key])[:, :, t0:t0 + W], ld[n], r=[("dram", id(S[key]))], w=[ld[n]])
            gt = gp.next()
            kb.dma("sp", gt[:, :, :], fmv(S["gS"])[:, :, t0:t0 + W], gt, r=[("dram", id(S["gS"]))], w=[gt])
            V("dve", lambda e: e.tensor_tensor(out=Of[:, :, :], in0=Of[:, :, :], in1=Ob[:, :, :], op=ALU.add), [Of, Ob], [Of])
            O4 = Of[:, :, :].rearrange("p b (h v) -> p b h v", h=8)
            sm, vr, rs = st8
            V("dve", lambda e: e.tensor_reduce(out=sm[:, :, :], in_=O4, axis=AX.X, op=ALU.add), [Of], [sm])
            V("dve", lambda e: e.tensor_scalar(out=sm[:, :, :], in0=sm[:, :, :], scalar1=1.0 / 64, scalar2=None, op0=ALU.mult), [sm], [sm])
            V("dve", lambda e: e.tensor_tensor(out=O4, in0=O4, in1=sm[:, :, :].unsqueeze(3).broadcast_to([128, NB, 8, 64]),
                                               op=ALU.subtract), [Of, sm], [Of])
            V("act", lambda e: e.activation(out=sqt[:, :, :], in_=Of[:, :, :], func=AF.Square), [Of], [sqt])
            V("dve", lambda e: e.tensor_reduce(out=vr[:, :, :], in_=sqt[:, :, :].rearrange("p b (h v) -> p b h v", h=8), axis=AX.X,
                                               op=ALU.add), [sqt], [vr])
            V("dve", lambda e: e.tensor_scalar(out=vr[:, :, :], in0=vr[:, :, :], scalar1=1.0 / 64, scalar2=GN_EPS, op0=ALU.mult,
                                               op1=ALU.add), [vr], [vr])
            V("act", lambda e: e.activation(out=vr[:, :, :], in_=vr[:, :, :], func=AF.Sqrt), [vr], [vr])
            V("dve", lambda e: e.reciprocal(out=rs[:, :, :], in_=vr[:, :, :]), [vr], [rs])
            V("dve", lambda e: e.tensor_tensor(out=onb[:, :, :].rearrange("p b (h v) -> p b h v", h=8), in0=O4,
                                               in1=rs[:, :, :].unsqueeze(3).broadcast_to([128, NB, 8, 64]), op=ALU.mult), [Of, rs], [onb])
            for tb in range(NB):
                P = self.psum.next()
                pb = P[:, 0:256].bitcast(BF16)
                for c in range(4):
                    kb.op("pe", lambda e: e.transpose(pb[:, c * 128:(c + 1) * 128], onb[:, tb, c * 128:(c + 1) * 128], idb[:, :]),
                          r=[onb, idb], w=[P])
                for c in range(4):
                    V("act", lambda e: e.activation(out=yt[:, c, tb * 128:(tb + 1) * 128], in_=pb[:, c * 128:(c + 1) * 128],
                                                    func=AF.Identity, scale=gng[:, c:c + 1], bias=gnb[:, c:c + 1]), [P, gng, gnb], [yt])
            V("pool", lambda e: e.tensor_tensor(out=yt[:, :, :], in0=yt[:, :, :], in1=ld["bon"][:, :, :], op=ALU.add), [yt, ld["bon"]], [yt])
            V("dve", lambda e: e.tensor_tensor(out=yr[:, :, :], in0=yt[:, :, :], in1=ld["g"][:, :, :], op=ALU.mult), [yt, ld["g"]], [yr])
            if "yrS" in S:
                kb.dma("sp", fmv(S["yrS"])[:, :, t0:t0 + W], yr[:, :, :], yr, r=[yr], w=[("dram", id(S["yrS"]))])
            ysrc = [ld["a"], yr, ld["s"]]
            for oc in range(NCH):
                Pa, Pb = self.psum.next(), self.psum.next()
                tgt = [(Pa, 0), (Pa, W), (Pb, 0)]
                for b_ in range(3):
                    Pt, o0 = tgt[b_]
                    for kc in range(4):
                        kb.op("pe", lambda e: e.matmul(Pt[:, o0:o0 + W], lhsT=bp[:, b_ * 4 + kc, oc * 128:(oc + 1) * 128],
                                                       rhs=ysrc[b_][:, kc, :], start=(kc == 0), stop=(kc == 3)), r=[bp, ysrc[b_]], w=[Pt])
                V("dve", lambda e: e.tensor_tensor(out=m1[:, :], in0=Pa[:, 0:W], in1=gt[:, oc, :], op=ALU.mult), [Pa, gt], [m1])
                V("dve", lambda e: e.tensor_tensor(out=m2[:, :], in0=Pa[:, W:2 * W], in1=gt[:, 8 + oc, :], op=ALU.mult), [Pa, gt], [m2])
                V("dve", lambda e: e.tensor_tensor(out=m3[:, :], in0=Pb[:, 0:W], in1=gt[:, 16 + oc, :], op=ALU.mult), [Pb, gt], [m3])
                V("dve", lambda e: e.tensor_tensor(out=m1[:, :], in0=m1[:, :], in1=m2[:, :], op=ALU.add), [m1, m2], [m1])
                V("pool", lambda e: e.tensor_tensor(out=mT[:, oc, :], in0=m1[:, :], in1=m3[:, :], op=ALU.add), [m1, m3], [mT])
            V("pool", lambda e: e.tensor_scalar(out=x[:, :, :], in0=x[:, :, :], scalar1=ALPHA, scalar2=None, op0=ALU.mult), [x], [x])
            for oc in range(NCH):
                if oc % 2 == 0:
                    P = self.psum.next()
                o0 = (oc % 2) * W
                for kc in range(NCH):
                    kb.op("pe", lambda e: e.matmul(P[:, o0:o0 + W], lhsT=wo[:, kc, oc * 128:(oc + 1) * 128], rhs=mT[:, kc, :],
                                                   start=(kc == 0), stop=(kc == NCH - 1)), r=[wo, mT], w=[P])
                V("dve", lambda e: e.scalar_tensor_tensor(out=x[:, oc, :], in0=P[:, o0:o0 + W],
                                                          scalar=self.mods[:, 5 * NCH + oc, seg:seg + 1], in1=x[:, oc, :],
                                                          op0=ALU.mult, op1=ALU.add), [P, x, self.mods], [x])
            P = self.psum.next()
            rstd, nmr = self.ln_stats(x, W, P, pl)
            self.normalize(xn, x, W, rstd, nmr)
            for c in range(NCH):
                V("act", lambda e: e.activation(out=xn[:, c, :], in_=xn[:, c, :], func=AF.Identity,
                                                scale=self.lng[:, lni + c:lni + c + 1], bias=self.lnb[:, lni + c:lni + c + 1]),
                  [xn, self.lng, self.lnb], [xn])
            kb.dma("sp", dstv[:, :, t0:t0 + W], xn[:, :, :], xn, r=[xn], w=[("dram", id(dst))])


Model.rwkv_scan_dir = _rwkv_scan_dir
Model.merge_declare = _merge_declare
Model.merge_phase = _merge_phase


def host_inputs_rwkv(inp):
    L = DEPTH
    d = {}
    w = inp["w_in"]
    rw = w[:, :, 768:2624]
    r, k, v = rw[:, :, 0:512], rw[:, :, 512:1024], rw[:, :, 1024:1536]
    wlo, alo, glo = rw[:, :, 1536:1664], rw[:, :, 1664:1728], rw[:, :, 1728:1856]
    pad = np.zeros_like(alo)
    d["w_inB"] = np.ascontiguousarray(np.concatenate([r, k, v, wlo, glo, alo, pad], -1))
    mu = inp["rwkv_mu"]
    mu_r = np.concatenate([mu[:, 0:1536], mu[:, 1536:1664], mu[:, 1728:1856], mu[:, 1664:1728], np.zeros((L, 64), np.float32)], -1)
    d["rk_mu"] = np.ascontiguousarray(mu_r.reshape(L * NCH_B, 128).T)
    d["rk_w0"] = np.ascontiguousarray(inp["rwkv_w0"].reshape(L * 8, 128).T)
    for n, src in (("a0", "rwkv_a0"), ("kk", "rwkv_k_k"), ("ka", "rwkv_k_a"), ("rk", "rwkv_r_k"), ("gng", "rwkv_gn_g"), ("gnb", "rwkv_gn_b")):
        d["rk_" + n] = np.ascontiguousarray(inp[src].reshape(L * 4, 128).T)
    d["rk_w2"] = np.ascontiguousarray(inp["rwkv_w2"].reshape(L, 128, 512))
    d["rk_a2"] = inp["rwkv_a2"]
    d["rk_g2"] = inp["rwkv_g2"]
    i = np.arange(128)
    d["bd64"] = np.ascontiguousarray(((i[:, None] // 64) == (i[None, :] // 64)).astype(np.float32))
    d["identb"] = np.eye(128, dtype=np.float32)
    on = np.ones((2, 128, 128), np.float32)
    on[0, :, 0] = 0.0
    on[1, :, 127] = 0.0
    d["ones0"] = on
    MT = np.zeros((2, 128, 512), np.float32)
    MN = np.zeros((2, 128, 512), np.float32)
    ii, tt = i[:, None], i[None, :]
    for dd in range(2):
        prev = (ii < tt) if dd == 0 else (ii > tt)
        incl = prev | (ii == tt)
        MT[dd, :, 0:128] = -(prev.astype(np.float32))
        MT[dd, :, 128:256] = incl
        MT[dd, :, 256:384] = prev
        MT[dd, :, 384:512] = incl
        MN[dd] = np.tile(-(prev.T.astype(np.float32)), (1, 4))
    d["rk_MT"], d["rk_MN"] = MT, MN
    d["branch_proj"] = inp["branch_proj"]
    d["w_out"] = inp["w_out"]
    return d


def _make_scratch(self):
    NT = self.NT
    S = {}
    for n, shp, dt in (("qS", [512, NT], BF16), ("kS", [256, NT], BF16), ("vS", [NT, 128], BF16), ("usS", [512, NT], F32),
                       ("gS", [3072, NT], BF16), ("ya", [512, NT], BF16), ("ysb", [512, NT], F32), ("ys", [512, NT], BF16),
                       ("gR", [512, NT], BF16), ("bon", [512, NT], BF16), ("vt", [NT, 512], BF16)):
        S[n] = self.scratch(n, shp, dt)
    for n in ("RHO", "KAP", "BET", "KTI"):
        S[n] = [self.scratch("%s%d" % (n, d), [512, NT], BF16) for d in range(2)]
    for n in ("bt", "kt"):
        S[n] = [self.scratch("%s%d" % (n, d), [NT, 512], BF16) for d in range(2)]
    S["sc"] = [self.scratch("sc%d" % d, [128, NT // 128, 4, 3], F32) for d in range(2)]
    S["O"] = [self.scratch("O%d" % d, [NT, 512], F32) for d in range(2)]
    if "yrS" in self.dbg:
        S["yrS"] = self.scratch("yrS", [512, NT], BF16)
    return S


def _mixer(self, l, src, dst, S, ctx_out):
    self.mixA_phase(l, src, S["qS"], S["kS"], S["vS"], S["usS"], S["gS"])
    self.attn_phase(l, S["qS"], S["kS"], S["vS"], S["ya"])
    self.rwkv_prep_phase(l, src, S)
    self.scan_phase(l, S)
    self.merge_phase(l, src, dst, S, ctx=ctx_out)


def _scan_phase(self, l, S):
    kb = self.kb
    for (ds5, drk) in ((1, 0), (0, 1)):
        with kb.phase() as st:
            PS2 = Pool(kb, "ps2", 2, [128, 1024], F32, psum=True, stack=st)
            psr = Pool(kb, "psr", 4, [128, 512], F32, psum=True, stack=st)
            g1 = self.s5_dir(l, ds5, S["usS"], S["ysb"], S["ys"], st, PS2)
            next(g1)
            g2 = self.rwkv_scan_dir(l, drk, S, st, psr)
            next(g2)
            alive = [g1, g2]
            while alive:
                for g in list(alive):
                    try:
                        next(g)
                    except StopIteration:
                        alive.remove(g)


Model.scan_phase = _scan_phase
Model.make_scratch = _make_scratch
Model.mixer = _mixer


def build_model(T, dbg=()):
    m = Model(T, dbg=dbg)
    m.declare_inputs()
    m.mixA_declare()
    m.s5_declare()
    m.rwkv_declare()
    m.merge_declare()
    m.setup_consts()
    NT = m.NT
    S = m.make_scratch()
    streams = [m.scratch("str%d" % i, [D, NT], F32) for i in range(3)]
    outT = m.dout("outT", [D, T])
    cur = m.xT
    for l in range(DEPTH):
        last = l == DEPTH - 1
        m.adaln_phase(l)
        m.ffn_phase(l, 0, cur, streams[0], 0, ctx=True)
        m.mixer(l, streams[0], streams[1], S, ctx_out=not last)
        if last:
            m.ffn_phase(l, 1, streams[1], outT, CTX, ctx=False)
        else:
            m.ffn_phase(l, 1, streams[1], streams[2], 0, ctx=True)
            cur = streams[2]
    m.kb.finish()
    return m


def all_host_inputs(inp, b, T):
    d = host_inputs(inp, b, T)
    d.update(host_inputs_A(inp, T))
    d.update(host_inputs_s5(inp))
    d.update(host_inputs_rwkv(inp))
    return d


T_FULL = 8192
N_CORES = 8


def kernel(**inputs):
    inp = {k: np.asarray(v) for k, v in inputs.items()}
    m = build_model(T_FULL)
    B = inp["x"].shape[0]
    shared = None
    in_maps = []
    for core in range(N_CORES):
        b = core % B
        d = all_host_inputs(inp, b, T_FULL) if shared is None else dict(shared)
        if shared is None:
            shared = d
        else:
            d["xT"] = np.ascontiguousarray(np.concatenate([inp["ctx"][b], inp["x"][b, :T_FULL]], 0).T)
            cond = np.stack([inp["c"][b], inp["c_ctx"]], -1)
            d["condT"] = np.ascontiguousarray(cond.reshape(NCH, 128, 2).transpose(1, 0, 2))
        in_maps.append({k: v for k, v in d.items() if k in m.dram_in})
    res = run_bass_kernel_spmd(m.nc, in_maps, core_ids=list(range(N_CORES)))
    out = np.stack([np.ascontiguousarray(res.results[b]["outT"].T) for b in range(B)], 0)
    return out.astype(np.float32)
```

```python
import contextlib
import numpy as np
import concourse.bass as bass
import concourse.mybir as mybir
from concourse.bass_utils import run_bass_kernel_spmd

F32 = mybir.dt.float32
BF16 = mybir.dt.bfloat16
AF = mybir.ActivationFunctionType
ALU = mybir.AluOpType
AX = mybir.AxisListType

D = 1024
NCH = 8
CTX = 256
DFF = 2816
NFF = 22
DEPTH = 2
ALPHA = (2.0 * DEPTH) ** 0.25
LN_EPS = 1e-6
DECAY_SCALE = 0.606531
GN_EPS = 64e-5
N_IN = 6208


class Sem:
    _n = 0

    def __init__(self, h):
        self.h = h
        Sem._n += 1
        self.uid = Sem._n


class Buf:
    def __init__(self, name, t):
        self.name = name
        self.t = t
        self.dsem = None
        self.dcnt = 0

    def __getitem__(self, k):
        return self.t[k]

    def __repr__(self):
        return "Buf(%s)" % self.name


class KB:
    EPOCH = 20000

    def __init__(self, nc):
        self.nc = nc
        self.es = contextlib.ExitStack()
        self.eng = {"pe": nc.tensor, "dve": nc.vector, "act": nc.scalar, "pool": nc.gpsimd, "sp": nc.sync}
        self.esem = {}
        self.ecnt = {}
        for e in self.eng:
            self.esem[e] = Sem(self.es.enter_context(nc.semaphore("c_%s_0" % e)))
            self.ecnt[e] = 0
        self.eepoch = {e: 0 for e in self.eng}
        self.seen = {e: {} for e in self.eng}
        self.lastw = {}
        self.reads = {}
        self.nbuf = 0
        self.ninstr = 0
        self.nwait = 0
        self.all_events = {}
        self.free_dsems = []
        self.phase_bufs = []
        self.ndsem = 0

    def sb(self, name, shape, dtype, stack=None):
        self.nbuf += 1
        t = (stack or self.es).enter_context(self.nc.sbuf_tensor("%s_%d" % (name, self.nbuf), list(shape), dtype))
        b = Buf(name, t)
        if stack is not None:
            self.phase_bufs.append(b)
        return b

    def ps(self, name, shape, dtype=F32, stack=None):
        self.nbuf += 1
        t = (stack or self.es).enter_context(self.nc.psum_tensor("%s_%d" % (name, self.nbuf), list(shape), dtype))
        return Buf(name, t)

    def _dsem(self, b):
        if b.dsem is None:
            if self.free_dsems:
                b.dsem, b.dcnt = self.free_dsems.pop()
            else:
                self.ndsem += 1
                b.dsem = Sem(self.es.enter_context(self.nc.semaphore("d_%d" % self.ndsem)))
                b.dcnt = 0
        return b.dsem

    @contextlib.contextmanager
    def phase(self):
        st = contextlib.ExitStack()
        self.phase_bufs = []
        try:
            yield st
        finally:
            self.barrier()
            for b in self.phase_bufs:
                if b.dsem is not None:
                    self.free_dsems.append((b.dsem, b.dcnt))
                    b.dsem = None
            self.phase_bufs = []
            st.close()

    def _need(self, e, r, w):
        need = {}

        def add(evs):
            for uid, (s, v) in evs.items():
                if uid not in need or need[uid][1] < v:
                    need[uid] = (s, v)
        for k in r:
            add(self.lastw.get(k, {}))
        for k in w:
            add(self.lastw.get(k, {}))
            add(self.reads.get(k, {}))
        return need

    def _wait(self, e, need, own_ok):
        eng = self.eng[e]
        for uid, (s, v) in need.items():
            if own_ok and uid == self.esem[e].uid:
                continue
            if self.seen[e].get(uid, 0) >= v:
                continue
            eng.wait_ge(s.h, v)
            self.nwait += 1
            self.seen[e][uid] = v

    def _record(self, ev, r, w):
        uid = ev[0].uid
        for k in r:
            self.reads.setdefault(k, {})[uid] = ev
        for k in w:
            self.lastw.setdefault(k, {})[uid] = ev
            self.reads[k] = {}
        self.all_events[uid] = ev

    def _bump(self, e):
        if self.ecnt[e] >= self.EPOCH:
            self.eepoch[e] += 1
            self.esem[e] = Sem(self.es.enter_context(self.nc.semaphore("c_%s_%d" % (e, self.eepoch[e]))))
            self.ecnt[e] = 0
        self.ecnt[e] += 1
        return (self.esem[e], self.ecnt[e])

    def op(self, e, fn, r=(), w=(), same_ok=False):
        need = self._need(e, r, w)
        self._wait(e, need, own_ok=(e == "pe" or same_ok))
        ins = fn(self.eng[e])
        ev = self._bump(e)
        ins.then_inc(ev[0].h, 1)
        self._record(ev, r, w)
        self.ninstr += 1
        return ins

    def dma(self, q, out, in_, sbuf, r=(), w=(), **kw):
        need = self._need(q, r, w)
        self._wait(q, need, own_ok=False)
        s = self._dsem(sbuf)
        ins = self.eng[q].dma_start(out=out, in_=in_, **kw)
        sbuf.dcnt += 16
        ev = (s, sbuf.dcnt)
        ins.then_inc(s.h, 16)
        self._record(ev, r, w)
        self.ninstr += 1
        return ins

    def barrier(self):
        for e in self.eng:
            self._wait(e, dict(self.all_events), own_ok=True)
        self.lastw = {}
        self.reads = {}
        self.all_events = {}

    def finish(self, e="sp"):
        self._wait(e, dict(self.all_events), own_ok=True)


class Pool:
    def __init__(self, kb, name, n, shape, dtype, psum=False, stack=None):
        self.bufs = [(kb.ps if psum else kb.sb)("%s%d" % (name, i), shape, dtype, stack=stack) for i in range(n)]
        self.i = 0

    def next(self):
        b = self.bufs[self.i % len(self.bufs)]
        self.i += 1
        return b


class Model:
    def __init__(self, T, dbg=(), nlayers=DEPTH, stop_after=None):
        self.T = T
        self.NT = CTX + T
        self.dbg = set(dbg)
        self.nlayers = nlayers
        self.stop_after = stop_after
        nc = bass.Bass("TRN2", target_bir_lowering=False)
        self.nc = nc
        self.kb = KB(nc)
        self.dram_in = {}
        self.dram_out = {}
        self.scr_n = 0

    def din(self, name, shape, dtype=F32):
        t = self.nc.dram_tensor(name, list(shape), dtype, kind="ExternalInput").ap()
        self.dram_in[name] = t
        return t

    def dout(self, name, shape, dtype=F32):
        t = self.nc.dram_tensor(name, list(shape), dtype, kind="ExternalOutput").ap()
        self.dram_out[name] = t
        return t

    def scratch(self, name, shape, dtype=F32):
        if name in self.dbg:
            return self.dout(name, shape, dtype)
        return self.nc.dram_tensor(name, list(shape), dtype, kind="Internal").ap()

    def tiles(self, W, ctx=True, lat=True):
        out = []
        if ctx:
            for t0 in range(0, CTX, W):
                out.append((1, t0, W))
        if lat:
            for t0 in range(0, self.T, W):
                out.append((0, CTX + t0, W))
        return out

    def declare_inputs(self):
        L = DEPTH
        self.xT = self.din("xT", [D, self.NT])
        self.condT = self.din("condT", [128, NCH, 2])
        self.w_ada = self.din("w_ada", [L, D, 9 * D])
        self.b_ada = self.din("b_ada", [L, 128, 72])
        self.ln_g = self.din("ln_g", [128, L * 3 * NCH])
        self.ln_b = self.din("ln_b", [128, L * 3 * NCH])
        self.ffn_w_in = self.din("ffn_w_in", [L, 2, D, 2 * DFF])
        self.ffn_w_out = self.din("ffn_w_out", [L, 2, DFF, D])

    def setup_consts(self):
        kb = self.kb
        self.onesb = kb.sb("onesb", [128, 128], BF16)
        kb.op("dve", lambda e: e.memset(self.onesb[:], 1.0 / D), w=[self.onesb])
        self.lng = kb.sb("lng", [128, DEPTH * 3 * NCH], F32)
        self.lnb = kb.sb("lnb", [128, DEPTH * 3 * NCH], F32)
        kb.dma("sp", self.lng[:], self.ln_g[:, :], self.lng, w=[self.lng])
        kb.dma("sp", self.lnb[:], self.ln_b[:, :], self.lnb, w=[self.lnb])
        self.scond = kb.sb("scond", [128, NCH, 2], F32)
        kb.dma("sp", self.scond[:], self.condT[:, :, :], self.scond, w=[self.scond])
        kb.op("act", lambda e: e.activation(out=self.scond[:], in_=self.scond[:], func=AF.Silu),
              r=[self.scond], w=[self.scond])
        self.mods = kb.sb("mods", [128, 72, 2], F32)
        self.modp1 = kb.sb("modp1", [128, 72, 2], F32)
        self.modh = kb.sb("modh", [128, 72, 2], F32)

    def adaln_phase(self, l):
        kb = self.kb
        with kb.phase() as st:
            self.psum = Pool(kb, "ps", 8, [128, 512], F32, psum=True, stack=st)
            wa = Pool(kb, "wa", 2, [128, NCH, D], F32, stack=st)
            bada = kb.sb("bada", [128, 72], F32, st)
            kb.dma("sp", bada[:], self.b_ada[l, :, :], bada, w=[bada])
            P = self.psum.next()
            for m in range(9):
                w = wa.next()
                for kc in range(NCH):
                    kb.dma("sp" if kc % 2 == 0 else "act", w[:, kc, :],
                           self.w_ada[l, kc * 128:(kc + 1) * 128, m * D:(m + 1) * D], w, w=[w])
                for oc in range(NCH):
                    j = m * NCH + oc
                    for kc in range(NCH):
                        kb.op("pe", lambda e: e.matmul(P[:, 2 * j:2 * j + 2], lhsT=w[:, kc, oc * 128:(oc + 1) * 128],
                                                       rhs=self.scond[:, kc, :], start=(kc == 0), stop=(kc == NCH - 1)),
                              r=[w, self.scond], w=[P])
            kb.op("dve", lambda e: e.tensor_tensor(out=self.mods[:], in0=P[:, 0:144].rearrange("p (j s) -> p j s", s=2),
                                                   in1=bada[:, :].unsqueeze(2).broadcast_to([128, 72, 2]), op=ALU.add),
                  r=[P, bada], w=[self.mods])
            kb.op("dve", lambda e: e.tensor_scalar(out=self.modp1[:], in0=self.mods[:], scalar1=1.0, scalar2=None,
                                                   op0=ALU.add), r=[self.mods], w=[self.modp1])
            kb.op("dve", lambda e: e.tensor_scalar(out=self.modh[:], in0=self.mods[:], scalar1=0.5, scalar2=None,
                                                   op0=ALU.mult), r=[self.mods], w=[self.modh])

    def ln_stats(self, x, W, P, pl):
        kb = self.kb
        xb, sq = pl["xb"].next(), pl["sq"].next()
        kb.op("act", lambda e: e.activation(out=xb[:, :, :W], in_=x[:, :, :W], func=AF.Copy), r=[x], w=[xb])
        kb.op("act", lambda e: e.activation(out=sq[:, :, :W], in_=x[:, :, :W], func=AF.Square), r=[x], w=[sq])
        for c in range(NCH):
            kb.op("pe", lambda e: e.matmul(P[:, 0:W], lhsT=self.onesb[:, :], rhs=xb[:, c, :W],
                                           start=(c == 0), stop=(c == NCH - 1)), r=[xb, self.onesb], w=[P])
        for c in range(NCH):
            kb.op("pe", lambda e: e.matmul(P[:, W:2 * W], lhsT=self.onesb[:, :], rhs=sq[:, c, :W],
                                           start=(c == 0), stop=(c == NCH - 1)), r=[sq, self.onesb], w=[P])
        m2, var, rstd, nmr = pl["m2"].next(), pl["var"].next(), pl["rstd"].next(), pl["nmr"].next()
        kb.op("act", lambda e: e.activation(out=m2[:, 0, :W], in_=P[:, 0:W], func=AF.Square), r=[P], w=[m2])
        kb.op("dve", lambda e: e.scalar_tensor_tensor(out=var[:, 0, :W], in0=P[:, W:2 * W], scalar=LN_EPS,
                                                      in1=m2[:, 0, :W], op0=ALU.add, op1=ALU.subtract),
              r=[P, m2], w=[var])
        kb.op("act", lambda e: e.activation(out=var[:, 0, :W], in_=var[:, 0, :W], func=AF.Sqrt), r=[var], w=[var])
        kb.op("dve", lambda e: e.reciprocal(out=rstd[:, 0, :W], in_=var[:, 0, :W]), r=[var], w=[rstd])
        kb.op("dve", lambda e: e.scalar_tensor_tensor(out=nmr[:, 0, :W], in0=P[:, 0:W], scalar=-1.0,
                                                      in1=rstd[:, 0, :W], op0=ALU.mult, op1=ALU.mult),
              r=[P, rstd], w=[nmr])
        return rstd, nmr

    def normalize(self, out, x, W, rstd, nmr):
        kb = self.kb
        kb.op("dve", lambda e: e.tensor_tensor(out=out[:, :, :W], in0=x[:, :, :W],
                                               in1=rstd[:, 0:1, :W].broadcast_to([128, NCH, W]), op=ALU.mult),
              r=[x, rstd], w=[out])
        kb.op("dve", lambda e: e.tensor_tensor(out=out[:, :, :W], in0=out[:, :, :W],
                                               in1=nmr[:, 0:1, :W].broadcast_to([128, NCH, W]), op=ALU.add),
              r=[out, nmr], w=[out])

    def stat_pools(self, W, st):
        kb = self.kb
        return {
            "xb": Pool(kb, "xb", 1, [128, NCH, W], BF16, stack=st),
            "sq": Pool(kb, "sq", 1, [128, NCH, W], BF16, stack=st),
            "m2": Pool(kb, "m2", 2, [128, 1, W], F32, stack=st),
            "var": Pool(kb, "var", 2, [128, 1, W], F32, stack=st),
            "rstd": Pool(kb, "rstd", 2, [128, 1, W], F32, stack=st),
            "nmr": Pool(kb, "nmr", 2, [128, 1, W], F32, stack=st),
        }

    def ffn_phase(self, l, s, src, dst, dst_off, ctx):
        kb = self.kb
        W = 256
        mb = 0 if s == 0 else 6
        lni = (l * 3 + (0 if s == 0 else 2)) * NCH
        with kb.phase() as st:
            self.psum = Pool(kb, "ps", 8, [128, 512], F32, psum=True, stack=st)
            w1 = kb.sb("w1", [128, NCH, 2 * DFF], BF16, st)
            w2 = kb.sb("w2", [128, NFF, D], BF16, st)
            for c in range(NCH):
                for hh in range(2):
                    kb.dma("pool", w1[:, c, hh * DFF:(hh + 1) * DFF],
                           self.ffn_w_in[l, s, c * 128:(c + 1) * 128, hh * DFF:(hh + 1) * DFF], w1, w=[w1])
            for c in range(NFF):
                kb.dma("pool", w2[:, c, :], self.ffn_w_out[l, s, c * 128:(c + 1) * 128, :], w2, w=[w2])
            pl = self.stat_pools(W, st)
            xp = Pool(kb, "x", 2, [128, NCH, W], F32, stack=st)
            xnp = Pool(kb, "xn", 2, [128, NCH, W], F32, stack=st)
            up = Pool(kb, "u", 2, [128, NCH, W], BF16, stack=st)
            hp = Pool(kb, "h", 1, [128, NFF, W], BF16, stack=st)
            gp = Pool(kb, "g", 2, [128, W], F32, stack=st)
            srcv = src.rearrange("(c p) t -> p c t", p=128)
            dstv = dst.rearrange("(c p) t -> p c t", p=128)
            tl = self.tiles(W, ctx=ctx)
            T_ = {}

            def stage_a(i):
                seg, t0, _ = tl[i]
                x = xp.next()
                kb.dma("sp", x[:, :, :], srcv[:, :, t0:t0 + W], x, r=[("dram", id(src))], w=[x])
                P = self.psum.next()
                rstd, nmr = self.ln_stats(x, W, P, pl)
                xn = xnp.next()
                self.normalize(xn, x, W, rstd, nmr)
                u = up.next()
                for c in range(NCH):
                    kb.op("act", lambda e: e.activation(out=u[:, c, :], in_=xn[:, c, :], func=AF.Identity,
                                                        scale=self.modp1[:, (mb + 1) * NCH + c, seg:seg + 1],
                                                        bias=self.mods[:, mb * NCH + c, seg:seg + 1]),
                          r=[xn, self.modp1, self.mods], w=[u])
                kb.op("pool", lambda e: e.tensor_scalar(out=x[:, :, :], in0=x[:, :, :], scalar1=ALPHA, scalar2=None,
                                                        op0=ALU.mult), r=[x], w=[x])
                T_[i] = (x, xn, u)

            def stage_b(i):
                x, xn, u = T_[i]
                h = hp.next()
                for j in range(NFF):
                    P = self.psum.next()
                    for c in range(NCH):
                        kb.op("pe", lambda e: e.matmul(P[:, 0:W], lhsT=w1[:, c, j * 128:(j + 1) * 128], rhs=u[:, c, :],
                                                       start=(c == 0), stop=(c == NCH - 1)), r=[w1, u], w=[P])
                    for c in range(NCH):
                        kb.op("pe", lambda e: e.matmul(P[:, W:2 * W], lhsT=w1[:, c, DFF + j * 128:DFF + (j + 1) * 128],
                                                       rhs=u[:, c, :], start=(c == 0), stop=(c == NCH - 1)),
                              r=[w1, u], w=[P])
                    g = gp.next()
                    kb.op("act", lambda e: e.activation(out=g[:, :], in_=P[:, 0:W], func=AF.Silu), r=[P], w=[g])
                    kb.op("dve", lambda e: e.tensor_tensor(out=h[:, j, :], in0=P[:, W:2 * W], in1=g[:, :], op=ALU.mult),
                          r=[P, g], w=[h])
                T_[i] = (x, xn, u, h)

            def stage_c(i):
                seg, t0, _ = tl[i]
                x, xn, u, h = T_.pop(i)
                for oc in range(NCH):
                    if oc % 2 == 0:
                        P = self.psum.next()
                    o0 = (oc % 2) * W
                    for j in range(NFF):
                        kb.op("pe", lambda e: e.matmul(P[:, o0:o0 + W], lhsT=w2[:, j, oc * 128:(oc + 1) * 128],
                                                       rhs=h[:, j, :], start=(j == 0), stop=(j == NFF - 1)),
                              r=[w2, h], w=[P])
                    kb.op("dve", lambda e: e.scalar_tensor_tensor(
                        out=x[:, oc, :], in0=P[:, o0:o0 + W], scalar=self.modh[:, (mb + 2) * NCH + oc, seg:seg + 1],
                        in1=x[:, oc, :], op0=ALU.mult, op1=ALU.add), r=[P, x, self.modh], w=[x])
                P = self.psum.next()
                rstd, nmr = self.ln_stats(x, W, P, pl)
                self.normalize(xn, x, W, rstd, nmr)
                for c in range(NCH):
                    kb.op("act", lambda e: e.activation(out=xn[:, c, :], in_=xn[:, c, :], func=AF.Identity,
                                                        scale=self.lng[:, lni + c:lni + c + 1],
                                                        bias=self.lnb[:, lni + c:lni + c + 1]),
                          r=[xn, self.lng, self.lnb], w=[xn])
                kb.dma("sp", dstv[:, :, t0 - dst_off:t0 - dst_off + W], xn[:, :, :], xn,
                       r=[xn], w=[("dram", id(dst))])

            stage_a(0)
            for i in range(len(tl)):
                stage_b(i)
                if i + 1 < len(tl):
                    stage_a(i + 1)
                stage_c(i)


def host_inputs(inp, b, T):
    L = DEPTH
    d = {}
    d["xT"] = np.ascontiguousarray(np.concatenate([inp["ctx"][b], inp["x"][b, :T]], 0).T)
    cond = np.stack([inp["c"][b], inp["c_ctx"]], -1)
    d["condT"] = np.ascontiguousarray(cond.reshape(NCH, 128, 2).transpose(1, 0, 2))
    d["w_ada"] = inp["w_ada"]
    d["b_ada"] = np.ascontiguousarray(inp["b_ada"].reshape(L, 72, 128).transpose(0, 2, 1))
    d["ln_g"] = np.ascontiguousarray(inp["ln_g"].reshape(L * 3 * NCH, 128).T)
    d["ln_b"] = np.ascontiguousarray(inp["ln_b"].reshape(L * 3 * NCH, 128).T)
    d["ffn_w_in"] = inp["ffn_w_in"]
    d["ffn_w_out"] = inp["ffn_w_out"]
    return d


NWA = 10 + 1 + 4 + 24
CH_Q, CH_QS, CH_K, CH_KS, CH_KB, CH_KBS, CH_V, CH_S5, CH_G = 0, 4, 8, 9, 10, 11, 12, 13, 17
NCH_A = 41


def _mixA_declare(self):
    L = DEPTH
    self.w_inA = self.din("w_inA", [L, D, NCH_A * 128])
    self.ropeC = self.din("ropeC", [128, self.NT])
    self.ropeS = self.din("ropeS", [128, self.NT])
    self.sinkT = self.din("sinkT", [L, 64, 8])
    self.maskP = self.din("maskP", [128, 512])
    self.maskN = self.din("maskN", [128, 512])


def _mixA_phase(self, l, src, qS, kS, vS, usS, gS):
    kb = self.kb
    W = 256
    with kb.phase() as st:
        self.psum = Pool(kb, "ps", 8, [128, 512], F32, psum=True, stack=st)
        w = kb.sb("wA", [128, NCH, NCH_A * 128], BF16, st)
        for c in range(NCH):
            for hh in range(2):
                n0, n1 = (0, 21 * 128) if hh == 0 else (21 * 128, NCH_A * 128)
                kb.dma("pool", w[:, c, n0:n1], self.w_inA[l, c * 128:(c + 1) * 128, n0:n1], w, w=[w])
        pl = self.stat_pools(W, st)
        xp = Pool(kb, "x", 2, [128, NCH, W], F32, stack=st)
        xnp = Pool(kb, "xn", 2, [128, NCH, W], F32, stack=st)
        up = Pool(kb, "u", 2, [128, NCH, W], BF16, stack=st)
        cp = Pool(kb, "rc", 2, [128, W], F32, stack=st)
        sp_ = Pool(kb, "rs", 2, [128, W], F32, stack=st)
        t1p = Pool(kb, "t1", 2, [128, W], F32, stack=st)
        t2p = Pool(kb, "t2", 2, [128, W], F32, stack=st)
        qp = Pool(kb, "qo", 2, [128, 4, W], BF16, stack=st)
        kp = Pool(kb, "ko", 2, [128, 2, W], BF16, stack=st)
        vp = Pool(kb, "vo", 2, [128, 2, 128], BF16, stack=st)
        usp = Pool(kb, "uso", 2, [128, 4, W], F32, stack=st)
        gp = Pool(kb, "go", 2, [128, 24, W], BF16, stack=st)
        srcv = src.rearrange("(c p) t -> p c t", p=128)

        def proj(P, o0, ch, u):
            for c in range(NCH):
                kb.op("pe", lambda e: e.matmul(P[:, o0:o0 + W], lhsT=w[:, c, ch * 128:(ch + 1) * 128], rhs=u[:, c, :],
                                               start=(c == 0), stop=(c == NCH - 1)), r=[w, u], w=[P])

        tl = self.tiles(W)
        T_ = {}

        def stage_a(i):
            seg, t0, _ = tl[i]
            x = xp.next()
            kb.dma("sp", x[:, :, :], srcv[:, :, t0:t0 + W], x, r=[("dram", id(src))], w=[x])
            cT, sT = cp.next(), sp_.next()
            kb.dma("sp", cT[:, :], self.ropeC[:, t0:t0 + W], cT, w=[cT])
            kb.dma("sp", sT[:, :], self.ropeS[:, t0:t0 + W], sT, w=[sT])
            P = self.psum.next()
            rstd, nmr = self.ln_stats(x, W, P, pl)
            xn = xnp.next()
            self.normalize(xn, x, W, rstd, nmr)
            u = up.next()
            for c in range(NCH):
                kb.op("act", lambda e: e.activation(out=u[:, c, :], in_=xn[:, c, :], func=AF.Identity,
                                                    scale=self.modp1[:, 4 * NCH + c, seg:seg + 1],
                                                    bias=self.mods[:, 3 * NCH + c, seg:seg + 1]),
                      r=[xn, self.modp1, self.mods], w=[u])
            T_[i] = (u, cT, sT)

        def stage_b1(i):
            seg, t0, _ = tl[i]
            u, cT, sT = T_[i]
            qo, ko = qp.next(), kp.next()

            def rope(dst_ap, dst, ch, chs):
                P = self.psum.next()
                proj(P, 0, ch, u)
                proj(P, W, chs, u)
                t1, t2 = t1p.next(), t2p.next()
                kb.op("dve", lambda e: e.tensor_tensor(out=t1[:, :], in0=P[:, 0:W], in1=cT[:, :], op=ALU.mult),
                      r=[P, cT], w=[t1])
                kb.op("dve", lambda e: e.tensor_tensor(out=t2[:, :], in0=P[:, W:2 * W], in1=sT[:, :], op=ALU.mult),
                      r=[P, sT], w=[t2])
                kb.op("dve", lambda e: e.tensor_tensor(out=dst_ap, in0=t1[:, :], in1=t2[:, :], op=ALU.add),
                      r=[t1, t2], w=[dst])
            for c in range(4):
                rope(qo[:, c, :], qo, CH_Q + c, CH_QS + c)
            rope(ko[:, 0, :], ko, CH_K, CH_KS)
            rope(ko[:, 1, :], ko, CH_KB, CH_KBS)
            kb.dma("sp", qS.rearrange("(c p) t -> p c t", p=128)[:, :, t0:t0 + W], qo[:, :, :], qo, r=[qo],
                   w=[("dram", id(qS))])
            kb.dma("sp", kS.rearrange("(c p) t -> p c t", p=128)[:, :, t0:t0 + W], ko[:, :, :], ko, r=[ko],
                   w=[("dram", id(kS))])

        def stage_b2(i):
            seg, t0, _ = tl[i]
            u, cT, sT = T_.pop(i)
            vo = vp.next()
            P = self.psum.next()
            for tb in range(W // 128):
                for c in range(NCH):
                    kb.op("pe", lambda e: e.matmul(P[:, tb * 128:(tb + 1) * 128], lhsT=u[:, c, tb * 128:(tb + 1) * 128],
                                                   rhs=w[:, c, CH_V * 128:(CH_V + 1) * 128],
                                                   start=(c == 0), stop=(c == NCH - 1)), r=[w, u], w=[P])
            kb.op("act", lambda e: e.activation(out=vo[:, :, :], in_=P[:, 0:W].rearrange("p (b n) -> p b n", b=2),
                                                func=AF.Copy), r=[P], w=[vo])
            kb.dma("act", vS[t0:t0 + W, :].rearrange("(b p) n -> p b n", p=128), vo[:, :, :], vo, r=[vo],
                   w=[("dram", id(vS))])
            uso = usp.next()
            for c in range(4):
                if c % 2 == 0:
                    P = self.psum.next()
                o0 = (c % 2) * W
                proj(P, o0, CH_S5 + c, u)
                kb.op("act", lambda e: e.activation(out=uso[:, c, :], in_=P[:, o0:o0 + W], func=AF.Copy),
                      r=[P], w=[uso])
            kb.dma("act", usS.rearrange("(c p) t -> p c t", p=128)[:, :, t0:t0 + W], uso[:, :, :], uso, r=[uso],
                   w=[("dram", id(usS))])
            go = gp.next()
            for c in range(24):
                if c % 2 == 0:
                    P = self.psum.next()
                o0 = (c % 2) * W
                proj(P, o0, CH_G + c, u)
                kb.op("act", lambda e: e.activation(out=go[:, c, :], in_=P[:, o0:o0 + W], func=AF.Sigmoid),
                      r=[P], w=[go])
            kb.dma("act", gS.rearrange("(c p) t -> p c t", p=128)[:, :, t0:t0 + W], go[:, :, :], go, r=[go],
                   w=[("dram", id(gS))])

        stage_a(0)
        for i in range(len(tl)):
            stage_b1(i)
            if i + 1 < len(tl):
                stage_a(i + 1)
            stage_b2(i)


def _attn_phase(self, l, qS, kS, vS, yaS):
    kb = self.kb
    NB = self.T // 128
    with kb.phase() as st:
        self.psum = Pool(kb, "ps", 8, [128, 512], F32, psum=True, stack=st)
        mP = kb.sb("mP", [128, 512], BF16, st)
        mN = kb.sb("mN", [128, 512], BF16, st)
        kb.dma("pool", mP[:, :], self.maskP[:, :], mP, w=[mP])
        kb.dma("pool", mN[:, :], self.maskN[:, :], mN, w=[mN])
        ones = kb.sb("ones64", [128, 64], BF16, st)
        kb.op("dve", lambda e: e.memset(ones[:], 1.0), w=[ones])
        esk = kb.sb("esk", [64, 8], F32, st)
        kb.dma("sp", esk[:, :], self.sinkT[l, :, :], esk, w=[esk])
        kb.op("act", lambda e: e.activation(out=esk[:, :], in_=esk[:, :], func=AF.Exp), r=[esk], w=[esk])
        eskb = kb.sb("eskb", [64, 8, 128], F32, st)
        kb.op("dve", lambda e: e.tensor_copy(out=eskb[:, :, :], in_=esk[:, :].unsqueeze(2).broadcast_to([64, 8, 128])),
              r=[esk], w=[eskb])
        kc = kb.sb("kc", [128, 2, CTX], BF16, st)
        kb.dma("sp", kc[:, :, :], kS.rearrange("(c p) t -> p c t", p=128)[:, :, 0:CTX], kc, r=[("dram", id(kS))], w=[kc])
        vc = kb.sb("vc", [128, 2, 128], BF16, st)
        kb.dma("sp", vc[:, :, :], vS[0:CTX, :].rearrange("(b p) n -> p b n", p=128), vc, r=[("dram", id(vS))], w=[vc])
        qp = Pool(kb, "aq", 2, [128, 4, 128], BF16, stack=st)
        kwp = Pool(kb, "akw", 2, [128, 2, 384], BF16, stack=st)
        vwp = Pool(kb, "avw", 2, [128, 3, 128], BF16, stack=st)
        pp = Pool(kb, "ap", 4, [128, 512], BF16, stack=st)
        dp = Pool(kb, "ad", 2, [64, 512], F32, stack=st)
        op_ = Pool(kb, "ao", 2, [64, 8, 128], BF16, stack=st)
        qv = qS.rearrange("(c p) t -> p c t", p=128)
        kv = kS.rearrange("(c p) t -> p c t", p=128)
        blocks = [(1, b) for b in range(CTX // 128)] + [(0, b) for b in range(NB)]
        self._acc_i = 0
        self._sc_i = 0
        for (seg, b) in blocks:
            t0 = b * 128 if seg == 1 else CTX + b * 128
            q = qp.next()
            kb.dma("sp", q[:, :, :], qv[:, :, t0:t0 + 128], q, r=[("dram", id(qS))], w=[q])
            keyblocks = []
            if seg == 0:
                lo = max(b - 1, 0)
                hi = min(b + 1, NB - 1)
                nb_ = hi - lo + 1
                kw, vw = kwp.next(), vwp.next()
                kb.dma("sp", kw[:, :, 0:nb_ * 128], kv[:, :, CTX + lo * 128:CTX + (hi + 1) * 128], kw,
                       r=[("dram", id(kS))], w=[kw])
                kb.dma("sp", vw[:, 0:nb_, :],
                       vS[CTX + lo * 128:CTX + (hi + 1) * 128, :].rearrange("(b p) n -> p b n", p=128), vw,
                       r=[("dram", id(vS))], w=[vw])
                for bb in range(lo, hi + 1):
                    i = bb - lo
                    mask = mP if bb < b else (mN if bb > b else None)
                    keyblocks.append((kw, i * 128, vw, i, mask))
            for i in range(CTX // 128):
                keyblocks.append((kc, i * 128, vc, i, None))
            oo = op_.next()
            accb = self.psum.bufs[0:4]
            scb = self.psum.bufs[4:8]
            for kvh in range(2):
                Pn = accb[(self._acc_i) % 4]
                Pd = accb[(self._acc_i + 1) % 4]
                self._acc_i += 2
                for bi, (kbuf, koff, vbuf, vi, mask) in enumerate(keyblocks):
                    Ps = [scb[self._sc_i % 4], scb[(self._sc_i + 1) % 4]]
                    self._sc_i += 2
                    pt = pp.next()
                    for par in range(2):
                        base = par * 64
                        var = 0 if (kvh * 64 == base) else 1
                        for j in range(2):
                            h = kvh * 4 + 2 * j + par
                            kb.op("pe", lambda e: e.matmul(Ps[par][:, j * 128:(j + 1) * 128],
                                                           lhsT=kbuf[base:base + 64, var, koff:koff + 128],
                                                           rhs=q[base:base + 64, h // 2, :], start=True, stop=True),
                                  r=[kbuf, q], w=[Ps[par]])
                        kb.op("act", lambda e: e.activation(out=pt[:, par * 256:(par + 1) * 256], in_=Ps[par][:, 0:256],
                                                            func=AF.Exp, scale=0.125), r=[Ps[par]], w=[pt])
                    if mask is not None:
                        kb.op("pool", lambda e: e.tensor_tensor(out=pt[:, :], in0=pt[:, :], in1=mask[:, :], op=ALU.mult),
                              r=[pt, mask], w=[pt])
                    first, last = bi == 0, bi == len(keyblocks) - 1
                    kb.op("pe", lambda e: e.matmul(Pn[0:64, :], lhsT=vbuf[:, vi, kvh * 64:(kvh + 1) * 64], rhs=pt[:, :],
                                                   start=first, stop=last), r=[vbuf, pt], w=[Pn])
                    kb.op("pe", lambda e: e.matmul(Pd[0:64, :], lhsT=ones[:, :], rhs=pt[:, :],
                                                   start=first, stop=last), r=[ones, pt], w=[Pd])
                den = dp.next()
                kb.op("dve", lambda e: e.tensor_tensor(
                    out=den[:, :].rearrange("p (r j q) -> p r j q", r=2, j=2), in0=Pd[0:64, :].rearrange("p (r j q) -> p r j q", r=2, j=2),
                    in1=eskb[:, kvh * 4:(kvh + 1) * 4, :].rearrange("p (j r) q -> p r j q", r=2),
                    op=ALU.add), r=[Pd, eskb], w=[den])
                kb.op("dve", lambda e: e.reciprocal(out=den[:, :], in_=den[:, :]), r=[den], w=[den])
                kb.op("dve", lambda e: e.tensor_tensor(
                    out=oo[:, kvh * 4:(kvh + 1) * 4, :].rearrange("p (j r) q -> p r j q", r=2),
                    in0=Pn[0:64, :].rearrange("p (r j q) -> p r j q", r=2, j=2),
                    in1=den[:, :].rearrange("p (r j q) -> p r j q", r=2, j=2), op=ALU.mult),
                      r=[Pn, den], w=[oo])
            kb.dma("sp", yaS.rearrange("(h p) t -> p h t", p=64)[:, :, t0:t0 + 128], oo[:, :, :], oo, r=[oo],
                   w=[("dram", id(yaS))])


Model.mixA_declare = _mixA_declare
Model.mixA_phase = _mixA_phase
Model.attn_phase = _attn_phase


def _rope_tables(T):
    NT = CTX + T
    C = np.ones((128, NT), np.float32)
    S = np.zeros((128, NT), np.float32)
    t = np.arange(T)
    row = (t // 64).astype(np.float32)
    col = (t % 64).astype(np.float32)
    inv = (10000.0 ** (-np.arange(16, dtype=np.float32) / 16)).astype(np.float32)
    for d in range(64):
        i = d % 16
        pos = row if d < 32 else col
        ang = (pos * inv[i]).astype(np.float32)
        sign = -1.0 if (d % 32) < 16 else 1.0
        for hb in (0, 64):
            C[hb + d, CTX:] = np.cos(ang)
            S[hb + d, CTX:] = sign * np.sin(ang)
    return C, S


def _swap_perm(n_heads):
    idx = []
    for h in range(n_heads):
        for d in range(64):
            p = d + 16 if (d % 32) < 16 else d - 16
            idx.append(h * 64 + p)
    return np.array(idx)


def host_inputs_A(inp, T):
    d = {}
    w = inp["w_in"]
    q = w[:, :, 0:512]
    k = w[:, :, 512:640]
    v = w[:, :, 640:768]
    kB = np.concatenate([k[:, :, 64:128], k[:, :, 0:64]], -1)
    s5 = w[:, :, 2624:3136]
    g = w[:, :, 3136:6208]
    d["w_inA"] = np.ascontiguousarray(np.concatenate(
        [q, q[:, :, _swap_perm(8)], k, k[:, :, _swap_perm(2)], kB, kB[:, :, _swap_perm(2)], v, s5, g], -1))
    C, S = _rope_tables(T)
    d["ropeC"], d["ropeS"] = C, S
    d["sinkT"] = np.ascontiguousarray(np.broadcast_to(inp["attn_sink"][:, None, :], (DEPTH, 64, 8)))
    j = np.arange(128)[:, None]
    i = np.arange(128)[None, :]
    d["maskP"] = np.ascontiguousarray(np.tile((j >= i).astype(np.float32), (1, 4)))
    d["maskN"] = np.ascontiguousarray(np.tile((j <= i).astype(np.float32), (1, 4)))
    return d


I32 = mybir.dt.int32
TWO_PI = 2.0 * np.pi


def _s5_declare(self):
    L = DEPTH
    self.s5_are = self.din("s5_are", [L, 2, 128, 4, 64])
    self.s5_aim = self.din("s5_aim", [L, 2, 128, 4, 64])
    self.s5_ls = self.din("s5_ls", [L, 2, 128, 4])
    self.s5_brT = self.din("s5_brT", [L, 128, 4, 64])
    self.s5_biT = self.din("s5_biT", [L, 128, 4, 64])
    self.s5_are2 = self.din("s5_are2", [L, 2, 128, 16])
    self.s5_aim2 = self.din("s5_aim2", [L, 2, 128, 16])
    self.s5_ls2 = self.din("s5_ls2", [L, 2, 128, 16])
    self.s5_crT = self.din("s5_crT", [L, 128, 16, 16])
    self.s5_ciT = self.din("s5_ciT", [L, 128, 16, 16])
    self.s5_rowmask = self.din("s5_rowmask", [128, 16, 2])
    self.s5_dT = self.din("s5_dT", [128, L * 4])
    self.s5_glub = self.din("s5_glub", [128, L * 4])
    self.s5_gluw = self.din("s5_gluw", [L, 512, 512])
    self.tauT = self.din("tauT", [128, 128])


def _sincos(self, ang, angk, n, S, Sk, C, Ck, st):
    kb = self.kb
    t = kb.sb("sc_t", [128, n], F32, st)
    ti = kb.sb("sc_i", [128, n], I32, st)
    tf = kb.sb("sc_f", [128, n], F32, st)
    for (off, dst, dk) in ((0.0, S, Sk), (0.25, C, Ck)):
        kb.op("dve", lambda e: e.tensor_scalar(out=t[:, :], in0=ang, scalar1=1.0 / TWO_PI, scalar2=off,
                                               op0=ALU.mult, op1=ALU.add), r=[angk], w=[t])
        kb.op("dve", lambda e: e.tensor_copy(out=ti[:, :], in_=t[:, :]), r=[t], w=[ti])
        kb.op("dve", lambda e: e.tensor_copy(out=tf[:, :], in_=ti[:, :]), r=[ti], w=[tf])
        kb.op("dve", lambda e: e.tensor_tensor(out=tf[:, :], in0=t[:, :], in1=tf[:, :], op=ALU.subtract),
              r=[t, tf], w=[tf])
        kb.op("act", lambda e: e.activation(out=dst, in_=tf[:, :], func=AF.Sin, scale=TWO_PI), r=[tf], w=[dk])


def _s5_dir(self, l, d, usS, ysbS, ysS, st, PS2):
    kb = self.kb
    NT = self.NT
    nchunk = NT // 128
    usv = usS.rearrange("(c p) t -> p c t", p=128)
    ybv = ysbS.rearrange("(c p) t -> p c t", p=128)
    ysv = ysS.rearrange("(c p) t -> p c t", p=128)
    V = lambda e_, f, r, w: kb.op(e_, f, r=r, w=w)
    _pers = {}
    for (n_, shp_, dt_) in (("DR", [128, 16, 128], BF16), ("DI", [128, 16, 128], BF16), ("COS", [128, 16, 128], F32),
                            ("SIN", [128, 16, 128], F32), ("RHO0", [128, 16, 128], F32), ("rho", [128, 16], F32),
                            ("lr2", [128, 16], F32), ("li2", [128, 16], F32), ("CR", [128, 16, 128], BF16),
                            ("CIn", [128, 16, 128], BF16), ("s5d", [128, 4], F32), ("s5gb", [128, 4], F32),
                            ("gluw", [128, 4, 512], BF16)):
        _pers[n_] = kb.sb(n_, shp_, dt_, st)
    sbf = lambda n, shp, dt=F32: _pers[n] if n in _pers else kb.sb(n, shp, dt, st)
    st2 = contextlib.ExitStack()
    tmpf = lambda n, shp, dt=F32: kb.sb(n, shp, dt, st2)
    are, aim = tmpf("are", [128, 4, 64]), tmpf("aim", [128, 4, 64])
    ls = tmpf("ls", [128, 4])
    br, bi = tmpf("br", [128, 4, 64]), tmpf("bi", [128, 4, 64])
    kb.dma("sp", are[:, :, :], self.s5_are[l, d], are, w=[are])
    kb.dma("sp", aim[:, :, :], self.s5_aim[l, d], aim, w=[aim])
    kb.dma("sp", ls[:, :], self.s5_ls[l, d], ls, w=[ls])
    kb.dma("sp", br[:, :, :], self.s5_brT[l], br, w=[br])
    kb.dma("sp", bi[:, :, :], self.s5_biT[l], bi, w=[bi])
    rmask = tmpf("rmask", [128, 16, 2])
    kb.dma("sp", rmask[:, :, :], self.s5_rowmask[:, :, :], rmask, w=[rmask])
    V("act", lambda e: e.activation(out=ls[:, :], in_=ls[:, :], func=AF.Exp), [ls], [ls])
    dtb = ls[:, :].unsqueeze(2).broadcast_to([128, 4, 64])
    adt, th = tmpf("adt", [128, 4, 64]), tmpf("th", [128, 4, 64])
    V("dve", lambda e: e.tensor_tensor(out=adt[:, :, :], in0=are[:, :, :], in1=dtb, op=ALU.mult), [are, ls], [adt])
    V("act", lambda e: e.activation(out=adt[:, :, :], in_=adt[:, :, :], func=AF.Exp), [adt], [adt])
    V("dve", lambda e: e.tensor_tensor(out=th[:, :, :], in0=aim[:, :, :], in1=dtb, op=ALU.mult), [aim, ls], [th])
    Sd, Cd = tmpf("Sd", [128, 256]), tmpf("Cd", [128, 256])
    thf = th[:, :, :].rearrange("p a b -> p (a b)")
    self.sincos(thf, th, 256, Sd[:, :], Sd, Cd[:, :], Cd, st2)
    lr, li = tmpf("lr", [128, 256]), tmpf("li", [128, 256])
    magf = adt[:, :, :].rearrange("p a b -> p (a b)")
    V("dve", lambda e: e.tensor_tensor(out=lr[:, :], in0=magf, in1=Cd[:, :], op=ALU.mult), [adt, Cd], [lr])
    V("dve", lambda e: e.tensor_tensor(out=li[:, :], in0=magf, in1=Sd[:, :], op=ALU.mult), [adt, Sd], [li])
    aref = are[:, :, :].rearrange("p a b -> p (a b)")
    aimf = aim[:, :, :].rearrange("p a b -> p (a b)")
    t1, t2, den = tmpf("t1", [128, 256]), tmpf("t2", [128, 256]), tmpf("den", [128, 256])
    V("dve", lambda e: e.tensor_tensor(out=t1[:, :], in0=aref, in1=aref, op=ALU.mult), [are], [t1])
    V("dve", lambda e: e.tensor_tensor(out=t2[:, :], in0=aimf, in1=aimf, op=ALU.mult), [aim], [t2])
    V("dve", lambda e: e.tensor_tensor(out=den[:, :], in0=t1[:, :], in1=t2[:, :], op=ALU.add), [t1, t2], [den])
    V("dve", lambda e: e.reciprocal(out=den[:, :], in_=den[:, :]), [den], [den])
    V("dve", lambda e: e.tensor_scalar(out=lr[:, :], in0=lr[:, :], scalar1=-1.0, scalar2=None, op0=ALU.add),
      [lr], [lr])
    cr, ci = tmpf("cr", [128, 256]), tmpf("ci", [128, 256])
    V("dve", lambda e: e.tensor_tensor(out=t1[:, :], in0=lr[:, :], in1=aref, op=ALU.mult), [lr, are], [t1])
    V("dve", lambda e: e.tensor_tensor(out=t2[:, :], in0=li[:, :], in1=aimf, op=ALU.mult), [li, aim], [t2])
    V("dve", lambda e: e.tensor_tensor(out=cr[:, :], in0=t1[:, :], in1=t2[:, :], op=ALU.add), [t1, t2], [cr])
    V("dve", lambda e: e.tensor_tensor(out=cr[:, :], in0=cr[:, :], in1=den[:, :], op=ALU.mult), [cr, den], [cr])
    V("dve", lambda e: e.tensor_tensor(out=t1[:, :], in0=li[:, :], in1=aref, op=ALU.mult), [li, are], [t1])
    V("dve", lambda e: e.tensor_tensor(out=t2[:, :], in0=lr[:, :], in1=aimf, op=ALU.mult), [lr, aim], [t2])
    V("dve", lambda e: e.tensor_tensor(out=ci[:, :], in0=t1[:, :], in1=t2[:, :], op=ALU.subtract), [t1, t2], [ci])
    V("dve", lambda e: e.tensor_tensor(out=ci[:, :], in0=ci[:, :], in1=den[:, :], op=ALU.mult), [ci, den], [ci])
    brf = br[:, :, :].rearrange("p a b -> p (a b)")
    bif = bi[:, :, :].rearrange("p a b -> p (a b)")
    bbr, bbi = tmpf("bbr", [128, 4, 64]), tmpf("bbi", [128, 4, 64])
    bbrf = bbr[:, :, :].rearrange("p a b -> p (a b)")
    bbif = bbi[:, :, :].rearrange("p a b -> p (a b)")
    V("dve", lambda e: e.tensor_tensor(out=t1[:, :], in0=cr[:, :], in1=brf, op=ALU.mult), [cr, br], [t1])
    V("dve", lambda e: e.tensor_tensor(out=t2[:, :], in0=ci[:, :], in1=bif, op=ALU.mult), [ci, bi], [t2])
    V("dve", lambda e: e.tensor_tensor(out=bbrf, in0=t1[:, :], in1=t2[:, :], op=ALU.subtract), [t1, t2], [bbr])
    V("dve", lambda e: e.tensor_tensor(out=t1[:, :], in0=cr[:, :], in1=bif, op=ALU.mult), [cr, bi], [t1])
    V("dve", lambda e: e.tensor_tensor(out=t2[:, :], in0=ci[:, :], in1=brf, op=ALU.mult), [ci, br], [t2])
    V("dve", lambda e: e.tensor_tensor(out=bbif, in0=t1[:, :], in1=t2[:, :], op=ALU.add), [t1, t2], [bbi])
    DR, DI = sbf("DR", [128, 16, 128], BF16), sbf("DI", [128, 16, 128], BF16)
    for j in range(16):
        for gp in range(2):
            for (dst, srcb) in ((DR, bbr), (DI, bbi)):
                V("dve", lambda e: e.tensor_scalar(out=dst[:, j, gp * 64:(gp + 1) * 64], in0=srcb[:, j // 4, :],
                                                   scalar1=rmask[:, j, gp:gp + 1], scalar2=None, op0=ALU.mult),
                  [srcb, rmask], [dst])
    are2, aim2, ls2 = tmpf("are2", [128, 16]), tmpf("aim2", [128, 16]), tmpf("ls2", [128, 16])
    kb.dma("sp", are2[:, :], self.s5_are2[l, d], are2, w=[are2])
    kb.dma("sp", aim2[:, :], self.s5_aim2[l, d], aim2, w=[aim2])
    kb.dma("sp", ls2[:, :], self.s5_ls2[l, d], ls2, w=[ls2])
    tau = tmpf("tau", [128, 128])
    kb.dma("sp", tau[:, :], self.tauT[:, :], tau, w=[tau])
    V("act", lambda e: e.activation(out=ls2[:, :], in_=ls2[:, :], func=AF.Exp), [ls2], [ls2])
    rho, th2 = sbf("rho", [128, 16]), tmpf("th2", [128, 16])
    V("dve", lambda e: e.tensor_tensor(out=rho[:, :], in0=are2[:, :], in1=ls2[:, :], op=ALU.mult), [are2, ls2], [rho])
    V("act", lambda e: e.activation(out=rho[:, :], in_=rho[:, :], func=AF.Exp), [rho], [rho])
    V("dve", lambda e: e.tensor_tensor(out=th2[:, :], in0=aim2[:, :], in1=ls2[:, :], op=ALU.mult), [aim2, ls2], [th2])
    ang = tmpf("ang", [128, 16, 128])
    V("dve", lambda e: e.tensor_tensor(out=ang[:, :, :], in0=th2[:, :].unsqueeze(2).broadcast_to([128, 16, 128]),
                                       in1=tau[:, :].unsqueeze(1).broadcast_to([128, 16, 128]), op=ALU.mult),
      [th2, tau], [ang])
    COS, SIN = sbf("COS", [128, 16, 128]), sbf("SIN", [128, 16, 128])
    self.sincos(ang[:, :, :].rearrange("p a b -> p (a b)"), ang, 2048,
                SIN[:, :, :].rearrange("p a b -> p (a b)"), SIN, COS[:, :, :].rearrange("p a b -> p (a b)"), COS, st2)
    S1, C1 = tmpf("S1", [128, 16]), tmpf("C1", [128, 16])
    self.sincos(th2[:, :], th2, 16, S1[:, :], S1, C1[:, :], C1, st2)
    lr2, li2 = sbf("lr2", [128, 16]), sbf("li2", [128, 16])
    V("dve", lambda e: e.tensor_tensor(out=lr2[:, :], in0=rho[:, :], in1=C1[:, :], op=ALU.mult), [rho, C1], [lr2])
    V("dve", lambda e: e.tensor_tensor(out=li2[:, :], in0=rho[:, :], in1=S1[:, :], op=ALU.mult), [rho, S1], [li2])
    RHO0 = sbf("RHO0", [128, 16, 128])
    V("dve", lambda e: e.tensor_copy(out=RHO0[:, :, :], in_=rho[:, :].unsqueeze(2).broadcast_to([128, 16, 128])),
      [rho], [RHO0])
    f0 = 127 if d == 1 else 0
    V("dve", lambda e: e.memset(RHO0[:, :, f0:f0 + 1], 0.0), [], [RHO0])
    crT, ciT = tmpf("crT", [128, 16, 16]), tmpf("ciT", [128, 16, 16])
    kb.dma("sp", crT[:, :, :], self.s5_crT[l], crT, w=[crT])
    kb.dma("sp", ciT[:, :, :], self.s5_ciT[l], ciT, w=[ciT])
    CR, CIn = sbf("CR", [128, 16, 128], BF16), sbf("CIn", [128, 16, 128], BF16)
    V("dve", lambda e: e.memset(CR[:, :, :], 0.0), [], [CR])
    V("dve", lambda e: e.memset(CIn[:, :, :], 0.0), [], [CIn])
    for j in range(16):
        for gp in range(2):
            c0 = 32 * (j % 4) + 16 * gp
            V("dve", lambda e: e.tensor_copy(out=CR[gp * 64:(gp + 1) * 64, j, c0:c0 + 16],
                                             in_=crT[gp * 64:(gp + 1) * 64, j, :]), [crT], [CR])
            V("dve", lambda e: e.tensor_scalar(out=CIn[gp * 64:(gp + 1) * 64, j, c0:c0 + 16],
                                               in0=ciT[gp * 64:(gp + 1) * 64, j, :], scalar1=-1.0, scalar2=None,
                                               op0=ALU.mult), [ciT], [CIn])
    if d == 0:
        dv, gb = sbf("s5d", [128, 4]), sbf("s5gb", [128, 4])
        kb.dma("sp", dv[:, :], self.s5_dT[:, l * 4:(l + 1) * 4], dv, w=[dv])
        kb.dma("sp", gb[:, :], self.s5_glub[:, l * 4:(l + 1) * 4], gb, w=[gb])
        gw = sbf("gluw", [128, 4, 512], BF16)
        for c in range(4):
            kb.dma("pool", gw[:, c, :], self.s5_gluw[l, c * 128:(c + 1) * 128, :], gw, w=[gw])
    kb.barrier()
    st2.close()
    usp = Pool(kb, "s5u", 2, [128, 4, 128], BF16, stack=st)
    usfp = Pool(kb, "s5uf", 2, [128, 4, 128], F32, stack=st)
    ybp = Pool(kb, "s5yb", 2, [128, 4, 128], F32, stack=st)
    mp = [Pool(kb, "s5m%d" % i, 1, [128, 8, 128], F32, stack=st) for i in range(4)]
    ZR, ZI = sbf("ZR", [128, 16, 128]), sbf("ZI", [128, 16, 128])
    XZR, XZI = sbf("XZR", [128, 16, 128]), sbf("XZI", [128, 16, 128])
    up_ = [Pool(kb, "s5t%d" % i, 1, [128, 16, 128], F32, stack=st) for i in range(2)]
    XR, XI = sbf("XR", [128, 16, 128], BF16), sbf("XI", [128, 16, 128], BF16)
    xlr, xli = sbf("xlr", [128, 16]), sbf("xli", [128, 16])
    cjr, cji = sbf("cjr", [128, 16]), sbf("cji", [128, 16])
    tt = [sbf("s5tt%d" % i, [128, 16]) for i in range(4)]
    yo = Pool(kb, "s5yo", 2, [128, 4, 128], F32, stack=st)
    rev = (d == 1)
    R3 = (lambda ap: ap[:, :, ::-1]) if rev else (lambda ap: ap)
    first, last = (127, 0) if rev else (0, 127)
    order = [0, 1] + list(range(2, nchunk))
    if rev:
        order = [1, 0] + list(range(nchunk - 1, 1, -1))
    yield
    for ci_, ch in enumerate(order):
        if ci_ > 0:
            yield
        t0 = ch * 128
        us = usp.next()
        kb.dma("pool", us[:, :, :], usv[:, :, t0:t0 + 128], us, r=[("dram", id(usS))], w=[us])
        for hf in range(2):
            PR, PI = PS2.next(), PS2.next()
            for jj in range(8):
                j = hf * 8 + jj
                kb.op("pe", lambda e: e.matmul(PR[:, jj * 128:(jj + 1) * 128], lhsT=DR[:, j, :], rhs=us[:, j // 4, :],
                                               start=True, stop=True), r=[DR, us], w=[PR])
                kb.op("pe", lambda e: e.matmul(PI[:, jj * 128:(jj + 1) * 128], lhsT=DI[:, j, :], rhs=us[:, j // 4, :],
                                               start=True, stop=True), r=[DI, us], w=[PI])
            prv = PR[:, :].rearrange("p (a b) -> p a b", a=8)
            piv = PI[:, :].rearrange("p (a b) -> p a b", a=8)
            cs = R3(COS[:, hf * 8:(hf + 1) * 8, :])
            sn = R3(SIN[:, hf * 8:(hf + 1) * 8, :])
            m = [p.next() for p in mp]
            V("dve", lambda e: e.tensor_tensor(out=m[0][:, :, :], in0=prv, in1=cs, op=ALU.mult), [PR, COS], [m[0]])
            V("dve", lambda e: e.tensor_tensor(out=m[1][:, :, :], in0=piv, in1=sn, op=ALU.mult), [PI, SIN], [m[1]])
            V("dve", lambda e: e.tensor_tensor(out=m[2][:, :, :], in0=piv, in1=cs, op=ALU.mult), [PI, COS], [m[2]])
            V("dve", lambda e: e.tensor_tensor(out=m[3][:, :, :], in0=prv, in1=sn, op=ALU.mult), [PR, SIN], [m[3]])
            V("pool", lambda e: e.tensor_tensor(out=ZR[:, hf * 8:(hf + 1) * 8, :], in0=m[0][:, :, :], in1=m[1][:, :, :],
                                                op=ALU.add), [m[0], m[1]], [ZR])
            V("pool", lambda e: e.tensor_tensor(out=ZI[:, hf * 8:(hf + 1) * 8, :], in0=m[2][:, :, :], in1=m[3][:, :, :],
                                                op=ALU.subtract), [m[2], m[3]], [ZI])
            yield
        if ci_ > 0:
            V("pool", lambda e: e.tensor_tensor(out=tt[0][:, :], in0=lr2[:, :], in1=xlr[:, :], op=ALU.mult), [lr2, xlr], [tt[0]])
            V("pool", lambda e: e.tensor_tensor(out=tt[1][:, :], in0=li2[:, :], in1=xli[:, :], op=ALU.mult), [li2, xli], [tt[1]])
            V("pool", lambda e: e.tensor_tensor(out=cjr[:, :], in0=tt[0][:, :], in1=tt[1][:, :], op=ALU.subtract), [tt[0], tt[1]], [cjr])
            V("pool", lambda e: e.tensor_tensor(out=tt[2][:, :], in0=lr2[:, :], in1=xli[:, :], op=ALU.mult), [lr2, xli], [tt[2]])
            V("pool", lambda e: e.tensor_tensor(out=tt[3][:, :], in0=li2[:, :], in1=xlr[:, :], op=ALU.mult), [li2, xlr], [tt[3]])
            V("pool", lambda e: e.tensor_tensor(out=cji[:, :], in0=tt[2][:, :], in1=tt[3][:, :], op=ALU.add), [tt[2], tt[3]], [cji])
            V("pool", lambda e: e.tensor_tensor(out=ZR[:, :, first], in0=ZR[:, :, first], in1=cjr[:, :], op=ALU.add), [ZR, cjr], [ZR])
            V("pool", lambda e: e.tensor_tensor(out=ZI[:, :, first], in0=ZI[:, :, first], in1=cji[:, :], op=ALU.add), [ZI, cji], [ZI])
        fl = lambda b_: (b_[:, :, :].rearrange("p a b -> p (a b)")[:, ::-1] if rev
                         else b_[:, :, :].rearrange("p a b -> p (a b)"))
        V("dve", lambda e: e.tensor_tensor_scan(out=fl(XZR), data0=fl(RHO0), data1=fl(ZR), initial=0.0,
                                                op0=ALU.mult, op1=ALU.add), [RHO0, ZR], [XZR])
        yield
        V("dve", lambda e: e.tensor_tensor_scan(out=fl(XZI), data0=fl(RHO0), data1=fl(ZI), initial=0.0,
                                                op0=ALU.mult, op1=ALU.add), [RHO0, ZI], [XZI])
        yield
        cs, sn = R3(COS[:, :, :]), R3(SIN[:, :, :])
        ua, ub = up_[0].next(), up_[1].next()
        V("dve", lambda e: e.tensor_tensor(out=ua[:, :, :], in0=XZR[:, :, :], in1=cs, op=ALU.mult), [XZR, COS], [ua])
        V("pool", lambda e: e.tensor_tensor(out=ub[:, :, :], in0=XZI[:, :, :], in1=sn, op=ALU.mult), [XZI, SIN], [ub])
        V("dve", lambda e: e.tensor_tensor(out=XR[:, :, :], in0=ua[:, :, :], in1=ub[:, :, :], op=ALU.subtract), [ua, ub], [XR])
        V("pool", lambda e: e.tensor_tensor(out=xlr[:, :], in0=ua[:, :, last], in1=ub[:, :, last], op=ALU.subtract), [ua, ub], [xlr])
        yield
        V("pool", lambda e: e.tensor_tensor(out=ua[:, :, :], in0=XZR[:, :, :], in1=sn, op=ALU.mult), [XZR, SIN], [ua])
        V("dve", lambda e: e.tensor_tensor(out=ub[:, :, :], in0=XZI[:, :, :], in1=cs, op=ALU.mult), [XZI, COS], [ub])
        V("pool", lambda e: e.tensor_tensor(out=XI[:, :, :], in0=ua[:, :, :], in1=ub[:, :, :], op=ALU.add), [ua, ub], [XI])
        V("pool", lambda e: e.tensor_tensor(out=xli[:, :], in0=ua[:, :, last], in1=ub[:, :, last], op=ALU.add), [ua, ub], [xli])
        yield
        PY = PS2.next()
        for cc in range(4):
            for jj in range(4):
                j = cc * 4 + jj
                kb.op("pe", lambda e: e.matmul(PY[:, cc * 128:(cc + 1) * 128], lhsT=CR[:, j, :], rhs=XR[:, j, :],
                                               start=(jj == 0), stop=False), r=[CR, XR], w=[PY])
                kb.op("pe", lambda e: e.matmul(PY[:, cc * 128:(cc + 1) * 128], lhsT=CIn[:, j, :], rhs=XI[:, j, :],
                                               start=False, stop=(jj == 3)), r=[CIn, XI], w=[PY])
        pyv = PY[:, 0:512].rearrange("p (a b) -> p a b", a=4)
        if d == 1:
            y = yo.next()
            V("act", lambda e: e.activation(out=y[:, :, :], in_=pyv, func=AF.Copy), [PY], [y])
            kb.dma("act", ybv[:, :, t0:t0 + 128], y[:, :, :], y, r=[y], w=[("dram", id(ysbS))])
        else:
            yb, usf = ybp.next(), usfp.next()
            kb.dma("sp", yb[:, :, :], ybv[:, :, t0:t0 + 128], yb, r=[("dram", id(ysbS))], w=[yb])
            kb.dma("sp", usf[:, :, :], usv[:, :, t0:t0 + 128], usf, r=[("dram", id(usS))], w=[usf])
            y = yo.next()
            V("dve", lambda e: e.tensor_tensor(out=y[:, :, :], in0=pyv, in1=yb[:, :, :], op=ALU.add), [PY, yb], [y])
            V("pool", lambda e: e.tensor_tensor(out=usf[:, :, :], in0=usf[:, :, :],
                                                in1=dv[:, :].unsqueeze(2).broadcast_to([128, 4, 128]), op=ALU.mult),
              [usf, dv], [usf])
            V("pool", lambda e: e.tensor_tensor(out=y[:, :, :], in0=y[:, :, :], in1=usf[:, :, :], op=ALU.add), [y, usf], [y])
            g1 = yb
            V("pool", lambda e: e.tensor_tensor(out=g1[:, :, :], in0=y[:, :, :], in1=y[:, :, :], op=ALU.mult), [y], [g1])
            V("dve", lambda e: e.tensor_scalar(out=g1[:, :, :], in0=g1[:, :, :], scalar1=0.044715, scalar2=1.0,
                                               op0=ALU.mult, op1=ALU.add), [g1], [g1])
            V("dve", lambda e: e.tensor_tensor(out=g1[:, :, :], in0=g1[:, :, :], in1=y[:, :, :], op=ALU.mult), [g1, y], [g1])
            V("act", lambda e: e.activation(out=g1[:, :, :], in_=g1[:, :, :], func=AF.Sigmoid, scale=1.5957691216057308),
              [g1], [g1])
            V("dve", lambda e: e.tensor_tensor(out=y[:, :, :], in0=y[:, :, :], in1=g1[:, :, :], op=ALU.mult), [y, g1], [y])
            geb = us
            V("act", lambda e: e.activation(out=geb[:, :, :], in_=y[:, :, :], func=AF.Copy), [y], [geb])
            PG = PS2.next()
            for oc in range(4):
                for kc in range(4):
                    kb.op("pe", lambda e: e.matmul(PG[:, oc * 128:(oc + 1) * 128], lhsT=gw[:, kc, oc * 128:(oc + 1) * 128],
                                                   rhs=geb[:, kc, :], start=(kc == 0), stop=(kc == 3)), r=[gw, geb], w=[PG])
                V("act", lambda e: e.activation(out=usf[:, oc, :], in_=PG[:, oc * 128:(oc + 1) * 128], func=AF.Sigmoid,
                                                bias=gb[:, oc:oc + 1]), [PG, gb], [usf])
            yso = usp.next()
            V("dve", lambda e: e.tensor_tensor(out=yso[:, :, :], in0=y[:, :, :], in1=usf[:, :, :], op=ALU.mult), [y, usf], [yso])
            kb.dma("sp", ysv[:, :, t0:t0 + 128], yso[:, :, :], yso, r=[yso], w=[("dram", id(ysS))])


Model.s5_declare = _s5_declare
Model.sincos = _sincos
Model.s5_dir = _s5_dir


def host_inputs_s5(inp):
    L = DEPTH
    d = {}

    def drive(a):
        a = a.reshape(L, 2, 4, 8, 1, 64)
        a = np.broadcast_to(a, (L, 2, 4, 8, 16, 64))
        return np.ascontiguousarray(a.transpose(0, 1, 3, 4, 2, 5).reshape(L, 2, 128, 4, 64))
    d["s5_are"] = drive(inp["s5_a_re"])
    d["s5_aim"] = drive(inp["s5_a_im"])
    lsd = inp["s5_log_step"].reshape(L, 2, 4, 8, 1)
    d["s5_ls"] = np.ascontiguousarray(np.broadcast_to(lsd, (L, 2, 4, 8, 16)).transpose(0, 1, 3, 4, 2).reshape(L, 2, 128, 4))

    def bT(b):
        b = b.reshape(L, 4, 8, 64, 16)
        return np.ascontiguousarray(b.transpose(0, 2, 4, 1, 3).reshape(L, 128, 4, 64))
    d["s5_brT"] = bT(inp["s5_b_re"])
    d["s5_biT"] = bT(inp["s5_b_im"])

    def st2(a):
        a = a.reshape(L, 2, 16, 2, 64)
        return np.ascontiguousarray(a.transpose(0, 1, 3, 4, 2).reshape(L, 2, 128, 16))
    d["s5_are2"] = st2(inp["s5_a_re"])
    d["s5_aim2"] = st2(inp["s5_a_im"])
    ls2 = np.broadcast_to(inp["s5_log_step"].reshape(L, 2, 16, 2, 1), (L, 2, 16, 2, 64))
    d["s5_ls2"] = np.ascontiguousarray(ls2.transpose(0, 1, 3, 4, 2).reshape(L, 2, 128, 16))

    def cT(c):
        c = c.reshape(L, 16, 2, 16, 64)
        return np.ascontiguousarray(c.transpose(0, 2, 4, 1, 3).reshape(L, 128, 16, 16))
    d["s5_crT"] = cT(inp["s5_c_re"])
    d["s5_ciT"] = cT(inp["s5_c_im"])
    k = np.arange(128)[:, None, None] // 16
    j = np.arange(16)[None, :, None]
    gp = np.arange(2)[None, None, :]
    d["s5_rowmask"] = np.ascontiguousarray((k == 2 * (j % 4) + gp).astype(np.float32))
    d["s5_dT"] = np.ascontiguousarray(inp["s5_d"].reshape(L * 4, 128).T)
    d["s5_glub"] = np.ascontiguousarray(inp["s5_glu_b"].reshape(L * 4, 128).T)
    d["s5_gluw"] = inp["s5_glu_w"]
    d["tauT"] = np.ascontiguousarray(np.broadcast_to(np.arange(128, dtype=np.float32)[None, :], (128, 128)))
    return d


NCH_B = 15


def _rwkv_declare(self):
    L = DEPTH
    self.w_inB = self.din("w_inB", [L, D, NCH_B * 128])
    self.rk_mu = self.din("rk_mu", [128, L * NCH_B])
    self.rk_w0 = self.din("rk_w0", [128, L * 8])
    for n in ("a0", "kk", "ka", "rk", "gng", "gnb"):
        setattr(self, "rk_" + n, self.din("rk_" + n, [128, L * 4]))
    self.rk_w2 = self.din("rk_w2", [L, 128, 512])
    self.rk_a2 = self.din("rk_a2", [L, 64, 512])
    self.rk_g2 = self.din("rk_g2", [L, 128, 512])
    self.bd64 = self.din("bd64", [128, 128])
    self.identb = self.din("identb", [128, 128])
    self.ones0 = self.din("ones0", [2, 128, 128])
    self.rk_MT = self.din("rk_MT", [2, 128, 512])
    self.rk_MN = self.din("rk_MN", [2, 128, 512])


def _rwkv_prep_phase(self, l, src, S):
    kb = self.kb
    W = 256
    Wh = W + 2
    NB = W // 128
    with kb.phase() as st:
        self.psum = Pool(kb, "ps", 8, [128, 512], F32, psum=True, stack=st)
        V = lambda e_, f, r, w: kb.op(e_, f, r=r, w=w)
        sbf = lambda n, shp, dt=F32: kb.sb(n, shp, dt, st)
        w = sbf("wB", [128, NCH, NCH_B * 128], BF16)
        for c in range(NCH):
            kb.dma("pool", w[:, c, :], self.w_inB[l, c * 128:(c + 1) * 128, :], w, w=[w])
        w2b, a2b, g2b = sbf("w2b", [128, 512], BF16), sbf("a2b", [64, 512], BF16), sbf("g2b", [128, 512], BF16)
        kb.dma("pool", w2b[:, :], self.rk_w2[l], w2b, w=[w2b])
        kb.dma("pool", a2b[:, :], self.rk_a2[l], a2b, w=[a2b])
        kb.dma("pool", g2b[:, :], self.rk_g2[l], g2b, w=[g2b])
        bd, idb = sbf("bd", [128, 128], BF16), sbf("idb", [128, 128], BF16)
        kb.dma("pool", bd[:, :], self.bd64[:, :], bd, w=[bd])
        kb.dma("pool", idb[:, :], self.identb[:, :], idb, w=[idb])
        on0 = sbf("on0", [128, 2, 128])
        kb.dma("sp", on0[:, :, :], self.ones0.rearrange("d p t -> p d t"), on0, w=[on0])
        ON = [sbf("ON%d" % d_, [128, 4 * (W // 128), 128]) for d_ in range(2)]
        for d_ in range(2):
            V("dve", lambda e: e.tensor_copy(out=ON[d_][:, :, :], in_=on0[:, d_:d_ + 1, :].broadcast_to([128, 4 * (W // 128), 128])),
              [on0], [ON[d_]])
        mu, omu, hmu = sbf("mu", [128, NCH_B]), sbf("omu", [128, NCH_B]), sbf("hmu", [128, NCH_B])
        kb.dma("sp", mu[:, :], self.rk_mu[:, l * NCH_B:(l + 1) * NCH_B], mu, w=[mu])
        V("dve", lambda e: e.tensor_scalar(out=omu[:, :], in0=mu[:, :], scalar1=-1.0, scalar2=1.0, op0=ALU.mult, op1=ALU.add), [mu], [omu])
        V("dve", lambda e: e.tensor_scalar(out=hmu[:, :], in0=mu[:, :], scalar1=0.5, scalar2=None, op0=ALU.mult), [mu], [hmu])
        w0 = sbf("w0", [128, 8])
        kb.dma("sp", w0[:, :], self.rk_w0[:, l * 8:(l + 1) * 8], w0, w=[w0])
        pv = {}
        for n in ("a0", "kk", "ka", "rk"):
            pv[n] = sbf("p_" + n, [128, 4])
            kb.dma("sp", pv[n][:, :], getattr(self, "rk_" + n)[:, l * 4:(l + 1) * 4], pv[n], w=[pv[n]])
        omka = sbf("omka", [128, 4])
        V("dve", lambda e: e.tensor_scalar(out=omka[:, :], in0=pv["ka"][:, :], scalar1=-1.0, scalar2=1.0, op0=ALU.mult, op1=ALU.add),
          [pv["ka"]], [omka])
        pl = self.stat_pools(Wh, st)
        xp = Pool(kb, "x", 2, [128, NCH, Wh], F32, stack=st)
        xn = sbf("xn", [128, NCH, Wh])
        u = sbf("u", [128, NCH, Wh], BF16)
        zcp = Pool(kb, "zc", 2, [128, Wh], F32, stack=st)
        tmpp = Pool(kb, "ztmp", 2, [128, W], F32, stack=st)
        Z = sbf("Z", [128, NCH_B, W])
        tw, sg, alb = sbf("tw", [128, W], BF16), sbf("sg", [128, W], BF16), sbf("alb", [64, W], BF16)
        LW = [sbf("LW%d" % d_, [128, 4, W]) for d_ in range(2)]
        A, KK, KM, Bv = sbf("A", [128, 4, W]), sbf("KK", [128, 4, W]), sbf("KM", [128, 4, W]), sbf("Bv", [128, 4, W])
        SQ = sbf("SQ", [128, 4, W], BF16)
        T1, T2 = sbf("T1", [128, 4, W]), sbf("T2", [128, 4, W])
        Gp = Pool(kb, "Go", 2, [128, 4, W], BF16, stack=st)
        Bop = Pool(kb, "Bo", 2, [128, 4, W], BF16, stack=st)
        Vb = sbf("Vb", [128, 4, W], BF16)
        tokp = Pool(kb, "tok", 3, [128, NB, 512], BF16, stack=st)
        Lc, Lr, Lq = sbf("Lc", [128, 4, W]), sbf("Lr", [128, 4, W]), sbf("Lq", [128, 4, W])
        E = sbf("E", [128, 4, W])
        outp = {n: Pool(kb, n, 2, [128, 4, W], BF16, stack=st) for n in ("RHO", "KAP", "BET", "KTI")}
        scp = Pool(kb, "sco", 2, [128, NB, 4, 3], F32, stack=st)
        lmn = sbf("lmn", [128, 4, NB])
        srcv = src.rearrange("(c p) t -> p c t", p=128)
        fmv = lambda t_: t_.rearrange("(c p) t -> p c t", p=128)

        def transpose_store(srcb, dst, t0):
            tk = tokp.next()
            for tb in range(NB):
                P = self.psum.next()
                pb = P[:, 0:256].bitcast(BF16)
                for c in range(4):
                    kb.op("pe", lambda e: e.transpose(pb[:, c * 128:(c + 1) * 128], srcb[:, c, tb * 128:(tb + 1) * 128], idb[:, :]),
                          r=[srcb, idb], w=[P])
                V("act", lambda e: e.activation(out=tk[:, tb, :], in_=pb[:, 0:512], func=AF.Copy), [P], [tk])
            kb.dma("act", dst[t0:t0 + W, :].rearrange("(b p) n -> p b n", p=128), tk[:, :, :], tk, r=[tk], w=[("dram", id(dst))])

        for (seg, t0, _) in self.tiles(W):
            seg_lo, seg_hi = (0, CTX) if seg == 1 else (CTX, self.NT)
            lo, hi = max(t0 - 1, seg_lo), min(t0 + W + 1, seg_hi)
            x = xp.next()
            c0 = lo - (t0 - 1)
            if c0 > 0:
                V("dve", lambda e: e.memset(x[:, :, 0:1], 0.0), [], [x])
            if hi < t0 + W + 1:
                V("dve", lambda e: e.memset(x[:, :, Wh - 1:Wh], 0.0), [], [x])
            kb.dma("sp", x[:, :, c0:c0 + (hi - lo)], srcv[:, :, lo:hi], x, r=[("dram", id(src))], w=[x])
            rstd, nmr = self.ln_stats_w(x, Wh, pl)
            self.normalize(xn, x, Wh, rstd, nmr)
            for c in range(NCH):
                V("act", lambda e: e.activation(out=u[:, c, :], in_=xn[:, c, :], func=AF.Identity,
                                                scale=self.modp1[:, 4 * NCH + c, seg:seg + 1],
                                                bias=self.mods[:, 3 * NCH + c, seg:seg + 1]), [xn, self.modp1, self.mods], [u])
            if c0 > 0:
                V("dve", lambda e: e.memset(u[:, :, 0:1], 0.0), [], [u])
            if hi < t0 + W + 1:
                V("dve", lambda e: e.memset(u[:, :, Wh - 1:Wh], 0.0), [], [u])
            for ch in range(NCH_B):
                P = self.psum.next()
                for c in range(NCH):
                    kb.op("pe", lambda e: e.matmul(P[:, 0:Wh], lhsT=w[:, c, ch * 128:(ch + 1) * 128], rhs=u[:, c, :],
                                                   start=(c == 0), stop=(c == NCH - 1)), r=[w, u], w=[P])
                zc, tm = zcp.next(), tmpp.next()
                V("act", lambda e: e.activation(out=zc[:, :], in_=P[:, 0:Wh], func=AF.Copy), [P], [zc])
                V("dve", lambda e: e.tensor_tensor(out=tm[:, :], in0=zc[:, 0:W], in1=zc[:, 2:W + 2], op=ALU.add), [zc], [tm])
                V("act", lambda e: e.activation(out=tm[:, :], in_=tm[:, :], func=AF.Identity, scale=hmu[:, ch:ch + 1]), [tm, hmu], [tm])
                V("dve", lambda e: e.scalar_tensor_tensor(out=Z[:, ch, :], in0=zc[:, 1:W + 1], scalar=omu[:, ch:ch + 1],
                                                          in1=tm[:, :], op0=ALU.mult, op1=ALU.add), [zc, omu, tm], [Z])
            R_, K_, V_ = Z[:, 0:4, :], Z[:, 4:8, :], Z[:, 8:12, :]
            V("act", lambda e: e.activation(out=tw[:, :], in_=Z[:, 12, :], func=AF.Tanh), [Z], [tw])
            V("act", lambda e: e.activation(out=sg[:, :], in_=Z[:, 13, :], func=AF.Sigmoid), [Z], [sg])
            V("act", lambda e: e.activation(out=alb[:, :], in_=Z[0:64, 14, :], func=AF.Copy), [Z], [alb])
            for d_ in range(2):
                for c in range(4):
                    P = self.psum.next()
                    kb.op("pe", lambda e: e.matmul(P[:, 0:W], lhsT=w2b[d_ * 64:(d_ + 1) * 64, c * 128:(c + 1) * 128],
                                                   rhs=tw[d_ * 64:(d_ + 1) * 64, :], start=True, stop=True), r=[w2b, tw], w=[P])
                    V("act", lambda e: e.activation(out=LW[d_][:, c, :], in_=P[:, 0:W], func=AF.Sigmoid,
                                                    bias=w0[:, d_ * 4 + c:d_ * 4 + c + 1]), [P, w0], [LW[d_]])
                V("pool", lambda e: e.tensor_scalar(out=LW[d_][:, :, :], in0=LW[d_][:, :, :], scalar1=-DECAY_SCALE, scalar2=None,
                                                    op0=ALU.mult), [LW[d_]], [LW[d_]])
            Go = Gp.next()
            for c in range(4):
                P = self.psum.next()
                kb.op("pe", lambda e: e.matmul(P[:, 0:W], lhsT=a2b[0:64, c * 128:(c + 1) * 128], rhs=alb[0:64, :],
                                               start=True, stop=True), r=[a2b, alb], w=[P])
                V("act", lambda e: e.activation(out=A[:, c, :], in_=P[:, 0:W], func=AF.Sigmoid, bias=pv["a0"][:, c:c + 1]),
                  [P, pv["a0"]], [A])
                P = self.psum.next()
                kb.op("pe", lambda e: e.matmul(P[:, 0:W], lhsT=g2b[:, c * 128:(c + 1) * 128], rhs=sg[:, :],
                                               start=True, stop=True), r=[g2b, sg], w=[P])
                V("act", lambda e: e.activation(out=Go[:, c, :], in_=P[:, 0:W], func=AF.Copy), [P], [Go])
            kb.dma("act", fmv(S["gR"])[:, :, t0:t0 + W], Go[:, :, :], Go, r=[Go], w=[("dram", id(S["gR"]))])
            for c in range(4):
                V("dve", lambda e: e.tensor_scalar(out=KK[:, c, :], in0=Z[:, 4 + c, :], scalar1=pv["kk"][:, c:c + 1], scalar2=None,
                                                    op0=ALU.mult), [Z, pv["kk"]], [KK])
            V("act", lambda e: e.activation(out=SQ[:, :, :], in_=KK[:, :, :], func=AF.Square), [KK], [SQ])
            for c in range(4):
                P = self.psum.next()
                kb.op("pe", lambda e: e.matmul(P[:, 0:W], lhsT=bd[:, :], rhs=SQ[:, c, :], start=True, stop=True), r=[bd, SQ], w=[P])
                V("dve", lambda e: e.tensor_scalar(out=T1[:, c, :], in0=P[:, 0:W], scalar1=1e-12, scalar2=None, op0=ALU.add), [P], [T1])
            V("act", lambda e: e.activation(out=T1[:, :, :], in_=T1[:, :, :], func=AF.Sqrt), [T1], [T1])
            V("dve", lambda e: e.reciprocal(out=T1[:, :, :], in_=T1[:, :, :]), [T1], [T1])
            V("dve", lambda e: e.tensor_tensor(out=KK[:, :, :], in0=KK[:, :, :], in1=T1[:, :, :], op=ALU.mult), [KK, T1], [KK])
            for c in range(4):
                V("dve", lambda e: e.tensor_scalar(out=T2[:, c, :], in0=A[:, c, :], scalar1=pv["ka"][:, c:c + 1],
                                                   scalar2=omka[:, c:c + 1], op0=ALU.mult, op1=ALU.add), [A, pv["ka"], omka], [T2])
            V("dve", lambda e: e.tensor_tensor(out=KM[:, :, :], in0=K_, in1=T2[:, :, :], op=ALU.mult), [Z, T2], [KM])
            V("pool", lambda e: e.tensor_tensor(out=Bv[:, :, :], in0=KK[:, :, :], in1=A[:, :, :], op=ALU.mult), [KK, A], [Bv])
            V("dve", lambda e: e.tensor_tensor(out=T1[:, :, :], in0=R_, in1=KM[:, :, :], op=ALU.mult), [Z, KM], [T1])
            for c in range(4):
                V("act", lambda e: e.activation(out=SQ[:, c, :], in_=T1[:, c, :], func=AF.Identity, scale=pv["rk"][:, c:c + 1]),
                  [T1, pv["rk"]], [SQ])
            Bo = Bop.next()
            for c in range(4):
                P = self.psum.next()
                kb.op("pe", lambda e: e.matmul(P[:, 0:W], lhsT=bd[:, :], rhs=SQ[:, c, :], start=True, stop=True), r=[bd, SQ], w=[P])
                V("dve", lambda e: e.tensor_tensor(out=Bo[:, c, :], in0=P[:, 0:W], in1=Z[:, 8 + c, :], op=ALU.mult), [P, Z], [Bo])
            kb.dma("sp", fmv(S["bon"])[:, :, t0:t0 + W], Bo[:, :, :], Bo, r=[Bo], w=[("dram", id(S["bon"]))])
            V("act", lambda e: e.activation(out=Vb[:, :, :], in_=V_, func=AF.Copy), [Z], [Vb])
            transpose_store(Vb, S["vt"], t0)
            for d_ in range(2):
                rev = d_ == 1
                fl = (lambda ap: ap.rearrange("p a b -> p (a b)")[:, ::-1]) if rev else (lambda ap: ap.rearrange("p a b -> p (a b)"))
                V("dve", lambda e: e.tensor_tensor_scan(out=fl(Lc[:, :, :]), data0=fl(ON[d_][:, :, :]), data1=fl(LW[d_][:, :, :]),
                                                        initial=0.0, op0=ALU.mult, op1=ALU.add), [ON[d_], LW[d_]], [Lc])
                mid, last = (64, 0) if rev else (63, 127)
                L4 = Lc[:, :, :].rearrange("p c (b t) -> p c b t", b=NB)
                sc = scp.next()
                V("dve", lambda e: e.tensor_copy(out=lmn[:, :, :], in_=L4[:, :, :, mid]), [Lc], [lmn])
                scv = sc[:, :, :, :].rearrange("p b c s -> p c b s")
                V("act", lambda e: e.activation(out=scv[:, :, :, 0], in_=lmn[:, :, :], func=AF.Exp), [lmn], [sc])
                V("act", lambda e: e.activation(out=scv[:, :, :, 2], in_=L4[:, :, :, last], func=AF.Exp), [Lc], [sc])
                V("dve", lambda e: e.tensor_tensor(out=scv[:, :, :, 1], in0=L4[:, :, :, last], in1=lmn[:, :, :], op=ALU.subtract),
                  [Lc, lmn], [sc])
                V("act", lambda e: e.activation(out=scv[:, :, :, 1], in_=scv[:, :, :, 1], func=AF.Exp), [sc], [sc])
                kb.dma("act", S["sc"][d_][:, t0 // 128:t0 // 128 + NB, :, :], sc[:, :, :, :], sc, r=[sc], w=[("dram", id(S["sc"][d_]))])
                Lr4 = Lr[:, :, :].rearrange("p c (b t) -> p c b t", b=NB)
                V("pool", lambda e: e.tensor_tensor(out=Lr4, in0=L4, in1=lmn[:, :, :].unsqueeze(3).broadcast_to([128, 4, NB, 128]),
                                                    op=ALU.subtract), [Lc, lmn], [Lr])
                V("dve", lambda e: e.tensor_tensor(out=Lq[:, :, :], in0=Lr[:, :, :], in1=LW[d_][:, :, :], op=ALU.subtract),
                  [Lr, LW[d_]], [Lq])
                o = {n: outp[n].next() for n in outp}
                V("act", lambda e: e.activation(out=E[:, :, :], in_=Lr[:, :, :], func=AF.Exp), [Lr], [E])
                V("dve", lambda e: e.tensor_tensor(out=o["RHO"][:, :, :], in0=R_, in1=E[:, :, :], op=ALU.mult), [Z, E], [o["RHO"]])
                V("act", lambda e: e.activation(out=E[:, :, :], in_=Lq[:, :, :], func=AF.Exp), [Lq], [E])
                V("dve", lambda e: e.tensor_tensor(out=o["KAP"][:, :, :], in0=KK[:, :, :], in1=E[:, :, :], op=ALU.mult), [KK, E], [o["KAP"]])
                V("act", lambda e: e.activation(out=E[:, :, :], in_=Lr[:, :, :], func=AF.Exp, scale=-1.0), [Lr], [E])
                V("dve", lambda e: e.tensor_tensor(out=o["BET"][:, :, :], in0=Bv[:, :, :], in1=E[:, :, :], op=ALU.mult), [Bv, E], [o["BET"]])
                V("pool", lambda e: e.tensor_tensor(out=o["KTI"][:, :, :], in0=KM[:, :, :], in1=E[:, :, :], op=ALU.mult), [KM, E], [o["KTI"]])
                for n in ("RHO", "KAP", "BET", "KTI"):
                    kb.dma("sp", fmv(S[n][d_])[:, :, t0:t0 + W], o[n][:, :, :], o[n], r=[o[n]], w=[("dram", id(S[n][d_]))])
                transpose_store(o["BET"], S["bt"][d_], t0)
                transpose_store(o["KTI"], S["kt"][d_], t0)


def _ln_stats_w(self, x, Wc, pl):
    kb = self.kb
    xb, sq = pl["xb"].next(), pl["sq"].next()
    kb.op("act", lambda e: e.activation(out=xb[:, :, :Wc], in_=x[:, :, :Wc], func=AF.Copy), r=[x], w=[xb])
    kb.op("act", lambda e: e.activation(out=sq[:, :, :Wc], in_=x[:, :, :Wc], func=AF.Square), r=[x], w=[sq])
    P1, P2 = self.psum.next(), self.psum.next()
    for c in range(NCH):
        kb.op("pe", lambda e: e.matmul(P1[:, 0:Wc], lhsT=self.onesb[:, :], rhs=xb[:, c, :Wc],
                                       start=(c == 0), stop=(c == NCH - 1)), r=[xb, self.onesb], w=[P1])
    for c in range(NCH):
        kb.op("pe", lambda e: e.matmul(P2[:, 0:Wc], lhsT=self.onesb[:, :], rhs=sq[:, c, :Wc],
                                       start=(c == 0), stop=(c == NCH - 1)), r=[sq, self.onesb], w=[P2])
    m2, var, rstd, nmr = pl["m2"].next(), pl["var"].next(), pl["rstd"].next(), pl["nmr"].next()
    kb.op("act", lambda e: e.activation(out=m2[:, 0, :Wc], in_=P1[:, 0:Wc], func=AF.Square), r=[P1], w=[m2])
    kb.op("dve", lambda e: e.scalar_tensor_tensor(out=var[:, 0, :Wc], in0=P2[:, 0:Wc], scalar=LN_EPS,
                                                  in1=m2[:, 0, :Wc], op0=ALU.add, op1=ALU.subtract), r=[P2, m2], w=[var])
    kb.op("act", lambda e: e.activation(out=var[:, 0, :Wc], in_=var[:, 0, :Wc], func=AF.Sqrt), r=[var], w=[var])
    kb.op("dve", lambda e: e.reciprocal(out=rstd[:, 0, :Wc], in_=var[:, 0, :Wc]), r=[var], w=[rstd])
    kb.op("dve", lambda e: e.scalar_tensor_tensor(out=nmr[:, 0, :Wc], in0=P1[:, 0:Wc], scalar=-1.0,
                                                  in1=rstd[:, 0, :Wc], op0=ALU.mult, op1=ALU.mult), r=[P1, rstd], w=[nmr])
    return rstd, nmr


Model.rwkv_declare = _rwkv_declare
Model.rwkv_prep_phase = _rwkv_prep_phase
Model.ln_stats_w = _ln_stats_w


def _rwkv_scan_dir(self, l, d, S, st, psr):
    kb = self.kb
    NT = self.NT
    nchunk = NT // 128
    fmv = lambda t_: t_.rearrange("(c p) t -> p c t", p=128)
    V = lambda e_, f, r, w: kb.op(e_, f, r=r, w=w)
    sbf = lambda n, shp, dt=F32: kb.sb(n, shp, dt, st)
    MT, MN = sbf("MT", [128, 512], BF16), sbf("MN", [128, 512], BF16)
    kb.dma("pool", MT[:, :], self.rk_MT[d], MT, w=[MT])
    kb.dma("pool", MN[:, :], self.rk_MN[d], MN, w=[MN])
    KRp = Pool(kb, "KR", 2, [128, 4, 2, 128], BF16, stack=st)
    BTZp = Pool(kb, "BTZ", 2, [128, 4, 2, 128], BF16, stack=st)
    KTZp = Pool(kb, "KTZ", 2, [128, 4, 2, 128], BF16, stack=st)
    KAZp = Pool(kb, "KAZ", 2, [128, 4, 2, 128], BF16, stack=st)
    for p_ in (BTZp, KTZp, KAZp):
        for b_ in p_.bufs:
            V("pool", lambda e: e.memset(b_[:, :, :, :], 0.0), [], [b_])
    S0Z = sbf("S0Z", [128, 4, 2, 64], BF16)
    V("pool", lambda e: e.memset(S0Z[:, :, :, :], 0.0), [], [S0Z])
    St = sbf("St", [128, 4, 64])
    V("pool", lambda e: e.memset(St[:, :, :], 0.0), [], [St])
    St1 = sbf("St1", [128, 4, 64])
    tokp = {n: Pool(kb, n, 2, [128, 512], BF16, stack=st) for n in ("Btok", "Ktok", "Vtok")}
    scp = Pool(kb, "sc", 2, [128, 4, 3], F32, stack=st)
    AMp = Pool(kb, "AM", 2, [128, 8, 512], BF16, stack=st)
    Pm = [Pool(kb, "Pm%d" % i, 2, [128, 8, 128], BF16, stack=st) for i in range(2)]
    PTm = Pool(kb, "PTm", 2, [128, 8, 128], BF16, stack=st)
    X32, X16p = sbf("X32", [128, 512]), Pool(kb, "X16", 2, [128, 512], BF16, stack=st)
    Oop = Pool(kb, "Oo", 2, [128, 512], F32, stack=st)
    order = [0, 1] + list(range(2, nchunk))
    if d == 1:
        order = [1, 0] + list(range(nchunk - 1, 1, -1))
    ev = 0
    yield
    for ci_, ch in enumerate(order):
        if ci_ > 0:
            yield
        t0 = ch * 128
        KR, BTZ, KTZ, KAZ = KRp.next(), BTZp.next(), KTZp.next(), KAZp.next()
        kb.dma("sp", KR[:, :, 0, :], fmv(S["KAP"][d])[:, :, t0:t0 + 128], KR, r=[("dram", id(S["KAP"][d]))], w=[KR])
        kb.dma("sp", KR[:, :, 1, :], fmv(S["RHO"][d])[:, :, t0:t0 + 128], KR, r=[("dram", id(S["RHO"][d]))], w=[KR])
        for par in range(2):
            ps_ = slice(par * 64, (par + 1) * 64)
            kb.dma("sp", BTZ[ps_, :, par, :], fmv(S["BET"][d])[ps_, :, t0:t0 + 128], BTZ, r=[("dram", id(S["BET"][d]))], w=[BTZ])
            kb.dma("sp", KTZ[ps_, :, par, :], fmv(S["KTI"][d])[ps_, :, t0:t0 + 128], KTZ, r=[("dram", id(S["KTI"][d]))], w=[KTZ])
            kb.dma("sp", KAZ[ps_, :, par, :], fmv(S["KAP"][d])[ps_, :, t0:t0 + 128], KAZ, r=[("dram", id(S["KAP"][d]))], w=[KAZ])
        tk = {}
        for n, key in (("Btok", "bt"), ("Ktok", "kt"), ("Vtok", "vt")):
            tk[n] = tokp[n].next()
            srcd = S[key][d] if key != "vt" else S[key]
            kb.dma("sp", tk[n][:, :], srcd[t0:t0 + 128, :], tk[n], r=[("dram", id(srcd))], w=[tk[n]])
        Btok, Ktok, Vtok = tk["Btok"], tk["Ktok"], tk["Vtok"]
        sc = scp.next()
        kb.dma("sp", sc[:, :, :], S["sc"][d][:, ch, :, :], sc, r=[("dram", id(S["sc"][d]))], w=[sc])
        for par in range(2):
            ps_ = slice(par * 64, (par + 1) * 64)
            V("pool", lambda e: e.tensor_tensor(out=S0Z[ps_, :, par, :], in0=St[ps_, :, :],
                                                in1=sc[ps_, :, 0:1].broadcast_to([64, 4, 64]), op=ALU.mult), [St, sc], [S0Z])
        AM = AMp.next()
        for h in range(8):
            c, par = h // 2, h % 2
            P = psr.next()
            rhs = KR[:, c, :, :].rearrange("p s t -> p (s t)")
            kb.op("pe", lambda e: e.matmul(P[:, 0:256], lhsT=BTZ[:, c, par, :], rhs=rhs, start=True, stop=True), r=[BTZ, KR], w=[P])
            kb.op("pe", lambda e: e.matmul(P[:, 256:512], lhsT=KTZ[:, c, par, :], rhs=rhs, start=True, stop=True), r=[KTZ, KR], w=[P])
            V("dve", lambda e: e.tensor_tensor(out=AM[:, h, :], in0=P[:, :], in1=MT[:, :], op=ALU.mult), [P, MT], [AM])
            if h == 3:
                yield
        yield
        Pj, PTj = Pm[0].next(), PTm.next()
        for g4 in range(2):
            P = psr.next()
            for hh in range(4):
                h = g4 * 4 + hh
                c, par = h // 2, h % 2
                kb.op("pe", lambda e: e.matmul(P[:, hh * 128:(hh + 1) * 128], lhsT=KAZ[:, c, par, :], rhs=BTZ[:, c, par, :],
                                               start=True, stop=True), r=[KAZ, BTZ], w=[P])
            V("dve", lambda e: e.tensor_tensor(out=Pj[:, g4 * 4:(g4 + 1) * 4, :].rearrange("p h t -> p (h t)"), in0=P[:, :],
                                               in1=MN[:, :], op=ALU.mult), [P, MN], [Pj])
        V("act", lambda e: e.activation(out=PTj[:, :, :], in_=AM[:, :, 0:128], func=AF.Copy), [AM], [PTj])
        P = psr.next()
        for h in range(8):
            c, par = h // 2, h % 2
            kb.op("pe", lambda e: e.matmul(P[:, h * 64:(h + 1) * 64], lhsT=KR[:, c, 0, :], rhs=S0Z[:, c, par, :],
                                           start=True, stop=False), r=[KR, S0Z], w=[P])
            kb.op("pe", lambda e: e.matmul(P[:, h * 64:(h + 1) * 64], lhsT=AM[:, h, 256:384], rhs=Vtok[:, h * 64:(h + 1) * 64],
                                           start=False, stop=True), r=[AM, Vtok], w=[P])
        V("act", lambda e: e.activation(out=X32[:, :], in_=P[:, :], func=AF.Identity, scale=-1.0), [P], [X32])
        X16 = X16p.next()
        V("act", lambda e: e.activation(out=X16[:, :], in_=X32[:, :], func=AF.Copy), [X32], [X16])
        yield
        for j in range(7):
            P = psr.next()
            for h in range(8):
                kb.op("pe", lambda e: e.matmul(P[:, h * 64:(h + 1) * 64], lhsT=PTj[:, h, :], rhs=X16[:, h * 64:(h + 1) * 64],
                                               start=True, stop=True), r=[PTj, X16], w=[P])
            V("dve", lambda e: e.tensor_tensor(out=X32[:, :], in0=P[:, :], in1=X32[:, :], op=ALU.add), [P, X32], [X32])
            X16 = X16p.next()
            V("act", lambda e: e.activation(out=X16[:, :], in_=X32[:, :], func=AF.Copy), [X32], [X16])
            if j < 6:
                PTn = PTm.next()
                Pn = Pm[(j + 1) % 2].next() if j < 5 else None
                for g4 in range(2):
                    P = psr.next()
                    for hh in range(4):
                        h = g4 * 4 + hh
                        kb.op("pe", lambda e: e.matmul(P[:, hh * 128:(hh + 1) * 128], lhsT=Pj[:, h, :], rhs=PTj[:, h, :],
                                                       start=True, stop=True), r=[Pj, PTj], w=[P])
                    eng = "act"
                    ev += 1
                    dst = PTn[:, g4 * 4:(g4 + 1) * 4, :].rearrange("p h t -> p (h t)")
                    if eng == "act":
                        V("act", lambda e: e.activation(out=dst, in_=P[:, :], func=AF.Copy), [P], [PTn])
                    else:
                        V("dve", lambda e: e.tensor_copy(out=dst, in_=P[:, :]), [P], [PTn])
                    if Pn is not None:
                        P = psr.next()
                        for hh in range(4):
                            h = g4 * 4 + hh
                            kb.op("pe", lambda e: e.matmul(P[:, hh * 128:(hh + 1) * 128], lhsT=PTj[:, h, :], rhs=Pj[:, h, :],
                                                           start=True, stop=True), r=[Pj, PTj], w=[P])
                        eng = "act"
                        ev += 1
                        dst = Pn[:, g4 * 4:(g4 + 1) * 4, :].rearrange("p h t -> p (h t)")
                        if eng == "act":
                            V("act", lambda e: e.activation(out=dst, in_=P[:, :], func=AF.Copy), [P], [Pn])
                        else:
                            V("dve", lambda e: e.tensor_copy(out=dst, in_=P[:, :]), [P], [Pn])
                PTj = PTn
                if Pn is not None:
                    Pj = Pn
            if j < 6:
                yield
        U16 = X16
        P = psr.next()
        for h in range(8):
            c, par = h // 2, h % 2
            hs = slice(h * 64, (h + 1) * 64)
            kb.op("pe", lambda e: e.matmul(P[:, hs], lhsT=KR[:, c, 1, :], rhs=S0Z[:, c, par, :], start=True, stop=False),
                  r=[KR, S0Z], w=[P])
            kb.op("pe", lambda e: e.matmul(P[:, hs], lhsT=AM[:, h, 128:256], rhs=U16[:, hs], start=False, stop=False),
                  r=[AM, U16], w=[P])
            kb.op("pe", lambda e: e.matmul(P[:, hs], lhsT=AM[:, h, 384:512], rhs=Vtok[:, hs], start=False, stop=True),
                  r=[AM, Vtok], w=[P])
        Oo = Oop.next()
        V("act", lambda e: e.activation(out=Oo[:, :], in_=P[:, :], func=AF.Copy), [P], [Oo])
        kb.dma("act", S["O"][d][t0:t0 + 128, :], Oo[:, :], Oo, r=[Oo], w=[("dram", id(S["O"][d]))])
        yield
        P = psr.next()
        for c in range(4):
            cs_ = slice(c * 128, (c + 1) * 128)
            kb.op("pe", lambda e: e.matmul(P[:, cs_], lhsT=Btok[:, cs_], rhs=U16[:, cs_], start=True, stop=False),
                  r=[Btok, U16], w=[P])
            kb.op("pe", lambda e: e.matmul(P[:, cs_], lhsT=Ktok[:, cs_], rhs=Vtok[:, cs_], start=False, stop=True),
                  r=[Ktok, Vtok], w=[P])
        V("pool", lambda e: e.tensor_tensor(out=St1[:, :, :], in0=St[:, :, :], in1=sc[:, :, 2:3].broadcast_to([128, 4, 64]),
                                            op=ALU.mult), [St, sc], [St1])
        pv_ = P[:, :].rearrange("p (c x) -> p c x", c=4)
        for par in range(2):
            ps_ = slice(par * 64, (par + 1) * 64)
            V("dve", lambda e: e.tensor_tensor(out=St[ps_, :, :], in0=pv_[ps_, :, par * 64:(par + 1) * 64],
                                               in1=sc[ps_, :, 1:2].broadcast_to([64, 4, 64]), op=ALU.mult), [P, sc, St1], [St])
        V("pool", lambda e: e.tensor_tensor(out=St[:, :, :], in0=St[:, :, :], in1=St1[:, :, :], op=ALU.add), [St, St1], [St])


def _merge_declare(self):
    L = DEPTH
    self.branch_proj = self.din("branch_proj", [L, 3, 512, D])
    self.w_out = self.din("w_out", [L, D, D])


def _merge_phase(self, l, src, dst, S, ctx):
    kb = self.kb
    W = 256
    NB = W // 128
    lni = (l * 3 + 1) * NCH
    with kb.phase() as st:
        self.psum = Pool(kb, "ps", 8, [128, 512], F32, psum=True, stack=st)
        V = lambda e_, f, r, w: kb.op(e_, f, r=r, w=w)
        sbf = lambda n, shp, dt=F32: kb.sb(n, shp, dt, st)
        bp = sbf("bp", [128, 12, D], BF16)
        wo = sbf("wo", [128, NCH, D], BF16)
        for b_ in range(3):
            for c in range(4):
                kb.dma("pool", bp[:, b_ * 4 + c, :], self.branch_proj[l, b_, c * 128:(c + 1) * 128, :], bp, w=[bp])
        for c in range(NCH):
            kb.dma("pool", wo[:, c, :], self.w_out[l, c * 128:(c + 1) * 128, :], wo, w=[wo])
        idb = sbf("idb", [128, 128], BF16)
        kb.dma("pool", idb[:, :], self.identb[:, :], idb, w=[idb])
        gng, gnb = sbf("gng", [128, 4]), sbf("gnb", [128, 4])
        kb.dma("sp", gng[:, :], self.rk_gng[:, l * 4:(l + 1) * 4], gng, w=[gng])
        kb.dma("sp", gnb[:, :], self.rk_gnb[:, l * 4:(l + 1) * 4], gnb, w=[gnb])
        pl = self.stat_pools(W, st)
        xp = Pool(kb, "x", 2, [128, NCH, W], F32, stack=st)
        xn = sbf("xn", [128, NCH, W])
        Ofp = Pool(kb, "Of", 2, [128, NB, 512], F32, stack=st)
        Obp = Pool(kb, "Ob", 2, [128, NB, 512], F32, stack=st)
        onb = sbf("onb", [128, NB, 512], BF16)
        st8 = [sbf("st8_%d" % i, [128, NB, 8]) for i in range(3)]
        sqt = sbf("sqt", [128, NB, 512])
        Y = {n: Pool(kb, "y" + n, 2, [128, 4, W], BF16, stack=st) for n in ("a", "s", "bon", "g")}
        yr = sbf("yr", [128, 4, W], BF16)
        yt = sbf("yrt", [128, 4, W])
        gp = Pool(kb, "gates", 2, [128, 24, W], BF16, stack=st)
        m1, m2, m3 = sbf("m1", [128, W]), sbf("m2", [128, W]), sbf("m3", [128, W])
        mT = sbf("mT", [128, NCH, W], BF16)
        srcv = src.rearrange("(c p) t -> p c t", p=128)
        dstv = dst.rearrange("(c p) t -> p c t", p=128)
        fmv = lambda t_: t_.rearrange("(c p) t -> p c t", p=128)
        for (seg, t0, _) in self.tiles(W, ctx=ctx):
            x = xp.next()
            kb.dma("sp", x[:, :, :], srcv[:, :, t0:t0 + W], x, r=[("dram", id(src))], w=[x])
            Of, Ob = Ofp.next(), Obp.next()
            kb.dma("sp", Of[:, :, :], S["O"][0][t0:t0 + W, :].rearrange("(b p) n -> p b n", p=128), Of, r=[("dram", id(S["O"][0]))], w=[Of])
            kb.dma("sp", Ob[:, :, :], S["O"][1][t0:t0 + W, :].rearrange("(b p) n -> p b n", p=128), Ob, r=[("dram", id(S["O"][1]))], w=[Ob])
            ld = {}
            for n, key in (("a", "ya"), ("s", "ys"), ("bon", "bon"), ("g", "gR")):
                ld[n] = Y[n].next()
                kb.dma("sp", ld[n][:, :, :], fmv(S[key])[:, :, t0:t0 + W], ld[n], r=[("dram", id(S[key]))], w=[ld[n]])
            gt = gp.next()
            kb.dma("sp", gt[:, :, :], fmv(S["gS"])[:, :, t0:t0 + W], gt, r=[("dram", id(S["gS"]))], w=[gt])
            V("dve", lambda e: e.tensor_tensor(out=Of[:, :, :], in0=Of[:, :, :], in1=Ob[:, :, :], op=ALU.add), [Of, Ob], [Of])
            O4 = Of[:, :, :].rearrange("p b (h v) -> p b h v", h=8)
            sm, vr, rs = st8
            V("dve", lambda e: e.tensor_reduce(out=sm[:, :, :], in_=O4, axis=AX.X, op=ALU.add), [Of], [sm])
            V("dve", lambda e: e.tensor_scalar(out=sm[:, :, :], in0=sm[:, :, :], scalar1=1.0 / 64, scalar2=None, op0=ALU.mult), [sm], [sm])
            V("dve", lambda e: e.tensor_tensor(out=O4, in0=O4, in1=sm[:, :, :].unsqueeze(3).broadcast_to([128, NB, 8, 64]),
                                               op=ALU.subtract), [Of, sm], [Of])
            V("act", lambda e: e.activation(out=sqt[:, :, :], in_=Of[:, :, :], func=AF.Square), [Of], [sqt])
            V("dve", lambda e: e.tensor_reduce(out=vr[:, :, :], in_=sqt[:, :, :].rearrange("p b (h v) -> p b h v", h=8), axis=AX.X,
                                               op=ALU.add), [sqt], [vr])
            V("dve", lambda e: e.tensor_scalar(out=vr[:, :, :], in0=vr[:, :, :], scalar1=1.0 / 64, scalar2=GN_EPS, op0=ALU.mult,
                                               op1=ALU.add), [vr], [vr])
            V("act", lambda e: e.activation(out=vr[:, :, :], in_=vr[:, :, :], func=AF.Sqrt), [vr], [vr])
            V("dve", lambda e: e.reciprocal(out=rs[:, :, :], in_=vr[:, :, :]), [vr], [rs])
            V("dve", lambda e: e.tensor_tensor(out=onb[:, :, :].rearrange("p b (h v) -> p b h v", h=8), in0=O4,
                                               in1=rs[:, :, :].unsqueeze(3).broadcast_to([128, NB, 8, 64]), op=ALU.mult), [Of, rs], [onb])
            for tb in range(NB):
                P = self.psum.next()
                pb = P[:, 0:256].bitcast(BF16)
                for c in range(4):
                    kb.op("pe", lambda e: e.transpose(pb[:, c * 128:(c + 1) * 128], onb[:, tb, c * 128:(c + 1) * 128], idb[:, :]),
                          r=[onb, idb], w=[P])
                for c in range(4):
                    V("act", lambda e: e.activation(out=yt[:, c, tb * 128:(tb + 1) * 128], in_=pb[:, c * 128:(c + 1) * 128],
                                                    func=AF.Identity, scale=gng[:, c:c + 1], bias=gnb[:, c:c + 1]), [P, gng, gnb], [yt])
            V("pool", lambda e: e.tensor_tensor(out=yt[:, :, :], in0=yt[:, :, :], in1=ld["bon"][:, :, :], op=ALU.add), [yt, ld["bon"]], [yt])
            V("dve", lambda e: e.tensor_tensor(out=yr[:, :, :], in0=yt[:, :, :], in1=ld["g"][:, :, :], op=ALU.mult), [yt, ld["g"]], [yr])
            if "yrS" in S:
                kb.dma("sp", fmv(S["yrS"])[:, :, t0:t0 + W], yr[:, :, :], yr, r=[yr], w=[("dram", id(S["yrS"]))])
            ysrc = [ld["a"], yr, ld["s"]]
            for oc in range(NCH):
                Pa, Pb = self.psum.next(), self.psum.next()
                tgt = [(Pa, 0), (Pa, W), (Pb, 0)]
                for b_ in range(3):
                    Pt, o0 = tgt[b_]
                    for kc in range(4):
                        kb.op("pe", lambda e: e.matmul(Pt[:, o0:o0 + W], lhsT=bp[:, b_ * 4 + kc, oc * 128:(oc + 1) * 128],
                                                       rhs=ysrc[b_][:, kc, :], start=(kc == 0), stop=(kc == 3)), r=[bp, ysrc[b_]], w=[Pt])
                V("dve", lambda e: e.tensor_tensor(out=m1[:, :], in0=Pa[:, 0:W], in1=gt[:, oc, :], op=ALU.mult), [Pa, gt], [m1])
                V("dve", lambda e: e.tensor_tensor(out=m2[:, :], in0=Pa[:, W:2 * W], in1=gt[:, 8 + oc, :], op=ALU.mult), [Pa, gt], [m2])
                V("dve", lambda e: e.tensor_tensor(out=m3[:, :], in0=Pb[:, 0:W], in1=gt[:, 16 + oc, :], op=ALU.mult), [Pb, gt], [m3])
                V("dve", lambda e: e.tensor_tensor(out=m1[:, :], in0=m1[:, :], in1=m2[:, :], op=ALU.add), [m1, m2], [m1])
                V("pool", lambda e: e.tensor_tensor(out=mT[:, oc, :], in0=m1[:, :], in1=m3[:, :], op=ALU.add), [m1, m3], [mT])
            V("pool", lambda e: e.tensor_scalar(out=x[:, :, :], in0=x[:, :, :], scalar1=ALPHA, scalar2=None, op0=ALU.mult), [x], [x])
            for oc in range(NCH):
                if oc % 2 == 0:
                    P = self.psum.next()
                o0 = (oc % 2) * W
                for kc in range(NCH):
                    kb.op("pe", lambda e: e.matmul(P[:, o0:o0 + W], lhsT=wo[:, kc, oc * 128:(oc + 1) * 128], rhs=mT[:, kc, :],
                                                   start=(kc == 0), stop=(kc == NCH - 1)), r=[wo, mT], w=[P])
                V("dve", lambda e: e.scalar_tensor_tensor(out=x[:, oc, :], in0=P[:, o0:o0 + W],
                                                          scalar=self.mods[:, 5 * NCH + oc, seg:seg + 1], in1=x[:, oc, :],
                                                          op0=ALU.mult, op1=ALU.add), [P, x, self.mods], [x])
            P = self.psum.next()
            rstd, nmr = self.ln_stats(x, W, P, pl)
            self.normalize(xn, x, W, rstd, nmr)
            for c in range(NCH):
                V("act", lambda e: e.activation(out=xn[:, c, :], in_=xn[:, c, :], func=AF.Identity,
                                                scale=self.lng[:, lni + c:lni + c + 1], bias=self.lnb[:, lni + c:lni + c + 1]),
                  [xn, self.lng, self.lnb], [xn])
            kb.dma("act", dstv[:, :, t0:t0 + W], xn[:, :, :], xn, r=[xn], w=[("dram", id(dst))])


Model.rwkv_scan_dir = _rwkv_scan_dir
Model.merge_declare = _merge_declare
Model.merge_phase = _merge_phase


def host_inputs_rwkv(inp):
    L = DEPTH
    d = {}
    w = inp["w_in"]
    rw = w[:, :, 768:2624]
    r, k, v = rw[:, :, 0:512], rw[:, :, 512:1024], rw[:, :, 1024:1536]
    wlo, alo, glo = rw[:, :, 1536:1664], rw[:, :, 1664:1728], rw[:, :, 1728:1856]
    pad = np.zeros_like(alo)
    d["w_inB"] = np.ascontiguousarray(np.concatenate([r, k, v, wlo, glo, alo, pad], -1))
    mu = inp["rwkv_mu"]
    mu_r = np.concatenate([mu[:, 0:1536], mu[:, 1536:1664], mu[:, 1728:1856], mu[:, 1664:1728], np.zeros((L, 64), np.float32)], -1)
    d["rk_mu"] = np.ascontiguousarray(mu_r.reshape(L * NCH_B, 128).T)
    d["rk_w0"] = np.ascontiguousarray(inp["rwkv_w0"].reshape(L * 8, 128).T)
    for n, src in (("a0", "rwkv_a0"), ("kk", "rwkv_k_k"), ("ka", "rwkv_k_a"), ("rk", "rwkv_r_k"), ("gng", "rwkv_gn_g"), ("gnb", "rwkv_gn_b")):
        d["rk_" + n] = np.ascontiguousarray(inp[src].reshape(L * 4, 128).T)
    d["rk_w2"] = np.ascontiguousarray(inp["rwkv_w2"].reshape(L, 128, 512))
    d["rk_a2"] = inp["rwkv_a2"]
    d["rk_g2"] = inp["rwkv_g2"]
    i = np.arange(128)
    d["bd64"] = np.ascontiguousarray(((i[:, None] // 64) == (i[None, :] // 64)).astype(np.float32))
    d["identb"] = np.eye(128, dtype=np.float32)
    on = np.ones((2, 128, 128), np.float32)
    on[0, :, 0] = 0.0
    on[1, :, 127] = 0.0
    d["ones0"] = on
    MT = np.zeros((2, 128, 512), np.float32)
    MN = np.zeros((2, 128, 512), np.float32)
    ii, tt = i[:, None], i[None, :]
    for dd in range(2):
        prev = (ii < tt) if dd == 0 else (ii > tt)
        incl = prev | (ii == tt)
        MT[dd, :, 0:128] = -(prev.astype(np.float32))
        MT[dd, :, 128:256] = incl
        MT[dd, :, 256:384] = prev
        MT[dd, :, 384:512] = incl
        MN[dd] = np.tile(-(prev.T.astype(np.float32)), (1, 4))
    d["rk_MT"], d["rk_MN"] = MT, MN
    d["branch_proj"] = inp["branch_proj"]
    d["w_out"] = inp["w_out"]
    return d


def _make_scratch(self):
    NT = self.NT
    S = {}
    for n, shp, dt in (("qS", [512, NT], BF16), ("kS", [256, NT], BF16), ("vS", [NT, 128], BF16), ("usS", [512, NT], F32),
                       ("gS", [3072, NT], BF16), ("ya", [512, NT], BF16), ("ysb", [512, NT], F32), ("ys", [512, NT], BF16),
                       ("gR", [512, NT], BF16), ("bon", [512, NT], BF16), ("vt", [NT, 512], BF16)):
        S[n] = self.scratch(n, shp, dt)
    for n in ("RHO", "KAP", "BET", "KTI"):
        S[n] = [self.scratch("%s%d" % (n, d), [512, NT], BF16) for d in range(2)]
    for n in ("bt", "kt"):
        S[n] = [self.scratch("%s%d" % (n, d), [NT, 512], BF16) for d in range(2)]
    S["sc"] = [self.scratch("sc%d" % d, [128, NT // 128, 4, 3], F32) for d in range(2)]
    S["O"] = [self.scratch("O%d" % d, [NT, 512], F32) for d in range(2)]
    if "yrS" in self.dbg:
        S["yrS"] = self.scratch("yrS", [512, NT], BF16)
    return S


def _mixer(self, l, src, dst, S, ctx_out):
    self.mixA_phase(l, src, S["qS"], S["kS"], S["vS"], S["usS"], S["gS"])
    self.attn_phase(l, S["qS"], S["kS"], S["vS"], S["ya"])
    self.rwkv_prep_phase(l, src, S)
    self.scan_phase(l, S)
    self.merge_phase(l, src, dst, S, ctx=ctx_out)


def _scan_phase(self, l, S):
    kb = self.kb
    for (ds5, drk) in ((1, 0), (0, 1)):
        with kb.phase() as st:
            PS2 = Pool(kb, "ps2", 2, [128, 1024], F32, psum=True, stack=st)
            psr = Pool(kb, "psr", 4, [128, 512], F32, psum=True, stack=st)
            g1 = self.s5_dir(l, ds5, S["usS"], S["ysb"], S["ys"], st, PS2)
            next(g1)
            g2 = self.rwkv_scan_dir(l, drk, S, st, psr)
            next(g2)
            alive = [g1, g2]
            while alive:
                for g in list(alive):
                    try:
                        next(g)
                    except StopIteration:
                        alive.remove(g)


Model.scan_phase = _scan_phase
Model.make_scratch = _make_scratch
Model.mixer = _mixer


def build_model(T, dbg=()):
    m = Model(T, dbg=dbg)
    m.declare_inputs()
    m.mixA_declare()
    m.s5_declare()
    m.rwkv_declare()
    m.merge_declare()
    m.setup_consts()
    NT = m.NT
    S = m.make_scratch()
    streams = [m.scratch("str%d" % i, [D, NT], F32) for i in range(3)]
    outT = m.dout("outT", [D, T])
    cur = m.xT
    for l in range(DEPTH):
        last = l == DEPTH - 1
        m.adaln_phase(l)
        m.ffn_phase(l, 0, cur, streams[0], 0, ctx=True)
        m.mixer(l, streams[0], streams[1], S, ctx_out=not last)
        if last:
            m.ffn_phase(l, 1, streams[1], outT, CTX, ctx=False)
        else:
            m.ffn_phase(l, 1, streams[1], streams[2], 0, ctx=True)
            cur = streams[2]
    m.kb.finish()
    return m


def all_host_inputs(inp, b, T):
    d = host_inputs(inp, b, T)
    d.update(host_inputs_A(inp, T))
    d.update(host_inputs_s5(inp))
    d.update(host_inputs_rwkv(inp))
    return d


T_FULL = 8192
N_CORES = 8


def kernel(**inputs):
    inp = {k: np.asarray(v) for k, v in inputs.items()}
    m = build_model(T_FULL)
    B = inp["x"].shape[0]
    shared = None
    in_maps = []
    for core in range(N_CORES):
        b = core % B
        d = all_host_inputs(inp, b, T_FULL) if shared is None else dict(shared)
        if shared is None:
            shared = d
        else:
            d["xT"] = np.ascontiguousarray(np.concatenate([inp["ctx"][b], inp["x"][b, :T_FULL]], 0).T)
            cond = np.stack([inp["c"][b], inp["c_ctx"]], -1)
            d["condT"] = np.ascontiguousarray(cond.reshape(NCH, 128, 2).transpose(1, 0, 2))
        in_maps.append({k: v for k, v in d.items() if k in m.dram_in})
    res = run_bass_kernel_spmd(m.nc, in_maps, core_ids=list(range(N_CORES)))
    out = np.stack([np.ascontiguousarray(res.results[b]["outT"].T) for b in range(B)], 0)
    return out.astype(np.float32)
```

```python
import contextlib
import numpy as np
import concourse.bass as bass
import concourse.mybir as mybir
from concourse.bass_utils import run_bass_kernel_spmd

F32 = mybir.dt.float32
BF16 = mybir.dt.bfloat16
AF = mybir.ActivationFunctionType
ALU = mybir.AluOpType
AX = mybir.AxisListType

D = 1024
NCH = 8
CTX = 256
DFF = 2816
NFF = 22
DEPTH = 2
ALPHA = (2.0 * DEPTH) ** 0.25
LN_EPS = 1e-6
DECAY_SCALE = 0.606531
GN_EPS = 64e-5
N_IN = 6208


class Sem:
    _n = 0

    def __init__(self, h):
        self.h = h
        Sem._n += 1
        self.uid = Sem._n


class Buf:
    def __init__(self, name, t):
        self.name = name
        self.t = t
        self.dsem = None
        self.dcnt = 0

    def __getitem__(self, k):
        return self.t[k]

    def __repr__(self):
        return "Buf(%s)" % self.name


class KB:
    EPOCH = 20000

    def __init__(self, nc):
        self.nc = nc
        self.es = contextlib.ExitStack()
        self.eng = {"pe": nc.tensor, "dve": nc.vector, "act": nc.scalar, "pool": nc.gpsimd, "sp": nc.sync}
        self.esem = {}
        self.ecnt = {}
        for e in self.eng:
            self.esem[e] = Sem(self.es.enter_context(nc.semaphore("c_%s_0" % e)))
            self.ecnt[e] = 0
        self.eepoch = {e: 0 for e in self.eng}
        self.seen = {e: {} for e in self.eng}
        self.lastw = {}
        self.reads = {}
        self.nbuf = 0
        self.ninstr = 0
        self.nwait = 0
        self.all_events = {}
        self.free_dsems = []
        self.phase_bufs = []
        self.ndsem = 0

    def sb(self, name, shape, dtype, stack=None):
        self.nbuf += 1
        t = (stack or self.es).enter_context(self.nc.sbuf_tensor("%s_%d" % (name, self.nbuf), list(shape), dtype))
        b = Buf(name, t)
        if stack is not None:
            self.phase_bufs.append(b)
        return b

    def ps(self, name, shape, dtype=F32, stack=None):
        self.nbuf += 1
        t = (stack or self.es).enter_context(self.nc.psum_tensor("%s_%d" % (name, self.nbuf), list(shape), dtype))
        return Buf(name, t)

    def _dsem(self, b):
        if b.dsem is None:
            if self.free_dsems:
                b.dsem, b.dcnt = self.free_dsems.pop()
            else:
                self.ndsem += 1
                b.dsem = Sem(self.es.enter_context(self.nc.semaphore("d_%d" % self.ndsem)))
                b.dcnt = 0
        return b.dsem

    @contextlib.contextmanager
    def phase(self):
        st = contextlib.ExitStack()
        self.phase_bufs = []
        try:
            yield st
        finally:
            self.barrier()
            for b in self.phase_bufs:
                if b.dsem is not None:
                    self.free_dsems.append((b.dsem, b.dcnt))
                    b.dsem = None
            self.phase_bufs = []
            st.close()

    def _need(self, e, r, w):
        need = {}

        def add(evs):
            for uid, (s, v) in evs.items():
                if uid not in need or need[uid][1] < v:
                    need[uid] = (s, v)
        for k in r:
            add(self.lastw.get(k, {}))
        for k in w:
            add(self.lastw.get(k, {}))
            add(self.reads.get(k, {}))
        return need

    def _wait(self, e, need, own_ok):
        eng = self.eng[e]
        for uid, (s, v) in need.items():
            if own_ok and uid == self.esem[e].uid:
                continue
            if self.seen[e].get(uid, 0) >= v:
                continue
            eng.wait_ge(s.h, v)
            self.nwait += 1
            self.seen[e][uid] = v

    def _record(self, ev, r, w):
        uid = ev[0].uid
        for k in r:
            self.reads.setdefault(k, {})[uid] = ev
        for k in w:
            self.lastw.setdefault(k, {})[uid] = ev
            self.reads[k] = {}
        self.all_events[uid] = ev

    def _bump(self, e):
        if self.ecnt[e] >= self.EPOCH:
            self.eepoch[e] += 1
            self.esem[e] = Sem(self.es.enter_context(self.nc.semaphore("c_%s_%d" % (e, self.eepoch[e]))))
            self.ecnt[e] = 0
        self.ecnt[e] += 1
        return (self.esem[e], self.ecnt[e])

    def op(self, e, fn, r=(), w=(), same_ok=False):
        need = self._need(e, r, w)
        self._wait(e, need, own_ok=(e == "pe" or same_ok))
        ins = fn(self.eng[e])
        ev = self._bump(e)
        ins.then_inc(ev[0].h, 1)
        self._record(ev, r, w)
        self.ninstr += 1
        return ins

    def dma(self, q, out, in_, sbuf, r=(), w=(), **kw):
        need = self._need(q, r, w)
        self._wait(q, need, own_ok=False)
        s = self._dsem(sbuf)
        ins = self.eng[q].dma_start(out=out, in_=in_, **kw)
        sbuf.dcnt += 16
        ev = (s, sbuf.dcnt)
        ins.then_inc(s.h, 16)
        self._record(ev, r, w)
        self.ninstr += 1
        return ins

    def barrier(self):
        for e in self.eng:
            self._wait(e, dict(self.all_events), own_ok=True)
        self.lastw = {}
        self.reads = {}
        self.all_events = {}

    def finish(self, e="sp"):
        self._wait(e, dict(self.all_events), own_ok=True)


class Pool:
    def __init__(self, kb, name, n, shape, dtype, psum=False, stack=None):
        self.bufs = [(kb.ps if psum else kb.sb)("%s%d" % (name, i), shape, dtype, stack=stack) for i in range(n)]
        self.i = 0

    def next(self):
        b = self.bufs[self.i % len(self.bufs)]
        self.i += 1
        return b


class Model:
    def __init__(self, T, dbg=(), nlayers=DEPTH, stop_after=None):
        self.T = T
        self.NT = CTX + T
        self.dbg = set(dbg)
        self.nlayers = nlayers
        self.stop_after = stop_after
        nc = bass.Bass("TRN2", target_bir_lowering=False)
        self.nc = nc
        self.kb = KB(nc)
        self.dram_in = {}
        self.dram_out = {}
        self.scr_n = 0

    def din(self, name, shape, dtype=F32):
        t = self.nc.dram_tensor(name, list(shape), dtype, kind="ExternalInput").ap()
        self.dram_in[name] = t
        return t

    def dout(self, name, shape, dtype=F32):
        t = self.nc.dram_tensor(name, list(shape), dtype, kind="ExternalOutput").ap()
        self.dram_out[name] = t
        return t

    def scratch(self, name, shape, dtype=F32):
        if name in self.dbg:
            return self.dout(name, shape, dtype)
        return self.nc.dram_tensor(name, list(shape), dtype, kind="Internal").ap()

    def tiles(self, W, ctx=True, lat=True):
        out = []
        if ctx:
            for t0 in range(0, CTX, W):
                out.append((1, t0, W))
        if lat:
            for t0 in range(0, self.T, W):
                out.append((0, CTX + t0, W))
        return out

    def declare_inputs(self):
        L = DEPTH
        self.xT = self.din("xT", [D, self.NT])
        self.condT = self.din("condT", [128, NCH, 2])
        self.w_ada = self.din("w_ada", [L, D, 9 * D])
        self.b_ada = self.din("b_ada", [L, 128, 72])
        self.ln_g = self.din("ln_g", [128, L * 3 * NCH])
        self.ln_b = self.din("ln_b", [128, L * 3 * NCH])
        self.ffn_w_in = self.din("ffn_w_in", [L, 2, D, 2 * DFF])
        self.ffn_w_out = self.din("ffn_w_out", [L, 2, DFF, D])

    def setup_consts(self):
        kb = self.kb
        self.onesb = kb.sb("onesb", [128, 128], BF16)
        kb.op("dve", lambda e: e.memset(self.onesb[:], 1.0 / D), w=[self.onesb])
        self.lng = kb.sb("lng", [128, DEPTH * 3 * NCH], F32)
        self.lnb = kb.sb("lnb", [128, DEPTH * 3 * NCH], F32)
        kb.dma("sp", self.lng[:], self.ln_g[:, :], self.lng, w=[self.lng])
        kb.dma("sp", self.lnb[:], self.ln_b[:, :], self.lnb, w=[self.lnb])
        self.scond = kb.sb("scond", [128, NCH, 2], F32)
        kb.dma("sp", self.scond[:], self.condT[:, :, :], self.scond, w=[self.scond])
        kb.op("act", lambda e: e.activation(out=self.scond[:], in_=self.scond[:], func=AF.Silu),
              r=[self.scond], w=[self.scond])
        self.mods = kb.sb("mods", [128, 72, 2], F32)
        self.modp1 = kb.sb("modp1", [128, 72, 2], F32)
        self.modh = kb.sb("modh", [128, 72, 2], F32)

    def adaln_phase(self, l):
        kb = self.kb
        with kb.phase() as st:
            self.psum = Pool(kb, "ps", 8, [128, 512], F32, psum=True, stack=st)
            wa = Pool(kb, "wa", 2, [128, NCH, D], F32, stack=st)
            bada = kb.sb("bada", [128, 72], F32, st)
            kb.dma("sp", bada[:], self.b_ada[l, :, :], bada, w=[bada])
            P = self.psum.next()
            for m in range(9):
                w = wa.next()
                for kc in range(NCH):
                    kb.dma("sp" if kc % 2 == 0 else "act", w[:, kc, :],
                           self.w_ada[l, kc * 128:(kc + 1) * 128, m * D:(m + 1) * D], w, w=[w])
                for oc in range(NCH):
                    j = m * NCH + oc
                    for kc in range(NCH):
                        kb.op("pe", lambda e: e.matmul(P[:, 2 * j:2 * j + 2], lhsT=w[:, kc, oc * 128:(oc + 1) * 128],
                                                       rhs=self.scond[:, kc, :], start=(kc == 0), stop=(kc == NCH - 1)),
                              r=[w, self.scond], w=[P])
            kb.op("dve", lambda e: e.tensor_tensor(out=self.mods[:], in0=P[:, 0:144].rearrange("p (j s) -> p j s", s=2),
                                                   in1=bada[:, :].unsqueeze(2).broadcast_to([128, 72, 2]), op=ALU.add),
                  r=[P, bada], w=[self.mods])
            kb.op("dve", lambda e: e.tensor_scalar(out=self.modp1[:], in0=self.mods[:], scalar1=1.0, scalar2=None,
                                                   op0=ALU.add), r=[self.mods], w=[self.modp1])
            kb.op("dve", lambda e: e.tensor_scalar(out=self.modh[:], in0=self.mods[:], scalar1=0.5, scalar2=None,
                                                   op0=ALU.mult), r=[self.mods], w=[self.modh])

    def ln_stats(self, x, W, P, pl):
        kb = self.kb
        xb, sq = pl["xb"].next(), pl["sq"].next()
        kb.op("act", lambda e: e.activation(out=xb[:, :, :W], in_=x[:, :, :W], func=AF.Copy), r=[x], w=[xb])
        kb.op("act", lambda e: e.activation(out=sq[:, :, :W], in_=x[:, :, :W], func=AF.Square), r=[x], w=[sq])
        for c in range(NCH):
            kb.op("pe", lambda e: e.matmul(P[:, 0:W], lhsT=self.onesb[:, :], rhs=xb[:, c, :W],
                                           start=(c == 0), stop=(c == NCH - 1)), r=[xb, self.onesb], w=[P])
        for c in range(NCH):
            kb.op("pe", lambda e: e.matmul(P[:, W:2 * W], lhsT=self.onesb[:, :], rhs=sq[:, c, :W],
                                           start=(c == 0), stop=(c == NCH - 1)), r=[sq, self.onesb], w=[P])
        m2, var, rstd, nmr = pl["m2"].next(), pl["var"].next(), pl["rstd"].next(), pl["nmr"].next()
        kb.op("act", lambda e: e.activation(out=m2[:, 0, :W], in_=P[:, 0:W], func=AF.Square), r=[P], w=[m2])
        kb.op("dve", lambda e: e.scalar_tensor_tensor(out=var[:, 0, :W], in0=P[:, W:2 * W], scalar=LN_EPS,
                                                      in1=m2[:, 0, :W], op0=ALU.add, op1=ALU.subtract),
              r=[P, m2], w=[var])
        kb.op("act", lambda e: e.activation(out=var[:, 0, :W], in_=var[:, 0, :W], func=AF.Sqrt), r=[var], w=[var])
        kb.op("dve", lambda e: e.reciprocal(out=rstd[:, 0, :W], in_=var[:, 0, :W]), r=[var], w=[rstd])
        kb.op("dve", lambda e: e.scalar_tensor_tensor(out=nmr[:, 0, :W], in0=P[:, 0:W], scalar=-1.0,
                                                      in1=rstd[:, 0, :W], op0=ALU.mult, op1=ALU.mult),
              r=[P, rstd], w=[nmr])
        return rstd, nmr

    def normalize(self, out, x, W, rstd, nmr):
        kb = self.kb
        kb.op("dve", lambda e: e.tensor_tensor(out=out[:, :, :W], in0=x[:, :, :W],
                                               in1=rstd[:, 0:1, :W].broadcast_to([128, NCH, W]), op=ALU.mult),
              r=[x, rstd], w=[out])
        kb.op("dve", lambda e: e.tensor_tensor(out=out[:, :, :W], in0=out[:, :, :W],
                                               in1=nmr[:, 0:1, :W].broadcast_to([128, NCH, W]), op=ALU.add),
              r=[out, nmr], w=[out])

    def stat_pools(self, W, st):
        kb = self.kb
        return {
            "xb": Pool(kb, "xb", 1, [128, NCH, W], BF16, stack=st),
            "sq": Pool(kb, "sq", 1, [128, NCH, W], BF16, stack=st),
            "m2": Pool(kb, "m2", 2, [128, 1, W], F32, stack=st),
            "var": Pool(kb, "var", 2, [128, 1, W], F32, stack=st),
            "rstd": Pool(kb, "rstd", 2, [128, 1, W], F32, stack=st),
            "nmr": Pool(kb, "nmr", 2, [128, 1, W], F32, stack=st),
        }

    def ffn_phase(self, l, s, src, dst, dst_off, ctx):
        kb = self.kb
        W = 256
        mb = 0 if s == 0 else 6
        lni = (l * 3 + (0 if s == 0 else 2)) * NCH
        with kb.phase() as st:
            self.psum = Pool(kb, "ps", 8, [128, 512], F32, psum=True, stack=st)
            w1 = kb.sb("w1", [128, NCH, 2 * DFF], BF16, st)
            w2 = kb.sb("w2", [128, NFF, D], BF16, st)
            for c in range(NCH):
                for hh in range(2):
                    kb.dma("pool", w1[:, c, hh * DFF:(hh + 1) * DFF],
                           self.ffn_w_in[l, s, c * 128:(c + 1) * 128, hh * DFF:(hh + 1) * DFF], w1, w=[w1])
            for c in range(NFF):
                kb.dma("pool", w2[:, c, :], self.ffn_w_out[l, s, c * 128:(c + 1) * 128, :], w2, w=[w2])
            pl = self.stat_pools(W, st)
            xp = Pool(kb, "x", 2, [128, NCH, W], F32, stack=st)
            xnp = Pool(kb, "xn", 2, [128, NCH, W], F32, stack=st)
            up = Pool(kb, "u", 2, [128, NCH, W], BF16, stack=st)
            hp = Pool(kb, "h", 1, [128, NFF, W], BF16, stack=st)
            gp = Pool(kb, "g", 2, [128, W], F32, stack=st)
            srcv = src.rearrange("(c p) t -> p c t", p=128)
            dstv = dst.rearrange("(c p) t -> p c t", p=128)
            tl = self.tiles(W, ctx=ctx)
            T_ = {}

            def stage_a(i):
                seg, t0, _ = tl[i]
                x = xp.next()
                kb.dma("sp", x[:, :, :], srcv[:, :, t0:t0 + W], x, r=[("dram", id(src))], w=[x])
                P = self.psum.next()
                rstd, nmr = self.ln_stats(x, W, P, pl)
                xn = xnp.next()
                self.normalize(xn, x, W, rstd, nmr)
                u = up.next()
                for c in range(NCH):
                    kb.op("act", lambda e: e.activation(out=u[:, c, :], in_=xn[:, c, :], func=AF.Identity,
                                                        scale=self.modp1[:, (mb + 1) * NCH + c, seg:seg + 1],
                                                        bias=self.mods[:, mb * NCH + c, seg:seg + 1]),
                          r=[xn, self.modp1, self.mods], w=[u])
                kb.op("pool", lambda e: e.tensor_scalar(out=x[:, :, :], in0=x[:, :, :], scalar1=ALPHA, scalar2=None,
                                                        op0=ALU.mult), r=[x], w=[x])
                T_[i] = (x, xn, u)

            def stage_b(i):
                x, xn, u = T_[i]
                h = hp.next()
                for j in range(NFF):
                    P = self.psum.next()
                    for c in range(NCH):
                        kb.op("pe", lambda e: e.matmul(P[:, 0:W], lhsT=w1[:, c, j * 128:(j + 1) * 128], rhs=u[:, c, :],
                                                       start=(c == 0), stop=(c == NCH - 1)), r=[w1, u], w=[P])
                    for c in range(NCH):
                        kb.op("pe", lambda e: e.matmul(P[:, W:2 * W], lhsT=w1[:, c, DFF + j * 128:DFF + (j + 1) * 128],
                                                       rhs=u[:, c, :], start=(c == 0), stop=(c == NCH - 1)),
                              r=[w1, u], w=[P])
                    g = gp.next()
                    kb.op("act", lambda e: e.activation(out=g[:, :], in_=P[:, 0:W], func=AF.Silu), r=[P], w=[g])
                    kb.op("dve", lambda e: e.tensor_tensor(out=h[:, j, :], in0=P[:, W:2 * W], in1=g[:, :], op=ALU.mult),
                          r=[P, g], w=[h])
                T_[i] = (x, xn, u, h)

            def stage_c(i):
                seg, t0, _ = tl[i]
                x, xn, u, h = T_.pop(i)
                for oc in range(NCH):
                    if oc % 2 == 0:
                        P = self.psum.next()
                    o0 = (oc % 2) * W
                    for j in range(NFF):
                        kb.op("pe", lambda e: e.matmul(P[:, o0:o0 + W], lhsT=w2[:, j, oc * 128:(oc + 1) * 128],
                                                       rhs=h[:, j, :], start=(j == 0), stop=(j == NFF - 1)),
                              r=[w2, h], w=[P])
                    kb.op("dve", lambda e: e.scalar_tensor_tensor(
                        out=x[:, oc, :], in0=P[:, o0:o0 + W], scalar=self.modh[:, (mb + 2) * NCH + oc, seg:seg + 1],
                        in1=x[:, oc, :], op0=ALU.mult, op1=ALU.add), r=[P, x, self.modh], w=[x])
                P = self.psum.next()
                rstd, nmr = self.ln_stats(x, W, P, pl)
                self.normalize(xn, x, W, rstd, nmr)
                for c in range(NCH):
                    kb.op("act", lambda e: e.activation(out=xn[:, c, :], in_=xn[:, c, :], func=AF.Identity,
                                                        scale=self.lng[:, lni + c:lni + c + 1],
                                                        bias=self.lnb[:, lni + c:lni + c + 1]),
                          r=[xn, self.lng, self.lnb], w=[xn])
                kb.dma("sp", dstv[:, :, t0 - dst_off:t0 - dst_off + W], xn[:, :, :], xn,
                       r=[xn], w=[("dram", id(dst))])

            stage_a(0)
            for i in range(len(tl)):
                stage_b(i)
                if i + 1 < len(tl):
                    stage_a(i + 1)
                stage_c(i)


def host_inputs(inp, b, T):
    L = DEPTH
    d = {}
    d["xT"] = np.ascontiguousarray(np.concatenate([inp["ctx"][b], inp["x"][b, :T]], 0).T)
    cond = np.stack([inp["c"][b], inp["c_ctx"]], -1)
    d["condT"] = np.ascontiguousarray(cond.reshape(NCH, 128, 2).transpose(1, 0, 2))
    d["w_ada"] = inp["w_ada"]
    d["b_ada"] = np.ascontiguousarray(inp["b_ada"].reshape(L, 72, 128).transpose(0, 2, 1))
    d["ln_g"] = np.ascontiguousarray(inp["ln_g"].reshape(L * 3 * NCH, 128).T)
    d["ln_b"] = np.ascontiguousarray(inp["ln_b"].reshape(L * 3 * NCH, 128).T)
    d["ffn_w_in"] = inp["ffn_w_in"]
    d["ffn_w_out"] = inp["ffn_w_out"]
    return d


NWA = 10 + 1 + 4 + 24
CH_Q, CH_QS, CH_K, CH_KS, CH_KB, CH_KBS, CH_V, CH_S5, CH_G = 0, 4, 8, 9, 10, 11, 12, 13, 17
NCH_A = 41


def _mixA_declare(self):
    L = DEPTH
    self.w_inA = self.din("w_inA", [L, D, NCH_A * 128])
    self.ropeC = self.din("ropeC", [128, self.NT])
    self.ropeS = self.din("ropeS", [128, self.NT])
    self.sinkT = self.din("sinkT", [L, 64, 8])
    self.maskP = self.din("maskP", [128, 512])
    self.maskN = self.din("maskN", [128, 512])


def _mixA_phase(self, l, src, qS, kS, vS, usS, gS):
    kb = self.kb
    W = 256
    with kb.phase() as st:
        self.psum = Pool(kb, "ps", 8, [128, 512], F32, psum=True, stack=st)
        w = kb.sb("wA", [128, NCH, NCH_A * 128], BF16, st)
        for c in range(NCH):
            for hh in range(2):
                n0, n1 = (0, 21 * 128) if hh == 0 else (21 * 128, NCH_A * 128)
                kb.dma("pool", w[:, c, n0:n1], self.w_inA[l, c * 128:(c + 1) * 128, n0:n1], w, w=[w])
        pl = self.stat_pools(W, st)
        xp = Pool(kb, "x", 2, [128, NCH, W], F32, stack=st)
        xnp = Pool(kb, "xn", 2, [128, NCH, W], F32, stack=st)
        up = Pool(kb, "u", 2, [128, NCH, W], BF16, stack=st)
        cp = Pool(kb, "rc", 2, [128, W], F32, stack=st)
        sp_ = Pool(kb, "rs", 2, [128, W], F32, stack=st)
        t1p = Pool(kb, "t1", 2, [128, W], F32, stack=st)
        t2p = Pool(kb, "t2", 2, [128, W], F32, stack=st)
        qp = Pool(kb, "qo", 2, [128, 4, W], BF16, stack=st)
        kp = Pool(kb, "ko", 2, [128, 2, W], BF16, stack=st)
        vp = Pool(kb, "vo", 2, [128, 2, 128], BF16, stack=st)
        usp = Pool(kb, "uso", 2, [128, 4, W], F32, stack=st)
        gp = Pool(kb, "go", 2, [128, 24, W], BF16, stack=st)
        srcv = src.rearrange("(c p) t -> p c t", p=128)

        def proj(P, o0, ch, u):
            for c in range(NCH):
                kb.op("pe", lambda e: e.matmul(P[:, o0:o0 + W], lhsT=w[:, c, ch * 128:(ch + 1) * 128], rhs=u[:, c, :],
                                               start=(c == 0), stop=(c == NCH - 1)), r=[w, u], w=[P])

        tl = self.tiles(W)
        T_ = {}

        def stage_a(i):
            seg, t0, _ = tl[i]
            x = xp.next()
            kb.dma("sp", x[:, :, :], srcv[:, :, t0:t0 + W], x, r=[("dram", id(src))], w=[x])
            cT, sT = cp.next(), sp_.next()
            kb.dma("sp", cT[:, :], self.ropeC[:, t0:t0 + W], cT, w=[cT])
            kb.dma("sp", sT[:, :], self.ropeS[:, t0:t0 + W], sT, w=[sT])
            P = self.psum.next()
            rstd, nmr = self.ln_stats(x, W, P, pl)
            xn = xnp.next()
            self.normalize(xn, x, W, rstd, nmr)
            u = up.next()
            for c in range(NCH):
                kb.op("act", lambda e: e.activation(out=u[:, c, :], in_=xn[:, c, :], func=AF.Identity,
                                                    scale=self.modp1[:, 4 * NCH + c, seg:seg + 1],
                                                    bias=self.mods[:, 3 * NCH + c, seg:seg + 1]),
                      r=[xn, self.modp1, self.mods], w=[u])
            T_[i] = (u, cT, sT)

        def stage_b1(i):
            seg, t0, _ = tl[i]
            u, cT, sT = T_[i]
            qo, ko = qp.next(), kp.next()

            def rope(dst_ap, dst, ch, chs):
                P = self.psum.next()
                proj(P, 0, ch, u)
                proj(P, W, chs, u)
                t1, t2 = t1p.next(), t2p.next()
                kb.op("dve", lambda e: e.tensor_tensor(out=t1[:, :], in0=P[:, 0:W], in1=cT[:, :], op=ALU.mult),
                      r=[P, cT], w=[t1])
                kb.op("dve", lambda e: e.tensor_tensor(out=t2[:, :], in0=P[:, W:2 * W], in1=sT[:, :], op=ALU.mult),
                      r=[P, sT], w=[t2])
                kb.op("dve", lambda e: e.tensor_tensor(out=dst_ap, in0=t1[:, :], in1=t2[:, :], op=ALU.add),
                      r=[t1, t2], w=[dst])
            for c in range(4):
                rope(qo[:, c, :], qo, CH_Q + c, CH_QS + c)
            rope(ko[:, 0, :], ko, CH_K, CH_KS)
            rope(ko[:, 1, :], ko, CH_KB, CH_KBS)
            kb.dma("sp", qS.rearrange("(c p) t -> p c t", p=128)[:, :, t0:t0 + W], qo[:, :, :], qo, r=[qo],
                   w=[("dram", id(qS))])
            kb.dma("sp", kS.rearrange("(c p) t -> p c t", p=128)[:, :, t0:t0 + W], ko[:, :, :], ko, r=[ko],
                   w=[("dram", id(kS))])

        def stage_b2(i):
            seg, t0, _ = tl[i]
            u, cT, sT = T_.pop(i)
            vo = vp.next()
            P = self.psum.next()
            for tb in range(W // 128):
                for c in range(NCH):
                    kb.op("pe", lambda e: e.matmul(P[:, tb * 128:(tb + 1) * 128], lhsT=u[:, c, tb * 128:(tb + 1) * 128],
                                                   rhs=w[:, c, CH_V * 128:(CH_V + 1) * 128],
                                                   start=(c == 0), stop=(c == NCH - 1)), r=[w, u], w=[P])
            kb.op("act", lambda e: e.activation(out=vo[:, :, :], in_=P[:, 0:W].rearrange("p (b n) -> p b n", b=2),
                                                func=AF.Copy), r=[P], w=[vo])
            kb.dma("act", vS[t0:t0 + W, :].rearrange("(b p) n -> p b n", p=128), vo[:, :, :], vo, r=[vo],
                   w=[("dram", id(vS))])
            uso = usp.next()
            for c in range(4):
                if c % 2 == 0:
                    P = self.psum.next()
                o0 = (c % 2) * W
                proj(P, o0, CH_S5 + c, u)
                kb.op("act", lambda e: e.activation(out=uso[:, c, :], in_=P[:, o0:o0 + W], func=AF.Copy),
                      r=[P], w=[uso])
            kb.dma("act", usS.rearrange("(c p) t -> p c t", p=128)[:, :, t0:t0 + W], uso[:, :, :], uso, r=[uso],
                   w=[("dram", id(usS))])
            go = gp.next()
            for c in range(24):
                if c % 2 == 0:
                    P = self.psum.next()
                o0 = (c % 2) * W
                proj(P, o0, CH_G + c, u)
                kb.op("act", lambda e: e.activation(out=go[:, c, :], in_=P[:, o0:o0 + W], func=AF.Sigmoid),
                      r=[P], w=[go])
            kb.dma("act", gS.rearrange("(c p) t -> p c t", p=128)[:, :, t0:t0 + W], go[:, :, :], go, r=[go],
                   w=[("dram", id(gS))])

        stage_a(0)
        for i in range(len(tl)):
            stage_b1(i)
            if i + 1 < len(tl):
                stage_a(i + 1)
            stage_b2(i)


def _attn_phase(self, l, qS, kS, vS, yaS):
    kb = self.kb
    NB = self.T // 128
    with kb.phase() as st:
        self.psum = Pool(kb, "ps", 8, [128, 512], F32, psum=True, stack=st)
        mP = kb.sb("mP", [128, 512], BF16, st)
        mN = kb.sb("mN", [128, 512], BF16, st)
        kb.dma("pool", mP[:, :], self.maskP[:, :], mP, w=[mP])
        kb.dma("pool", mN[:, :], self.maskN[:, :], mN, w=[mN])
        ones = kb.sb("ones64", [128, 64], BF16, st)
        kb.op("dve", lambda e: e.memset(ones[:], 1.0), w=[ones])
        esk = kb.sb("esk", [64, 8], F32, st)
        kb.dma("sp", esk[:, :], self.sinkT[l, :, :], esk, w=[esk])
        kb.op("act", lambda e: e.activation(out=esk[:, :], in_=esk[:, :], func=AF.Exp), r=[esk], w=[esk])
        eskb = kb.sb("eskb", [64, 8, 128], F32, st)
        kb.op("dve", lambda e: e.tensor_copy(out=eskb[:, :, :], in_=esk[:, :].unsqueeze(2).broadcast_to([64, 8, 128])),
              r=[esk], w=[eskb])
        kc = kb.sb("kc", [128, 2, CTX], BF16, st)
        kb.dma("sp", kc[:, :, :], kS.rearrange("(c p) t -> p c t", p=128)[:, :, 0:CTX], kc, r=[("dram", id(kS))], w=[kc])
        vc = kb.sb("vc", [128, 2, 128], BF16, st)
        kb.dma("sp", vc[:, :, :], vS[0:CTX, :].rearrange("(b p) n -> p b n", p=128), vc, r=[("dram", id(vS))], w=[vc])
        qp = Pool(kb, "aq", 2, [128, 4, 128], BF16, stack=st)
        kwp = Pool(kb, "akw", 2, [128, 2, 384], BF16, stack=st)
        vwp = Pool(kb, "avw", 2, [128, 3, 128], BF16, stack=st)
        pp = Pool(kb, "ap", 4, [128, 512], BF16, stack=st)
        dp = Pool(kb, "ad", 2, [64, 512], F32, stack=st)
        op_ = Pool(kb, "ao", 2, [64, 8, 128], BF16, stack=st)
        qv = qS.rearrange("(c p) t -> p c t", p=128)
        kv = kS.rearrange("(c p) t -> p c t", p=128)
        blocks = [(1, b) for b in range(CTX // 128)] + [(0, b) for b in range(NB)]
        self._acc_i = 0
        self._sc_i = 0
        for (seg, b) in blocks:
            t0 = b * 128 if seg == 1 else CTX + b * 128
            q = qp.next()
            kb.dma("sp", q[:, :, :], qv[:, :, t0:t0 + 128], q, r=[("dram", id(qS))], w=[q])
            keyblocks = []
            if seg == 0:
                lo = max(b - 1, 0)
                hi = min(b + 1, NB - 1)
                nb_ = hi - lo + 1
                kw, vw = kwp.next(), vwp.next()
                kb.dma("sp", kw[:, :, 0:nb_ * 128], kv[:, :, CTX + lo * 128:CTX + (hi + 1) * 128], kw,
                       r=[("dram", id(kS))], w=[kw])
                kb.dma("sp", vw[:, 0:nb_, :],
                       vS[CTX + lo * 128:CTX + (hi + 1) * 128, :].rearrange("(b p) n -> p b n", p=128), vw,
                       r=[("dram", id(vS))], w=[vw])
                for bb in range(lo, hi + 1):
                    i = bb - lo
                    mask = mP if bb < b else (mN if bb > b else None)
                    keyblocks.append((kw, i * 128, vw, i, mask))
            for i in range(CTX // 128):
                keyblocks.append((kc, i * 128, vc, i, None))
            oo = op_.next()
            accb = self.psum.bufs[0:4]
            scb = self.psum.bufs[4:8]
            for kvh in range(2):
                Pn = accb[(self._acc_i) % 4]
                Pd = accb[(self._acc_i + 1) % 4]
                self._acc_i += 2
                for bi, (kbuf, koff, vbuf, vi, mask) in enumerate(keyblocks):
                    Ps = [scb[self._sc_i % 4], scb[(self._sc_i + 1) % 4]]
                    self._sc_i += 2
                    pt = pp.next()
                    for par in range(2):
                        base = par * 64
                        var = 0 if (kvh * 64 == base) else 1
                        for j in range(2):
                            h = kvh * 4 + 2 * j + par
                            kb.op("pe", lambda e: e.matmul(Ps[par][:, j * 128:(j + 1) * 128],
                                                           lhsT=kbuf[base:base + 64, var, koff:koff + 128],
                                                           rhs=q[base:base + 64, h // 2, :], start=True, stop=True),
                                  r=[kbuf, q], w=[Ps[par]])
                        kb.op("act", lambda e: e.activation(out=pt[:, par * 256:(par + 1) * 256], in_=Ps[par][:, 0:256],
                                                            func=AF.Exp, scale=0.125), r=[Ps[par]], w=[pt])
                    if mask is not None:
                        kb.op("pool", lambda e: e.tensor_tensor(out=pt[:, :], in0=pt[:, :], in1=mask[:, :], op=ALU.mult),
                              r=[pt, mask], w=[pt])
                    first, last = bi == 0, bi == len(keyblocks) - 1
                    kb.op("pe", lambda e: e.matmul(Pn[0:64, :], lhsT=vbuf[:, vi, kvh * 64:(kvh + 1) * 64], rhs=pt[:, :],
                                                   start=first, stop=last), r=[vbuf, pt], w=[Pn])
                    kb.op("pe", lambda e: e.matmul(Pd[0:64, :], lhsT=ones[:, :], rhs=pt[:, :],
                                                   start=first, stop=last), r=[ones, pt], w=[Pd])
                den = dp.next()
                kb.op("dve", lambda e: e.tensor_tensor(
                    out=den[:, :].rearrange("p (r j q) -> p r j q", r=2, j=2), in0=Pd[0:64, :].rearrange("p (r j q) -> p r j q", r=2, j=2),
                    in1=eskb[:, kvh * 4:(kvh + 1) * 4, :].rearrange("p (j r) q -> p r j q", r=2),
                    op=ALU.add), r=[Pd, eskb], w=[den])
                kb.op("dve", lambda e: e.reciprocal(out=den[:, :], in_=den[:, :]), r=[den], w=[den])
                kb.op("dve", lambda e: e.tensor_tensor(
                    out=oo[:, kvh * 4:(kvh + 1) * 4, :].rearrange("p (j r) q -> p r j q", r=2),
                    in0=Pn[0:64, :].rearrange("p (r j q) -> p r j q", r=2, j=2),
                    in1=den[:, :].rearrange("p (r j q) -> p r j q", r=2, j=2), op=ALU.mult),
                      r=[Pn, den], w=[oo])
            kb.dma("pool", yaS.rearrange("(h p) t -> p h t", p=64)[:, :, t0:t0 + 128], oo[:, :, :], oo, r=[oo],
                   w=[("dram", id(yaS))])


Model.mixA_declare = _mixA_declare
Model.mixA_phase = _mixA_phase
Model.attn_phase = _attn_phase


def _rope_tables(T):
    NT = CTX + T
    C = np.ones((128, NT), np.float32)
    S = np.zeros((128, NT), np.float32)
    t = np.arange(T)
    row = (t // 64).astype(np.float32)
    col = (t % 64).astype(np.float32)
    inv = (10000.0 ** (-np.arange(16, dtype=np.float32) / 16)).astype(np.float32)
    for d in range(64):
        i = d % 16
        pos = row if d < 32 else col
        ang = (pos * inv[i]).astype(np.float32)
        sign = -1.0 if (d % 32) < 16 else 1.0
        for hb in (0, 64):
            C[hb + d, CTX:] = np.cos(ang)
            S[hb + d, CTX:] = sign * np.sin(ang)
    return C, S


def _swap_perm(n_heads):
    idx = []
    for h in range(n_heads):
        for d in range(64):
            p = d + 16 if (d % 32) < 16 else d - 16
            idx.append(h * 64 + p)
    return np.array(idx)


def host_inputs_A(inp, T):
    d = {}
    w = inp["w_in"]
    q = w[:, :, 0:512]
    k = w[:, :, 512:640]
    v = w[:, :, 640:768]
    kB = np.concatenate([k[:, :, 64:128], k[:, :, 0:64]], -1)
    s5 = w[:, :, 2624:3136]
    g = w[:, :, 3136:6208]
    d["w_inA"] = np.ascontiguousarray(np.concatenate(
        [q, q[:, :, _swap_perm(8)], k, k[:, :, _swap_perm(2)], kB, kB[:, :, _swap_perm(2)], v, s5, g], -1))
    C, S = _rope_tables(T)
    d["ropeC"], d["ropeS"] = C, S
    d["sinkT"] = np.ascontiguousarray(np.broadcast_to(inp["attn_sink"][:, None, :], (DEPTH, 64, 8)))
    j = np.arange(128)[:, None]
    i = np.arange(128)[None, :]
    d["maskP"] = np.ascontiguousarray(np.tile((j >= i).astype(np.float32), (1, 4)))
    d["maskN"] = np.ascontiguousarray(np.tile((j <= i).astype(np.float32), (1, 4)))
    return d


I32 = mybir.dt.int32
TWO_PI = 2.0 * np.pi


def _s5_declare(self):
    L = DEPTH
    self.s5_are = self.din("s5_are", [L, 2, 128, 4, 64])
    self.s5_aim = self.din("s5_aim", [L, 2, 128, 4, 64])
    self.s5_ls = self.din("s5_ls", [L, 2, 128, 4])
    self.s5_brT = self.din("s5_brT", [L, 128, 4, 64])
    self.s5_biT = self.din("s5_biT", [L, 128, 4, 64])
    self.s5_are2 = self.din("s5_are2", [L, 2, 128, 16])
    self.s5_aim2 = self.din("s5_aim2", [L, 2, 128, 16])
    self.s5_ls2 = self.din("s5_ls2", [L, 2, 128, 16])
    self.s5_crT = self.din("s5_crT", [L, 128, 16, 16])
    self.s5_ciT = self.din("s5_ciT", [L, 128, 16, 16])
    self.s5_rowmask = self.din("s5_rowmask", [128, 16, 2])
    self.s5_dT = self.din("s5_dT", [128, L * 4])
    self.s5_glub = self.din("s5_glub", [128, L * 4])
    self.s5_gluw = self.din("s5_gluw", [L, 512, 512])
    self.tauT = self.din("tauT", [128, 128])


def _sincos(self, ang, angk, n, S, Sk, C, Ck, st):
    kb = self.kb
    t = kb.sb("sc_t", [128, n], F32, st)
    ti = kb.sb("sc_i", [128, n], I32, st)
    tf = kb.sb("sc_f", [128, n], F32, st)
    for (off, dst, dk) in ((0.0, S, Sk), (0.25, C, Ck)):
        kb.op("dve", lambda e: e.tensor_scalar(out=t[:, :], in0=ang, scalar1=1.0 / TWO_PI, scalar2=off,
                                               op0=ALU.mult, op1=ALU.add), r=[angk], w=[t])
        kb.op("dve", lambda e: e.tensor_copy(out=ti[:, :], in_=t[:, :]), r=[t], w=[ti])
        kb.op("dve", lambda e: e.tensor_copy(out=tf[:, :], in_=ti[:, :]), r=[ti], w=[tf])
        kb.op("dve", lambda e: e.tensor_tensor(out=tf[:, :], in0=t[:, :], in1=tf[:, :], op=ALU.subtract),
              r=[t, tf], w=[tf])
        kb.op("act", lambda e: e.activation(out=dst, in_=tf[:, :], func=AF.Sin, scale=TWO_PI), r=[tf], w=[dk])


def _s5_dir(self, l, d, usS, ysbS, ysS, st, PS2):
    kb = self.kb
    NT = self.NT
    nchunk = NT // 128
    usv = usS.rearrange("(c p) t -> p c t", p=128)
    ybv = ysbS.rearrange("(c p) t -> p c t", p=128)
    ysv = ysS.rearrange("(c p) t -> p c t", p=128)
    V = lambda e_, f, r, w: kb.op(e_, f, r=r, w=w)
    _pers = {}
    for (n_, shp_, dt_) in (("DR", [128, 16, 128], BF16), ("DI", [128, 16, 128], BF16), ("COS", [128, 16, 128], F32),
                            ("SIN", [128, 16, 128], F32), ("RHO0", [128, 16, 128], F32), ("rho", [128, 16], F32),
                            ("lr2", [128, 16], F32), ("li2", [128, 16], F32), ("CR", [128, 16, 128], BF16),
                            ("CIn", [128, 16, 128], BF16), ("s5d", [128, 4], F32), ("s5gb", [128, 4], F32),
                            ("gluw", [128, 4, 512], BF16)):
        _pers[n_] = kb.sb(n_, shp_, dt_, st)
    sbf = lambda n, shp, dt=F32: _pers[n] if n in _pers else kb.sb(n, shp, dt, st)
    st2 = contextlib.ExitStack()
    tmpf = lambda n, shp, dt=F32: kb.sb(n, shp, dt, st2)
    are, aim = tmpf("are", [128, 4, 64]), tmpf("aim", [128, 4, 64])
    ls = tmpf("ls", [128, 4])
    br, bi = tmpf("br", [128, 4, 64]), tmpf("bi", [128, 4, 64])
    kb.dma("sp", are[:, :, :], self.s5_are[l, d], are, w=[are])
    kb.dma("sp", aim[:, :, :], self.s5_aim[l, d], aim, w=[aim])
    kb.dma("sp", ls[:, :], self.s5_ls[l, d], ls, w=[ls])
    kb.dma("sp", br[:, :, :], self.s5_brT[l], br, w=[br])
    kb.dma("sp", bi[:, :, :], self.s5_biT[l], bi, w=[bi])
    rmask = tmpf("rmask", [128, 16, 2])
    kb.dma("sp", rmask[:, :, :], self.s5_rowmask[:, :, :], rmask, w=[rmask])
    V("act", lambda e: e.activation(out=ls[:, :], in_=ls[:, :], func=AF.Exp), [ls], [ls])
    dtb = ls[:, :].unsqueeze(2).broadcast_to([128, 4, 64])
    adt, th = tmpf("adt", [128, 4, 64]), tmpf("th", [128, 4, 64])
    V("dve", lambda e: e.tensor_tensor(out=adt[:, :, :], in0=are[:, :, :], in1=dtb, op=ALU.mult), [are, ls], [adt])
    V("act", lambda e: e.activation(out=adt[:, :, :], in_=adt[:, :, :], func=AF.Exp), [adt], [adt])
    V("dve", lambda e: e.tensor_tensor(out=th[:, :, :], in0=aim[:, :, :], in1=dtb, op=ALU.mult), [aim, ls], [th])
    Sd, Cd = tmpf("Sd", [128, 256]), tmpf("Cd", [128, 256])
    thf = th[:, :, :].rearrange("p a b -> p (a b)")
    self.sincos(thf, th, 256, Sd[:, :], Sd, Cd[:, :], Cd, st2)
    lr, li = tmpf("lr", [128, 256]), tmpf("li", [128, 256])
    magf = adt[:, :, :].rearrange("p a b -> p (a b)")
    V("dve", lambda e: e.tensor_tensor(out=lr[:, :], in0=magf, in1=Cd[:, :], op=ALU.mult), [adt, Cd], [lr])
    V("dve", lambda e: e.tensor_tensor(out=li[:, :], in0=magf, in1=Sd[:, :], op=ALU.mult), [adt, Sd], [li])
    aref = are[:, :, :].rearrange("p a b -> p (a b)")
    aimf = aim[:, :, :].rearrange("p a b -> p (a b)")
    t1, t2, den = tmpf("t1", [128, 256]), tmpf("t2", [128, 256]), tmpf("den", [128, 256])
    V("dve", lambda e: e.tensor_tensor(out=t1[:, :], in0=aref, in1=aref, op=ALU.mult), [are], [t1])
    V("dve", lambda e: e.tensor_tensor(out=t2[:, :], in0=aimf, in1=aimf, op=ALU.mult), [aim], [t2])
    V("dve", lambda e: e.tensor_tensor(out=den[:, :], in0=t1[:, :], in1=t2[:, :], op=ALU.add), [t1, t2], [den])
    V("dve", lambda e: e.reciprocal(out=den[:, :], in_=den[:, :]), [den], [den])
    V("dve", lambda e: e.tensor_scalar(out=lr[:, :], in0=lr[:, :], scalar1=-1.0, scalar2=None, op0=ALU.add),
      [lr], [lr])
    cr, ci = tmpf("cr", [128, 256]), tmpf("ci", [128, 256])
    V("dve", lambda e: e.tensor_tensor(out=t1[:, :], in0=lr[:, :], in1=aref, op=ALU.mult), [lr, are], [t1])
    V("dve", lambda e: e.tensor_tensor(out=t2[:, :], in0=li[:, :], in1=aimf, op=ALU.mult), [li, aim], [t2])
    V("dve", lambda e: e.tensor_tensor(out=cr[:, :], in0=t1[:, :], in1=t2[:, :], op=ALU.add), [t1, t2], [cr])
    V("dve", lambda e: e.tensor_tensor(out=cr[:, :], in0=cr[:, :], in1=den[:, :], op=ALU.mult), [cr, den], [cr])
    V("dve", lambda e: e.tensor_tensor(out=t1[:, :], in0=li[:, :], in1=aref, op=ALU.mult), [li, are], [t1])
    V("dve", lambda e: e.tensor_tensor(out=t2[:, :], in0=lr[:, :], in1=aimf, op=ALU.mult), [lr, aim], [t2])
    V("dve", lambda e: e.tensor_tensor(out=ci[:, :], in0=t1[:, :], in1=t2[:, :], op=ALU.subtract), [t1, t2], [ci])
    V("dve", lambda e: e.tensor_tensor(out=ci[:, :], in0=ci[:, :], in1=den[:, :], op=ALU.mult), [ci, den], [ci])
    brf = br[:, :, :].rearrange("p a b -> p (a b)")
    bif = bi[:, :, :].rearrange("p a b -> p (a b)")
    bbr, bbi = tmpf("bbr", [128, 4, 64]), tmpf("bbi", [128, 4, 64])
    bbrf = bbr[:, :, :].rearrange("p a b -> p (a b)")
    bbif = bbi[:, :, :].rearrange("p a b -> p (a b)")
    V("dve", lambda e: e.tensor_tensor(out=t1[:, :], in0=cr[:, :], in1=brf, op=ALU.mult), [cr, br], [t1])
    V("dve", lambda e: e.tensor_tensor(out=t2[:, :], in0=ci[:, :], in1=bif, op=ALU.mult), [ci, bi], [t2])
    V("dve", lambda e: e.tensor_tensor(out=bbrf, in0=t1[:, :], in1=t2[:, :], op=ALU.subtract), [t1, t2], [bbr])
    V("dve", lambda e: e.tensor_tensor(out=t1[:, :], in0=cr[:, :], in1=bif, op=ALU.mult), [cr, bi], [t1])
    V("dve", lambda e: e.tensor_tensor(out=t2[:, :], in0=ci[:, :], in1=brf, op=ALU.mult), [ci, br], [t2])
    V("dve", lambda e: e.tensor_tensor(out=bbif, in0=t1[:, :], in1=t2[:, :], op=ALU.add), [t1, t2], [bbi])
    DR, DI = sbf("DR", [128, 16, 128], BF16), sbf("DI", [128, 16, 128], BF16)
    for j in range(16):
        for gp in range(2):
            for (dst, srcb) in ((DR, bbr), (DI, bbi)):
                V("dve", lambda e: e.tensor_scalar(out=dst[:, j, gp * 64:(gp + 1) * 64], in0=srcb[:, j // 4, :],
                                                   scalar1=rmask[:, j, gp:gp + 1], scalar2=None, op0=ALU.mult),
                  [srcb, rmask], [dst])
    are2, aim2, ls2 = tmpf("are2", [128, 16]), tmpf("aim2", [128, 16]), tmpf("ls2", [128, 16])
    kb.dma("sp", are2[:, :], self.s5_are2[l, d], are2, w=[are2])
    kb.dma("sp", aim2[:, :], self.s5_aim2[l, d], aim2, w=[aim2])
    kb.dma("sp", ls2[:, :], self.s5_ls2[l, d], ls2, w=[ls2])
    tau = tmpf("tau", [128, 128])
    kb.dma("sp", tau[:, :], self.tauT[:, :], tau, w=[tau])
    V("act", lambda e: e.activation(out=ls2[:, :], in_=ls2[:, :], func=AF.Exp), [ls2], [ls2])
    rho, th2 = sbf("rho", [128, 16]), tmpf("th2", [128, 16])
    V("dve", lambda e: e.tensor_tensor(out=rho[:, :], in0=are2[:, :], in1=ls2[:, :], op=ALU.mult), [are2, ls2], [rho])
    V("act", lambda e: e.activation(out=rho[:, :], in_=rho[:, :], func=AF.Exp), [rho], [rho])
    V("dve", lambda e: e.tensor_tensor(out=th2[:, :], in0=aim2[:, :], in1=ls2[:, :], op=ALU.mult), [aim2, ls2], [th2])
    ang = tmpf("ang", [128, 16, 128])
    V("dve", lambda e: e.tensor_tensor(out=ang[:, :, :], in0=th2[:, :].unsqueeze(2).broadcast_to([128, 16, 128]),
                                       in1=tau[:, :].unsqueeze(1).broadcast_to([128, 16, 128]), op=ALU.mult),
      [th2, tau], [ang])
    COS, SIN = sbf("COS", [128, 16, 128]), sbf("SIN", [128, 16, 128])
    self.sincos(ang[:, :, :].rearrange("p a b -> p (a b)"), ang, 2048,
                SIN[:, :, :].rearrange("p a b -> p (a b)"), SIN, COS[:, :, :].rearrange("p a b -> p (a b)"), COS, st2)
    S1, C1 = tmpf("S1", [128, 16]), tmpf("C1", [128, 16])
    self.sincos(th2[:, :], th2, 16, S1[:, :], S1, C1[:, :], C1, st2)
    lr2, li2 = sbf("lr2", [128, 16]), sbf("li2", [128, 16])
    V("dve", lambda e: e.tensor_tensor(out=lr2[:, :], in0=rho[:, :], in1=C1[:, :], op=ALU.mult), [rho, C1], [lr2])
    V("dve", lambda e: e.tensor_tensor(out=li2[:, :], in0=rho[:, :], in1=S1[:, :], op=ALU.mult), [rho, S1], [li2])
    RHO0 = sbf("RHO0", [128, 16, 128])
    V("dve", lambda e: e.tensor_copy(out=RHO0[:, :, :], in_=rho[:, :].unsqueeze(2).broadcast_to([128, 16, 128])),
      [rho], [RHO0])
    f0 = 127 if d == 1 else 0
    V("dve", lambda e: e.memset(RHO0[:, :, f0:f0 + 1], 0.0), [], [RHO0])
    crT, ciT = tmpf("crT", [128, 16, 16]), tmpf("ciT", [128, 16, 16])
    kb.dma("sp", crT[:, :, :], self.s5_crT[l], crT, w=[crT])
    kb.dma("sp", ciT[:, :, :], self.s5_ciT[l], ciT, w=[ciT])
    CR, CIn = sbf("CR", [128, 16, 128], BF16), sbf("CIn", [128, 16, 128], BF16)
    V("dve", lambda e: e.memset(CR[:, :, :], 0.0), [], [CR])
    V("dve", lambda e: e.memset(CIn[:, :, :], 0.0), [], [CIn])
    for j in range(16):
        for gp in range(2):
            c0 = 32 * (j % 4) + 16 * gp
            V("dve", lambda e: e.tensor_copy(out=CR[gp * 64:(gp + 1) * 64, j, c0:c0 + 16],
                                             in_=crT[gp * 64:(gp + 1) * 64, j, :]), [crT], [CR])
            V("dve", lambda e: e.tensor_scalar(out=CIn[gp * 64:(gp + 1) * 64, j, c0:c0 + 16],
                                               in0=ciT[gp * 64:(gp + 1) * 64, j, :], scalar1=-1.0, scalar2=None,
                                               op0=ALU.mult), [ciT], [CIn])
    if d == 0:
        dv, gb = sbf("s5d", [128, 4]), sbf("s5gb", [128, 4])
        kb.dma("sp", dv[:, :], self.s5_dT[:, l * 4:(l + 1) * 4], dv, w=[dv])
        kb.dma("sp", gb[:, :], self.s5_glub[:, l * 4:(l + 1) * 4], gb, w=[gb])
        gw = sbf("gluw", [128, 4, 512], BF16)
        for c in range(4):
            kb.dma("pool", gw[:, c, :], self.s5_gluw[l, c * 128:(c + 1) * 128, :], gw, w=[gw])
    kb.barrier()
    st2.close()
    usp = Pool(kb, "s5u", 2, [128, 4, 128], BF16, stack=st)
    usfp = Pool(kb, "s5uf", 2, [128, 4, 128], F32, stack=st)
    ybp = Pool(kb, "s5yb", 2, [128, 4, 128], F32, stack=st)
    mp = [Pool(kb, "s5m%d" % i, 1, [128, 8, 128], F32, stack=st) for i in range(4)]
    ZR, ZI = sbf("ZR", [128, 16, 128]), sbf("ZI", [128, 16, 128])
    XZR, XZI = sbf("XZR", [128, 16, 128]), sbf("XZI", [128, 16, 128])
    up_ = [Pool(kb, "s5t%d" % i, 1, [128, 16, 128], F32, stack=st) for i in range(2)]
    XR, XI = sbf("XR", [128, 16, 128], BF16), sbf("XI", [128, 16, 128], BF16)
    xlr, xli = sbf("xlr", [128, 16]), sbf("xli", [128, 16])
    cjr, cji = sbf("cjr", [128, 16]), sbf("cji", [128, 16])
    tt = [sbf("s5tt%d" % i, [128, 16]) for i in range(4)]
    yo = Pool(kb, "s5yo", 2, [128, 4, 128], F32, stack=st)
    rev = (d == 1)
    R3 = (lambda ap: ap[:, :, ::-1]) if rev else (lambda ap: ap)
    first, last = (127, 0) if rev else (0, 127)
    order = [0, 1] + list(range(2, nchunk))
    if rev:
        order = [1, 0] + list(range(nchunk - 1, 1, -1))
    yield
    for ci_, ch in enumerate(order):
        if ci_ > 0:
            yield
        t0 = ch * 128
        us = usp.next()
        kb.dma("pool", us[:, :, :], usv[:, :, t0:t0 + 128], us, r=[("dram", id(usS))], w=[us])
        for hf in range(2):
            PR, PI = PS2.next(), PS2.next()
            for jj in range(8):
                j = hf * 8 + jj
                kb.op("pe", lambda e: e.matmul(PR[:, jj * 128:(jj + 1) * 128], lhsT=DR[:, j, :], rhs=us[:, j // 4, :],
                                               start=True, stop=True), r=[DR, us], w=[PR])
                kb.op("pe", lambda e: e.matmul(PI[:, jj * 128:(jj + 1) * 128], lhsT=DI[:, j, :], rhs=us[:, j // 4, :],
                                               start=True, stop=True), r=[DI, us], w=[PI])
            prv = PR[:, :].rearrange("p (a b) -> p a b", a=8)
            piv = PI[:, :].rearrange("p (a b) -> p a b", a=8)
            cs = R3(COS[:, hf * 8:(hf + 1) * 8, :])
            sn = R3(SIN[:, hf * 8:(hf + 1) * 8, :])
            m = [p.next() for p in mp]
            V("dve", lambda e: e.tensor_tensor(out=m[0][:, :, :], in0=prv, in1=cs, op=ALU.mult), [PR, COS], [m[0]])
            V("dve", lambda e: e.tensor_tensor(out=m[1][:, :, :], in0=piv, in1=sn, op=ALU.mult), [PI, SIN], [m[1]])
            V("dve", lambda e: e.tensor_tensor(out=m[2][:, :, :], in0=piv, in1=cs, op=ALU.mult), [PI, COS], [m[2]])
            V("dve", lambda e: e.tensor_tensor(out=m[3][:, :, :], in0=prv, in1=sn, op=ALU.mult), [PR, SIN], [m[3]])
            V("pool", lambda e: e.tensor_tensor(out=ZR[:, hf * 8:(hf + 1) * 8, :], in0=m[0][:, :, :], in1=m[1][:, :, :],
                                                op=ALU.add), [m[0], m[1]], [ZR])
            V("pool", lambda e: e.tensor_tensor(out=ZI[:, hf * 8:(hf + 1) * 8, :], in0=m[2][:, :, :], in1=m[3][:, :, :],
                                                op=ALU.subtract), [m[2], m[3]], [ZI])
            yield
        if ci_ > 0:
            V("pool", lambda e: e.tensor_tensor(out=tt[0][:, :], in0=lr2[:, :], in1=xlr[:, :], op=ALU.mult), [lr2, xlr], [tt[0]])
            V("pool", lambda e: e.tensor_tensor(out=tt[1][:, :], in0=li2[:, :], in1=xli[:, :], op=ALU.mult), [li2, xli], [tt[1]])
            V("pool", lambda e: e.tensor_tensor(out=cjr[:, :], in0=tt[0][:, :], in1=tt[1][:, :], op=ALU.subtract), [tt[0], tt[1]], [cjr])
            V("pool", lambda e: e.tensor_tensor(out=tt[2][:, :], in0=lr2[:, :], in1=xli[:, :], op=ALU.mult), [lr2, xli], [tt[2]])
            V("pool", lambda e: e.tensor_tensor(out=tt[3][:, :], in0=li2[:, :], in1=xlr[:, :], op=ALU.mult), [li2, xlr], [tt[3]])
            V("pool", lambda e: e.tensor_tensor(out=cji[:, :], in0=tt[2][:, :], in1=tt[3][:, :], op=ALU.add), [tt[2], tt[3]], [cji])
            V("pool", lambda e: e.tensor_tensor(out=ZR[:, :, first], in0=ZR[:, :, first], in1=cjr[:, :], op=ALU.add), [ZR, cjr], [ZR])
            V("pool", lambda e: e.tensor_tensor(out=ZI[:, :, first], in0=ZI[:, :, first], in1=cji[:, :], op=ALU.add), [ZI, cji], [ZI])
        fl = lambda b_: (b_[:, :, :].rearrange("p a b -> p (a b)")[:, ::-1] if rev
                         else b_[:, :, :].rearrange("p a b -> p (a b)"))
        V("dve", lambda e: e.tensor_tensor_scan(out=fl(XZR), data0=fl(RHO0), data1=fl(ZR), initial=0.0,
                                                op0=ALU.mult, op1=ALU.add), [RHO0, ZR], [XZR])
        yield
        V("dve", lambda e: e.tensor_tensor_scan(out=fl(XZI), data0=fl(RHO0), data1=fl(ZI), initial=0.0,
                                                op0=ALU.mult, op1=ALU.add), [RHO0, ZI], [XZI])
        yield
        cs, sn = R3(COS[:, :, :]), R3(SIN[:, :, :])
        ua, ub = up_[0].next(), up_[1].next()
        V("dve", lambda e: e.tensor_tensor(out=ua[:, :, :], in0=XZR[:, :, :], in1=cs, op=ALU.mult), [XZR, COS], [ua])
        V("pool", lambda e: e.tensor_tensor(out=ub[:, :, :], in0=XZI[:, :, :], in1=sn, op=ALU.mult), [XZI, SIN], [ub])
        V("dve", lambda e: e.tensor_tensor(out=XR[:, :, :], in0=ua[:, :, :], in1=ub[:, :, :], op=ALU.subtract), [ua, ub], [XR])
        V("pool", lambda e: e.tensor_tensor(out=xlr[:, :], in0=ua[:, :, last], in1=ub[:, :, last], op=ALU.subtract), [ua, ub], [xlr])
        yield
        V("pool", lambda e: e.tensor_tensor(out=ua[:, :, :], in0=XZR[:, :, :], in1=sn, op=ALU.mult), [XZR, SIN], [ua])
        V("dve", lambda e: e.tensor_tensor(out=ub[:, :, :], in0=XZI[:, :, :], in1=cs, op=ALU.mult), [XZI, COS], [ub])
        V("pool", lambda e: e.tensor_tensor(out=XI[:, :, :], in0=ua[:, :, :], in1=ub[:, :, :], op=ALU.add), [ua, ub], [XI])
        V("pool", lambda e: e.tensor_tensor(out=xli[:, :], in0=ua[:, :, last], in1=ub[:, :, last], op=ALU.add), [ua, ub], [xli])
        yield
        PY = PS2.next()
        for cc in range(4):
            for jj in range(4):
                j = cc * 4 + jj
                kb.op("pe", lambda e: e.matmul(PY[:, cc * 128:(cc + 1) * 128], lhsT=CR[:, j, :], rhs=XR[:, j, :],
                                               start=(jj == 0), stop=False), r=[CR, XR], w=[PY])
                kb.op("pe", lambda e: e.matmul(PY[:, cc * 128:(cc + 1) * 128], lhsT=CIn[:, j, :], rhs=XI[:, j, :],
                                               start=False, stop=(jj == 3)), r=[CIn, XI], w=[PY])
        pyv = PY[:, 0:512].rearrange("p (a b) -> p a b", a=4)
        if d == 1:
            y = yo.next()
            V("act", lambda e: e.activation(out=y[:, :, :], in_=pyv, func=AF.Copy), [PY], [y])
            kb.dma("act", ybv[:, :, t0:t0 + 128], y[:, :, :], y, r=[y], w=[("dram", id(ysbS))])
        else:
            yb, usf = ybp.next(), usfp.next()
            kb.dma("sp", yb[:, :, :], ybv[:, :, t0:t0 + 128], yb, r=[("dram", id(ysbS))], w=[yb])
            kb.dma("sp", usf[:, :, :], usv[:, :, t0:t0 + 128], usf, r=[("dram", id(usS))], w=[usf])
            y = yo.next()
            V("dve", lambda e: e.tensor_tensor(out=y[:, :, :], in0=pyv, in1=yb[:, :, :], op=ALU.add), [PY, yb], [y])
            V("pool", lambda e: e.tensor_tensor(out=usf[:, :, :], in0=usf[:, :, :],
                                                in1=dv[:, :].unsqueeze(2).broadcast_to([128, 4, 128]), op=ALU.mult),
              [usf, dv], [usf])
            V("pool", lambda e: e.tensor_tensor(out=y[:, :, :], in0=y[:, :, :], in1=usf[:, :, :], op=ALU.add), [y, usf], [y])
            g1 = yb
            V("pool", lambda e: e.tensor_tensor(out=g1[:, :, :], in0=y[:, :, :], in1=y[:, :, :], op=ALU.mult), [y], [g1])
            V("dve", lambda e: e.tensor_scalar(out=g1[:, :, :], in0=g1[:, :, :], scalar1=0.044715, scalar2=1.0,
                                               op0=ALU.mult, op1=ALU.add), [g1], [g1])
            V("dve", lambda e: e.tensor_tensor(out=g1[:, :, :], in0=g1[:, :, :], in1=y[:, :, :], op=ALU.mult), [g1, y], [g1])
            V("act", lambda e: e.activation(out=g1[:, :, :], in_=g1[:, :, :], func=AF.Sigmoid, scale=1.5957691216057308),
              [g1], [g1])
            V("dve", lambda e: e.tensor_tensor(out=y[:, :, :], in0=y[:, :, :], in1=g1[:, :, :], op=ALU.mult), [y, g1], [y])
            geb = us
            V("act", lambda e: e.activation(out=geb[:, :, :], in_=y[:, :, :], func=AF.Copy), [y], [geb])
            PG = PS2.next()
            for oc in range(4):
                for kc in range(4):
                    kb.op("pe", lambda e: e.matmul(PG[:, oc * 128:(oc + 1) * 128], lhsT=gw[:, kc, oc * 128:(oc + 1) * 128],
                                                   rhs=geb[:, kc, :], start=(kc == 0), stop=(kc == 3)), r=[gw, geb], w=[PG])
                V("act", lambda e: e.activation(out=usf[:, oc, :], in_=PG[:, oc * 128:(oc + 1) * 128], func=AF.Sigmoid,
                                                bias=gb[:, oc:oc + 1]), [PG, gb], [usf])
            yso = usp.next()
            V("dve", lambda e: e.tensor_tensor(out=yso[:, :, :], in0=y[:, :, :], in1=usf[:, :, :], op=ALU.mult), [y, usf], [yso])
            kb.dma("sp", ysv[:, :, t0:t0 + 128], yso[:, :, :], yso, r=[yso], w=[("dram", id(ysS))])


Model.s5_declare = _s5_declare
Model.sincos = _sincos
Model.s5_dir = _s5_dir


def host_inputs_s5(inp):
    L = DEPTH
    d = {}

    def drive(a):
        a = a.reshape(L, 2, 4, 8, 1, 64)
        a = np.broadcast_to(a, (L, 2, 4, 8, 16, 64))
        return np.ascontiguousarray(a.transpose(0, 1, 3, 4, 2, 5).reshape(L, 2, 128, 4, 64))
    d["s5_are"] = drive(inp["s5_a_re"])
    d["s5_aim"] = drive(inp["s5_a_im"])
    lsd = inp["s5_log_step"].reshape(L, 2, 4, 8, 1)
    d["s5_ls"] = np.ascontiguousarray(np.broadcast_to(lsd, (L, 2, 4, 8, 16)).transpose(0, 1, 3, 4, 2).reshape(L, 2, 128, 4))

    def bT(b):
        b = b.reshape(L, 4, 8, 64, 16)
        return np.ascontiguousarray(b.transpose(0, 2, 4, 1, 3).reshape(L, 128, 4, 64))
    d["s5_brT"] = bT(inp["s5_b_re"])
    d["s5_biT"] = bT(inp["s5_b_im"])

    def st2(a):
        a = a.reshape(L, 2, 16, 2, 64)
        return np.ascontiguousarray(a.transpose(0, 1, 3, 4, 2).reshape(L, 2, 128, 16))
    d["s5_are2"] = st2(inp["s5_a_re"])
    d["s5_aim2"] = st2(inp["s5_a_im"])
    ls2 = np.broadcast_to(inp["s5_log_step"].reshape(L, 2, 16, 2, 1), (L, 2, 16, 2, 64))
    d["s5_ls2"] = np.ascontiguousarray(ls2.transpose(0, 1, 3, 4, 2).reshape(L, 2, 128, 16))

    def cT(c):
        c = c.reshape(L, 16, 2, 16, 64)
        return np.ascontiguousarray(c.transpose(0, 2, 4, 1, 3).reshape(L, 128, 16, 16))
    d["s5_crT"] = cT(inp["s5_c_re"])
    d["s5_ciT"] = cT(inp["s5_c_im"])
    k = np.arange(128)[:, None, None] // 16
    j = np.arange(16)[None, :, None]
    gp = np.arange(2)[None, None, :]
    d["s5_rowmask"] = np.ascontiguousarray((k == 2 * (j % 4) + gp).astype(np.float32))
    d["s5_dT"] = np.ascontiguousarray(inp["s5_d"].reshape(L * 4, 128).T)
    d["s5_glub"] = np.ascontiguousarray(inp["s5_glu_b"].reshape(L * 4, 128).T)
    d["s5_gluw"] = inp["s5_glu_w"]
    d["tauT"] = np.ascontiguousarray(np.broadcast_to(np.arange(128, dtype=np.float32)[None, :], (128, 128)))
    return d


NCH_B = 15


def _rwkv_declare(self):
    L = DEPTH
    self.w_inB = self.din("w_inB", [L, D, NCH_B * 128])
    self.rk_mu = self.din("rk_mu", [128, L * NCH_B])
    self.rk_w0 = self.din("rk_w0", [128, L * 8])
    for n in ("a0", "kk", "ka", "rk", "gng", "gnb"):
        setattr(self, "rk_" + n, self.din("rk_" + n, [128, L * 4]))
    self.rk_w2 = self.din("rk_w2", [L, 128, 512])
    self.rk_a2 = self.din("rk_a2", [L, 64, 512])
    self.rk_g2 = self.din("rk_g2", [L, 128, 512])
    self.bd64 = self.din("bd64", [128, 128])
    self.identb = self.din("identb", [128, 128])
    self.ones0 = self.din("ones0", [2, 128, 128])
    self.rk_MT = self.din("rk_MT", [2, 128, 512])
    self.rk_MN = self.din("rk_MN", [2, 128, 512])


def _rwkv_prep_phase(self, l, src, S):
    kb = self.kb
    W = 256
    Wh = W + 2
    NB = W // 128
    with kb.phase() as st:
        self.psum = Pool(kb, "ps", 8, [128, 512], F32, psum=True, stack=st)
        V = lambda e_, f, r, w: kb.op(e_, f, r=r, w=w)
        sbf = lambda n, shp, dt=F32: kb.sb(n, shp, dt, st)
        w = sbf("wB", [128, NCH, NCH_B * 128], BF16)
        for c in range(NCH):
            kb.dma("pool", w[:, c, :], self.w_inB[l, c * 128:(c + 1) * 128, :], w, w=[w])
        w2b, a2b, g2b = sbf("w2b", [128, 512], BF16), sbf("a2b", [64, 512], BF16), sbf("g2b", [128, 512], BF16)
        kb.dma("pool", w2b[:, :], self.rk_w2[l], w2b, w=[w2b])
        kb.dma("pool", a2b[:, :], self.rk_a2[l], a2b, w=[a2b])
        kb.dma("pool", g2b[:, :], self.rk_g2[l], g2b, w=[g2b])
        bd, idb = sbf("bd", [128, 128], BF16), sbf("idb", [128, 128], BF16)
        kb.dma("pool", bd[:, :], self.bd64[:, :], bd, w=[bd])
        kb.dma("pool", idb[:, :], self.identb[:, :], idb, w=[idb])
        on0 = sbf("on0", [128, 2, 128])
        kb.dma("sp", on0[:, :, :], self.ones0.rearrange("d p t -> p d t"), on0, w=[on0])
        ON = [sbf("ON%d" % d_, [128, 4 * (W // 128), 128]) for d_ in range(2)]
        for d_ in range(2):
            V("dve", lambda e: e.tensor_copy(out=ON[d_][:, :, :], in_=on0[:, d_:d_ + 1, :].broadcast_to([128, 4 * (W // 128), 128])),
              [on0], [ON[d_]])
        mu, omu, hmu = sbf("mu", [128, NCH_B]), sbf("omu", [128, NCH_B]), sbf("hmu", [128, NCH_B])
        kb.dma("sp", mu[:, :], self.rk_mu[:, l * NCH_B:(l + 1) * NCH_B], mu, w=[mu])
        V("dve", lambda e: e.tensor_scalar(out=omu[:, :], in0=mu[:, :], scalar1=-1.0, scalar2=1.0, op0=ALU.mult, op1=ALU.add), [mu], [omu])
        V("dve", lambda e: e.tensor_scalar(out=hmu[:, :], in0=mu[:, :], scalar1=0.5, scalar2=None, op0=ALU.mult), [mu], [hmu])
        w0 = sbf("w0", [128, 8])
        kb.dma("sp", w0[:, :], self.rk_w0[:, l * 8:(l + 1) * 8], w0, w=[w0])
        pv = {}
        for n in ("a0", "kk", "ka", "rk"):
            pv[n] = sbf("p_" + n, [128, 4])
            kb.dma("sp", pv[n][:, :], getattr(self, "rk_" + n)[:, l * 4:(l + 1) * 4], pv[n], w=[pv[n]])
        omka = sbf("omka", [128, 4])
        V("dve", lambda e: e.tensor_scalar(out=omka[:, :], in0=pv["ka"][:, :], scalar1=-1.0, scalar2=1.0, op0=ALU.mult, op1=ALU.add),
          [pv["ka"]], [omka])
        pl = self.stat_pools(Wh, st)
        xp = Pool(kb, "x", 2, [128, NCH, Wh], F32, stack=st)
        xn = sbf("xn", [128, NCH, Wh])
        u = sbf("u", [128, NCH, Wh], BF16)
        zcp = Pool(kb, "zc", 2, [128, Wh], F32, stack=st)
        tmpp = Pool(kb, "ztmp", 2, [128, W], F32, stack=st)
        Z = sbf("Z", [128, NCH_B, W])
        tw, sg, alb = sbf("tw", [128, W], BF16), sbf("sg", [128, W], BF16), sbf("alb", [64, W], BF16)
        LW = [sbf("LW%d" % d_, [128, 4, W]) for d_ in range(2)]
        A, KK, KM, Bv = sbf("A", [128, 4, W]), sbf("KK", [128, 4, W]), sbf("KM", [128, 4, W]), sbf("Bv", [128, 4, W])
        SQ = sbf("SQ", [128, 4, W], BF16)
        T1, T2 = sbf("T1", [128, 4, W]), sbf("T2", [128, 4, W])
        Gp = Pool(kb, "Go", 2, [128, 4, W], BF16, stack=st)
        Bop = Pool(kb, "Bo", 2, [128, 4, W], BF16, stack=st)
        Vb = sbf("Vb", [128, 4, W], BF16)
        tokp = Pool(kb, "tok", 3, [128, NB, 512], BF16, stack=st)
        Lc, Lr, Lq = sbf("Lc", [128, 4, W]), sbf("Lr", [128, 4, W]), sbf("Lq", [128, 4, W])
        E = sbf("E", [128, 4, W])
        outp = {n: Pool(kb, n, 2, [128, 4, W], BF16, stack=st) for n in ("RHO", "KAP", "BET", "KTI")}
        scp = Pool(kb, "sco", 2, [128, NB, 4, 3], F32, stack=st)
        lmn = sbf("lmn", [128, 4, NB])
        srcv = src.rearrange("(c p) t -> p c t", p=128)
        fmv = lambda t_: t_.rearrange("(c p) t -> p c t", p=128)

        def transpose_store(srcb, dst, t0):
            tk = tokp.next()
            for tb in range(NB):
                P = self.psum.next()
                pb = P[:, 0:256].bitcast(BF16)
                for c in range(4):
                    kb.op("pe", lambda e: e.transpose(pb[:, c * 128:(c + 1) * 128], srcb[:, c, tb * 128:(tb + 1) * 128], idb[:, :]),
                          r=[srcb, idb], w=[P])
                V("act", lambda e: e.activation(out=tk[:, tb, :], in_=pb[:, 0:512], func=AF.Copy), [P], [tk])
            kb.dma("act", dst[t0:t0 + W, :].rearrange("(b p) n -> p b n", p=128), tk[:, :, :], tk, r=[tk], w=[("dram", id(dst))])

        for (seg, t0, _) in self.tiles(W):
            seg_lo, seg_hi = (0, CTX) if seg == 1 else (CTX, self.NT)
            lo, hi = max(t0 - 1, seg_lo), min(t0 + W + 1, seg_hi)
            x = xp.next()
            c0 = lo - (t0 - 1)
            if c0 > 0:
                V("dve", lambda e: e.memset(x[:, :, 0:1], 0.0), [], [x])
            if hi < t0 + W + 1:
                V("dve", lambda e: e.memset(x[:, :, Wh - 1:Wh], 0.0), [], [x])
            kb.dma("sp", x[:, :, c0:c0 + (hi - lo)], srcv[:, :, lo:hi], x, r=[("dram", id(src))], w=[x])
            rstd, nmr = self.ln_stats_w(x, Wh, pl)
            self.normalize(xn, x, Wh, rstd, nmr)
            for c in range(NCH):
                V("act", lambda e: e.activation(out=u[:, c, :], in_=xn[:, c, :], func=AF.Identity,
                                                scale=self.modp1[:, 4 * NCH + c, seg:seg + 1],
                                                bias=self.mods[:, 3 * NCH + c, seg:seg + 1]), [xn, self.modp1, self.mods], [u])
            if c0 > 0:
                V("dve", lambda e: e.memset(u[:, :, 0:1], 0.0), [], [u])
            if hi < t0 + W + 1:
                V("dve", lambda e: e.memset(u[:, :, Wh - 1:Wh], 0.0), [], [u])
            for ch in range(NCH_B):
                P = self.psum.next()
                for c in range(NCH):
                    kb.op("pe", lambda e: e.matmul(P[:, 0:Wh], lhsT=w[:, c, ch * 128:(ch + 1) * 128], rhs=u[:, c, :],
                                                   start=(c == 0), stop=(c == NCH - 1)), r=[w, u], w=[P])
                zc, tm = zcp.next(), tmpp.next()
                V("act", lambda e: e.activation(out=zc[:, :], in_=P[:, 0:Wh], func=AF.Copy), [P], [zc])
                V("dve", lambda e: e.tensor_tensor(out=tm[:, :], in0=zc[:, 0:W], in1=zc[:, 2:W + 2], op=ALU.add), [zc], [tm])
                V("act", lambda e: e.activation(out=tm[:, :], in_=tm[:, :], func=AF.Identity, scale=hmu[:, ch:ch + 1]), [tm, hmu], [tm])
                V("dve", lambda e: e.scalar_tensor_tensor(out=Z[:, ch, :], in0=zc[:, 1:W + 1], scalar=omu[:, ch:ch + 1],
                                                          in1=tm[:, :], op0=ALU.mult, op1=ALU.add), [zc, omu, tm], [Z])
            R_, K_, V_ = Z[:, 0:4, :], Z[:, 4:8, :], Z[:, 8:12, :]
            V("act", lambda e: e.activation(out=tw[:, :], in_=Z[:, 12, :], func=AF.Tanh), [Z], [tw])
            V("act", lambda e: e.activation(out=sg[:, :], in_=Z[:, 13, :], func=AF.Sigmoid), [Z], [sg])
            V("act", lambda e: e.activation(out=alb[:, :], in_=Z[0:64, 14, :], func=AF.Copy), [Z], [alb])
            for d_ in range(2):
                for c in range(4):
                    P = self.psum.next()
                    kb.op("pe", lambda e: e.matmul(P[:, 0:W], lhsT=w2b[d_ * 64:(d_ + 1) * 64, c * 128:(c + 1) * 128],
                                                   rhs=tw[d_ * 64:(d_ + 1) * 64, :], start=True, stop=True), r=[w2b, tw], w=[P])
                    V("act", lambda e: e.activation(out=LW[d_][:, c, :], in_=P[:, 0:W], func=AF.Sigmoid,
                                                    bias=w0[:, d_ * 4 + c:d_ * 4 + c + 1]), [P, w0], [LW[d_]])
                V("pool", lambda e: e.tensor_scalar(out=LW[d_][:, :, :], in0=LW[d_][:, :, :], scalar1=-DECAY_SCALE, scalar2=None,
                                                    op0=ALU.mult), [LW[d_]], [LW[d_]])
            Go = Gp.next()
            for c in range(4):
                P = self.psum.next()
                kb.op("pe", lambda e: e.matmul(P[:, 0:W], lhsT=a2b[0:64, c * 128:(c + 1) * 128], rhs=alb[0:64, :],
                                               start=True, stop=True), r=[a2b, alb], w=[P])
                V("act", lambda e: e.activation(out=A[:, c, :], in_=P[:, 0:W], func=AF.Sigmoid, bias=pv["a0"][:, c:c + 1]),
                  [P, pv["a0"]], [A])
                P = self.psum.next()
                kb.op("pe", lambda e: e.matmul(P[:, 0:W], lhsT=g2b[:, c * 128:(c + 1) * 128], rhs=sg[:, :],
                                               start=True, stop=True), r=[g2b, sg], w=[P])
                V("act", lambda e: e.activation(out=Go[:, c, :], in_=P[:, 0:W], func=AF.Copy), [P], [Go])
            kb.dma("act", fmv(S["gR"])[:, :, t0:t0 + W], Go[:, :, :], Go, r=[Go], w=[("dram", id(S["gR"]))])
            for c in range(4):
                V("dve", lambda e: e.tensor_scalar(out=KK[:, c, :], in0=Z[:, 4 + c, :], scalar1=pv["kk"][:, c:c + 1], scalar2=None,
                                                    op0=ALU.mult), [Z, pv["kk"]], [KK])
            V("act", lambda e: e.activation(out=SQ[:, :, :], in_=KK[:, :, :], func=AF.Square), [KK], [SQ])
            for c in range(4):
                P = self.psum.next()
                kb.op("pe", lambda e: e.matmul(P[:, 0:W], lhsT=bd[:, :], rhs=SQ[:, c, :], start=True, stop=True), r=[bd, SQ], w=[P])
                V("dve", lambda e: e.tensor_scalar(out=T1[:, c, :], in0=P[:, 0:W], scalar1=1e-12, scalar2=None, op0=ALU.add), [P], [T1])
            V("act", lambda e: e.activation(out=T1[:, :, :], in_=T1[:, :, :], func=AF.Sqrt), [T1], [T1])
            V("dve", lambda e: e.reciprocal(out=T1[:, :, :], in_=T1[:, :, :]), [T1], [T1])
            V("dve", lambda e: e.tensor_tensor(out=KK[:, :, :], in0=KK[:, :, :], in1=T1[:, :, :], op=ALU.mult), [KK, T1], [KK])
            for c in range(4):
                V("dve", lambda e: e.tensor_scalar(out=T2[:, c, :], in0=A[:, c, :], scalar1=pv["ka"][:, c:c + 1],
                                                   scalar2=omka[:, c:c + 1], op0=ALU.mult, op1=ALU.add), [A, pv["ka"], omka], [T2])
            V("dve", lambda e: e.tensor_tensor(out=KM[:, :, :], in0=K_, in1=T2[:, :, :], op=ALU.mult), [Z, T2], [KM])
            V("pool", lambda e: e.tensor_tensor(out=Bv[:, :, :], in0=KK[:, :, :], in1=A[:, :, :], op=ALU.mult), [KK, A], [Bv])
            V("dve", lambda e: e.tensor_tensor(out=T1[:, :, :], in0=R_, in1=KM[:, :, :], op=ALU.mult), [Z, KM], [T1])
            for c in range(4):
                V("act", lambda e: e.activation(out=SQ[:, c, :], in_=T1[:, c, :], func=AF.Identity, scale=pv["rk"][:, c:c + 1]),
                  [T1, pv["rk"]], [SQ])
            Bo = Bop.next()
            for c in range(4):
                P = self.psum.next()
                kb.op("pe", lambda e: e.matmul(P[:, 0:W], lhsT=bd[:, :], rhs=SQ[:, c, :], start=True, stop=True), r=[bd, SQ], w=[P])
                V("dve", lambda e: e.tensor_tensor(out=Bo[:, c, :], in0=P[:, 0:W], in1=Z[:, 8 + c, :], op=ALU.mult), [P, Z], [Bo])
            kb.dma("sp", fmv(S["bon"])[:, :, t0:t0 + W], Bo[:, :, :], Bo, r=[Bo], w=[("dram", id(S["bon"]))])
            V("act", lambda e: e.activation(out=Vb[:, :, :], in_=V_, func=AF.Copy), [Z], [Vb])
            transpose_store(Vb, S["vt"], t0)
            for d_ in range(2):
                rev = d_ == 1
                fl = (lambda ap: ap.rearrange("p a b -> p (a b)")[:, ::-1]) if rev else (lambda ap: ap.rearrange("p a b -> p (a b)"))
                V("dve", lambda e: e.tensor_tensor_scan(out=fl(Lc[:, :, :]), data0=fl(ON[d_][:, :, :]), data1=fl(LW[d_][:, :, :]),
                                                        initial=0.0, op0=ALU.mult, op1=ALU.add), [ON[d_], LW[d_]], [Lc])
                mid, last = (64, 0) if rev else (63, 127)
                L4 = Lc[:, :, :].rearrange("p c (b t) -> p c b t", b=NB)
                sc = scp.next()
                V("dve", lambda e: e.tensor_copy(out=lmn[:, :, :], in_=L4[:, :, :, mid]), [Lc], [lmn])
                scv = sc[:, :, :, :].rearrange("p b c s -> p c b s")
                V("act", lambda e: e.activation(out=scv[:, :, :, 0], in_=lmn[:, :, :], func=AF.Exp), [lmn], [sc])
                V("act", lambda e: e.activation(out=scv[:, :, :, 2], in_=L4[:, :, :, last], func=AF.Exp), [Lc], [sc])
                V("dve", lambda e: e.tensor_tensor(out=scv[:, :, :, 1], in0=L4[:, :, :, last], in1=lmn[:, :, :], op=ALU.subtract),
                  [Lc, lmn], [sc])
                V("act", lambda e: e.activation(out=scv[:, :, :, 1], in_=scv[:, :, :, 1], func=AF.Exp), [sc], [sc])
                kb.dma("act", S["sc"][d_][:, t0 // 128:t0 // 128 + NB, :, :], sc[:, :, :, :], sc, r=[sc], w=[("dram", id(S["sc"][d_]))])
                Lr4 = Lr[:, :, :].rearrange("p c (b t) -> p c b t", b=NB)
                V("pool", lambda e: e.tensor_tensor(out=Lr4, in0=L4, in1=lmn[:, :, :].unsqueeze(3).broadcast_to([128, 4, NB, 128]),
                                                    op=ALU.subtract), [Lc, lmn], [Lr])
                V("dve", lambda e: e.tensor_tensor(out=Lq[:, :, :], in0=Lr[:, :, :], in1=LW[d_][:, :, :], op=ALU.subtract),
                  [Lr, LW[d_]], [Lq])
                o = {n: outp[n].next() for n in outp}
                V("act", lambda e: e.activation(out=E[:, :, :], in_=Lr[:, :, :], func=AF.Exp), [Lr], [E])
                V("dve", lambda e: e.tensor_tensor(out=o["RHO"][:, :, :], in0=R_, in1=E[:, :, :], op=ALU.mult), [Z, E], [o["RHO"]])
                V("act", lambda e: e.activation(out=E[:, :, :], in_=Lq[:, :, :], func=AF.Exp), [Lq], [E])
                V("dve", lambda e: e.tensor_tensor(out=o["KAP"][:, :, :], in0=KK[:, :, :], in1=E[:, :, :], op=ALU.mult), [KK, E], [o["KAP"]])
                V("act", lambda e: e.activation(out=E[:, :, :], in_=Lr[:, :, :], func=AF.Exp, scale=-1.0), [Lr], [E])
                V("dve", lambda e: e.tensor_tensor(out=o["BET"][:, :, :], in0=Bv[:, :, :], in1=E[:, :, :], op=ALU.mult), [Bv, E], [o["BET"]])
                V("pool", lambda e: e.tensor_tensor(out=o["KTI"][:, :, :], in0=KM[:, :, :], in1=E[:, :, :], op=ALU.mult), [KM, E], [o["KTI"]])
                for n in ("RHO", "KAP", "BET", "KTI"):
                    kb.dma("sp", fmv(S[n][d_])[:, :, t0:t0 + W], o[n][:, :, :], o[n], r=[o[n]], w=[("dram", id(S[n][d_]))])
                transpose_store(o["BET"], S["bt"][d_], t0)
                transpose_store(o["KTI"], S["kt"][d_], t0)


def _ln_stats_w(self, x, Wc, pl):
    kb = self.kb
    xb, sq = pl["xb"].next(), pl["sq"].next()
    kb.op("act", lambda e: e.activation(out=xb[:, :, :Wc], in_=x[:, :, :Wc], func=AF.Copy), r=[x], w=[xb])
    kb.op("act", lambda e: e.activation(out=sq[:, :, :Wc], in_=x[:, :, :Wc], func=AF.Square), r=[x], w=[sq])
    P1, P2 = self.psum.next(), self.psum.next()
    for c in range(NCH):
        kb.op("pe", lambda e: e.matmul(P1[:, 0:Wc], lhsT=self.onesb[:, :], rhs=xb[:, c, :Wc],
                                       start=(c == 0), stop=(c == NCH - 1)), r=[xb, self.onesb], w=[P1])
    for c in range(NCH):
        kb.op("pe", lambda e: e.matmul(P2[:, 0:Wc], lhsT=self.onesb[:, :], rhs=sq[:, c, :Wc],
                                       start=(c == 0), stop=(c == NCH - 1)), r=[sq, self.onesb], w=[P2])
    m2, var, rstd, nmr = pl["m2"].next(), pl["var"].next(), pl["rstd"].next(), pl["nmr"].next()
    kb.op("act", lambda e: e.activation(out=m2[:, 0, :Wc], in_=P1[:, 0:Wc], func=AF.Square), r=[P1], w=[m2])
    kb.op("dve", lambda e: e.scalar_tensor_tensor(out=var[:, 0, :Wc], in0=P2[:, 0:Wc], scalar=LN_EPS,
                                                  in1=m2[:, 0, :Wc], op0=ALU.add, op1=ALU.subtract), r=[P2, m2], w=[var])
    kb.op("act", lambda e: e.activation(out=var[:, 0, :Wc], in_=var[:, 0, :Wc], func=AF.Sqrt), r=[var], w=[var])
    kb.op("dve", lambda e: e.reciprocal(out=rstd[:, 0, :Wc], in_=var[:, 0, :Wc]), r=[var], w=[rstd])
    kb.op("dve", lambda e: e.scalar_tensor_tensor(out=nmr[:, 0, :Wc], in0=P1[:, 0:Wc], scalar=-1.0,
                                                  in1=rstd[:, 0, :Wc], op0=ALU.mult, op1=ALU.mult), r=[P1, rstd], w=[nmr])
    return rstd, nmr


Model.rwkv_declare = _rwkv_declare
Model.rwkv_prep_phase = _rwkv_prep_phase
Model.ln_stats_w = _ln_stats_w


def _rwkv_scan_dir(self, l, d, S, st, psr):
    kb = self.kb
    NT = self.NT
    nchunk = NT // 128
    fmv = lambda t_: t_.rearrange("(c p) t -> p c t", p=128)
    V = lambda e_, f, r, w: kb.op(e_, f, r=r, w=w)
    sbf = lambda n, shp, dt=F32: kb.sb(n, shp, dt, st)
    MT, MN = sbf("MT", [128, 512], BF16), sbf("MN", [128, 512], BF16)
    kb.dma("pool", MT[:, :], self.rk_MT[d], MT, w=[MT])
    kb.dma("pool", MN[:, :], self.rk_MN[d], MN, w=[MN])
    KRp = Pool(kb, "KR", 2, [128, 4, 2, 128], BF16, stack=st)
    BTZp = Pool(kb, "BTZ", 2, [128, 4, 2, 128], BF16, stack=st)
    KTZp = Pool(kb, "KTZ", 2, [128, 4, 2, 128], BF16, stack=st)
    KAZp = Pool(kb, "KAZ", 2, [128, 4, 2, 128], BF16, stack=st)
    for p_ in (BTZp, KTZp, KAZp):
        for b_ in p_.bufs:
            V("pool", lambda e: e.memset(b_[:, :, :, :], 0.0), [], [b_])
    S0Z = sbf("S0Z", [128, 4, 2, 64], BF16)
    V("pool", lambda e: e.memset(S0Z[:, :, :, :], 0.0), [], [S0Z])
    St = sbf("St", [128, 4, 64])
    V("pool", lambda e: e.memset(St[:, :, :], 0.0), [], [St])
    St1 = sbf("St1", [128, 4, 64])
    tokp = {n: Pool(kb, n, 2, [128, 512], BF16, stack=st) for n in ("Btok", "Ktok", "Vtok")}
    scp = Pool(kb, "sc", 2, [128, 4, 3], F32, stack=st)
    AMp = Pool(kb, "AM", 2, [128, 8, 512], BF16, stack=st)
    Pm = [Pool(kb, "Pm%d" % i, 2, [128, 8, 128], BF16, stack=st) for i in range(2)]
    PTm = Pool(kb, "PTm", 2, [128, 8, 128], BF16, stack=st)
    X32, X16p = sbf("X32", [128, 512]), Pool(kb, "X16", 2, [128, 512], BF16, stack=st)
    Oop = Pool(kb, "Oo", 2, [128, 512], F32, stack=st)
    order = [0, 1] + list(range(2, nchunk))
    if d == 1:
        order = [1, 0] + list(range(nchunk - 1, 1, -1))
    ev = 0
    yield
    for ci_, ch in enumerate(order):
        if ci_ > 0:
            yield
        t0 = ch * 128
        KR, BTZ, KTZ, KAZ = KRp.next(), BTZp.next(), KTZp.next(), KAZp.next()
        kb.dma("sp", KR[:, :, 0, :], fmv(S["KAP"][d])[:, :, t0:t0 + 128], KR, r=[("dram", id(S["KAP"][d]))], w=[KR])
        kb.dma("sp", KR[:, :, 1, :], fmv(S["RHO"][d])[:, :, t0:t0 + 128], KR, r=[("dram", id(S["RHO"][d]))], w=[KR])
        for par in range(2):
            ps_ = slice(par * 64, (par + 1) * 64)
            kb.dma("sp", BTZ[ps_, :, par, :], fmv(S["BET"][d])[ps_, :, t0:t0 + 128], BTZ, r=[("dram", id(S["BET"][d]))], w=[BTZ])
            kb.dma("sp", KTZ[ps_, :, par, :], fmv(S["KTI"][d])[ps_, :, t0:t0 + 128], KTZ, r=[("dram", id(S["KTI"][d]))], w=[KTZ])
            kb.dma("sp", KAZ[ps_, :, par, :], fmv(S["KAP"][d])[ps_, :, t0:t0 + 128], KAZ, r=[("dram", id(S["KAP"][d]))], w=[KAZ])
        tk = {}
        for n, key in (("Btok", "bt"), ("Ktok", "kt"), ("Vtok", "vt")):
            tk[n] = tokp[n].next()
            srcd = S[key][d] if key != "vt" else S[key]
            kb.dma("sp", tk[n][:, :], srcd[t0:t0 + 128, :], tk[n], r=[("dram", id(srcd))], w=[tk[n]])
        Btok, Ktok, Vtok = tk["Btok"], tk["Ktok"], tk["Vtok"]
        sc = scp.next()
        kb.dma("sp", sc[:, :, :], S["sc"][d][:, ch, :, :], sc, r=[("dram", id(S["sc"][d]))], w=[sc])
        for par in range(2):
            ps_ = slice(par * 64, (par + 1) * 64)
            V("pool", lambda e: e.tensor_tensor(out=S0Z[ps_, :, par, :], in0=St[ps_, :, :],
                                                in1=sc[ps_, :, 0:1].broadcast_to([64, 4, 64]), op=ALU.mult), [St, sc], [S0Z])
        AM = AMp.next()
        for h in range(8):
            c, par = h // 2, h % 2
            P = psr.next()
            rhs = KR[:, c, :, :].rearrange("p s t -> p (s t)")
            kb.op("pe", lambda e: e.matmul(P[:, 0:256], lhsT=BTZ[:, c, par, :], rhs=rhs, start=True, stop=True), r=[BTZ, KR], w=[P])
            kb.op("pe", lambda e: e.matmul(P[:, 256:512], lhsT=KTZ[:, c, par, :], rhs=rhs, start=True, stop=True), r=[KTZ, KR], w=[P])
            V("dve", lambda e: e.tensor_tensor(out=AM[:, h, :], in0=P[:, :], in1=MT[:, :], op=ALU.mult), [P, MT], [AM])
            if h == 3:
                yield
        yield
        Pj, PTj = Pm[0].next(), PTm.next()
        for g4 in range(2):
            P = psr.next()
            for hh in range(4):
                h = g4 * 4 + hh
                c, par = h // 2, h % 2
                kb.op("pe", lambda e: e.matmul(P[:, hh * 128:(hh + 1) * 128], lhsT=KAZ[:, c, par, :], rhs=BTZ[:, c, par, :],
                                               start=True, stop=True), r=[KAZ, BTZ], w=[P])
            V("dve", lambda e: e.tensor_tensor(out=Pj[:, g4 * 4:(g4 + 1) * 4, :].rearrange("p h t -> p (h t)"), in0=P[:, :],
                                               in1=MN[:, :], op=ALU.mult), [P, MN], [Pj])
        V("act", lambda e: e.activation(out=PTj[:, :, :], in_=AM[:, :, 0:128], func=AF.Copy), [AM], [PTj])
        P = psr.next()
        for h in range(8):
            c, par = h // 2, h % 2
            kb.op("pe", lambda e: e.matmul(P[:, h * 64:(h + 1) * 64], lhsT=KR[:, c, 0, :], rhs=S0Z[:, c, par, :],
                                           start=True, stop=False), r=[KR, S0Z], w=[P])
            kb.op("pe", lambda e: e.matmul(P[:, h * 64:(h + 1) * 64], lhsT=AM[:, h, 256:384], rhs=Vtok[:, h * 64:(h + 1) * 64],
                                           start=False, stop=True), r=[AM, Vtok], w=[P])
        V("act", lambda e: e.activation(out=X32[:, :], in_=P[:, :], func=AF.Identity, scale=-1.0), [P], [X32])
        X16 = X16p.next()
        V("act", lambda e: e.activation(out=X16[:, :], in_=X32[:, :], func=AF.Copy), [X32], [X16])
        yield
        for j in range(7):
            P = psr.next()
            for h in range(8):
                kb.op("pe", lambda e: e.matmul(P[:, h * 64:(h + 1) * 64], lhsT=PTj[:, h, :], rhs=X16[:, h * 64:(h + 1) * 64],
                                               start=True, stop=True), r=[PTj, X16], w=[P])
            V("dve", lambda e: e.tensor_tensor(out=X32[:, :], in0=P[:, :], in1=X32[:, :], op=ALU.add), [P, X32], [X32])
            X16 = X16p.next()
            V("act", lambda e: e.activation(out=X16[:, :], in_=X32[:, :], func=AF.Copy), [X32], [X16])
            if j < 6:
                PTn = PTm.next()
                Pn = Pm[(j + 1) % 2].next() if j < 5 else None
                for g4 in range(2):
                    P = psr.next()
                    for hh in range(4):
                        h = g4 * 4 + hh
                        kb.op("pe", lambda e: e.matmul(P[:, hh * 128:(hh + 1) * 128], lhsT=Pj[:, h, :], rhs=PTj[:, h, :],
                                                       start=True, stop=True), r=[Pj, PTj], w=[P])
                    eng = "act"
                    ev += 1
                    dst = PTn[:, g4 * 4:(g4 + 1) * 4, :].rearrange("p h t -> p (h t)")
                    if eng == "act":
                        V("act", lambda e: e.activation(out=dst, in_=P[:, :], func=AF.Copy), [P], [PTn])
                    else:
                        V("dve", lambda e: e.tensor_copy(out=dst, in_=P[:, :]), [P], [PTn])
                    if Pn is not None:
                        P = psr.next()
                        for hh in range(4):
                            h = g4 * 4 + hh
                            kb.op("pe", lambda e: e.matmul(P[:, hh * 128:(hh + 1) * 128], lhsT=PTj[:, h, :], rhs=Pj[:, h, :],
                                                           start=True, stop=True), r=[Pj, PTj], w=[P])
                        eng = "act"
                        ev += 1
                        dst = Pn[:, g4 * 4:(g4 + 1) * 4, :].rearrange("p h t -> p (h t)")
                        if eng == "act":
                            V("act", lambda e: e.activation(out=dst, in_=P[:, :], func=AF.Copy), [P], [Pn])
                        else:
                            V("dve", lambda e: e.tensor_copy(out=dst, in_=P[:, :]), [P], [Pn])
                PTj = PTn
                if Pn is not None:
                    Pj = Pn
            if j < 6:
                yield
        U16 = X16
        P = psr.next()
        for h in range(8):
            c, par = h // 2, h % 2
            hs = slice(h * 64, (h + 1) * 64)
            kb.op("pe", lambda e: e.matmul(P[:, hs], lhsT=KR[:, c, 1, :], rhs=S0Z[:, c, par, :], start=True, stop=False),
                  r=[KR, S0Z], w=[P])
            kb.op("pe", lambda e: e.matmul(P[:, hs], lhsT=AM[:, h, 128:256], rhs=U16[:, hs], start=False, stop=False),
                  r=[AM, U16], w=[P])
            kb.op("pe", lambda e: e.matmul(P[:, hs], lhsT=AM[:, h, 384:512], rhs=Vtok[:, hs], start=False, stop=True),
                  r=[AM, Vtok], w=[P])
        Oo = Oop.next()
        V("act", lambda e: e.activation(out=Oo[:, :], in_=P[:, :], func=AF.Copy), [P], [Oo])
        kb.dma("act", S["O"][d][t0:t0 + 128, :], Oo[:, :], Oo, r=[Oo], w=[("dram", id(S["O"][d]))])
        yield
        P = psr.next()
        for c in range(4):
            cs_ = slice(c * 128, (c + 1) * 128)
            kb.op("pe", lambda e: e.matmul(P[:, cs_], lhsT=Btok[:, cs_], rhs=U16[:, cs_], start=True, stop=False),
                  r=[Btok, U16], w=[P])
            kb.op("pe", lambda e: e.matmul(P[:, cs_], lhsT=Ktok[:, cs_], rhs=Vtok[:, cs_], start=False, stop=True),
                  r=[Ktok, Vtok], w=[P])
        V("pool", lambda e: e.tensor_tensor(out=St1[:, :, :], in0=St[:, :, :], in1=sc[:, :, 2:3].broadcast_to([128, 4, 64]),
                                            op=ALU.mult), [St, sc], [St1])
        pv_ = P[:, :].rearrange("p (c x) -> p c x", c=4)
        for par in range(2):
            ps_ = slice(par * 64, (par + 1) * 64)
            V("dve", lambda e: e.tensor_tensor(out=St[ps_, :, :], in0=pv_[ps_, :, par * 64:(par + 1) * 64],
                                               in1=sc[ps_, :, 1:2].broadcast_to([64, 4, 64]), op=ALU.mult), [P, sc, St1], [St])
        V("pool", lambda e: e.tensor_tensor(out=St[:, :, :], in0=St[:, :, :], in1=St1[:, :, :], op=ALU.add), [St, St1], [St])


def _merge_declare(self):
    L = DEPTH
    self.branch_proj = self.din("branch_proj", [L, 3, 512, D])
    self.w_out = self.din("w_out", [L, D, D])


def _merge_phase(self, l, src, dst, S, ctx):
    kb = self.kb
    W = 256
    NB = W // 128
    lni = (l * 3 + 1) * NCH
    with kb.phase() as st:
        self.psum = Pool(kb, "ps", 8, [128, 512], F32, psum=True, stack=st)
        V = lambda e_, f, r, w: kb.op(e_, f, r=r, w=w)
        sbf = lambda n, shp, dt=F32: kb.sb(n, shp, dt, st)
        bp = sbf("bp", [128, 12, D], BF16)
        wo = sbf("wo", [128, NCH, D], BF16)
        for b_ in range(3):
            for c in range(4):
                kb.dma("pool", bp[:, b_ * 4 + c, :], self.branch_proj[l, b_, c * 128:(c + 1) * 128, :], bp, w=[bp])
        for c in range(NCH):
            kb.dma("pool", wo[:, c, :], self.w_out[l, c * 128:(c + 1) * 128, :], wo, w=[wo])
        idb = sbf("idb", [128, 128], BF16)
        kb.dma("pool", idb[:, :], self.identb[:, :], idb, w=[idb])
        gng, gnb = sbf("gng", [128, 4]), sbf("gnb", [128, 4])
        kb.dma("sp", gng[:, :], self.rk_gng[:, l * 4:(l + 1) * 4], gng, w=[gng])
        kb.dma("sp", gnb[:, :], self.rk_gnb[:, l * 4:(l + 1) * 4], gnb, w=[gnb])
        pl = self.stat_pools(W, st)
        xp = Pool(kb, "x", 2, [128, NCH, W], F32, stack=st)
        xn = sbf("xn", [128, NCH, W])
        Ofp = Pool(kb, "Of", 2, [128, NB, 512], F32, stack=st)
        Obp = Pool(kb, "Ob", 2, [128, NB, 512], F32, stack=st)
        onb = sbf("onb", [128, NB, 512], BF16)
        st8 = [sbf("st8_%d" % i, [128, NB, 8]) for i in range(3)]
        sqt = sbf("sqt", [128, NB, 512])
        Y = {n: Pool(kb, "y" + n, 2, [128, 4, W], BF16, stack=st) for n in ("a", "s", "bon", "g")}
        yr = sbf("yr", [128, 4, W], BF16)
        yt = sbf("yrt", [128, 4, W])
        gp = Pool(kb, "gates", 2, [128, 24, W], BF16, stack=st)
        m1, m2, m3 = sbf("m1", [128, W]), sbf("m2", [128, W]), sbf("m3", [128, W])
        mT = sbf("mT", [128, NCH, W], BF16)
        srcv = src.rearrange("(c p) t -> p c t", p=128)
        dstv = dst.rearrange("(c p) t -> p c t", p=128)
        fmv = lambda t_: t_.rearrange("(c p) t -> p c t", p=128)
        for (seg, t0, _) in self.tiles(W, ctx=ctx):
            x = xp.next()
            kb.dma("sp", x[:, :, :], srcv[:, :, t0:t0 + W], x, r=[("dram", id(src))], w=[x])
            Of, Ob = Ofp.next(), Obp.next()
            kb.dma("sp", Of[:, :, :], S["O"][0][t0:t0 + W, :].rearrange("(b p) n -> p b n", p=128), Of, r=[("dram", id(S["O"][0]))], w=[Of])
            kb.dma("sp", Ob[:, :, :], S["O"][1][t0:t0 + W, :].rearrange("(b p) n -> p b n", p=128), Ob, r=[("dram", id(S["O"][1]))], w=[Ob])
            ld = {}
            for n, key in (("a", "ya"), ("s", "ys"), ("bon", "bon"), ("g", "gR")):
                ld[n] = Y[n].next()
                kb.dma("sp", ld[n][:, :, :], fmv(S[key])[:, :, t0:t0 + W], ld[n], r=[("dram", id(S[key]))], w=[ld[n]])
            gt = gp.next()
            kb.dma("sp", gt[:, :, :], fmv(S["gS"])[:, :, t0:t0 + W], gt, r=[("dram", id(S["gS"]))], w=[gt])
            V("dve", lambda e: e.tensor_tensor(out=Of[:, :, :], in0=Of[:, :, :], in1=Ob[:, :, :], op=ALU.add), [Of, Ob], [Of])
            O4 = Of[:, :, :].rearrange("p b (h v) -> p b h v", h=8)
            sm, vr, rs = st8
            V("dve", lambda e: e.tensor_reduce(out=sm[:, :, :], in_=O4, axis=AX.X, op=ALU.add), [Of], [sm])
            V("dve", lambda e: e.tensor_scalar(out=sm[:, :, :], in0=sm[:, :, :], scalar1=1.0 / 64, scalar2=None, op0=ALU.mult), [sm], [sm])
            V("dve", lambda e: e.tensor_tensor(out=O4, in0=O4, in1=sm[:, :, :].unsqueeze(3).broadcast_to([128, NB, 8, 64]),
                                               op=ALU.subtract), [Of, sm], [Of])
            V("act", lambda e: e.activation(out=sqt[:, :, :], in_=Of[:, :, :], func=AF.Square), [Of], [sqt])
            V("dve", lambda e: e.tensor_reduce(out=vr[:, :, :], in_=sqt[:, :, :].rearrange("p b (h v) -> p b h v", h=8), axis=AX.X,
                                               op=ALU.add), [sqt], [vr])
            V("dve", lambda e: e.tensor_scalar(out=vr[:, :, :], in0=vr[:, :, :], scalar1=1.0 / 64, scalar2=GN_EPS, op0=ALU.mult,
                                               op1=ALU.add), [vr], [vr])
            V("act", lambda e: e.activation(out=vr[:, :, :], in_=vr[:, :, :], func=AF.Sqrt), [vr], [vr])
            V("dve", lambda e: e.reciprocal(out=rs[:, :, :], in_=vr[:, :, :]), [vr], [rs])
            V("dve", lambda e: e.tensor_tensor(out=onb[:, :, :].rearrange("p b (h v) -> p b h v", h=8), in0=O4,
                                               in1=rs[:, :, :].unsqueeze(3).broadcast_to([128, NB, 8, 64]), op=ALU.mult), [Of, rs], [onb])
            for tb in range(NB):
                P = self.psum.next()
                pb = P[:, 0:256].bitcast(BF16)
                for c in range(4):
                    kb.op("pe", lambda e: e.transpose(pb[:, c * 128:(c + 1) * 128], onb[:, tb, c * 128:(c + 1) * 128], idb[:, :]),
                          r=[onb, idb], w=[P])
                for c in range(4):
                    V("act", lambda e: e.activation(out=yt[:, c, tb * 128:(tb + 1) * 128], in_=pb[:, c * 128:(c + 1) * 128],
                                                    func=AF.Identity, scale=gng[:, c:c + 1], bias=gnb[:, c:c + 1]), [P, gng, gnb], [yt])
            V("pool", lambda e: e.tensor_tensor(out=yt[:, :, :], in0=yt[:, :, :], in1=ld["bon"][:, :, :], op=ALU.add), [yt, ld["bon"]], [yt])
            V("dve", lambda e: e.tensor_tensor(out=yr[:, :, :], in0=yt[:, :, :], in1=ld["g"][:, :, :], op=ALU.mult), [yt, ld["g"]], [yr])
            if "yrS" in S:
                kb.dma("sp", fmv(S["yrS"])[:, :, t0:t0 + W], yr[:, :, :], yr, r=[yr], w=[("dram", id(S["yrS"]))])
            ysrc = [ld["a"], yr, ld["s"]]
            for oc in range(NCH):
                Pa, Pb = self.psum.next(), self.psum.next()
                tgt = [(Pa, 0), (Pa, W), (Pb, 0)]
                for b_ in range(3):
                    Pt, o0 = tgt[b_]
                    for kc in range(4):
                        kb.op("pe", lambda e: e.matmul(Pt[:, o0:o0 + W], lhsT=bp[:, b_ * 4 + kc, oc * 128:(oc + 1) * 128],
                                                       rhs=ysrc[b_][:, kc, :], start=(kc == 0), stop=(kc == 3)), r=[bp, ysrc[b_]], w=[Pt])
                V("dve", lambda e: e.tensor_tensor(out=m1[:, :], in0=Pa[:, 0:W], in1=gt[:, oc, :], op=ALU.mult), [Pa, gt], [m1])
                V("dve", lambda e: e.tensor_tensor(out=m2[:, :], in0=Pa[:, W:2 * W], in1=gt[:, 8 + oc, :], op=ALU.mult), [Pa, gt], [m2])
                V("dve", lambda e: e.tensor_tensor(out=m3[:, :], in0=Pb[:, 0:W], in1=gt[:, 16 + oc, :], op=ALU.mult), [Pb, gt], [m3])
                V("dve", lambda e: e.tensor_tensor(out=m1[:, :], in0=m1[:, :], in1=m2[:, :], op=ALU.add), [m1, m2], [m1])
                V("pool", lambda e: e.tensor_tensor(out=mT[:, oc, :], in0=m1[:, :], in1=m3[:, :], op=ALU.add), [m1, m3], [mT])
            V("pool", lambda e: e.tensor_scalar(out=x[:, :, :], in0=x[:, :, :], scalar1=ALPHA, scalar2=None, op0=ALU.mult), [x], [x])
            for oc in range(NCH):
                if oc % 2 == 0:
                    P = self.psum.next()
                o0 = (oc % 2) * W
                for kc in range(NCH):
                    kb.op("pe", lambda e: e.matmul(P[:, o0:o0 + W], lhsT=wo[:, kc, oc * 128:(oc + 1) * 128], rhs=mT[:, kc, :],
                                                   start=(kc == 0), stop=(kc == NCH - 1)), r=[wo, mT], w=[P])
                V("dve", lambda e: e.scalar_tensor_tensor(out=x[:, oc, :], in0=P[:, o0:o0 + W],
                                                          scalar=self.mods[:, 5 * NCH + oc, seg:seg + 1], in1=x[:, oc, :],
                                                          op0=ALU.mult, op1=ALU.add), [P, x, self.mods], [x])
            P = self.psum.next()
            rstd, nmr = self.ln_stats(x, W, P, pl)
            self.normalize(xn, x, W, rstd, nmr)
            for c in range(NCH):
                V("act", lambda e: e.activation(out=xn[:, c, :], in_=xn[:, c, :], func=AF.Identity,
                                                scale=self.lng[:, lni + c:lni + c + 1], bias=self.lnb[:, lni + c:lni + c + 1]),
                  [xn, self.lng, self.lnb], [xn])
            kb.dma("act", dstv[:, :, t0:t0 + W], xn[:, :, :], xn, r=[xn], w=[("dram", id(dst))])


Model.rwkv_scan_dir = _rwkv_scan_dir
Model.merge_declare = _merge_declare
Model.merge_phase = _merge_phase


def host_inputs_rwkv(inp):
    L = DEPTH
    d = {}
    w = inp["w_in"]
    rw = w[:, :, 768:2624]
    r, k, v = rw[:, :, 0:512], rw[:, :, 512:1024], rw[:, :, 1024:1536]
    wlo, alo, glo = rw[:, :, 1536:1664], rw[:, :, 1664:1728], rw[:, :, 1728:1856]
    pad = np.zeros_like(alo)
    d["w_inB"] = np.ascontiguousarray(np.concatenate([r, k, v, wlo, glo, alo, pad], -1))
    mu = inp["rwkv_mu"]
    mu_r = np.concatenate([mu[:, 0:1536], mu[:, 1536:1664], mu[:, 1728:1856], mu[:, 1664:1728], np.zeros((L, 64), np.float32)], -1)
    d["rk_mu"] = np.ascontiguousarray(mu_r.reshape(L * NCH_B, 128).T)
    d["rk_w0"] = np.ascontiguousarray(inp["rwkv_w0"].reshape(L * 8, 128).T)
    for n, src in (("a0", "rwkv_a0"), ("kk", "rwkv_k_k"), ("ka", "rwkv_k_a"), ("rk", "rwkv_r_k"), ("gng", "rwkv_gn_g"), ("gnb", "rwkv_gn_b")):
        d["rk_" + n] = np.ascontiguousarray(inp[src].reshape(L * 4, 128).T)
    d["rk_w2"] = np.ascontiguousarray(inp["rwkv_w2"].reshape(L, 128, 512))
    d["rk_a2"] = inp["rwkv_a2"]
    d["rk_g2"] = inp["rwkv_g2"]
    i = np.arange(128)
    d["bd64"] = np.ascontiguousarray(((i[:, None] // 64) == (i[None, :] // 64)).astype(np.float32))
    d["identb"] = np.eye(128, dtype=np.float32)
    on = np.ones((2, 128, 128), np.float32)
    on[0, :, 0] = 0.0
    on[1, :, 127] = 0.0
    d["ones0"] = on
    MT = np.zeros((2, 128, 512), np.float32)
    MN = np.zeros((2, 128, 512), np.float32)
    ii, tt = i[:, None], i[None, :]
    for dd in range(2):
        prev = (ii < tt) if dd == 0 else (ii > tt)
        incl = prev | (ii == tt)
        MT[dd, :, 0:128] = -(prev.astype(np.float32))
        MT[dd, :, 128:256] = incl
        MT[dd, :, 256:384] = prev
        MT[dd, :, 384:512] = incl
        MN[dd] = np.tile(-(prev.T.astype(np.float32)), (1, 4))
    d["rk_MT"], d["rk_MN"] = MT, MN
    d["branch_proj"] = inp["branch_proj"]
    d["w_out"] = inp["w_out"]
    return d


def _make_scratch(self):
    NT = self.NT
    S = {}
    for n, shp, dt in (("qS", [512, NT], BF16), ("kS", [256, NT], BF16), ("vS", [NT, 128], BF16), ("usS", [512, NT], F32),
                       ("gS", [3072, NT], BF16), ("ya", [512, NT], BF16), ("ysb", [512, NT], F32), ("ys", [512, NT], BF16),
                       ("gR", [512, NT], BF16), ("bon", [512, NT], BF16), ("vt", [NT, 512], BF16)):
        S[n] = self.scratch(n, shp, dt)
    for n in ("RHO", "KAP", "BET", "KTI"):
        S[n] = [self.scratch("%s%d" % (n, d), [512, NT], BF16) for d in range(2)]
    for n in ("bt", "kt"):
        S[n] = [self.scratch("%s%d" % (n, d), [NT, 512], BF16) for d in range(2)]
    S["sc"] = [self.scratch("sc%d" % d, [128, NT // 128, 4, 3], F32) for d in range(2)]
    S["O"] = [self.scratch("O%d" % d, [NT, 512], F32) for d in range(2)]
    if "yrS" in self.dbg:
        S["yrS"] = self.scratch("yrS", [512, NT], BF16)
    return S


def _mixer(self, l, src, dst, S, ctx_out):
    self.mixA_phase(l, src, S["qS"], S["kS"], S["vS"], S["usS"], S["gS"])
    self.attn_phase(l, S["qS"], S["kS"], S["vS"], S["ya"])
    self.rwkv_prep_phase(l, src, S)
    self.scan_phase(l, S)
    self.merge_phase(l, src, dst, S, ctx=ctx_out)


def _scan_phase(self, l, S):
    kb = self.kb
    for (ds5, drk) in ((1, 0), (0, 1)):
        with kb.phase() as st:
            PS2 = Pool(kb, "ps2", 2, [128, 1024], F32, psum=True, stack=st)
            psr = Pool(kb, "psr", 4, [128, 512], F32, psum=True, stack=st)
            g1 = self.s5_dir(l, ds5, S["usS"], S["ysb"], S["ys"], st, PS2)
            next(g1)
            g2 = self.rwkv_scan_dir(l, drk, S, st, psr)
            next(g2)
            alive = [g1, g2]
            while alive:
                for g in list(alive):
                    try:
                        next(g)
                    except StopIteration:
                        alive.remove(g)


Model.scan_phase = _scan_phase
Model.make_scratch = _make_scratch
Model.mixer = _mixer


def build_model(T, dbg=()):
    m = Model(T, dbg=dbg)
    m.declare_inputs()
    m.mixA_declare()
    m.s5_declare()
    m.rwkv_declare()
    m.merge_declare()
    m.setup_consts()
    NT = m.NT
    S = m.make_scratch()
    streams = [m.scratch("str%d" % i, [D, NT], F32) for i in range(3)]
    outT = m.dout("outT", [D, T])
    cur = m.xT
    for l in range(DEPTH):
        last = l == DEPTH - 1
        m.adaln_phase(l)
        m.ffn_phase(l, 0, cur, streams[0], 0, ctx=True)
        m.mixer(l, streams[0], streams[1], S, ctx_out=not last)
        if last:
            m.ffn_phase(l, 1, streams[1], outT, CTX, ctx=False)
        else:
            m.ffn_phase(l, 1, streams[1], streams[2], 0, ctx=True)
            cur = streams[2]
    m.kb.finish()
    return m


def all_host_inputs(inp, b, T):
    d = host_inputs(inp, b, T)
    d.update(host_inputs_A(inp, T))
    d.update(host_inputs_s5(inp))
    d.update(host_inputs_rwkv(inp))
    return d


T_FULL = 8192
N_CORES = 8


def kernel(**inputs):
    inp = {k: np.asarray(v) for k, v in inputs.items()}
    m = build_model(T_FULL)
    B = inp["x"].shape[0]
    shared = None
    in_maps = []
    for core in range(N_CORES):
        b = core % B
        d = all_host_inputs(inp, b, T_FULL) if shared is None else dict(shared)
        if shared is None:
            shared = d
        else:
            d["xT"] = np.ascontiguousarray(np.concatenate([inp["ctx"][b], inp["x"][b, :T_FULL]], 0).T)
            cond = np.stack([inp["c"][b], inp["c_ctx"]], -1)
            d["condT"] = np.ascontiguousarray(cond.reshape(NCH, 128, 2).transpose(1, 0, 2))
        in_maps.append({k: v for k, v in d.items() if k in m.dram_in})
    res = run_bass_kernel_spmd(m.nc, in_maps, core_ids=list(range(N_CORES)))
    out = np.stack([np.ascontiguousarray(res.results[b]["outT"].T) for b in range(B)], 0)
    return out.astype(np.float32)
```
